# Optimizing a Trainium2 kernel written in Bass

```python
import math
import jax
import jax.numpy as jnp
from jax import lax
import numpy as np

D_MODEL = 1024
BATCH = 16
SEQ = 2048
DEPTH = 2

GRID_W = 64
CTX_LEN = 256
EPS = 1e-6

N_BRANCH = 4
BRANCH_W = D_MODEL // 4

MLA_HEADS = 4
MLA_NOPE = 64
MLA_ROPE = 32
MLA_V = 64
MLA_QK = MLA_NOPE + MLA_ROPE
Q_LORA = 256
KV_LORA = 128
ROPE_THETA = 10000.0
Q_BLOCK = 128

S5_GROUP = 16
S5_GROUPS = BRANCH_W // S5_GROUP
S5_STATE = 64

LRU_BLOCKS = 4
LRU_BLOCK_W = BRANCH_W // LRU_BLOCKS
LRU_CONV = 4
LRU_C = 8.0

POOL_WINDOWS = (2, 4, 8, 16)
POOL_GROUP_W = BRANCH_W // len(POOL_WINDOWS)

OFF_KROPE = KV_LORA
OFF_S5 = OFF_KROPE + MLA_ROPE
OFF_LRU = OFF_S5 + BRANCH_W
OFF_CQ = OFF_LRU + BRANCH_W
OFF_POOL = OFF_CQ + Q_LORA
OFF_GATE = OFF_POOL + BRANCH_W
OFF_MERGE = OFF_GATE + N_BRANCH * BRANCH_W
IN_W = OFF_MERGE + N_BRANCH * D_MODEL
IN_SPLITS = (OFF_KROPE, OFF_S5, OFF_LRU, OFF_CQ, OFF_POOL, OFF_GATE, OFF_MERGE)
MEM_W = OFF_CQ
MEM_SPLITS = (OFF_KROPE, OFF_S5, OFF_LRU)

kernel_name = 'hybrid_mla_s5_rglru_pool_diffusion_block'


def _rmsnorm(x, g):
    x32 = x.astype(jnp.float32)
    y = x32 * lax.rsqrt(jnp.mean(x32 * x32, axis=-1, keepdims=True) + EPS)
    return (y * g.astype(jnp.float32)).astype(x.dtype)


def _rope_tables(L):
    rows_n = L // GRID_W
    row = jnp.repeat(jnp.arange(rows_n, dtype=jnp.int32), GRID_W).astype(jnp.float32)
    col = jnp.tile(jnp.arange(GRID_W, dtype=jnp.int32), rows_n).astype(jnp.float32)
    nf = MLA_ROPE // 4
    inv = ROPE_THETA ** (-jnp.arange(nf, dtype=jnp.float32) / nf)
    ang_r = row[:, None] * inv
    ang_c = col[:, None] * inv
    return (jnp.cos(ang_r), jnp.sin(ang_r), jnp.cos(ang_c), jnp.sin(ang_c))


def _rot_half(v, cos, sin):
    nf = v.shape[-1] // 2
    v1, v2 = v[..., :nf], v[..., nf:]
    return jnp.concatenate([v1 * cos - v2 * sin, v1 * sin + v2 * cos], axis=-1)


def _axial_rope(v, tables):
    cr, sr, cc, sc = [t[None, :, None, :].astype(v.dtype) for t in tables]
    half = MLA_ROPE // 2
    return jnp.concatenate([_rot_half(v[..., :half], cr, sr), _rot_half(v[..., half:], cc, sc)], axis=-1)


def _mla_kv(ckv, krope, kv_norm_g, w_ukv, k_gain, tables):
    B, L, _ = ckv.shape
    kv = (_rmsnorm(ckv, kv_norm_g) @ w_ukv).reshape(B, L, MLA_HEADS, MLA_NOPE + MLA_V)
    k_nope, v = kv[..., :MLA_NOPE], kv[..., MLA_NOPE:]
    k_r = jnp.broadcast_to(krope[:, :, None, :], (B, L, MLA_HEADS, MLA_ROPE))
    k = _rmsnorm(jnp.concatenate([k_nope, k_r], axis=-1), k_gain)
    if tables is not None:
        k = jnp.concatenate([k[..., :MLA_NOPE], _axial_rope(k[..., MLA_NOPE:], tables)], axis=-1)
    return k, v


def _mla_q(cq, q_norm_g, w_uq, q_gain, tables):
    B, L, _ = cq.shape
    q = (_rmsnorm(cq, q_norm_g) @ w_uq).reshape(B, L, MLA_HEADS, MLA_QK)
    q = _rmsnorm(q, q_gain)
    if tables is not None:
        q = jnp.concatenate([q[..., :MLA_NOPE], _axial_rope(q[..., MLA_NOPE:], tables)], axis=-1)
    return q


def _attend(q, k, v):
    s = jnp.einsum('bqhd,bkhd->bhqk', q, k).astype(jnp.float32) * (MLA_QK ** -0.5)
    p = jax.nn.softmax(s, axis=-1).astype(v.dtype)
    return jnp.einsum('bhqk,bkhd->bqhd', p, v)


def _attend_blocked(q, k, v):
    B, L, H, Dq = q.shape
    nb = L // Q_BLOCK
    qb = q.reshape(B, nb, Q_BLOCK, H, Dq).transpose(1, 0, 2, 3, 4)
    o = lax.map(lambda qi: _attend(qi, k, v), qb)
    return o.transpose(1, 0, 2, 3, 4).reshape(B, L, H * v.shape[-1])


def _complex_scan(a_re, a_im, b_re, b_im, h0_re, h0_im, reverse):
    def comb(e1, e2):
        a1r, a1i, b1r, b1i = e1
        a2r, a2i, b2r, b2i = e2
        return (a2r * a1r - a2i * a1i, a2r * a1i + a2i * a1r,
                a2r * b1r - a2i * b1i + b2r, a2r * b1i + a2i * b1r + b2i)
    a_re = jnp.broadcast_to(a_re, b_re.shape)
    a_im = jnp.broadcast_to(a_im, b_re.shape)
    ar, ai, br, bi = lax.associative_scan(comb, (a_re, a_im, b_re, b_im), reverse=reverse, axis=1)
    h0r, h0i = h0_re[:, None], h0_im[:, None]
    return ar * h0r - ai * h0i + br, ar * h0i + ai * h0r + bi


def _real_scan(a, b, h0, reverse):
    def comb(e1, e2):
        a1, b1 = e1
        a2, b2 = e2
        return a1 * a2, a2 * b1 + b2
    ac, bc = lax.associative_scan(comb, (a, b), reverse=reverse, axis=1)
    return ac * h0[:, None] + bc


def _s5_discretize(a_re, a_im, log_dt, b_re, b_im):
    f32 = jnp.float32
    a_re, a_im, b_re, b_im = a_re.astype(f32), a_im.astype(f32), b_re.astype(f32), b_im.astype(f32)
    dt = jnp.exp(log_dt.astype(f32))[:, None]
    mag = jnp.exp(a_re * dt)
    ab_re = mag * jnp.cos(a_im * dt)
    ab_im = mag * jnp.sin(a_im * dt)
    den = a_re * a_re + a_im * a_im
    f_re = ((ab_re - 1.0) * a_re + ab_im * a_im) / den
    f_im = (ab_im * a_re - (ab_re - 1.0) * a_im) / den
    bb_re = f_re[..., None] * b_re - f_im[..., None] * b_im
    bb_im = f_re[..., None] * b_im + f_im[..., None] * b_re
    return ab_re, ab_im, bb_re, bb_im


def _s5_states(ug, h0_re, h0_im, disc, reverse):
    ab_re, ab_im, bb_re, bb_im = disc
    bu_re = jnp.einsum('blgi,gpi->blgp', ug, bb_re)
    bu_im = jnp.einsum('blgi,gpi->blgp', ug, bb_im)
    return _complex_scan(ab_re, ab_im, bu_re, bu_im, h0_re, h0_im, reverse)


def _s5_readout(h_re, h_im, c_re, c_im):
    return jnp.einsum('gip,blgp->blgi', c_re, h_re) - jnp.einsum('gip,blgp->blgi', c_im, h_im)


def _s5_glu(y, u, d, w_glu):
    f32 = jnp.float32
    g = jax.nn.gelu(y + d.astype(f32) * u.astype(f32))
    return (g * jax.nn.sigmoid(g @ w_glu.astype(f32))).astype(u.dtype)


def _s5_branch(u, uc, a_re, a_im, log_dt, b_re, b_im, c_re, c_im, d, w_glu, with_ctx):
    f32 = jnp.float32
    B, L, W = u.shape
    Lc = uc.shape[1]
    ug = u.astype(f32).reshape(B, L, S5_GROUPS, S5_GROUP)
    ucg = uc.astype(f32).reshape(B, Lc, S5_GROUPS, S5_GROUP)
    zero = jnp.zeros((B, S5_GROUPS, S5_STATE), f32)
    y = 0.0
    yc = 0.0
    for dr in range(2):
        rev = dr == 1
        disc = _s5_discretize(a_re[dr], a_im[dr], log_dt[dr], b_re[dr], b_im[dr])
        hc_re, hc_im = _s5_states(ucg, zero, zero, disc, rev)
        last = 0 if rev else Lc - 1
        hl_re, hl_im = _s5_states(ug, hc_re[:, last], hc_im[:, last], disc, rev)
        y = y + _s5_readout(hl_re, hl_im, c_re[dr], c_im[dr])
        if with_ctx:
            yc = yc + _s5_readout(hc_re, hc_im, c_re[dr], c_im[dr])
    out = _s5_glu(y.reshape(B, L, W), u, d, w_glu)
    out_c = _s5_glu(yc.reshape(B, Lc, W), uc, d, w_glu) if with_ctx else None
    return out, out_c


def _short_conv(x, w, b):
    L = x.shape[1]
    left = LRU_CONV // 2
    xp = jnp.pad(x, ((0, 0), (left, LRU_CONV - 1 - left), (0, 0)))
    out = b
    for k in range(LRU_CONV):
        out = out + xp[:, k:k + L] * w[k]
    return out


def _block_diag(x, w, b):
    B, L, W = x.shape
    y = jnp.einsum('blnj,njk->blnk', x.reshape(B, L, LRU_BLOCKS, LRU_BLOCK_W), w)
    return y.reshape(B, L, W) + b


def _rglru_states(x, h0, lam, w_a, b_a, w_x, b_x, reverse):
    f32 = jnp.float32
    r = jax.nn.sigmoid(_block_diag(x, w_a, b_a).astype(f32))
    i = jax.nn.sigmoid(_block_diag(x, w_x, b_x).astype(f32))
    log_a = -LRU_C * r * jax.nn.softplus(-lam.astype(f32))
    a = jnp.exp(log_a)
    b = jnp.sqrt(-jnp.expm1(2.0 * log_a)) * (i * x.astype(f32))
    return _real_scan(a, b, h0, reverse)


def _lru_branch(x, xc, conv_w, conv_b, lam, w_a, b_a, w_x, b_x, with_ctx):
    xl = _short_conv(x, conv_w, conv_b)
    xcc = _short_conv(xc, conv_w, conv_b)
    B, Lc, W = xc.shape
    zero = jnp.zeros((B, W), jnp.float32)
    y = 0.0
    yc = 0.0
    for dr in range(2):
        rev = dr == 1
        hc = _rglru_states(xcc, zero, lam[dr], w_a[dr], b_a[dr], w_x[dr], b_x[dr], rev)
        last = 0 if rev else Lc - 1
        y = y + _rglru_states(xl, hc[:, last], lam[dr], w_a[dr], b_a[dr], w_x[dr], b_x[dr], rev)
        if with_ctx:
            yc = yc + hc
    return y.astype(x.dtype), (yc.astype(xc.dtype) if with_ctx else None)


def _pool_mix(x, w, b, scale):
    f32 = jnp.float32
    B, L, W = x.shape
    x32 = x.astype(f32)
    cs = jnp.concatenate([jnp.zeros((B, 1, W), f32), jnp.cumsum(x32, axis=1)], axis=1)
    t = jnp.arange(L, dtype=jnp.int32)
    parts = []
    for gi, win in enumerate(POOL_WINDOWS):
        sl = slice(gi * POOL_GROUP_W, (gi + 1) * POOL_GROUP_W)
        lo = jnp.clip(t - win // 2, 0, L)
        hi = jnp.clip(t + win // 2, 0, L)
        csg = cs[..., sl]
        s = jnp.take(csg, hi, axis=1) - jnp.take(csg, lo, axis=1)
        cnt = (hi - lo).astype(f32)[None, :, None]
        parts.append(s / cnt - x32[..., sl])
    p = jnp.stack(parts, axis=2)
    y = jnp.einsum('blgi,gio->blgo', p, w).reshape(B, L, W) + b
    return (y * scale).astype(x.dtype)


def _merge(branches, gate_paths, merge_logits, w_branch, w_out):
    gp = jnp.split(gate_paths, N_BRANCH, axis=-1)
    ml = jnp.split(merge_logits, N_BRANCH, axis=-1)
    y = 0.0
    for n in range(N_BRANCH):
        y = y + jax.nn.sigmoid(ml[n]) * ((branches[n] * jax.nn.silu(gp[n])) @ w_branch[n])
    return y @ w_out


def setup_inputs(seed: int = 0) -> dict:
    key = jax.random.key(seed)
    ks = iter(jax.random.split(key, 48))
    f32 = jnp.float32

    def nrm(shape, s):
        return jax.random.normal(next(ks), shape, f32) * s

    W, G, P = BRANCH_W, S5_GROUPS, S5_STATE
    x = nrm((BATCH, SEQ, D_MODEL), 1.0)
    c = nrm((BATCH, D_MODEL), 1.0)
    ctx = nrm((BATCH, CTX_LEN, D_MODEL), 1.0)
    c_ctx = nrm((D_MODEL,), 1.0)
    w_ada = nrm((DEPTH, D_MODEL, 3 * D_MODEL), 0.5 * D_MODEL ** -0.5)
    b_ada = nrm((DEPTH, 3 * D_MODEL), 0.01)
    norm_g = 1.0 + nrm((DEPTH, D_MODEL), 0.05)
    w_in = nrm((DEPTH, D_MODEL, IN_W), D_MODEL ** -0.5)
    mla_q_norm = 1.0 + nrm((DEPTH, Q_LORA), 0.05)
    mla_kv_norm = 1.0 + nrm((DEPTH, KV_LORA), 0.05)
    mla_w_uq = nrm((DEPTH, Q_LORA, MLA_HEADS * MLA_QK), Q_LORA ** -0.5)
    mla_w_ukv = nrm((DEPTH, KV_LORA, MLA_HEADS * (MLA_NOPE + MLA_V)), KV_LORA ** -0.5)
    mla_q_gain = 1.0 + nrm((DEPTH, MLA_QK), 0.05)
    mla_k_gain = 1.0 + nrm((DEPTH, MLA_QK), 0.05)
    n_idx = jnp.arange(P, dtype=f32)
    s5_a_re = -0.5 + nrm((DEPTH, 2, G, P), 0.01)
    s5_a_im = math.pi * n_idx + nrm((DEPTH, 2, G, P), 0.01)
    s5_log_dt = jax.random.uniform(next(ks), (DEPTH, 2, G), f32, math.log(1e-3), math.log(1e-1))
    s5_b_re = nrm((DEPTH, 2, G, P, S5_GROUP), (2.0 * S5_GROUP) ** -0.5)
    s5_b_im = nrm((DEPTH, 2, G, P, S5_GROUP), (2.0 * S5_GROUP) ** -0.5)
    s5_c_re = nrm((DEPTH, 2, G, S5_GROUP, P), P ** -0.5)
    s5_c_im = nrm((DEPTH, 2, G, S5_GROUP, P), P ** -0.5)
    s5_d = nrm((DEPTH, W), 1.0)
    s5_w_glu = nrm((DEPTH, W, W), W ** -0.5)
    lru_conv_w = nrm((DEPTH, LRU_CONV, W), 0.5)
    lru_conv_b = nrm((DEPTH, W), 0.01)
    a0 = jax.random.uniform(next(ks), (DEPTH, 2, W), f32, 0.9, 0.999)
    s0 = a0 ** (1.0 / LRU_C)
    lru_lambda = jnp.log(s0) - jnp.log1p(-s0)
    lru_w_a = nrm((DEPTH, 2, LRU_BLOCKS, LRU_BLOCK_W, LRU_BLOCK_W), LRU_BLOCK_W ** -0.5)
    lru_b_a = nrm((DEPTH, 2, W), 0.01)
    lru_w_x = nrm((DEPTH, 2, LRU_BLOCKS, LRU_BLOCK_W, LRU_BLOCK_W), LRU_BLOCK_W ** -0.5)
    lru_b_x = nrm((DEPTH, 2, W), 0.01)
    pool_w = nrm((DEPTH, len(POOL_WINDOWS), POOL_GROUP_W, POOL_GROUP_W), POOL_GROUP_W ** -0.5)
    pool_b = nrm((DEPTH, W), 0.01)
    pool_scale = 1.0 + nrm((DEPTH, W), 0.05)
    w_branch = nrm((DEPTH, N_BRANCH, W, D_MODEL), W ** -0.5)
    w_out = nrm((DEPTH, D_MODEL, D_MODEL), D_MODEL ** -0.5)
    return {'x': x, 'c': c, 'ctx': ctx, 'c_ctx': c_ctx,
            'w_ada': w_ada, 'b_ada': b_ada, 'norm_g': norm_g, 'w_in': w_in,
            'mla_q_norm': mla_q_norm, 'mla_kv_norm': mla_kv_norm, 'mla_w_uq': mla_w_uq,
            'mla_w_ukv': mla_w_ukv, 'mla_q_gain': mla_q_gain, 'mla_k_gain': mla_k_gain,
            's5_a_re': s5_a_re, 's5_a_im': s5_a_im, 's5_log_dt': s5_log_dt,
            's5_b_re': s5_b_re, 's5_b_im': s5_b_im, 's5_c_re': s5_c_re, 's5_c_im': s5_c_im,
            's5_d': s5_d, 's5_w_glu': s5_w_glu,
            'lru_conv_w': lru_conv_w, 'lru_conv_b': lru_conv_b, 'lru_lambda': lru_lambda,
            'lru_w_a': lru_w_a, 'lru_b_a': lru_b_a, 'lru_w_x': lru_w_x, 'lru_b_x': lru_b_x,
            'pool_w': pool_w, 'pool_b': pool_b, 'pool_scale': pool_scale,
            'w_branch': w_branch, 'w_out': w_out}


def reference(x, c, ctx, c_ctx, w_ada, b_ada, norm_g, w_in,
              mla_q_norm, mla_kv_norm, mla_w_uq, mla_w_ukv, mla_q_gain, mla_k_gain,
              s5_a_re, s5_a_im, s5_log_dt, s5_b_re, s5_b_im, s5_c_re, s5_c_im, s5_d, s5_w_glu,
              lru_conv_w, lru_conv_b, lru_lambda, lru_w_a, lru_b_a, lru_w_x, lru_b_x,
              pool_w, pool_b, pool_scale, w_branch, w_out):
    B, L, _ = x.shape
    Lc = ctx.shape[1]
    c_act = jax.nn.silu(c)
    cctx_act = jax.nn.silu(c_ctx)
    tables = _rope_tables(L)
    xc = ctx
    for l in range(DEPTH):
        with_ctx = l < DEPTH - 1
        shift, scale, gate = jnp.split(c_act @ w_ada[l] + b_ada[l], 3, axis=-1)
        shift_c, scale_c, gate_c = jnp.split(cctx_act @ w_ada[l] + b_ada[l], 3, axis=-1)
        h = _rmsnorm(x, norm_g[l]) * (1.0 + scale[:, None]) + shift[:, None]
        hc = _rmsnorm(xc, norm_g[l]) * (1.0 + scale_c) + shift_c
        ckv, krope, u_s5, x_lru, cq, x_pool, gpath, mlogit = jnp.split(h @ w_in[l], IN_SPLITS, axis=-1)
        if with_ctx:
            ckv_c, krope_c, u_s5_c, x_lru_c, cq_c, x_pool_c, gpath_c, mlogit_c = jnp.split(
                hc @ w_in[l], IN_SPLITS, axis=-1)
        else:
            ckv_c, krope_c, u_s5_c, x_lru_c = jnp.split(hc @ w_in[l, :, :MEM_W], MEM_SPLITS, axis=-1)

        k_l, v_l = _mla_kv(ckv, krope, mla_kv_norm[l], mla_w_ukv[l], mla_k_gain[l], tables)
        k_c, v_c = _mla_kv(ckv_c, krope_c, mla_kv_norm[l], mla_w_ukv[l], mla_k_gain[l], None)
        q_l = _mla_q(cq, mla_q_norm[l], mla_w_uq[l], mla_q_gain[l], tables)
        o_mla = _attend_blocked(q_l, jnp.concatenate([k_c, k_l], axis=1), jnp.concatenate([v_c, v_l], axis=1))

        o_s5, o_s5_c = _s5_branch(u_s5, u_s5_c, s5_a_re[l], s5_a_im[l], s5_log_dt[l], s5_b_re[l], s5_b_im[l],
                                  s5_c_re[l], s5_c_im[l], s5_d[l], s5_w_glu[l], with_ctx)
        o_lru, o_lru_c = _lru_branch(x_lru, x_lru_c, lru_conv_w[l], lru_conv_b[l], lru_lambda[l],
                                     lru_w_a[l], lru_b_a[l], lru_w_x[l], lru_b_x[l], with_ctx)
        o_pool = _pool_mix(x_pool, pool_w[l], pool_b[l], pool_scale[l])

        y = _merge((o_mla, o_s5, o_lru, o_pool), gpath, mlogit, w_branch[l], w_out[l])
        if with_ctx:
            q_c = _mla_q(cq_c, mla_q_norm[l], mla_w_uq[l], mla_q_gain[l], None)
            o_mla_c = _attend(q_c, k_c, v_c).reshape(B, Lc, MLA_HEADS * MLA_V)
            o_pool_c = _pool_mix(x_pool_c, pool_w[l], pool_b[l], pool_scale[l])
            y_c = _merge((o_mla_c, o_s5_c, o_lru_c, o_pool_c), gpath_c, mlogit_c, w_branch[l], w_out[l])
            xc = xc + gate_c * y_c
        x = x + gate[:, None] * y
    return x
```

```python
import math
import contextlib
import numpy as np
import concourse.bass as bass
import concourse.mybir as mybir
from concourse.bass_utils import run_bass_kernel_spmd

F32 = mybir.dt.float32
BF16 = mybir.dt.bfloat16
I32 = mybir.dt.int32
AF = mybir.ActivationFunctionType
ALU = mybir.AluOpType
AX = mybir.AxisListType

ENGS = ("pe", "act", "dve", "pool", "sp")
NDMA = 24

D = 1024
KC = 8
T = 2048
TC = 256
NPOS = T + TC
DEPTH = 2
O_KROPE, O_S5, O_LRU, O_CQ, O_POOL, O_GATE, O_MERGE, IN_W = 128, 160, 416, 672, 928, 1184, 2208, 6304
EPS = 1e-6
NCH = NPOS // 8


class Buf:
    __slots__ = ("name", "w", "r")

    def __init__(self, name):
        self.name = name
        self.w = None
        self.r = []


class TV:
    def __init__(self, ap, buf):
        self.ap, self.buf = ap, buf

    def __getitem__(self, k):
        return TV(self.ap[k], self.buf)

    def re(self, s, **kw):
        return TV(self.ap.rearrange(s, **kw), self.buf)

    def bc(self, shape):
        return TV(self.ap.to_broadcast(list(shape)), self.buf)

    def us(self, ax):
        return TV(self.ap.unsqueeze(ax), self.buf)

    def pbc(self, n):
        return TV(self.ap.partition_broadcast(n), self.buf)

    @property
    def shape(self):
        return tuple(self.ap.shape)


def _bufs(*xs):
    out = []
    for x in xs:
        if isinstance(x, TV):
            out.append(x.buf)
        elif isinstance(x, (list, tuple)):
            out.extend(_bufs(*x))
    return out


def _ap(x):
    return x.ap if isinstance(x, TV) else x


class Prog:
    def __init__(self, nc):
        self.nc = nc
        self.ops = {e: [] for e in ENGS}
        self.cnt = {e: 0 for e in ENGS}
        self.waited = {e: {} for e in ENGS}
        self.dma_i = 0
        self.dma_last = [0] * NDMA

    def _deps(self, eng, reads, writes):
        toks = []
        for b in reads:
            if b.w is not None:
                toks.append(b.w)
        for b in writes:
            if b.w is not None:
                toks.append(b.w)
            toks.extend(b.r)
        need = {}
        for (k, v, e) in toks:
            if e == "pe" and eng == "pe":
                continue
            if self.waited[eng].get(k, 0) >= v:
                continue
            if need.get(k, 0) < v:
                need[k] = v
        for k, v in need.items():
            self.waited[eng][k] = v
        return list(need.items())

    def _mark(self, tok, reads, writes):
        for b in writes:
            b.w = tok
            b.r = []
        for b in reads:
            if b in writes:
                continue
            b.r.append(tok)
            if len(b.r) > 16:
                best = {}
                for (k, v, e) in b.r:
                    if k not in best or best[k][1] < v:
                        best[k] = (k, v, e)
                b.r = list(best.values())

    def op(self, eng, fn, reads=(), writes=()):
        reads = list(dict.fromkeys(reads))
        writes = list(dict.fromkeys(writes))
        waits = self._deps(eng, reads, writes)
        self.cnt[eng] += 1
        tok = (eng, self.cnt[eng], eng)
        self.ops[eng].append((waits, fn, (eng, 1)))
        self._mark(tok, reads, writes)

    def dma(self, eng, fn, reads=(), writes=()):
        reads = list(dict.fromkeys(reads))
        writes = list(dict.fromkeys(writes))
        waits = self._deps(eng, reads, writes)
        slot = self.dma_i % NDMA
        self.dma_i += 1
        key = ("dma", slot)
        prev = self.dma_last[slot]
        if prev and self.waited[eng].get(key, 0) < prev:
            waits.append((key, prev))
            self.waited[eng][key] = prev
        val = prev + 16
        self.dma_last[slot] = val
        tok = (key, val, "dma")
        self.ops[eng].append((waits, fn, (key, 16)))
        self._mark(tok, reads, writes)

    def barrier(self):
        for E in ENGS:
            waits = []
            for e in ENGS:
                v = self.cnt[e]
                if v and self.waited[E].get(e, 0) < v:
                    waits.append((e, v))
                    self.waited[E][e] = v
            for slot in range(NDMA):
                v = self.dma_last[slot]
                key = ("dma", slot)
                if v and self.waited[E].get(key, 0) < v:
                    waits.append((key, v))
                    self.waited[E][key] = v
            if waits:
                self.ops[E].append((waits, None, None))

    def emit(self):
        nc = self.nc
        sems = {}
        with contextlib.ExitStack() as st:
            for e in ENGS:
                sems[e] = st.enter_context(nc.semaphore("s_" + e))
            for i in range(NDMA):
                sems[("dma", i)] = st.enter_context(nc.semaphore("s_dma%d" % i))
            block = st.enter_context(nc.Block())

            def run(engname):
                def body(eng):
                    for waits, fn, inc in self.ops[engname]:
                        for k, v in waits:
                            eng.wait_ge(sems[k], v)
                        if fn is None:
                            continue
                        ins = fn(eng)
                        ins.then_inc(sems[inc[0]], inc[1])
                return body
            block.tensor(run("pe"))
            block.scalar(run("act"))
            block.vector(run("dve"))
            block.gpsimd(run("pool"))
            block.sync(run("sp"))


class KB:
    def __init__(self, nc, nb, layers, dbg):
        self.nc = nc
        self.P = Prog(nc)
        self.nb = nb
        self.layers = layers
        self.dbg = dbg
        self.st = contextlib.ExitStack()
        self.din = {}
        self.dout = {}
        self.ps_i = 0
        self.psb_i = 0

    def tt(self, eng, out, a, b, op):
        self.P.op(eng, lambda e: e.tensor_tensor(out=out.ap, in0=a.ap, in1=b.ap, op=op), _bufs(a, b), _bufs(out))

    def ts(self, eng, out, a, s1, op0, s2=None, op1=None):
        if op1 is None:
            self.P.op(eng, lambda e: e.tensor_scalar(out=out.ap, in0=a.ap, scalar1=_ap(s1), scalar2=None, op0=op0),
                      _bufs(a, s1), _bufs(out))
        else:
            self.P.op(eng, lambda e: e.tensor_scalar(out=out.ap, in0=a.ap, scalar1=_ap(s1), scalar2=_ap(s2), op0=op0, op1=op1),
                      _bufs(a, s1, s2), _bufs(out))

    def stt(self, out, a, s, b, op0, op1):
        self.P.op("dve", lambda e: e.scalar_tensor_tensor(out=out.ap, in0=a.ap, scalar=_ap(s), in1=b.ap, op0=op0, op1=op1),
                  _bufs(a, s, b), _bufs(out))

    def act(self, out, a, func, bias=0.0, scale=1.0, accum=None):
        if accum is None:
            self.P.op("act", lambda e: e.activation(out=out.ap, in_=a.ap, func=func, bias=_ap(bias), scale=_ap(scale)),
                      _bufs(a, bias, scale), _bufs(out))
        else:
            self.P.op("act", lambda e: e.activation(out=out.ap, in_=a.ap, func=func, bias=_ap(bias), scale=_ap(scale), accum_out=accum.ap),
                      _bufs(a, bias, scale), _bufs(out, accum))

    def cp(self, eng, out, a):
        if eng == "act":
            self.P.op("act", lambda e: e.copy(out=out.ap, in_=a.ap), _bufs(a), _bufs(out))
        else:
            self.P.op(eng, lambda e: e.tensor_copy(out=out.ap, in_=a.ap), _bufs(a), _bufs(out))

    def memset(self, eng, out, v):
        self.P.op(eng, lambda e: e.memset(out.ap, v), [], _bufs(out))

    def recip(self, out, a):
        self.P.op("dve", lambda e: e.reciprocal(out=out.ap, in_=a.ap), _bufs(a), _bufs(out))

    def mm(self, out, lhsT, rhs, start, stop):
        self.P.op("pe", lambda e: e.matmul(out.ap, lhsT=lhsT.ap, rhs=rhs.ap, start=start, stop=stop), _bufs(lhsT, rhs), _bufs(out))

    def tr(self, out, a, ident):
        self.P.op("pe", lambda e: e.transpose(out.ap, a.ap, ident.ap), _bufs(a, ident), _bufs(out))

    def scan(self, out, d0, d1, init):
        self.P.op("dve", lambda e: e.tensor_tensor_scan(out=out.ap, data0=d0.ap, data1=d1.ap, initial=_ap(init), op0=ALU.mult, op1=ALU.add),
                  _bufs(d0, d1, init), _bufs(out))

    def reduce(self, out, a, op=ALU.add):
        self.P.op("dve", lambda e: e.tensor_reduce(out=out.ap, in_=a.ap, axis=AX.X, op=op), _bufs(a), _bufs(out))

    def dma(self, q, out, a, slow=False):
        if slow:
            self.P.dma(q, lambda e: e.dma_start(out=out.ap, in_=a.ap, allow_slow_non_contiguous=True), _bufs(a), _bufs(out))
        else:
            self.P.dma(q, lambda e: e.dma_start(out=out.ap, in_=a.ap), _bufs(a), _bufs(out))

    def dram_in(self, name, shape, dt=F32):
        t = TV(self.nc.dram_tensor(name, list(shape), dt, kind="ExternalInput").ap(), Buf(name))
        self.din[name] = t
        return t

    def dram_out(self, name, shape, dt=F32):
        t = TV(self.nc.dram_tensor(name, list(shape), dt, kind="ExternalOutput").ap(), Buf(name))
        self.dout[name] = t
        return t

    def dram_tmp(self, name, shape, dt=F32):
        return TV(self.nc.dram_tensor(name, list(shape), dt, kind="Internal").ap(), Buf(name))

    def sb(self, name, shape, dt=F32):
        t = self.st.enter_context(self.nc.sbuf_tensor(name, list(shape), dt))
        return TV(t[:], Buf(name))

    def psum_f(self):
        i = self.ps_i % 6
        self.ps_i += 1
        return self.psf[i]

    def psum_b(self):
        i = self.psb_i % 2
        self.psb_i += 1
        return self.psb[i]


class Arena:
    def __init__(self, kb, words, ap=None):
        self.kb = kb
        self.words = words
        if ap is None:
            self.f = kb.st.enter_context(kb.nc.sbuf_tensor("arena_f", [128, words], F32))
        else:
            self.f = ap
        self.lo = 0
        self.hi = words

    def alloc(self, name, free, dt=F32, top=False, buf=None, at=None):
        nel = int(np.prod(free))
        w = nel if dt == F32 else (nel + 1) // 2
        w = ((w + 15) // 16) * 16
        if at is not None:
            off = at
        elif top:
            self.hi -= w
            off = self.hi
        else:
            off = self.lo
            self.lo += w
        assert self.lo <= self.hi, (name, self.lo, self.hi)
        assert off + w <= self.words
        if dt != F32:
            ap = self.f[:, off:off + w].bitcast(dt)[:, 0:nel]
        else:
            ap = self.f[:, off:off + nel]
        if len(free) > 1:
            names = " ".join("d%d" % i for i in range(len(free)))
            kw = {"d%d" % i: int(free[i]) for i in range(len(free))}
            ap = ap.rearrange("p (%s) -> p %s" % (names, names), **kw)
        tv = TV(ap, buf if buf is not None else Buf(name))
        tv.off = off
        tv.w = w
        return tv

    def reset(self, top=False):
        self.kb.P.barrier()
        self.lo = 0
        if top:
            self.hi = self.words


def lockstep(gens):
    gens = list(gens)
    while gens:
        nxt = []
        for g in gens:
            try:
                next(g)
                nxt.append(g)
            except StopIteration:
                pass
        gens = nxt


def build_program(nb=2, layers=(0, 1), dbg=None):
    nc = bass.Bass("TRN2", target_bir_lowering=False)
    kb = KB(nc, nb, layers, dbg)
    P = kb.P
    tt, ts, stt, act, cp, mm, tr, dma = kb.tt, kb.ts, kb.stt, kb.act, kb.cp, kb.mm, kb.tr, kb.dma
    dbg = dbg or set()

    x_in = kb.dram_in("x", [nb, T, D])
    ctx_in = kb.dram_in("ctx", [nb, TC, D])
    cvec = kb.dram_in("cvec", [3, D])
    w_ada = kb.dram_in("w_ada", [DEPTH, D, 3 * D])
    b_ada = kb.dram_in("b_ada", [DEPTH, 3 * D])
    norm_g = kb.dram_in("norm_g", [DEPTH, D])
    w_in = kb.dram_in("w_in", [DEPTH, D, IN_W])
    mla_q_norm = kb.dram_in("mla_q_norm", [DEPTH, 256])
    mla_kv_norm = kb.dram_in("mla_kv_norm", [DEPTH, 128])
    mla_w_uq = kb.dram_in("mla_w_uq", [DEPTH, 256, 384])
    mla_w_ukv = kb.dram_in("mla_w_ukv", [DEPTH, 128, 512])
    mla_q_gain = kb.dram_in("mla_q_gain", [DEPTH, 96])
    mla_k_gain = kb.dram_in("mla_k_gain", [DEPTH, 96])
    s5_a_re = kb.dram_in("s5_a_re", [DEPTH, 2, 16, 64])
    s5_a_im = kb.dram_in("s5_a_im", [DEPTH, 2, 16, 64])
    s5_log_dt = kb.dram_in("s5_log_dt", [DEPTH, 2, 16])
    s5_b_re = kb.dram_in("s5_b_re", [DEPTH, 2, 16, 64, 16])
    s5_b_im = kb.dram_in("s5_b_im", [DEPTH, 2, 16, 64, 16])
    s5_c_re = kb.dram_in("s5_c_re", [DEPTH, 2, 16, 16, 64])
    s5_c_im = kb.dram_in("s5_c_im", [DEPTH, 2, 16, 16, 64])
    s5_d = kb.dram_in("s5_d", [DEPTH, 256])
    s5_w_glu = kb.dram_in("s5_w_glu", [DEPTH, 256, 256])
    lru_conv_w = kb.dram_in("lru_conv_w", [DEPTH, 4, 256])
    lru_conv_b = kb.dram_in("lru_conv_b", [DEPTH, 256])
    lru_lambda = kb.dram_in("lru_lambda", [DEPTH, 2, 256])
    lru_w_a = kb.dram_in("lru_w_a", [DEPTH, 2, 4, 64, 64])
    lru_b_a = kb.dram_in("lru_b_a", [DEPTH, 2, 256])
    lru_w_x = kb.dram_in("lru_w_x", [DEPTH, 2, 4, 64, 64])
    lru_b_x = kb.dram_in("lru_b_x", [DEPTH, 2, 256])
    pool_w = kb.dram_in("pool_w", [DEPTH, 4, 64, 64])
    pool_b = kb.dram_in("pool_b", [DEPTH, 256])
    pool_scale = kb.dram_in("pool_scale", [DEPTH, 256])
    w_branch = kb.dram_in("w_branch", [DEPTH, 4, 256, D])
    w_out = kb.dram_in("w_out", [DEPTH, D, D])
    c_ident = kb.dram_in("c_ident", [128, 128])
    c_ropeC = kb.dram_in("c_ropeC", [T, 32])
    c_ropeS = kb.dram_in("c_ropeS", [T, 32])
    c_pool = kb.dram_in("c_pool", [128, 2 + 16 + 16])
    c_mask = kb.dram_in("c_mask", [128, 3, 128])
    y_out = kb.dram_out("y", [nb, T, D])
    xmid = kb.dram_tmp("xmid", [nb, T, D])
    xcmid = kb.dram_tmp("xcmid", [nb, TC, D])
    modD = kb.dram_tmp("modD", [DEPTH, 3, 3 * D])
    st_main = kb.dram_tmp("st_main", [3, 128, 2048])
    st_dir = kb.dram_tmp("st_dir", [2, 128, 2048 + 16 * NCH + 8])
    dbg_out = {}

    def tap(name, shape):
        if name not in dbg_out:
            dbg_out[name] = kb.dram_out("dbg_" + name, shape)
        return dbg_out[name]

    kb.psf = []
    for i in range(6):
        t = kb.st.enter_context(nc.psum_tensor("psf%d" % i, [128, 512], F32))
        kb.psf.append(TV(t[:], Buf("psf%d" % i)))
    kb.psb = []
    for i in range(2):
        t = kb.st.enter_context(nc.psum_tensor("psb%d" % i, [128, 1024], BF16))
        kb.psb.append(TV(t[:], Buf("psb%d" % i)))

    hT = kb.sb("hT", [128, KC, NPOS], BF16)
    bigbr = kb.sb("bigbr", [128, 8 * NPOS], BF16)
    _slot = {0: 0, 2: 2, 3: 4, 1: 6}
    brT = [[TV(bigbr.ap[:, (_slot[n] + c) * NPOS:(_slot[n] + c + 1) * NPOS], Buf("brT%d%d" % (n, c))) for c in range(2)]
           for n in range(4)]
    identf = kb.sb("identf", [128, 128])
    identb = kb.sb("identb", [128, 128], BF16)
    ropeC = kb.sb("ropeC", [128, 16, 32])
    ropeS = kb.sb("ropeS", [128, 16, 32])
    cpool = kb.sb("cpool", [128, 34])
    cmask = kb.sb("cmask", [128, 3, 128])
    modA = kb.sb("modA", [128, 3, KC])
    modS = kb.sb("modS", [128, 3, KC])
    lp = kb.sb("lp", [128, 64])
    lruBD = kb.sb("lruBD", [128, 2, 2, 2, 128])
    poolBD = kb.sb("poolBD", [128, 2, 128])
    wukv = kb.sb("wukv", [128, 512], BF16)
    wuq = kb.sb("wuq", [128, 2, 384], BF16)
    gains = kb.sb("gains", [128, 2, 96])
    wglu = kb.sb("wglu", [128, 2, 256], BF16)
    AR = Arena(kb, 28000)
    AR2 = Arena(kb, 3 * NPOS, ap=bigbr.ap[:, 0:6 * NPOS].bitcast(F32))

    dma("sp", identf, c_ident)
    cp("dve", identb, identf)
    dma("sp", ropeC, c_ropeC.re("(i p) f -> p i f", p=128))
    dma("sp", ropeS, c_ropeS.re("(i p) f -> p i f", p=128))
    dma("sp", cpool, c_pool)
    dma("sp", cmask, c_mask)

    LP_CW = 0
    LP_CB = 8
    LP_NSP = 10
    LP_NSP2 = 14
    LP_BA = 18
    LP_BX = 22
    LP_PSC = 26
    LP_PBS = 28
    LP_G = 30
    LP_TMP = 40

    def prep_layer(l):
        AR.reset(top=True)
        cT = AR.alloc("cT", [KC, 3])
        for v in range(3):
            dma("sp", cT[:, :, v], cvec[v].re("(c p) -> p c", p=128), slow=True)
        cact = AR.alloc("cact", [KC, 3])
        act(cact, cT, AF.Silu)
        brow = AR.alloc("brow", [3 * D])
        dma("sp", brow[0:3, :], b_ada[l:l + 1, :].re("o n -> (o n)").pbc(3))
        modrow = AR.alloc("modrow", [3 * D])
        wa = [AR.alloc("wa%d" % i, [KC, 512]) for i in range(2)]
        for cb in range(6):
            w = wa[cb % 2]
            dma("sp" if cb % 2 == 0 else "act", w, w_ada[l, :, cb * 512:(cb + 1) * 512].re("(c p) n -> p c n", p=128))
            ps = kb.psum_f()
            for k in range(KC):
                mm(ps[0:3, :], cact[:, k, :], w[:, k, :], k == 0, k == KC - 1)
            tt("dve", modrow[0:3, cb * 512:(cb + 1) * 512], ps[0:3, :], brow[0:3, cb * 512:(cb + 1) * 512], ALU.add)
        dma("sp", modD[l], modrow[0:3, :])
        sc = AR.alloc("sc", [3, KC])
        for v in range(3):
            dma("sp", modS[:, v, :], modD[l, v, 0:D].re("(c p) -> p c", p=128), slow=True)
            dma("sp", sc[:, v, :], modD[l, v, D:2 * D].re("(c p) -> p c", p=128), slow=True)
        dma("sp", lp[:, LP_G:LP_G + 8], norm_g[l].re("(c p) -> p c", p=128), slow=True)
        for v in range(3):
            stt(modA[:, v, :], sc[:, v, :], 1.0, lp[:, LP_G:LP_G + 8], ALU.add, ALU.mult)
        for k in range(4):
            dma("sp", lp[:, LP_CW:LP_CW + 8].re("p (c k) -> p c k", c=2)[:, :, k], lru_conv_w[l, k].re("(c p) -> p c", p=128), slow=True)
        dma("sp", lp[:, LP_CB:LP_CB + 2], lru_conv_b[l].re("(c p) -> p c", p=128), slow=True)
        lam = lp[:, LP_TMP:LP_TMP + 4]
        for d in range(2):
            dma("sp", lam[:, d * 2:d * 2 + 2], lru_lambda[l, d].re("(c p) -> p c", p=128), slow=True)
            dma("sp", lp[:, LP_BA + d * 2:LP_BA + d * 2 + 2], lru_b_a[l, d].re("(c p) -> p c", p=128), slow=True)
            dma("sp", lp[:, LP_BX + d * 2:LP_BX + d * 2 + 2], lru_b_x[l, d].re("(c p) -> p c", p=128), slow=True)
        t0 = lp[:, LP_TMP + 4:LP_TMP + 8]
        t1 = lp[:, LP_TMP + 8:LP_TMP + 12]
        t2 = lp[:, LP_TMP + 12:LP_TMP + 16]
        t3 = lp[:, LP_TMP + 16:LP_TMP + 20]
        ts("dve", t0, lam, -1.0, ALU.mult)
        tt("dve", t0, t0, lam, ALU.max)
        act(t1, t0, AF.Exp, scale=-1.0)
        ts("dve", t2, t1, 2.0, ALU.add)
        kb.recip(t2, t2)
        tt("dve", t2, t2, t1, ALU.mult)
        tt("dve", t3, t2, t2, ALU.mult)
        ts("dve", t0, t3, 1.0 / 11.0, ALU.mult, 1.0 / 9.0, ALU.add)
        for cf in (1.0 / 7.0, 1.0 / 5.0, 1.0 / 3.0, 1.0):
            tt("dve", t0, t0, t3, ALU.mult)
            ts("dve", t0, t0, cf, ALU.add)
        tt("dve", t0, t0, t2, ALU.mult)
        ts("dve", t1, lam, -1.0, ALU.mult, 0.0, ALU.max)
        stt(t0, t0, 2.0, t1, ALU.mult, ALU.add)
        ts("dve", lp[:, LP_NSP:LP_NSP + 4], t0, -8.0, ALU.mult)
        ts("dve", lp[:, LP_NSP2:LP_NSP2 + 4], t0, -16.0, ALU.mult)
        kb.memset("dve", lruBD, 0.0)
        for d in range(2):
            for gi, wsrc in enumerate((lru_w_a, lru_w_x)):
                for c in range(2):
                    for h in range(2):
                        dma("sp", lruBD[h * 64:(h + 1) * 64, d, gi, c, h * 64:(h + 1) * 64], wsrc[l, d, 2 * c + h])
        kb.memset("dve", poolBD, 0.0)
        for c in range(2):
            for h in range(2):
                dma("sp", poolBD[h * 64:(h + 1) * 64, c, h * 64:(h + 1) * 64], pool_w[l, 2 * c + h])
        dma("sp", lp[:, LP_PSC:LP_PSC + 2], pool_scale[l].re("(c p) -> p c", p=128), slow=True)
        dma("sp", lp[:, LP_PBS:LP_PBS + 2], pool_b[l].re("(c p) -> p c", p=128), slow=True)
        tt("dve", lp[:, LP_PBS:LP_PBS + 2], lp[:, LP_PBS:LP_PBS + 2], lp[:, LP_PSC:LP_PSC + 2], ALU.mult)
        kvn = lp[:, LP_TMP + 20:LP_TMP + 21]
        qn = lp[:, LP_TMP + 21:LP_TMP + 23]
        dma("sp", kvn, mla_kv_norm[l].re("(p o) -> p o", o=1), slow=True)
        dma("sp", qn, mla_q_norm[l].re("(c p) -> p c", p=128), slow=True)
        wtmp = AR.alloc("wtmp", [2, 512])
        dma("sp", wtmp[:, 0, :], mla_w_ukv[l])
        ts("dve", wukv, wtmp[:, 0, :], kvn, ALU.mult)
        wtmp2 = AR.alloc("wtmp2", [2, 384])
        dma("sp", wtmp2, mla_w_uq[l].re("(c p) n -> p c n", p=128))
        for c in range(2):
            ts("dve", wuq[:, c, :], wtmp2[:, c, :], qn[:, c:c + 1], ALU.mult)
        dma("sp", gains[:, 0, :], mla_q_gain[l:l + 1, :].re("o n -> (o n)").pbc(128))
        dma("sp", gains[:, 1, :], mla_k_gain[l:l + 1, :].re("o n -> (o n)").pbc(128))
        ts("dve", gains[:, 0, :], gains[:, 0, :], 96.0 ** -0.5, ALU.mult)
        dma("pool", wglu, s5_w_glu[l].re("(c p) n -> p c n", p=128))

    def src_tile(l, b, i):
        if i < 2:
            return (ctx_in if l == 0 else xcmid)[b, i * 128:(i + 1) * 128, :]
        return (x_in if l == 0 else xmid)[b, (i - 2) * 128:(i - 1) * 128, :]

    def phase_norm(l, b):
        AR.reset(top=True)
        NX = 4
        xt = [AR.alloc("xt%d" % i, [D]) for i in range(NX)]
        junk = [AR.alloc("junk%d" % i, [D]) for i in range(2)]
        xn = [AR.alloc("xn%d" % i, [D], BF16) for i in range(NX)]
        st4 = [AR.alloc("st%d" % i, [4]) for i in range(NX)]
        def tile_gen(i):
            v = 2 if i < 2 else b
            x_t, xn_t, s4 = xt[i % NX], xn[i % NX], st4[i % NX]
            dma("sp", x_t, src_tile(l, b, i))
            yield
            act(junk[i % 2], x_t, AF.Square, accum=s4[:, 0:1])
            yield
            ts("dve", s4[:, 1:2], s4[:, 0:1], 1.0 / D, ALU.mult, EPS, ALU.add)
            yield
            act(s4[:, 2:3], s4[:, 1:2], AF.Sqrt)
            yield
            kb.recip(s4[:, 3:4], s4[:, 2:3])
            ts("dve", xn_t, x_t, s4[:, 3:4], ALU.mult)
            yield
            ps = kb.psum_b()
            for c in range(KC):
                tr(ps[:, c * 128:(c + 1) * 128], xn_t[:, c * 128:(c + 1) * 128], identb)
            yield
            for c in range(KC):
                o = hT[:, c, i * 128:(i + 1) * 128]
                if c % 2 == 0:
                    act(o, ps[:, c * 128:(c + 1) * 128], AF.Identity, bias=modS[:, v, c:c + 1], scale=modA[:, v, c:c + 1])
                else:
                    ts("dve", o, ps[:, c * 128:(c + 1) * 128], modA[:, v, c:c + 1], ALU.mult, modS[:, v, c:c + 1], ALU.add)
            yield
        for g0 in range(0, 18, 2):
            lockstep([tile_gen(g0), tile_gen(g0 + 1)])

    def load_win(l, name, c0, ncols, top=False):
        w = AR.alloc(name, [KC, ncols], BF16, top=top)
        dma("pool", w, w_in[l, :, c0:c0 + ncols].re("(c p) n -> p c n", p=128))
        return w

    def proj_fm(w, col0, dst, p0, p1, evac):
        pos = p0
        while pos < p1:
            n = min(512, p1 - pos)
            ps = kb.psum_f()
            for k in range(KC):
                mm(ps[:, 0:n], w[:, k, col0:col0 + 128], hT[:, k, pos:pos + n], k == 0, k == KC - 1)
            evac(ps, pos, n)
            pos += n

    def phase_lru(l, b, with_ctx):
        AR.reset()
        w = load_win(l, "w_lru", O_LRU, 256)
        xr = AR.alloc("xr", [2, NPOS])
        xc_ = AR.alloc("xc", [2, NPOS])
        ysum = AR.alloc("ysum", [2, NPOS])
        NB_ = 768
        tmp_sets = [{nm: AR.alloc("%s%d" % (nm, q), [NB_]) for nm in ("r", "i", "a", "a2", "bb")} for q in range(2)]
        for c in range(2):
            proj_fm(w, c * 128, xr, 0, NPOS, lambda ps, pos, n, c=c: cp("act", xr[:, c, pos:pos + n], ps[:, 0:n]))
        for c in range(2):
            cw = lambda k: lp[:, LP_CW + c * 4 + k:LP_CW + c * 4 + k + 1]
            for (s0, s1) in ((0, TC), (TC, NPOS)):
                ts("dve", xc_[:, c, s0:s1], xr[:, c, s0:s1], cw(2), ALU.mult, lp[:, LP_CB + c:LP_CB + c + 1], ALU.add)
                stt(xc_[:, c, s0 + 1:s1], xr[:, c, s0:s1 - 1], cw(1), xc_[:, c, s0 + 1:s1], ALU.mult, ALU.add)
                stt(xc_[:, c, s0 + 2:s1], xr[:, c, s0:s1 - 2], cw(0), xc_[:, c, s0 + 2:s1], ALU.mult, ALU.add)
                stt(xc_[:, c, s0:s1 - 1], xr[:, c, s0 + 1:s1], cw(3), xc_[:, c, s0:s1 - 1], ALU.mult, ALU.add)
        blocks = [(0, TC)] + [(TC + j * NB_, min(TC + (j + 1) * NB_, NPOS)) for j in range((T + NB_ - 1) // NB_)]

        def chain(d, c):
            dc = d * 2 + c
            tmp = tmp_sets[d]
            out = ysum if d == 0 else xr
            order = blocks if d == 0 else [blocks[0]] + blocks[:0:-1]
            prev = None
            for (s0, s1) in order:
                n = s1 - s0
                r_, i_, a_, a2_, bb_ = (tmp[k][:, 0:n] for k in ("r", "i", "a", "a2", "bb"))
                for gi, dst in ((0, r_), (1, i_)):
                    q = 0
                    while q < n:
                        m = min(512, n - q)
                        ps = kb.psum_f()
                        mm(ps[:, 0:m], lruBD[:, d, gi, c, :], xc_[:, c, s0 + q:s0 + q + m], True, True)
                        bcol = (LP_BA if gi == 0 else LP_BX) + dc
                        act(dst[:, q:q + m], ps[:, 0:m], AF.Sigmoid, bias=lp[:, bcol:bcol + 1])
                        q += m
                yield
                act(a_, r_, AF.Exp, scale=lp[:, LP_NSP + dc:LP_NSP + dc + 1])
                act(a2_, r_, AF.Exp, scale=lp[:, LP_NSP2 + dc:LP_NSP2 + dc + 1])
                tt("pool", bb_, i_, xc_[:, c, s0:s1], ALU.mult)
                yield
                ts("dve", a2_, a2_, -1.0, ALU.mult, 1.0, ALU.add)
                yield
                act(a2_, a2_, AF.Sqrt)
                yield
                tt("dve", bb_, bb_, a2_, ALU.mult)
                init = 0.0 if prev is None else prev
                if d == 0:
                    kb.scan(out[:, c, s0:s1], a_, bb_, init)
                    prev = out[:, c, s1 - 1:s1]
                else:
                    kb.scan(out[:, c, s0:s1][:, ::-1], a_[:, ::-1], bb_[:, ::-1], init)
                    prev = out[:, c, s0:s0 + 1]
                yield
        for c in range(2):
            lockstep([chain(0, c), chain(1, c)])
        for c in range(2):
            lo = 0 if with_ctx else TC
            tt("dve", brT[2][c][:, lo:NPOS], ysum[:, c, lo:NPOS], xr[:, c, lo:NPOS], ALU.add)

    AR_car = [kb.sb("car%d" % i, [128, 1]) for i in range(2)]

    def phase_pool(l, b, with_ctx):
        AR.reset()
        w = load_win(l, "w_pool", O_POOL, 256)
        xp = AR.alloc("xp", [2, NPOS])
        cs = AR.alloc("cs", [2, NPOS + 2 * 17 + 2])
        pm = AR.alloc("pm", [2, NPOS])
        ones = AR.alloc("ones", [T])
        kb.memset("pool", ones, 1.0)
        for c in range(2):
            proj_fm(w, c * 128, xp, 0, NPOS, lambda ps, pos, n, c=c: cp("act", xp[:, c, pos:pos + n], ps[:, 0:n]))
        segs = [(0, TC, 0)] + [(TC, NPOS, TC + 17)]
        if not with_ctx:
            segs = segs[1:]
        for c in range(2):
            for (s0, s1, o0) in segs:
                L = s1 - s0
                kb.memset("pool", cs[:, c, o0:o0 + 9], 0.0)
                kb.scan(cs[:, c, o0 + 9:o0 + 9 + L], ones[:, 0:L], xp[:, c, s0:s1], 0.0)
                ts("dve", cs[:, c, o0 + 9 + L:o0 + 17 + L], cs[:, c, o0:o0 + 8], cs[:, c, o0 + 8 + L:o0 + 9 + L], ALU.add)
                for h in range(2):
                    hw = (1, 2, 4, 8)[2 * c + h]
                    rows = slice(h * 64, (h + 1) * 64)
                    base = o0 + 8
                    tt("dve", pm[rows, c, s0:s1], cs[rows, c, base + hw:base + hw + L], cs[rows, c, base - hw:base - hw + L], ALU.subtract)
                ts("dve", pm[:, c, s0:s1], pm[:, c, s0:s1], cpool[:, c:c + 1], ALU.mult)
                tt("dve", pm[:, c, s0:s0 + 8], pm[:, c, s0:s0 + 8], cpool[:, 2 + c * 8:2 + c * 8 + 8], ALU.mult)
                tt("dve", pm[:, c, s1 - 8:s1], pm[:, c, s1 - 8:s1], cpool[:, 18 + c * 8:18 + c * 8 + 8], ALU.mult)
                tt("pool", pm[:, c, s0:s1], pm[:, c, s0:s1], xp[:, c, s0:s1], ALU.subtract)
                pos = s0
                while pos < s1:
                    n = min(512, s1 - pos)
                    ps = kb.psum_f()
                    mm(ps[:, 0:n], poolBD[:, c, :], pm[:, c, pos:pos + n], True, True)
                    act(brT[3][c][:, pos:pos + n], ps[:, 0:n], AF.Identity, bias=lp[:, LP_PBS + c:LP_PBS + c + 1],
                        scale=lp[:, LP_PSC + c:LP_PSC + c + 1])
                    pos += n

    def phase_mla(l, b, with_ctx):
        AR.reset()
        wkv = load_win(l, "w_kv", 0, 160)
        wq = load_win(l, "w_cq", O_CQ, 256)
        qT = AR.alloc("qT", [4, NPOS], BF16)
        kT = AR.alloc("kT", [4, NPOS], BF16)
        Vt = AR.alloc("Vt", [18, 4, 68], BF16)
        omla = AR.alloc("omla", [18, 256], BF16)
        kb.memset("dve", Vt.re("p a b c -> p (a b c)"), 1.0)
        for h_ in range(4):
            kb.memset("pool", qT[:, h_, :], 0.0)
            kb.memset("pool", kT[:, h_, :], 0.0)
        NS = 2
        sm = [AR.alloc("sm%d" % i, [32]) for i in range(NS)]
        kr = [AR.alloc("kr%d" % i, [32]) for i in range(NS)]
        cn = [AR.alloc("cn%d" % i, [384], BF16) for i in range(NS)]
        cnT = [AR.alloc("cnT%d" % i, [3, 128], BF16) for i in range(NS)]
        sq = [AR.alloc("sq%d" % i, [704]) for i in range(NS)]
        qk = [AR.alloc("qk%d" % i, [2, 4, 96]) for i in range(NS)]
        rg = [AR.alloc("rg%d" % i, [2, 4, 96]) for i in range(NS)]
        qkb = [AR.alloc("qkb%d" % i, [2, 4, 96], BF16) for i in range(NS)]
        rt = [AR.alloc("rt%d" % i, [2, 2, 4, 32]) for i in range(NS)]
        kvs_ = [AR.alloc("kvs%d" % i, [512]) for i in range(NS)]
        qs_ = [AR.alloc("qs%d" % i, [384]) for i in range(NS)]

        def info(i):
            is_ctx = i < 2
            do_q = (not is_ctx) or with_ctx
            return is_ctx, do_q, i % NS, slice(i * 128, (i + 1) * 128)

        def stage_a(i):
            is_ctx, do_q, j, pos = info(i)
            s, cn_, cnT_, sq_ = sm[j], cn[j], cnT[j], sq[j]
            ps1 = kb.psum_f()
            for k in range(KC):
                mm(ps1[:, 0:160], hT[:, k, pos], wkv[:, k, :], k == 0, k == KC - 1)
            if do_q:
                for k in range(KC):
                    mm(ps1[:, 160:416], hT[:, k, pos], wq[:, k, :], k == 0, k == KC - 1)
            yield
            act(sq_[:, 0:128], ps1[:, 0:128], AF.Square, accum=s[:, 0:1])
            if do_q:
                act(sq_[:, 128:384], ps1[:, 160:416], AF.Square, accum=s[:, 1:2])
            else:
                kb.memset("pool", s[:, 1:2], 1.0)
            cp("act", kr[j], ps1[:, 128:160])
            yield
            act(s[:, 2:3], s[:, 0:1], AF.Sqrt, scale=1.0 / 128, bias=EPS)
            act(s[:, 3:4], s[:, 1:2], AF.Sqrt, scale=1.0 / 256, bias=EPS)
            yield
            kb.recip(s[:, 4:6], s[:, 2:4])
            ts("dve", cn_[:, 0:128], ps1[:, 0:128], s[:, 4:5], ALU.mult)
            if do_q:
                ts("dve", cn_[:, 128:384], ps1[:, 160:416], s[:, 5:6], ALU.mult)
            yield
            psT = kb.psum_b()
            for c in range(3 if do_q else 1):
                tr(psT[:, c * 128:(c + 1) * 128], cn_[:, c * 128:(c + 1) * 128], identb)
            yield
            cp("act", cnT_[:, 0:(3 if do_q else 1), :], psT[:, 0:(384 if do_q else 128)].re("p (c n) -> p c n", n=128))
            yield

        def stage_b(i):
            is_ctx, do_q, j, pos = info(i)
            s, cnT_, sq_, qk_, qkb_, rt_, rg_ = sm[j], cnT[j], sq[j], qk[j], qkb[j], rt[j], rg[j]
            pskv = kb.psum_f()
            mm(pskv, cnT_[:, 0, :], wukv, True, True)
            if do_q:
                psq = kb.psum_f()
                for c in range(2):
                    mm(psq[:, 0:384], cnT_[:, 1 + c, :], wuq[:, c, :], c == 0, c == 1)
            yield
            cp("act", kvs_[j], pskv)
            if do_q:
                cp("act", qs_[j], psq[:, 0:384])
            kv3 = kvs_[j].re("p (h e) -> p h e", h=4)
            q3 = qs_[j].re("p (h e) -> p h e", h=4)
            krope = kr[j]
            act(sq_[:, 640:672], krope, AF.Square, accum=s[:, 7:8])
            cp("act", Vt[:, i, :, 0:64], kv3[:, :, 64:128])
            yield
            ksq = sq_[:, 0:256].re("p (h e) -> p h e", h=4)
            qsq = sq_[:, 256:640].re("p (h e) -> p h e", h=4)
            tt("pool", ksq, kv3[:, :, 0:64], kv3[:, :, 0:64], ALU.mult)
            if do_q:
                tt("pool", qsq, q3, q3, ALU.mult)
            yield
            kb.reduce(s[:, 12:16], ksq)
            if do_q:
                kb.reduce(s[:, 8:12], qsq)
            else:
                kb.memset("dve", s[:, 8:12], 1.0)
            ts("dve", s[:, 12:16], s[:, 12:16], s[:, 7:8], ALU.add)
            yield
            act(s[:, 8:16], s[:, 8:16], AF.Sqrt, scale=1.0 / 96, bias=EPS)
            yield
            kb.recip(s[:, 16:24], s[:, 8:16])
            yield
            tt("pool", rg_, s[:, 16:24].re("p (w h) -> p w h", w=2).us(3).bc([128, 2, 4, 96]),
               gains.us(2).bc([128, 2, 4, 96]), ALU.mult)
            yield
            if do_q:
                tt("dve", qk_[:, 0], q3, rg_[:, 0], ALU.mult)
            else:
                kb.memset("pool", qk_[:, 0], 0.0)
            tt("dve", qk_[:, 1, :, 0:64], kv3[:, :, 0:64], rg_[:, 1, :, 0:64], ALU.mult)
            tt("pool", qk_[:, 1, :, 64:96], krope.us(1).bc([128, 4, 32]), rg_[:, 1, :, 64:96], ALU.mult)
            yield
            if not is_ctx:
                ti = i - 2
                v = qk_[:, :, :, 64:96]
                t1_ = rt_[:, 0]
                t2_ = rt_[:, 1]
                Cb = ropeC[:, ti, :].us(1).us(1).bc([128, 2, 4, 32])
                tt("pool", t1_, v, Cb, ALU.mult)
                for a in range(2):
                    for s_ in range(2):
                        o_ = t2_[:, :, :, a * 16 + s_ * 8:a * 16 + s_ * 8 + 8]
                        i_ = qk_[:, :, :, 64 + a * 16 + (1 - s_) * 8:64 + a * 16 + (1 - s_) * 8 + 8]
                        Sb = ropeS[:, ti, a * 16 + s_ * 8:a * 16 + s_ * 8 + 8].us(1).us(1).bc([128, 2, 4, 8])
                        tt("dve" if (a + s_) % 2 == 0 else "pool", o_, i_, Sb, ALU.mult)
                yield
                tt("dve", v, t1_, t2_, ALU.add)
            cp("dve", qkb_, qk_)
            yield

        def stage_c(i):
            is_ctx, do_q, j, pos = info(i)
            qkb_ = qkb[j]
            psT2 = kb.psum_b()
            for w_ in range(2):
                if w_ == 0 and not do_q:
                    continue
                for h in range(4):
                    tr(psT2[0:96, (w_ * 4 + h) * 128:(w_ * 4 + h + 1) * 128], qkb_[:, w_, h, :], identb)
            yield
            if do_q:
                cp("act", qT[0:96, :, pos], psT2[0:96, 0:512].re("p (h n) -> p h n", h=4))
            cp("act", kT[0:96, :, pos], psT2[0:96, 512:1024].re("p (h n) -> p h n", h=4))

        def tile_gen(i):
            yield from stage_a(i)
            yield from stage_b(i)
            yield from stage_c(i)
        for g0 in range(0, 18, 2):
            lockstep([tile_gen(i) for i in range(g0, g0 + 2)])
        if "mla_stop1" in dbg:
            return
        pT = [AR.alloc("pT%d" % i, [512], BF16) for i in range(3)]
        rc = [AR.alloc("rc%d" % i, [4]) for i in range(2)]
        jobs = []
        if with_ctx:
            jobs.append((0, TC, 0, 2))
        for qb in range(4):
            jobs.append((TC + qb * 512, TC + (qb + 1) * 512, 0, 18))
        pi = 0
        for (q0, q1, kt0, kt1) in jobs:
            nq = q1 - q0
            nqt = nq // 128
            for h in range(4):
                pso = [kb.psf[2 + t_] for t_ in range(nqt)]

                def s_mm(kt):
                    mm(kb.psf[kt % 2][:, 0:nq], kT[:, h, kt * 128:(kt + 1) * 128], qT[:, h, q0:q1], True, True)
                s_mm(kt0)
                for kt in range(kt0, kt1):
                    if kt + 1 < kt1:
                        s_mm(kt + 1)
                    pss = kb.psf[kt % 2]
                    p_ = pT[pi % 3]
                    pi += 1
                    act(p_[:, 0:nq], pss[:, 0:nq], AF.Exp)
                    for t_ in range(nqt):
                        mm(pso[t_][:, 0:68], p_[:, t_ * 128:(t_ + 1) * 128], Vt[:, kt, h, :], kt == kt0, kt == kt1 - 1)
                for t_ in range(nqt):
                    r_ = rc[t_ % 2]
                    kb.recip(r_[:, 0:1], pso[t_][:, 64:65])
                    ts("dve", omla[:, (q0 // 128) + t_, h * 64:(h + 1) * 64], pso[t_][:, 0:64], r_[:, 0:1], ALU.mult)
        for i in range(0 if with_ctx else 2, 18):
            psT = kb.psum_b()
            for c in range(2):
                tr(psT[:, c * 128:(c + 1) * 128], omla[:, i, c * 128:(c + 1) * 128], identb)
            for c in range(2):
                cp("act", brT[0][c][:, i * 128:(i + 1) * 128], psT[:, c * 128:(c + 1) * 128])

    def phase_s5(l, b, with_ctx):
        AR.reset()
        AR2.lo = 0
        U = AR.alloc("U", [16, NCH])
        SN = [[AR.alloc("SN%d%d" % (d, ri), [8, NCH + 2]) for ri in range(2)] for d in range(2)]
        Ere = AR.alloc("Ere", [2, 8, 128])
        nEim = AR.alloc("nEim", [2, 8, 128])
        W3 = AR.alloc("W3", [16, 128])
        scr0 = AR.lo
        ws5 = load_win(l, "w_s5", O_S5, 256)
        Uc = AR.alloc("Uc", [16, 8, 16])
        kblocks = [(0, 32, 0), (32, 128, TC), (160, 128, TC + 1024)]
        for (k0, M, p0) in kblocks:
            pss = [kb.psum_f() for _ in range(4)]
            for r in range(8):
                ps = pss[r // 2]
                for k in range(KC):
                    mm(ps[0:M, (r % 2) * 256:(r % 2 + 1) * 256], hT[:, k, p0 + r:p0 + 8 * M:8], ws5[:, k, :], k == 0, k == KC - 1)
            for q in range(4):
                src = pss[q][0:M, :].re("p (r g i) -> p r g i", r=2, g=16)
                dst = Uc[0:M, :, 2 * q:2 * q + 2, :].re("p g r i -> p r g i")
                cp("act" if q % 2 == 0 else "dve", dst, src)
            for gq in range(4):
                ps = kb.psum_f()
                for gg_ in range(4):
                    g = gq * 4 + gg_
                    tr(ps[:, gg_ * 128:gg_ * 128 + M], Uc[0:M, g, :, :].re("p r i -> p (r i)"), identf[0:M, 0:M])
                cp("act" if gq % 2 == 0 else "dve", U[:, gq * 4:gq * 4 + 4, k0:k0 + M],
                   ps.re("p (g n) -> p g n", g=4)[:, :, 0:M])
        if "s5_U" in dbg:
            dma("sp", tap("s5U", [128, 16 * NCH]), U.re("p g k -> p (g k)"))
        P.barrier()
        AR.lo = scr0
        if "s5_stop1" in dbg:
            return
        cached = (b > 0) and ("s5_nocache" not in dbg)
        if not cached:
            kb.memset("pool", W3, 0.0)
        else:
            dma("sp", Ere.re("p d g n -> p (d g n)"), st_main[0])
            dma("act", nEim.re("p d g n -> p (d g n)"), st_main[1])
            dma("sp", W3.re("p g n -> p (g n)"), st_main[2])
        prm = AR.alloc("prm", [40, 8])
        PW = [AR.alloc("pw%d" % i, [10, 8]) for i in range(2)]
        Bb = [AR.alloc("Bb%d" % i, [8, 16]) for i in range(2)]
        Braw = [AR.alloc("Braw%d" % i, [8, 16]) for i in range(2)]
        Craw = [AR.alloc("Craw%d" % i, [8, 16]) for i in range(2)]
        Dbc = AR.alloc("Dbc", [256])
        t8 = AR.alloc("t8", [8, 16])
        w3t = [AR.alloc("w3t%d" % i, [128]) for i in range(2)]
        tE = AR.alloc("tE", [8, 128])
        Fm = [AR.alloc("F%d" % i, [8, 8, 16]) for i in range(2)]
        Ep = [AR.alloc("Ep%d" % i, [8, 128]) for i in range(2)]
        Cnat = [AR.alloc("Cnat%d" % i, [16, 64], at=Ep[i].off, buf=Ep[i].buf) for i in range(2)]
        rs_sets = [[AR.alloc("rs%d_%d" % (q, i), [NCH], at=Fm[0].off + (q * 7 + i) * NCH) for i in range(6)]
                   for q in range(2)]
        rho_sets = [AR.alloc("rho1_%d" % q, [NCH], at=Fm[0].off + (q * 7 + 6) * NCH) for q in range(2)]
        assert Fm[0].off + 14 * NCH <= Ep[1].off + Ep[1].w
        W1 = [AR2.alloc("W1%d" % i, [16, 64]) for i in range(2)]
        tab = [AR2.alloc("tab%d" % i, [8, NCH]) for i in range(2)]
        dma("sp", Dbc, s5_d[l:l + 1, :].re("o n -> (o n)").pbc(128))
        dsel = cmask[:, 2, :]

        def pv(i):
            return prm[:, i, :]

        for d in range(2):
            if d == 1:
                P.barrier()
            if not cached:
                dma("sp", pv(0), s5_a_re[l, d].re("(gp h) p -> (h p) gp", h=2), slow=True)
                dma("act", pv(1), s5_a_im[l, d].re("(gp h) p -> (h p) gp", h=2), slow=True)
                ldt = t8[:, 0, :]
                dma("sp", ldt, s5_log_dt[l, d:d + 1, :].re("o g -> (o g)").pbc(128))
                dma("sp", Cnat[0][0:16], s5_c_re[l, d].re("g o p -> o g p"))
                dma("act", Cnat[1][0:16], s5_c_im[l, d].re("g o p -> o g p"))
                for h in range(2):
                    rows = slice(h * 64, (h + 1) * 64)
                    cp("dve", prm[rows, 2, :], ldt[rows, h:16:2])
                    dma("sp", Braw[0][rows], s5_b_re[l, d].re("(gp h) p i -> h p gp i", h=2)[h])
                    dma("act", Braw[1][rows], s5_b_im[l, d].re("(gp h) p i -> h p gp i", h=2)[h])
                for ri in range(2):
                    ps = kb.psum_f()
                    for g in range(16):
                        gp, h = g // 2, g % 2
                        mm(ps[h * 64:(h + 1) * 64, gp * 16:(gp + 1) * 16], Cnat[ri][0:16, g, :], identf[0:16, 0:16], True, True)
                    cp("act", Craw[ri], ps[:, 0:128].re("p (g o) -> p g o", g=8))
                if "s5_b1" in dbg:
                    return
                act(pv(2), pv(2), AF.Exp)
                tt("dve", pv(3), pv(0), pv(2), ALU.mult)
                tt("dve", pv(4), pv(1), pv(2), ALU.mult)
                act(pv(5), pv(3), AF.Exp)
                for (dst, shift) in ((6, 0.0), (7, math.pi / 2)):
                    xx, kk, ki = pv(30), pv(31), prm[:, 32, :]
                    ts("dve", xx, pv(4), shift, ALU.add)
                    ts("dve", kk, xx, 1.0 / (2 * math.pi), ALU.mult)
                    kint = TV(ki.ap.bitcast(I32), ki.buf)
                    cp("dve", kint, kk)
                    cp("dve", kk, kint)
                    stt(xx, kk, -6.28125, xx, ALU.mult, ALU.add)
                    stt(xx, kk, -(2 * math.pi - 6.28125), xx, ALU.mult, ALU.add)
                    ts("dve", xx, xx, math.pi, ALU.min, -math.pi, ALU.max)
                    act(pv(dst), xx, AF.Sin)
                kb.memset("dve", PW[0][:, 0, :], 1.0)
                kb.memset("dve", PW[1][:, 0, :], 0.0)
                tt("dve", PW[0][:, 1, :], pv(5), pv(7), ALU.mult)
                tt("dve", PW[1][:, 1, :], pv(5), pv(6), ALU.mult)
                for j in range(2, 9):
                    tt("dve", pv(30), PW[0][:, j - 1, :], PW[0][:, 1, :], ALU.mult)
                    tt("dve", pv(31), PW[1][:, j - 1, :], PW[1][:, 1, :], ALU.mult)
                    tt("dve", PW[0][:, j, :], pv(30), pv(31), ALU.subtract)
                    tt("dve", pv(30), PW[0][:, j - 1, :], PW[1][:, 1, :], ALU.mult)
                    tt("dve", pv(31), PW[1][:, j - 1, :], PW[0][:, 1, :], ALU.mult)
                    tt("dve", PW[1][:, j, :], pv(30), pv(31), ALU.add)
                act(pv(8), pv(3), AF.Exp, scale=-16.0)
                tt("dve", PW[0][:, 9, :], PW[0][:, 8, :], pv(8), ALU.mult)
                tt("dve", PW[1][:, 9, :], PW[1][:, 8, :], pv(8), ALU.mult)
                ts("dve", PW[1][:, 9, :], PW[1][:, 9, :], -1.0, ALU.mult)
                act(pv(9), pv(3), AF.Exp, scale=8.0)
                act(pv(10), pv(3), AF.Exp, scale=-8.0)
                tt("dve", pv(11), PW[0][:, 8, :], pv(10), ALU.mult)
                tt("dve", pv(12), PW[1][:, 8, :], pv(10), ALU.mult)
                ts("dve", pv(13), PW[0][:, 1, :], -1.0, ALU.add)
                tt("dve", pv(14), pv(0), pv(0), ALU.mult)
                tt("dve", pv(15), pv(1), pv(1), ALU.mult)
                tt("dve", pv(14), pv(14), pv(15), ALU.add)
                kb.recip(pv(14), pv(14))
                tt("dve", pv(15), pv(13), pv(0), ALU.mult)
                tt("dve", pv(16), PW[1][:, 1, :], pv(1), ALU.mult)
                tt("dve", pv(15), pv(15), pv(16), ALU.add)
                tt("dve", pv(15), pv(15), pv(14), ALU.mult)
                tt("dve", pv(16), PW[1][:, 1, :], pv(0), ALU.mult)
                tt("dve", pv(17), pv(13), pv(1), ALU.mult)
                tt("dve", pv(16), pv(16), pv(17), ALU.subtract)
                tt("dve", pv(16), pv(16), pv(14), ALU.mult)
                fre = pv(15).us(2).bc([128, 8, 16])
                fim = pv(16).us(2).bc([128, 8, 16])
                tt("dve", Bb[0], Braw[0], fre, ALU.mult)
                tt("dve", t8, Braw[1], fim, ALU.mult)
                tt("dve", Bb[0], Bb[0], t8, ALU.subtract)
                tt("dve", Bb[1], Braw[1], fre, ALU.mult)
                tt("dve", t8, Braw[0], fim, ALU.mult)
                tt("dve", Bb[1], Bb[1], t8, ALU.add)
                for t_ in range(8):
                    f_ = t_ + 1 if d == 0 else 8 - t_
                    pr = PW[0][:, f_, :].us(2).bc([128, 8, 16])
                    pi_ = PW[1][:, f_, :].us(2).bc([128, 8, 16])
                    eo = Ere[:, d, :, t_ * 16:(t_ + 1) * 16]
                    ei = nEim[:, d, :, t_ * 16:(t_ + 1) * 16]
                    tt("dve", eo, Craw[0], pr, ALU.mult)
                    tt("dve", t8, Craw[1], pi_, ALU.mult)
                    tt("dve", eo, eo, t8, ALU.subtract)
                    tt("dve", ei, Craw[0], pi_, ALU.mult)
                    tt("dve", t8, Craw[1], pr, ALU.mult)
                    tt("dve", ei, ei, t8, ALU.add)
                    ts("dve", ei, ei, -1.0, ALU.mult)
                for r in range(8):
                    e_ = 7 - r if d == 0 else r
                    pr = PW[0][:, e_, :].us(2).bc([128, 8, 16])
                    pi_ = PW[1][:, e_, :].us(2).bc([128, 8, 16])
                    tt("dve", Fm[0][:, :, r, :], Bb[0], pr, ALU.mult)
                    tt("dve", t8, Bb[1], pi_, ALU.mult)
                    tt("dve", Fm[0][:, :, r, :], Fm[0][:, :, r, :], t8, ALU.subtract)
                    tt("dve", Fm[1][:, :, r, :], Bb[1], pr, ALU.mult)
                    tt("dve", t8, Bb[0], pi_, ALU.mult)
                    tt("dve", Fm[1][:, :, r, :], Fm[1][:, :, r, :], t8, ALU.add)
                qr = PW[0][:, 9, :].us(2).bc([128, 8, 128])
                qi = PW[1][:, 9, :].us(2).bc([128, 8, 128])
                tt("dve", Ep[0], Ere[:, d], qr, ALU.mult)
                tt("dve", tE, nEim[:, d], qi, ALU.mult)
                tt("dve", Ep[0], Ep[0], tE, ALU.add)
                tt("dve", Ep[1], nEim[:, d], qr, ALU.mult)
                tt("dve", tE, Ere[:, d], qi, ALU.mult)
                tt("dve", Ep[1], Ep[1], tE, ALU.subtract)
                if "s5_b2" in dbg:
                    return
                for ri in range(2):
                    for gq in range(4):
                        ps = kb.psum_f()
                        for gg_ in range(4):
                            g = gq * 4 + gg_
                            gp, h = g // 2, g % 2
                            rows = slice(h * 64, (h + 1) * 64)
                            mm(ps[:, gg_ * 64:(gg_ + 1) * 64], Fm[ri][:, gp].re("p r i -> p (r i)"), identf[:, h * 64:(h + 1) * 64], True, True)
                        cp("act", W1[ri][:, gq * 4:gq * 4 + 4, :], ps[:, 0:256].re("p (g n) -> p g n", g=4))
                if "s5_b3" in dbg:
                    return
                for g in range(16):
                    gp, h = g // 2, g % 2
                    rows = slice(h * 64, (h + 1) * 64)
                    ps = kb.psf[h * 2 + (g // 2) % 2]
                    mm(ps[:, 0:128], Fm[0][rows, gp].re("p r i -> p (r i)"), Ep[0][rows, gp, :], True, False)
                    mm(ps[:, 0:128], Fm[1][rows, gp].re("p r i -> p (r i)"), Ep[1][rows, gp, :], False, True)
                    wt = w3t[g % 2]
                    tt("dve", wt, ps[:, 0:128], cmask[:, d, :], ALU.mult)
                    tt("pool", W3[:, g, :], W3[:, g, :], wt, ALU.add)
                if d == 0:
                    for g in range(16):
                        dcol = Dbc[:, g * 16:(g + 1) * 16].us(1).bc([128, 8, 16])
                        wt = w3t[g % 2]
                        tt("dve", wt.re("p (t o) -> p t o", t=8), dsel.re("p (t o) -> p t o", t=8), dcol, ALU.mult)
                        tt("pool", W3[:, g, :], W3[:, g, :], wt, ALU.add)
                if "s5_b4" in dbg:
                    return
                kb.memset("dve", tab[0][:, :, 0:1], 1.0)
                kb.memset("dve", tab[1][:, :, 0:1], 0.0)
                cp("dve", tab[0][:, :, 1:2], pv(11).us(2))
                cp("dve", tab[1][:, :, 1:2], pv(12).us(2))
                n = 2
                while n < NCH:
                    m = min(n, NCH - n)
                    tt("dve", pv(30), tab[0][:, :, n - 1], pv(11), ALU.mult)
                    tt("dve", pv(31), tab[1][:, :, n - 1], pv(12), ALU.mult)
                    tt("dve", pv(33), pv(30), pv(31), ALU.subtract)
                    tt("dve", pv(30), tab[0][:, :, n - 1], pv(12), ALU.mult)
                    tt("dve", pv(31), tab[1][:, :, n - 1], pv(11), ALU.mult)
                    tt("dve", pv(34), pv(30), pv(31), ALU.add)
                    Pr = pv(33).us(2).bc([128, 8, m])
                    Pi = pv(34).us(2).bc([128, 8, m])
                    ta = tE[:, :, 0:m]
                    tt("dve", tab[0][:, :, n:n + m], tab[0][:, :, 0:m], Pr, ALU.mult)
                    tt("dve", ta, tab[1][:, :, 0:m], Pi, ALU.mult)
                    tt("dve", tab[0][:, :, n:n + m], tab[0][:, :, n:n + m], ta, ALU.subtract)
                    tt("dve", tab[1][:, :, n:n + m], tab[0][:, :, 0:m], Pi, ALU.mult)
                    tt("dve", ta, tab[1][:, :, 0:m], Pr, ALU.mult)
                    tt("dve", tab[1][:, :, n:n + m], tab[1][:, :, n:n + m], ta, ALU.add)
                    n *= 2
                dma("sp", st_dir[d, :, 0:1024], W1[0].re("p g n -> p (g n)"))
                dma("sp", st_dir[d, :, 1024:2048], W1[1].re("p g n -> p (g n)"))
                dma("act", st_dir[d, :, 2048:2048 + 8 * NCH], tab[0].re("p g n -> p (g n)"))
                dma("act", st_dir[d, :, 2048 + 8 * NCH:2048 + 16 * NCH], tab[1].re("p g n -> p (g n)"))
                dma("sp", st_dir[d, :, 2048 + 16 * NCH:2048 + 16 * NCH + 8], pv(9))
            else:
                dma("sp", W1[0].re("p g n -> p (g n)"), st_dir[d, :, 0:1024])
                dma("sp", W1[1].re("p g n -> p (g n)"), st_dir[d, :, 1024:2048])
                dma("act", tab[0].re("p g n -> p (g n)"), st_dir[d, :, 2048:2048 + 8 * NCH])
                dma("act", tab[1].re("p g n -> p (g n)"), st_dir[d, :, 2048 + 8 * NCH:2048 + 16 * NCH])
                dma("sp", pv(9), st_dir[d, :, 2048 + 16 * NCH:2048 + 16 * NCH + 8])
            if "s5_stop2" in dbg:
                return
            P.barrier()
            def nat(tv, sl, rev):
                v = tv[:, sl]
                return v[:, ::-1] if rev else v

            def gp_gen(gp, d=d):
                rs, rho1 = rs_sets[gp % 2], rho_sets[gp % 2]
                psv = [kb.psum_f(), kb.psum_f()]
                for ri in range(2):
                    for h in range(2):
                        g = gp * 2 + h
                        mm(psv[ri][h * 64:(h + 1) * 64, 0:NCH], W1[ri][:, g, :], U[:, g, :], True, True)
                cp("pool", rho1, pv(9)[:, gp:gp + 1].bc([128, NCH]))
                yield
                Cn, Sn = tab[0][:, gp, :], tab[1][:, gp, :]
                if d == 0:
                    segs = [(slice(0, NCH), slice(0, NCH), False)]
                else:
                    segs = [(slice(0, 32), slice(0, 32), True), (slice(32, NCH), slice(32, NCH), True)]
                for (js, ks, rev) in segs:
                    vre, vim = nat(psv[0][:, 0:NCH], ks, rev), nat(psv[1][:, 0:NCH], ks, rev)
                    tt("dve", rs[0][:, js], vre, Cn[:, js], ALU.mult)
                    tt("dve", rs[1][:, js], vim, Sn[:, js], ALU.mult)
                    tt("dve", rs[4][:, js], vim, Cn[:, js], ALU.mult)
                    tt("dve", rs[5][:, js], vre, Sn[:, js], ALU.mult)
                yield
                tt("pool", rs[2], rs[0], rs[1], ALU.add)
                tt("pool", rs[3], rs[4], rs[5], ALU.subtract)
                yield
                kb.scan(rs[4], rho1, rs[2], 0.0)
                kb.scan(rs[5], rho1, rs[3], 0.0)
                yield
                tt("dve", rs[0], rs[4], Cn, ALU.mult)
                tt("dve", rs[1], rs[5], Sn, ALU.mult)
                tt("dve", rs[2], rs[4], Sn, ALU.mult)
                tt("dve", rs[3], rs[5], Cn, ALU.mult)
                yield
                for (js, ks, rev) in segs:
                    if d == 0:
                        osl = slice(1, NCH + 1)
                    else:
                        osl = slice(0, 32) if ks.start == 0 else slice(33, NCH + 1)
                    ore = nat(SN[d][0][:, gp, :], osl, rev)
                    oim = nat(SN[d][1][:, gp, :], osl, rev)
                    tt("pool", ore, rs[0][:, js], rs[1][:, js], ALU.subtract)
                    tt("pool", oim, rs[2][:, js], rs[3][:, js], ALU.add)
                yield
            for gp0 in range(0, 8, 2):
                lockstep([gp_gen(gp0), gp_gen(gp0 + 1)])
            for ri in range(2):
                if d == 0:
                    kb.memset("pool", SN[0][ri][:, :, 0:1], 0.0)
                else:
                    kb.memset("pool", SN[1][ri][:, :, 32:33], 0.0)
                    cp("pool", SN[1][ri][:, :, NCH + 1:NCH + 2], SN[1][ri][:, :, 0:1])
        if not cached:
            dma("sp", st_main[0], Ere.re("p d g n -> p (d g n)"))
            dma("act", st_main[1], nEim.re("p d g n -> p (d g n)"))
            dma("sp", st_main[2], W3.re("p g n -> p (g n)"))
        if "s5_stop3" in dbg:
            return
        P.barrier()
        AR.lo = scr0
        AR2.lo = 0
        Yc = AR.alloc("Yc", [8, 256])
        gg = AR.alloc("gg", [8, 256])
        g2 = AR.alloc("g2", [8, 256])
        sg = [AR.alloc("sg%d" % i, [512]) for i in range(2)]
        ggT = AR2.alloc("ggT", [2, NPOS])
        ggb = AR2.alloc("ggb", [2, NPOS], BF16)
        for (k0, M, p0) in kblocks:
            if k0 == 0 and not with_ctx:
                continue
            banks = [kb.psf[0], kb.psf[2], kb.psf[1], kb.psf[3]]
            for g in range(16):
                gp, h = g // 2, g % 2
                rows = slice(h * 64, (h + 1) * 64)
                bank = banks[h * 2 + (gp // 4)]
                o = bank[0:M, (gp % 4) * 128:(gp % 4 + 1) * 128]
                cf = slice(k0, k0 + M)
                cb_ = slice(k0 + 1, k0 + 1 + M) if k0 == 0 else slice(k0 + 2, k0 + 2 + M)
                mm(o, SN[0][0][rows, gp, cf], Ere[rows, 0, gp, :], True, False)
                mm(o, SN[0][1][rows, gp, cf], nEim[rows, 0, gp, :], False, False)
                mm(o, SN[1][0][rows, gp, cb_], Ere[rows, 1, gp, :], False, False)
                mm(o, SN[1][1][rows, gp, cb_], nEim[rows, 1, gp, :], False, False)
                mm(o, U[:, g, k0:k0 + M], W3[:, g, :], False, True)
            for h in range(2):
                for q in range(2):
                    bank = banks[h * 2 + q]
                    src = bank[0:M, :].re("p (g t o) -> p g t o", g=4, t=8)
                    dst = Yc[0:M].re("p t (g2 h o) -> p h g2 t o", h=2, o=16)[:, h, 4 * q:4 * q + 4]
                    cp("act" if h == 0 else "dve", dst, src)
            if "s5_Y" in dbg:
                dma("sp", tap("s5Y", [NCH, 8 * 256])[k0:k0 + M, :], Yc[0:M].re("p t f -> p (t f)"))
            yv, gv, g2v = Yc[0:M], gg[0:M], g2[0:M]
            tt("pool", g2v, yv, yv, ALU.mult)
            ts("dve", g2v, g2v, 0.044715, ALU.mult, 1.0, ALU.add)
            tt("pool", g2v, g2v, yv, ALU.mult)
            act(g2v, g2v, AF.Sigmoid, scale=1.5957691216057308)
            tt("dve", gv, g2v, yv, ALU.mult)
            for t_ in range(8):
                ps = kb.psum_f()
                for c in range(2):
                    tr(ps[:, c * 128:c * 128 + M], gv[:, t_, c * 128:(c + 1) * 128], identf[0:M, 0:M])
                for c in range(2):
                    cp("act" if c == 0 else "dve", ggT[:, c, p0 + t_:p0 + 8 * M:8], ps[:, c * 128:c * 128 + M])
        lo = 0 if with_ctx else TC
        for c in range(2):
            cp("dve", ggb[:, c, lo:NPOS], ggT[:, c, lo:NPOS])
        for c in range(2):
            pos = lo
            while pos < NPOS:
                n = min(512, NPOS - pos)
                ps = kb.psum_f()
                for k in range(2):
                    mm(ps[:, 0:n], wglu[:, k, c * 128:(c + 1) * 128], ggb[:, k, pos:pos + n], k == 0, k == 1)
                s_ = sg[(pos // 512) % 2]
                act(s_[:, 0:n], ps[:, 0:n], AF.Sigmoid)
                tt("dve", brT[1][c][:, pos:pos + n], s_[:, 0:n], ggT[:, c, pos:pos + n], ALU.mult)
                pos += n

    def phase_merge(l, b, with_ctx):
        AR.reset(top=True)
        lo = 0 if with_ctx else TC
        blocks = []
        pos = lo
        while pos < NPOS:
            n = min(512, NPOS - pos) if pos >= TC else TC - pos
            blocks.append((pos, n))
            pos += n
        ypT = AR.alloc("ypT", [KC, NPOS], BF16, top=True)
        wbr = AR.alloc("wbr", [4, 2, D], BF16, top=True)
        wo = AR.alloc("wo", [KC, D], BF16, top=True)
        wml = [AR.alloc("wml%d" % i, [KC, 4, 128], BF16, top=True) for i in range(2)]

        def load_wml(oc):
            for n_ in range(4):
                c0 = O_MERGE + n_ * D + oc * 128
                dma("pool", wml[oc % 2][:, :, n_, :], w_in[l, :, c0:c0 + 128].re("(c p) n -> p c n", p=128))
        wg = AR.alloc("w_gate", [KC, 1024], BF16)
        for cc in range(8):
            dma("pool", wg[:, :, cc * 128:(cc + 1) * 128],
                w_in[l, :, O_GATE + cc * 128:O_GATE + (cc + 1) * 128].re("(c p) n -> p c n", p=128))
        for n_ in range(4):
            dma("pool", wbr[:, n_], w_branch[l, n_].re("(c p) n -> p c n", p=128))
        load_wml(0)
        load_wml(1)
        dma("pool", wo, w_out[l].re("(c p) n -> p c n", p=128))
        sl = [AR.alloc("sl%d" % i, [512], BF16) for i in range(2)]
        it = 0
        for cc in range(8):
            for (p0, n) in blocks:
                ps = kb.psum_f()
                for k in range(KC):
                    mm(ps[:, 0:n], wg[:, k, cc * 128:(cc + 1) * 128], hT[:, k, p0:p0 + n], k == 0, k == KC - 1)
                s_ = sl[it % 2]
                it += 1
                act(s_[:, 0:n], ps[:, 0:n], AF.Silu)
                br = brT[cc // 2][cc % 2][:, p0:p0 + n]
                tt("dve", br, br, s_[:, 0:n], ALU.mult)
        AR.reset()
        sg = [AR.alloc("sg%d" % i, [512]) for i in range(2)]
        tm = [AR.alloc("tm%d" % i, [512]) for i in range(2)]
        acc = [AR.alloc("acc%d" % i, [512]) for i in range(2)]
        it = 0
        ib = 0
        for oc in range(8):
            wm = wml[oc % 2]
            if 1 <= oc and oc + 1 < 8:
                load_wml(oc + 1)
            for (p0, n) in blocks:
                ac = acc[ib % 2]
                ib += 1
                for n_ in range(4):
                    psA = kb.psum_f()
                    for k in range(2):
                        mm(psA[:, 0:n], wbr[:, n_, k, oc * 128:(oc + 1) * 128], brT[n_][k][:, p0:p0 + n], k == 0, k == 1)
                    psB = kb.psum_f()
                    for k in range(KC):
                        mm(psB[:, 0:n], wm[:, k, n_, :], hT[:, k, p0:p0 + n], k == 0, k == KC - 1)
                    s_ = sg[it % 2]
                    t_ = tm[it % 2]
                    it += 1
                    act(s_[:, 0:n], psB[:, 0:n], AF.Sigmoid)
                    if n_ == 0:
                        tt("dve", ac[:, 0:n], psA[:, 0:n], s_[:, 0:n], ALU.mult)
                    elif n_ < 3:
                        tt("dve", t_[:, 0:n], psA[:, 0:n], s_[:, 0:n], ALU.mult)
                        tt("dve", ac[:, 0:n], ac[:, 0:n], t_[:, 0:n], ALU.add)
                    else:
                        tt("dve", t_[:, 0:n], psA[:, 0:n], s_[:, 0:n], ALU.mult)
                        tt("dve", ypT[:, oc, p0:p0 + n], ac[:, 0:n], t_[:, 0:n], ALU.add)
        AR.reset()
        gbc = AR.alloc("gbc", [2, D])
        dma("sp", gbc[:, 0, :], modD[l, b, 2 * D:3 * D].pbc(128))
        if with_ctx:
            dma("sp", gbc[:, 1, :], modD[l, 2, 2 * D:3 * D].pbc(128))
        xt = [AR.alloc("xt%d" % i, [D]) for i in range(2)]
        yt = [AR.alloc("yt%d" % i, [D]) for i in range(2)]
        for i in range(0 if with_ctx else 2, 18):
            x_t, y_t = xt[i % 2], yt[i % 2]
            dma("sp", x_t, src_tile(l, b, i))
            for hf in range(2):
                ps = kb.psum_f()
                for k in range(KC):
                    mm(ps, ypT[:, k, i * 128:(i + 1) * 128], wo[:, k, hf * 512:(hf + 1) * 512], k == 0, k == KC - 1)
                tt("dve", y_t[:, hf * 512:(hf + 1) * 512], ps, gbc[:, 1 if i < 2 else 0, hf * 512:(hf + 1) * 512], ALU.mult)
            tt("pool", y_t, y_t, x_t, ALU.add)
            if i < 2:
                dst = (tap("xc1", [nb, TC, D]) if "x1out" in dbg else xcmid)[b, i * 128:(i + 1) * 128, :]
            else:
                dst = (xmid if (l < DEPTH - 1 and "x1out" not in dbg) else y_out)[b, (i - 2) * 128:(i - 1) * 128, :]
            dma("act", dst, y_t)

    def dump_br(name, n, lo):
        o = tap(name, [2, 128, NPOS])
        tmpf = AR.alloc("dump_" + name, [NPOS])
        for c in range(2):
            cp("dve", tmpf[:, lo:NPOS], brT[n][c][:, lo:NPOS])
            dma("sp", o[c, :, lo:NPOS], tmpf[:, lo:NPOS])

    for l in layers:
        with_ctx = l < DEPTH - 1
        prep_layer(l)
        for b in range(nb):
            phase_norm(l, b)
            if "hT" in dbg and b == 0 and l == layers[0]:
                o = tap("hT", [KC, 128, NPOS])
                AR.reset()
                tmpf = AR.alloc("dump_hT", [NPOS])
                for c in range(KC):
                    cp("dve", tmpf, hT[:, c, :])
                    dma("sp", o[c], tmpf)
            only = dbg & {"only_lru", "only_pool", "only_mla", "only_s5"}
            if not only or "only_s5" in only:
                phase_s5(l, b, with_ctx)
            if not only or "only_lru" in only:
                phase_lru(l, b, with_ctx)
            if not only or "only_pool" in only:
                phase_pool(l, b, with_ctx)
            if not only or "only_mla" in only:
                phase_mla(l, b, with_ctx)
            if "br" in dbg and b == 0 and l == layers[0]:
                AR.reset()
                lo = 0 if with_ctx else TC
                for n_, nm in enumerate(("mla", "s5", "lru", "pool")):
                    dump_br(nm, n_, lo)
            if "nomerge" not in dbg:
                phase_merge(l, b, with_ctx)
    P.barrier()
    P.emit()
    kb.st.close()
    return nc, kb


def _consts():
    ident = np.eye(128, dtype=np.float32)
    rows_n = T // 64
    row = np.repeat(np.arange(rows_n), 64).astype(np.float32)
    col = np.tile(np.arange(64), rows_n).astype(np.float32)
    nf = 8
    inv = (np.float32(10000.0) ** (-np.arange(nf, dtype=np.float32) / nf)).astype(np.float32)
    ar = (row[:, None] * inv).astype(np.float32)
    ac = (col[:, None] * inv).astype(np.float32)
    cr, sr, cc, sc = np.cos(ar), np.sin(ar), np.cos(ac), np.sin(ac)
    ropeC = np.concatenate([cr, cr, cc, cc], axis=1).astype(np.float32)
    ropeS = np.concatenate([-sr, sr, -sc, sc], axis=1).astype(np.float32)
    cpool = np.ones((128, 34), np.float32)
    for c in range(2):
        for p in range(128):
            w = (2, 4, 8, 16)[2 * c + p // 64]
            hw = w // 2
            cpool[p, c] = 1.0 / w
            for t in range(8):
                cnt = t + hw if t < hw else w
                cpool[p, 2 + c * 8 + t] = w / cnt
            for j in range(8):
                dist = 8 - j
                cnt = dist + hw if dist < hw else w
                cpool[p, 18 + c * 8 + j] = w / cnt
    m = np.zeros((128, 3, 128), np.float32)
    for r in range(8):
        for t in range(8):
            if r <= t:
                m[r * 16:(r + 1) * 16, 0, t * 16:(t + 1) * 16] = 1.0
            if r >= t:
                m[r * 16:(r + 1) * 16, 1, t * 16:(t + 1) * 16] = 1.0
            if r == t:
                m[r * 16:(r + 1) * 16, 2, t * 16:(t + 1) * 16] = np.eye(16, dtype=np.float32)
    return {"c_ident": ident, "c_ropeC": ropeC, "c_ropeS": ropeS, "c_pool": cpool, "c_mask": m}


_WNAMES = ["w_ada", "b_ada", "norm_g", "w_in", "mla_q_norm", "mla_kv_norm", "mla_w_uq", "mla_w_ukv", "mla_q_gain",
           "mla_k_gain", "s5_a_re", "s5_a_im", "s5_log_dt", "s5_b_re", "s5_b_im", "s5_c_re", "s5_c_im", "s5_d",
           "s5_w_glu", "lru_conv_w", "lru_conv_b", "lru_lambda", "lru_w_a", "lru_b_a", "lru_w_x", "lru_b_x",
           "pool_w", "pool_b", "pool_scale", "w_branch", "w_out"]


def make_in_maps(inputs, n_cores, nb):
    consts = _consts()
    maps = []
    for r in range(n_cores):
        bs = slice(r * nb, (r + 1) * nb)
        m = {"x": np.ascontiguousarray(inputs["x"][bs], dtype=np.float32),
             "ctx": np.ascontiguousarray(inputs["ctx"][bs], dtype=np.float32)}
        cv = np.zeros((3, D), np.float32)
        cv[0:nb] = np.asarray(inputs["c"], dtype=np.float32)[bs]
        cv[2] = np.asarray(inputs["c_ctx"], dtype=np.float32)
        m["cvec"] = cv
        for k in _WNAMES:
            m[k] = np.ascontiguousarray(inputs[k], dtype=np.float32)
        m.update(consts)
        maps.append(m)
    return maps


_CACHE = {}


def kernel(**inputs):
    n_cores, nb = 8, 2
    if "nc" not in _CACHE:
        _CACHE["nc"] = build_program(nb=nb)[0]
    nc = _CACHE["nc"]
    maps = make_in_maps(inputs, n_cores, nb)
    res = run_bass_kernel_spmd(nc, maps, core_ids=list(range(n_cores)))
    out = np.concatenate([np.asarray(r["y"], dtype=np.float32) for r in res.results], axis=0)
    return out
```

```python
import math
import contextlib
import numpy as np
import concourse.bass as bass
import concourse.mybir as mybir
from concourse.bass_utils import run_bass_kernel_spmd

F32 = mybir.dt.float32
BF16 = mybir.dt.bfloat16
I32 = mybir.dt.int32
AF = mybir.ActivationFunctionType
ALU = mybir.AluOpType
AX = mybir.AxisListType

ENGS = ("pe", "act", "dve", "pool", "sp")
NDMA = 24

D = 1024
KC = 8
T = 2048
TC = 256
NPOS = T + TC
DEPTH = 2
O_KROPE, O_S5, O_LRU, O_CQ, O_POOL, O_GATE, O_MERGE, IN_W = 128, 160, 416, 672, 928, 1184, 2208, 6304
EPS = 1e-6
NCH = NPOS // 8


class Buf:
    __slots__ = ("name", "w", "r")

    def __init__(self, name):
        self.name = name
        self.w = None
        self.r = []


class TV:
    def __init__(self, ap, buf):
        self.ap, self.buf = ap, buf

    def __getitem__(self, k):
        return TV(self.ap[k], self.buf)

    def re(self, s, **kw):
        return TV(self.ap.rearrange(s, **kw), self.buf)

    def bc(self, shape):
        return TV(self.ap.to_broadcast(list(shape)), self.buf)

    def us(self, ax):
        return TV(self.ap.unsqueeze(ax), self.buf)

    def pbc(self, n):
        return TV(self.ap.partition_broadcast(n), self.buf)

    @property
    def shape(self):
        return tuple(self.ap.shape)


def _bufs(*xs):
    out = []
    for x in xs:
        if isinstance(x, TV):
            out.append(x.buf)
        elif isinstance(x, (list, tuple)):
            out.extend(_bufs(*x))
    return out


def _ap(x):
    return x.ap if isinstance(x, TV) else x


class Prog:
    def __init__(self, nc):
        self.nc = nc
        self.ops = {e: [] for e in ENGS}
        self.cnt = {e: 0 for e in ENGS}
        self.waited = {e: {} for e in ENGS}
        self.dma_i = 0
        self.dma_last = [0] * NDMA

    def _deps(self, eng, reads, writes):
        toks = []
        for b in reads:
            if b.w is not None:
                toks.append(b.w)
        for b in writes:
            if b.w is not None:
                toks.append(b.w)
            toks.extend(b.r)
        need = {}
        for (k, v, e) in toks:
            if e == "pe" and eng == "pe":
                continue
            if self.waited[eng].get(k, 0) >= v:
                continue
            if need.get(k, 0) < v:
                need[k] = v
        for k, v in need.items():
            self.waited[eng][k] = v
        return list(need.items())

    def _mark(self, tok, reads, writes):
        for b in writes:
            b.w = tok
            b.r = []
        for b in reads:
            if b in writes:
                continue
            b.r.append(tok)
            if len(b.r) > 16:
                best = {}
                for (k, v, e) in b.r:
                    if k not in best or best[k][1] < v:
                        best[k] = (k, v, e)
                b.r = list(best.values())

    def op(self, eng, fn, reads=(), writes=()):
        reads = list(dict.fromkeys(reads))
        writes = list(dict.fromkeys(writes))
        waits = self._deps(eng, reads, writes)
        self.cnt[eng] += 1
        tok = (eng, self.cnt[eng], eng)
        self.ops[eng].append((waits, fn, (eng, 1)))
        self._mark(tok, reads, writes)

    def dma(self, eng, fn, reads=(), writes=()):
        reads = list(dict.fromkeys(reads))
        writes = list(dict.fromkeys(writes))
        waits = self._deps(eng, reads, writes)
        slot = self.dma_i % NDMA
        self.dma_i += 1
        key = ("dma", slot)
        prev = self.dma_last[slot]
        if prev and self.waited[eng].get(key, 0) < prev:
            waits.append((key, prev))
            self.waited[eng][key] = prev
        val = prev + 16
        self.dma_last[slot] = val
        tok = (key, val, "dma")
        self.ops[eng].append((waits, fn, (key, 16)))
        self._mark(tok, reads, writes)

    def barrier(self):
        for E in ENGS:
            waits = []
            for e in ENGS:
                v = self.cnt[e]
                if v and self.waited[E].get(e, 0) < v:
                    waits.append((e, v))
                    self.waited[E][e] = v
            for slot in range(NDMA):
                v = self.dma_last[slot]
                key = ("dma", slot)
                if v and self.waited[E].get(key, 0) < v:
                    waits.append((key, v))
                    self.waited[E][key] = v
            if waits:
                self.ops[E].append((waits, None, None))

    def emit(self):
        nc = self.nc
        sems = {}
        with contextlib.ExitStack() as st:
            for e in ENGS:
                sems[e] = st.enter_context(nc.semaphore("s_" + e))
            for i in range(NDMA):
                sems[("dma", i)] = st.enter_context(nc.semaphore("s_dma%d" % i))
            block = st.enter_context(nc.Block())

            def run(engname):
                def body(eng):
                    for waits, fn, inc in self.ops[engname]:
                        for k, v in waits:
                            eng.wait_ge(sems[k], v)
                        if fn is None:
                            continue
                        ins = fn(eng)
                        ins.then_inc(sems[inc[0]], inc[1])
                return body
            block.tensor(run("pe"))
            block.scalar(run("act"))
            block.vector(run("dve"))
            block.gpsimd(run("pool"))
            block.sync(run("sp"))


class KB:
    def __init__(self, nc, nb, layers, dbg):
        self.nc = nc
        self.P = Prog(nc)
        self.nb = nb
        self.layers = layers
        self.dbg = dbg
        self.st = contextlib.ExitStack()
        self.din = {}
        self.dout = {}
        self.ps_i = 0
        self.psb_i = 0
        self.ps8_i = 0

    def tt(self, eng, out, a, b, op):
        self.P.op(eng, lambda e: e.tensor_tensor(out=out.ap, in0=a.ap, in1=b.ap, op=op), _bufs(a, b), _bufs(out))

    def ts(self, eng, out, a, s1, op0, s2=None, op1=None):
        if op1 is None:
            self.P.op(eng, lambda e: e.tensor_scalar(out=out.ap, in0=a.ap, scalar1=_ap(s1), scalar2=None, op0=op0),
                      _bufs(a, s1), _bufs(out))
        else:
            self.P.op(eng, lambda e: e.tensor_scalar(out=out.ap, in0=a.ap, scalar1=_ap(s1), scalar2=_ap(s2), op0=op0, op1=op1),
                      _bufs(a, s1, s2), _bufs(out))

    def stt(self, out, a, s, b, op0, op1):
        self.P.op("dve", lambda e: e.scalar_tensor_tensor(out=out.ap, in0=a.ap, scalar=_ap(s), in1=b.ap, op0=op0, op1=op1),
                  _bufs(a, s, b), _bufs(out))

    def act(self, out, a, func, bias=0.0, scale=1.0, accum=None):
        if accum is None:
            self.P.op("act", lambda e: e.activation(out=out.ap, in_=a.ap, func=func, bias=_ap(bias), scale=_ap(scale)),
                      _bufs(a, bias, scale), _bufs(out))
        else:
            self.P.op("act", lambda e: e.activation(out=out.ap, in_=a.ap, func=func, bias=_ap(bias), scale=_ap(scale), accum_out=accum.ap),
                      _bufs(a, bias, scale), _bufs(out, accum))

    def cp(self, eng, out, a):
        if eng == "act":
            self.P.op("act", lambda e: e.copy(out=out.ap, in_=a.ap), _bufs(a), _bufs(out))
        else:
            self.P.op(eng, lambda e: e.tensor_copy(out=out.ap, in_=a.ap), _bufs(a), _bufs(out))

    def memset(self, eng, out, v):
        self.P.op(eng, lambda e: e.memset(out.ap, v), [], _bufs(out))

    def recip(self, out, a):
        self.P.op("dve", lambda e: e.reciprocal(out=out.ap, in_=a.ap), _bufs(a), _bufs(out))

    def mm(self, out, lhsT, rhs, start, stop):
        self.P.op("pe", lambda e: e.matmul(out.ap, lhsT=lhsT.ap, rhs=rhs.ap, start=start, stop=stop), _bufs(lhsT, rhs), _bufs(out))

    def tr(self, out, a, ident):
        self.P.op("pe", lambda e: e.transpose(out.ap, a.ap, ident.ap), _bufs(a, ident), _bufs(out))

    def scan(self, out, d0, d1, init):
        self.P.op("dve", lambda e: e.tensor_tensor_scan(out=out.ap, data0=d0.ap, data1=d1.ap, initial=_ap(init), op0=ALU.mult, op1=ALU.add),
                  _bufs(d0, d1, init), _bufs(out))

    def reduce(self, out, a, op=ALU.add):
        self.P.op("dve", lambda e: e.tensor_reduce(out=out.ap, in_=a.ap, axis=AX.X, op=op), _bufs(a), _bufs(out))

    def dma(self, q, out, a, slow=False):
        if slow:
            self.P.dma(q, lambda e: e.dma_start(out=out.ap, in_=a.ap, allow_slow_non_contiguous=True), _bufs(a), _bufs(out))
        else:
            self.P.dma(q, lambda e: e.dma_start(out=out.ap, in_=a.ap), _bufs(a), _bufs(out))

    def dram_in(self, name, shape, dt=F32):
        t = TV(self.nc.dram_tensor(name, list(shape), dt, kind="ExternalInput").ap(), Buf(name))
        self.din[name] = t
        return t

    def dram_out(self, name, shape, dt=F32):
        t = TV(self.nc.dram_tensor(name, list(shape), dt, kind="ExternalOutput").ap(), Buf(name))
        self.dout[name] = t
        return t

    def dram_tmp(self, name, shape, dt=F32):
        return TV(self.nc.dram_tensor(name, list(shape), dt, kind="Internal").ap(), Buf(name))

    def sb(self, name, shape, dt=F32):
        t = self.st.enter_context(self.nc.sbuf_tensor(name, list(shape), dt))
        return TV(t[:], Buf(name))

    def psum_f(self):
        i = self.ps_i % 6
        self.ps_i += 1
        return self.psf[i]

    def psum_b(self):
        i = self.psb_i % 2
        self.psb_i += 1
        return self.psb[i]

    def psum_any(self, bf16=False):
        i = self.ps8_i % 8
        self.ps8_i += 1
        return self.ps8b[i] if bf16 else self.ps8[i]


class Arena:
    def __init__(self, kb, words, ap=None):
        self.kb = kb
        self.words = words
        if ap is None:
            self.f = kb.st.enter_context(kb.nc.sbuf_tensor("arena_f", [128, words], F32))
        else:
            self.f = ap
        self.lo = 0
        self.hi = words

    def alloc(self, name, free, dt=F32, top=False, buf=None, at=None):
        nel = int(np.prod(free))
        w = nel if dt == F32 else (nel + 1) // 2
        w = ((w + 15) // 16) * 16
        if at is not None:
            off = at
        elif top:
            self.hi -= w
            off = self.hi
        else:
            off = self.lo
            self.lo += w
        assert self.lo <= self.hi, (name, self.lo, self.hi)
        assert off + w <= self.words
        if dt != F32:
            ap = self.f[:, off:off + w].bitcast(dt)[:, 0:nel]
        else:
            ap = self.f[:, off:off + nel]
        if len(free) > 1:
            names = " ".join("d%d" % i for i in range(len(free)))
            kw = {"d%d" % i: int(free[i]) for i in range(len(free))}
            ap = ap.rearrange("p (%s) -> p %s" % (names, names), **kw)
        tv = TV(ap, buf if buf is not None else Buf(name))
        tv.off = off
        tv.w = w
        return tv

    def reset(self, top=False):
        self.kb.P.barrier()
        self.lo = 0
        if top:
            self.hi = self.words


def lockstep(gens):
    gens = list(gens)
    while gens:
        nxt = []
        for g in gens:
            try:
                next(g)
                nxt.append(g)
            except StopIteration:
                pass
        gens = nxt


def build_program(nb=2, layers=(0, 1), dbg=None):
    nc = bass.Bass("TRN2", target_bir_lowering=False)
    kb = KB(nc, nb, layers, dbg)
    P = kb.P
    tt, ts, stt, act, cp, mm, tr, dma = kb.tt, kb.ts, kb.stt, kb.act, kb.cp, kb.mm, kb.tr, kb.dma
    dbg = dbg or set()

    x_in = kb.dram_in("x", [nb, T, D])
    ctx_in = kb.dram_in("ctx", [nb, TC, D])
    cvec = kb.dram_in("cvec", [3, D])
    w_ada = kb.dram_in("w_ada", [DEPTH, D, 3 * D])
    b_ada = kb.dram_in("b_ada", [DEPTH, 3 * D])
    norm_g = kb.dram_in("norm_g", [DEPTH, D])
    w_in = kb.dram_in("w_in", [DEPTH, D, IN_W])
    mla_q_norm = kb.dram_in("mla_q_norm", [DEPTH, 256])
    mla_kv_norm = kb.dram_in("mla_kv_norm", [DEPTH, 128])
    mla_w_uq = kb.dram_in("mla_w_uq", [DEPTH, 256, 384])
    mla_w_ukv = kb.dram_in("mla_w_ukv", [DEPTH, 128, 512])
    mla_q_gain = kb.dram_in("mla_q_gain", [DEPTH, 96])
    mla_k_gain = kb.dram_in("mla_k_gain", [DEPTH, 96])
    s5_a_re = kb.dram_in("s5_a_re", [DEPTH, 2, 16, 64])
    s5_a_im = kb.dram_in("s5_a_im", [DEPTH, 2, 16, 64])
    s5_log_dt = kb.dram_in("s5_log_dt", [DEPTH, 2, 16])
    s5_b_re = kb.dram_in("s5_b_re", [DEPTH, 2, 16, 64, 16])
    s5_b_im = kb.dram_in("s5_b_im", [DEPTH, 2, 16, 64, 16])
    s5_c_re = kb.dram_in("s5_c_re", [DEPTH, 2, 16, 16, 64])
    s5_c_im = kb.dram_in("s5_c_im", [DEPTH, 2, 16, 16, 64])
    s5_d = kb.dram_in("s5_d", [DEPTH, 256])
    s5_w_glu = kb.dram_in("s5_w_glu", [DEPTH, 256, 256])
    lru_conv_w = kb.dram_in("lru_conv_w", [DEPTH, 4, 256])
    lru_conv_b = kb.dram_in("lru_conv_b", [DEPTH, 256])
    lru_lambda = kb.dram_in("lru_lambda", [DEPTH, 2, 256])
    lru_w_a = kb.dram_in("lru_w_a", [DEPTH, 2, 4, 64, 64])
    lru_b_a = kb.dram_in("lru_b_a", [DEPTH, 2, 256])
    lru_w_x = kb.dram_in("lru_w_x", [DEPTH, 2, 4, 64, 64])
    lru_b_x = kb.dram_in("lru_b_x", [DEPTH, 2, 256])
    pool_w = kb.dram_in("pool_w", [DEPTH, 4, 64, 64])
    pool_b = kb.dram_in("pool_b", [DEPTH, 256])
    pool_scale = kb.dram_in("pool_scale", [DEPTH, 256])
    w_branch = kb.dram_in("w_branch", [DEPTH, 4, 256, D])
    w_out = kb.dram_in("w_out", [DEPTH, D, D])
    c_ident = kb.dram_in("c_ident", [128, 128])
    c_ropeC = kb.dram_in("c_ropeC", [T, 32])
    c_ropeS = kb.dram_in("c_ropeS", [T, 32])
    c_pool = kb.dram_in("c_pool", [128, 2 + 16 + 16])
    c_mask = kb.dram_in("c_mask", [128, 3, 128])
    y_out = kb.dram_out("y", [nb, T, D])
    xmid = kb.dram_tmp("xmid", [nb, T, D])
    xcmid = kb.dram_tmp("xcmid", [nb, TC, D])
    modD = kb.dram_tmp("modD", [DEPTH, 3, 3 * D])
    st_main = kb.dram_tmp("st_main", [3, 128, 2048])
    st_dir = kb.dram_tmp("st_dir", [2, 128, 2048 + 16 * NCH + 8])
    dbg_out = {}

    def tap(name, shape):
        if name not in dbg_out:
            dbg_out[name] = kb.dram_out("dbg_" + name, shape)
        return dbg_out[name]

    kb.ps8 = []
    for i in range(8):
        t = kb.st.enter_context(nc.psum_tensor("psf%d" % i, [128, 512], F32))
        kb.ps8.append(TV(t[:], Buf("psf%d" % i)))
    kb.psf = kb.ps8[0:6]
    kb.psb = [TV(kb.ps8[i].ap.bitcast(BF16), kb.ps8[i].buf) for i in (6, 7)]
    kb.ps8b = [TV(kb.ps8[i].ap.bitcast(BF16), kb.ps8[i].buf) for i in range(8)]

    hT = kb.sb("hT", [128, KC, NPOS], BF16)
    bigbr = kb.sb("bigbr", [128, 8 * NPOS], BF16)
    _slot = {0: 0, 2: 2, 3: 4, 1: 6}
    brT = [[TV(bigbr.ap[:, (_slot[n] + c) * NPOS:(_slot[n] + c + 1) * NPOS], Buf("brT%d%d" % (n, c))) for c in range(2)]
           for n in range(4)]
    identf = kb.sb("identf", [128, 128])
    identb = kb.sb("identb", [128, 128], BF16)
    ropeC = kb.sb("ropeC", [128, 16, 32])
    ropeS = kb.sb("ropeS", [128, 16, 32])
    cpool = kb.sb("cpool", [128, 34])
    cmask = kb.sb("cmask", [128, 3, 128])
    modA = kb.sb("modA", [128, 3, KC])
    modS = kb.sb("modS", [128, 3, KC])
    lp = kb.sb("lp", [128, 64])
    lruBD = kb.sb("lruBD", [128, 2, 2, 2, 128])
    poolBD = kb.sb("poolBD", [128, 2, 128])
    wukv = kb.sb("wukv", [128, 512], BF16)
    wuq = kb.sb("wuq", [128, 2, 384], BF16)
    gains = kb.sb("gains", [128, 2, 96])
    wglu = kb.sb("wglu", [128, 2, 256], BF16)
    AR = Arena(kb, 28000)
    AR2 = Arena(kb, 3 * NPOS, ap=bigbr.ap[:, 0:6 * NPOS].bitcast(F32))

    dma("sp", identf, c_ident)
    cp("dve", identb, identf)
    dma("sp", ropeC, c_ropeC.re("(i p) f -> p i f", p=128))
    dma("sp", ropeS, c_ropeS.re("(i p) f -> p i f", p=128))
    dma("sp", cpool, c_pool)
    dma("sp", cmask, c_mask)

    LP_CW = 0
    LP_CB = 8
    LP_NSP = 10
    LP_NSP2 = 14
    LP_BA = 18
    LP_BX = 22
    LP_PSC = 26
    LP_PBS = 28
    LP_G = 30
    LP_TMP = 40

    def prep_layer(l):
        AR.reset(top=True)
        cT = AR.alloc("cT", [KC, 3])
        for v in range(3):
            dma("sp", cT[:, :, v], cvec[v].re("(c p) -> p c", p=128), slow=True)
        cact = AR.alloc("cact", [KC, 3])
        act(cact, cT, AF.Silu)
        brow = AR.alloc("brow", [3 * D])
        dma("sp", brow[0:3, :], b_ada[l:l + 1, :].re("o n -> (o n)").pbc(3))
        modrow = AR.alloc("modrow", [3 * D])
        wa = [AR.alloc("wa%d" % i, [KC, 512]) for i in range(4)]
        for cb in range(4):
            dma("sp" if cb % 2 == 0 else "act", wa[cb], w_ada[l, :, cb * 512:(cb + 1) * 512].re("(c p) n -> p c n", p=128))
        for cb in range(6):
            w = wa[cb % 4]
            if cb >= 4:
                dma("sp" if cb % 2 == 0 else "act", w, w_ada[l, :, cb * 512:(cb + 1) * 512].re("(c p) n -> p c n", p=128))
            ps = kb.psum_f()
            for k in range(KC):
                mm(ps[0:3, :], cact[:, k, :], w[:, k, :], k == 0, k == KC - 1)
            tt("dve", modrow[0:3, cb * 512:(cb + 1) * 512], ps[0:3, :], brow[0:3, cb * 512:(cb + 1) * 512], ALU.add)
        dma("sp", modD[l], modrow[0:3, :])
        sc = AR.alloc("sc", [3, KC])
        for v in range(3):
            dma("sp", modS[:, v, :], modD[l, v, 0:D].re("(c p) -> p c", p=128), slow=True)
            dma("sp", sc[:, v, :], modD[l, v, D:2 * D].re("(c p) -> p c", p=128), slow=True)
        dma("sp", lp[:, LP_G:LP_G + 8], norm_g[l].re("(c p) -> p c", p=128), slow=True)
        for v in range(3):
            stt(modA[:, v, :], sc[:, v, :], 1.0, lp[:, LP_G:LP_G + 8], ALU.add, ALU.mult)
        for k in range(4):
            dma("sp", lp[:, LP_CW:LP_CW + 8].re("p (c k) -> p c k", c=2)[:, :, k], lru_conv_w[l, k].re("(c p) -> p c", p=128), slow=True)
        dma("sp", lp[:, LP_CB:LP_CB + 2], lru_conv_b[l].re("(c p) -> p c", p=128), slow=True)
        lam = lp[:, LP_TMP:LP_TMP + 4]
        for d in range(2):
            dma("sp", lam[:, d * 2:d * 2 + 2], lru_lambda[l, d].re("(c p) -> p c", p=128), slow=True)
            dma("sp", lp[:, LP_BA + d * 2:LP_BA + d * 2 + 2], lru_b_a[l, d].re("(c p) -> p c", p=128), slow=True)
            dma("sp", lp[:, LP_BX + d * 2:LP_BX + d * 2 + 2], lru_b_x[l, d].re("(c p) -> p c", p=128), slow=True)
        t0 = lp[:, LP_TMP + 4:LP_TMP + 8]
        t1 = lp[:, LP_TMP + 8:LP_TMP + 12]
        t2 = lp[:, LP_TMP + 12:LP_TMP + 16]
        t3 = lp[:, LP_TMP + 16:LP_TMP + 20]
        ts("dve", t0, lam, -1.0, ALU.mult)
        tt("dve", t0, t0, lam, ALU.max)
        act(t1, t0, AF.Exp, scale=-1.0)
        ts("dve", t2, t1, 2.0, ALU.add)
        kb.recip(t2, t2)
        tt("dve", t2, t2, t1, ALU.mult)
        tt("dve", t3, t2, t2, ALU.mult)
        ts("dve", t0, t3, 1.0 / 11.0, ALU.mult, 1.0 / 9.0, ALU.add)
        for cf in (1.0 / 7.0, 1.0 / 5.0, 1.0 / 3.0, 1.0):
            tt("dve", t0, t0, t3, ALU.mult)
            ts("dve", t0, t0, cf, ALU.add)
        tt("dve", t0, t0, t2, ALU.mult)
        ts("dve", t1, lam, -1.0, ALU.mult, 0.0, ALU.max)
        stt(t0, t0, 2.0, t1, ALU.mult, ALU.add)
        ts("dve", lp[:, LP_NSP:LP_NSP + 4], t0, -8.0, ALU.mult)
        ts("dve", lp[:, LP_NSP2:LP_NSP2 + 4], t0, -16.0, ALU.mult)
        kb.memset("dve", lruBD, 0.0)
        for d in range(2):
            for gi, wsrc in enumerate((lru_w_a, lru_w_x)):
                for c in range(2):
                    for h in range(2):
                        dma("sp", lruBD[h * 64:(h + 1) * 64, d, gi, c, h * 64:(h + 1) * 64], wsrc[l, d, 2 * c + h])
        kb.memset("dve", poolBD, 0.0)
        for c in range(2):
            for h in range(2):
                dma("sp", poolBD[h * 64:(h + 1) * 64, c, h * 64:(h + 1) * 64], pool_w[l, 2 * c + h])
        dma("sp", lp[:, LP_PSC:LP_PSC + 2], pool_scale[l].re("(c p) -> p c", p=128), slow=True)
        dma("sp", lp[:, LP_PBS:LP_PBS + 2], pool_b[l].re("(c p) -> p c", p=128), slow=True)
        tt("dve", lp[:, LP_PBS:LP_PBS + 2], lp[:, LP_PBS:LP_PBS + 2], lp[:, LP_PSC:LP_PSC + 2], ALU.mult)
        kvn = lp[:, LP_TMP + 20:LP_TMP + 21]
        qn = lp[:, LP_TMP + 21:LP_TMP + 23]
        dma("sp", kvn, mla_kv_norm[l].re("(p o) -> p o", o=1), slow=True)
        dma("sp", qn, mla_q_norm[l].re("(c p) -> p c", p=128), slow=True)
        wtmp = AR.alloc("wtmp", [2, 512])
        dma("sp", wtmp[:, 0, :], mla_w_ukv[l])
        ts("dve", wukv, wtmp[:, 0, :], kvn, ALU.mult)
        wtmp2 = AR.alloc("wtmp2", [2, 384])
        dma("sp", wtmp2, mla_w_uq[l].re("(c p) n -> p c n", p=128))
        for c in range(2):
            ts("dve", wuq[:, c, :], wtmp2[:, c, :], qn[:, c:c + 1], ALU.mult)
        dma("sp", gains[:, 0, :], mla_q_gain[l:l + 1, :].re("o n -> (o n)").pbc(128))
        dma("sp", gains[:, 1, :], mla_k_gain[l:l + 1, :].re("o n -> (o n)").pbc(128))
        ts("dve", gains[:, 0, :], gains[:, 0, :], 96.0 ** -0.5, ALU.mult)
        dma("pool", wglu, s5_w_glu[l].re("(c p) n -> p c n", p=128))

    def src_tile(l, b, i):
        if i < 2:
            return (ctx_in if l == 0 else xcmid)[b, i * 128:(i + 1) * 128, :]
        return (x_in if l == 0 else xmid)[b, (i - 2) * 128:(i - 1) * 128, :]

    def phase_norm(l, b):
        AR.reset(top=True)
        NX = 4
        xt = [AR.alloc("xt%d" % i, [D]) for i in range(NX)]
        junk = [AR.alloc("junk%d" % i, [D]) for i in range(NX)]
        xn = [AR.alloc("xn%d" % i, [D], BF16) for i in range(NX)]
        st4 = [AR.alloc("st%d" % i, [4]) for i in range(NX)]
        def tile_gen(i):
            v = 2 if i < 2 else b
            x_t, xn_t, s4 = xt[i % NX], xn[i % NX], st4[i % NX]
            dma("sp", x_t, src_tile(l, b, i))
            yield
            act(junk[i % NX], x_t, AF.Square, accum=s4[:, 0:1])
            yield
            ts("dve", s4[:, 1:2], s4[:, 0:1], 1.0 / D, ALU.mult, EPS, ALU.add)
            yield
            act(s4[:, 2:3], s4[:, 1:2], AF.Sqrt)
            yield
            kb.recip(s4[:, 3:4], s4[:, 2:3])
            ts("dve", xn_t, x_t, s4[:, 3:4], ALU.mult)
            yield
            ps = kb.psum_any(bf16=True)
            for c in range(KC):
                tr(ps[:, c * 128:(c + 1) * 128], xn_t[:, c * 128:(c + 1) * 128], identb)
            yield
            for c in range(KC):
                o = hT[:, c, i * 128:(i + 1) * 128]
                if c % 2 == 0:
                    act(o, ps[:, c * 128:(c + 1) * 128], AF.Identity, bias=modS[:, v, c:c + 1], scale=modA[:, v, c:c + 1])
                else:
                    ts("dve", o, ps[:, c * 128:(c + 1) * 128], modA[:, v, c:c + 1], ALU.mult, modS[:, v, c:c + 1], ALU.add)
            yield
        for g0 in range(0, 18, NX):
            lockstep([tile_gen(i) for i in range(g0, min(g0 + NX, 18))])

    def load_win(l, name, c0, ncols, top=False):
        w = AR.alloc(name, [KC, ncols], BF16, top=top)
        dma("pool", w, w_in[l, :, c0:c0 + ncols].re("(c p) n -> p c n", p=128))
        return w

    def proj_fm(w, col0, dst, p0, p1, evac):
        pos = p0
        while pos < p1:
            n = min(512, p1 - pos)
            ps = kb.psum_f()
            for k in range(KC):
                mm(ps[:, 0:n], w[:, k, col0:col0 + 128], hT[:, k, pos:pos + n], k == 0, k == KC - 1)
            evac(ps, pos, n)
            pos += n

    def phase_lru(l, b, with_ctx):
        AR.reset()
        w = load_win(l, "w_lru", O_LRU, 256)
        xr = AR.alloc("xr", [2, NPOS])
        xc_ = AR.alloc("xc", [2, NPOS])
        ysum = AR.alloc("ysum", [2, NPOS])
        NB_ = 768
        tmp_sets = [{nm: AR.alloc("%s%d" % (nm, q), [NB_]) for nm in ("r", "i", "a", "a2", "bb")} for q in range(2)]
        for c in range(2):
            proj_fm(w, c * 128, xr, 0, NPOS, lambda ps, pos, n, c=c: cp("act", xr[:, c, pos:pos + n], ps[:, 0:n]))
        for c in range(2):
            cw = lambda k: lp[:, LP_CW + c * 4 + k:LP_CW + c * 4 + k + 1]
            for (s0, s1) in ((0, TC), (TC, NPOS)):
                ts("dve", xc_[:, c, s0:s1], xr[:, c, s0:s1], cw(2), ALU.mult, lp[:, LP_CB + c:LP_CB + c + 1], ALU.add)
                stt(xc_[:, c, s0 + 1:s1], xr[:, c, s0:s1 - 1], cw(1), xc_[:, c, s0 + 1:s1], ALU.mult, ALU.add)
                stt(xc_[:, c, s0 + 2:s1], xr[:, c, s0:s1 - 2], cw(0), xc_[:, c, s0 + 2:s1], ALU.mult, ALU.add)
                stt(xc_[:, c, s0:s1 - 1], xr[:, c, s0 + 1:s1], cw(3), xc_[:, c, s0:s1 - 1], ALU.mult, ALU.add)
        blocks = [(0, TC)] + [(TC + j * NB_, min(TC + (j + 1) * NB_, NPOS)) for j in range((T + NB_ - 1) // NB_)]

        def chain(d, c):
            dc = d * 2 + c
            tmp = tmp_sets[d]
            out = ysum if d == 0 else xr
            order = blocks if d == 0 else [blocks[0]] + blocks[:0:-1]
            prev = None
            for (s0, s1) in order:
                n = s1 - s0
                r_, i_, a_, a2_, bb_ = (tmp[k][:, 0:n] for k in ("r", "i", "a", "a2", "bb"))
                for gi, dst in ((0, r_), (1, i_)):
                    q = 0
                    while q < n:
                        m = min(512, n - q)
                        ps = kb.psum_f()
                        mm(ps[:, 0:m], lruBD[:, d, gi, c, :], xc_[:, c, s0 + q:s0 + q + m], True, True)
                        bcol = (LP_BA if gi == 0 else LP_BX) + dc
                        act(dst[:, q:q + m], ps[:, 0:m], AF.Sigmoid, bias=lp[:, bcol:bcol + 1])
                        q += m
                yield
                act(a_, r_, AF.Exp, scale=lp[:, LP_NSP + dc:LP_NSP + dc + 1])
                act(a2_, r_, AF.Exp, scale=lp[:, LP_NSP2 + dc:LP_NSP2 + dc + 1])
                tt("pool", bb_, i_, xc_[:, c, s0:s1], ALU.mult)
                yield
                ts("dve", a2_, a2_, -1.0, ALU.mult, 1.0, ALU.add)
                yield
                act(a2_, a2_, AF.Sqrt)
                yield
                tt("dve", bb_, bb_, a2_, ALU.mult)
                init = 0.0 if prev is None else prev
                if d == 0:
                    kb.scan(out[:, c, s0:s1], a_, bb_, init)
                    prev = out[:, c, s1 - 1:s1]
                else:
                    kb.scan(out[:, c, s0:s1][:, ::-1], a_[:, ::-1], bb_[:, ::-1], init)
                    prev = out[:, c, s0:s0 + 1]
                yield
        for c in range(2):
            lockstep([chain(0, c), chain(1, c)])
        for c in range(2):
            lo = 0 if with_ctx else TC
            tt("dve", brT[2][c][:, lo:NPOS], ysum[:, c, lo:NPOS], xr[:, c, lo:NPOS], ALU.add)

    AR_car = [kb.sb("car%d" % i, [128, 1]) for i in range(2)]

    def phase_pool(l, b, with_ctx):
        AR.reset()
        w = load_win(l, "w_pool", O_POOL, 256)
        xp = AR.alloc("xp", [2, NPOS])
        cs = AR.alloc("cs", [2, NPOS + 2 * 17 + 2])
        pm = AR.alloc("pm", [2, NPOS])
        ones = AR.alloc("ones", [T])
        kb.memset("pool", ones, 1.0)
        for c in range(2):
            proj_fm(w, c * 128, xp, 0, NPOS, lambda ps, pos, n, c=c: cp("act", xp[:, c, pos:pos + n], ps[:, 0:n]))
        segs = [(0, TC, 0)] + [(TC, NPOS, TC + 17)]
        if not with_ctx:
            segs = segs[1:]
        for c in range(2):
            for (s0, s1, o0) in segs:
                L = s1 - s0
                kb.memset("pool", cs[:, c, o0:o0 + 9], 0.0)
                kb.scan(cs[:, c, o0 + 9:o0 + 9 + L], ones[:, 0:L], xp[:, c, s0:s1], 0.0)
                ts("dve", cs[:, c, o0 + 9 + L:o0 + 17 + L], cs[:, c, o0:o0 + 8], cs[:, c, o0 + 8 + L:o0 + 9 + L], ALU.add)
                for h in range(2):
                    hw = (1, 2, 4, 8)[2 * c + h]
                    rows = slice(h * 64, (h + 1) * 64)
                    base = o0 + 8
                    tt("dve", pm[rows, c, s0:s1], cs[rows, c, base + hw:base + hw + L], cs[rows, c, base - hw:base - hw + L], ALU.subtract)
                ts("dve", pm[:, c, s0:s1], pm[:, c, s0:s1], cpool[:, c:c + 1], ALU.mult)
                tt("dve", pm[:, c, s0:s0 + 8], pm[:, c, s0:s0 + 8], cpool[:, 2 + c * 8:2 + c * 8 + 8], ALU.mult)
                tt("dve", pm[:, c, s1 - 8:s1], pm[:, c, s1 - 8:s1], cpool[:, 18 + c * 8:18 + c * 8 + 8], ALU.mult)
                tt("pool", pm[:, c, s0:s1], pm[:, c, s0:s1], xp[:, c, s0:s1], ALU.subtract)
                pos = s0
                while pos < s1:
                    n = min(512, s1 - pos)
                    ps = kb.psum_f()
                    mm(ps[:, 0:n], poolBD[:, c, :], pm[:, c, pos:pos + n], True, True)
                    act(brT[3][c][:, pos:pos + n], ps[:, 0:n], AF.Identity, bias=lp[:, LP_PBS + c:LP_PBS + c + 1],
                        scale=lp[:, LP_PSC + c:LP_PSC + c + 1])
                    pos += n

    def phase_mla(l, b, with_ctx):
        AR.reset()
        wkv = load_win(l, "w_kv", 0, 160)
        wq = load_win(l, "w_cq", O_CQ, 256)
        qT = AR.alloc("qT", [4, NPOS], BF16)
        kT = AR.alloc("kT", [4, NPOS], BF16)
        Vt = AR.alloc("Vt", [18, 4, 68], BF16)
        lo_fixed = AR.lo
        kb.memset("dve", Vt.re("p a b c -> p (a b c)"), 1.0)
        for h_ in range(4):
            kb.memset("pool", qT[:, h_, :], 0.0)
            kb.memset("pool", kT[:, h_, :], 0.0)
        NS = 3
        sm = [AR.alloc("sm%d" % i, [32]) for i in range(NS)]
        kr = [AR.alloc("kr%d" % i, [32]) for i in range(NS)]
        cn = [AR.alloc("cn%d" % i, [384], BF16) for i in range(NS)]
        cnT = [AR.alloc("cnT%d" % i, [3, 128], BF16) for i in range(NS)]
        sq = [AR.alloc("sq%d" % i, [704]) for i in range(NS)]
        qk = [AR.alloc("qk%d" % i, [2, 4, 96]) for i in range(NS)]
        rg = [AR.alloc("rg%d" % i, [2, 4, 96]) for i in range(NS)]
        qkb = [AR.alloc("qkb%d" % i, [2, 4, 96], BF16) for i in range(NS)]
        rt = [AR.alloc("rt%d" % i, [2, 2, 4, 32]) for i in range(NS)]
        kvs_ = [AR.alloc("kvs%d" % i, [512]) for i in range(NS)]
        qs_ = [AR.alloc("qs%d" % i, [384]) for i in range(NS)]

        def info(i):
            is_ctx = i < 2
            do_q = (not is_ctx) or with_ctx
            return is_ctx, do_q, i % NS, slice(i * 128, (i + 1) * 128)

        def stage_a(i):
            is_ctx, do_q, j, pos = info(i)
            s, cn_, cnT_, sq_ = sm[j], cn[j], cnT[j], sq[j]
            ps1 = kb.psum_any()
            for k in range(KC):
                mm(ps1[:, 0:160], hT[:, k, pos], wkv[:, k, :], k == 0, k == KC - 1)
            if do_q:
                for k in range(KC):
                    mm(ps1[:, 160:416], hT[:, k, pos], wq[:, k, :], k == 0, k == KC - 1)
            yield
            act(sq_[:, 0:128], ps1[:, 0:128], AF.Square, accum=s[:, 0:1])
            if do_q:
                act(sq_[:, 128:384], ps1[:, 160:416], AF.Square, accum=s[:, 1:2])
            else:
                kb.memset("pool", s[:, 1:2], 1.0)
            cp("act", kr[j], ps1[:, 128:160])
            yield
            act(s[:, 2:3], s[:, 0:1], AF.Sqrt, scale=1.0 / 128, bias=EPS)
            act(s[:, 3:4], s[:, 1:2], AF.Sqrt, scale=1.0 / 256, bias=EPS)
            yield
            kb.recip(s[:, 4:6], s[:, 2:4])
            ts("dve", cn_[:, 0:128], ps1[:, 0:128], s[:, 4:5], ALU.mult)
            if do_q:
                ts("dve", cn_[:, 128:384], ps1[:, 160:416], s[:, 5:6], ALU.mult)
            yield
            psT = kb.psum_any(bf16=True)
            for c in range(3 if do_q else 1):
                tr(psT[:, c * 128:(c + 1) * 128], cn_[:, c * 128:(c + 1) * 128], identb)
            yield
            cp("act", cnT_[:, 0:(3 if do_q else 1), :], psT[:, 0:(384 if do_q else 128)].re("p (c n) -> p c n", n=128))
            yield

        def stage_b(i):
            is_ctx, do_q, j, pos = info(i)
            s, cnT_, sq_, qk_, qkb_, rt_, rg_ = sm[j], cnT[j], sq[j], qk[j], qkb[j], rt[j], rg[j]
            pskv = kb.psum_any()
            mm(pskv, cnT_[:, 0, :], wukv, True, True)
            if do_q:
                psq = kb.psum_any()
                for c in range(2):
                    mm(psq[:, 0:384], cnT_[:, 1 + c, :], wuq[:, c, :], c == 0, c == 1)
            yield
            cp("act", kvs_[j], pskv)
            if do_q:
                cp("act", qs_[j], psq[:, 0:384])
            kv3 = kvs_[j].re("p (h e) -> p h e", h=4)
            q3 = qs_[j].re("p (h e) -> p h e", h=4)
            krope = kr[j]
            act(sq_[:, 640:672], krope, AF.Square, accum=s[:, 7:8])
            cp("act", Vt[:, i, :, 0:64], kv3[:, :, 64:128])
            yield
            ksq = sq_[:, 0:256].re("p (h e) -> p h e", h=4)
            qsq = sq_[:, 256:640].re("p (h e) -> p h e", h=4)
            tt("pool", ksq, kv3[:, :, 0:64], kv3[:, :, 0:64], ALU.mult)
            if do_q:
                tt("pool", qsq, q3, q3, ALU.mult)
            yield
            kb.reduce(s[:, 12:16], ksq)
            if do_q:
                kb.reduce(s[:, 8:12], qsq)
            else:
                kb.memset("dve", s[:, 8:12], 1.0)
            ts("dve", s[:, 12:16], s[:, 12:16], s[:, 7:8], ALU.add)
            yield
            act(s[:, 8:16], s[:, 8:16], AF.Sqrt, scale=1.0 / 96, bias=EPS)
            yield
            kb.recip(s[:, 16:24], s[:, 8:16])
            yield
            tt("pool", rg_, s[:, 16:24].re("p (w h) -> p w h", w=2).us(3).bc([128, 2, 4, 96]),
               gains.us(2).bc([128, 2, 4, 96]), ALU.mult)
            yield
            if do_q:
                tt("dve", qk_[:, 0], q3, rg_[:, 0], ALU.mult)
            else:
                kb.memset("pool", qk_[:, 0], 0.0)
            tt("dve", qk_[:, 1, :, 0:64], kv3[:, :, 0:64], rg_[:, 1, :, 0:64], ALU.mult)
            tt("pool", qk_[:, 1, :, 64:96], krope.us(1).bc([128, 4, 32]), rg_[:, 1, :, 64:96], ALU.mult)
            yield
            if not is_ctx:
                ti = i - 2
                v = qk_[:, :, :, 64:96]
                t1_ = rt_[:, 0]
                t2_ = rt_[:, 1]
                Cb = ropeC[:, ti, :].us(1).us(1).bc([128, 2, 4, 32])
                tt("pool", t1_, v, Cb, ALU.mult)
                for a in range(2):
                    for s_ in range(2):
                        o_ = t2_[:, :, :, a * 16 + s_ * 8:a * 16 + s_ * 8 + 8]
                        i_ = qk_[:, :, :, 64 + a * 16 + (1 - s_) * 8:64 + a * 16 + (1 - s_) * 8 + 8]
                        Sb = ropeS[:, ti, a * 16 + s_ * 8:a * 16 + s_ * 8 + 8].us(1).us(1).bc([128, 2, 4, 8])
                        tt("dve" if (a + s_) % 2 == 0 else "pool", o_, i_, Sb, ALU.mult)
                yield
                tt("dve", v, t1_, t2_, ALU.add)
            cp("dve", qkb_, qk_)
            yield

        def stage_c(i):
            is_ctx, do_q, j, pos = info(i)
            qkb_ = qkb[j]
            psT2 = kb.psum_any(bf16=True)
            for w_ in range(2):
                if w_ == 0 and not do_q:
                    continue
                for h in range(4):
                    tr(psT2[0:96, (w_ * 4 + h) * 128:(w_ * 4 + h + 1) * 128], qkb_[:, w_, h, :], identb)
            yield
            if do_q:
                cp("act", qT[0:96, :, pos], psT2[0:96, 0:512].re("p (h n) -> p h n", h=4))
            cp("act", kT[0:96, :, pos], psT2[0:96, 512:1024].re("p (h n) -> p h n", h=4))

        def tile_gen(i):
            yield from stage_a(i)
            yield from stage_b(i)
            yield from stage_c(i)
        for g0 in range(0, 18, NS):
            lockstep([tile_gen(i) for i in range(g0, g0 + NS)])
        if "mla_stop1" in dbg:
            return
        P.barrier()
        AR.lo = lo_fixed
        omla = AR.alloc("omla", [18, 256], BF16)
        pT = [AR.alloc("pT%d" % i, [512], BF16) for i in range(3)]
        rc = [AR.alloc("rc%d" % i, [4]) for i in range(2)]
        jobs = []
        if with_ctx:
            jobs.append((0, TC, 0, 2))
        for qb in range(4):
            jobs.append((TC + qb * 512, TC + (qb + 1) * 512, 0, 18))
        pi = 0
        for (q0, q1, kt0, kt1) in jobs:
            nq = q1 - q0
            nqt = nq // 128
            for h in range(4):
                pso = [kb.psf[2 + t_] for t_ in range(nqt)]

                def s_mm(kt):
                    mm(kb.psf[kt % 2][:, 0:nq], kT[:, h, kt * 128:(kt + 1) * 128], qT[:, h, q0:q1], True, True)
                s_mm(kt0)
                for kt in range(kt0, kt1):
                    if kt + 1 < kt1:
                        s_mm(kt + 1)
                    pss = kb.psf[kt % 2]
                    p_ = pT[pi % 3]
                    pi += 1
                    act(p_[:, 0:nq], pss[:, 0:nq], AF.Exp)
                    for t_ in range(nqt):
                        mm(pso[t_][:, 0:68], p_[:, t_ * 128:(t_ + 1) * 128], Vt[:, kt, h, :], kt == kt0, kt == kt1 - 1)
                for t_ in range(nqt):
                    r_ = rc[t_ % 2]
                    kb.recip(r_[:, 0:1], pso[t_][:, 64:65])
                    ts("dve", omla[:, (q0 // 128) + t_, h * 64:(h + 1) * 64], pso[t_][:, 0:64], r_[:, 0:1], ALU.mult)
        for i in range(0 if with_ctx else 2, 18):
            psT = kb.psum_b()
            for c in range(2):
                tr(psT[:, c * 128:(c + 1) * 128], omla[:, i, c * 128:(c + 1) * 128], identb)
            for c in range(2):
                cp("act", brT[0][c][:, i * 128:(i + 1) * 128], psT[:, c * 128:(c + 1) * 128])

    def phase_s5(l, b, with_ctx):
        AR.reset()
        AR2.lo = 0
        U = AR.alloc("U", [16, NCH])
        SN = [[AR.alloc("SN%d%d" % (d, ri), [8, NCH + 2]) for ri in range(2)] for d in range(2)]
        Ere = AR.alloc("Ere", [2, 8, 128])
        nEim = AR.alloc("nEim", [2, 8, 128])
        W3 = AR.alloc("W3", [16, 128])
        scr0 = AR.lo
        ws5 = load_win(l, "w_s5", O_S5, 256)
        Uc = AR.alloc("Uc", [16, 8, 16])
        kblocks = [(0, 32, 0), (32, 128, TC), (160, 128, TC + 1024)]
        for (k0, M, p0) in kblocks:
            pss = [kb.psum_f() for _ in range(4)]
            for r in range(8):
                ps = pss[r // 2]
                for k in range(KC):
                    mm(ps[0:M, (r % 2) * 256:(r % 2 + 1) * 256], hT[:, k, p0 + r:p0 + 8 * M:8], ws5[:, k, :], k == 0, k == KC - 1)
            for q in range(4):
                src = pss[q][0:M, :].re("p (r g i) -> p r g i", r=2, g=16)
                dst = Uc[0:M, :, 2 * q:2 * q + 2, :].re("p g r i -> p r g i")
                cp("act" if q % 2 == 0 else "dve", dst, src)
            for gq in range(4):
                ps = kb.psum_f()
                for gg_ in range(4):
                    g = gq * 4 + gg_
                    tr(ps[:, gg_ * 128:gg_ * 128 + M], Uc[0:M, g, :, :].re("p r i -> p (r i)"), identf[0:M, 0:M])
                cp("act" if gq % 2 == 0 else "dve", U[:, gq * 4:gq * 4 + 4, k0:k0 + M],
                   ps.re("p (g n) -> p g n", g=4)[:, :, 0:M])
        if "s5_U" in dbg:
            dma("sp", tap("s5U", [128, 16 * NCH]), U.re("p g k -> p (g k)"))
        P.barrier()
        AR.lo = scr0
        if "s5_stop1" in dbg:
            return
        cached = (b > 0) and ("s5_nocache" not in dbg)
        if not cached:
            kb.memset("pool", W3, 0.0)
        else:
            dma("sp", Ere.re("p d g n -> p (d g n)"), st_main[0])
            dma("act", nEim.re("p d g n -> p (d g n)"), st_main[1])
            dma("sp", W3.re("p g n -> p (g n)"), st_main[2])
        prm = AR.alloc("prm", [40, 8])
        PW = [AR.alloc("pw%d" % i, [10, 8]) for i in range(2)]
        Bb = [AR.alloc("Bb%d" % i, [8, 16]) for i in range(2)]
        Braw = [AR.alloc("Braw%d" % i, [8, 16]) for i in range(2)]
        Craw = [AR.alloc("Craw%d" % i, [8, 16]) for i in range(2)]
        Dbc = AR.alloc("Dbc", [256])
        t8 = AR.alloc("t8", [8, 16])
        w3t = [AR.alloc("w3t%d" % i, [128]) for i in range(2)]
        tE = AR.alloc("tE", [8, 128])
        Fm = [AR.alloc("F%d" % i, [8, 8, 16]) for i in range(2)]
        Ep = [AR.alloc("Ep%d" % i, [8, 128]) for i in range(2)]
        Cnat = [AR.alloc("Cnat%d" % i, [16, 64], at=Ep[i].off, buf=Ep[i].buf) for i in range(2)]
        rs_sets = [[AR.alloc("rs%d_%d" % (q, i), [NCH], at=Fm[0].off + (q * 7 + i) * NCH) for i in range(6)]
                   for q in range(2)]
        rho_sets = [AR.alloc("rho1_%d" % q, [NCH], at=Fm[0].off + (q * 7 + 6) * NCH) for q in range(2)]
        assert Fm[0].off + 14 * NCH <= Ep[1].off + Ep[1].w
        W1 = [AR2.alloc("W1%d" % i, [16, 64]) for i in range(2)]
        tab = [AR2.alloc("tab%d" % i, [8, NCH]) for i in range(2)]
        dma("sp", Dbc, s5_d[l:l + 1, :].re("o n -> (o n)").pbc(128))
        dsel = cmask[:, 2, :]

        def pv(i):
            return prm[:, i, :]

        for d in range(2):
            if d == 1:
                P.barrier()
            if not cached:
                dma("sp", pv(0), s5_a_re[l, d].re("(gp h) p -> (h p) gp", h=2), slow=True)
                dma("act", pv(1), s5_a_im[l, d].re("(gp h) p -> (h p) gp", h=2), slow=True)
                ldt = t8[:, 0, :]
                dma("sp", ldt, s5_log_dt[l, d:d + 1, :].re("o g -> (o g)").pbc(128))
                dma("sp", Cnat[0][0:16], s5_c_re[l, d].re("g o p -> o g p"))
                dma("act", Cnat[1][0:16], s5_c_im[l, d].re("g o p -> o g p"))
                for h in range(2):
                    rows = slice(h * 64, (h + 1) * 64)
                    cp("dve", prm[rows, 2, :], ldt[rows, h:16:2])
                    dma("sp", Braw[0][rows], s5_b_re[l, d].re("(gp h) p i -> h p gp i", h=2)[h])
                    dma("act", Braw[1][rows], s5_b_im[l, d].re("(gp h) p i -> h p gp i", h=2)[h])
                for ri in range(2):
                    ps = kb.psum_f()
                    for g in range(16):
                        gp, h = g // 2, g % 2
                        mm(ps[h * 64:(h + 1) * 64, gp * 16:(gp + 1) * 16], Cnat[ri][0:16, g, :], identf[0:16, 0:16], True, True)
                    cp("act", Craw[ri], ps[:, 0:128].re("p (g o) -> p g o", g=8))
                if "s5_b1" in dbg:
                    return
                act(pv(2), pv(2), AF.Exp)
                tt("dve", pv(3), pv(0), pv(2), ALU.mult)
                tt("dve", pv(4), pv(1), pv(2), ALU.mult)
                act(pv(5), pv(3), AF.Exp)
                for (dst, shift) in ((6, 0.0), (7, math.pi / 2)):
                    xx, kk, ki = pv(30), pv(31), prm[:, 32, :]
                    ts("dve", xx, pv(4), shift, ALU.add)
                    ts("dve", kk, xx, 1.0 / (2 * math.pi), ALU.mult)
                    kint = TV(ki.ap.bitcast(I32), ki.buf)
                    cp("dve", kint, kk)
                    cp("dve", kk, kint)
                    stt(xx, kk, -6.28125, xx, ALU.mult, ALU.add)
                    stt(xx, kk, -(2 * math.pi - 6.28125), xx, ALU.mult, ALU.add)
                    ts("dve", xx, xx, math.pi, ALU.min, -math.pi, ALU.max)
                    act(pv(dst), xx, AF.Sin)
                kb.memset("dve", PW[0][:, 0, :], 1.0)
                kb.memset("dve", PW[1][:, 0, :], 0.0)
                tt("dve", PW[0][:, 1, :], pv(5), pv(7), ALU.mult)
                tt("dve", PW[1][:, 1, :], pv(5), pv(6), ALU.mult)
                for j in range(2, 9):
                    tt("dve", pv(30), PW[0][:, j - 1, :], PW[0][:, 1, :], ALU.mult)
                    tt("dve", pv(31), PW[1][:, j - 1, :], PW[1][:, 1, :], ALU.mult)
                    tt("dve", PW[0][:, j, :], pv(30), pv(31), ALU.subtract)
                    tt("dve", pv(30), PW[0][:, j - 1, :], PW[1][:, 1, :], ALU.mult)
                    tt("dve", pv(31), PW[1][:, j - 1, :], PW[0][:, 1, :], ALU.mult)
                    tt("dve", PW[1][:, j, :], pv(30), pv(31), ALU.add)
                act(pv(8), pv(3), AF.Exp, scale=-16.0)
                tt("dve", PW[0][:, 9, :], PW[0][:, 8, :], pv(8), ALU.mult)
                tt("dve", PW[1][:, 9, :], PW[1][:, 8, :], pv(8), ALU.mult)
                ts("dve", PW[1][:, 9, :], PW[1][:, 9, :], -1.0, ALU.mult)
                act(pv(9), pv(3), AF.Exp, scale=8.0)
                act(pv(10), pv(3), AF.Exp, scale=-8.0)
                tt("dve", pv(11), PW[0][:, 8, :], pv(10), ALU.mult)
                tt("dve", pv(12), PW[1][:, 8, :], pv(10), ALU.mult)
                ts("dve", pv(13), PW[0][:, 1, :], -1.0, ALU.add)
                tt("dve", pv(14), pv(0), pv(0), ALU.mult)
                tt("dve", pv(15), pv(1), pv(1), ALU.mult)
                tt("dve", pv(14), pv(14), pv(15), ALU.add)
                kb.recip(pv(14), pv(14))
                tt("dve", pv(15), pv(13), pv(0), ALU.mult)
                tt("dve", pv(16), PW[1][:, 1, :], pv(1), ALU.mult)
                tt("dve", pv(15), pv(15), pv(16), ALU.add)
                tt("dve", pv(15), pv(15), pv(14), ALU.mult)
                tt("dve", pv(16), PW[1][:, 1, :], pv(0), ALU.mult)
                tt("dve", pv(17), pv(13), pv(1), ALU.mult)
                tt("dve", pv(16), pv(16), pv(17), ALU.subtract)
                tt("dve", pv(16), pv(16), pv(14), ALU.mult)
                fre = pv(15).us(2).bc([128, 8, 16])
                fim = pv(16).us(2).bc([128, 8, 16])
                tt("dve", Bb[0], Braw[0], fre, ALU.mult)
                tt("dve", t8, Braw[1], fim, ALU.mult)
                tt("dve", Bb[0], Bb[0], t8, ALU.subtract)
                tt("dve", Bb[1], Braw[1], fre, ALU.mult)
                tt("dve", t8, Braw[0], fim, ALU.mult)
                tt("dve", Bb[1], Bb[1], t8, ALU.add)
                for t_ in range(8):
                    f_ = t_ + 1 if d == 0 else 8 - t_
                    pr = PW[0][:, f_, :].us(2).bc([128, 8, 16])
                    pi_ = PW[1][:, f_, :].us(2).bc([128, 8, 16])
                    eo = Ere[:, d, :, t_ * 16:(t_ + 1) * 16]
                    ei = nEim[:, d, :, t_ * 16:(t_ + 1) * 16]
                    tt("dve", eo, Craw[0], pr, ALU.mult)
                    tt("dve", t8, Craw[1], pi_, ALU.mult)
                    tt("dve", eo, eo, t8, ALU.subtract)
                    tt("dve", ei, Craw[0], pi_, ALU.mult)
                    tt("dve", t8, Craw[1], pr, ALU.mult)
                    tt("dve", ei, ei, t8, ALU.add)
                    ts("dve", ei, ei, -1.0, ALU.mult)
                for r in range(8):
                    e_ = 7 - r if d == 0 else r
                    pr = PW[0][:, e_, :].us(2).bc([128, 8, 16])
                    pi_ = PW[1][:, e_, :].us(2).bc([128, 8, 16])
                    tt("dve", Fm[0][:, :, r, :], Bb[0], pr, ALU.mult)
                    tt("dve", t8, Bb[1], pi_, ALU.mult)
                    tt("dve", Fm[0][:, :, r, :], Fm[0][:, :, r, :], t8, ALU.subtract)
                    tt("dve", Fm[1][:, :, r, :], Bb[1], pr, ALU.mult)
                    tt("dve", t8, Bb[0], pi_, ALU.mult)
                    tt("dve", Fm[1][:, :, r, :], Fm[1][:, :, r, :], t8, ALU.add)
                qr = PW[0][:, 9, :].us(2).bc([128, 8, 128])
                qi = PW[1][:, 9, :].us(2).bc([128, 8, 128])
                tt("dve", Ep[0], Ere[:, d], qr, ALU.mult)
                tt("dve", tE, nEim[:, d], qi, ALU.mult)
                tt("dve", Ep[0], Ep[0], tE, ALU.add)
                tt("dve", Ep[1], nEim[:, d], qr, ALU.mult)
                tt("dve", tE, Ere[:, d], qi, ALU.mult)
                tt("dve", Ep[1], Ep[1], tE, ALU.subtract)
                if "s5_b2" in dbg:
                    return
                for ri in range(2):
                    for gq in range(4):
                        ps = kb.psum_f()
                        for gg_ in range(4):
                            g = gq * 4 + gg_
                            gp, h = g // 2, g % 2
                            rows = slice(h * 64, (h + 1) * 64)
                            mm(ps[:, gg_ * 64:(gg_ + 1) * 64], Fm[ri][:, gp].re("p r i -> p (r i)"), identf[:, h * 64:(h + 1) * 64], True, True)
                        cp("act", W1[ri][:, gq * 4:gq * 4 + 4, :], ps[:, 0:256].re("p (g n) -> p g n", g=4))
                if "s5_b3" in dbg:
                    return
                for g in range(16):
                    gp, h = g // 2, g % 2
                    rows = slice(h * 64, (h + 1) * 64)
                    ps = kb.psf[h * 2 + (g // 2) % 2]
                    mm(ps[:, 0:128], Fm[0][rows, gp].re("p r i -> p (r i)"), Ep[0][rows, gp, :], True, False)
                    mm(ps[:, 0:128], Fm[1][rows, gp].re("p r i -> p (r i)"), Ep[1][rows, gp, :], False, True)
                    wt = w3t[g % 2]
                    tt("dve", wt, ps[:, 0:128], cmask[:, d, :], ALU.mult)
                    tt("pool", W3[:, g, :], W3[:, g, :], wt, ALU.add)
                if d == 0:
                    for g in range(16):
                        dcol = Dbc[:, g * 16:(g + 1) * 16].us(1).bc([128, 8, 16])
                        wt = w3t[g % 2]
                        tt("dve", wt.re("p (t o) -> p t o", t=8), dsel.re("p (t o) -> p t o", t=8), dcol, ALU.mult)
                        tt("pool", W3[:, g, :], W3[:, g, :], wt, ALU.add)
                if "s5_b4" in dbg:
                    return
                kb.memset("dve", tab[0][:, :, 0:1], 1.0)
                kb.memset("dve", tab[1][:, :, 0:1], 0.0)
                cp("dve", tab[0][:, :, 1:2], pv(11).us(2))
                cp("dve", tab[1][:, :, 1:2], pv(12).us(2))
                n = 2
                while n < NCH:
                    m = min(n, NCH - n)
                    tt("dve", pv(30), tab[0][:, :, n - 1], pv(11), ALU.mult)
                    tt("dve", pv(31), tab[1][:, :, n - 1], pv(12), ALU.mult)
                    tt("dve", pv(33), pv(30), pv(31), ALU.subtract)
                    tt("dve", pv(30), tab[0][:, :, n - 1], pv(12), ALU.mult)
                    tt("dve", pv(31), tab[1][:, :, n - 1], pv(11), ALU.mult)
                    tt("dve", pv(34), pv(30), pv(31), ALU.add)
                    Pr = pv(33).us(2).bc([128, 8, m])
                    Pi = pv(34).us(2).bc([128, 8, m])
                    ta = tE[:, :, 0:m]
                    tt("dve", tab[0][:, :, n:n + m], tab[0][:, :, 0:m], Pr, ALU.mult)
                    tt("dve", ta, tab[1][:, :, 0:m], Pi, ALU.mult)
                    tt("dve", tab[0][:, :, n:n + m], tab[0][:, :, n:n + m], ta, ALU.subtract)
                    tt("dve", tab[1][:, :, n:n + m], tab[0][:, :, 0:m], Pi, ALU.mult)
                    tt("dve", ta, tab[1][:, :, 0:m], Pr, ALU.mult)
                    tt("dve", tab[1][:, :, n:n + m], tab[1][:, :, n:n + m], ta, ALU.add)
                    n *= 2
                dma("sp", st_dir[d, :, 0:1024], W1[0].re("p g n -> p (g n)"))
                dma("sp", st_dir[d, :, 1024:2048], W1[1].re("p g n -> p (g n)"))
                dma("act", st_dir[d, :, 2048:2048 + 8 * NCH], tab[0].re("p g n -> p (g n)"))
                dma("act", st_dir[d, :, 2048 + 8 * NCH:2048 + 16 * NCH], tab[1].re("p g n -> p (g n)"))
                dma("sp", st_dir[d, :, 2048 + 16 * NCH:2048 + 16 * NCH + 8], pv(9))
            else:
                dma("sp", W1[0].re("p g n -> p (g n)"), st_dir[d, :, 0:1024])
                dma("sp", W1[1].re("p g n -> p (g n)"), st_dir[d, :, 1024:2048])
                dma("act", tab[0].re("p g n -> p (g n)"), st_dir[d, :, 2048:2048 + 8 * NCH])
                dma("act", tab[1].re("p g n -> p (g n)"), st_dir[d, :, 2048 + 8 * NCH:2048 + 16 * NCH])
                dma("sp", pv(9), st_dir[d, :, 2048 + 16 * NCH:2048 + 16 * NCH + 8])
            if "s5_stop2" in dbg:
                return
            P.barrier()
            def nat(tv, sl, rev):
                v = tv[:, sl]
                return v[:, ::-1] if rev else v

            def gp_gen(gp, d=d):
                rs, rho1 = rs_sets[gp % 2], rho_sets[gp % 2]
                psv = [kb.psum_f(), kb.psum_f()]
                for ri in range(2):
                    for h in range(2):
                        g = gp * 2 + h
                        mm(psv[ri][h * 64:(h + 1) * 64, 0:NCH], W1[ri][:, g, :], U[:, g, :], True, True)
                cp("pool", rho1, pv(9)[:, gp:gp + 1].bc([128, NCH]))
                yield
                Cn, Sn = tab[0][:, gp, :], tab[1][:, gp, :]
                if d == 0:
                    segs = [(slice(0, NCH), slice(0, NCH), False)]
                else:
                    segs = [(slice(0, 32), slice(0, 32), True), (slice(32, NCH), slice(32, NCH), True)]
                for (js, ks, rev) in segs:
                    vre, vim = nat(psv[0][:, 0:NCH], ks, rev), nat(psv[1][:, 0:NCH], ks, rev)
                    tt("dve", rs[0][:, js], vre, Cn[:, js], ALU.mult)
                    tt("dve", rs[1][:, js], vim, Sn[:, js], ALU.mult)
                    tt("dve", rs[4][:, js], vim, Cn[:, js], ALU.mult)
                    tt("dve", rs[5][:, js], vre, Sn[:, js], ALU.mult)
                yield
                tt("pool", rs[2], rs[0], rs[1], ALU.add)
                tt("pool", rs[3], rs[4], rs[5], ALU.subtract)
                yield
                kb.scan(rs[4], rho1, rs[2], 0.0)
                kb.scan(rs[5], rho1, rs[3], 0.0)
                yield
                tt("dve", rs[0], rs[4], Cn, ALU.mult)
                tt("dve", rs[1], rs[5], Sn, ALU.mult)
                tt("dve", rs[2], rs[4], Sn, ALU.mult)
                tt("dve", rs[3], rs[5], Cn, ALU.mult)
                yield
                for (js, ks, rev) in segs:
                    if d == 0:
                        osl = slice(1, NCH + 1)
                    else:
                        osl = slice(0, 32) if ks.start == 0 else slice(33, NCH + 1)
                    ore = nat(SN[d][0][:, gp, :], osl, rev)
                    oim = nat(SN[d][1][:, gp, :], osl, rev)
                    tt("pool", ore, rs[0][:, js], rs[1][:, js], ALU.subtract)
                    tt("pool", oim, rs[2][:, js], rs[3][:, js], ALU.add)
                yield
            for gp0 in range(0, 8, 2):
                lockstep([gp_gen(gp0), gp_gen(gp0 + 1)])
            for ri in range(2):
                if d == 0:
                    kb.memset("pool", SN[0][ri][:, :, 0:1], 0.0)
                else:
                    kb.memset("pool", SN[1][ri][:, :, 32:33], 0.0)
                    cp("pool", SN[1][ri][:, :, NCH + 1:NCH + 2], SN[1][ri][:, :, 0:1])
        if not cached:
            dma("sp", st_main[0], Ere.re("p d g n -> p (d g n)"))
            dma("act", st_main[1], nEim.re("p d g n -> p (d g n)"))
            dma("sp", st_main[2], W3.re("p g n -> p (g n)"))
        if "s5_stop3" in dbg:
            return
        P.barrier()
        AR.lo = scr0
        AR2.lo = 0
        Yc = AR.alloc("Yc", [8, 256])
        gg = AR.alloc("gg", [8, 256])
        g2 = AR.alloc("g2", [8, 256])
        sg = [AR.alloc("sg%d" % i, [512]) for i in range(2)]
        ggT = AR2.alloc("ggT", [2, NPOS])
        ggb = AR2.alloc("ggb", [2, NPOS], BF16)
        for (k0, M, p0) in kblocks:
            if k0 == 0 and not with_ctx:
                continue
            banks = [kb.psf[0], kb.psf[2], kb.psf[1], kb.psf[3]]
            for g in range(16):
                gp, h = g // 2, g % 2
                rows = slice(h * 64, (h + 1) * 64)
                bank = banks[h * 2 + (gp // 4)]
                o = bank[0:M, (gp % 4) * 128:(gp % 4 + 1) * 128]
                cf = slice(k0, k0 + M)
                cb_ = slice(k0 + 1, k0 + 1 + M) if k0 == 0 else slice(k0 + 2, k0 + 2 + M)
                mm(o, SN[0][0][rows, gp, cf], Ere[rows, 0, gp, :], True, False)
                mm(o, SN[0][1][rows, gp, cf], nEim[rows, 0, gp, :], False, False)
                mm(o, SN[1][0][rows, gp, cb_], Ere[rows, 1, gp, :], False, False)
                mm(o, SN[1][1][rows, gp, cb_], nEim[rows, 1, gp, :], False, False)
                mm(o, U[:, g, k0:k0 + M], W3[:, g, :], False, True)
            for h in range(2):
                for q in range(2):
                    bank = banks[h * 2 + q]
                    src = bank[0:M, :].re("p (g t o) -> p g t o", g=4, t=8)
                    dst = Yc[0:M].re("p t (g2 h o) -> p h g2 t o", h=2, o=16)[:, h, 4 * q:4 * q + 4]
                    cp("act" if h == 0 else "dve", dst, src)
            if "s5_Y" in dbg:
                dma("sp", tap("s5Y", [NCH, 8 * 256])[k0:k0 + M, :], Yc[0:M].re("p t f -> p (t f)"))
            yv, gv, g2v = Yc[0:M], gg[0:M], g2[0:M]
            tt("pool", g2v, yv, yv, ALU.mult)
            ts("dve", g2v, g2v, 0.044715, ALU.mult, 1.0, ALU.add)
            tt("pool", g2v, g2v, yv, ALU.mult)
            act(g2v, g2v, AF.Sigmoid, scale=1.5957691216057308)
            tt("dve", gv, g2v, yv, ALU.mult)
            for t_ in range(8):
                ps = kb.psum_f()
                for c in range(2):
                    tr(ps[:, c * 128:c * 128 + M], gv[:, t_, c * 128:(c + 1) * 128], identf[0:M, 0:M])
                for c in range(2):
                    cp("act" if c == 0 else "dve", ggT[:, c, p0 + t_:p0 + 8 * M:8], ps[:, c * 128:c * 128 + M])
        lo = 0 if with_ctx else TC
        for c in range(2):
            cp("dve", ggb[:, c, lo:NPOS], ggT[:, c, lo:NPOS])
        for c in range(2):
            pos = lo
            while pos < NPOS:
                n = min(512, NPOS - pos)
                ps = kb.psum_f()
                for k in range(2):
                    mm(ps[:, 0:n], wglu[:, k, c * 128:(c + 1) * 128], ggb[:, k, pos:pos + n], k == 0, k == 1)
                s_ = sg[(pos // 512) % 2]
                act(s_[:, 0:n], ps[:, 0:n], AF.Sigmoid)
                tt("dve", brT[1][c][:, pos:pos + n], s_[:, 0:n], ggT[:, c, pos:pos + n], ALU.mult)
                pos += n

    def phase_merge(l, b, with_ctx):
        AR.reset(top=True)
        lo = 0 if with_ctx else TC
        blocks = []
        pos = lo
        while pos < NPOS:
            n = min(512, NPOS - pos) if pos >= TC else TC - pos
            blocks.append((pos, n))
            pos += n
        ypT = AR.alloc("ypT", [KC, NPOS], BF16, top=True)
        wbr = AR.alloc("wbr", [4, 2, D], BF16, top=True)
        wo = AR.alloc("wo", [KC, D], BF16, top=True)
        wml = [AR.alloc("wml%d" % i, [KC, 4, 128], BF16, top=True) for i in range(2)]

        def load_wml(oc):
            for n_ in range(4):
                c0 = O_MERGE + n_ * D + oc * 128
                dma("pool", wml[oc % 2][:, :, n_, :], w_in[l, :, c0:c0 + 128].re("(c p) n -> p c n", p=128))
        wg = AR.alloc("w_gate", [KC, 1024], BF16)
        for cc in range(8):
            dma("pool", wg[:, :, cc * 128:(cc + 1) * 128],
                w_in[l, :, O_GATE + cc * 128:O_GATE + (cc + 1) * 128].re("(c p) n -> p c n", p=128))
        for n_ in range(4):
            dma("pool", wbr[:, n_], w_branch[l, n_].re("(c p) n -> p c n", p=128))
        load_wml(0)
        load_wml(1)
        dma("pool", wo, w_out[l].re("(c p) n -> p c n", p=128))
        sl = [AR.alloc("sl%d" % i, [512], BF16) for i in range(2)]
        it = 0
        for cc in range(8):
            for (p0, n) in blocks:
                ps = kb.psum_f()
                for k in range(KC):
                    mm(ps[:, 0:n], wg[:, k, cc * 128:(cc + 1) * 128], hT[:, k, p0:p0 + n], k == 0, k == KC - 1)
                s_ = sl[it % 2]
                it += 1
                act(s_[:, 0:n], ps[:, 0:n], AF.Silu)
                br = brT[cc // 2][cc % 2][:, p0:p0 + n]
                tt("dve", br, br, s_[:, 0:n], ALU.mult)
        AR.reset()
        sg = [AR.alloc("sg%d" % i, [512]) for i in range(2)]
        tm = [AR.alloc("tm%d" % i, [512]) for i in range(2)]
        acc = [AR.alloc("acc%d" % i, [512]) for i in range(2)]
        it = 0
        ib = 0
        for oc in range(8):
            wm = wml[oc % 2]
            if 1 <= oc and oc + 1 < 8:
                load_wml(oc + 1)
            for (p0, n) in blocks:
                ac = acc[ib % 2]
                ib += 1
                for n_ in range(4):
                    psA = kb.psum_f()
                    for k in range(2):
                        mm(psA[:, 0:n], wbr[:, n_, k, oc * 128:(oc + 1) * 128], brT[n_][k][:, p0:p0 + n], k == 0, k == 1)
                    psB = kb.psum_f()
                    for k in range(KC):
                        mm(psB[:, 0:n], wm[:, k, n_, :], hT[:, k, p0:p0 + n], k == 0, k == KC - 1)
                    s_ = sg[it % 2]
                    t_ = tm[it % 2]
                    it += 1
                    act(s_[:, 0:n], psB[:, 0:n], AF.Sigmoid)
                    if n_ == 0:
                        tt("dve", ac[:, 0:n], psA[:, 0:n], s_[:, 0:n], ALU.mult)
                    elif n_ < 3:
                        tt("dve", t_[:, 0:n], psA[:, 0:n], s_[:, 0:n], ALU.mult)
                        tt("dve", ac[:, 0:n], ac[:, 0:n], t_[:, 0:n], ALU.add)
                    else:
                        tt("dve", t_[:, 0:n], psA[:, 0:n], s_[:, 0:n], ALU.mult)
                        tt("dve", ypT[:, oc, p0:p0 + n], ac[:, 0:n], t_[:, 0:n], ALU.add)
        AR.reset()
        gbc = AR.alloc("gbc", [2, D])
        dma("sp", gbc[:, 0, :], modD[l, b, 2 * D:3 * D].pbc(128))
        if with_ctx:
            dma("sp", gbc[:, 1, :], modD[l, 2, 2 * D:3 * D].pbc(128))
        xt = [AR.alloc("xt%d" % i, [D]) for i in range(2)]
        yt = [AR.alloc("yt%d" % i, [D]) for i in range(2)]
        for i in range(0 if with_ctx else 2, 18):
            x_t, y_t = xt[i % 2], yt[i % 2]
            dma("sp", x_t, src_tile(l, b, i))
            for hf in range(2):
                ps = kb.psum_f()
                for k in range(KC):
                    mm(ps, ypT[:, k, i * 128:(i + 1) * 128], wo[:, k, hf * 512:(hf + 1) * 512], k == 0, k == KC - 1)
                tt("dve", y_t[:, hf * 512:(hf + 1) * 512], ps, gbc[:, 1 if i < 2 else 0, hf * 512:(hf + 1) * 512], ALU.mult)
            tt("pool", y_t, y_t, x_t, ALU.add)
            if i < 2:
                dst = (tap("xc1", [nb, TC, D]) if "x1out" in dbg else xcmid)[b, i * 128:(i + 1) * 128, :]
            else:
                dst = (xmid if (l < DEPTH - 1 and "x1out" not in dbg) else y_out)[b, (i - 2) * 128:(i - 1) * 128, :]
            dma("act", dst, y_t)

    def dump_br(name, n, lo):
        o = tap(name, [2, 128, NPOS])
        tmpf = AR.alloc("dump_" + name, [NPOS])
        for c in range(2):
            cp("dve", tmpf[:, lo:NPOS], brT[n][c][:, lo:NPOS])
            dma("sp", o[c, :, lo:NPOS], tmpf[:, lo:NPOS])

    for l in layers:
        with_ctx = l < DEPTH - 1
        prep_layer(l)
        for b in range(nb):
            if "prep_only" in dbg:
                continue
            phase_norm(l, b)
            if "hT" in dbg and b == 0 and l == layers[0]:
                o = tap("hT", [KC, 128, NPOS])
                AR.reset()
                tmpf = AR.alloc("dump_hT", [NPOS])
                for c in range(KC):
                    cp("dve", tmpf, hT[:, c, :])
                    dma("sp", o[c], tmpf)
            only = dbg & {"only_lru", "only_pool", "only_mla", "only_s5", "only_norm"}
            if not only or "only_s5" in only:
                phase_s5(l, b, with_ctx)
            if not only or "only_lru" in only:
                phase_lru(l, b, with_ctx)
            if not only or "only_pool" in only:
                phase_pool(l, b, with_ctx)
            if not only or "only_mla" in only:
                phase_mla(l, b, with_ctx)
            if "br" in dbg and b == 0 and l == layers[0]:
                AR.reset()
                lo = 0 if with_ctx else TC
                for n_, nm in enumerate(("mla", "s5", "lru", "pool")):
                    dump_br(nm, n_, lo)
            if "nomerge" not in dbg:
                phase_merge(l, b, with_ctx)
    P.barrier()
    P.emit()
    kb.st.close()
    return nc, kb


def _consts():
    ident = np.eye(128, dtype=np.float32)
    rows_n = T // 64
    row = np.repeat(np.arange(rows_n), 64).astype(np.float32)
    col = np.tile(np.arange(64), rows_n).astype(np.float32)
    nf = 8
    inv = (np.float32(10000.0) ** (-np.arange(nf, dtype=np.float32) / nf)).astype(np.float32)
    ar = (row[:, None] * inv).astype(np.float32)
    ac = (col[:, None] * inv).astype(np.float32)
    cr, sr, cc, sc = np.cos(ar), np.sin(ar), np.cos(ac), np.sin(ac)
    ropeC = np.concatenate([cr, cr, cc, cc], axis=1).astype(np.float32)
    ropeS = np.concatenate([-sr, sr, -sc, sc], axis=1).astype(np.float32)
    cpool = np.ones((128, 34), np.float32)
    for c in range(2):
        for p in range(128):
            w = (2, 4, 8, 16)[2 * c + p // 64]
            hw = w // 2
            cpool[p, c] = 1.0 / w
            for t in range(8):
                cnt = t + hw if t < hw else w
                cpool[p, 2 + c * 8 + t] = w / cnt
            for j in range(8):
                dist = 8 - j
                cnt = dist + hw if dist < hw else w
                cpool[p, 18 + c * 8 + j] = w / cnt
    m = np.zeros((128, 3, 128), np.float32)
    for r in range(8):
        for t in range(8):
            if r <= t:
                m[r * 16:(r + 1) * 16, 0, t * 16:(t + 1) * 16] = 1.0
            if r >= t:
                m[r * 16:(r + 1) * 16, 1, t * 16:(t + 1) * 16] = 1.0
            if r == t:
                m[r * 16:(r + 1) * 16, 2, t * 16:(t + 1) * 16] = np.eye(16, dtype=np.float32)
    return {"c_ident": ident, "c_ropeC": ropeC, "c_ropeS": ropeS, "c_pool": cpool, "c_mask": m}


_WNAMES = ["w_ada", "b_ada", "norm_g", "w_in", "mla_q_norm", "mla_kv_norm", "mla_w_uq", "mla_w_ukv", "mla_q_gain",
           "mla_k_gain", "s5_a_re", "s5_a_im", "s5_log_dt", "s5_b_re", "s5_b_im", "s5_c_re", "s5_c_im", "s5_d",
           "s5_w_glu", "lru_conv_w", "lru_conv_b", "lru_lambda", "lru_w_a", "lru_b_a", "lru_w_x", "lru_b_x",
           "pool_w", "pool_b", "pool_scale", "w_branch", "w_out"]


def make_in_maps(inputs, n_cores, nb):
    consts = _consts()
    maps = []
    for r in range(n_cores):
        bs = slice(r * nb, (r + 1) * nb)
        m = {"x": np.ascontiguousarray(inputs["x"][bs], dtype=np.float32),
             "ctx": np.ascontiguousarray(inputs["ctx"][bs], dtype=np.float32)}
        cv = np.zeros((3, D), np.float32)
        cv[0:nb] = np.asarray(inputs["c"], dtype=np.float32)[bs]
        cv[2] = np.asarray(inputs["c_ctx"], dtype=np.float32)
        m["cvec"] = cv
        for k in _WNAMES:
            m[k] = np.ascontiguousarray(inputs[k], dtype=np.float32)
        m.update(consts)
        maps.append(m)
    return maps


_CACHE = {}


def kernel(**inputs):
    n_cores, nb = 8, 2
    if "nc" not in _CACHE:
        _CACHE["nc"] = build_program(nb=nb)[0]
    nc = _CACHE["nc"]
    maps = make_in_maps(inputs, n_cores, nb)
    res = run_bass_kernel_spmd(nc, maps, core_ids=list(range(n_cores)))
    out = np.concatenate([np.asarray(r["y"], dtype=np.float32) for r in res.results], axis=0)
    return out
```

```python
import math
import contextlib
import numpy as np
import concourse.bass as bass
import concourse.mybir as mybir
from concourse.bass_utils import run_bass_kernel_spmd

F32 = mybir.dt.float32
BF16 = mybir.dt.bfloat16
I32 = mybir.dt.int32
AF = mybir.ActivationFunctionType
ALU = mybir.AluOpType
AX = mybir.AxisListType

ENGS = ("pe", "act", "dve", "pool", "sp")
NDMA = 24

D = 1024
KC = 8
T = 2048
TC = 256
NPOS = T + TC
DEPTH = 2
O_KROPE, O_S5, O_LRU, O_CQ, O_POOL, O_GATE, O_MERGE, IN_W = 128, 160, 416, 672, 928, 1184, 2208, 6304
EPS = 1e-6
NCH = NPOS // 8


PAR_OFF = set()


class Buf:
    __slots__ = ("name", "w", "wp", "r")

    def __init__(self, name):
        self.name = name
        self.w = []
        self.wp = []
        self.r = []


class TV:
    par_ = False

    def __init__(self, ap, buf):
        self.ap, self.buf = ap, buf

    def par(self, tag=""):
        t = TV(self.ap, self.buf)
        t.par_ = tag not in PAR_OFF
        return t

    def __getitem__(self, k):
        return TV(self.ap[k], self.buf)

    def re(self, s, **kw):
        return TV(self.ap.rearrange(s, **kw), self.buf)

    def bc(self, shape):
        return TV(self.ap.to_broadcast(list(shape)), self.buf)

    def us(self, ax):
        return TV(self.ap.unsqueeze(ax), self.buf)

    def pbc(self, n):
        return TV(self.ap.partition_broadcast(n), self.buf)

    @property
    def shape(self):
        return tuple(self.ap.shape)


def _bufs(*xs):
    out = []
    for x in xs:
        if isinstance(x, TV):
            out.append(x.buf)
        elif isinstance(x, (list, tuple)):
            out.extend(_bufs(*x))
    return out


def _ap(x):
    return x.ap if isinstance(x, TV) else x


class Prog:
    def __init__(self, nc):
        self.nc = nc
        self.ops = {e: [] for e in ENGS}
        self.cnt = {e: 0 for e in ENGS}
        self.waited = {e: {} for e in ENGS}
        self.dma_i = 0
        self.dma_last = [0] * NDMA

    def _deps(self, eng, reads, writes, par=False):
        toks = []
        for b in reads:
            toks.extend(b.w)
            toks.extend(b.wp)
        for b in writes:
            toks.extend(b.w)
            if not par:
                toks.extend(b.wp)
            toks.extend(b.r)
        need = {}
        for (k, v, e) in toks:
            if e == "pe" and eng == "pe":
                continue
            if self.waited[eng].get(k, 0) >= v:
                continue
            if need.get(k, 0) < v:
                need[k] = v
        for k, v in need.items():
            self.waited[eng][k] = v
        return list(need.items())

    def _mark(self, tok, reads, writes, par=False):
        for b in writes:
            if par:
                best = {}
                for (k, v, e) in b.wp + [tok]:
                    if k not in best or best[k][1] < v:
                        best[k] = (k, v, e)
                b.wp = list(best.values())
            else:
                b.w = [tok]
                b.wp = []
                b.r = []
        for b in reads:
            if b in writes:
                continue
            b.r.append(tok)
            if len(b.r) > 16:
                best = {}
                for (k, v, e) in b.r:
                    if k not in best or best[k][1] < v:
                        best[k] = (k, v, e)
                b.r = list(best.values())

    def op(self, eng, fn, reads=(), writes=(), par=False):
        reads = list(dict.fromkeys(reads))
        writes = list(dict.fromkeys(writes))
        waits = self._deps(eng, reads, writes, par)
        self.cnt[eng] += 1
        tok = (eng, self.cnt[eng], eng)
        self.ops[eng].append((waits, fn, (eng, 1)))
        self._mark(tok, reads, writes, par)

    def dma(self, eng, fn, reads=(), writes=(), par=False):
        reads = list(dict.fromkeys(reads))
        writes = list(dict.fromkeys(writes))
        waits = self._deps(eng, reads, writes, par)
        slot = self.dma_i % NDMA
        self.dma_i += 1
        key = ("dma", slot)
        prev = self.dma_last[slot]
        if prev and self.waited[eng].get(key, 0) < prev:
            waits.append((key, prev))
            self.waited[eng][key] = prev
        val = prev + 16
        self.dma_last[slot] = val
        tok = (key, val, "dma")
        self.ops[eng].append((waits, fn, (key, 16)))
        self._mark(tok, reads, writes, par)

    def barrier(self):
        for E in ENGS:
            waits = []
            for e in ENGS:
                v = self.cnt[e]
                if v and self.waited[E].get(e, 0) < v:
                    waits.append((e, v))
                    self.waited[E][e] = v
            for slot in range(NDMA):
                v = self.dma_last[slot]
                key = ("dma", slot)
                if v and self.waited[E].get(key, 0) < v:
                    waits.append((key, v))
                    self.waited[E][key] = v
            if waits:
                self.ops[E].append((waits, None, None))

    def emit(self):
        nc = self.nc
        sems = {}
        with contextlib.ExitStack() as st:
            for e in ENGS:
                sems[e] = st.enter_context(nc.semaphore("s_" + e))
            for i in range(NDMA):
                sems[("dma", i)] = st.enter_context(nc.semaphore("s_dma%d" % i))
            block = st.enter_context(nc.Block())

            def run(engname):
                def body(eng):
                    for waits, fn, inc in self.ops[engname]:
                        for k, v in waits:
                            eng.wait_ge(sems[k], v)
                        if fn is None:
                            continue
                        ins = fn(eng)
                        ins.then_inc(sems[inc[0]], inc[1])
                return body
            block.tensor(run("pe"))
            block.scalar(run("act"))
            block.vector(run("dve"))
            block.gpsimd(run("pool"))
            block.sync(run("sp"))


class KB:
    def __init__(self, nc, nb, layers, dbg):
        self.nc = nc
        self.P = Prog(nc)
        self.nb = nb
        self.layers = layers
        self.dbg = dbg
        self.st = contextlib.ExitStack()
        self.din = {}
        self.dout = {}
        self.ps_i = 0
        self.psb_i = 0
        self.ps8_i = 0

    def tt(self, eng, out, a, b, op):
        self.P.op(eng, lambda e: e.tensor_tensor(out=out.ap, in0=a.ap, in1=b.ap, op=op), _bufs(a, b), _bufs(out), par=out.par_)

    def ts(self, eng, out, a, s1, op0, s2=None, op1=None):
        if op1 is None:
            self.P.op(eng, lambda e: e.tensor_scalar(out=out.ap, in0=a.ap, scalar1=_ap(s1), scalar2=None, op0=op0),
                      _bufs(a, s1), _bufs(out), par=out.par_)
        else:
            self.P.op(eng, lambda e: e.tensor_scalar(out=out.ap, in0=a.ap, scalar1=_ap(s1), scalar2=_ap(s2), op0=op0, op1=op1),
                      _bufs(a, s1, s2), _bufs(out), par=out.par_)

    def stt(self, out, a, s, b, op0, op1):
        self.P.op("dve", lambda e: e.scalar_tensor_tensor(out=out.ap, in0=a.ap, scalar=_ap(s), in1=b.ap, op0=op0, op1=op1),
                  _bufs(a, s, b), _bufs(out))

    def act(self, out, a, func, bias=0.0, scale=1.0, accum=None):
        if accum is None:
            self.P.op("act", lambda e: e.activation(out=out.ap, in_=a.ap, func=func, bias=_ap(bias), scale=_ap(scale)),
                      _bufs(a, bias, scale), _bufs(out), par=out.par_)
        else:
            self.P.op("act", lambda e: e.activation(out=out.ap, in_=a.ap, func=func, bias=_ap(bias), scale=_ap(scale), accum_out=accum.ap),
                      _bufs(a, bias, scale), _bufs(out, accum))

    def cp(self, eng, out, a):
        if eng == "act":
            self.P.op("act", lambda e: e.copy(out=out.ap, in_=a.ap), _bufs(a), _bufs(out), par=out.par_)
        else:
            self.P.op(eng, lambda e: e.tensor_copy(out=out.ap, in_=a.ap), _bufs(a), _bufs(out), par=out.par_)

    def memset(self, eng, out, v):
        self.P.op(eng, lambda e: e.memset(out.ap, v), [], _bufs(out))

    def recip(self, out, a):
        self.P.op("dve", lambda e: e.reciprocal(out=out.ap, in_=a.ap), _bufs(a), _bufs(out))

    def mm(self, out, lhsT, rhs, start, stop):
        self.P.op("pe", lambda e: e.matmul(out.ap, lhsT=lhsT.ap, rhs=rhs.ap, start=start, stop=stop), _bufs(lhsT, rhs), _bufs(out))

    def tr(self, out, a, ident):
        self.P.op("pe", lambda e: e.transpose(out.ap, a.ap, ident.ap), _bufs(a, ident), _bufs(out))

    def scan(self, out, d0, d1, init):
        self.P.op("dve", lambda e: e.tensor_tensor_scan(out=out.ap, data0=d0.ap, data1=d1.ap, initial=_ap(init), op0=ALU.mult, op1=ALU.add),
                  _bufs(d0, d1, init), _bufs(out))

    def reduce(self, out, a, op=ALU.add):
        self.P.op("dve", lambda e: e.tensor_reduce(out=out.ap, in_=a.ap, axis=AX.X, op=op), _bufs(a), _bufs(out))

    def dma(self, q, out, a, slow=False):
        if slow:
            self.P.dma(q, lambda e: e.dma_start(out=out.ap, in_=a.ap, allow_slow_non_contiguous=True), _bufs(a), _bufs(out), par=out.par_)
        else:
            self.P.dma(q, lambda e: e.dma_start(out=out.ap, in_=a.ap), _bufs(a), _bufs(out), par=out.par_)

    def dram_in(self, name, shape, dt=F32):
        t = TV(self.nc.dram_tensor(name, list(shape), dt, kind="ExternalInput").ap(), Buf(name))
        self.din[name] = t
        return t

    def dram_out(self, name, shape, dt=F32):
        t = TV(self.nc.dram_tensor(name, list(shape), dt, kind="ExternalOutput").ap(), Buf(name))
        self.dout[name] = t
        return t

    def dram_tmp(self, name, shape, dt=F32):
        return TV(self.nc.dram_tensor(name, list(shape), dt, kind="Internal").ap(), Buf(name))

    def sb(self, name, shape, dt=F32):
        t = self.st.enter_context(self.nc.sbuf_tensor(name, list(shape), dt))
        return TV(t[:], Buf(name))

    def psum_f(self):
        i = self.ps_i % 6
        self.ps_i += 1
        return self.psf[i]

    def psum_b(self):
        i = self.psb_i % 2
        self.psb_i += 1
        return self.psb[i]

    def psum_any(self, bf16=False):
        i = self.ps8_i % 8
        self.ps8_i += 1
        return self.ps8b[i] if bf16 else self.ps8[i]


class Arena:
    def __init__(self, kb, words, ap=None):
        self.kb = kb
        self.words = words
        if ap is None:
            self.f = kb.st.enter_context(kb.nc.sbuf_tensor("arena_f", [128, words], F32))
        else:
            self.f = ap
        self.lo = 0
        self.hi = words

    def alloc(self, name, free, dt=F32, top=False, buf=None, at=None):
        nel = int(np.prod(free))
        w = nel if dt == F32 else (nel + 1) // 2
        w = ((w + 15) // 16) * 16
        if at is not None:
            off = at
        elif top:
            self.hi -= w
            off = self.hi
        else:
            off = self.lo
            self.lo += w
        assert self.lo <= self.hi, (name, self.lo, self.hi)
        assert off + w <= self.words
        if dt != F32:
            ap = self.f[:, off:off + w].bitcast(dt)[:, 0:nel]
        else:
            ap = self.f[:, off:off + nel]
        if len(free) > 1:
            names = " ".join("d%d" % i for i in range(len(free)))
            kw = {"d%d" % i: int(free[i]) for i in range(len(free))}
            ap = ap.rearrange("p (%s) -> p %s" % (names, names), **kw)
        tv = TV(ap, buf if buf is not None else Buf(name))
        tv.off = off
        tv.w = w
        return tv

    def reset(self, top=False):
        self.kb.P.barrier()
        self.lo = 0
        if top:
            self.hi = self.words


def lockstep(gens):
    gens = list(gens)
    while gens:
        nxt = []
        for g in gens:
            try:
                next(g)
                nxt.append(g)
            except StopIteration:
                pass
        gens = nxt


def build_program(nb=2, layers=(0, 1), dbg=None):
    nc = bass.Bass("TRN2", target_bir_lowering=False)
    kb = KB(nc, nb, layers, dbg)
    P = kb.P
    tt, ts, stt, act, cp, mm, tr, dma = kb.tt, kb.ts, kb.stt, kb.act, kb.cp, kb.mm, kb.tr, kb.dma
    dbg = dbg or set()
    PAR_OFF.clear()
    PAR_OFF.update(x[6:] for x in dbg if x.startswith("nopar_"))

    x_in = kb.dram_in("x", [nb, T, D])
    ctx_in = kb.dram_in("ctx", [nb, TC, D])
    cvec = kb.dram_in("cvec", [3, D])
    w_ada = kb.dram_in("w_ada", [DEPTH, D, 3 * D])
    b_ada = kb.dram_in("b_ada", [DEPTH, 3 * D])
    norm_g = kb.dram_in("norm_g", [DEPTH, D])
    w_in = kb.dram_in("w_in", [DEPTH, D, IN_W])
    mla_q_norm = kb.dram_in("mla_q_norm", [DEPTH, 256])
    mla_kv_norm = kb.dram_in("mla_kv_norm", [DEPTH, 128])
    mla_w_uq = kb.dram_in("mla_w_uq", [DEPTH, 256, 384])
    mla_w_ukv = kb.dram_in("mla_w_ukv", [DEPTH, 128, 512])
    mla_q_gain = kb.dram_in("mla_q_gain", [DEPTH, 96])
    mla_k_gain = kb.dram_in("mla_k_gain", [DEPTH, 96])
    s5_a_re = kb.dram_in("s5_a_re", [DEPTH, 2, 16, 64])
    s5_a_im = kb.dram_in("s5_a_im", [DEPTH, 2, 16, 64])
    s5_log_dt = kb.dram_in("s5_log_dt", [DEPTH, 2, 16])
    s5_b_re = kb.dram_in("s5_b_re", [DEPTH, 2, 16, 64, 16])
    s5_b_im = kb.dram_in("s5_b_im", [DEPTH, 2, 16, 64, 16])
    s5_c_re = kb.dram_in("s5_c_re", [DEPTH, 2, 16, 16, 64])
    s5_c_im = kb.dram_in("s5_c_im", [DEPTH, 2, 16, 16, 64])
    s5_d = kb.dram_in("s5_d", [DEPTH, 256])
    s5_w_glu = kb.dram_in("s5_w_glu", [DEPTH, 256, 256])
    lru_conv_w = kb.dram_in("lru_conv_w", [DEPTH, 4, 256])
    lru_conv_b = kb.dram_in("lru_conv_b", [DEPTH, 256])
    lru_lambda = kb.dram_in("lru_lambda", [DEPTH, 2, 256])
    lru_w_a = kb.dram_in("lru_w_a", [DEPTH, 2, 4, 64, 64])
    lru_b_a = kb.dram_in("lru_b_a", [DEPTH, 2, 256])
    lru_w_x = kb.dram_in("lru_w_x", [DEPTH, 2, 4, 64, 64])
    lru_b_x = kb.dram_in("lru_b_x", [DEPTH, 2, 256])
    pool_w = kb.dram_in("pool_w", [DEPTH, 4, 64, 64])
    pool_b = kb.dram_in("pool_b", [DEPTH, 256])
    pool_scale = kb.dram_in("pool_scale", [DEPTH, 256])
    w_branch = kb.dram_in("w_branch", [DEPTH, 4, 256, D])
    w_out = kb.dram_in("w_out", [DEPTH, D, D])
    c_ident = kb.dram_in("c_ident", [128, 128])
    c_ropeC = kb.dram_in("c_ropeC", [T, 32])
    c_ropeS = kb.dram_in("c_ropeS", [T, 32])
    c_pool = kb.dram_in("c_pool", [128, 2 + 16 + 16])
    c_mask = kb.dram_in("c_mask", [128, 3, 128])
    y_out = kb.dram_out("y", [nb, T, D])
    xmid = kb.dram_tmp("xmid", [nb, T, D])
    xcmid = kb.dram_tmp("xcmid", [nb, TC, D])
    modD = kb.dram_tmp("modD", [DEPTH, 3, 3 * D])
    st_main = kb.dram_tmp("st_main", [3, 128, 2048])
    st_dir = kb.dram_tmp("st_dir", [2, 128, 2048 + 16 * NCH + 8])
    dbg_out = {}

    def tap(name, shape):
        if name not in dbg_out:
            dbg_out[name] = kb.dram_out("dbg_" + name, shape)
        return dbg_out[name]

    kb.ps8 = []
    for i in range(8):
        t = kb.st.enter_context(nc.psum_tensor("psf%d" % i, [128, 512], F32))
        kb.ps8.append(TV(t[:], Buf("psf%d" % i)))
    kb.psf = kb.ps8[0:6]
    kb.psb = [TV(kb.ps8[i].ap.bitcast(BF16), kb.ps8[i].buf) for i in (6, 7)]
    kb.ps8b = [TV(kb.ps8[i].ap.bitcast(BF16), kb.ps8[i].buf) for i in range(8)]

    hT = kb.sb("hT", [128, KC, NPOS], BF16)
    bigbr = kb.sb("bigbr", [128, 8 * NPOS], BF16)
    _slot = {0: 0, 2: 2, 3: 4, 1: 6}
    brT = [[TV(bigbr.ap[:, (_slot[n] + c) * NPOS:(_slot[n] + c + 1) * NPOS], Buf("brT%d%d" % (n, c))) for c in range(2)]
           for n in range(4)]
    identf = kb.sb("identf", [128, 128])
    identb = kb.sb("identb", [128, 128], BF16)
    ropeC = kb.sb("ropeC", [128, 16, 32])
    ropeS = kb.sb("ropeS", [128, 16, 32])
    cpool = kb.sb("cpool", [128, 34])
    cmask = kb.sb("cmask", [128, 3, 128])
    modA = kb.sb("modA", [128, 3, KC])
    modS = kb.sb("modS", [128, 3, KC])
    lp = kb.sb("lp", [128, 64])
    lruBD = kb.sb("lruBD", [128, 2, 2, 2, 128])
    poolBD = kb.sb("poolBD", [128, 2, 128])
    wukv = kb.sb("wukv", [128, 512], BF16)
    wuq = kb.sb("wuq", [128, 2, 384], BF16)
    gains = kb.sb("gains", [128, 2, 96])
    wglu = kb.sb("wglu", [128, 2, 256], BF16)
    AR = Arena(kb, 28000)
    AR2 = Arena(kb, 3 * NPOS, ap=bigbr.ap[:, 0:6 * NPOS].bitcast(F32))

    dma("sp", identf, c_ident)
    cp("dve", identb, identf)
    dma("sp", ropeC, c_ropeC.re("(i p) f -> p i f", p=128))
    dma("sp", ropeS, c_ropeS.re("(i p) f -> p i f", p=128))
    dma("sp", cpool, c_pool)
    dma("sp", cmask, c_mask)

    LP_CW = 0
    LP_CB = 8
    LP_NSP = 10
    LP_NSP2 = 14
    LP_BA = 18
    LP_BX = 22
    LP_PSC = 26
    LP_PBS = 28
    LP_G = 30
    LP_TMP = 40

    def prep_layer(l):
        AR.reset(top=True)
        cT = AR.alloc("cT", [KC, 3])
        for v in range(3):
            dma("sp", cT[:, :, v], cvec[v].re("(c p) -> p c", p=128), slow=True)
        cact = AR.alloc("cact", [KC, 3])
        act(cact, cT, AF.Silu)
        brow = AR.alloc("brow", [3 * D])
        dma("sp", brow[0:3, :], b_ada[l:l + 1, :].re("o n -> (o n)").pbc(3))
        modrow = AR.alloc("modrow", [3 * D])
        wa = [AR.alloc("wa%d" % i, [KC, 512]) for i in range(4)]
        for cb in range(4):
            dma("sp" if cb % 2 == 0 else "act", wa[cb], w_ada[l, :, cb * 512:(cb + 1) * 512].re("(c p) n -> p c n", p=128))
        for cb in range(6):
            w = wa[cb % 4]
            if cb >= 4:
                dma("sp" if cb % 2 == 0 else "act", w, w_ada[l, :, cb * 512:(cb + 1) * 512].re("(c p) n -> p c n", p=128))
            ps = kb.psum_f()
            for k in range(KC):
                mm(ps[0:3, :], cact[:, k, :], w[:, k, :], k == 0, k == KC - 1)
            tt("dve", modrow[0:3, cb * 512:(cb + 1) * 512], ps[0:3, :], brow[0:3, cb * 512:(cb + 1) * 512], ALU.add)
        dma("sp", modD[l], modrow[0:3, :])
        sc = AR.alloc("sc", [3, KC])
        for v in range(3):
            dma("sp", modS[:, v, :], modD[l, v, 0:D].re("(c p) -> p c", p=128), slow=True)
            dma("sp", sc[:, v, :], modD[l, v, D:2 * D].re("(c p) -> p c", p=128), slow=True)
        dma("sp", lp[:, LP_G:LP_G + 8], norm_g[l].re("(c p) -> p c", p=128), slow=True)
        for v in range(3):
            stt(modA[:, v, :], sc[:, v, :], 1.0, lp[:, LP_G:LP_G + 8], ALU.add, ALU.mult)
        for k in range(4):
            dma("sp", lp[:, LP_CW:LP_CW + 8].re("p (c k) -> p c k", c=2)[:, :, k], lru_conv_w[l, k].re("(c p) -> p c", p=128), slow=True)
        dma("sp", lp[:, LP_CB:LP_CB + 2], lru_conv_b[l].re("(c p) -> p c", p=128), slow=True)
        lam = lp[:, LP_TMP:LP_TMP + 4]
        for d in range(2):
            dma("sp", lam[:, d * 2:d * 2 + 2], lru_lambda[l, d].re("(c p) -> p c", p=128), slow=True)
            dma("sp", lp[:, LP_BA + d * 2:LP_BA + d * 2 + 2], lru_b_a[l, d].re("(c p) -> p c", p=128), slow=True)
            dma("sp", lp[:, LP_BX + d * 2:LP_BX + d * 2 + 2], lru_b_x[l, d].re("(c p) -> p c", p=128), slow=True)
        t0 = lp[:, LP_TMP + 4:LP_TMP + 8]
        t1 = lp[:, LP_TMP + 8:LP_TMP + 12]
        t2 = lp[:, LP_TMP + 12:LP_TMP + 16]
        t3 = lp[:, LP_TMP + 16:LP_TMP + 20]
        ts("dve", t0, lam, -1.0, ALU.mult)
        tt("dve", t0, t0, lam, ALU.max)
        act(t1, t0, AF.Exp, scale=-1.0)
        ts("dve", t2, t1, 2.0, ALU.add)
        kb.recip(t2, t2)
        tt("dve", t2, t2, t1, ALU.mult)
        tt("dve", t3, t2, t2, ALU.mult)
        ts("dve", t0, t3, 1.0 / 11.0, ALU.mult, 1.0 / 9.0, ALU.add)
        for cf in (1.0 / 7.0, 1.0 / 5.0, 1.0 / 3.0, 1.0):
            tt("dve", t0, t0, t3, ALU.mult)
            ts("dve", t0, t0, cf, ALU.add)
        tt("dve", t0, t0, t2, ALU.mult)
        ts("dve", t1, lam, -1.0, ALU.mult, 0.0, ALU.max)
        stt(t0, t0, 2.0, t1, ALU.mult, ALU.add)
        ts("dve", lp[:, LP_NSP:LP_NSP + 4], t0, -8.0, ALU.mult)
        ts("dve", lp[:, LP_NSP2:LP_NSP2 + 4], t0, -16.0, ALU.mult)
        kb.memset("dve", lruBD, 0.0)
        for d in range(2):
            for gi, wsrc in enumerate((lru_w_a, lru_w_x)):
                for c in range(2):
                    for h in range(2):
                        dma("sp" if (c + h) % 2 == 0 else "act", lruBD[h * 64:(h + 1) * 64, d, gi, c, h * 64:(h + 1) * 64].par("ldw"), wsrc[l, d, 2 * c + h])
        kb.memset("dve", poolBD, 0.0)
        for c in range(2):
            for h in range(2):
                dma("sp", poolBD[h * 64:(h + 1) * 64, c, h * 64:(h + 1) * 64].par("ldw"), pool_w[l, 2 * c + h])
        dma("sp", lp[:, LP_PSC:LP_PSC + 2], pool_scale[l].re("(c p) -> p c", p=128), slow=True)
        dma("sp", lp[:, LP_PBS:LP_PBS + 2], pool_b[l].re("(c p) -> p c", p=128), slow=True)
        tt("dve", lp[:, LP_PBS:LP_PBS + 2], lp[:, LP_PBS:LP_PBS + 2], lp[:, LP_PSC:LP_PSC + 2], ALU.mult)
        kvn = lp[:, LP_TMP + 20:LP_TMP + 21]
        qn = lp[:, LP_TMP + 21:LP_TMP + 23]
        dma("sp", kvn, mla_kv_norm[l].re("(p o) -> p o", o=1), slow=True)
        dma("sp", qn, mla_q_norm[l].re("(c p) -> p c", p=128), slow=True)
        wtmp = AR.alloc("wtmp", [2, 512])
        dma("sp", wtmp[:, 0, :], mla_w_ukv[l])
        ts("dve", wukv, wtmp[:, 0, :], kvn, ALU.mult)
        wtmp2 = AR.alloc("wtmp2", [2, 384])
        dma("sp", wtmp2, mla_w_uq[l].re("(c p) n -> p c n", p=128))
        for c in range(2):
            ts("dve", wuq[:, c, :], wtmp2[:, c, :], qn[:, c:c + 1], ALU.mult)
        dma("sp", gains[:, 0, :], mla_q_gain[l:l + 1, :].re("o n -> (o n)").pbc(128))
        dma("sp", gains[:, 1, :], mla_k_gain[l:l + 1, :].re("o n -> (o n)").pbc(128))
        ts("dve", gains[:, 0, :], gains[:, 0, :], 96.0 ** -0.5, ALU.mult)
        dma("pool", wglu, s5_w_glu[l].re("(c p) n -> p c n", p=128))

    def src_tile(l, b, i):
        if i < 2:
            return (ctx_in if l == 0 else xcmid)[b, i * 128:(i + 1) * 128, :]
        return (x_in if l == 0 else xmid)[b, (i - 2) * 128:(i - 1) * 128, :]

    def phase_norm(l, b):
        AR.reset(top=True)
        NX = 4
        xt = [AR.alloc("xt%d" % i, [D]) for i in range(NX)]
        junk = [AR.alloc("junk%d" % i, [D]) for i in range(NX)]
        xn = [AR.alloc("xn%d" % i, [D], BF16) for i in range(NX)]
        st4 = [AR.alloc("st%d" % i, [4]) for i in range(NX)]
        def tile_gen(i):
            v = 2 if i < 2 else b
            x_t, xn_t, s4 = xt[i % NX], xn[i % NX], st4[i % NX]
            dma("sp", x_t, src_tile(l, b, i))
            yield
            act(junk[i % NX], x_t, AF.Square, accum=s4[:, 0:1])
            yield
            ts("dve", s4[:, 1:2], s4[:, 0:1], 1.0 / D, ALU.mult, EPS, ALU.add)
            yield
            act(s4[:, 2:3], s4[:, 1:2], AF.Sqrt)
            yield
            kb.recip(s4[:, 3:4], s4[:, 2:3])
            ts("dve", xn_t, x_t, s4[:, 3:4], ALU.mult)
            yield
            ps = kb.psum_any(bf16=True)
            for c in range(KC):
                tr(ps[:, c * 128:(c + 1) * 128], xn_t[:, c * 128:(c + 1) * 128], identb)
            yield
            for c in range(KC):
                o = hT[:, c, i * 128:(i + 1) * 128]
                if c % 2 == 0:
                    act(o, ps[:, c * 128:(c + 1) * 128], AF.Identity, bias=modS[:, v, c:c + 1], scale=modA[:, v, c:c + 1])
                else:
                    ts("dve", o, ps[:, c * 128:(c + 1) * 128], modA[:, v, c:c + 1], ALU.mult, modS[:, v, c:c + 1], ALU.add)
            yield
        for g0 in range(0, 18, NX):
            lockstep([tile_gen(i) for i in range(g0, min(g0 + NX, 18))])

    def load_win(l, name, c0, ncols, top=False):
        w = AR.alloc(name, [KC, ncols], BF16, top=top)
        dma("pool", w, w_in[l, :, c0:c0 + ncols].re("(c p) n -> p c n", p=128))
        return w

    def proj_fm(w, col0, dst, p0, p1, evac):
        pos = p0
        while pos < p1:
            n = min(512, p1 - pos)
            ps = kb.psum_f()
            for k in range(KC):
                mm(ps[:, 0:n], w[:, k, col0:col0 + 128], hT[:, k, pos:pos + n], k == 0, k == KC - 1)
            evac(ps, pos, n)
            pos += n

    def phase_lru(l, b, with_ctx):
        AR.reset()
        w = load_win(l, "w_lru", O_LRU, 256)
        xr = AR.alloc("xr", [2, NPOS])
        xc_ = AR.alloc("xc", [2, NPOS])
        ysum = AR.alloc("ysum", [2, NPOS])
        NB_ = 768
        tmp_sets = [{nm: AR.alloc("%s%d" % (nm, q), [NB_]) for nm in ("r", "i", "a", "a2", "bb")} for q in range(2)]
        for c in range(2):
            proj_fm(w, c * 128, xr, 0, NPOS, lambda ps, pos, n, c=c: cp("act", xr[:, c, pos:pos + n].par("proj"), ps[:, 0:n]))
        for c in range(2):
            cw = lambda k: lp[:, LP_CW + c * 4 + k:LP_CW + c * 4 + k + 1]
            for (s0, s1) in ((0, TC), (TC, NPOS)):
                ts("dve", xc_[:, c, s0:s1], xr[:, c, s0:s1], cw(2), ALU.mult, lp[:, LP_CB + c:LP_CB + c + 1], ALU.add)
                stt(xc_[:, c, s0 + 1:s1], xr[:, c, s0:s1 - 1], cw(1), xc_[:, c, s0 + 1:s1], ALU.mult, ALU.add)
                stt(xc_[:, c, s0 + 2:s1], xr[:, c, s0:s1 - 2], cw(0), xc_[:, c, s0 + 2:s1], ALU.mult, ALU.add)
                stt(xc_[:, c, s0:s1 - 1], xr[:, c, s0 + 1:s1], cw(3), xc_[:, c, s0:s1 - 1], ALU.mult, ALU.add)
        blocks = [(0, TC)] + [(TC + j * NB_, min(TC + (j + 1) * NB_, NPOS)) for j in range((T + NB_ - 1) // NB_)]

        def chain(d, c):
            dc = d * 2 + c
            tmp = tmp_sets[d]
            out = ysum if d == 0 else xr
            order = blocks if d == 0 else [blocks[0]] + blocks[:0:-1]
            prev = None
            for (s0, s1) in order:
                n = s1 - s0
                r_, i_, a_, a2_, bb_ = (tmp[k][:, 0:n] for k in ("r", "i", "a", "a2", "bb"))
                for gi, dst in ((0, r_), (1, i_)):
                    q = 0
                    while q < n:
                        m = min(512, n - q)
                        ps = kb.psum_f()
                        mm(ps[:, 0:m], lruBD[:, d, gi, c, :], xc_[:, c, s0 + q:s0 + q + m], True, True)
                        bcol = (LP_BA if gi == 0 else LP_BX) + dc
                        act(dst[:, q:q + m], ps[:, 0:m], AF.Sigmoid, bias=lp[:, bcol:bcol + 1])
                        q += m
                yield
                act(a_, r_, AF.Exp, scale=lp[:, LP_NSP + dc:LP_NSP + dc + 1])
                act(a2_, r_, AF.Exp, scale=lp[:, LP_NSP2 + dc:LP_NSP2 + dc + 1])
                tt("pool", bb_, i_, xc_[:, c, s0:s1], ALU.mult)
                yield
                ts("dve", a2_, a2_, -1.0, ALU.mult, 1.0, ALU.add)
                yield
                act(a2_, a2_, AF.Sqrt)
                yield
                tt("dve", bb_, bb_, a2_, ALU.mult)
                init = 0.0 if prev is None else prev
                if d == 0:
                    kb.scan(out[:, c, s0:s1], a_, bb_, init)
                    prev = out[:, c, s1 - 1:s1]
                else:
                    kb.scan(out[:, c, s0:s1][:, ::-1], a_[:, ::-1], bb_[:, ::-1], init)
                    prev = out[:, c, s0:s0 + 1]
                yield
        for c in range(2):
            lockstep([chain(0, c), chain(1, c)])
        for c in range(2):
            lo = 0 if with_ctx else TC
            tt("dve", brT[2][c][:, lo:NPOS], ysum[:, c, lo:NPOS], xr[:, c, lo:NPOS], ALU.add)

    AR_car = [kb.sb("car%d" % i, [128, 1]) for i in range(2)]

    def phase_pool(l, b, with_ctx):
        AR.reset()
        w = load_win(l, "w_pool", O_POOL, 256)
        xp = AR.alloc("xp", [2, NPOS])
        cs = AR.alloc("cs", [2, NPOS + 2 * 17 + 2])
        pm = AR.alloc("pm", [2, NPOS])
        ones = AR.alloc("ones", [T])
        kb.memset("pool", ones, 1.0)
        for c in range(2):
            proj_fm(w, c * 128, xp, 0, NPOS, lambda ps, pos, n, c=c: cp("act", xp[:, c, pos:pos + n].par("proj"), ps[:, 0:n]))
        segs = [(0, TC, 0)] + [(TC, NPOS, TC + 17)]
        if not with_ctx:
            segs = segs[1:]
        for c in range(2):
            for (s0, s1, o0) in segs:
                L = s1 - s0
                kb.memset("pool", cs[:, c, o0:o0 + 9], 0.0)
                kb.scan(cs[:, c, o0 + 9:o0 + 9 + L], ones[:, 0:L], xp[:, c, s0:s1], 0.0)
                ts("dve", cs[:, c, o0 + 9 + L:o0 + 17 + L], cs[:, c, o0:o0 + 8], cs[:, c, o0 + 8 + L:o0 + 9 + L], ALU.add)
                for h in range(2):
                    hw = (1, 2, 4, 8)[2 * c + h]
                    rows = slice(h * 64, (h + 1) * 64)
                    base = o0 + 8
                    tt("dve", pm[rows, c, s0:s1], cs[rows, c, base + hw:base + hw + L], cs[rows, c, base - hw:base - hw + L], ALU.subtract)
                ts("dve", pm[:, c, s0:s1], pm[:, c, s0:s1], cpool[:, c:c + 1], ALU.mult)
                tt("dve", pm[:, c, s0:s0 + 8], pm[:, c, s0:s0 + 8], cpool[:, 2 + c * 8:2 + c * 8 + 8], ALU.mult)
                tt("dve", pm[:, c, s1 - 8:s1], pm[:, c, s1 - 8:s1], cpool[:, 18 + c * 8:18 + c * 8 + 8], ALU.mult)
                tt("pool", pm[:, c, s0:s1], pm[:, c, s0:s1], xp[:, c, s0:s1], ALU.subtract)
                pos = s0
                while pos < s1:
                    n = min(512, s1 - pos)
                    ps = kb.psum_f()
                    mm(ps[:, 0:n], poolBD[:, c, :], pm[:, c, pos:pos + n], True, True)
                    act(brT[3][c][:, pos:pos + n].par("pool"), ps[:, 0:n], AF.Identity, bias=lp[:, LP_PBS + c:LP_PBS + c + 1],
                        scale=lp[:, LP_PSC + c:LP_PSC + c + 1])
                    pos += n

    def phase_mla(l, b, with_ctx):
        AR.reset()
        wkv = load_win(l, "w_kv", 0, 160)
        wq = load_win(l, "w_cq", O_CQ, 256)
        qT = AR.alloc("qT", [4, NPOS], BF16)
        kT = AR.alloc("kT", [4, NPOS], BF16)
        Vt = AR.alloc("Vt", [18, 4, 68], BF16)
        lo_fixed = AR.lo
        kb.memset("dve", Vt.re("p a b c -> p (a b c)"), 1.0)
        for h_ in range(4):
            kb.memset("pool", qT[:, h_, :], 0.0)
            kb.memset("pool", kT[:, h_, :], 0.0)
        NS = 3
        sm = [AR.alloc("sm%d" % i, [32]) for i in range(NS)]
        kr = [AR.alloc("kr%d" % i, [32]) for i in range(NS)]
        cn = [AR.alloc("cn%d" % i, [384], BF16) for i in range(NS)]
        cnT = [AR.alloc("cnT%d" % i, [3, 128], BF16) for i in range(NS)]
        sq = [AR.alloc("sq%d" % i, [704]) for i in range(NS)]
        qk = [AR.alloc("qk%d" % i, [2, 4, 96]) for i in range(NS)]
        rg = [AR.alloc("rg%d" % i, [2, 4, 96]) for i in range(NS)]
        qkb = [AR.alloc("qkb%d" % i, [2, 4, 96], BF16) for i in range(NS)]
        rt = [AR.alloc("rt%d" % i, [2, 2, 4, 32]) for i in range(NS)]
        kvs_ = [AR.alloc("kvs%d" % i, [512]) for i in range(NS)]
        qs_ = [AR.alloc("qs%d" % i, [384]) for i in range(NS)]

        def info(i):
            is_ctx = i < 2
            do_q = (not is_ctx) or with_ctx
            return is_ctx, do_q, i % NS, slice(i * 128, (i + 1) * 128)

        def stage_a(i):
            is_ctx, do_q, j, pos = info(i)
            s, cn_, cnT_, sq_ = sm[j], cn[j], cnT[j], sq[j]
            ps1 = kb.psum_any()
            for k in range(KC):
                mm(ps1[:, 0:160], hT[:, k, pos], wkv[:, k, :], k == 0, k == KC - 1)
            if do_q:
                for k in range(KC):
                    mm(ps1[:, 160:416], hT[:, k, pos], wq[:, k, :], k == 0, k == KC - 1)
            yield
            act(sq_[:, 0:128], ps1[:, 0:128], AF.Square, accum=s[:, 0:1])
            if do_q:
                act(sq_[:, 128:384], ps1[:, 160:416], AF.Square, accum=s[:, 1:2])
            else:
                kb.memset("pool", s[:, 1:2], 1.0)
            cp("act", kr[j], ps1[:, 128:160])
            yield
            act(s[:, 2:3], s[:, 0:1], AF.Sqrt, scale=1.0 / 128, bias=EPS)
            act(s[:, 3:4], s[:, 1:2], AF.Sqrt, scale=1.0 / 256, bias=EPS)
            yield
            kb.recip(s[:, 4:6], s[:, 2:4])
            ts("dve", cn_[:, 0:128], ps1[:, 0:128], s[:, 4:5], ALU.mult)
            if do_q:
                ts("dve", cn_[:, 128:384], ps1[:, 160:416], s[:, 5:6], ALU.mult)
            yield
            psT = kb.psum_any(bf16=True)
            for c in range(3 if do_q else 1):
                tr(psT[:, c * 128:(c + 1) * 128], cn_[:, c * 128:(c + 1) * 128], identb)
            yield
            cp("act", cnT_[:, 0:(3 if do_q else 1), :], psT[:, 0:(384 if do_q else 128)].re("p (c n) -> p c n", n=128))
            yield

        def stage_b(i):
            is_ctx, do_q, j, pos = info(i)
            s, cnT_, sq_, qk_, qkb_, rt_, rg_ = sm[j], cnT[j], sq[j], qk[j], qkb[j], rt[j], rg[j]
            pskv = kb.psum_any()
            mm(pskv, cnT_[:, 0, :], wukv, True, True)
            if do_q:
                psq = kb.psum_any()
                for c in range(2):
                    mm(psq[:, 0:384], cnT_[:, 1 + c, :], wuq[:, c, :], c == 0, c == 1)
            yield
            cp("act", kvs_[j], pskv)
            if do_q:
                cp("act", qs_[j], psq[:, 0:384])
            kv3 = kvs_[j].re("p (h e) -> p h e", h=4)
            q3 = qs_[j].re("p (h e) -> p h e", h=4)
            krope = kr[j]
            act(sq_[:, 640:672], krope, AF.Square, accum=s[:, 7:8])
            cp("act", Vt[:, i, :, 0:64].par("mlav"), kv3[:, :, 64:128])
            yield
            ksq = sq_[:, 0:256].re("p (h e) -> p h e", h=4)
            qsq = sq_[:, 256:640].re("p (h e) -> p h e", h=4)
            tt("pool", ksq, kv3[:, :, 0:64], kv3[:, :, 0:64], ALU.mult)
            if do_q:
                tt("pool", qsq, q3, q3, ALU.mult)
            yield
            kb.reduce(s[:, 12:16], ksq)
            if do_q:
                kb.reduce(s[:, 8:12], qsq)
            else:
                kb.memset("dve", s[:, 8:12], 1.0)
            ts("dve", s[:, 12:16], s[:, 12:16], s[:, 7:8], ALU.add)
            yield
            act(s[:, 8:16], s[:, 8:16], AF.Sqrt, scale=1.0 / 96, bias=EPS)
            yield
            kb.recip(s[:, 16:24], s[:, 8:16])
            yield
            tt("pool", rg_, s[:, 16:24].re("p (w h) -> p w h", w=2).us(3).bc([128, 2, 4, 96]),
               gains.us(2).bc([128, 2, 4, 96]), ALU.mult)
            yield
            if do_q:
                tt("dve", qk_[:, 0], q3, rg_[:, 0], ALU.mult)
            else:
                kb.memset("pool", qk_[:, 0], 0.0)
            tt("dve", qk_[:, 1, :, 0:64], kv3[:, :, 0:64], rg_[:, 1, :, 0:64], ALU.mult)
            tt("pool", qk_[:, 1, :, 64:96], krope.us(1).bc([128, 4, 32]), rg_[:, 1, :, 64:96], ALU.mult)
            yield
            if not is_ctx:
                ti = i - 2
                v = qk_[:, :, :, 64:96]
                t1_ = rt_[:, 0]
                t2_ = rt_[:, 1]
                Cb = ropeC[:, ti, :].us(1).us(1).bc([128, 2, 4, 32])
                tt("pool", t1_, v, Cb, ALU.mult)
                for a in range(2):
                    for s_ in range(2):
                        o_ = t2_[:, :, :, a * 16 + s_ * 8:a * 16 + s_ * 8 + 8]
                        i_ = qk_[:, :, :, 64 + a * 16 + (1 - s_) * 8:64 + a * 16 + (1 - s_) * 8 + 8]
                        Sb = ropeS[:, ti, a * 16 + s_ * 8:a * 16 + s_ * 8 + 8].us(1).us(1).bc([128, 2, 4, 8])
                        tt("dve" if (a + s_) % 2 == 0 else "pool", o_, i_, Sb, ALU.mult)
                yield
                tt("dve", v, t1_, t2_, ALU.add)
            cp("dve", qkb_, qk_)
            yield

        def stage_c(i):
            is_ctx, do_q, j, pos = info(i)
            qkb_ = qkb[j]
            psT2 = kb.psum_any(bf16=True)
            for w_ in range(2):
                if w_ == 0 and not do_q:
                    continue
                for h in range(4):
                    tr(psT2[0:96, (w_ * 4 + h) * 128:(w_ * 4 + h + 1) * 128], qkb_[:, w_, h, :], identb)
            yield
            if do_q:
                cp("act", qT[0:96, :, pos].par("mlaqk"), psT2[0:96, 0:512].re("p (h n) -> p h n", h=4))
            cp("act", kT[0:96, :, pos].par("mlaqk"), psT2[0:96, 512:1024].re("p (h n) -> p h n", h=4))

        def tile_gen(i):
            yield from stage_a(i)
            yield from stage_b(i)
            yield from stage_c(i)
        for g0 in range(0, 18, NS):
            lockstep([tile_gen(i) for i in range(g0, g0 + NS)])
        if "mla_stop1" in dbg:
            return
        P.barrier()
        AR.lo = lo_fixed
        omla = AR.alloc("omla", [18, 256], BF16)
        pT = [AR.alloc("pT%d" % i, [512], BF16) for i in range(3)]
        rc = [AR.alloc("rc%d" % i, [4]) for i in range(2)]
        jobs = []
        if with_ctx:
            jobs.append((0, TC, 0, 2))
        for qb in range(4):
            jobs.append((TC + qb * 512, TC + (qb + 1) * 512, 0, 18))
        pi = 0
        for (q0, q1, kt0, kt1) in jobs:
            nq = q1 - q0
            nqt = nq // 128
            for h in range(4):
                pso = [kb.psf[2 + t_] for t_ in range(nqt)]

                def s_mm(kt):
                    mm(kb.psf[kt % 2][:, 0:nq], kT[:, h, kt * 128:(kt + 1) * 128], qT[:, h, q0:q1], True, True)
                s_mm(kt0)
                for kt in range(kt0, kt1):
                    if kt + 1 < kt1:
                        s_mm(kt + 1)
                    pss = kb.psf[kt % 2]
                    p_ = pT[pi % 3]
                    pi += 1
                    act(p_[:, 0:nq], pss[:, 0:nq], AF.Exp)
                    for t_ in range(nqt):
                        mm(pso[t_][:, 0:68], p_[:, t_ * 128:(t_ + 1) * 128], Vt[:, kt, h, :], kt == kt0, kt == kt1 - 1)
                for t_ in range(nqt):
                    r_ = rc[t_ % 2]
                    kb.recip(r_[:, 0:1], pso[t_][:, 64:65])
                    ts("dve", omla[:, (q0 // 128) + t_, h * 64:(h + 1) * 64].par("omla"), pso[t_][:, 0:64], r_[:, 0:1], ALU.mult)
        for i in range(0 if with_ctx else 2, 18):
            psT = kb.psum_b()
            for c in range(2):
                tr(psT[:, c * 128:(c + 1) * 128], omla[:, i, c * 128:(c + 1) * 128], identb)
            for c in range(2):
                cp("act", brT[0][c][:, i * 128:(i + 1) * 128].par("brt0"), psT[:, c * 128:(c + 1) * 128])

    def phase_s5(l, b, with_ctx):
        AR.reset()
        AR2.lo = 0
        U = AR.alloc("U", [16, NCH])
        SN = [[AR.alloc("SN%d%d" % (d, ri), [8, NCH + 2]) for ri in range(2)] for d in range(2)]
        Ere = AR.alloc("Ere", [2, 8, 128])
        nEim = AR.alloc("nEim", [2, 8, 128])
        W3 = AR.alloc("W3", [16, 128])
        scr0 = AR.lo
        ws5 = load_win(l, "w_s5", O_S5, 256)
        Uc = AR.alloc("Uc", [16, 8, 16])
        kblocks = [(0, 32, 0), (32, 128, TC), (160, 128, TC + 1024)]
        for (k0, M, p0) in kblocks:
            pss = [kb.psum_f() for _ in range(4)]
            for r in range(8):
                ps = pss[r // 2]
                for k in range(KC):
                    mm(ps[0:M, (r % 2) * 256:(r % 2 + 1) * 256], hT[:, k, p0 + r:p0 + 8 * M:8], ws5[:, k, :], k == 0, k == KC - 1)
            for q in range(4):
                src = pss[q][0:M, :].re("p (r g i) -> p r g i", r=2, g=16)
                dst = Uc[0:M, :, 2 * q:2 * q + 2, :].re("p g r i -> p r g i").par("s5uc")
                cp("act" if q % 2 == 0 else "dve", dst, src)
            for gq in range(4):
                ps = kb.psum_f()
                for gg_ in range(4):
                    g = gq * 4 + gg_
                    tr(ps[:, gg_ * 128:gg_ * 128 + M], Uc[0:M, g, :, :].re("p r i -> p (r i)"), identf[0:M, 0:M])
                cp("act" if gq % 2 == 0 else "dve", U[:, gq * 4:gq * 4 + 4, k0:k0 + M].par("s5u"),
                   ps.re("p (g n) -> p g n", g=4)[:, :, 0:M])
        if "s5_U" in dbg:
            dma("sp", tap("s5U", [128, 16 * NCH]), U.re("p g k -> p (g k)"))
        P.barrier()
        AR.lo = scr0
        if "s5_stop1" in dbg:
            return
        cached = (b > 0) and ("s5_nocache" not in dbg)
        if not cached:
            kb.memset("pool", W3, 0.0)
        else:
            dma("sp", Ere.re("p d g n -> p (d g n)"), st_main[0])
            dma("act", nEim.re("p d g n -> p (d g n)"), st_main[1])
            dma("sp", W3.re("p g n -> p (g n)"), st_main[2])
        prm = AR.alloc("prm", [40, 8])
        PW = [AR.alloc("pw%d" % i, [10, 8]) for i in range(2)]
        Bb = [AR.alloc("Bb%d" % i, [8, 16]) for i in range(2)]
        Braw = [AR.alloc("Braw%d" % i, [8, 16]) for i in range(2)]
        Craw = [AR.alloc("Craw%d" % i, [8, 16]) for i in range(2)]
        Dbc = AR.alloc("Dbc", [256])
        t8 = AR.alloc("t8", [8, 16])
        w3t = [AR.alloc("w3t%d" % i, [128]) for i in range(2)]
        tE = AR.alloc("tE", [8, 128])
        Fm = [AR.alloc("F%d" % i, [8, 8, 16]) for i in range(2)]
        Ep = [AR.alloc("Ep%d" % i, [8, 128]) for i in range(2)]
        Cnat = [AR.alloc("Cnat%d" % i, [16, 64], at=Ep[i].off, buf=Ep[i].buf) for i in range(2)]
        rs_sets = [[AR.alloc("rs%d_%d" % (q, i), [NCH], at=Fm[0].off + (q * 7 + i) * NCH) for i in range(6)]
                   for q in range(2)]
        rho_sets = [AR.alloc("rho1_%d" % q, [NCH], at=Fm[0].off + (q * 7 + 6) * NCH) for q in range(2)]
        assert Fm[0].off + 14 * NCH <= Ep[1].off + Ep[1].w
        W1 = [AR2.alloc("W1%d" % i, [16, 64]) for i in range(2)]
        tab = [AR2.alloc("tab%d" % i, [8, NCH]) for i in range(2)]
        dma("sp", Dbc, s5_d[l:l + 1, :].re("o n -> (o n)").pbc(128))
        dsel = cmask[:, 2, :]

        def pv(i):
            return prm[:, i, :]

        for d in range(2):
            if d == 1:
                P.barrier()
            if not cached:
                dma("sp", pv(0), s5_a_re[l, d].re("(gp h) p -> (h p) gp", h=2), slow=True)
                dma("act", pv(1), s5_a_im[l, d].re("(gp h) p -> (h p) gp", h=2), slow=True)
                ldt = t8[:, 0, :]
                dma("sp", ldt, s5_log_dt[l, d:d + 1, :].re("o g -> (o g)").pbc(128))
                dma("sp", Cnat[0][0:16], s5_c_re[l, d].re("g o p -> o g p"))
                dma("act", Cnat[1][0:16], s5_c_im[l, d].re("g o p -> o g p"))
                for h in range(2):
                    rows = slice(h * 64, (h + 1) * 64)
                    cp("dve", prm[rows, 2, :], ldt[rows, h:16:2])
                    dma("sp", Braw[0][rows].par("ldw"), s5_b_re[l, d].re("(gp h) p i -> h p gp i", h=2)[h])
                    dma("act", Braw[1][rows].par("ldw"), s5_b_im[l, d].re("(gp h) p i -> h p gp i", h=2)[h])
                for ri in range(2):
                    ps = kb.psum_f()
                    for g in range(16):
                        gp, h = g // 2, g % 2
                        mm(ps[h * 64:(h + 1) * 64, gp * 16:(gp + 1) * 16], Cnat[ri][0:16, g, :], identf[0:16, 0:16], True, True)
                    cp("act", Craw[ri], ps[:, 0:128].re("p (g o) -> p g o", g=8))
                if "s5_b1" in dbg:
                    return
                act(pv(2), pv(2), AF.Exp)
                tt("dve", pv(3), pv(0), pv(2), ALU.mult)
                tt("dve", pv(4), pv(1), pv(2), ALU.mult)
                act(pv(5), pv(3), AF.Exp)
                for (dst, shift) in ((6, 0.0), (7, math.pi / 2)):
                    xx, kk, ki = pv(30), pv(31), prm[:, 32, :]
                    ts("dve", xx, pv(4), shift, ALU.add)
                    ts("dve", kk, xx, 1.0 / (2 * math.pi), ALU.mult)
                    kint = TV(ki.ap.bitcast(I32), ki.buf)
                    cp("dve", kint, kk)
                    cp("dve", kk, kint)
                    stt(xx, kk, -6.28125, xx, ALU.mult, ALU.add)
                    stt(xx, kk, -(2 * math.pi - 6.28125), xx, ALU.mult, ALU.add)
                    ts("dve", xx, xx, math.pi, ALU.min, -math.pi, ALU.max)
                    act(pv(dst), xx, AF.Sin)
                kb.memset("dve", PW[0][:, 0, :], 1.0)
                kb.memset("dve", PW[1][:, 0, :], 0.0)
                tt("dve", PW[0][:, 1, :], pv(5), pv(7), ALU.mult)
                tt("dve", PW[1][:, 1, :], pv(5), pv(6), ALU.mult)
                for j in range(2, 9):
                    tt("dve", pv(30), PW[0][:, j - 1, :], PW[0][:, 1, :], ALU.mult)
                    tt("dve", pv(31), PW[1][:, j - 1, :], PW[1][:, 1, :], ALU.mult)
                    tt("dve", PW[0][:, j, :], pv(30), pv(31), ALU.subtract)
                    tt("dve", pv(30), PW[0][:, j - 1, :], PW[1][:, 1, :], ALU.mult)
                    tt("dve", pv(31), PW[1][:, j - 1, :], PW[0][:, 1, :], ALU.mult)
                    tt("dve", PW[1][:, j, :], pv(30), pv(31), ALU.add)
                act(pv(8), pv(3), AF.Exp, scale=-16.0)
                tt("dve", PW[0][:, 9, :], PW[0][:, 8, :], pv(8), ALU.mult)
                tt("dve", PW[1][:, 9, :], PW[1][:, 8, :], pv(8), ALU.mult)
                ts("dve", PW[1][:, 9, :], PW[1][:, 9, :], -1.0, ALU.mult)
                act(pv(9), pv(3), AF.Exp, scale=8.0)
                act(pv(10), pv(3), AF.Exp, scale=-8.0)
                tt("dve", pv(11), PW[0][:, 8, :], pv(10), ALU.mult)
                tt("dve", pv(12), PW[1][:, 8, :], pv(10), ALU.mult)
                ts("dve", pv(13), PW[0][:, 1, :], -1.0, ALU.add)
                tt("dve", pv(14), pv(0), pv(0), ALU.mult)
                tt("dve", pv(15), pv(1), pv(1), ALU.mult)
                tt("dve", pv(14), pv(14), pv(15), ALU.add)
                kb.recip(pv(14), pv(14))
                tt("dve", pv(15), pv(13), pv(0), ALU.mult)
                tt("dve", pv(16), PW[1][:, 1, :], pv(1), ALU.mult)
                tt("dve", pv(15), pv(15), pv(16), ALU.add)
                tt("dve", pv(15), pv(15), pv(14), ALU.mult)
                tt("dve", pv(16), PW[1][:, 1, :], pv(0), ALU.mult)
                tt("dve", pv(17), pv(13), pv(1), ALU.mult)
                tt("dve", pv(16), pv(16), pv(17), ALU.subtract)
                tt("dve", pv(16), pv(16), pv(14), ALU.mult)
                fre = pv(15).us(2).bc([128, 8, 16])
                fim = pv(16).us(2).bc([128, 8, 16])
                tt("dve", Bb[0], Braw[0], fre, ALU.mult)
                tt("dve", t8, Braw[1], fim, ALU.mult)
                tt("dve", Bb[0], Bb[0], t8, ALU.subtract)
                tt("dve", Bb[1], Braw[1], fre, ALU.mult)
                tt("dve", t8, Braw[0], fim, ALU.mult)
                tt("dve", Bb[1], Bb[1], t8, ALU.add)
                for t_ in range(8):
                    f_ = t_ + 1 if d == 0 else 8 - t_
                    pr = PW[0][:, f_, :].us(2).bc([128, 8, 16])
                    pi_ = PW[1][:, f_, :].us(2).bc([128, 8, 16])
                    eo = Ere[:, d, :, t_ * 16:(t_ + 1) * 16]
                    ei = nEim[:, d, :, t_ * 16:(t_ + 1) * 16]
                    tt("dve", eo, Craw[0], pr, ALU.mult)
                    tt("dve", t8, Craw[1], pi_, ALU.mult)
                    tt("dve", eo, eo, t8, ALU.subtract)
                    tt("dve", ei, Craw[0], pi_, ALU.mult)
                    tt("dve", t8, Craw[1], pr, ALU.mult)
                    tt("dve", ei, ei, t8, ALU.add)
                    ts("dve", ei, ei, -1.0, ALU.mult)
                for r in range(8):
                    e_ = 7 - r if d == 0 else r
                    pr = PW[0][:, e_, :].us(2).bc([128, 8, 16])
                    pi_ = PW[1][:, e_, :].us(2).bc([128, 8, 16])
                    tt("dve", Fm[0][:, :, r, :], Bb[0], pr, ALU.mult)
                    tt("dve", t8, Bb[1], pi_, ALU.mult)
                    tt("dve", Fm[0][:, :, r, :], Fm[0][:, :, r, :], t8, ALU.subtract)
                    tt("dve", Fm[1][:, :, r, :], Bb[1], pr, ALU.mult)
                    tt("dve", t8, Bb[0], pi_, ALU.mult)
                    tt("dve", Fm[1][:, :, r, :], Fm[1][:, :, r, :], t8, ALU.add)
                qr = PW[0][:, 9, :].us(2).bc([128, 8, 128])
                qi = PW[1][:, 9, :].us(2).bc([128, 8, 128])
                tt("dve", Ep[0], Ere[:, d], qr, ALU.mult)
                tt("dve", tE, nEim[:, d], qi, ALU.mult)
                tt("dve", Ep[0], Ep[0], tE, ALU.add)
                tt("dve", Ep[1], nEim[:, d], qr, ALU.mult)
                tt("dve", tE, Ere[:, d], qi, ALU.mult)
                tt("dve", Ep[1], Ep[1], tE, ALU.subtract)
                if "s5_b2" in dbg:
                    return
                for ri in range(2):
                    for gq in range(4):
                        ps = kb.psum_f()
                        for gg_ in range(4):
                            g = gq * 4 + gg_
                            gp, h = g // 2, g % 2
                            rows = slice(h * 64, (h + 1) * 64)
                            mm(ps[:, gg_ * 64:(gg_ + 1) * 64], Fm[ri][:, gp].re("p r i -> p (r i)"), identf[:, h * 64:(h + 1) * 64], True, True)
                        cp("act", W1[ri][:, gq * 4:gq * 4 + 4, :], ps[:, 0:256].re("p (g n) -> p g n", g=4))
                if "s5_b3" in dbg:
                    return
                for g in range(16):
                    gp, h = g // 2, g % 2
                    rows = slice(h * 64, (h + 1) * 64)
                    ps = kb.psf[h * 2 + (g // 2) % 2]
                    mm(ps[:, 0:128], Fm[0][rows, gp].re("p r i -> p (r i)"), Ep[0][rows, gp, :], True, False)
                    mm(ps[:, 0:128], Fm[1][rows, gp].re("p r i -> p (r i)"), Ep[1][rows, gp, :], False, True)
                    wt = w3t[g % 2]
                    tt("dve", wt, ps[:, 0:128], cmask[:, d, :], ALU.mult)
                    tt("pool", W3[:, g, :], W3[:, g, :], wt, ALU.add)
                if d == 0:
                    for g in range(16):
                        dcol = Dbc[:, g * 16:(g + 1) * 16].us(1).bc([128, 8, 16])
                        wt = w3t[g % 2]
                        tt("dve", wt.re("p (t o) -> p t o", t=8), dsel.re("p (t o) -> p t o", t=8), dcol, ALU.mult)
                        tt("pool", W3[:, g, :], W3[:, g, :], wt, ALU.add)
                if "s5_b4" in dbg:
                    return
                kb.memset("dve", tab[0][:, :, 0:1], 1.0)
                kb.memset("dve", tab[1][:, :, 0:1], 0.0)
                cp("dve", tab[0][:, :, 1:2], pv(11).us(2))
                cp("dve", tab[1][:, :, 1:2], pv(12).us(2))
                n = 2
                while n < NCH:
                    m = min(n, NCH - n)
                    tt("dve", pv(30), tab[0][:, :, n - 1], pv(11), ALU.mult)
                    tt("dve", pv(31), tab[1][:, :, n - 1], pv(12), ALU.mult)
                    tt("dve", pv(33), pv(30), pv(31), ALU.subtract)
                    tt("dve", pv(30), tab[0][:, :, n - 1], pv(12), ALU.mult)
                    tt("dve", pv(31), tab[1][:, :, n - 1], pv(11), ALU.mult)
                    tt("dve", pv(34), pv(30), pv(31), ALU.add)
                    Pr = pv(33).us(2).bc([128, 8, m])
                    Pi = pv(34).us(2).bc([128, 8, m])
                    ta = tE[:, :, 0:m]
                    tt("dve", tab[0][:, :, n:n + m], tab[0][:, :, 0:m], Pr, ALU.mult)
                    tt("dve", ta, tab[1][:, :, 0:m], Pi, ALU.mult)
                    tt("dve", tab[0][:, :, n:n + m], tab[0][:, :, n:n + m], ta, ALU.subtract)
                    tt("dve", tab[1][:, :, n:n + m], tab[0][:, :, 0:m], Pi, ALU.mult)
                    tt("dve", ta, tab[1][:, :, 0:m], Pr, ALU.mult)
                    tt("dve", tab[1][:, :, n:n + m], tab[1][:, :, n:n + m], ta, ALU.add)
                    n *= 2
                dma("sp", st_dir[d, :, 0:1024], W1[0].re("p g n -> p (g n)"))
                dma("sp", st_dir[d, :, 1024:2048], W1[1].re("p g n -> p (g n)"))
                dma("act", st_dir[d, :, 2048:2048 + 8 * NCH], tab[0].re("p g n -> p (g n)"))
                dma("act", st_dir[d, :, 2048 + 8 * NCH:2048 + 16 * NCH], tab[1].re("p g n -> p (g n)"))
                dma("sp", st_dir[d, :, 2048 + 16 * NCH:2048 + 16 * NCH + 8], pv(9))
            else:
                dma("sp", W1[0].re("p g n -> p (g n)"), st_dir[d, :, 0:1024])
                dma("sp", W1[1].re("p g n -> p (g n)"), st_dir[d, :, 1024:2048])
                dma("act", tab[0].re("p g n -> p (g n)"), st_dir[d, :, 2048:2048 + 8 * NCH])
                dma("act", tab[1].re("p g n -> p (g n)"), st_dir[d, :, 2048 + 8 * NCH:2048 + 16 * NCH])
                dma("sp", pv(9), st_dir[d, :, 2048 + 16 * NCH:2048 + 16 * NCH + 8])
            if "s5_stop2" in dbg:
                return
            P.barrier()
            def nat(tv, sl, rev):
                v = tv[:, sl]
                return v[:, ::-1] if rev else v

            def gp_gen(gp, d=d):
                rs, rho1 = rs_sets[gp % 2], rho_sets[gp % 2]
                psv = [kb.psum_f(), kb.psum_f()]
                for ri in range(2):
                    for h in range(2):
                        g = gp * 2 + h
                        mm(psv[ri][h * 64:(h + 1) * 64, 0:NCH], W1[ri][:, g, :], U[:, g, :], True, True)
                cp("pool", rho1, pv(9)[:, gp:gp + 1].bc([128, NCH]))
                yield
                Cn, Sn = tab[0][:, gp, :], tab[1][:, gp, :]
                if d == 0:
                    segs = [(slice(0, NCH), slice(0, NCH), False)]
                else:
                    segs = [(slice(0, 32), slice(0, 32), True), (slice(32, NCH), slice(32, NCH), True)]
                for (js, ks, rev) in segs:
                    vre, vim = nat(psv[0][:, 0:NCH], ks, rev), nat(psv[1][:, 0:NCH], ks, rev)
                    tt("dve", rs[0][:, js], vre, Cn[:, js], ALU.mult)
                    tt("dve", rs[1][:, js], vim, Sn[:, js], ALU.mult)
                    tt("dve", rs[4][:, js], vim, Cn[:, js], ALU.mult)
                    tt("dve", rs[5][:, js], vre, Sn[:, js], ALU.mult)
                yield
                tt("pool", rs[2], rs[0], rs[1], ALU.add)
                tt("pool", rs[3], rs[4], rs[5], ALU.subtract)
                yield
                kb.scan(rs[4], rho1, rs[2], 0.0)
                kb.scan(rs[5], rho1, rs[3], 0.0)
                yield
                tt("dve", rs[0], rs[4], Cn, ALU.mult)
                tt("dve", rs[1], rs[5], Sn, ALU.mult)
                tt("dve", rs[2], rs[4], Sn, ALU.mult)
                tt("dve", rs[3], rs[5], Cn, ALU.mult)
                yield
                for (js, ks, rev) in segs:
                    if d == 0:
                        osl = slice(1, NCH + 1)
                    else:
                        osl = slice(0, 32) if ks.start == 0 else slice(33, NCH + 1)
                    ore = nat(SN[d][0][:, gp, :], osl, rev).par("s5sn")
                    oim = nat(SN[d][1][:, gp, :], osl, rev).par("s5sn")
                    tt("pool", ore, rs[0][:, js], rs[1][:, js], ALU.subtract)
                    tt("pool", oim, rs[2][:, js], rs[3][:, js], ALU.add)
                yield
            for gp0 in range(0, 8, 2):
                lockstep([gp_gen(gp0), gp_gen(gp0 + 1)])
            for ri in range(2):
                if d == 0:
                    kb.memset("pool", SN[0][ri][:, :, 0:1], 0.0)
                else:
                    kb.memset("pool", SN[1][ri][:, :, 32:33], 0.0)
                    cp("pool", SN[1][ri][:, :, NCH + 1:NCH + 2], SN[1][ri][:, :, 0:1])
        if not cached:
            dma("sp", st_main[0], Ere.re("p d g n -> p (d g n)"))
            dma("act", st_main[1], nEim.re("p d g n -> p (d g n)"))
            dma("sp", st_main[2], W3.re("p g n -> p (g n)"))
        if "s5_stop3" in dbg:
            return
        P.barrier()
        AR.lo = scr0
        AR2.lo = 0
        Yc = AR.alloc("Yc", [8, 256])
        gg = AR.alloc("gg", [8, 256])
        g2 = AR.alloc("g2", [8, 256])
        sg = [AR.alloc("sg%d" % i, [512]) for i in range(2)]
        ggT = AR2.alloc("ggT", [2, NPOS])
        ggb = AR2.alloc("ggb", [2, NPOS], BF16)
        for (k0, M, p0) in kblocks:
            if k0 == 0 and not with_ctx:
                continue
            banks = [kb.psf[0], kb.psf[2], kb.psf[1], kb.psf[3]]
            for g in range(16):
                gp, h = g // 2, g % 2
                rows = slice(h * 64, (h + 1) * 64)
                bank = banks[h * 2 + (gp // 4)]
                o = bank[0:M, (gp % 4) * 128:(gp % 4 + 1) * 128]
                cf = slice(k0, k0 + M)
                cb_ = slice(k0 + 1, k0 + 1 + M) if k0 == 0 else slice(k0 + 2, k0 + 2 + M)
                mm(o, SN[0][0][rows, gp, cf], Ere[rows, 0, gp, :], True, False)
                mm(o, SN[0][1][rows, gp, cf], nEim[rows, 0, gp, :], False, False)
                mm(o, SN[1][0][rows, gp, cb_], Ere[rows, 1, gp, :], False, False)
                mm(o, SN[1][1][rows, gp, cb_], nEim[rows, 1, gp, :], False, False)
                mm(o, U[:, g, k0:k0 + M], W3[:, g, :], False, True)
            for h in range(2):
                for q in range(2):
                    bank = banks[h * 2 + q]
                    src = bank[0:M, :].re("p (g t o) -> p g t o", g=4, t=8)
                    dst = Yc[0:M].re("p t (g2 h o) -> p h g2 t o", h=2, o=16)[:, h, 4 * q:4 * q + 4].par("s5yc")
                    cp("act" if h == 0 else "dve", dst, src)
            if "s5_Y" in dbg:
                dma("sp", tap("s5Y", [NCH, 8 * 256])[k0:k0 + M, :], Yc[0:M].re("p t f -> p (t f)"))
            yv, gv, g2v = Yc[0:M], gg[0:M], g2[0:M]
            tt("pool", g2v, yv, yv, ALU.mult)
            ts("dve", g2v, g2v, 0.044715, ALU.mult, 1.0, ALU.add)
            tt("pool", g2v, g2v, yv, ALU.mult)
            act(g2v, g2v, AF.Sigmoid, scale=1.5957691216057308)
            tt("dve", gv, g2v, yv, ALU.mult)
            for t_ in range(8):
                ps = kb.psum_f()
                for c in range(2):
                    tr(ps[:, c * 128:c * 128 + M], gv[:, t_, c * 128:(c + 1) * 128], identf[0:M, 0:M])
                for c in range(2):
                    cp("act" if c == 0 else "dve", ggT[:, c, p0 + t_:p0 + 8 * M:8], ps[:, c * 128:c * 128 + M])
        lo = 0 if with_ctx else TC
        for c in range(2):
            cp("dve", ggb[:, c, lo:NPOS], ggT[:, c, lo:NPOS])
        for c in range(2):
            pos = lo
            while pos < NPOS:
                n = min(512, NPOS - pos)
                ps = kb.psum_f()
                for k in range(2):
                    mm(ps[:, 0:n], wglu[:, k, c * 128:(c + 1) * 128], ggb[:, k, pos:pos + n], k == 0, k == 1)
                s_ = sg[(pos // 512) % 2]
                act(s_[:, 0:n], ps[:, 0:n], AF.Sigmoid)
                tt("dve", brT[1][c][:, pos:pos + n], s_[:, 0:n], ggT[:, c, pos:pos + n], ALU.mult)
                pos += n

    def phase_merge(l, b, with_ctx):
        AR.reset(top=True)
        lo = 0 if with_ctx else TC
        blocks = []
        pos = lo
        while pos < NPOS:
            n = min(512, NPOS - pos) if pos >= TC else TC - pos
            blocks.append((pos, n))
            pos += n
        ypT = AR.alloc("ypT", [KC, NPOS], BF16, top=True)
        wbr = AR.alloc("wbr", [4, 2, D], BF16, top=True)
        wo = AR.alloc("wo", [KC, D], BF16, top=True)
        wml = [AR.alloc("wml%d" % i, [KC, 4, 128], BF16, top=True) for i in range(2)]

        def load_wml(oc):
            for n_ in range(4):
                c0 = O_MERGE + n_ * D + oc * 128
                dma("pool", wml[oc % 2][:, :, n_, :].par("ldw"), w_in[l, :, c0:c0 + 128].re("(c p) n -> p c n", p=128))
        wg = AR.alloc("w_gate", [KC, 1024], BF16)
        for cc in range(8):
            dma("pool", wg[:, :, cc * 128:(cc + 1) * 128].par("ldw"),
                w_in[l, :, O_GATE + cc * 128:O_GATE + (cc + 1) * 128].re("(c p) n -> p c n", p=128))
        for n_ in range(4):
            dma("pool", wbr[:, n_].par("ldw"), w_branch[l, n_].re("(c p) n -> p c n", p=128))
        load_wml(0)
        load_wml(1)
        dma("pool", wo, w_out[l].re("(c p) n -> p c n", p=128))
        sl = [AR.alloc("sl%d" % i, [512], BF16) for i in range(2)]
        it = 0
        for cc in range(8):
            for (p0, n) in blocks:
                ps = kb.psum_f()
                for k in range(KC):
                    mm(ps[:, 0:n], wg[:, k, cc * 128:(cc + 1) * 128], hT[:, k, p0:p0 + n], k == 0, k == KC - 1)
                s_ = sl[it % 2]
                it += 1
                act(s_[:, 0:n], ps[:, 0:n], AF.Silu)
                br = brT[cc // 2][cc % 2][:, p0:p0 + n]
                tt("dve", br, br, s_[:, 0:n], ALU.mult)
        AR.reset()
        sg = [AR.alloc("sg%d" % i, [512]) for i in range(2)]
        tm = [AR.alloc("tm%d" % i, [512]) for i in range(2)]
        acc = [AR.alloc("acc%d" % i, [512]) for i in range(2)]
        it = 0
        ib = 0
        for oc in range(8):
            wm = wml[oc % 2]
            if 1 <= oc and oc + 1 < 8:
                load_wml(oc + 1)
            for (p0, n) in blocks:
                ac = acc[ib % 2]
                ib += 1
                for n_ in range(4):
                    psA = kb.psum_f()
                    for k in range(2):
                        mm(psA[:, 0:n], wbr[:, n_, k, oc * 128:(oc + 1) * 128], brT[n_][k][:, p0:p0 + n], k == 0, k == 1)
                    psB = kb.psum_f()
                    for k in range(KC):
                        mm(psB[:, 0:n], wm[:, k, n_, :], hT[:, k, p0:p0 + n], k == 0, k == KC - 1)
                    s_ = sg[it % 2]
                    t_ = tm[it % 2]
                    it += 1
                    act(s_[:, 0:n], psB[:, 0:n], AF.Sigmoid)
                    if n_ == 0:
                        tt("dve", ac[:, 0:n], psA[:, 0:n], s_[:, 0:n], ALU.mult)
                    elif n_ < 3:
                        tt("dve", t_[:, 0:n], psA[:, 0:n], s_[:, 0:n], ALU.mult)
                        tt("dve", ac[:, 0:n], ac[:, 0:n], t_[:, 0:n], ALU.add)
                    else:
                        tt("dve", t_[:, 0:n], psA[:, 0:n], s_[:, 0:n], ALU.mult)
                        tt("dve", ypT[:, oc, p0:p0 + n].par("ypt"), ac[:, 0:n], t_[:, 0:n], ALU.add)
        AR.reset()
        gbc = AR.alloc("gbc", [2, D])
        dma("sp", gbc[:, 0, :], modD[l, b, 2 * D:3 * D].pbc(128))
        if with_ctx:
            dma("sp", gbc[:, 1, :], modD[l, 2, 2 * D:3 * D].pbc(128))
        xt = [AR.alloc("xt%d" % i, [D]) for i in range(2)]
        yt = [AR.alloc("yt%d" % i, [D]) for i in range(2)]
        for i in range(0 if with_ctx else 2, 18):
            x_t, y_t = xt[i % 2], yt[i % 2]
            dma("sp", x_t, src_tile(l, b, i))
            for hf in range(2):
                ps = kb.psum_f()
                for k in range(KC):
                    mm(ps, ypT[:, k, i * 128:(i + 1) * 128], wo[:, k, hf * 512:(hf + 1) * 512], k == 0, k == KC - 1)
                tt("dve", y_t[:, hf * 512:(hf + 1) * 512], ps, gbc[:, 1 if i < 2 else 0, hf * 512:(hf + 1) * 512], ALU.mult)
            tt("pool", y_t, y_t, x_t, ALU.add)
            if i < 2:
                dst = (tap("xc1", [nb, TC, D]) if "x1out" in dbg else xcmid)[b, i * 128:(i + 1) * 128, :]
            else:
                dst = (xmid if (l < DEPTH - 1 and "x1out" not in dbg) else y_out)[b, (i - 2) * 128:(i - 1) * 128, :]
            dma("act", dst, y_t)

    def dump_br(name, n, lo):
        o = tap(name, [2, 128, NPOS])
        tmpf = AR.alloc("dump_" + name, [NPOS])
        for c in range(2):
            cp("dve", tmpf[:, lo:NPOS], brT[n][c][:, lo:NPOS])
            dma("sp", o[c, :, lo:NPOS], tmpf[:, lo:NPOS])

    for l in layers:
        with_ctx = l < DEPTH - 1
        prep_layer(l)
        for b in range(nb):
            if "prep_only" in dbg:
                continue
            phase_norm(l, b)
            if "hT" in dbg and b == 0 and l == layers[0]:
                o = tap("hT", [KC, 128, NPOS])
                AR.reset()
                tmpf = AR.alloc("dump_hT", [NPOS])
                for c in range(KC):
                    cp("dve", tmpf, hT[:, c, :])
                    dma("sp", o[c], tmpf)
            only = dbg & {"only_lru", "only_pool", "only_mla", "only_s5", "only_norm"}
            if not only or "only_s5" in only:
                phase_s5(l, b, with_ctx)
            if not only or "only_lru" in only:
                phase_lru(l, b, with_ctx)
            if not only or "only_pool" in only:
                phase_pool(l, b, with_ctx)
            if not only or "only_mla" in only:
                phase_mla(l, b, with_ctx)
            if "br" in dbg and b == 0 and l == layers[0]:
                AR.reset()
                lo = 0 if with_ctx else TC
                for n_, nm in enumerate(("mla", "s5", "lru", "pool")):
                    dump_br(nm, n_, lo)
            if "nomerge" not in dbg:
                phase_merge(l, b, with_ctx)
    P.barrier()
    P.emit()
    kb.st.close()
    return nc, kb


def _consts():
    ident = np.eye(128, dtype=np.float32)
    rows_n = T // 64
    row = np.repeat(np.arange(rows_n), 64).astype(np.float32)
    col = np.tile(np.arange(64), rows_n).astype(np.float32)
    nf = 8
    inv = (np.float32(10000.0) ** (-np.arange(nf, dtype=np.float32) / nf)).astype(np.float32)
    ar = (row[:, None] * inv).astype(np.float32)
    ac = (col[:, None] * inv).astype(np.float32)
    cr, sr, cc, sc = np.cos(ar), np.sin(ar), np.cos(ac), np.sin(ac)
    ropeC = np.concatenate([cr, cr, cc, cc], axis=1).astype(np.float32)
    ropeS = np.concatenate([-sr, sr, -sc, sc], axis=1).astype(np.float32)
    cpool = np.ones((128, 34), np.float32)
    for c in range(2):
        for p in range(128):
            w = (2, 4, 8, 16)[2 * c + p // 64]
            hw = w // 2
            cpool[p, c] = 1.0 / w
            for t in range(8):
                cnt = t + hw if t < hw else w
                cpool[p, 2 + c * 8 + t] = w / cnt
            for j in range(8):
                dist = 8 - j
                cnt = dist + hw if dist < hw else w
                cpool[p, 18 + c * 8 + j] = w / cnt
    m = np.zeros((128, 3, 128), np.float32)
    for r in range(8):
        for t in range(8):
            if r <= t:
                m[r * 16:(r + 1) * 16, 0, t * 16:(t + 1) * 16] = 1.0
            if r >= t:
                m[r * 16:(r + 1) * 16, 1, t * 16:(t + 1) * 16] = 1.0
            if r == t:
                m[r * 16:(r + 1) * 16, 2, t * 16:(t + 1) * 16] = np.eye(16, dtype=np.float32)
    return {"c_ident": ident, "c_ropeC": ropeC, "c_ropeS": ropeS, "c_pool": cpool, "c_mask": m}


_WNAMES = ["w_ada", "b_ada", "norm_g", "w_in", "mla_q_norm", "mla_kv_norm", "mla_w_uq", "mla_w_ukv", "mla_q_gain",
           "mla_k_gain", "s5_a_re", "s5_a_im", "s5_log_dt", "s5_b_re", "s5_b_im", "s5_c_re", "s5_c_im", "s5_d",
           "s5_w_glu", "lru_conv_w", "lru_conv_b", "lru_lambda", "lru_w_a", "lru_b_a", "lru_w_x", "lru_b_x",
           "pool_w", "pool_b", "pool_scale", "w_branch", "w_out"]


def make_in_maps(inputs, n_cores, nb):
    consts = _consts()
    maps = []
    for r in range(n_cores):
        bs = slice(r * nb, (r + 1) * nb)
        m = {"x": np.ascontiguousarray(inputs["x"][bs], dtype=np.float32),
             "ctx": np.ascontiguousarray(inputs["ctx"][bs], dtype=np.float32)}
        cv = np.zeros((3, D), np.float32)
        cv[0:nb] = np.asarray(inputs["c"], dtype=np.float32)[bs]
        cv[2] = np.asarray(inputs["c_ctx"], dtype=np.float32)
        m["cvec"] = cv
        for k in _WNAMES:
            m[k] = np.ascontiguousarray(inputs[k], dtype=np.float32)
        m.update(consts)
        maps.append(m)
    return maps


_CACHE = {}


def kernel(**inputs):
    n_cores, nb = 8, 2
    if "nc" not in _CACHE:
        _CACHE["nc"] = build_program(nb=nb)[0]
    nc = _CACHE["nc"]
    maps = make_in_maps(inputs, n_cores, nb)
    res = run_bass_kernel_spmd(nc, maps, core_ids=list(range(n_cores)))
    out = np.concatenate([np.asarray(r["y"], dtype=np.float32) for r in res.results], axis=0)
    return out
```

```python
import math
import contextlib
import numpy as np
import concourse.bass as bass
import concourse.mybir as mybir
from concourse.bass_utils import run_bass_kernel_spmd

F32 = mybir.dt.float32
BF16 = mybir.dt.bfloat16
I32 = mybir.dt.int32
AF = mybir.ActivationFunctionType
ALU = mybir.AluOpType
AX = mybir.AxisListType

ENGS = ("pe", "act", "dve", "pool", "sp")
NDMA = 24

D = 1024
KC = 8
T = 2048
TC = 256
NPOS = T + TC
DEPTH = 2
O_KROPE, O_S5, O_LRU, O_CQ, O_POOL, O_GATE, O_MERGE, IN_W = 128, 160, 416, 672, 928, 1184, 2208, 6304
EPS = 1e-6
NCH = NPOS // 8


PAR_OFF = set()


class Buf:
    __slots__ = ("name", "w", "wp", "r")

    def __init__(self, name):
        self.name = name
        self.w = []
        self.wp = []
        self.r = []


class TV:
    par_ = False

    def __init__(self, ap, buf):
        self.ap, self.buf = ap, buf

    def par(self, tag=""):
        t = TV(self.ap, self.buf)
        t.par_ = tag not in PAR_OFF
        return t

    def __getitem__(self, k):
        return TV(self.ap[k], self.buf)

    def re(self, s, **kw):
        return TV(self.ap.rearrange(s, **kw), self.buf)

    def bc(self, shape):
        return TV(self.ap.to_broadcast(list(shape)), self.buf)

    def us(self, ax):
        return TV(self.ap.unsqueeze(ax), self.buf)

    def pbc(self, n):
        return TV(self.ap.partition_broadcast(n), self.buf)

    @property
    def shape(self):
        return tuple(self.ap.shape)


def _bufs(*xs):
    out = []
    for x in xs:
        if isinstance(x, TV):
            out.append(x.buf)
        elif isinstance(x, (list, tuple)):
            out.extend(_bufs(*x))
    return out


def _ap(x):
    return x.ap if isinstance(x, TV) else x


class Prog:
    def __init__(self, nc):
        self.nc = nc
        self.ops = {e: [] for e in ENGS}
        self.cnt = {e: 0 for e in ENGS}
        self.waited = {e: {} for e in ENGS}
        self.dma_i = 0
        self.dma_last = [0] * NDMA

    def _deps(self, eng, reads, writes, par=False):
        toks = []
        for b in reads:
            toks.extend(b.w)
            toks.extend(b.wp)
        for b in writes:
            toks.extend(b.w)
            if not par:
                toks.extend(b.wp)
            toks.extend(b.r)
        need = {}
        for (k, v, e) in toks:
            if e == "pe" and eng == "pe":
                continue
            if self.waited[eng].get(k, 0) >= v:
                continue
            if need.get(k, 0) < v:
                need[k] = v
        for k, v in need.items():
            self.waited[eng][k] = v
        return list(need.items())

    def _mark(self, tok, reads, writes, par=False):
        for b in writes:
            if par:
                best = {}
                for (k, v, e) in b.wp + [tok]:
                    if k not in best or best[k][1] < v:
                        best[k] = (k, v, e)
                b.wp = list(best.values())
            else:
                b.w = [tok]
                b.wp = []
                b.r = []
        for b in reads:
            if b in writes:
                continue
            b.r.append(tok)
            if len(b.r) > 16:
                best = {}
                for (k, v, e) in b.r:
                    if k not in best or best[k][1] < v:
                        best[k] = (k, v, e)
                b.r = list(best.values())

    def op(self, eng, fn, reads=(), writes=(), par=False):
        reads = list(dict.fromkeys(reads))
        writes = list(dict.fromkeys(writes))
        waits = self._deps(eng, reads, writes, par)
        self.cnt[eng] += 1
        tok = (eng, self.cnt[eng], eng)
        self.ops[eng].append((waits, fn, (eng, 1)))
        self._mark(tok, reads, writes, par)

    def dma(self, eng, fn, reads=(), writes=(), par=False):
        reads = list(dict.fromkeys(reads))
        writes = list(dict.fromkeys(writes))
        waits = self._deps(eng, reads, writes, par)
        slot = self.dma_i % NDMA
        self.dma_i += 1
        key = ("dma", slot)
        prev = self.dma_last[slot]
        if prev and self.waited[eng].get(key, 0) < prev:
            waits.append((key, prev))
            self.waited[eng][key] = prev
        val = prev + 16
        self.dma_last[slot] = val
        tok = (key, val, "dma")
        self.ops[eng].append((waits, fn, (key, 16)))
        self._mark(tok, reads, writes, par)

    def barrier(self):
        for E in ENGS:
            waits = []
            for e in ENGS:
                v = self.cnt[e]
                if v and self.waited[E].get(e, 0) < v:
                    waits.append((e, v))
                    self.waited[E][e] = v
            for slot in range(NDMA):
                v = self.dma_last[slot]
                key = ("dma", slot)
                if v and self.waited[E].get(key, 0) < v:
                    waits.append((key, v))
                    self.waited[E][key] = v
            if waits:
                self.ops[E].append((waits, None, None))

    def emit(self):
        nc = self.nc
        sems = {}
        with contextlib.ExitStack() as st:
            for e in ENGS:
                sems[e] = st.enter_context(nc.semaphore("s_" + e))
            for i in range(NDMA):
                sems[("dma", i)] = st.enter_context(nc.semaphore("s_dma%d" % i))
            block = st.enter_context(nc.Block())

            def run(engname):
                def body(eng):
                    for waits, fn, inc in self.ops[engname]:
                        for k, v in waits:
                            eng.wait_ge(sems[k], v)
                        if fn is None:
                            continue
                        ins = fn(eng)
                        ins.then_inc(sems[inc[0]], inc[1])
                return body
            block.tensor(run("pe"))
            block.scalar(run("act"))
            block.vector(run("dve"))
            block.gpsimd(run("pool"))
            block.sync(run("sp"))


class KB:
    def __init__(self, nc, nb, layers, dbg):
        self.nc = nc
        self.P = Prog(nc)
        self.nb = nb
        self.layers = layers
        self.dbg = dbg
        self.st = contextlib.ExitStack()
        self.din = {}
        self.dout = {}
        self.ps_i = 0
        self.psb_i = 0
        self.ps8_i = 0

    def tt(self, eng, out, a, b, op):
        self.P.op(eng, lambda e: e.tensor_tensor(out=out.ap, in0=a.ap, in1=b.ap, op=op), _bufs(a, b), _bufs(out), par=out.par_)

    def ts(self, eng, out, a, s1, op0, s2=None, op1=None):
        if op1 is None:
            self.P.op(eng, lambda e: e.tensor_scalar(out=out.ap, in0=a.ap, scalar1=_ap(s1), scalar2=None, op0=op0),
                      _bufs(a, s1), _bufs(out), par=out.par_)
        else:
            self.P.op(eng, lambda e: e.tensor_scalar(out=out.ap, in0=a.ap, scalar1=_ap(s1), scalar2=_ap(s2), op0=op0, op1=op1),
                      _bufs(a, s1, s2), _bufs(out), par=out.par_)

    def stt(self, out, a, s, b, op0, op1):
        self.P.op("dve", lambda e: e.scalar_tensor_tensor(out=out.ap, in0=a.ap, scalar=_ap(s), in1=b.ap, op0=op0, op1=op1),
                  _bufs(a, s, b), _bufs(out))

    def act(self, out, a, func, bias=0.0, scale=1.0, accum=None):
        if accum is None:
            self.P.op("act", lambda e: e.activation(out=out.ap, in_=a.ap, func=func, bias=_ap(bias), scale=_ap(scale)),
                      _bufs(a, bias, scale), _bufs(out), par=out.par_)
        else:
            self.P.op("act", lambda e: e.activation(out=out.ap, in_=a.ap, func=func, bias=_ap(bias), scale=_ap(scale), accum_out=accum.ap),
                      _bufs(a, bias, scale), _bufs(out, accum))

    def cp(self, eng, out, a):
        if eng == "act":
            self.P.op("act", lambda e: e.copy(out=out.ap, in_=a.ap), _bufs(a), _bufs(out), par=out.par_)
        else:
            self.P.op(eng, lambda e: e.tensor_copy(out=out.ap, in_=a.ap), _bufs(a), _bufs(out), par=out.par_)

    def memset(self, eng, out, v):
        self.P.op(eng, lambda e: e.memset(out.ap, v), [], _bufs(out))

    def recip(self, out, a):
        self.P.op("dve", lambda e: e.reciprocal(out=out.ap, in_=a.ap), _bufs(a), _bufs(out))

    def mm(self, out, lhsT, rhs, start, stop):
        self.P.op("pe", lambda e: e.matmul(out.ap, lhsT=lhsT.ap, rhs=rhs.ap, start=start, stop=stop), _bufs(lhsT, rhs), _bufs(out))

    def tr(self, out, a, ident):
        self.P.op("pe", lambda e: e.transpose(out.ap, a.ap, ident.ap), _bufs(a, ident), _bufs(out))

    def scan(self, out, d0, d1, init):
        self.P.op("dve", lambda e: e.tensor_tensor_scan(out=out.ap, data0=d0.ap, data1=d1.ap, initial=_ap(init), op0=ALU.mult, op1=ALU.add),
                  _bufs(d0, d1, init), _bufs(out))

    def reduce(self, out, a, op=ALU.add):
        self.P.op("dve", lambda e: e.tensor_reduce(out=out.ap, in_=a.ap, axis=AX.X, op=op), _bufs(a), _bufs(out))

    def dma(self, q, out, a, slow=False):
        if slow:
            self.P.dma(q, lambda e: e.dma_start(out=out.ap, in_=a.ap, allow_slow_non_contiguous=True), _bufs(a), _bufs(out), par=out.par_)
        else:
            self.P.dma(q, lambda e: e.dma_start(out=out.ap, in_=a.ap), _bufs(a), _bufs(out), par=out.par_)

    def dram_in(self, name, shape, dt=F32):
        t = TV(self.nc.dram_tensor(name, list(shape), dt, kind="ExternalInput").ap(), Buf(name))
        self.din[name] = t
        return t

    def dram_out(self, name, shape, dt=F32):
        t = TV(self.nc.dram_tensor(name, list(shape), dt, kind="ExternalOutput").ap(), Buf(name))
        self.dout[name] = t
        return t

    def dram_tmp(self, name, shape, dt=F32):
        return TV(self.nc.dram_tensor(name, list(shape), dt, kind="Internal").ap(), Buf(name))

    def sb(self, name, shape, dt=F32):
        t = self.st.enter_context(self.nc.sbuf_tensor(name, list(shape), dt))
        return TV(t[:], Buf(name))

    def psum_f(self):
        i = self.ps_i % 6
        self.ps_i += 1
        return self.psf[i]

    def psum_b(self):
        i = self.psb_i % 2
        self.psb_i += 1
        return self.psb[i]

    def psum_any(self, bf16=False):
        i = self.ps8_i % 8
        self.ps8_i += 1
        return self.ps8b[i] if bf16 else self.ps8[i]


class Arena:
    def __init__(self, kb, words, ap=None):
        self.kb = kb
        self.words = words
        if ap is None:
            self.f = kb.st.enter_context(kb.nc.sbuf_tensor("arena_f", [128, words], F32))
        else:
            self.f = ap
        self.lo = 0
        self.hi = words

    def alloc(self, name, free, dt=F32, top=False, buf=None, at=None):
        nel = int(np.prod(free))
        w = nel if dt == F32 else (nel + 1) // 2
        w = ((w + 15) // 16) * 16
        if at is not None:
            off = at
        elif top:
            self.hi -= w
            off = self.hi
        else:
            off = self.lo
            self.lo += w
        assert self.lo <= self.hi, (name, self.lo, self.hi)
        assert off + w <= self.words
        if dt != F32:
            ap = self.f[:, off:off + w].bitcast(dt)[:, 0:nel]
        else:
            ap = self.f[:, off:off + nel]
        if len(free) > 1:
            names = " ".join("d%d" % i for i in range(len(free)))
            kw = {"d%d" % i: int(free[i]) for i in range(len(free))}
            ap = ap.rearrange("p (%s) -> p %s" % (names, names), **kw)
        tv = TV(ap, buf if buf is not None else Buf(name))
        tv.off = off
        tv.w = w
        return tv

    def reset(self, top=False):
        self.kb.P.barrier()
        self.lo = 0
        if top:
            self.hi = self.words


def lockstep(gens):
    gens = list(gens)
    while gens:
        nxt = []
        for g in gens:
            try:
                next(g)
                nxt.append(g)
            except StopIteration:
                pass
        gens = nxt


def build_program(nb=2, layers=(0, 1), dbg=None):
    nc = bass.Bass("TRN2", target_bir_lowering=False)
    kb = KB(nc, nb, layers, dbg)
    P = kb.P
    tt, ts, stt, act, cp, mm, tr, dma = kb.tt, kb.ts, kb.stt, kb.act, kb.cp, kb.mm, kb.tr, kb.dma
    dbg = dbg or set()
    PAR_OFF.clear()
    PAR_OFF.update(x[6:] for x in dbg if x.startswith("nopar_"))

    x_in = kb.dram_in("x", [nb, T, D])
    ctx_in = kb.dram_in("ctx", [nb, TC, D])
    cvec = kb.dram_in("cvec", [3, D])
    w_ada = kb.dram_in("w_ada", [DEPTH, D, 3 * D])
    b_ada = kb.dram_in("b_ada", [DEPTH, 3 * D])
    norm_g = kb.dram_in("norm_g", [DEPTH, D])
    w_in = kb.dram_in("w_in", [DEPTH, D, IN_W])
    mla_q_norm = kb.dram_in("mla_q_norm", [DEPTH, 256])
    mla_kv_norm = kb.dram_in("mla_kv_norm", [DEPTH, 128])
    mla_w_uq = kb.dram_in("mla_w_uq", [DEPTH, 256, 384])
    mla_w_ukv = kb.dram_in("mla_w_ukv", [DEPTH, 128, 512])
    mla_q_gain = kb.dram_in("mla_q_gain", [DEPTH, 96])
    mla_k_gain = kb.dram_in("mla_k_gain", [DEPTH, 96])
    s5_a_re = kb.dram_in("s5_a_re", [DEPTH, 2, 16, 64])
    s5_a_im = kb.dram_in("s5_a_im", [DEPTH, 2, 16, 64])
    s5_log_dt = kb.dram_in("s5_log_dt", [DEPTH, 2, 16])
    s5_b_re = kb.dram_in("s5_b_re", [DEPTH, 2, 16, 64, 16])
    s5_b_im = kb.dram_in("s5_b_im", [DEPTH, 2, 16, 64, 16])
    s5_c_re = kb.dram_in("s5_c_re", [DEPTH, 2, 16, 16, 64])
    s5_c_im = kb.dram_in("s5_c_im", [DEPTH, 2, 16, 16, 64])
    s5_d = kb.dram_in("s5_d", [DEPTH, 256])
    s5_w_glu = kb.dram_in("s5_w_glu", [DEPTH, 256, 256])
    lru_conv_w = kb.dram_in("lru_conv_w", [DEPTH, 4, 256])
    lru_conv_b = kb.dram_in("lru_conv_b", [DEPTH, 256])
    lru_lambda = kb.dram_in("lru_lambda", [DEPTH, 2, 256])
    lru_w_a = kb.dram_in("lru_w_a", [DEPTH, 2, 4, 64, 64])
    lru_b_a = kb.dram_in("lru_b_a", [DEPTH, 2, 256])
    lru_w_x = kb.dram_in("lru_w_x", [DEPTH, 2, 4, 64, 64])
    lru_b_x = kb.dram_in("lru_b_x", [DEPTH, 2, 256])
    pool_w = kb.dram_in("pool_w", [DEPTH, 4, 64, 64])
    pool_b = kb.dram_in("pool_b", [DEPTH, 256])
    pool_scale = kb.dram_in("pool_scale", [DEPTH, 256])
    w_branch = kb.dram_in("w_branch", [DEPTH, 4, 256, D])
    w_out = kb.dram_in("w_out", [DEPTH, D, D])
    c_ident = kb.dram_in("c_ident", [128, 128])
    c_ropeC = kb.dram_in("c_ropeC", [T, 32])
    c_ropeS = kb.dram_in("c_ropeS", [T, 32])
    c_pool = kb.dram_in("c_pool", [128, 2 + 16 + 16])
    c_mask = kb.dram_in("c_mask", [128, 3, 128])
    y_out = kb.dram_out("y", [nb, T, D])
    xmid = kb.dram_tmp("xmid", [nb, T, D])
    xcmid = kb.dram_tmp("xcmid", [nb, TC, D])
    modD = kb.dram_tmp("modD", [DEPTH, 3, 3 * D])
    st_main = kb.dram_tmp("st_main", [3, 128, 2048])
    st_dir = kb.dram_tmp("st_dir", [2, 128, 2048 + 16 * NCH + 8])
    dbg_out = {}

    def tap(name, shape):
        if name not in dbg_out:
            dbg_out[name] = kb.dram_out("dbg_" + name, shape)
        return dbg_out[name]

    kb.ps8 = []
    for i in range(8):
        t = kb.st.enter_context(nc.psum_tensor("psf%d" % i, [128, 512], F32))
        kb.ps8.append(TV(t[:], Buf("psf%d" % i)))
    kb.psf = kb.ps8[0:6]
    kb.psb = [TV(kb.ps8[i].ap.bitcast(BF16), kb.ps8[i].buf) for i in (6, 7)]
    kb.ps8b = [TV(kb.ps8[i].ap.bitcast(BF16), kb.ps8[i].buf) for i in range(8)]

    hT = kb.sb("hT", [128, KC, NPOS], BF16)
    bigbr = kb.sb("bigbr", [128, 8 * NPOS], BF16)
    _slot = {0: 0, 2: 2, 3: 4, 1: 6}
    brT = [[TV(bigbr.ap[:, (_slot[n] + c) * NPOS:(_slot[n] + c + 1) * NPOS], Buf("brT%d%d" % (n, c))) for c in range(2)]
           for n in range(4)]
    identf = kb.sb("identf", [128, 128])
    identb = kb.sb("identb", [128, 128], BF16)
    ropeC = kb.sb("ropeC", [128, 16, 32])
    ropeS = kb.sb("ropeS", [128, 16, 32])
    cpool = kb.sb("cpool", [128, 34])
    cmask = kb.sb("cmask", [128, 3, 128])
    modA = kb.sb("modA", [128, 3, KC])
    modS = kb.sb("modS", [128, 3, KC])
    lp = kb.sb("lp", [128, 64])
    lruBD = kb.sb("lruBD", [128, 2, 2, 2, 128])
    poolBD = kb.sb("poolBD", [128, 2, 128])
    wukv = kb.sb("wukv", [128, 512], BF16)
    wuq = kb.sb("wuq", [128, 2, 384], BF16)
    gains = kb.sb("gains", [128, 2, 96])
    wglu = kb.sb("wglu", [128, 2, 256], BF16)
    AR = Arena(kb, 28000)
    wst = kb.sb("wst", [128, KC, 416], BF16)

    def prefetch(l, parts):
        for (d0, c0, n) in parts:
            dma("pool", wst[:, :, d0:d0 + n].par("wst"), w_in[l, :, c0:c0 + n].re("(c p) n -> p c n", p=128))
    PF_S5 = [(0, O_S5, 256)]
    PF_LRU = [(0, O_LRU, 256)]
    PF_POOL = [(0, O_POOL, 256)]
    PF_MLA = [(0, 0, 160), (160, O_CQ, 256)]
    PF_GATE0 = [(0, O_GATE, 128)]
    AR2 = Arena(kb, 3 * NPOS, ap=bigbr.ap[:, 0:6 * NPOS].bitcast(F32))

    dma("sp", identf, c_ident)
    cp("dve", identb, identf)
    dma("sp", ropeC, c_ropeC.re("(i p) f -> p i f", p=128))
    dma("sp", ropeS, c_ropeS.re("(i p) f -> p i f", p=128))
    dma("sp", cpool, c_pool)
    dma("sp", cmask, c_mask)

    LP_CW = 0
    LP_CB = 8
    LP_NSP = 10
    LP_NSP2 = 14
    LP_BA = 18
    LP_BX = 22
    LP_PSC = 26
    LP_PBS = 28
    LP_G = 30
    LP_TMP = 40

    def prep_layer(l):
        AR.reset(top=True)
        cT = AR.alloc("cT", [KC, 3])
        for v in range(3):
            dma("sp", cT[:, :, v], cvec[v].re("(c p) -> p c", p=128), slow=True)
        cact = AR.alloc("cact", [KC, 3])
        act(cact, cT, AF.Silu)
        brow = AR.alloc("brow", [3 * D])
        dma("sp", brow[0:3, :], b_ada[l:l + 1, :].re("o n -> (o n)").pbc(3))
        modrow = AR.alloc("modrow", [3 * D])
        wa = [AR.alloc("wa%d" % i, [KC, 512]) for i in range(4)]
        for cb in range(4):
            dma("sp" if cb % 2 == 0 else "act", wa[cb], w_ada[l, :, cb * 512:(cb + 1) * 512].re("(c p) n -> p c n", p=128))
        for cb in range(6):
            w = wa[cb % 4]
            if cb >= 4:
                dma("sp" if cb % 2 == 0 else "act", w, w_ada[l, :, cb * 512:(cb + 1) * 512].re("(c p) n -> p c n", p=128))
            ps = kb.psum_f()
            for k in range(KC):
                mm(ps[0:3, :], cact[:, k, :], w[:, k, :], k == 0, k == KC - 1)
            tt("dve", modrow[0:3, cb * 512:(cb + 1) * 512], ps[0:3, :], brow[0:3, cb * 512:(cb + 1) * 512], ALU.add)
        dma("sp", modD[l], modrow[0:3, :])
        sc = AR.alloc("sc", [3, KC])
        for v in range(3):
            dma("sp", modS[:, v, :], modD[l, v, 0:D].re("(c p) -> p c", p=128), slow=True)
            dma("sp", sc[:, v, :], modD[l, v, D:2 * D].re("(c p) -> p c", p=128), slow=True)
        dma("sp", lp[:, LP_G:LP_G + 8], norm_g[l].re("(c p) -> p c", p=128), slow=True)
        for v in range(3):
            stt(modA[:, v, :], sc[:, v, :], 1.0, lp[:, LP_G:LP_G + 8], ALU.add, ALU.mult)
        for k in range(4):
            dma("sp", lp[:, LP_CW:LP_CW + 8].re("p (c k) -> p c k", c=2)[:, :, k], lru_conv_w[l, k].re("(c p) -> p c", p=128), slow=True)
        dma("sp", lp[:, LP_CB:LP_CB + 2], lru_conv_b[l].re("(c p) -> p c", p=128), slow=True)
        lam = lp[:, LP_TMP:LP_TMP + 4]
        for d in range(2):
            dma("sp", lam[:, d * 2:d * 2 + 2], lru_lambda[l, d].re("(c p) -> p c", p=128), slow=True)
            dma("sp", lp[:, LP_BA + d * 2:LP_BA + d * 2 + 2], lru_b_a[l, d].re("(c p) -> p c", p=128), slow=True)
            dma("sp", lp[:, LP_BX + d * 2:LP_BX + d * 2 + 2], lru_b_x[l, d].re("(c p) -> p c", p=128), slow=True)
        t0 = lp[:, LP_TMP + 4:LP_TMP + 8]
        t1 = lp[:, LP_TMP + 8:LP_TMP + 12]
        t2 = lp[:, LP_TMP + 12:LP_TMP + 16]
        t3 = lp[:, LP_TMP + 16:LP_TMP + 20]
        ts("dve", t0, lam, -1.0, ALU.mult)
        tt("dve", t0, t0, lam, ALU.max)
        act(t1, t0, AF.Exp, scale=-1.0)
        ts("dve", t2, t1, 2.0, ALU.add)
        kb.recip(t2, t2)
        tt("dve", t2, t2, t1, ALU.mult)
        tt("dve", t3, t2, t2, ALU.mult)
        ts("dve", t0, t3, 1.0 / 11.0, ALU.mult, 1.0 / 9.0, ALU.add)
        for cf in (1.0 / 7.0, 1.0 / 5.0, 1.0 / 3.0, 1.0):
            tt("dve", t0, t0, t3, ALU.mult)
            ts("dve", t0, t0, cf, ALU.add)
        tt("dve", t0, t0, t2, ALU.mult)
        ts("dve", t1, lam, -1.0, ALU.mult, 0.0, ALU.max)
        stt(t0, t0, 2.0, t1, ALU.mult, ALU.add)
        ts("dve", lp[:, LP_NSP:LP_NSP + 4], t0, -8.0, ALU.mult)
        ts("dve", lp[:, LP_NSP2:LP_NSP2 + 4], t0, -16.0, ALU.mult)
        kb.memset("dve", lruBD, 0.0)
        for d in range(2):
            for gi, wsrc in enumerate((lru_w_a, lru_w_x)):
                for c in range(2):
                    for h in range(2):
                        dma("sp" if (c + h) % 2 == 0 else "act", lruBD[h * 64:(h + 1) * 64, d, gi, c, h * 64:(h + 1) * 64].par("ldw"), wsrc[l, d, 2 * c + h])
        kb.memset("dve", poolBD, 0.0)
        for c in range(2):
            for h in range(2):
                dma("sp", poolBD[h * 64:(h + 1) * 64, c, h * 64:(h + 1) * 64].par("ldw"), pool_w[l, 2 * c + h])
        dma("sp", lp[:, LP_PSC:LP_PSC + 2], pool_scale[l].re("(c p) -> p c", p=128), slow=True)
        dma("sp", lp[:, LP_PBS:LP_PBS + 2], pool_b[l].re("(c p) -> p c", p=128), slow=True)
        tt("dve", lp[:, LP_PBS:LP_PBS + 2], lp[:, LP_PBS:LP_PBS + 2], lp[:, LP_PSC:LP_PSC + 2], ALU.mult)
        kvn = lp[:, LP_TMP + 20:LP_TMP + 21]
        qn = lp[:, LP_TMP + 21:LP_TMP + 23]
        dma("sp", kvn, mla_kv_norm[l].re("(p o) -> p o", o=1), slow=True)
        dma("sp", qn, mla_q_norm[l].re("(c p) -> p c", p=128), slow=True)
        wtmp = AR.alloc("wtmp", [2, 512])
        dma("sp", wtmp[:, 0, :], mla_w_ukv[l])
        ts("dve", wukv, wtmp[:, 0, :], kvn, ALU.mult)
        wtmp2 = AR.alloc("wtmp2", [2, 384])
        dma("sp", wtmp2, mla_w_uq[l].re("(c p) n -> p c n", p=128))
        for c in range(2):
            ts("dve", wuq[:, c, :], wtmp2[:, c, :], qn[:, c:c + 1], ALU.mult)
        dma("sp", gains[:, 0, :], mla_q_gain[l:l + 1, :].re("o n -> (o n)").pbc(128))
        dma("sp", gains[:, 1, :], mla_k_gain[l:l + 1, :].re("o n -> (o n)").pbc(128))
        ts("dve", gains[:, 0, :], gains[:, 0, :], 96.0 ** -0.5, ALU.mult)
        dma("pool", wglu, s5_w_glu[l].re("(c p) n -> p c n", p=128))

    def src_tile(l, b, i):
        if i < 2:
            return (ctx_in if l == 0 else xcmid)[b, i * 128:(i + 1) * 128, :]
        return (x_in if l == 0 else xmid)[b, (i - 2) * 128:(i - 1) * 128, :]

    def phase_norm(l, b):
        AR.reset(top=True)
        NX = 4
        xt = [AR.alloc("xt%d" % i, [D]) for i in range(NX)]
        junk = [AR.alloc("junk%d" % i, [D]) for i in range(NX)]
        xn = [AR.alloc("xn%d" % i, [D], BF16) for i in range(NX)]
        st4 = [AR.alloc("st%d" % i, [4]) for i in range(NX)]
        def tile_gen(i):
            v = 2 if i < 2 else b
            x_t, xn_t, s4 = xt[i % NX], xn[i % NX], st4[i % NX]
            dma("sp", x_t, src_tile(l, b, i))
            yield
            act(junk[i % NX], x_t, AF.Square, accum=s4[:, 0:1])
            yield
            ts("dve", s4[:, 1:2], s4[:, 0:1], 1.0 / D, ALU.mult, EPS, ALU.add)
            yield
            act(s4[:, 2:3], s4[:, 1:2], AF.Sqrt)
            yield
            kb.recip(s4[:, 3:4], s4[:, 2:3])
            ts("dve", xn_t, x_t, s4[:, 3:4], ALU.mult)
            yield
            ps = kb.psum_any(bf16=True)
            for c in range(KC):
                tr(ps[:, c * 128:(c + 1) * 128], xn_t[:, c * 128:(c + 1) * 128], identb)
            yield
            for c in range(KC):
                o = hT[:, c, i * 128:(i + 1) * 128]
                if c % 2 == 0:
                    act(o, ps[:, c * 128:(c + 1) * 128], AF.Identity, bias=modS[:, v, c:c + 1], scale=modA[:, v, c:c + 1])
                else:
                    ts("dve", o, ps[:, c * 128:(c + 1) * 128], modA[:, v, c:c + 1], ALU.mult, modS[:, v, c:c + 1], ALU.add)
            yield
        for g0 in range(0, 18, NX):
            lockstep([tile_gen(i) for i in range(g0, min(g0 + NX, 18))])

    def load_win(l, name, c0, ncols, top=False):
        w = AR.alloc(name, [KC, ncols], BF16, top=top)
        dma("pool", w, w_in[l, :, c0:c0 + ncols].re("(c p) n -> p c n", p=128))
        return w

    def proj_fm(w, col0, dst, p0, p1, evac):
        pos = p0
        while pos < p1:
            n = min(512, p1 - pos)
            ps = kb.psum_f()
            for k in range(KC):
                mm(ps[:, 0:n], w[:, k, col0:col0 + 128], hT[:, k, pos:pos + n], k == 0, k == KC - 1)
            evac(ps, pos, n)
            pos += n

    def phase_lru(l, b, with_ctx):
        AR.reset()
        w = wst[:, :, 0:256]
        xr = AR.alloc("xr", [2, NPOS])
        xc_ = AR.alloc("xc", [2, NPOS])
        ysum = AR.alloc("ysum", [2, NPOS])
        NB_ = 512
        tmp_sets = [{nm: AR.alloc("%s%d" % (nm, q), [NB_]) for nm in ("r", "i", "a", "a2", "bb")} for q in range(4)]
        for c in range(2):
            proj_fm(w, c * 128, xr, 0, NPOS, lambda ps, pos, n, c=c: cp("act", xr[:, c, pos:pos + n].par("proj"), ps[:, 0:n]))
        prefetch(l, PF_POOL)
        for c in range(2):
            cw = lambda k: lp[:, LP_CW + c * 4 + k:LP_CW + c * 4 + k + 1]
            for (s0, s1) in ((0, TC), (TC, NPOS)):
                ts("dve", xc_[:, c, s0:s1], xr[:, c, s0:s1], cw(2), ALU.mult, lp[:, LP_CB + c:LP_CB + c + 1], ALU.add)
                stt(xc_[:, c, s0 + 1:s1], xr[:, c, s0:s1 - 1], cw(1), xc_[:, c, s0 + 1:s1], ALU.mult, ALU.add)
                stt(xc_[:, c, s0 + 2:s1], xr[:, c, s0:s1 - 2], cw(0), xc_[:, c, s0 + 2:s1], ALU.mult, ALU.add)
                stt(xc_[:, c, s0:s1 - 1], xr[:, c, s0 + 1:s1], cw(3), xc_[:, c, s0:s1 - 1], ALU.mult, ALU.add)
        blocks = [(0, TC)] + [(TC + j * NB_, min(TC + (j + 1) * NB_, NPOS)) for j in range((T + NB_ - 1) // NB_)]

        def chain(d, c):
            dc = d * 2 + c
            tmp = tmp_sets[dc]
            out = ysum if d == 0 else xr
            order = blocks if d == 0 else [blocks[0]] + blocks[:0:-1]
            prev = None
            for (s0, s1) in order:
                n = s1 - s0
                r_, i_, a_, a2_, bb_ = (tmp[k][:, 0:n] for k in ("r", "i", "a", "a2", "bb"))
                for gi, dst in ((0, r_), (1, i_)):
                    q = 0
                    while q < n:
                        m = min(512, n - q)
                        ps = kb.psum_f()
                        mm(ps[:, 0:m], lruBD[:, d, gi, c, :], xc_[:, c, s0 + q:s0 + q + m], True, True)
                        bcol = (LP_BA if gi == 0 else LP_BX) + dc
                        act(dst[:, q:q + m], ps[:, 0:m], AF.Sigmoid, bias=lp[:, bcol:bcol + 1])
                        q += m
                yield
                act(a_, r_, AF.Exp, scale=lp[:, LP_NSP + dc:LP_NSP + dc + 1])
                act(a2_, r_, AF.Exp, scale=lp[:, LP_NSP2 + dc:LP_NSP2 + dc + 1])
                tt("pool", bb_, i_, xc_[:, c, s0:s1], ALU.mult)
                yield
                ts("dve", a2_, a2_, -1.0, ALU.mult, 1.0, ALU.add)
                yield
                act(a2_, a2_, AF.Sqrt)
                yield
                tt("dve", bb_, bb_, a2_, ALU.mult)
                init = 0.0 if prev is None else prev
                if d == 0:
                    kb.scan(out[:, c, s0:s1], a_, bb_, init)
                    prev = out[:, c, s1 - 1:s1]
                else:
                    kb.scan(out[:, c, s0:s1][:, ::-1], a_[:, ::-1], bb_[:, ::-1], init)
                    prev = out[:, c, s0:s0 + 1]
                yield
        lockstep([chain(0, 0), chain(1, 0), chain(0, 1), chain(1, 1)])
        for c in range(2):
            lo = 0 if with_ctx else TC
            tt("dve", brT[2][c][:, lo:NPOS], ysum[:, c, lo:NPOS], xr[:, c, lo:NPOS], ALU.add)

    AR_car = [kb.sb("car%d" % i, [128, 1]) for i in range(2)]

    def phase_pool(l, b, with_ctx):
        AR.reset()
        w = wst[:, :, 0:256]
        xp = AR.alloc("xp", [2, NPOS])
        cs = AR.alloc("cs", [2, NPOS + 2 * 17 + 2])
        pm = AR.alloc("pm", [2, NPOS])
        ones = AR.alloc("ones", [T])
        kb.memset("pool", ones, 1.0)
        for c in range(2):
            proj_fm(w, c * 128, xp, 0, NPOS, lambda ps, pos, n, c=c: cp("act", xp[:, c, pos:pos + n].par("proj"), ps[:, 0:n]))
        prefetch(l, PF_MLA)
        segs = [(0, TC, 0)] + [(TC, NPOS, TC + 17)]
        if not with_ctx:
            segs = segs[1:]
        for c in range(2):
            for (s0, s1, o0) in segs:
                L = s1 - s0
                kb.memset("pool", cs[:, c, o0:o0 + 9], 0.0)
                kb.scan(cs[:, c, o0 + 9:o0 + 9 + L], ones[:, 0:L], xp[:, c, s0:s1], 0.0)
                ts("dve", cs[:, c, o0 + 9 + L:o0 + 17 + L], cs[:, c, o0:o0 + 8], cs[:, c, o0 + 8 + L:o0 + 9 + L], ALU.add)
                for h in range(2):
                    hw = (1, 2, 4, 8)[2 * c + h]
                    rows = slice(h * 64, (h + 1) * 64)
                    base = o0 + 8
                    tt("dve", pm[rows, c, s0:s1], cs[rows, c, base + hw:base + hw + L], cs[rows, c, base - hw:base - hw + L], ALU.subtract)
                ts("dve", pm[:, c, s0:s1], pm[:, c, s0:s1], cpool[:, c:c + 1], ALU.mult)
                tt("dve", pm[:, c, s0:s0 + 8], pm[:, c, s0:s0 + 8], cpool[:, 2 + c * 8:2 + c * 8 + 8], ALU.mult)
                tt("dve", pm[:, c, s1 - 8:s1], pm[:, c, s1 - 8:s1], cpool[:, 18 + c * 8:18 + c * 8 + 8], ALU.mult)
                tt("pool", pm[:, c, s0:s1], pm[:, c, s0:s1], xp[:, c, s0:s1], ALU.subtract)
                pos = s0
                while pos < s1:
                    n = min(512, s1 - pos)
                    ps = kb.psum_f()
                    mm(ps[:, 0:n], poolBD[:, c, :], pm[:, c, pos:pos + n], True, True)
                    act(brT[3][c][:, pos:pos + n].par("pool"), ps[:, 0:n], AF.Identity, bias=lp[:, LP_PBS + c:LP_PBS + c + 1],
                        scale=lp[:, LP_PSC + c:LP_PSC + c + 1])
                    pos += n

    def phase_mla(l, b, with_ctx):
        AR.reset()
        wkv = wst[:, :, 0:160]
        wq = wst[:, :, 160:416]
        qT = AR.alloc("qT", [4, NPOS], BF16)
        kT = AR.alloc("kT", [4, NPOS], BF16)
        Vt = AR.alloc("Vt", [18, 4, 68], BF16)
        lo_fixed = AR.lo
        kb.memset("dve", Vt.re("p a b c -> p (a b c)"), 1.0)
        for h_ in range(4):
            kb.memset("pool", qT[:, h_, :], 0.0)
            kb.memset("pool", kT[:, h_, :], 0.0)
        NS = 3
        sm = [AR.alloc("sm%d" % i, [32]) for i in range(NS)]
        kr = [AR.alloc("kr%d" % i, [32]) for i in range(NS)]
        cn = [AR.alloc("cn%d" % i, [384], BF16) for i in range(NS)]
        cnT = [AR.alloc("cnT%d" % i, [3, 128], BF16) for i in range(NS)]
        sq = [AR.alloc("sq%d" % i, [704]) for i in range(NS)]
        qk = [AR.alloc("qk%d" % i, [2, 4, 96]) for i in range(NS)]
        rg = [AR.alloc("rg%d" % i, [2, 4, 96]) for i in range(NS)]
        qkb = [AR.alloc("qkb%d" % i, [2, 4, 96], BF16) for i in range(NS)]
        rt = [AR.alloc("rt%d" % i, [2, 2, 4, 32]) for i in range(NS)]
        kvs_ = [AR.alloc("kvs%d" % i, [512]) for i in range(NS)]
        qs_ = [AR.alloc("qs%d" % i, [384]) for i in range(NS)]

        def info(i):
            is_ctx = i < 2
            do_q = (not is_ctx) or with_ctx
            return is_ctx, do_q, i % NS, slice(i * 128, (i + 1) * 128)

        def stage_a(i):
            is_ctx, do_q, j, pos = info(i)
            s, cn_, cnT_, sq_ = sm[j], cn[j], cnT[j], sq[j]
            ps1 = kb.psum_any()
            for k in range(KC):
                mm(ps1[:, 0:160], hT[:, k, pos], wkv[:, k, :], k == 0, k == KC - 1)
            if do_q:
                for k in range(KC):
                    mm(ps1[:, 160:416], hT[:, k, pos], wq[:, k, :], k == 0, k == KC - 1)
            yield
            act(sq_[:, 0:128], ps1[:, 0:128], AF.Square, accum=s[:, 0:1])
            if do_q:
                act(sq_[:, 128:384], ps1[:, 160:416], AF.Square, accum=s[:, 1:2])
            else:
                kb.memset("pool", s[:, 1:2], 1.0)
            cp("act", kr[j], ps1[:, 128:160])
            yield
            act(s[:, 2:3], s[:, 0:1], AF.Sqrt, scale=1.0 / 128, bias=EPS)
            act(s[:, 3:4], s[:, 1:2], AF.Sqrt, scale=1.0 / 256, bias=EPS)
            yield
            kb.recip(s[:, 4:6], s[:, 2:4])
            ts("dve", cn_[:, 0:128], ps1[:, 0:128], s[:, 4:5], ALU.mult)
            if do_q:
                ts("dve", cn_[:, 128:384], ps1[:, 160:416], s[:, 5:6], ALU.mult)
            yield
            psT = kb.psum_any(bf16=True)
            for c in range(3 if do_q else 1):
                tr(psT[:, c * 128:(c + 1) * 128], cn_[:, c * 128:(c + 1) * 128], identb)
            yield
            cp("act", cnT_[:, 0:(3 if do_q else 1), :], psT[:, 0:(384 if do_q else 128)].re("p (c n) -> p c n", n=128))
            yield

        def stage_b(i):
            is_ctx, do_q, j, pos = info(i)
            s, cnT_, sq_, qk_, qkb_, rt_, rg_ = sm[j], cnT[j], sq[j], qk[j], qkb[j], rt[j], rg[j]
            pskv = kb.psum_any()
            mm(pskv, cnT_[:, 0, :], wukv, True, True)
            if do_q:
                psq = kb.psum_any()
                for c in range(2):
                    mm(psq[:, 0:384], cnT_[:, 1 + c, :], wuq[:, c, :], c == 0, c == 1)
            yield
            cp("act", kvs_[j], pskv)
            if do_q:
                cp("act", qs_[j], psq[:, 0:384])
            kv3 = kvs_[j].re("p (h e) -> p h e", h=4)
            q3 = qs_[j].re("p (h e) -> p h e", h=4)
            krope = kr[j]
            act(sq_[:, 640:672], krope, AF.Square, accum=s[:, 7:8])
            cp("act", Vt[:, i, :, 0:64].par("mlav"), kv3[:, :, 64:128])
            yield
            ksq = sq_[:, 0:256].re("p (h e) -> p h e", h=4)
            qsq = sq_[:, 256:640].re("p (h e) -> p h e", h=4)
            tt("pool", ksq, kv3[:, :, 0:64], kv3[:, :, 0:64], ALU.mult)
            if do_q:
                tt("pool", qsq, q3, q3, ALU.mult)
            yield
            kb.reduce(s[:, 12:16], ksq)
            if do_q:
                kb.reduce(s[:, 8:12], qsq)
            else:
                kb.memset("dve", s[:, 8:12], 1.0)
            ts("dve", s[:, 12:16], s[:, 12:16], s[:, 7:8], ALU.add)
            yield
            act(s[:, 8:16], s[:, 8:16], AF.Sqrt, scale=1.0 / 96, bias=EPS)
            yield
            kb.recip(s[:, 16:24], s[:, 8:16])
            yield
            tt("pool", rg_, s[:, 16:24].re("p (w h) -> p w h", w=2).us(3).bc([128, 2, 4, 96]),
               gains.us(2).bc([128, 2, 4, 96]), ALU.mult)
            yield
            if do_q:
                tt("dve", qk_[:, 0], q3, rg_[:, 0], ALU.mult)
            else:
                kb.memset("pool", qk_[:, 0], 0.0)
            tt("dve", qk_[:, 1, :, 0:64], kv3[:, :, 0:64], rg_[:, 1, :, 0:64], ALU.mult)
            tt("pool", qk_[:, 1, :, 64:96], krope.us(1).bc([128, 4, 32]), rg_[:, 1, :, 64:96], ALU.mult)
            yield
            if not is_ctx:
                ti = i - 2
                v = qk_[:, :, :, 64:96]
                t1_ = rt_[:, 0]
                t2_ = rt_[:, 1]
                Cb = ropeC[:, ti, :].us(1).us(1).bc([128, 2, 4, 32])
                tt("pool", t1_, v, Cb, ALU.mult)
                for a in range(2):
                    for s_ in range(2):
                        o_ = t2_[:, :, :, a * 16 + s_ * 8:a * 16 + s_ * 8 + 8]
                        i_ = qk_[:, :, :, 64 + a * 16 + (1 - s_) * 8:64 + a * 16 + (1 - s_) * 8 + 8]
                        Sb = ropeS[:, ti, a * 16 + s_ * 8:a * 16 + s_ * 8 + 8].us(1).us(1).bc([128, 2, 4, 8])
                        tt("dve" if (a + s_) % 2 == 0 else "pool", o_, i_, Sb, ALU.mult)
                yield
                tt("dve", v, t1_, t2_, ALU.add)
            cp("dve", qkb_, qk_)
            yield

        def stage_c(i):
            is_ctx, do_q, j, pos = info(i)
            qkb_ = qkb[j]
            psT2 = kb.psum_any(bf16=True)
            for w_ in range(2):
                if w_ == 0 and not do_q:
                    continue
                for h in range(4):
                    tr(psT2[0:96, (w_ * 4 + h) * 128:(w_ * 4 + h + 1) * 128], qkb_[:, w_, h, :], identb)
            yield
            if do_q:
                cp("act", qT[0:96, :, pos].par("mlaqk"), psT2[0:96, 0:512].re("p (h n) -> p h n", h=4))
            cp("act", kT[0:96, :, pos].par("mlaqk"), psT2[0:96, 512:1024].re("p (h n) -> p h n", h=4))

        def tile_gen(i):
            yield from stage_a(i)
            yield from stage_b(i)
            yield from stage_c(i)
        for g0 in range(0, 18, NS):
            lockstep([tile_gen(i) for i in range(g0, g0 + NS)])
        if "mla_stop1" in dbg:
            return
        prefetch(l, PF_GATE0)
        P.barrier()
        AR.lo = lo_fixed
        omla = AR.alloc("omla", [18, 256], BF16)
        pT = [AR.alloc("pT%d" % i, [512], BF16) for i in range(3)]
        rc = [AR.alloc("rc%d" % i, [4]) for i in range(2)]
        jobs = []
        if with_ctx:
            jobs.append((0, TC, 0, 2))
        for qb in range(4):
            jobs.append((TC + qb * 512, TC + (qb + 1) * 512, 0, 18))
        pi = 0
        for (q0, q1, kt0, kt1) in jobs:
            nq = q1 - q0
            nqt = nq // 128
            for h in range(4):
                pso = [kb.psf[2 + t_] for t_ in range(nqt)]

                def s_mm(kt):
                    mm(kb.psf[kt % 2][:, 0:nq], kT[:, h, kt * 128:(kt + 1) * 128], qT[:, h, q0:q1], True, True)
                s_mm(kt0)
                for kt in range(kt0, kt1):
                    if kt + 1 < kt1:
                        s_mm(kt + 1)
                    pss = kb.psf[kt % 2]
                    p_ = pT[pi % 3]
                    pi += 1
                    act(p_[:, 0:nq], pss[:, 0:nq], AF.Exp)
                    for t_ in range(nqt):
                        mm(pso[t_][:, 0:68], p_[:, t_ * 128:(t_ + 1) * 128], Vt[:, kt, h, :], kt == kt0, kt == kt1 - 1)
                for t_ in range(nqt):
                    r_ = rc[t_ % 2]
                    kb.recip(r_[:, 0:1], pso[t_][:, 64:65])
                    ts("dve", omla[:, (q0 // 128) + t_, h * 64:(h + 1) * 64].par("omla"), pso[t_][:, 0:64], r_[:, 0:1], ALU.mult)
        for i in range(0 if with_ctx else 2, 18):
            psT = kb.psum_b()
            for c in range(2):
                tr(psT[:, c * 128:(c + 1) * 128], omla[:, i, c * 128:(c + 1) * 128], identb)
            for c in range(2):
                cp("act", brT[0][c][:, i * 128:(i + 1) * 128].par("brt0"), psT[:, c * 128:(c + 1) * 128])

    def phase_s5(l, b, with_ctx):
        AR.reset()
        AR2.lo = 0
        U = AR.alloc("U", [16, NCH])
        SN = [[AR.alloc("SN%d%d" % (d, ri), [8, NCH + 2]) for ri in range(2)] for d in range(2)]
        Ere = AR.alloc("Ere", [2, 8, 128])
        nEim = AR.alloc("nEim", [2, 8, 128])
        W3 = AR.alloc("W3", [16, 128])
        scr0 = AR.lo
        ws5 = wst[:, :, 0:256]
        Uc = AR.alloc("Uc", [16, 8, 16])
        kblocks = [(0, 32, 0), (32, 128, TC), (160, 128, TC + 1024)]
        for (k0, M, p0) in kblocks:
            pss = [kb.psum_f() for _ in range(4)]
            for r in range(8):
                ps = pss[r // 2]
                for k in range(KC):
                    mm(ps[0:M, (r % 2) * 256:(r % 2 + 1) * 256], hT[:, k, p0 + r:p0 + 8 * M:8], ws5[:, k, :], k == 0, k == KC - 1)
            for q in range(4):
                src = pss[q][0:M, :].re("p (r g i) -> p r g i", r=2, g=16)
                dst = Uc[0:M, :, 2 * q:2 * q + 2, :].re("p g r i -> p r g i").par("s5uc")
                cp("act" if q % 2 == 0 else "dve", dst, src)
            for gq in range(4):
                ps = kb.psum_f()
                for gg_ in range(4):
                    g = gq * 4 + gg_
                    tr(ps[:, gg_ * 128:gg_ * 128 + M], Uc[0:M, g, :, :].re("p r i -> p (r i)"), identf[0:M, 0:M])
                cp("act" if gq % 2 == 0 else "dve", U[:, gq * 4:gq * 4 + 4, k0:k0 + M].par("s5u"),
                   ps.re("p (g n) -> p g n", g=4)[:, :, 0:M])
        prefetch(l, PF_LRU)
        P.barrier()
        AR.lo = scr0
        if "s5_stop1" in dbg:
            return
        cached = (b > 0) and ("s5_nocache" not in dbg)
        if not cached:
            kb.memset("pool", W3, 0.0)
        else:
            dma("sp", Ere.re("p d g n -> p (d g n)"), st_main[0])
            dma("act", nEim.re("p d g n -> p (d g n)"), st_main[1])
            dma("sp", W3.re("p g n -> p (g n)"), st_main[2])
        prm = AR.alloc("prm", [40, 8])
        PW = [AR.alloc("pw%d" % i, [10, 8]) for i in range(2)]
        Bb = [AR.alloc("Bb%d" % i, [8, 16]) for i in range(2)]
        Braw = [AR.alloc("Braw%d" % i, [8, 16]) for i in range(2)]
        Craw = [AR.alloc("Craw%d" % i, [8, 16]) for i in range(2)]
        Dbc = AR.alloc("Dbc", [256])
        t8 = AR.alloc("t8", [8, 16])
        w3t = [AR.alloc("w3t%d" % i, [128]) for i in range(2)]
        tE = AR.alloc("tE", [8, 128])
        Fm = [AR.alloc("F%d" % i, [8, 8, 16]) for i in range(2)]
        Ep = [AR.alloc("Ep%d" % i, [8, 128]) for i in range(2)]
        Cnat = [AR.alloc("Cnat%d" % i, [16, 64], at=Ep[i].off, buf=Ep[i].buf) for i in range(2)]
        rs_sets = [[AR.alloc("rs%d_%d" % (q, i), [NCH], at=Fm[0].off + (q * 7 + i) * NCH) for i in range(6)]
                   for q in range(2)]
        rho_sets = [AR.alloc("rho1_%d" % q, [NCH], at=Fm[0].off + (q * 7 + 6) * NCH) for q in range(2)]
        assert Fm[0].off + 14 * NCH <= Ep[1].off + Ep[1].w
        W1 = [AR2.alloc("W1%d" % i, [16, 64]) for i in range(2)]
        tab = [AR2.alloc("tab%d" % i, [8, NCH]) for i in range(2)]
        dma("sp", Dbc, s5_d[l:l + 1, :].re("o n -> (o n)").pbc(128))
        dsel = cmask[:, 2, :]

        def pv(i):
            return prm[:, i, :]

        for d in range(2):
            if d == 1:
                P.barrier()
            if not cached:
                dma("sp", pv(0), s5_a_re[l, d].re("(gp h) p -> (h p) gp", h=2), slow=True)
                dma("act", pv(1), s5_a_im[l, d].re("(gp h) p -> (h p) gp", h=2), slow=True)
                ldt = t8[:, 0, :]
                dma("sp", ldt, s5_log_dt[l, d:d + 1, :].re("o g -> (o g)").pbc(128))
                dma("sp", Cnat[0][0:16], s5_c_re[l, d].re("g o p -> o g p"))
                dma("act", Cnat[1][0:16], s5_c_im[l, d].re("g o p -> o g p"))
                for h in range(2):
                    rows = slice(h * 64, (h + 1) * 64)
                    cp("dve", prm[rows, 2, :], ldt[rows, h:16:2])
                    dma("sp", Braw[0][rows].par("ldw"), s5_b_re[l, d].re("(gp h) p i -> h p gp i", h=2)[h])
                    dma("act", Braw[1][rows].par("ldw"), s5_b_im[l, d].re("(gp h) p i -> h p gp i", h=2)[h])
                for ri in range(2):
                    ps = kb.psum_f()
                    for g in range(16):
                        gp, h = g // 2, g % 2
                        mm(ps[h * 64:(h + 1) * 64, gp * 16:(gp + 1) * 16], Cnat[ri][0:16, g, :], identf[0:16, 0:16], True, True)
                    cp("act", Craw[ri], ps[:, 0:128].re("p (g o) -> p g o", g=8))
                if "s5_b1" in dbg:
                    return
                act(pv(2), pv(2), AF.Exp)
                tt("dve", pv(3), pv(0), pv(2), ALU.mult)
                tt("dve", pv(4), pv(1), pv(2), ALU.mult)
                act(pv(5), pv(3), AF.Exp)
                for (dst, shift) in ((6, 0.0), (7, math.pi / 2)):
                    xx, kk, ki = pv(30), pv(31), prm[:, 32, :]
                    ts("dve", xx, pv(4), shift, ALU.add)
                    ts("dve", kk, xx, 1.0 / (2 * math.pi), ALU.mult)
                    kint = TV(ki.ap.bitcast(I32), ki.buf)
                    cp("dve", kint, kk)
                    cp("dve", kk, kint)
                    stt(xx, kk, -6.28125, xx, ALU.mult, ALU.add)
                    stt(xx, kk, -(2 * math.pi - 6.28125), xx, ALU.mult, ALU.add)
                    ts("dve", xx, xx, math.pi, ALU.min, -math.pi, ALU.max)
                    act(pv(dst), xx, AF.Sin)
                kb.memset("dve", PW[0][:, 0, :], 1.0)
                kb.memset("dve", PW[1][:, 0, :], 0.0)
                tt("dve", PW[0][:, 1, :], pv(5), pv(7), ALU.mult)
                tt("dve", PW[1][:, 1, :], pv(5), pv(6), ALU.mult)
                for j in range(2, 9):
                    tt("dve", pv(30), PW[0][:, j - 1, :], PW[0][:, 1, :], ALU.mult)
                    tt("dve", pv(31), PW[1][:, j - 1, :], PW[1][:, 1, :], ALU.mult)
                    tt("dve", PW[0][:, j, :], pv(30), pv(31), ALU.subtract)
                    tt("dve", pv(30), PW[0][:, j - 1, :], PW[1][:, 1, :], ALU.mult)
                    tt("dve", pv(31), PW[1][:, j - 1, :], PW[0][:, 1, :], ALU.mult)
                    tt("dve", PW[1][:, j, :], pv(30), pv(31), ALU.add)
                act(pv(8), pv(3), AF.Exp, scale=-16.0)
                tt("dve", PW[0][:, 9, :], PW[0][:, 8, :], pv(8), ALU.mult)
                tt("dve", PW[1][:, 9, :], PW[1][:, 8, :], pv(8), ALU.mult)
                ts("dve", PW[1][:, 9, :], PW[1][:, 9, :], -1.0, ALU.mult)
                act(pv(9), pv(3), AF.Exp, scale=8.0)
                act(pv(10), pv(3), AF.Exp, scale=-8.0)
                tt("dve", pv(11), PW[0][:, 8, :], pv(10), ALU.mult)
                tt("dve", pv(12), PW[1][:, 8, :], pv(10), ALU.mult)
                ts("dve", pv(13), PW[0][:, 1, :], -1.0, ALU.add)
                tt("dve", pv(14), pv(0), pv(0), ALU.mult)
                tt("dve", pv(15), pv(1), pv(1), ALU.mult)
                tt("dve", pv(14), pv(14), pv(15), ALU.add)
                kb.recip(pv(14), pv(14))
                tt("dve", pv(15), pv(13), pv(0), ALU.mult)
                tt("dve", pv(16), PW[1][:, 1, :], pv(1), ALU.mult)
                tt("dve", pv(15), pv(15), pv(16), ALU.add)
                tt("dve", pv(15), pv(15), pv(14), ALU.mult)
                tt("dve", pv(16), PW[1][:, 1, :], pv(0), ALU.mult)
                tt("dve", pv(17), pv(13), pv(1), ALU.mult)
                tt("dve", pv(16), pv(16), pv(17), ALU.subtract)
                tt("dve", pv(16), pv(16), pv(14), ALU.mult)
                fre = pv(15).us(2).bc([128, 8, 16])
                fim = pv(16).us(2).bc([128, 8, 16])
                tt("dve", Bb[0], Braw[0], fre, ALU.mult)
                tt("dve", t8, Braw[1], fim, ALU.mult)
                tt("dve", Bb[0], Bb[0], t8, ALU.subtract)
                tt("dve", Bb[1], Braw[1], fre, ALU.mult)
                tt("dve", t8, Braw[0], fim, ALU.mult)
                tt("dve", Bb[1], Bb[1], t8, ALU.add)
                for t_ in range(8):
                    f_ = t_ + 1 if d == 0 else 8 - t_
                    pr = PW[0][:, f_, :].us(2).bc([128, 8, 16])
                    pi_ = PW[1][:, f_, :].us(2).bc([128, 8, 16])
                    eo = Ere[:, d, :, t_ * 16:(t_ + 1) * 16]
                    ei = nEim[:, d, :, t_ * 16:(t_ + 1) * 16]
                    tt("dve", eo, Craw[0], pr, ALU.mult)
                    tt("dve", t8, Craw[1], pi_, ALU.mult)
                    tt("dve", eo, eo, t8, ALU.subtract)
                    tt("dve", ei, Craw[0], pi_, ALU.mult)
                    tt("dve", t8, Craw[1], pr, ALU.mult)
                    tt("dve", ei, ei, t8, ALU.add)
                    ts("dve", ei, ei, -1.0, ALU.mult)
                for r in range(8):
                    e_ = 7 - r if d == 0 else r
                    pr = PW[0][:, e_, :].us(2).bc([128, 8, 16])
                    pi_ = PW[1][:, e_, :].us(2).bc([128, 8, 16])
                    tt("dve", Fm[0][:, :, r, :], Bb[0], pr, ALU.mult)
                    tt("dve", t8, Bb[1], pi_, ALU.mult)
                    tt("dve", Fm[0][:, :, r, :], Fm[0][:, :, r, :], t8, ALU.subtract)
                    tt("dve", Fm[1][:, :, r, :], Bb[1], pr, ALU.mult)
                    tt("dve", t8, Bb[0], pi_, ALU.mult)
                    tt("dve", Fm[1][:, :, r, :], Fm[1][:, :, r, :], t8, ALU.add)
                qr = PW[0][:, 9, :].us(2).bc([128, 8, 128])
                qi = PW[1][:, 9, :].us(2).bc([128, 8, 128])
                tt("dve", Ep[0], Ere[:, d], qr, ALU.mult)
                tt("dve", tE, nEim[:, d], qi, ALU.mult)
                tt("dve", Ep[0], Ep[0], tE, ALU.add)
                tt("dve", Ep[1], nEim[:, d], qr, ALU.mult)
                tt("dve", tE, Ere[:, d], qi, ALU.mult)
                tt("dve", Ep[1], Ep[1], tE, ALU.subtract)
                if "s5_b2" in dbg:
                    return
                for ri in range(2):
                    for gq in range(4):
                        ps = kb.psum_f()
                        for gg_ in range(4):
                            g = gq * 4 + gg_
                            gp, h = g // 2, g % 2
                            rows = slice(h * 64, (h + 1) * 64)
                            mm(ps[:, gg_ * 64:(gg_ + 1) * 64], Fm[ri][:, gp].re("p r i -> p (r i)"), identf[:, h * 64:(h + 1) * 64], True, True)
                        cp("act", W1[ri][:, gq * 4:gq * 4 + 4, :], ps[:, 0:256].re("p (g n) -> p g n", g=4))
                if "s5_b3" in dbg:
                    return
                for g in range(16):
                    gp, h = g // 2, g % 2
                    rows = slice(h * 64, (h + 1) * 64)
                    ps = kb.psf[h * 2 + (g // 2) % 2]
                    mm(ps[:, 0:128], Fm[0][rows, gp].re("p r i -> p (r i)"), Ep[0][rows, gp, :], True, False)
                    mm(ps[:, 0:128], Fm[1][rows, gp].re("p r i -> p (r i)"), Ep[1][rows, gp, :], False, True)
                    wt = w3t[g % 2]
                    tt("dve", wt, ps[:, 0:128], cmask[:, d, :], ALU.mult)
                    tt("pool", W3[:, g, :], W3[:, g, :], wt, ALU.add)
                if d == 0:
                    for g in range(16):
                        dcol = Dbc[:, g * 16:(g + 1) * 16].us(1).bc([128, 8, 16])
                        wt = w3t[g % 2]
                        tt("dve", wt.re("p (t o) -> p t o", t=8), dsel.re("p (t o) -> p t o", t=8), dcol, ALU.mult)
                        tt("pool", W3[:, g, :], W3[:, g, :], wt, ALU.add)
                if "s5_b4" in dbg:
                    return
                kb.memset("dve", tab[0][:, :, 0:1], 1.0)
                kb.memset("dve", tab[1][:, :, 0:1], 0.0)
                cp("dve", tab[0][:, :, 1:2], pv(11).us(2))
                cp("dve", tab[1][:, :, 1:2], pv(12).us(2))
                n = 2
                while n < NCH:
                    m = min(n, NCH - n)
                    tt("dve", pv(30), tab[0][:, :, n - 1], pv(11), ALU.mult)
                    tt("dve", pv(31), tab[1][:, :, n - 1], pv(12), ALU.mult)
                    tt("dve", pv(33), pv(30), pv(31), ALU.subtract)
                    tt("dve", pv(30), tab[0][:, :, n - 1], pv(12), ALU.mult)
                    tt("dve", pv(31), tab[1][:, :, n - 1], pv(11), ALU.mult)
                    tt("dve", pv(34), pv(30), pv(31), ALU.add)
                    Pr = pv(33).us(2).bc([128, 8, m])
                    Pi = pv(34).us(2).bc([128, 8, m])
                    ta = tE[:, :, 0:m]
                    tt("dve", tab[0][:, :, n:n + m], tab[0][:, :, 0:m], Pr, ALU.mult)
                    tt("dve", ta, tab[1][:, :, 0:m], Pi, ALU.mult)
                    tt("dve", tab[0][:, :, n:n + m], tab[0][:, :, n:n + m], ta, ALU.subtract)
                    tt("dve", tab[1][:, :, n:n + m], tab[0][:, :, 0:m], Pi, ALU.mult)
                    tt("dve", ta, tab[1][:, :, 0:m], Pr, ALU.mult)
                    tt("dve", tab[1][:, :, n:n + m], tab[1][:, :, n:n + m], ta, ALU.add)
                    n *= 2
                dma("sp", st_dir[d, :, 0:1024], W1[0].re("p g n -> p (g n)"))
                dma("sp", st_dir[d, :, 1024:2048], W1[1].re("p g n -> p (g n)"))
                dma("act", st_dir[d, :, 2048:2048 + 8 * NCH], tab[0].re("p g n -> p (g n)"))
                dma("act", st_dir[d, :, 2048 + 8 * NCH:2048 + 16 * NCH], tab[1].re("p g n -> p (g n)"))
                dma("sp", st_dir[d, :, 2048 + 16 * NCH:2048 + 16 * NCH + 8], pv(9))
            else:
                dma("sp", W1[0].re("p g n -> p (g n)"), st_dir[d, :, 0:1024])
                dma("sp", W1[1].re("p g n -> p (g n)"), st_dir[d, :, 1024:2048])
                dma("act", tab[0].re("p g n -> p (g n)"), st_dir[d, :, 2048:2048 + 8 * NCH])
                dma("act", tab[1].re("p g n -> p (g n)"), st_dir[d, :, 2048 + 8 * NCH:2048 + 16 * NCH])
                dma("sp", pv(9), st_dir[d, :, 2048 + 16 * NCH:2048 + 16 * NCH + 8])
            if "s5_stop2" in dbg:
                return
            P.barrier()
            def nat(tv, sl, rev):
                v = tv[:, sl]
                return v[:, ::-1] if rev else v

            def gp_gen(gp, d=d):
                rs, rho1 = rs_sets[gp % 2], rho_sets[gp % 2]
                psv = [kb.psum_f(), kb.psum_f()]
                for ri in range(2):
                    for h in range(2):
                        g = gp * 2 + h
                        mm(psv[ri][h * 64:(h + 1) * 64, 0:NCH], W1[ri][:, g, :], U[:, g, :], True, True)
                cp("pool", rho1, pv(9)[:, gp:gp + 1].bc([128, NCH]))
                yield
                Cn, Sn = tab[0][:, gp, :], tab[1][:, gp, :]
                if d == 0:
                    segs = [(slice(0, NCH), slice(0, NCH), False)]
                else:
                    segs = [(slice(0, 32), slice(0, 32), True), (slice(32, NCH), slice(32, NCH), True)]
                for (js, ks, rev) in segs:
                    vre, vim = nat(psv[0][:, 0:NCH], ks, rev), nat(psv[1][:, 0:NCH], ks, rev)
                    tt("dve", rs[0][:, js], vre, Cn[:, js], ALU.mult)
                    tt("dve", rs[1][:, js], vim, Sn[:, js], ALU.mult)
                    tt("dve", rs[4][:, js], vim, Cn[:, js], ALU.mult)
                    tt("dve", rs[5][:, js], vre, Sn[:, js], ALU.mult)
                yield
                tt("pool", rs[2], rs[0], rs[1], ALU.add)
                tt("pool", rs[3], rs[4], rs[5], ALU.subtract)
                yield
                kb.scan(rs[4], rho1, rs[2], 0.0)
                kb.scan(rs[5], rho1, rs[3], 0.0)
                yield
                tt("dve", rs[0], rs[4], Cn, ALU.mult)
                tt("dve", rs[1], rs[5], Sn, ALU.mult)
                tt("dve", rs[2], rs[4], Sn, ALU.mult)
                tt("dve", rs[3], rs[5], Cn, ALU.mult)
                yield
                for (js, ks, rev) in segs:
                    if d == 0:
                        osl = slice(1, NCH + 1)
                    else:
                        osl = slice(0, 32) if ks.start == 0 else slice(33, NCH + 1)
                    ore = nat(SN[d][0][:, gp, :], osl, rev).par("s5sn")
                    oim = nat(SN[d][1][:, gp, :], osl, rev).par("s5sn")
                    tt("pool", ore, rs[0][:, js], rs[1][:, js], ALU.subtract)
                    tt("pool", oim, rs[2][:, js], rs[3][:, js], ALU.add)
                yield
            for gp0 in range(0, 8, 2):
                lockstep([gp_gen(gp0), gp_gen(gp0 + 1)])
            for ri in range(2):
                if d == 0:
                    kb.memset("pool", SN[0][ri][:, :, 0:1], 0.0)
                else:
                    kb.memset("pool", SN[1][ri][:, :, 32:33], 0.0)
                    cp("pool", SN[1][ri][:, :, NCH + 1:NCH + 2], SN[1][ri][:, :, 0:1])
        if not cached:
            dma("sp", st_main[0], Ere.re("p d g n -> p (d g n)"))
            dma("act", st_main[1], nEim.re("p d g n -> p (d g n)"))
            dma("sp", st_main[2], W3.re("p g n -> p (g n)"))
        if "s5_stop3" in dbg:
            return
        P.barrier()
        AR.lo = scr0
        AR2.lo = 0
        Yc = AR.alloc("Yc", [8, 256])
        gg = AR.alloc("gg", [8, 256])
        g2 = AR.alloc("g2", [8, 256])
        sg = [AR.alloc("sg%d" % i, [512]) for i in range(2)]
        ggT = AR2.alloc("ggT", [2, NPOS])
        ggb = AR2.alloc("ggb", [2, NPOS], BF16)
        for (k0, M, p0) in kblocks:
            if k0 == 0 and not with_ctx:
                continue
            banks = [kb.psf[0], kb.psf[2], kb.psf[1], kb.psf[3]]
            for g in range(16):
                gp, h = g // 2, g % 2
                rows = slice(h * 64, (h + 1) * 64)
                bank = banks[h * 2 + (gp // 4)]
                o = bank[0:M, (gp % 4) * 128:(gp % 4 + 1) * 128]
                cf = slice(k0, k0 + M)
                cb_ = slice(k0 + 1, k0 + 1 + M) if k0 == 0 else slice(k0 + 2, k0 + 2 + M)
                mm(o, SN[0][0][rows, gp, cf], Ere[rows, 0, gp, :], True, False)
                mm(o, SN[0][1][rows, gp, cf], nEim[rows, 0, gp, :], False, False)
                mm(o, SN[1][0][rows, gp, cb_], Ere[rows, 1, gp, :], False, False)
                mm(o, SN[1][1][rows, gp, cb_], nEim[rows, 1, gp, :], False, False)
                mm(o, U[:, g, k0:k0 + M], W3[:, g, :], False, True)
            for h in range(2):
                for q in range(2):
                    bank = banks[h * 2 + q]
                    src = bank[0:M, :].re("p (g t o) -> p g t o", g=4, t=8)
                    dst = Yc[0:M].re("p t (g2 h o) -> p h g2 t o", h=2, o=16)[:, h, 4 * q:4 * q + 4].par("s5yc")
                    cp("act" if h == 0 else "dve", dst, src)
            if "s5_Y" in dbg:
                dma("sp", tap("s5Y", [NCH, 8 * 256])[k0:k0 + M, :], Yc[0:M].re("p t f -> p (t f)"))
            yv, gv, g2v = Yc[0:M], gg[0:M], g2[0:M]
            tt("pool", g2v, yv, yv, ALU.mult)
            ts("dve", g2v, g2v, 0.044715, ALU.mult, 1.0, ALU.add)
            tt("pool", g2v, g2v, yv, ALU.mult)
            act(g2v, g2v, AF.Sigmoid, scale=1.5957691216057308)
            tt("dve", gv, g2v, yv, ALU.mult)
            for t_ in range(8):
                ps = kb.psum_f()
                for c in range(2):
                    tr(ps[:, c * 128:c * 128 + M], gv[:, t_, c * 128:(c + 1) * 128], identf[0:M, 0:M])
                for c in range(2):
                    cp("act" if c == 0 else "dve", ggT[:, c, p0 + t_:p0 + 8 * M:8], ps[:, c * 128:c * 128 + M])
        lo = 0 if with_ctx else TC
        for c in range(2):
            cp("dve", ggb[:, c, lo:NPOS], ggT[:, c, lo:NPOS])
        for c in range(2):
            pos = lo
            while pos < NPOS:
                n = min(512, NPOS - pos)
                ps = kb.psum_f()
                for k in range(2):
                    mm(ps[:, 0:n], wglu[:, k, c * 128:(c + 1) * 128], ggb[:, k, pos:pos + n], k == 0, k == 1)
                s_ = sg[(pos // 512) % 2]
                act(s_[:, 0:n], ps[:, 0:n], AF.Sigmoid)
                tt("dve", brT[1][c][:, pos:pos + n], s_[:, 0:n], ggT[:, c, pos:pos + n], ALU.mult)
                pos += n

    def phase_merge(l, b, with_ctx, nxt=None):
        AR.reset(top=True)
        lo = 0 if with_ctx else TC
        blocks = []
        pos = lo
        while pos < NPOS:
            n = min(512, NPOS - pos) if pos >= TC else TC - pos
            blocks.append((pos, n))
            pos += n
        ypT = AR.alloc("ypT", [KC, NPOS], BF16, top=True)
        wbr = AR.alloc("wbr", [4, 2, D], BF16, top=True)
        wo = AR.alloc("wo", [KC, D], BF16, top=True)
        wml = [AR.alloc("wml%d" % i, [KC, 4, 128], BF16, top=True) for i in range(2)]

        def load_wml(oc):
            for n_ in range(4):
                c0 = O_MERGE + n_ * D + oc * 128
                dma("pool", wml[oc % 2][:, :, n_, :].par("ldw"), w_in[l, :, c0:c0 + 128].re("(c p) n -> p c n", p=128))
        wg = AR.alloc("w_gate", [KC, 1024], BF16)
        for cc in range(1, 8):
            dma("pool", wg[:, :, cc * 128:(cc + 1) * 128].par("ldw"),
                w_in[l, :, O_GATE + cc * 128:O_GATE + (cc + 1) * 128].re("(c p) n -> p c n", p=128))
        for n_ in range(4):
            dma("pool", wbr[:, n_].par("ldw"), w_branch[l, n_].re("(c p) n -> p c n", p=128))
        load_wml(0)
        load_wml(1)
        dma("pool", wo, w_out[l].re("(c p) n -> p c n", p=128))
        sl = [AR.alloc("sl%d" % i, [512], BF16) for i in range(2)]
        it = 0
        for cc in range(8):
            for (p0, n) in blocks:
                ps = kb.psum_f()
                for k in range(KC):
                    wsl = wst[:, k, 0:128] if cc == 0 else wg[:, k, cc * 128:(cc + 1) * 128]
                    mm(ps[:, 0:n], wsl, hT[:, k, p0:p0 + n], k == 0, k == KC - 1)
                s_ = sl[it % 2]
                it += 1
                act(s_[:, 0:n], ps[:, 0:n], AF.Silu)
                br = brT[cc // 2][cc % 2][:, p0:p0 + n]
                tt("dve", br, br, s_[:, 0:n], ALU.mult)
            if cc == 0 and nxt is not None:
                prefetch(nxt, PF_S5)
        AR.reset()
        sg = [AR.alloc("sg%d" % i, [512]) for i in range(2)]
        tm = [AR.alloc("tm%d" % i, [512]) for i in range(2)]
        acc = [AR.alloc("acc%d" % i, [512]) for i in range(2)]
        it = 0
        ib = 0
        for oc in range(8):
            wm = wml[oc % 2]
            if 1 <= oc and oc + 1 < 8:
                load_wml(oc + 1)
            for (p0, n) in blocks:
                ac = acc[ib % 2]
                ib += 1
                for n_ in range(4):
                    psA = kb.psum_f()
                    for k in range(2):
                        mm(psA[:, 0:n], wbr[:, n_, k, oc * 128:(oc + 1) * 128], brT[n_][k][:, p0:p0 + n], k == 0, k == 1)
                    psB = kb.psum_f()
                    for k in range(KC):
                        mm(psB[:, 0:n], wm[:, k, n_, :], hT[:, k, p0:p0 + n], k == 0, k == KC - 1)
                    s_ = sg[it % 2]
                    t_ = tm[it % 2]
                    it += 1
                    act(s_[:, 0:n], psB[:, 0:n], AF.Sigmoid)
                    if n_ == 0:
                        tt("dve", ac[:, 0:n], psA[:, 0:n], s_[:, 0:n], ALU.mult)
                    elif n_ < 3:
                        tt("dve", t_[:, 0:n], psA[:, 0:n], s_[:, 0:n], ALU.mult)
                        tt("dve", ac[:, 0:n], ac[:, 0:n], t_[:, 0:n], ALU.add)
                    else:
                        tt("dve", t_[:, 0:n], psA[:, 0:n], s_[:, 0:n], ALU.mult)
                        tt("dve", ypT[:, oc, p0:p0 + n].par("ypt"), ac[:, 0:n], t_[:, 0:n], ALU.add)
        AR.reset()
        gbc = AR.alloc("gbc", [2, D])
        dma("sp", gbc[:, 0, :], modD[l, b, 2 * D:3 * D].pbc(128))
        if with_ctx:
            dma("sp", gbc[:, 1, :], modD[l, 2, 2 * D:3 * D].pbc(128))
        xt = [AR.alloc("xt%d" % i, [D]) for i in range(2)]
        yt = [AR.alloc("yt%d" % i, [D]) for i in range(2)]
        for i in range(0 if with_ctx else 2, 18):
            x_t, y_t = xt[i % 2], yt[i % 2]
            dma("sp", x_t, src_tile(l, b, i))
            for hf in range(2):
                ps = kb.psum_f()
                for k in range(KC):
                    mm(ps, ypT[:, k, i * 128:(i + 1) * 128], wo[:, k, hf * 512:(hf + 1) * 512], k == 0, k == KC - 1)
                tt("dve", y_t[:, hf * 512:(hf + 1) * 512], ps, gbc[:, 1 if i < 2 else 0, hf * 512:(hf + 1) * 512], ALU.mult)
            tt("pool", y_t, y_t, x_t, ALU.add)
            if i < 2:
                dst = (tap("xc1", [nb, TC, D]) if "x1out" in dbg else xcmid)[b, i * 128:(i + 1) * 128, :]
            else:
                dst = (xmid if (l < DEPTH - 1 and "x1out" not in dbg) else y_out)[b, (i - 2) * 128:(i - 1) * 128, :]
            dma("act", dst, y_t)

    def dump_br(name, n, lo):
        o = tap(name, [2, 128, NPOS])
        tmpf = AR.alloc("dump_" + name, [NPOS])
        for c in range(2):
            cp("dve", tmpf[:, lo:NPOS], brT[n][c][:, lo:NPOS])
            dma("sp", o[c, :, lo:NPOS], tmpf[:, lo:NPOS])

    prefetch(layers[0], PF_S5)
    for l in layers:
        with_ctx = l < DEPTH - 1
        prep_layer(l)
        for b in range(nb):
            if "prep_only" in dbg:
                continue
            phase_norm(l, b)
            if "hT" in dbg and b == 0 and l == layers[0]:
                o = tap("hT", [KC, 128, NPOS])
                AR.reset()
                tmpf = AR.alloc("dump_hT", [NPOS])
                for c in range(KC):
                    cp("dve", tmpf, hT[:, c, :])
                    dma("sp", o[c], tmpf)
            only = dbg & {"only_lru", "only_pool", "only_mla", "only_s5", "only_norm"}
            if not only or "only_s5" in only:
                phase_s5(l, b, with_ctx)
            if not only or "only_lru" in only:
                phase_lru(l, b, with_ctx)
            if not only or "only_pool" in only:
                phase_pool(l, b, with_ctx)
            if not only or "only_mla" in only:
                phase_mla(l, b, with_ctx)
            if "br" in dbg and b == 0 and l == layers[0]:
                AR.reset()
                lo = 0 if with_ctx else TC
                for n_, nm in enumerate(("mla", "s5", "lru", "pool")):
                    dump_br(nm, n_, lo)
            if "nomerge" not in dbg:
                if b + 1 < nb:
                    nxt = l
                else:
                    li = list(layers).index(l)
                    nxt = layers[li + 1] if li + 1 < len(layers) else None
                phase_merge(l, b, with_ctx, nxt)
    P.barrier()
    P.emit()
    kb.st.close()
    return nc, kb


def _consts():
    ident = np.eye(128, dtype=np.float32)
    rows_n = T // 64
    row = np.repeat(np.arange(rows_n), 64).astype(np.float32)
    col = np.tile(np.arange(64), rows_n).astype(np.float32)
    nf = 8
    inv = (np.float32(10000.0) ** (-np.arange(nf, dtype=np.float32) / nf)).astype(np.float32)
    ar = (row[:, None] * inv).astype(np.float32)
    ac = (col[:, None] * inv).astype(np.float32)
    cr, sr, cc, sc = np.cos(ar), np.sin(ar), np.cos(ac), np.sin(ac)
    ropeC = np.concatenate([cr, cr, cc, cc], axis=1).astype(np.float32)
    ropeS = np.concatenate([-sr, sr, -sc, sc], axis=1).astype(np.float32)
    cpool = np.ones((128, 34), np.float32)
    for c in range(2):
        for p in range(128):
            w = (2, 4, 8, 16)[2 * c + p // 64]
            hw = w // 2
            cpool[p, c] = 1.0 / w
            for t in range(8):
                cnt = t + hw if t < hw else w
                cpool[p, 2 + c * 8 + t] = w / cnt
            for j in range(8):
                dist = 8 - j
                cnt = dist + hw if dist < hw else w
                cpool[p, 18 + c * 8 + j] = w / cnt
    m = np.zeros((128, 3, 128), np.float32)
    for r in range(8):
        for t in range(8):
            if r <= t:
                m[r * 16:(r + 1) * 16, 0, t * 16:(t + 1) * 16] = 1.0
            if r >= t:
                m[r * 16:(r + 1) * 16, 1, t * 16:(t + 1) * 16] = 1.0
            if r == t:
                m[r * 16:(r + 1) * 16, 2, t * 16:(t + 1) * 16] = np.eye(16, dtype=np.float32)
    return {"c_ident": ident, "c_ropeC": ropeC, "c_ropeS": ropeS, "c_pool": cpool, "c_mask": m}


_WNAMES = ["w_ada", "b_ada", "norm_g", "w_in", "mla_q_norm", "mla_kv_norm", "mla_w_uq", "mla_w_ukv", "mla_q_gain",
           "mla_k_gain", "s5_a_re", "s5_a_im", "s5_log_dt", "s5_b_re", "s5_b_im", "s5_c_re", "s5_c_im", "s5_d",
           "s5_w_glu", "lru_conv_w", "lru_conv_b", "lru_lambda", "lru_w_a", "lru_b_a", "lru_w_x", "lru_b_x",
           "pool_w", "pool_b", "pool_scale", "w_branch", "w_out"]


def make_in_maps(inputs, n_cores, nb):
    consts = _consts()
    maps = []
    for r in range(n_cores):
        bs = slice(r * nb, (r + 1) * nb)
        m = {"x": np.ascontiguousarray(inputs["x"][bs], dtype=np.float32),
             "ctx": np.ascontiguousarray(inputs["ctx"][bs], dtype=np.float32)}
        cv = np.zeros((3, D), np.float32)
        cv[0:nb] = np.asarray(inputs["c"], dtype=np.float32)[bs]
        cv[2] = np.asarray(inputs["c_ctx"], dtype=np.float32)
        m["cvec"] = cv
        for k in _WNAMES:
            m[k] = np.ascontiguousarray(inputs[k], dtype=np.float32)
        m.update(consts)
        maps.append(m)
    return maps


_CACHE = {}


def kernel(**inputs):
    n_cores, nb = 8, 2
    if "nc" not in _CACHE:
        _CACHE["nc"] = build_program(nb=nb)[0]
    nc = _CACHE["nc"]
    maps = make_in_maps(inputs, n_cores, nb)
    res = run_bass_kernel_spmd(nc, maps, core_ids=list(range(n_cores)))
    out = np.concatenate([np.asarray(r["y"], dtype=np.float32) for r in res.results], axis=0)
    return out
```

```python
import math
import contextlib
import numpy as np
import concourse.bass as bass
import concourse.mybir as mybir
from concourse.bass_utils import run_bass_kernel_spmd

F32 = mybir.dt.float32
BF16 = mybir.dt.bfloat16
I32 = mybir.dt.int32
AF = mybir.ActivationFunctionType
ALU = mybir.AluOpType
AX = mybir.AxisListType

ENGS = ("pe", "act", "dve", "pool", "sp")
NDMA = 24

D = 1024
KC = 8
T = 2048
TC = 256
NPOS = T + TC
DEPTH = 2
O_KROPE, O_S5, O_LRU, O_CQ, O_POOL, O_GATE, O_MERGE, IN_W = 128, 160, 416, 672, 928, 1184, 2208, 6304
EPS = 1e-6
NCH = NPOS // 8


PAR_OFF = set()


class Buf:
    __slots__ = ("name", "w", "wp", "r")

    def __init__(self, name):
        self.name = name
        self.w = []
        self.wp = []
        self.r = []


class TV:
    par_ = False

    def __init__(self, ap, buf):
        self.ap, self.buf = ap, buf

    def par(self, tag=""):
        t = TV(self.ap, self.buf)
        t.par_ = tag not in PAR_OFF
        return t

    def __getitem__(self, k):
        return TV(self.ap[k], self.buf)

    def re(self, s, **kw):
        return TV(self.ap.rearrange(s, **kw), self.buf)

    def bc(self, shape):
        return TV(self.ap.to_broadcast(list(shape)), self.buf)

    def us(self, ax):
        return TV(self.ap.unsqueeze(ax), self.buf)

    def pbc(self, n):
        return TV(self.ap.partition_broadcast(n), self.buf)

    @property
    def shape(self):
        return tuple(self.ap.shape)


def _bufs(*xs):
    out = []
    for x in xs:
        if isinstance(x, TV):
            out.append(x.buf)
        elif isinstance(x, (list, tuple)):
            out.extend(_bufs(*x))
    return out


def _ap(x):
    return x.ap if isinstance(x, TV) else x


class Prog:
    def __init__(self, nc):
        self.nc = nc
        self.ops = {e: [] for e in ENGS}
        self.cnt = {e: 0 for e in ENGS}
        self.waited = {e: {} for e in ENGS}
        self.dma_i = 0
        self.dma_last = [0] * NDMA

    def _deps(self, eng, reads, writes, par=False):
        toks = []
        for b in reads:
            toks.extend(b.w)
            toks.extend(b.wp)
        for b in writes:
            toks.extend(b.w)
            if not par:
                toks.extend(b.wp)
            toks.extend(b.r)
        need = {}
        for (k, v, e) in toks:
            if e == "pe" and eng == "pe":
                continue
            if self.waited[eng].get(k, 0) >= v:
                continue
            if need.get(k, 0) < v:
                need[k] = v
        for k, v in need.items():
            self.waited[eng][k] = v
        return list(need.items())

    def _mark(self, tok, reads, writes, par=False):
        for b in writes:
            if par:
                best = {}
                for (k, v, e) in b.wp + [tok]:
                    if k not in best or best[k][1] < v:
                        best[k] = (k, v, e)
                b.wp = list(best.values())
            else:
                b.w = [tok]
                b.wp = []
                b.r = []
        for b in reads:
            if b in writes:
                continue
            b.r.append(tok)
            if len(b.r) > 16:
                best = {}
                for (k, v, e) in b.r:
                    if k not in best or best[k][1] < v:
                        best[k] = (k, v, e)
                b.r = list(best.values())

    def op(self, eng, fn, reads=(), writes=(), par=False):
        reads = list(dict.fromkeys(reads))
        writes = list(dict.fromkeys(writes))
        waits = self._deps(eng, reads, writes, par)
        self.cnt[eng] += 1
        tok = (eng, self.cnt[eng], eng)
        self.ops[eng].append((waits, fn, (eng, 1)))
        self._mark(tok, reads, writes, par)

    def dma(self, eng, fn, reads=(), writes=(), par=False):
        reads = list(dict.fromkeys(reads))
        writes = list(dict.fromkeys(writes))
        waits = self._deps(eng, reads, writes, par)
        slot = self.dma_i % NDMA
        self.dma_i += 1
        key = ("dma", slot)
        prev = self.dma_last[slot]
        if prev and self.waited[eng].get(key, 0) < prev:
            waits.append((key, prev))
            self.waited[eng][key] = prev
        val = prev + 16
        self.dma_last[slot] = val
        tok = (key, val, "dma")
        self.ops[eng].append((waits, fn, (key, 16)))
        self._mark(tok, reads, writes, par)

    def barrier(self):
        for E in ENGS:
            waits = []
            for e in ENGS:
                v = self.cnt[e]
                if v and self.waited[E].get(e, 0) < v:
                    waits.append((e, v))
                    self.waited[E][e] = v
            for slot in range(NDMA):
                v = self.dma_last[slot]
                key = ("dma", slot)
                if v and self.waited[E].get(key, 0) < v:
                    waits.append((key, v))
                    self.waited[E][key] = v
            if waits:
                self.ops[E].append((waits, None, None))

    def emit(self):
        nc = self.nc
        sems = {}
        with contextlib.ExitStack() as st:
            for e in ENGS:
                sems[e] = st.enter_context(nc.semaphore("s_" + e))
            for i in range(NDMA):
                sems[("dma", i)] = st.enter_context(nc.semaphore("s_dma%d" % i))
            block = st.enter_context(nc.Block())

            def run(engname):
                def body(eng):
                    for waits, fn, inc in self.ops[engname]:
                        for k, v in waits:
                            eng.wait_ge(sems[k], v)
                        if fn is None:
                            continue
                        ins = fn(eng)
                        ins.then_inc(sems[inc[0]], inc[1])
                return body
            block.tensor(run("pe"))
            block.scalar(run("act"))
            block.vector(run("dve"))
            block.gpsimd(run("pool"))
            block.sync(run("sp"))


class KB:
    def __init__(self, nc, nb, layers, dbg):
        self.nc = nc
        self.P = Prog(nc)
        self.nb = nb
        self.layers = layers
        self.dbg = dbg
        self.st = contextlib.ExitStack()
        self.din = {}
        self.dout = {}
        self.ps_i = 0
        self.psb_i = 0
        self.ps8_i = 0

    def tt(self, eng, out, a, b, op):
        self.P.op(eng, lambda e: e.tensor_tensor(out=out.ap, in0=a.ap, in1=b.ap, op=op), _bufs(a, b), _bufs(out), par=out.par_)

    def ts(self, eng, out, a, s1, op0, s2=None, op1=None):
        if op1 is None:
            self.P.op(eng, lambda e: e.tensor_scalar(out=out.ap, in0=a.ap, scalar1=_ap(s1), scalar2=None, op0=op0),
                      _bufs(a, s1), _bufs(out), par=out.par_)
        else:
            self.P.op(eng, lambda e: e.tensor_scalar(out=out.ap, in0=a.ap, scalar1=_ap(s1), scalar2=_ap(s2), op0=op0, op1=op1),
                      _bufs(a, s1, s2), _bufs(out), par=out.par_)

    def stt(self, out, a, s, b, op0, op1):
        self.P.op("dve", lambda e: e.scalar_tensor_tensor(out=out.ap, in0=a.ap, scalar=_ap(s), in1=b.ap, op0=op0, op1=op1),
                  _bufs(a, s, b), _bufs(out))

    def act(self, out, a, func, bias=0.0, scale=1.0, accum=None):
        if accum is None:
            self.P.op("act", lambda e: e.activation(out=out.ap, in_=a.ap, func=func, bias=_ap(bias), scale=_ap(scale)),
                      _bufs(a, bias, scale), _bufs(out), par=out.par_)
        else:
            self.P.op("act", lambda e: e.activation(out=out.ap, in_=a.ap, func=func, bias=_ap(bias), scale=_ap(scale), accum_out=accum.ap),
                      _bufs(a, bias, scale), _bufs(out, accum))

    def cp(self, eng, out, a):
        if eng == "act":
            self.P.op("act", lambda e: e.copy(out=out.ap, in_=a.ap), _bufs(a), _bufs(out), par=out.par_)
        else:
            self.P.op(eng, lambda e: e.tensor_copy(out=out.ap, in_=a.ap), _bufs(a), _bufs(out), par=out.par_)

    def memset(self, eng, out, v):
        self.P.op(eng, lambda e: e.memset(out.ap, v), [], _bufs(out))

    def recip(self, out, a):
        self.P.op("dve", lambda e: e.reciprocal(out=out.ap, in_=a.ap), _bufs(a), _bufs(out))

    def mm(self, out, lhsT, rhs, start, stop):
        self.P.op("pe", lambda e: e.matmul(out.ap, lhsT=lhsT.ap, rhs=rhs.ap, start=start, stop=stop), _bufs(lhsT, rhs), _bufs(out))

    def tr(self, out, a, ident):
        self.P.op("pe", lambda e: e.transpose(out.ap, a.ap, ident.ap), _bufs(a, ident), _bufs(out))

    def scan(self, out, d0, d1, init):
        self.P.op("dve", lambda e: e.tensor_tensor_scan(out=out.ap, data0=d0.ap, data1=d1.ap, initial=_ap(init), op0=ALU.mult, op1=ALU.add),
                  _bufs(d0, d1, init), _bufs(out))

    def reduce(self, out, a, op=ALU.add):
        self.P.op("dve", lambda e: e.tensor_reduce(out=out.ap, in_=a.ap, axis=AX.X, op=op), _bufs(a), _bufs(out))

    def dma(self, q, out, a, slow=False):
        if slow:
            self.P.dma(q, lambda e: e.dma_start(out=out.ap, in_=a.ap, allow_slow_non_contiguous=True), _bufs(a), _bufs(out), par=out.par_)
        else:
            self.P.dma(q, lambda e: e.dma_start(out=out.ap, in_=a.ap), _bufs(a), _bufs(out), par=out.par_)

    def dram_in(self, name, shape, dt=F32):
        t = TV(self.nc.dram_tensor(name, list(shape), dt, kind="ExternalInput").ap(), Buf(name))
        self.din[name] = t
        return t

    def dram_out(self, name, shape, dt=F32):
        t = TV(self.nc.dram_tensor(name, list(shape), dt, kind="ExternalOutput").ap(), Buf(name))
        self.dout[name] = t
        return t

    def dram_tmp(self, name, shape, dt=F32):
        return TV(self.nc.dram_tensor(name, list(shape), dt, kind="Internal").ap(), Buf(name))

    def sb(self, name, shape, dt=F32):
        t = self.st.enter_context(self.nc.sbuf_tensor(name, list(shape), dt))
        return TV(t[:], Buf(name))

    def psum_f(self):
        i = self.ps_i % 6
        self.ps_i += 1
        return self.psf[i]

    def psum_b(self):
        i = self.psb_i % 2
        self.psb_i += 1
        return self.psb[i]

    def psum_any(self, bf16=False):
        i = self.ps8_i % 8
        self.ps8_i += 1
        return self.ps8b[i] if bf16 else self.ps8[i]


class Arena:
    def __init__(self, kb, words, ap=None):
        self.kb = kb
        self.words = words
        if ap is None:
            self.f = kb.st.enter_context(kb.nc.sbuf_tensor("arena_f", [128, words], F32))
        else:
            self.f = ap
        self.lo = 0
        self.hi = words

    def alloc(self, name, free, dt=F32, top=False, buf=None, at=None):
        nel = int(np.prod(free))
        w = nel if dt == F32 else (nel + 1) // 2
        w = ((w + 15) // 16) * 16
        if at is not None:
            off = at
        elif top:
            self.hi -= w
            off = self.hi
        else:
            off = self.lo
            self.lo += w
        assert self.lo <= self.hi, (name, self.lo, self.hi)
        assert off + w <= self.words
        if dt != F32:
            ap = self.f[:, off:off + w].bitcast(dt)[:, 0:nel]
        else:
            ap = self.f[:, off:off + nel]
        if len(free) > 1:
            names = " ".join("d%d" % i for i in range(len(free)))
            kw = {"d%d" % i: int(free[i]) for i in range(len(free))}
            ap = ap.rearrange("p (%s) -> p %s" % (names, names), **kw)
        tv = TV(ap, buf if buf is not None else Buf(name))
        tv.off = off
        tv.w = w
        return tv

    def reset(self, top=False):
        self.kb.P.barrier()
        self.lo = 0
        if top:
            self.hi = self.words


def lockstep(gens):
    gens = list(gens)
    while gens:
        nxt = []
        for g in gens:
            try:
                next(g)
                nxt.append(g)
            except StopIteration:
                pass
        gens = nxt


def build_program(nb=2, layers=(0, 1), dbg=None):
    nc = bass.Bass("TRN2", target_bir_lowering=False)
    kb = KB(nc, nb, layers, dbg)
    P = kb.P
    tt, ts, stt, act, cp, mm, tr, dma = kb.tt, kb.ts, kb.stt, kb.act, kb.cp, kb.mm, kb.tr, kb.dma
    dbg = dbg or set()
    PAR_OFF.clear()
    PAR_OFF.update(x[6:] for x in dbg if x.startswith("nopar_"))

    x_in = kb.dram_in("x", [nb, T, D])
    ctx_in = kb.dram_in("ctx", [nb, TC, D])
    cvec = kb.dram_in("cvec", [3, D])
    w_ada = kb.dram_in("w_ada", [DEPTH, D, 3 * D])
    b_ada = kb.dram_in("b_ada", [DEPTH, 3 * D])
    norm_g = kb.dram_in("norm_g", [DEPTH, D])
    w_in = kb.dram_in("w_in", [DEPTH, D, IN_W])
    mla_q_norm = kb.dram_in("mla_q_norm", [DEPTH, 256])
    mla_kv_norm = kb.dram_in("mla_kv_norm", [DEPTH, 128])
    mla_w_uq = kb.dram_in("mla_w_uq", [DEPTH, 256, 384])
    mla_w_ukv = kb.dram_in("mla_w_ukv", [DEPTH, 128, 512])
    mla_q_gain = kb.dram_in("mla_q_gain", [DEPTH, 96])
    mla_k_gain = kb.dram_in("mla_k_gain", [DEPTH, 96])
    s5_a_re = kb.dram_in("s5_a_re", [DEPTH, 2, 16, 64])
    s5_a_im = kb.dram_in("s5_a_im", [DEPTH, 2, 16, 64])
    s5_log_dt = kb.dram_in("s5_log_dt", [DEPTH, 2, 16])
    s5_b_re = kb.dram_in("s5_b_re", [DEPTH, 2, 16, 64, 16])
    s5_b_im = kb.dram_in("s5_b_im", [DEPTH, 2, 16, 64, 16])
    s5_c_re = kb.dram_in("s5_c_re", [DEPTH, 2, 16, 16, 64])
    s5_c_im = kb.dram_in("s5_c_im", [DEPTH, 2, 16, 16, 64])
    s5_d = kb.dram_in("s5_d", [DEPTH, 256])
    s5_w_glu = kb.dram_in("s5_w_glu", [DEPTH, 256, 256])
    lru_conv_w = kb.dram_in("lru_conv_w", [DEPTH, 4, 256])
    lru_conv_b = kb.dram_in("lru_conv_b", [DEPTH, 256])
    lru_lambda = kb.dram_in("lru_lambda", [DEPTH, 2, 256])
    lru_w_a = kb.dram_in("lru_w_a", [DEPTH, 2, 4, 64, 64])
    lru_b_a = kb.dram_in("lru_b_a", [DEPTH, 2, 256])
    lru_w_x = kb.dram_in("lru_w_x", [DEPTH, 2, 4, 64, 64])
    lru_b_x = kb.dram_in("lru_b_x", [DEPTH, 2, 256])
    pool_w = kb.dram_in("pool_w", [DEPTH, 4, 64, 64])
    pool_b = kb.dram_in("pool_b", [DEPTH, 256])
    pool_scale = kb.dram_in("pool_scale", [DEPTH, 256])
    w_branch = kb.dram_in("w_branch", [DEPTH, 4, 256, D])
    w_out = kb.dram_in("w_out", [DEPTH, D, D])
    c_ident = kb.dram_in("c_ident", [128, 128])
    c_ropeC = kb.dram_in("c_ropeC", [T, 32])
    c_ropeS = kb.dram_in("c_ropeS", [T, 32])
    c_pool = kb.dram_in("c_pool", [128, 2 + 16 + 16])
    c_mask = kb.dram_in("c_mask", [128, 3, 128])
    y_out = kb.dram_out("y", [nb, T, D])
    xmid = kb.dram_tmp("xmid", [nb, T, D])
    xcmid = kb.dram_tmp("xcmid", [nb, TC, D])
    modD = kb.dram_tmp("modD", [DEPTH, 3, 3 * D])
    st_main = kb.dram_tmp("st_main", [3, 128, 2048])
    st_dir = kb.dram_tmp("st_dir", [2, 128, 2048 + 16 * NCH + 8])
    dbg_out = {}

    def tap(name, shape):
        if name not in dbg_out:
            dbg_out[name] = kb.dram_out("dbg_" + name, shape)
        return dbg_out[name]

    kb.ps8 = []
    for i in range(8):
        t = kb.st.enter_context(nc.psum_tensor("psf%d" % i, [128, 512], F32))
        kb.ps8.append(TV(t[:], Buf("psf%d" % i)))
    kb.psf = kb.ps8[0:6]
    kb.psb = [TV(kb.ps8[i].ap.bitcast(BF16), kb.ps8[i].buf) for i in (6, 7)]
    kb.ps8b = [TV(kb.ps8[i].ap.bitcast(BF16), kb.ps8[i].buf) for i in range(8)]

    hT = kb.sb("hT", [128, KC, NPOS], BF16)
    bigbr = kb.sb("bigbr", [128, 8 * NPOS], BF16)
    _slot = {0: 0, 2: 2, 3: 4, 1: 6}
    brT = [[TV(bigbr.ap[:, (_slot[n] + c) * NPOS:(_slot[n] + c + 1) * NPOS], Buf("brT%d%d" % (n, c))) for c in range(2)]
           for n in range(4)]
    identf = kb.sb("identf", [128, 128])
    identb = kb.sb("identb", [128, 128], BF16)
    ropeC = kb.sb("ropeC", [128, 16, 32])
    ropeS = kb.sb("ropeS", [128, 16, 32])
    cpool = kb.sb("cpool", [128, 34])
    cmask = kb.sb("cmask", [128, 3, 128])
    modA = kb.sb("modA", [128, 3, KC])
    modS = kb.sb("modS", [128, 3, KC])
    lp = kb.sb("lp", [128, 64])
    lruBD = kb.sb("lruBD", [128, 2, 2, 2, 128])
    poolBD = kb.sb("poolBD", [128, 2, 128])
    wukv = kb.sb("wukv", [128, 512], BF16)
    wuq = kb.sb("wuq", [128, 2, 384], BF16)
    gains = kb.sb("gains", [128, 2, 96])
    wglu = kb.sb("wglu", [128, 2, 256], BF16)
    AR = Arena(kb, 28000)
    wst = kb.sb("wst", [128, KC, 416], BF16)

    def prefetch(l, parts):
        for (d0, c0, n) in parts:
            dma("pool", wst[:, :, d0:d0 + n].par("wst"), w_in[l, :, c0:c0 + n].re("(c p) n -> p c n", p=128))
    PF_S5 = [(0, O_S5, 256)]
    PF_LRU = [(0, O_LRU, 256)]
    PF_POOL = [(0, O_POOL, 256)]
    PF_MLA = [(0, 0, 160), (160, O_CQ, 256)]
    PF_GATE0 = [(0, O_GATE, 128)]
    AR2 = Arena(kb, 3 * NPOS, ap=bigbr.ap[:, 0:6 * NPOS].bitcast(F32))

    dma("sp", identf, c_ident)
    cp("dve", identb, identf)
    dma("sp", ropeC, c_ropeC.re("(i p) f -> p i f", p=128))
    dma("sp", ropeS, c_ropeS.re("(i p) f -> p i f", p=128))
    dma("sp", cpool, c_pool)
    dma("sp", cmask, c_mask)

    LP_CW = 0
    LP_CB = 8
    LP_NSP = 10
    LP_NSP2 = 14
    LP_BA = 18
    LP_BX = 22
    LP_PSC = 26
    LP_PBS = 28
    LP_G = 30
    LP_TMP = 40

    def prep_layer(l):
        AR.reset(top=True)
        cT = AR.alloc("cT", [KC, 3])
        for v in range(3):
            dma("sp", cT[:, :, v], cvec[v].re("(c p) -> p c", p=128), slow=True)
        cact = AR.alloc("cact", [KC, 3])
        act(cact, cT, AF.Silu)
        brow = AR.alloc("brow", [3 * D])
        dma("sp", brow[0:3, :], b_ada[l:l + 1, :].re("o n -> (o n)").pbc(3))
        modrow = AR.alloc("modrow", [3 * D])
        wa = [AR.alloc("wa%d" % i, [KC, 512]) for i in range(4)]
        for cb in range(4):
            dma("sp" if cb % 2 == 0 else "act", wa[cb], w_ada[l, :, cb * 512:(cb + 1) * 512].re("(c p) n -> p c n", p=128))
        for cb in range(6):
            w = wa[cb % 4]
            if cb >= 4:
                dma("sp" if cb % 2 == 0 else "act", w, w_ada[l, :, cb * 512:(cb + 1) * 512].re("(c p) n -> p c n", p=128))
            ps = kb.psum_f()
            for k in range(KC):
                mm(ps[0:3, :], cact[:, k, :], w[:, k, :], k == 0, k == KC - 1)
            tt("dve", modrow[0:3, cb * 512:(cb + 1) * 512], ps[0:3, :], brow[0:3, cb * 512:(cb + 1) * 512], ALU.add)
        dma("sp", modD[l], modrow[0:3, :])
        sc = AR.alloc("sc", [3, KC])
        for v in range(3):
            dma("sp", modS[:, v, :], modD[l, v, 0:D].re("(c p) -> p c", p=128), slow=True)
            dma("sp", sc[:, v, :], modD[l, v, D:2 * D].re("(c p) -> p c", p=128), slow=True)
        dma("sp", lp[:, LP_G:LP_G + 8], norm_g[l].re("(c p) -> p c", p=128), slow=True)
        for v in range(3):
            stt(modA[:, v, :], sc[:, v, :], 1.0, lp[:, LP_G:LP_G + 8], ALU.add, ALU.mult)
        for k in range(4):
            dma("sp", lp[:, LP_CW:LP_CW + 8].re("p (c k) -> p c k", c=2)[:, :, k], lru_conv_w[l, k].re("(c p) -> p c", p=128), slow=True)
        dma("sp", lp[:, LP_CB:LP_CB + 2], lru_conv_b[l].re("(c p) -> p c", p=128), slow=True)
        lam = lp[:, LP_TMP:LP_TMP + 4]
        for d in range(2):
            dma("sp", lam[:, d * 2:d * 2 + 2], lru_lambda[l, d].re("(c p) -> p c", p=128), slow=True)
            dma("sp", lp[:, LP_BA + d * 2:LP_BA + d * 2 + 2], lru_b_a[l, d].re("(c p) -> p c", p=128), slow=True)
            dma("sp", lp[:, LP_BX + d * 2:LP_BX + d * 2 + 2], lru_b_x[l, d].re("(c p) -> p c", p=128), slow=True)
        t0 = lp[:, LP_TMP + 4:LP_TMP + 8]
        t1 = lp[:, LP_TMP + 8:LP_TMP + 12]
        t2 = lp[:, LP_TMP + 12:LP_TMP + 16]
        t3 = lp[:, LP_TMP + 16:LP_TMP + 20]
        ts("dve", t0, lam, -1.0, ALU.mult)
        tt("dve", t0, t0, lam, ALU.max)
        act(t1, t0, AF.Exp, scale=-1.0)
        ts("dve", t2, t1, 2.0, ALU.add)
        kb.recip(t2, t2)
        tt("dve", t2, t2, t1, ALU.mult)
        tt("dve", t3, t2, t2, ALU.mult)
        ts("dve", t0, t3, 1.0 / 11.0, ALU.mult, 1.0 / 9.0, ALU.add)
        for cf in (1.0 / 7.0, 1.0 / 5.0, 1.0 / 3.0, 1.0):
            tt("dve", t0, t0, t3, ALU.mult)
            ts("dve", t0, t0, cf, ALU.add)
        tt("dve", t0, t0, t2, ALU.mult)
        ts("dve", t1, lam, -1.0, ALU.mult, 0.0, ALU.max)
        stt(t0, t0, 2.0, t1, ALU.mult, ALU.add)
        ts("dve", lp[:, LP_NSP:LP_NSP + 4], t0, -8.0, ALU.mult)
        ts("dve", lp[:, LP_NSP2:LP_NSP2 + 4], t0, -16.0, ALU.mult)
        kb.memset("dve", lruBD, 0.0)
        for d in range(2):
            for gi, wsrc in enumerate((lru_w_a, lru_w_x)):
                for c in range(2):
                    for h in range(2):
                        dma("sp" if (c + h) % 2 == 0 else "act", lruBD[h * 64:(h + 1) * 64, d, gi, c, h * 64:(h + 1) * 64].par("ldw"), wsrc[l, d, 2 * c + h])
        kb.memset("dve", poolBD, 0.0)
        for c in range(2):
            for h in range(2):
                dma("sp", poolBD[h * 64:(h + 1) * 64, c, h * 64:(h + 1) * 64].par("ldw"), pool_w[l, 2 * c + h])
        dma("sp", lp[:, LP_PSC:LP_PSC + 2], pool_scale[l].re("(c p) -> p c", p=128), slow=True)
        dma("sp", lp[:, LP_PBS:LP_PBS + 2], pool_b[l].re("(c p) -> p c", p=128), slow=True)
        tt("dve", lp[:, LP_PBS:LP_PBS + 2], lp[:, LP_PBS:LP_PBS + 2], lp[:, LP_PSC:LP_PSC + 2], ALU.mult)
        kvn = lp[:, LP_TMP + 20:LP_TMP + 21]
        qn = lp[:, LP_TMP + 21:LP_TMP + 23]
        dma("sp", kvn, mla_kv_norm[l].re("(p o) -> p o", o=1), slow=True)
        dma("sp", qn, mla_q_norm[l].re("(c p) -> p c", p=128), slow=True)
        wtmp = AR.alloc("wtmp", [2, 512])
        dma("sp", wtmp[:, 0, :], mla_w_ukv[l])
        ts("dve", wukv, wtmp[:, 0, :], kvn, ALU.mult)
        wtmp2 = AR.alloc("wtmp2", [2, 384])
        dma("sp", wtmp2, mla_w_uq[l].re("(c p) n -> p c n", p=128))
        for c in range(2):
            ts("dve", wuq[:, c, :], wtmp2[:, c, :], qn[:, c:c + 1], ALU.mult)
        dma("sp", gains[:, 0, :], mla_q_gain[l:l + 1, :].re("o n -> (o n)").pbc(128))
        dma("sp", gains[:, 1, :], mla_k_gain[l:l + 1, :].re("o n -> (o n)").pbc(128))
        ts("dve", gains[:, 0, :], gains[:, 0, :], 96.0 ** -0.5, ALU.mult)
        dma("pool", wglu, s5_w_glu[l].re("(c p) n -> p c n", p=128))

    def src_tile(l, b, i):
        if i < 2:
            return (ctx_in if l == 0 else xcmid)[b, i * 128:(i + 1) * 128, :]
        return (x_in if l == 0 else xmid)[b, (i - 2) * 128:(i - 1) * 128, :]

    def phase_norm(l, b):
        AR.reset(top=True)
        NX = 4
        xt = [AR.alloc("xt%d" % i, [D]) for i in range(NX)]
        junk = [AR.alloc("junk%d" % i, [D]) for i in range(NX)]
        xn = [AR.alloc("xn%d" % i, [D], BF16) for i in range(NX)]
        st4 = [AR.alloc("st%d" % i, [4]) for i in range(NX)]
        def tile_gen(i):
            v = 2 if i < 2 else b
            x_t, xn_t, s4 = xt[i % NX], xn[i % NX], st4[i % NX]
            dma("sp" if i % 2 == 0 else "act", x_t, src_tile(l, b, i))
            yield
            act(junk[i % NX], x_t, AF.Square, accum=s4[:, 0:1])
            yield
            ts("dve", s4[:, 1:2], s4[:, 0:1], 1.0 / D, ALU.mult, EPS, ALU.add)
            yield
            act(s4[:, 2:3], s4[:, 1:2], AF.Sqrt)
            yield
            kb.recip(s4[:, 3:4], s4[:, 2:3])
            ts("dve", xn_t, x_t, s4[:, 3:4], ALU.mult)
            yield
            psA = kb.psum_any(bf16=True)
            psB = kb.psum_any(bf16=True)
            for c in range(KC):
                pz = psA if c % 2 == 0 else psB
                tr(pz[:, (c // 2) * 128:(c // 2 + 1) * 128], xn_t[:, c * 128:(c + 1) * 128], identb)
            yield
            for c in range(KC):
                o = hT[:, c, i * 128:(i + 1) * 128].par("norm")
                if c % 2 == 0:
                    act(o, psA[:, (c // 2) * 128:(c // 2 + 1) * 128], AF.Identity, bias=modS[:, v, c:c + 1], scale=modA[:, v, c:c + 1])
                else:
                    ts("dve", o, psB[:, (c // 2) * 128:(c // 2 + 1) * 128], modA[:, v, c:c + 1], ALU.mult, modS[:, v, c:c + 1], ALU.add)
            yield
        for g0 in range(0, 18, NX):
            lockstep([tile_gen(i) for i in range(g0, min(g0 + NX, 18))])

    def load_win(l, name, c0, ncols, top=False):
        w = AR.alloc(name, [KC, ncols], BF16, top=top)
        dma("pool", w, w_in[l, :, c0:c0 + ncols].re("(c p) n -> p c n", p=128))
        return w

    def proj_fm(w, col0, dst, p0, p1, evac):
        pos = p0
        while pos < p1:
            n = min(512, p1 - pos)
            ps = kb.psum_f()
            for k in range(KC):
                mm(ps[:, 0:n], w[:, k, col0:col0 + 128], hT[:, k, pos:pos + n], k == 0, k == KC - 1)
            evac(ps, pos, n)
            pos += n

    def phase_lru(l, b, with_ctx):
        AR.reset()
        w = wst[:, :, 0:256]
        xr = AR.alloc("xr", [2, NPOS])
        xc_ = AR.alloc("xc", [2, NPOS])
        ysum = AR.alloc("ysum", [2, NPOS])
        NB_ = 512
        tmp_sets = [{nm: AR.alloc("%s%d" % (nm, q), [NB_]) for nm in ("r", "i", "a", "a2", "bb")} for q in range(4)]
        for c in range(2):
            proj_fm(w, c * 128, xr, 0, NPOS, lambda ps, pos, n, c=c: cp("act", xr[:, c, pos:pos + n].par("proj"), ps[:, 0:n]))
        prefetch(l, PF_POOL)
        for c in range(2):
            cw = lambda k: lp[:, LP_CW + c * 4 + k:LP_CW + c * 4 + k + 1]
            for (s0, s1) in ((0, TC), (TC, NPOS)):
                ts("dve", xc_[:, c, s0:s1], xr[:, c, s0:s1], cw(2), ALU.mult, lp[:, LP_CB + c:LP_CB + c + 1], ALU.add)
                stt(xc_[:, c, s0 + 1:s1], xr[:, c, s0:s1 - 1], cw(1), xc_[:, c, s0 + 1:s1], ALU.mult, ALU.add)
                stt(xc_[:, c, s0 + 2:s1], xr[:, c, s0:s1 - 2], cw(0), xc_[:, c, s0 + 2:s1], ALU.mult, ALU.add)
                stt(xc_[:, c, s0:s1 - 1], xr[:, c, s0 + 1:s1], cw(3), xc_[:, c, s0:s1 - 1], ALU.mult, ALU.add)
        blocks = [(0, TC)] + [(TC + j * NB_, min(TC + (j + 1) * NB_, NPOS)) for j in range((T + NB_ - 1) // NB_)]

        def chain(d, c):
            dc = d * 2 + c
            tmp = tmp_sets[dc]
            out = ysum if d == 0 else xr
            order = blocks if d == 0 else [blocks[0]] + blocks[:0:-1]
            prev = None
            for (s0, s1) in order:
                n = s1 - s0
                r_, i_, a_, a2_, bb_ = (tmp[k][:, 0:n] for k in ("r", "i", "a", "a2", "bb"))
                for gi, dst in ((0, r_), (1, i_)):
                    q = 0
                    while q < n:
                        m = min(512, n - q)
                        ps = kb.psum_f()
                        mm(ps[:, 0:m], lruBD[:, d, gi, c, :], xc_[:, c, s0 + q:s0 + q + m], True, True)
                        bcol = (LP_BA if gi == 0 else LP_BX) + dc
                        act(dst[:, q:q + m], ps[:, 0:m], AF.Sigmoid, bias=lp[:, bcol:bcol + 1])
                        q += m
                yield
                act(a_, r_, AF.Exp, scale=lp[:, LP_NSP + dc:LP_NSP + dc + 1])
                act(a2_, r_, AF.Exp, scale=lp[:, LP_NSP2 + dc:LP_NSP2 + dc + 1])
                tt("pool", bb_, i_, xc_[:, c, s0:s1], ALU.mult)
                yield
                ts("dve", a2_, a2_, -1.0, ALU.mult, 1.0, ALU.add)
                yield
                act(a2_, a2_, AF.Sqrt)
                yield
                tt("dve", bb_, bb_, a2_, ALU.mult)
                init = 0.0 if prev is None else prev
                if d == 0:
                    kb.scan(out[:, c, s0:s1], a_, bb_, init)
                    prev = out[:, c, s1 - 1:s1]
                else:
                    kb.scan(out[:, c, s0:s1][:, ::-1], a_[:, ::-1], bb_[:, ::-1], init)
                    prev = out[:, c, s0:s0 + 1]
                yield
        lockstep([chain(0, 0), chain(1, 0), chain(0, 1), chain(1, 1)])
        for c in range(2):
            lo = 0 if with_ctx else TC
            tt("dve", brT[2][c][:, lo:NPOS], ysum[:, c, lo:NPOS], xr[:, c, lo:NPOS], ALU.add)

    AR_car = [kb.sb("car%d" % i, [128, 1]) for i in range(2)]

    def phase_pool(l, b, with_ctx):
        AR.reset()
        w = wst[:, :, 0:256]
        xp = AR.alloc("xp", [2, NPOS])
        cs = AR.alloc("cs", [2, NPOS + 2 * 17 + 2])
        pm = AR.alloc("pm", [2, NPOS])
        ones = AR.alloc("ones", [T])
        kb.memset("pool", ones, 1.0)
        for c in range(2):
            proj_fm(w, c * 128, xp, 0, NPOS, lambda ps, pos, n, c=c: cp("act", xp[:, c, pos:pos + n].par("proj"), ps[:, 0:n]))
        prefetch(l, PF_MLA)
        segs = [(0, TC, 0)] + [(TC, NPOS, TC + 17)]
        if not with_ctx:
            segs = segs[1:]
        for c in range(2):
            for (s0, s1, o0) in segs:
                L = s1 - s0
                kb.memset("pool", cs[:, c, o0:o0 + 9], 0.0)
                kb.scan(cs[:, c, o0 + 9:o0 + 9 + L], ones[:, 0:L], xp[:, c, s0:s1], 0.0)
                ts("dve", cs[:, c, o0 + 9 + L:o0 + 17 + L], cs[:, c, o0:o0 + 8], cs[:, c, o0 + 8 + L:o0 + 9 + L], ALU.add)
                for h in range(2):
                    hw = (1, 2, 4, 8)[2 * c + h]
                    rows = slice(h * 64, (h + 1) * 64)
                    base = o0 + 8
                    tt("dve", pm[rows, c, s0:s1], cs[rows, c, base + hw:base + hw + L], cs[rows, c, base - hw:base - hw + L], ALU.subtract)
                ts("dve", pm[:, c, s0:s1], pm[:, c, s0:s1], cpool[:, c:c + 1], ALU.mult)
                tt("dve", pm[:, c, s0:s0 + 8], pm[:, c, s0:s0 + 8], cpool[:, 2 + c * 8:2 + c * 8 + 8], ALU.mult)
                tt("dve", pm[:, c, s1 - 8:s1], pm[:, c, s1 - 8:s1], cpool[:, 18 + c * 8:18 + c * 8 + 8], ALU.mult)
                tt("pool", pm[:, c, s0:s1], pm[:, c, s0:s1], xp[:, c, s0:s1], ALU.subtract)
                pos = s0
                while pos < s1:
                    n = min(512, s1 - pos)
                    ps = kb.psum_f()
                    mm(ps[:, 0:n], poolBD[:, c, :], pm[:, c, pos:pos + n], True, True)
                    act(brT[3][c][:, pos:pos + n].par("pool"), ps[:, 0:n], AF.Identity, bias=lp[:, LP_PBS + c:LP_PBS + c + 1],
                        scale=lp[:, LP_PSC + c:LP_PSC + c + 1])
                    pos += n

    def phase_mla(l, b, with_ctx):
        AR.reset()
        wkv = wst[:, :, 0:160]
        wq = wst[:, :, 160:416]
        qT = AR.alloc("qT", [4, NPOS], BF16)
        kT = AR.alloc("kT", [4, NPOS], BF16)
        Vt = AR.alloc("Vt", [18, 4, 68], BF16)
        lo_fixed = AR.lo
        kb.memset("dve", Vt.re("p a b c -> p (a b c)"), 1.0)
        for h_ in range(4):
            kb.memset("pool", qT[:, h_, :], 0.0)
            kb.memset("pool", kT[:, h_, :], 0.0)
        NS = 3
        sm = [AR.alloc("sm%d" % i, [32]) for i in range(NS)]
        kr = [AR.alloc("kr%d" % i, [32]) for i in range(NS)]
        cn = [AR.alloc("cn%d" % i, [384], BF16) for i in range(NS)]
        cnT = [AR.alloc("cnT%d" % i, [3, 128], BF16) for i in range(NS)]
        sq = [AR.alloc("sq%d" % i, [704]) for i in range(NS)]
        qk = [AR.alloc("qk%d" % i, [2, 4, 96]) for i in range(NS)]
        rg = [AR.alloc("rg%d" % i, [2, 4, 96]) for i in range(NS)]
        qkb = [AR.alloc("qkb%d" % i, [2, 4, 96], BF16) for i in range(NS)]
        rt = [AR.alloc("rt%d" % i, [2, 2, 4, 32]) for i in range(NS)]
        kvs_ = [AR.alloc("kvs%d" % i, [512]) for i in range(NS)]
        qs_ = [AR.alloc("qs%d" % i, [384]) for i in range(NS)]

        def info(i):
            is_ctx = i < 2
            do_q = (not is_ctx) or with_ctx
            return is_ctx, do_q, i % NS, slice(i * 128, (i + 1) * 128)

        def stage_a(i):
            is_ctx, do_q, j, pos = info(i)
            s, cn_, cnT_, sq_ = sm[j], cn[j], cnT[j], sq[j]
            ps1 = kb.psum_any()
            for k in range(KC):
                mm(ps1[:, 0:160], hT[:, k, pos], wkv[:, k, :], k == 0, k == KC - 1)
            if do_q:
                for k in range(KC):
                    mm(ps1[:, 160:416], hT[:, k, pos], wq[:, k, :], k == 0, k == KC - 1)
            yield
            act(sq_[:, 0:128], ps1[:, 0:128], AF.Square, accum=s[:, 0:1])
            if do_q:
                act(sq_[:, 128:384], ps1[:, 160:416], AF.Square, accum=s[:, 1:2])
            else:
                kb.memset("pool", s[:, 1:2], 1.0)
            cp("act", kr[j], ps1[:, 128:160])
            yield
            act(s[:, 2:3], s[:, 0:1], AF.Sqrt, scale=1.0 / 128, bias=EPS)
            act(s[:, 3:4], s[:, 1:2], AF.Sqrt, scale=1.0 / 256, bias=EPS)
            yield
            kb.recip(s[:, 4:6], s[:, 2:4])
            ts("dve", cn_[:, 0:128], ps1[:, 0:128], s[:, 4:5], ALU.mult)
            if do_q:
                ts("dve", cn_[:, 128:384], ps1[:, 160:416], s[:, 5:6], ALU.mult)
            yield
            psT = kb.psum_any(bf16=True)
            for c in range(3 if do_q else 1):
                tr(psT[:, c * 128:(c + 1) * 128], cn_[:, c * 128:(c + 1) * 128], identb)
            yield
            cp("act", cnT_[:, 0:(3 if do_q else 1), :], psT[:, 0:(384 if do_q else 128)].re("p (c n) -> p c n", n=128))
            yield

        def stage_b(i):
            is_ctx, do_q, j, pos = info(i)
            s, cnT_, sq_, qk_, qkb_, rt_, rg_ = sm[j], cnT[j], sq[j], qk[j], qkb[j], rt[j], rg[j]
            pskv = kb.psum_any()
            mm(pskv, cnT_[:, 0, :], wukv, True, True)
            if do_q:
                psq = kb.psum_any()
                for c in range(2):
                    mm(psq[:, 0:384], cnT_[:, 1 + c, :], wuq[:, c, :], c == 0, c == 1)
            yield
            cp("act", kvs_[j], pskv)
            if do_q:
                cp("act", qs_[j], psq[:, 0:384])
            kv3 = kvs_[j].re("p (h e) -> p h e", h=4)
            q3 = qs_[j].re("p (h e) -> p h e", h=4)
            krope = kr[j]
            act(sq_[:, 640:672], krope, AF.Square, accum=s[:, 7:8])
            cp("act", Vt[:, i, :, 0:64].par("mlav"), kv3[:, :, 64:128])
            yield
            ksq = sq_[:, 0:256].re("p (h e) -> p h e", h=4)
            qsq = sq_[:, 256:640].re("p (h e) -> p h e", h=4)
            tt("pool", ksq, kv3[:, :, 0:64], kv3[:, :, 0:64], ALU.mult)
            if do_q:
                tt("pool", qsq, q3, q3, ALU.mult)
            yield
            kb.reduce(s[:, 12:16], ksq)
            if do_q:
                kb.reduce(s[:, 8:12], qsq)
            else:
                kb.memset("dve", s[:, 8:12], 1.0)
            ts("dve", s[:, 12:16], s[:, 12:16], s[:, 7:8], ALU.add)
            yield
            act(s[:, 8:16], s[:, 8:16], AF.Sqrt, scale=1.0 / 96, bias=EPS)
            yield
            kb.recip(s[:, 16:24], s[:, 8:16])
            yield
            tt("pool", rg_, s[:, 16:24].re("p (w h) -> p w h", w=2).us(3).bc([128, 2, 4, 96]),
               gains.us(2).bc([128, 2, 4, 96]), ALU.mult)
            yield
            if do_q:
                tt("dve", qk_[:, 0], q3, rg_[:, 0], ALU.mult)
            else:
                kb.memset("pool", qk_[:, 0], 0.0)
            tt("dve", qk_[:, 1, :, 0:64], kv3[:, :, 0:64], rg_[:, 1, :, 0:64], ALU.mult)
            tt("pool", qk_[:, 1, :, 64:96], krope.us(1).bc([128, 4, 32]), rg_[:, 1, :, 64:96], ALU.mult)
            yield
            if not is_ctx:
                ti = i - 2
                v = qk_[:, :, :, 64:96]
                t1_ = rt_[:, 0]
                t2_ = rt_[:, 1]
                Cb = ropeC[:, ti, :].us(1).us(1).bc([128, 2, 4, 32])
                tt("pool", t1_, v, Cb, ALU.mult)
                for a in range(2):
                    for s_ in range(2):
                        o_ = t2_[:, :, :, a * 16 + s_ * 8:a * 16 + s_ * 8 + 8]
                        i_ = qk_[:, :, :, 64 + a * 16 + (1 - s_) * 8:64 + a * 16 + (1 - s_) * 8 + 8]
                        Sb = ropeS[:, ti, a * 16 + s_ * 8:a * 16 + s_ * 8 + 8].us(1).us(1).bc([128, 2, 4, 8])
                        tt("dve" if (a + s_) % 2 == 0 else "pool", o_, i_, Sb, ALU.mult)
                yield
                tt("dve", v, t1_, t2_, ALU.add)
            cp("dve", qkb_, qk_)
            yield

        def stage_c(i):
            is_ctx, do_q, j, pos = info(i)
            qkb_ = qkb[j]
            psT2 = kb.psum_any(bf16=True)
            for w_ in range(2):
                if w_ == 0 and not do_q:
                    continue
                for h in range(4):
                    tr(psT2[0:96, (w_ * 4 + h) * 128:(w_ * 4 + h + 1) * 128], qkb_[:, w_, h, :], identb)
            yield
            if do_q:
                cp("act", qT[0:96, :, pos].par("mlaqk"), psT2[0:96, 0:512].re("p (h n) -> p h n", h=4))
            cp("act", kT[0:96, :, pos].par("mlaqk"), psT2[0:96, 512:1024].re("p (h n) -> p h n", h=4))

        def tile_gen(i):
            yield from stage_a(i)
            yield from stage_b(i)
            yield from stage_c(i)
        for g0 in range(0, 18, NS):
            lockstep([tile_gen(i) for i in range(g0, g0 + NS)])
        if "mla_stop1" in dbg:
            return
        prefetch(l, PF_GATE0)
        P.barrier()
        AR.lo = lo_fixed
        omla = AR.alloc("omla", [18, 256], BF16)
        pT = [AR.alloc("pT%d" % i, [512], BF16) for i in range(3)]
        rc = [AR.alloc("rc%d" % i, [4]) for i in range(2)]
        jobs = []
        if with_ctx:
            jobs.append((0, TC, 0, 2))
        for qb in range(4):
            jobs.append((TC + qb * 512, TC + (qb + 1) * 512, 0, 18))
        pi = 0
        for (q0, q1, kt0, kt1) in jobs:
            nq = q1 - q0
            nqt = nq // 128
            for h in range(4):
                pso = [kb.psf[2 + t_] for t_ in range(nqt)]

                def s_mm(kt):
                    mm(kb.psf[kt % 2][:, 0:nq], kT[:, h, kt * 128:(kt + 1) * 128], qT[:, h, q0:q1], True, True)
                s_mm(kt0)
                for kt in range(kt0, kt1):
                    if kt + 1 < kt1:
                        s_mm(kt + 1)
                    pss = kb.psf[kt % 2]
                    p_ = pT[pi % 3]
                    pi += 1
                    act(p_[:, 0:nq], pss[:, 0:nq], AF.Exp)
                    for t_ in range(nqt):
                        mm(pso[t_][:, 0:68], p_[:, t_ * 128:(t_ + 1) * 128], Vt[:, kt, h, :], kt == kt0, kt == kt1 - 1)
                for t_ in range(nqt):
                    r_ = rc[t_ % 2]
                    kb.recip(r_[:, 0:1], pso[t_][:, 64:65])
                    ts("dve", omla[:, (q0 // 128) + t_, h * 64:(h + 1) * 64].par("omla"), pso[t_][:, 0:64], r_[:, 0:1], ALU.mult)
        for i in range(0 if with_ctx else 2, 18):
            psT = kb.psum_b()
            for c in range(2):
                tr(psT[:, c * 128:(c + 1) * 128], omla[:, i, c * 128:(c + 1) * 128], identb)
            for c in range(2):
                cp("act", brT[0][c][:, i * 128:(i + 1) * 128].par("brt0"), psT[:, c * 128:(c + 1) * 128])

    def phase_s5(l, b, with_ctx):
        AR.reset()
        AR2.lo = 0
        U = AR.alloc("U", [16, NCH])
        SN = [[AR.alloc("SN%d%d" % (d, ri), [8, NCH + 2]) for ri in range(2)] for d in range(2)]
        Ere = AR.alloc("Ere", [2, 8, 128])
        nEim = AR.alloc("nEim", [2, 8, 128])
        W3 = AR.alloc("W3", [16, 128])
        scr0 = AR.lo
        ws5 = wst[:, :, 0:256]
        Uc = AR.alloc("Uc", [16, 8, 16])
        kblocks = [(0, 32, 0), (32, 128, TC), (160, 128, TC + 1024)]
        for (k0, M, p0) in kblocks:
            pss = [kb.psum_f() for _ in range(4)]
            for r in range(8):
                ps = pss[r // 2]
                for k in range(KC):
                    mm(ps[0:M, (r % 2) * 256:(r % 2 + 1) * 256], hT[:, k, p0 + r:p0 + 8 * M:8], ws5[:, k, :], k == 0, k == KC - 1)
            for q in range(4):
                src = pss[q][0:M, :].re("p (r g i) -> p r g i", r=2, g=16)
                dst = Uc[0:M, :, 2 * q:2 * q + 2, :].re("p g r i -> p r g i").par("s5uc")
                cp("act" if q % 2 == 0 else "dve", dst, src)
            for gq in range(4):
                ps = kb.psum_f()
                for gg_ in range(4):
                    g = gq * 4 + gg_
                    tr(ps[:, gg_ * 128:gg_ * 128 + M], Uc[0:M, g, :, :].re("p r i -> p (r i)"), identf[0:M, 0:M])
                cp("act" if gq % 2 == 0 else "dve", U[:, gq * 4:gq * 4 + 4, k0:k0 + M].par("s5u"),
                   ps.re("p (g n) -> p g n", g=4)[:, :, 0:M])
        prefetch(l, PF_LRU)
        P.barrier()
        AR.lo = scr0
        if "s5_stop1" in dbg:
            return
        cached = (b > 0) and ("s5_nocache" not in dbg)
        if not cached:
            kb.memset("pool", W3, 0.0)
        else:
            dma("sp", Ere.re("p d g n -> p (d g n)"), st_main[0])
            dma("act", nEim.re("p d g n -> p (d g n)"), st_main[1])
            dma("sp", W3.re("p g n -> p (g n)"), st_main[2])
        prm = AR.alloc("prm", [40, 8])
        PW = [AR.alloc("pw%d" % i, [10, 8]) for i in range(2)]
        Bb = [AR.alloc("Bb%d" % i, [8, 16]) for i in range(2)]
        Braw = [AR.alloc("Braw%d" % i, [8, 16]) for i in range(2)]
        Craw = [AR.alloc("Craw%d" % i, [8, 16]) for i in range(2)]
        Dbc = AR.alloc("Dbc", [256])
        t8 = AR.alloc("t8", [8, 16])
        w3t = [AR.alloc("w3t%d" % i, [128]) for i in range(2)]
        tE = AR.alloc("tE", [8, 128])
        Fm = [AR.alloc("F%d" % i, [8, 8, 16]) for i in range(2)]
        Ep = [AR.alloc("Ep%d" % i, [8, 128]) for i in range(2)]
        Cnat = [AR.alloc("Cnat%d" % i, [16, 64], at=Ep[i].off, buf=Ep[i].buf) for i in range(2)]
        rs_sets = [[AR.alloc("rs%d_%d" % (q, i), [NCH], at=Fm[0].off + (q * 7 + i) * NCH) for i in range(6)]
                   for q in range(2)]
        rho_sets = [AR.alloc("rho1_%d" % q, [NCH], at=Fm[0].off + (q * 7 + 6) * NCH) for q in range(2)]
        assert Fm[0].off + 14 * NCH <= Ep[1].off + Ep[1].w
        W1 = [AR2.alloc("W1%d" % i, [16, 64]) for i in range(2)]
        tab = [AR2.alloc("tab%d" % i, [8, NCH]) for i in range(2)]
        dma("sp", Dbc, s5_d[l:l + 1, :].re("o n -> (o n)").pbc(128))
        dsel = cmask[:, 2, :]

        prm_bufs = [Buf("prm%d" % i) for i in range(40)]
        pw_bufs = [[Buf("pw%d_%d" % (ri, j)) for j in range(10)] for ri in range(2)]

        def pv(i):
            return TV(prm.ap[:, i, :], prm_bufs[i])

        def pw(ri, j):
            return TV(PW[ri].ap[:, j, :], pw_bufs[ri][j])

        for d in range(2):
            if d == 1:
                P.barrier()
            if not cached:
                dma("sp", pv(0), s5_a_re[l, d].re("(gp h) p -> (h p) gp", h=2), slow=True)
                dma("act", pv(1), s5_a_im[l, d].re("(gp h) p -> (h p) gp", h=2), slow=True)
                ldt = t8[:, 0, :]
                dma("sp", ldt, s5_log_dt[l, d:d + 1, :].re("o g -> (o g)").pbc(128))
                dma("sp", Cnat[0][0:16], s5_c_re[l, d].re("g o p -> o g p"))
                dma("act", Cnat[1][0:16], s5_c_im[l, d].re("g o p -> o g p"))
                for h in range(2):
                    rows = slice(h * 64, (h + 1) * 64)
                    cp("dve", pv(2)[rows], ldt[rows, h:16:2])
                    dma("sp", Braw[0][rows].par("ldw"), s5_b_re[l, d].re("(gp h) p i -> h p gp i", h=2)[h])
                    dma("act", Braw[1][rows].par("ldw"), s5_b_im[l, d].re("(gp h) p i -> h p gp i", h=2)[h])
                for ri in range(2):
                    ps = kb.psum_f()
                    for g in range(16):
                        gp, h = g // 2, g % 2
                        mm(ps[h * 64:(h + 1) * 64, gp * 16:(gp + 1) * 16], Cnat[ri][0:16, g, :], identf[0:16, 0:16], True, True)
                    cp("act", Craw[ri], ps[:, 0:128].re("p (g o) -> p g o", g=8))
                if "s5_b1" in dbg:
                    return
                act(pv(2), pv(2), AF.Exp)
                tt("dve", pv(3), pv(0), pv(2), ALU.mult)
                tt("dve", pv(4), pv(1), pv(2), ALU.mult)
                act(pv(5), pv(3), AF.Exp)
                for (dst, shift) in ((6, 0.0), (7, math.pi / 2)):
                    xx, kk, ki = pv(30), pv(31), pv(32)
                    ts("dve", xx, pv(4), shift, ALU.add)
                    ts("dve", kk, xx, 1.0 / (2 * math.pi), ALU.mult)
                    kint = TV(ki.ap.bitcast(I32), ki.buf)
                    cp("dve", kint, kk)
                    cp("dve", kk, kint)
                    stt(xx, kk, -6.28125, xx, ALU.mult, ALU.add)
                    stt(xx, kk, -(2 * math.pi - 6.28125), xx, ALU.mult, ALU.add)
                    ts("dve", xx, xx, math.pi, ALU.min, -math.pi, ALU.max)
                    act(pv(dst), xx, AF.Sin)
                kb.memset("dve", pw(0, 0), 1.0)
                kb.memset("dve", pw(1, 0), 0.0)
                tt("dve", pw(0, 1), pv(5), pv(7), ALU.mult)
                tt("dve", pw(1, 1), pv(5), pv(6), ALU.mult)
                for j in range(2, 9):
                    tt("dve", pv(30), pw(0, j - 1), pw(0, 1), ALU.mult)
                    tt("dve", pv(31), pw(1, j - 1), pw(1, 1), ALU.mult)
                    tt("dve", pw(0, j), pv(30), pv(31), ALU.subtract)
                    tt("dve", pv(30), pw(0, j - 1), pw(1, 1), ALU.mult)
                    tt("dve", pv(31), pw(1, j - 1), pw(0, 1), ALU.mult)
                    tt("dve", pw(1, j), pv(30), pv(31), ALU.add)
                act(pv(8), pv(3), AF.Exp, scale=-16.0)
                tt("dve", pw(0, 9), pw(0, 8), pv(8), ALU.mult)
                tt("dve", pw(1, 9), pw(1, 8), pv(8), ALU.mult)
                ts("dve", pw(1, 9), pw(1, 9), -1.0, ALU.mult)
                act(pv(9), pv(3), AF.Exp, scale=8.0)
                act(pv(10), pv(3), AF.Exp, scale=-8.0)
                tt("dve", pv(11), pw(0, 8), pv(10), ALU.mult)
                tt("dve", pv(12), pw(1, 8), pv(10), ALU.mult)
                ts("dve", pv(13), pw(0, 1), -1.0, ALU.add)
                tt("dve", pv(14), pv(0), pv(0), ALU.mult)
                tt("dve", pv(15), pv(1), pv(1), ALU.mult)
                tt("dve", pv(14), pv(14), pv(15), ALU.add)
                kb.recip(pv(14), pv(14))
                tt("dve", pv(15), pv(13), pv(0), ALU.mult)
                tt("dve", pv(16), pw(1, 1), pv(1), ALU.mult)
                tt("dve", pv(15), pv(15), pv(16), ALU.add)
                tt("dve", pv(15), pv(15), pv(14), ALU.mult)
                tt("dve", pv(16), pw(1, 1), pv(0), ALU.mult)
                tt("dve", pv(17), pv(13), pv(1), ALU.mult)
                tt("dve", pv(16), pv(16), pv(17), ALU.subtract)
                tt("dve", pv(16), pv(16), pv(14), ALU.mult)
                fre = pv(15).us(2).bc([128, 8, 16])
                fim = pv(16).us(2).bc([128, 8, 16])
                tt("dve", Bb[0], Braw[0], fre, ALU.mult)
                tt("dve", t8, Braw[1], fim, ALU.mult)
                tt("dve", Bb[0], Bb[0], t8, ALU.subtract)
                tt("dve", Bb[1], Braw[1], fre, ALU.mult)
                tt("dve", t8, Braw[0], fim, ALU.mult)
                tt("dve", Bb[1], Bb[1], t8, ALU.add)
                for t_ in range(8):
                    f_ = t_ + 1 if d == 0 else 8 - t_
                    pr = pw(0, f_).us(2).bc([128, 8, 16])
                    pi_ = pw(1, f_).us(2).bc([128, 8, 16])
                    eo = Ere[:, d, :, t_ * 16:(t_ + 1) * 16]
                    ei = nEim[:, d, :, t_ * 16:(t_ + 1) * 16]
                    tt("dve", eo, Craw[0], pr, ALU.mult)
                    tt("dve", t8, Craw[1], pi_, ALU.mult)
                    tt("dve", eo, eo, t8, ALU.subtract)
                    tt("dve", ei, Craw[0], pi_, ALU.mult)
                    tt("dve", t8, Craw[1], pr, ALU.mult)
                    tt("dve", ei, ei, t8, ALU.add)
                    ts("dve", ei, ei, -1.0, ALU.mult)
                for r in range(8):
                    e_ = 7 - r if d == 0 else r
                    pr = pw(0, e_).us(2).bc([128, 8, 16])
                    pi_ = pw(1, e_).us(2).bc([128, 8, 16])
                    tt("dve", Fm[0][:, :, r, :], Bb[0], pr, ALU.mult)
                    tt("dve", t8, Bb[1], pi_, ALU.mult)
                    tt("dve", Fm[0][:, :, r, :], Fm[0][:, :, r, :], t8, ALU.subtract)
                    tt("dve", Fm[1][:, :, r, :], Bb[1], pr, ALU.mult)
                    tt("dve", t8, Bb[0], pi_, ALU.mult)
                    tt("dve", Fm[1][:, :, r, :], Fm[1][:, :, r, :], t8, ALU.add)
                qr = pw(0, 9).us(2).bc([128, 8, 128])
                qi = pw(1, 9).us(2).bc([128, 8, 128])
                tt("dve", Ep[0], Ere[:, d], qr, ALU.mult)
                tt("dve", tE, nEim[:, d], qi, ALU.mult)
                tt("dve", Ep[0], Ep[0], tE, ALU.add)
                tt("dve", Ep[1], nEim[:, d], qr, ALU.mult)
                tt("dve", tE, Ere[:, d], qi, ALU.mult)
                tt("dve", Ep[1], Ep[1], tE, ALU.subtract)
                if "s5_b2" in dbg:
                    return
                for ri in range(2):
                    for gq in range(4):
                        ps = kb.psum_f()
                        for gg_ in range(4):
                            g = gq * 4 + gg_
                            gp, h = g // 2, g % 2
                            rows = slice(h * 64, (h + 1) * 64)
                            mm(ps[:, gg_ * 64:(gg_ + 1) * 64], Fm[ri][:, gp].re("p r i -> p (r i)"), identf[:, h * 64:(h + 1) * 64], True, True)
                        cp("act", W1[ri][:, gq * 4:gq * 4 + 4, :], ps[:, 0:256].re("p (g n) -> p g n", g=4))
                if "s5_b3" in dbg:
                    return
                for g in range(16):
                    gp, h = g // 2, g % 2
                    rows = slice(h * 64, (h + 1) * 64)
                    ps = kb.psf[h * 2 + (g // 2) % 2]
                    mm(ps[:, 0:128], Fm[0][rows, gp].re("p r i -> p (r i)"), Ep[0][rows, gp, :], True, False)
                    mm(ps[:, 0:128], Fm[1][rows, gp].re("p r i -> p (r i)"), Ep[1][rows, gp, :], False, True)
                    wt = w3t[g % 2]
                    tt("dve", wt, ps[:, 0:128], cmask[:, d, :], ALU.mult)
                    tt("pool", W3[:, g, :], W3[:, g, :], wt, ALU.add)
                if d == 0:
                    for g in range(16):
                        dcol = Dbc[:, g * 16:(g + 1) * 16].us(1).bc([128, 8, 16])
                        wt = w3t[g % 2]
                        tt("dve", wt.re("p (t o) -> p t o", t=8), dsel.re("p (t o) -> p t o", t=8), dcol, ALU.mult)
                        tt("pool", W3[:, g, :], W3[:, g, :], wt, ALU.add)
                if "s5_b4" in dbg:
                    return
                kb.memset("dve", tab[0][:, :, 0:1], 1.0)
                kb.memset("dve", tab[1][:, :, 0:1], 0.0)
                cp("dve", tab[0][:, :, 1:2], pv(11).us(2))
                cp("dve", tab[1][:, :, 1:2], pv(12).us(2))
                n = 2
                while n < NCH:
                    m = min(n, NCH - n)
                    tt("dve", pv(30), tab[0][:, :, n - 1], pv(11), ALU.mult)
                    tt("dve", pv(31), tab[1][:, :, n - 1], pv(12), ALU.mult)
                    tt("dve", pv(33), pv(30), pv(31), ALU.subtract)
                    tt("dve", pv(30), tab[0][:, :, n - 1], pv(12), ALU.mult)
                    tt("dve", pv(31), tab[1][:, :, n - 1], pv(11), ALU.mult)
                    tt("dve", pv(34), pv(30), pv(31), ALU.add)
                    Pr = pv(33).us(2).bc([128, 8, m])
                    Pi = pv(34).us(2).bc([128, 8, m])
                    ta = tE[:, :, 0:m]
                    tt("dve", tab[0][:, :, n:n + m], tab[0][:, :, 0:m], Pr, ALU.mult)
                    tt("dve", ta, tab[1][:, :, 0:m], Pi, ALU.mult)
                    tt("dve", tab[0][:, :, n:n + m], tab[0][:, :, n:n + m], ta, ALU.subtract)
                    tt("dve", tab[1][:, :, n:n + m], tab[0][:, :, 0:m], Pi, ALU.mult)
                    tt("dve", ta, tab[1][:, :, 0:m], Pr, ALU.mult)
                    tt("dve", tab[1][:, :, n:n + m], tab[1][:, :, n:n + m], ta, ALU.add)
                    n *= 2
                dma("sp", st_dir[d, :, 0:1024], W1[0].re("p g n -> p (g n)"))
                dma("sp", st_dir[d, :, 1024:2048], W1[1].re("p g n -> p (g n)"))
                dma("act", st_dir[d, :, 2048:2048 + 8 * NCH], tab[0].re("p g n -> p (g n)"))
                dma("act", st_dir[d, :, 2048 + 8 * NCH:2048 + 16 * NCH], tab[1].re("p g n -> p (g n)"))
                dma("sp", st_dir[d, :, 2048 + 16 * NCH:2048 + 16 * NCH + 8], pv(9))
            else:
                dma("sp", W1[0].re("p g n -> p (g n)"), st_dir[d, :, 0:1024])
                dma("sp", W1[1].re("p g n -> p (g n)"), st_dir[d, :, 1024:2048])
                dma("act", tab[0].re("p g n -> p (g n)"), st_dir[d, :, 2048:2048 + 8 * NCH])
                dma("act", tab[1].re("p g n -> p (g n)"), st_dir[d, :, 2048 + 8 * NCH:2048 + 16 * NCH])
                dma("sp", pv(9), st_dir[d, :, 2048 + 16 * NCH:2048 + 16 * NCH + 8])
            if "s5_stop2" in dbg:
                return
            P.barrier()
            def nat(tv, sl, rev):
                v = tv[:, sl]
                return v[:, ::-1] if rev else v

            def gp_gen(gp, d=d):
                rs, rho1 = rs_sets[gp % 2], rho_sets[gp % 2]
                psv = [kb.psum_f(), kb.psum_f()]
                for ri in range(2):
                    for h in range(2):
                        g = gp * 2 + h
                        mm(psv[ri][h * 64:(h + 1) * 64, 0:NCH], W1[ri][:, g, :], U[:, g, :], True, True)
                cp("pool", rho1, pv(9)[:, gp:gp + 1].bc([128, NCH]))
                yield
                Cn, Sn = tab[0][:, gp, :], tab[1][:, gp, :]
                if d == 0:
                    segs = [(slice(0, NCH), slice(0, NCH), False)]
                else:
                    segs = [(slice(0, 32), slice(0, 32), True), (slice(32, NCH), slice(32, NCH), True)]
                for (js, ks, rev) in segs:
                    vre, vim = nat(psv[0][:, 0:NCH], ks, rev), nat(psv[1][:, 0:NCH], ks, rev)
                    tt("dve", rs[0][:, js], vre, Cn[:, js], ALU.mult)
                    tt("dve", rs[1][:, js], vim, Sn[:, js], ALU.mult)
                    tt("dve", rs[4][:, js], vim, Cn[:, js], ALU.mult)
                    tt("dve", rs[5][:, js], vre, Sn[:, js], ALU.mult)
                yield
                tt("pool", rs[2], rs[0], rs[1], ALU.add)
                tt("pool", rs[3], rs[4], rs[5], ALU.subtract)
                yield
                kb.scan(rs[4], rho1, rs[2], 0.0)
                kb.scan(rs[5], rho1, rs[3], 0.0)
                yield
                tt("dve", rs[0], rs[4], Cn, ALU.mult)
                tt("dve", rs[1], rs[5], Sn, ALU.mult)
                tt("dve", rs[2], rs[4], Sn, ALU.mult)
                tt("dve", rs[3], rs[5], Cn, ALU.mult)
                yield
                for (js, ks, rev) in segs:
                    if d == 0:
                        osl = slice(1, NCH + 1)
                    else:
                        osl = slice(0, 32) if ks.start == 0 else slice(33, NCH + 1)
                    ore = nat(SN[d][0][:, gp, :], osl, rev).par("s5sn")
                    oim = nat(SN[d][1][:, gp, :], osl, rev).par("s5sn")
                    tt("pool", ore, rs[0][:, js], rs[1][:, js], ALU.subtract)
                    tt("pool", oim, rs[2][:, js], rs[3][:, js], ALU.add)
                yield
            for gp0 in range(0, 8, 2):
                lockstep([gp_gen(gp0), gp_gen(gp0 + 1)])
            for ri in range(2):
                if d == 0:
                    kb.memset("pool", SN[0][ri][:, :, 0:1], 0.0)
                else:
                    kb.memset("pool", SN[1][ri][:, :, 32:33], 0.0)
                    cp("pool", SN[1][ri][:, :, NCH + 1:NCH + 2], SN[1][ri][:, :, 0:1])
        if not cached:
            dma("sp", st_main[0], Ere.re("p d g n -> p (d g n)"))
            dma("act", st_main[1], nEim.re("p d g n -> p (d g n)"))
            dma("sp", st_main[2], W3.re("p g n -> p (g n)"))
        if "s5_stop3" in dbg:
            return
        P.barrier()
        AR.lo = scr0
        AR2.lo = 0
        Yc = AR.alloc("Yc", [8, 256])
        gg = AR.alloc("gg", [8, 256])
        g2 = AR.alloc("g2", [8, 256])
        sg = [AR.alloc("sg%d" % i, [512]) for i in range(2)]
        ggT = AR2.alloc("ggT", [2, NPOS])
        ggb = AR2.alloc("ggb", [2, NPOS], BF16)
        for (k0, M, p0) in kblocks:
            if k0 == 0 and not with_ctx:
                continue
            banks = [kb.psf[0], kb.psf[2], kb.psf[1], kb.psf[3]]
            for g in range(16):
                gp, h = g // 2, g % 2
                rows = slice(h * 64, (h + 1) * 64)
                bank = banks[h * 2 + (gp // 4)]
                o = bank[0:M, (gp % 4) * 128:(gp % 4 + 1) * 128]
                cf = slice(k0, k0 + M)
                cb_ = slice(k0 + 1, k0 + 1 + M) if k0 == 0 else slice(k0 + 2, k0 + 2 + M)
                mm(o, SN[0][0][rows, gp, cf], Ere[rows, 0, gp, :], True, False)
                mm(o, SN[0][1][rows, gp, cf], nEim[rows, 0, gp, :], False, False)
                mm(o, SN[1][0][rows, gp, cb_], Ere[rows, 1, gp, :], False, False)
                mm(o, SN[1][1][rows, gp, cb_], nEim[rows, 1, gp, :], False, False)
                mm(o, U[:, g, k0:k0 + M], W3[:, g, :], False, True)
            for h in range(2):
                for q in range(2):
                    bank = banks[h * 2 + q]
                    src = bank[0:M, :].re("p (g t o) -> p g t o", g=4, t=8)
                    dst = Yc[0:M].re("p t (g2 h o) -> p h g2 t o", h=2, o=16)[:, h, 4 * q:4 * q + 4].par("s5yc")
                    cp("act" if h == 0 else "dve", dst, src)
            if "s5_Y" in dbg:
                dma("sp", tap("s5Y", [NCH, 8 * 256])[k0:k0 + M, :], Yc[0:M].re("p t f -> p (t f)"))
            yv, gv, g2v = Yc[0:M], gg[0:M], g2[0:M]
            tt("pool", g2v, yv, yv, ALU.mult)
            ts("dve", g2v, g2v, 0.044715, ALU.mult, 1.0, ALU.add)
            tt("pool", g2v, g2v, yv, ALU.mult)
            act(g2v, g2v, AF.Sigmoid, scale=1.5957691216057308)
            tt("dve", gv, g2v, yv, ALU.mult)
            for t_ in range(8):
                pz = [kb.psum_f(), kb.psum_f()]
                for c in range(2):
                    tr(pz[c][:, 0:M], gv[:, t_, c * 128:(c + 1) * 128], identf[0:M, 0:M])
                for c in range(2):
                    cp("act" if c == 0 else "dve", ggT[:, c, p0 + t_:p0 + 8 * M:8].par("s5gg"), pz[c][:, 0:M])
        lo = 0 if with_ctx else TC
        for c in range(2):
            cp("dve", ggb[:, c, lo:NPOS], ggT[:, c, lo:NPOS])
        for c in range(2):
            pos = lo
            while pos < NPOS:
                n = min(512, NPOS - pos)
                ps = kb.psum_f()
                for k in range(2):
                    mm(ps[:, 0:n], wglu[:, k, c * 128:(c + 1) * 128], ggb[:, k, pos:pos + n], k == 0, k == 1)
                s_ = sg[(pos // 512) % 2]
                act(s_[:, 0:n], ps[:, 0:n], AF.Sigmoid)
                tt("dve", brT[1][c][:, pos:pos + n], s_[:, 0:n], ggT[:, c, pos:pos + n], ALU.mult)
                pos += n

    def phase_merge(l, b, with_ctx, nxt=None):
        AR.reset(top=True)
        lo = 0 if with_ctx else TC
        blocks = []
        pos = lo
        while pos < NPOS:
            n = min(512, NPOS - pos) if pos >= TC else TC - pos
            blocks.append((pos, n))
            pos += n
        ypT = AR.alloc("ypT", [KC, NPOS], BF16, top=True)
        wbr = AR.alloc("wbr", [4, 2, D], BF16, top=True)
        wo = AR.alloc("wo", [KC, D], BF16, top=True)
        wml = [AR.alloc("wml%d" % i, [KC, 4, 128], BF16, top=True) for i in range(2)]

        def load_wml(oc):
            for n_ in range(4):
                c0 = O_MERGE + n_ * D + oc * 128
                dma("pool", wml[oc % 2][:, :, n_, :].par("ldw"), w_in[l, :, c0:c0 + 128].re("(c p) n -> p c n", p=128))
        wg = AR.alloc("w_gate", [KC, 1024], BF16)
        for cc in range(1, 8):
            dma("pool", wg[:, :, cc * 128:(cc + 1) * 128].par("ldw"),
                w_in[l, :, O_GATE + cc * 128:O_GATE + (cc + 1) * 128].re("(c p) n -> p c n", p=128))
        for n_ in range(4):
            dma("pool", wbr[:, n_].par("ldw"), w_branch[l, n_].re("(c p) n -> p c n", p=128))
        load_wml(0)
        load_wml(1)
        dma("pool", wo, w_out[l].re("(c p) n -> p c n", p=128))
        sl = [AR.alloc("sl%d" % i, [512], BF16) for i in range(2)]
        it = 0
        for cc in range(8):
            for (p0, n) in blocks:
                ps = kb.psum_f()
                for k in range(KC):
                    wsl = wst[:, k, 0:128] if cc == 0 else wg[:, k, cc * 128:(cc + 1) * 128]
                    mm(ps[:, 0:n], wsl, hT[:, k, p0:p0 + n], k == 0, k == KC - 1)
                s_ = sl[it % 2]
                it += 1
                act(s_[:, 0:n], ps[:, 0:n], AF.Silu)
                br = brT[cc // 2][cc % 2][:, p0:p0 + n]
                tt("dve", br, br, s_[:, 0:n], ALU.mult)
            if cc == 0 and nxt is not None:
                prefetch(nxt, PF_S5)
        AR.reset()
        sg = [AR.alloc("sg%d" % i, [512]) for i in range(2)]
        tm = [AR.alloc("tm%d" % i, [512]) for i in range(2)]
        acc = [AR.alloc("acc%d" % i, [512]) for i in range(2)]
        it = 0
        ib = 0
        for oc in range(8):
            wm = wml[oc % 2]
            if 1 <= oc and oc + 1 < 8:
                load_wml(oc + 1)
            for (p0, n) in blocks:
                ac = acc[ib % 2]
                ib += 1
                for n_ in range(4):
                    psA = kb.psum_f()
                    for k in range(2):
                        mm(psA[:, 0:n], wbr[:, n_, k, oc * 128:(oc + 1) * 128], brT[n_][k][:, p0:p0 + n], k == 0, k == 1)
                    psB = kb.psum_f()
                    for k in range(KC):
                        mm(psB[:, 0:n], wm[:, k, n_, :], hT[:, k, p0:p0 + n], k == 0, k == KC - 1)
                    s_ = sg[it % 2]
                    t_ = tm[it % 2]
                    it += 1
                    act(s_[:, 0:n], psB[:, 0:n], AF.Sigmoid)
                    if n_ == 0:
                        tt("dve", ac[:, 0:n], psA[:, 0:n], s_[:, 0:n], ALU.mult)
                    elif n_ < 3:
                        tt("dve", t_[:, 0:n], psA[:, 0:n], s_[:, 0:n], ALU.mult)
                        tt("dve", ac[:, 0:n], ac[:, 0:n], t_[:, 0:n], ALU.add)
                    else:
                        tt("dve", t_[:, 0:n], psA[:, 0:n], s_[:, 0:n], ALU.mult)
                        tt("dve", ypT[:, oc, p0:p0 + n].par("ypt"), ac[:, 0:n], t_[:, 0:n], ALU.add)
        AR.reset()
        gbc = AR.alloc("gbc", [2, D])
        dma("sp", gbc[:, 0, :], modD[l, b, 2 * D:3 * D].pbc(128))
        if with_ctx:
            dma("sp", gbc[:, 1, :], modD[l, 2, 2 * D:3 * D].pbc(128))
        xt = [AR.alloc("xt%d" % i, [D]) for i in range(2)]
        yt = [AR.alloc("yt%d" % i, [D]) for i in range(2)]
        for i in range(0 if with_ctx else 2, 18):
            x_t, y_t = xt[i % 2], yt[i % 2]
            dma("sp", x_t, src_tile(l, b, i))
            for hf in range(2):
                ps = kb.psum_f()
                for k in range(KC):
                    mm(ps, ypT[:, k, i * 128:(i + 1) * 128], wo[:, k, hf * 512:(hf + 1) * 512], k == 0, k == KC - 1)
                tt("dve", y_t[:, hf * 512:(hf + 1) * 512], ps, gbc[:, 1 if i < 2 else 0, hf * 512:(hf + 1) * 512], ALU.mult)
            tt("pool", y_t, y_t, x_t, ALU.add)
            if i < 2:
                dst = (tap("xc1", [nb, TC, D]) if "x1out" in dbg else xcmid)[b, i * 128:(i + 1) * 128, :]
            else:
                dst = (xmid if (l < DEPTH - 1 and "x1out" not in dbg) else y_out)[b, (i - 2) * 128:(i - 1) * 128, :]
            dma("act", dst, y_t)

    def dump_br(name, n, lo):
        o = tap(name, [2, 128, NPOS])
        tmpf = AR.alloc("dump_" + name, [NPOS])
        for c in range(2):
            cp("dve", tmpf[:, lo:NPOS], brT[n][c][:, lo:NPOS])
            dma("sp", o[c, :, lo:NPOS], tmpf[:, lo:NPOS])

    prefetch(layers[0], PF_S5)
    for l in layers:
        with_ctx = l < DEPTH - 1
        prep_layer(l)
        for b in range(nb):
            if "prep_only" in dbg:
                continue
            phase_norm(l, b)
            if "hT" in dbg and b == 0 and l == layers[0]:
                o = tap("hT", [KC, 128, NPOS])
                AR.reset()
                tmpf = AR.alloc("dump_hT", [NPOS])
                for c in range(KC):
                    cp("dve", tmpf, hT[:, c, :])
                    dma("sp", o[c], tmpf)
            only = dbg & {"only_lru", "only_pool", "only_mla", "only_s5", "only_norm"}
            if not only or "only_s5" in only:
                phase_s5(l, b, with_ctx)
            if not only or "only_lru" in only:
                phase_lru(l, b, with_ctx)
            if not only or "only_pool" in only:
                phase_pool(l, b, with_ctx)
            if not only or "only_mla" in only:
                phase_mla(l, b, with_ctx)
            if "br" in dbg and b == 0 and l == layers[0]:
                AR.reset()
                lo = 0 if with_ctx else TC
                for n_, nm in enumerate(("mla", "s5", "lru", "pool")):
                    dump_br(nm, n_, lo)
            if "nomerge" not in dbg:
                if b + 1 < nb:
                    nxt = l
                else:
                    li = list(layers).index(l)
                    nxt = layers[li + 1] if li + 1 < len(layers) else None
                phase_merge(l, b, with_ctx, nxt)
    P.barrier()
    P.emit()
    kb.st.close()
    return nc, kb


def _consts():
    ident = np.eye(128, dtype=np.float32)
    rows_n = T // 64
    row = np.repeat(np.arange(rows_n), 64).astype(np.float32)
    col = np.tile(np.arange(64), rows_n).astype(np.float32)
    nf = 8
    inv = (np.float32(10000.0) ** (-np.arange(nf, dtype=np.float32) / nf)).astype(np.float32)
    ar = (row[:, None] * inv).astype(np.float32)
    ac = (col[:, None] * inv).astype(np.float32)
    cr, sr, cc, sc = np.cos(ar), np.sin(ar), np.cos(ac), np.sin(ac)
    ropeC = np.concatenate([cr, cr, cc, cc], axis=1).astype(np.float32)
    ropeS = np.concatenate([-sr, sr, -sc, sc], axis=1).astype(np.float32)
    cpool = np.ones((128, 34), np.float32)
    for c in range(2):
        for p in range(128):
            w = (2, 4, 8, 16)[2 * c + p // 64]
            hw = w // 2
            cpool[p, c] = 1.0 / w
            for t in range(8):
                cnt = t + hw if t < hw else w
                cpool[p, 2 + c * 8 + t] = w / cnt
            for j in range(8):
                dist = 8 - j
                cnt = dist + hw if dist < hw else w
                cpool[p, 18 + c * 8 + j] = w / cnt
    m = np.zeros((128, 3, 128), np.float32)
    for r in range(8):
        for t in range(8):
            if r <= t:
                m[r * 16:(r + 1) * 16, 0, t * 16:(t + 1) * 16] = 1.0
            if r >= t:
                m[r * 16:(r + 1) * 16, 1, t * 16:(t + 1) * 16] = 1.0
            if r == t:
                m[r * 16:(r + 1) * 16, 2, t * 16:(t + 1) * 16] = np.eye(16, dtype=np.float32)
    return {"c_ident": ident, "c_ropeC": ropeC, "c_ropeS": ropeS, "c_pool": cpool, "c_mask": m}


_WNAMES = ["w_ada", "b_ada", "norm_g", "w_in", "mla_q_norm", "mla_kv_norm", "mla_w_uq", "mla_w_ukv", "mla_q_gain",
           "mla_k_gain", "s5_a_re", "s5_a_im", "s5_log_dt", "s5_b_re", "s5_b_im", "s5_c_re", "s5_c_im", "s5_d",
           "s5_w_glu", "lru_conv_w", "lru_conv_b", "lru_lambda", "lru_w_a", "lru_b_a", "lru_w_x", "lru_b_x",
           "pool_w", "pool_b", "pool_scale", "w_branch", "w_out"]


def make_in_maps(inputs, n_cores, nb):
    consts = _consts()
    maps = []
    for r in range(n_cores):
        bs = slice(r * nb, (r + 1) * nb)
        m = {"x": np.ascontiguousarray(inputs["x"][bs], dtype=np.float32),
             "ctx": np.ascontiguousarray(inputs["ctx"][bs], dtype=np.float32)}
        cv = np.zeros((3, D), np.float32)
        cv[0:nb] = np.asarray(inputs["c"], dtype=np.float32)[bs]
        cv[2] = np.asarray(inputs["c_ctx"], dtype=np.float32)
        m["cvec"] = cv
        for k in _WNAMES:
            m[k] = np.ascontiguousarray(inputs[k], dtype=np.float32)
        m.update(consts)
        maps.append(m)
    return maps


_CACHE = {}


def kernel(**inputs):
    n_cores, nb = 8, 2
    if "nc" not in _CACHE:
        _CACHE["nc"] = build_program(nb=nb)[0]
    nc = _CACHE["nc"]
    maps = make_in_maps(inputs, n_cores, nb)
    res = run_bass_kernel_spmd(nc, maps, core_ids=list(range(n_cores)))
    out = np.concatenate([np.asarray(r["y"], dtype=np.float32) for r in res.results], axis=0)
    return out
```

```python
import math
import contextlib
import numpy as np
import concourse.bass as bass
import concourse.mybir as mybir
from concourse.bass_utils import run_bass_kernel_spmd

F32 = mybir.dt.float32
BF16 = mybir.dt.bfloat16
I32 = mybir.dt.int32
AF = mybir.ActivationFunctionType
ALU = mybir.AluOpType
AX = mybir.AxisListType

ENGS = ("pe", "act", "dve", "pool", "sp")
NDMA = 24

D = 1024
KC = 8
T = 2048
TC = 256
NPOS = T + TC
DEPTH = 2
O_KROPE, O_S5, O_LRU, O_CQ, O_POOL, O_GATE, O_MERGE, IN_W = 128, 160, 416, 672, 928, 1184, 2208, 6304
EPS = 1e-6
NCH = NPOS // 8


PAR_OFF = set()


class Buf:
    __slots__ = ("name", "w", "wp", "r")

    def __init__(self, name):
        self.name = name
        self.w = []
        self.wp = []
        self.r = []


class TV:
    par_ = False

    def __init__(self, ap, buf):
        self.ap, self.buf = ap, buf

    def par(self, tag=""):
        t = TV(self.ap, self.buf)
        t.par_ = tag not in PAR_OFF
        return t

    def __getitem__(self, k):
        return TV(self.ap[k], self.buf)

    def re(self, s, **kw):
        return TV(self.ap.rearrange(s, **kw), self.buf)

    def bc(self, shape):
        return TV(self.ap.to_broadcast(list(shape)), self.buf)

    def us(self, ax):
        return TV(self.ap.unsqueeze(ax), self.buf)

    def pbc(self, n):
        return TV(self.ap.partition_broadcast(n), self.buf)

    @property
    def shape(self):
        return tuple(self.ap.shape)


def _bufs(*xs):
    out = []
    for x in xs:
        if isinstance(x, TV):
            out.append(x.buf)
        elif isinstance(x, (list, tuple)):
            out.extend(_bufs(*x))
    return out


def _ap(x):
    return x.ap if isinstance(x, TV) else x


class Prog:
    def __init__(self, nc):
        self.nc = nc
        self.ops = {e: [] for e in ENGS}
        self.cnt = {e: 0 for e in ENGS}
        self.waited = {e: {} for e in ENGS}
        self.dma_i = 0
        self.dma_last = [0] * NDMA

    def _deps(self, eng, reads, writes, par=False):
        toks = []
        for b in reads:
            toks.extend(b.w)
            toks.extend(b.wp)
        for b in writes:
            toks.extend(b.w)
            if not par:
                toks.extend(b.wp)
            toks.extend(b.r)
        need = {}
        for (k, v, e) in toks:
            if e == "pe" and eng == "pe":
                continue
            if self.waited[eng].get(k, 0) >= v:
                continue
            if need.get(k, 0) < v:
                need[k] = v
        for k, v in need.items():
            self.waited[eng][k] = v
        return list(need.items())

    def _mark(self, tok, reads, writes, par=False):
        for b in writes:
            if par:
                best = {}
                for (k, v, e) in b.wp + [tok]:
                    if k not in best or best[k][1] < v:
                        best[k] = (k, v, e)
                b.wp = list(best.values())
            else:
                b.w = [tok]
                b.wp = []
                b.r = []
        for b in reads:
            if b in writes:
                continue
            b.r.append(tok)
            if len(b.r) > 16:
                best = {}
                for (k, v, e) in b.r:
                    if k not in best or best[k][1] < v:
                        best[k] = (k, v, e)
                b.r = list(best.values())

    def op(self, eng, fn, reads=(), writes=(), par=False):
        reads = list(dict.fromkeys(reads))
        writes = list(dict.fromkeys(writes))
        waits = self._deps(eng, reads, writes, par)
        self.cnt[eng] += 1
        tok = (eng, self.cnt[eng], eng)
        self.ops[eng].append((waits, fn, (eng, 1)))
        self._mark(tok, reads, writes, par)

    def dma(self, eng, fn, reads=(), writes=(), par=False):
        reads = list(dict.fromkeys(reads))
        writes = list(dict.fromkeys(writes))
        waits = self._deps(eng, reads, writes, par)
        slot = self.dma_i % NDMA
        self.dma_i += 1
        key = ("dma", slot)
        prev = self.dma_last[slot]
        if prev and self.waited[eng].get(key, 0) < prev:
            waits.append((key, prev))
            self.waited[eng][key] = prev
        val = prev + 16
        self.dma_last[slot] = val
        tok = (key, val, "dma")
        self.ops[eng].append((waits, fn, (key, 16)))
        self._mark(tok, reads, writes, par)

    def barrier(self):
        for E in ENGS:
            waits = []
            for e in ENGS:
                v = self.cnt[e]
                if v and self.waited[E].get(e, 0) < v:
                    waits.append((e, v))
                    self.waited[E][e] = v
            for slot in range(NDMA):
                v = self.dma_last[slot]
                key = ("dma", slot)
                if v and self.waited[E].get(key, 0) < v:
                    waits.append((key, v))
                    self.waited[E][key] = v
            if waits:
                self.ops[E].append((waits, None, None))

    def emit(self):
        nc = self.nc
        sems = {}
        with contextlib.ExitStack() as st:
            for e in ENGS:
                sems[e] = st.enter_context(nc.semaphore("s_" + e))
            for i in range(NDMA):
                sems[("dma", i)] = st.enter_context(nc.semaphore("s_dma%d" % i))
            block = st.enter_context(nc.Block())

            def run(engname):
                def body(eng):
                    for waits, fn, inc in self.ops[engname]:
                        for k, v in waits:
                            eng.wait_ge(sems[k], v)
                        if fn is None:
                            continue
                        ins = fn(eng)
                        ins.then_inc(sems[inc[0]], inc[1])
                return body
            block.tensor(run("pe"))
            block.scalar(run("act"))
            block.vector(run("dve"))
            block.gpsimd(run("pool"))
            block.sync(run("sp"))


class KB:
    def __init__(self, nc, nb, layers, dbg):
        self.nc = nc
        self.P = Prog(nc)
        self.nb = nb
        self.layers = layers
        self.dbg = dbg
        self.st = contextlib.ExitStack()
        self.din = {}
        self.dout = {}
        self.ps_i = 0
        self.psb_i = 0
        self.ps8_i = 0

    def tt(self, eng, out, a, b, op):
        self.P.op(eng, lambda e: e.tensor_tensor(out=out.ap, in0=a.ap, in1=b.ap, op=op), _bufs(a, b), _bufs(out), par=out.par_)

    def ts(self, eng, out, a, s1, op0, s2=None, op1=None):
        if op1 is None:
            self.P.op(eng, lambda e: e.tensor_scalar(out=out.ap, in0=a.ap, scalar1=_ap(s1), scalar2=None, op0=op0),
                      _bufs(a, s1), _bufs(out), par=out.par_)
        else:
            self.P.op(eng, lambda e: e.tensor_scalar(out=out.ap, in0=a.ap, scalar1=_ap(s1), scalar2=_ap(s2), op0=op0, op1=op1),
                      _bufs(a, s1, s2), _bufs(out), par=out.par_)

    def stt(self, out, a, s, b, op0, op1):
        self.P.op("dve", lambda e: e.scalar_tensor_tensor(out=out.ap, in0=a.ap, scalar=_ap(s), in1=b.ap, op0=op0, op1=op1),
                  _bufs(a, s, b), _bufs(out))

    def act(self, out, a, func, bias=0.0, scale=1.0, accum=None):
        if accum is None:
            self.P.op("act", lambda e: e.activation(out=out.ap, in_=a.ap, func=func, bias=_ap(bias), scale=_ap(scale)),
                      _bufs(a, bias, scale), _bufs(out), par=out.par_)
        else:
            self.P.op("act", lambda e: e.activation(out=out.ap, in_=a.ap, func=func, bias=_ap(bias), scale=_ap(scale), accum_out=accum.ap),
                      _bufs(a, bias, scale), _bufs(out, accum))

    def cp(self, eng, out, a):
        if eng == "act":
            self.P.op("act", lambda e: e.copy(out=out.ap, in_=a.ap), _bufs(a), _bufs(out), par=out.par_)
        else:
            self.P.op(eng, lambda e: e.tensor_copy(out=out.ap, in_=a.ap), _bufs(a), _bufs(out), par=out.par_)

    def memset(self, eng, out, v):
        self.P.op(eng, lambda e: e.memset(out.ap, v), [], _bufs(out), par=out.par_)

    def recip(self, out, a):
        self.P.op("dve", lambda e: e.reciprocal(out=out.ap, in_=a.ap), _bufs(a), _bufs(out))

    def mm(self, out, lhsT, rhs, start, stop):
        self.P.op("pe", lambda e: e.matmul(out.ap, lhsT=lhsT.ap, rhs=rhs.ap, start=start, stop=stop), _bufs(lhsT, rhs), _bufs(out))

    def tr(self, out, a, ident):
        self.P.op("pe", lambda e: e.transpose(out.ap, a.ap, ident.ap), _bufs(a, ident), _bufs(out))

    def scan(self, out, d0, d1, init):
        self.P.op("dve", lambda e: e.tensor_tensor_scan(out=out.ap, data0=d0.ap, data1=d1.ap, initial=_ap(init), op0=ALU.mult, op1=ALU.add),
                  _bufs(d0, d1, init), _bufs(out))

    def reduce(self, out, a, op=ALU.add):
        self.P.op("dve", lambda e: e.tensor_reduce(out=out.ap, in_=a.ap, axis=AX.X, op=op), _bufs(a), _bufs(out))

    def dma(self, q, out, a, slow=False):
        if slow:
            self.P.dma(q, lambda e: e.dma_start(out=out.ap, in_=a.ap, allow_slow_non_contiguous=True), _bufs(a), _bufs(out), par=out.par_)
        else:
            self.P.dma(q, lambda e: e.dma_start(out=out.ap, in_=a.ap), _bufs(a), _bufs(out), par=out.par_)

    def dram_in(self, name, shape, dt=F32):
        t = TV(self.nc.dram_tensor(name, list(shape), dt, kind="ExternalInput").ap(), Buf(name))
        self.din[name] = t
        return t

    def dram_out(self, name, shape, dt=F32):
        t = TV(self.nc.dram_tensor(name, list(shape), dt, kind="ExternalOutput").ap(), Buf(name))
        self.dout[name] = t
        return t

    def dram_tmp(self, name, shape, dt=F32):
        return TV(self.nc.dram_tensor(name, list(shape), dt, kind="Internal").ap(), Buf(name))

    def sb(self, name, shape, dt=F32):
        t = self.st.enter_context(self.nc.sbuf_tensor(name, list(shape), dt))
        return TV(t[:], Buf(name))

    def psum_f(self):
        i = self.ps_i % 6
        self.ps_i += 1
        return self.psf[i]

    def psum_b(self):
        i = self.psb_i % 2
        self.psb_i += 1
        return self.psb[i]

    def psum_any(self, bf16=False):
        i = self.ps8_i % 8
        self.ps8_i += 1
        return self.ps8b[i] if bf16 else self.ps8[i]


class Arena:
    def __init__(self, kb, words, ap=None):
        self.kb = kb
        self.words = words
        if ap is None:
            self.f = kb.st.enter_context(kb.nc.sbuf_tensor("arena_f", [128, words], F32))
        else:
            self.f = ap
        self.lo = 0
        self.hi = words

    def alloc(self, name, free, dt=F32, top=False, buf=None, at=None):
        nel = int(np.prod(free))
        w = nel if dt == F32 else (nel + 1) // 2
        w = ((w + 15) // 16) * 16
        if at is not None:
            off = at
        elif top:
            self.hi -= w
            off = self.hi
        else:
            off = self.lo
            self.lo += w
        assert self.lo <= self.hi, (name, self.lo, self.hi)
        assert off + w <= self.words
        if dt != F32:
            ap = self.f[:, off:off + w].bitcast(dt)[:, 0:nel]
        else:
            ap = self.f[:, off:off + nel]
        if len(free) > 1:
            names = " ".join("d%d" % i for i in range(len(free)))
            kw = {"d%d" % i: int(free[i]) for i in range(len(free))}
            ap = ap.rearrange("p (%s) -> p %s" % (names, names), **kw)
        tv = TV(ap, buf if buf is not None else Buf(name))
        tv.off = off
        tv.w = w
        return tv

    def reset(self, top=False):
        self.kb.P.barrier()
        self.lo = 0
        if top:
            self.hi = self.words


def lockstep(gens):
    gens = list(gens)
    while gens:
        nxt = []
        for g in gens:
            try:
                next(g)
                nxt.append(g)
            except StopIteration:
                pass
        gens = nxt


def build_program(nb=2, layers=(0, 1), dbg=None):
    nc = bass.Bass("TRN2", target_bir_lowering=False)
    kb = KB(nc, nb, layers, dbg)
    P = kb.P
    tt, ts, stt, act, cp, mm, tr, dma = kb.tt, kb.ts, kb.stt, kb.act, kb.cp, kb.mm, kb.tr, kb.dma
    dbg = dbg or set()
    PAR_OFF.clear()
    PAR_OFF.update(x[6:] for x in dbg if x.startswith("nopar_"))

    x_in = kb.dram_in("x", [nb, T, D])
    ctx_in = kb.dram_in("ctx", [nb, TC, D])
    cvec = kb.dram_in("cvec", [3, D])
    w_ada = kb.dram_in("w_ada", [DEPTH, D, 3 * D])
    b_ada = kb.dram_in("b_ada", [DEPTH, 3 * D])
    norm_g = kb.dram_in("norm_g", [DEPTH, D])
    w_in = kb.dram_in("w_in", [DEPTH, D, IN_W])
    mla_q_norm = kb.dram_in("mla_q_norm", [DEPTH, 256])
    mla_kv_norm = kb.dram_in("mla_kv_norm", [DEPTH, 128])
    mla_w_uq = kb.dram_in("mla_w_uq", [DEPTH, 256, 384])
    mla_w_ukv = kb.dram_in("mla_w_ukv", [DEPTH, 128, 512])
    mla_q_gain = kb.dram_in("mla_q_gain", [DEPTH, 96])
    mla_k_gain = kb.dram_in("mla_k_gain", [DEPTH, 96])
    s5_a_re = kb.dram_in("s5_a_re", [DEPTH, 2, 16, 64])
    s5_a_im = kb.dram_in("s5_a_im", [DEPTH, 2, 16, 64])
    s5_log_dt = kb.dram_in("s5_log_dt", [DEPTH, 2, 16])
    s5_b_re = kb.dram_in("s5_b_re", [DEPTH, 2, 16, 64, 16])
    s5_b_im = kb.dram_in("s5_b_im", [DEPTH, 2, 16, 64, 16])
    s5_c_re = kb.dram_in("s5_c_re", [DEPTH, 2, 16, 16, 64])
    s5_c_im = kb.dram_in("s5_c_im", [DEPTH, 2, 16, 16, 64])
    s5_d = kb.dram_in("s5_d", [DEPTH, 256])
    s5_w_glu = kb.dram_in("s5_w_glu", [DEPTH, 256, 256])
    lru_conv_w = kb.dram_in("lru_conv_w", [DEPTH, 4, 256])
    lru_conv_b = kb.dram_in("lru_conv_b", [DEPTH, 256])
    lru_lambda = kb.dram_in("lru_lambda", [DEPTH, 2, 256])
    lru_w_a = kb.dram_in("lru_w_a", [DEPTH, 2, 4, 64, 64])
    lru_b_a = kb.dram_in("lru_b_a", [DEPTH, 2, 256])
    lru_w_x = kb.dram_in("lru_w_x", [DEPTH, 2, 4, 64, 64])
    lru_b_x = kb.dram_in("lru_b_x", [DEPTH, 2, 256])
    pool_w = kb.dram_in("pool_w", [DEPTH, 4, 64, 64])
    pool_b = kb.dram_in("pool_b", [DEPTH, 256])
    pool_scale = kb.dram_in("pool_scale", [DEPTH, 256])
    w_branch = kb.dram_in("w_branch", [DEPTH, 4, 256, D])
    w_out = kb.dram_in("w_out", [DEPTH, D, D])
    c_ident = kb.dram_in("c_ident", [128, 128])
    c_ropeC = kb.dram_in("c_ropeC", [T, 32])
    c_ropeS = kb.dram_in("c_ropeS", [T, 32])
    c_pool = kb.dram_in("c_pool", [128, 2 + 16 + 16])
    c_mask = kb.dram_in("c_mask", [128, 3, 128])
    y_out = kb.dram_out("y", [nb, T, D])
    xmid = kb.dram_tmp("xmid", [nb, T, D])
    xcmid = kb.dram_tmp("xcmid", [nb, TC, D])
    modD = kb.dram_tmp("modD", [DEPTH, 3, 3 * D])
    st_main = kb.dram_tmp("st_main", [3, 128, 2048])
    st_dir = kb.dram_tmp("st_dir", [2, 128, 2048 + 16 * NCH + 8])
    dbg_out = {}

    def tap(name, shape):
        if name not in dbg_out:
            dbg_out[name] = kb.dram_out("dbg_" + name, shape)
        return dbg_out[name]

    kb.ps8 = []
    for i in range(8):
        t = kb.st.enter_context(nc.psum_tensor("psf%d" % i, [128, 512], F32))
        kb.ps8.append(TV(t[:], Buf("psf%d" % i)))
    kb.psf = kb.ps8[0:6]
    kb.psb = [TV(kb.ps8[i].ap.bitcast(BF16), kb.ps8[i].buf) for i in (6, 7)]
    kb.ps8b = [TV(kb.ps8[i].ap.bitcast(BF16), kb.ps8[i].buf) for i in range(8)]

    hT = kb.sb("hT", [128, KC, NPOS], BF16)
    bigbr = kb.sb("bigbr", [128, 8 * NPOS], BF16)
    _slot = {0: 0, 2: 2, 3: 4, 1: 6}
    brT = [[TV(bigbr.ap[:, (_slot[n] + c) * NPOS:(_slot[n] + c + 1) * NPOS], Buf("brT%d%d" % (n, c))) for c in range(2)]
           for n in range(4)]
    identf = kb.sb("identf", [128, 128])
    identb = kb.sb("identb", [128, 128], BF16)
    ropeC = kb.sb("ropeC", [128, 16, 32])
    ropeS = kb.sb("ropeS", [128, 16, 32])
    cpool = kb.sb("cpool", [128, 34])
    cmask = kb.sb("cmask", [128, 3, 128])
    modA = kb.sb("modA", [128, 3, KC])
    modS = kb.sb("modS", [128, 3, KC])
    lp = kb.sb("lp", [128, 64])
    lruBD = kb.sb("lruBD", [128, 2, 2, 2, 128])
    poolBD = kb.sb("poolBD", [128, 2, 128])
    wukv = kb.sb("wukv", [128, 512], BF16)
    wuq = kb.sb("wuq", [128, 2, 384], BF16)
    gains = kb.sb("gains", [128, 2, 96])
    wglu = kb.sb("wglu", [128, 2, 256], BF16)
    AR = Arena(kb, 28000)
    wst = kb.sb("wst", [128, KC, 416], BF16)

    def prefetch(l, parts):
        for (d0, c0, n) in parts:
            dma("pool", wst[:, :, d0:d0 + n].par("wst"), w_in[l, :, c0:c0 + n].re("(c p) n -> p c n", p=128))
    PF_S5 = [(0, O_S5, 256)]
    PF_LRU = [(0, O_LRU, 256)]
    PF_POOL = [(0, O_POOL, 256)]
    PF_MLA = [(0, 0, 160), (160, O_CQ, 256)]
    PF_GATE0 = [(0, O_GATE, 128)]
    AR2 = Arena(kb, 3 * NPOS, ap=bigbr.ap[:, 0:6 * NPOS].bitcast(F32))

    dma("sp", identf, c_ident)
    cp("dve", identb, identf)
    dma("sp", ropeC, c_ropeC.re("(i p) f -> p i f", p=128))
    dma("sp", ropeS, c_ropeS.re("(i p) f -> p i f", p=128))
    dma("sp", cpool, c_pool)
    dma("sp", cmask, c_mask)

    LP_CW = 0
    LP_CB = 8
    LP_NSP = 10
    LP_NSP2 = 14
    LP_BA = 18
    LP_BX = 22
    LP_PSC = 26
    LP_PBS = 28
    LP_G = 30
    LP_TMP = 40

    def prep_layer(l):
        AR.reset(top=True)
        cT = AR.alloc("cT", [KC, 3])
        for v in range(3):
            dma("sp", cT[:, :, v], cvec[v].re("(c p) -> p c", p=128), slow=True)
        cact = AR.alloc("cact", [KC, 3])
        act(cact, cT, AF.Silu)
        brow = AR.alloc("brow", [3 * D])
        dma("sp", brow[0:3, :], b_ada[l:l + 1, :].re("o n -> (o n)").pbc(3))
        modrow = AR.alloc("modrow", [3 * D])
        wa = [AR.alloc("wa%d" % i, [KC, 512]) for i in range(4)]
        for cb in range(4):
            dma("sp" if cb % 2 == 0 else "act", wa[cb], w_ada[l, :, cb * 512:(cb + 1) * 512].re("(c p) n -> p c n", p=128))
        for cb in range(6):
            w = wa[cb % 4]
            if cb >= 4:
                dma("sp" if cb % 2 == 0 else "act", w, w_ada[l, :, cb * 512:(cb + 1) * 512].re("(c p) n -> p c n", p=128))
            ps = kb.psum_f()
            for k in range(KC):
                mm(ps[0:3, :], cact[:, k, :], w[:, k, :], k == 0, k == KC - 1)
            tt("dve", modrow[0:3, cb * 512:(cb + 1) * 512], ps[0:3, :], brow[0:3, cb * 512:(cb + 1) * 512], ALU.add)
        dma("sp", modD[l], modrow[0:3, :])
        sc = AR.alloc("sc", [3, KC])
        for v in range(3):
            dma("sp", modS[:, v, :], modD[l, v, 0:D].re("(c p) -> p c", p=128), slow=True)
            dma("sp", sc[:, v, :], modD[l, v, D:2 * D].re("(c p) -> p c", p=128), slow=True)
        dma("sp", lp[:, LP_G:LP_G + 8], norm_g[l].re("(c p) -> p c", p=128), slow=True)
        for v in range(3):
            stt(modA[:, v, :], sc[:, v, :], 1.0, lp[:, LP_G:LP_G + 8], ALU.add, ALU.mult)
        for k in range(4):
            dma("sp", lp[:, LP_CW:LP_CW + 8].re("p (c k) -> p c k", c=2)[:, :, k], lru_conv_w[l, k].re("(c p) -> p c", p=128), slow=True)
        dma("sp", lp[:, LP_CB:LP_CB + 2], lru_conv_b[l].re("(c p) -> p c", p=128), slow=True)
        lam = lp[:, LP_TMP:LP_TMP + 4]
        for d in range(2):
            dma("sp", lam[:, d * 2:d * 2 + 2], lru_lambda[l, d].re("(c p) -> p c", p=128), slow=True)
            dma("sp", lp[:, LP_BA + d * 2:LP_BA + d * 2 + 2], lru_b_a[l, d].re("(c p) -> p c", p=128), slow=True)
            dma("sp", lp[:, LP_BX + d * 2:LP_BX + d * 2 + 2], lru_b_x[l, d].re("(c p) -> p c", p=128), slow=True)
        t0 = lp[:, LP_TMP + 4:LP_TMP + 8]
        t1 = lp[:, LP_TMP + 8:LP_TMP + 12]
        t2 = lp[:, LP_TMP + 12:LP_TMP + 16]
        t3 = lp[:, LP_TMP + 16:LP_TMP + 20]
        ts("dve", t0, lam, -1.0, ALU.mult)
        tt("dve", t0, t0, lam, ALU.max)
        act(t1, t0, AF.Exp, scale=-1.0)
        ts("dve", t2, t1, 2.0, ALU.add)
        kb.recip(t2, t2)
        tt("dve", t2, t2, t1, ALU.mult)
        tt("dve", t3, t2, t2, ALU.mult)
        ts("dve", t0, t3, 1.0 / 11.0, ALU.mult, 1.0 / 9.0, ALU.add)
        for cf in (1.0 / 7.0, 1.0 / 5.0, 1.0 / 3.0, 1.0):
            tt("dve", t0, t0, t3, ALU.mult)
            ts("dve", t0, t0, cf, ALU.add)
        tt("dve", t0, t0, t2, ALU.mult)
        ts("dve", t1, lam, -1.0, ALU.mult, 0.0, ALU.max)
        stt(t0, t0, 2.0, t1, ALU.mult, ALU.add)
        ts("dve", lp[:, LP_NSP:LP_NSP + 4], t0, -8.0, ALU.mult)
        ts("dve", lp[:, LP_NSP2:LP_NSP2 + 4], t0, -16.0, ALU.mult)
        kb.memset("dve", lruBD, 0.0)
        for d in range(2):
            for gi, wsrc in enumerate((lru_w_a, lru_w_x)):
                for c in range(2):
                    for h in range(2):
                        dma("sp" if (c + h) % 2 == 0 else "act", lruBD[h * 64:(h + 1) * 64, d, gi, c, h * 64:(h + 1) * 64].par("ldw"), wsrc[l, d, 2 * c + h])
        kb.memset("dve", poolBD, 0.0)
        for c in range(2):
            for h in range(2):
                dma("sp", poolBD[h * 64:(h + 1) * 64, c, h * 64:(h + 1) * 64].par("ldw"), pool_w[l, 2 * c + h])
        dma("sp", lp[:, LP_PSC:LP_PSC + 2], pool_scale[l].re("(c p) -> p c", p=128), slow=True)
        dma("sp", lp[:, LP_PBS:LP_PBS + 2], pool_b[l].re("(c p) -> p c", p=128), slow=True)
        tt("dve", lp[:, LP_PBS:LP_PBS + 2], lp[:, LP_PBS:LP_PBS + 2], lp[:, LP_PSC:LP_PSC + 2], ALU.mult)
        kvn = lp[:, LP_TMP + 20:LP_TMP + 21]
        qn = lp[:, LP_TMP + 21:LP_TMP + 23]
        dma("sp", kvn, mla_kv_norm[l].re("(p o) -> p o", o=1), slow=True)
        dma("sp", qn, mla_q_norm[l].re("(c p) -> p c", p=128), slow=True)
        wtmp = AR.alloc("wtmp", [2, 512])
        dma("sp", wtmp[:, 0, :], mla_w_ukv[l])
        ts("dve", wukv, wtmp[:, 0, :], kvn, ALU.mult)
        wtmp2 = AR.alloc("wtmp2", [2, 384])
        dma("sp", wtmp2, mla_w_uq[l].re("(c p) n -> p c n", p=128))
        for c in range(2):
            ts("dve", wuq[:, c, :], wtmp2[:, c, :], qn[:, c:c + 1], ALU.mult)
        dma("sp", gains[:, 0, :], mla_q_gain[l:l + 1, :].re("o n -> (o n)").pbc(128))
        dma("sp", gains[:, 1, :], mla_k_gain[l:l + 1, :].re("o n -> (o n)").pbc(128))
        ts("dve", gains[:, 0, :], gains[:, 0, :], 96.0 ** -0.5, ALU.mult)
        dma("pool", wglu, s5_w_glu[l].re("(c p) n -> p c n", p=128))

    def src_tile(l, b, i):
        if i < 2:
            return (ctx_in if l == 0 else xcmid)[b, i * 128:(i + 1) * 128, :]
        return (x_in if l == 0 else xmid)[b, (i - 2) * 128:(i - 1) * 128, :]

    def phase_norm(l, b):
        AR.reset(top=True)
        NX = 4
        xt = [AR.alloc("xt%d" % i, [D]) for i in range(NX)]
        junk = [AR.alloc("junk%d" % i, [D]) for i in range(NX)]
        xn = [AR.alloc("xn%d" % i, [D], BF16) for i in range(NX)]
        st4 = [AR.alloc("st%d" % i, [4]) for i in range(NX)]
        def tile_gen(i):
            v = 2 if i < 2 else b
            x_t, xn_t, s4 = xt[i % NX], xn[i % NX], st4[i % NX]
            dma("sp" if i % 2 == 0 else "act", x_t, src_tile(l, b, i))
            yield
            act(junk[i % NX], x_t, AF.Square, accum=s4[:, 0:1])
            yield
            ts("dve", s4[:, 1:2], s4[:, 0:1], 1.0 / D, ALU.mult, EPS, ALU.add)
            yield
            act(s4[:, 2:3], s4[:, 1:2], AF.Sqrt)
            yield
            kb.recip(s4[:, 3:4], s4[:, 2:3])
            ts("dve", xn_t, x_t, s4[:, 3:4], ALU.mult)
            yield
            psA = kb.psum_any(bf16=True)
            psB = kb.psum_any(bf16=True)
            for c in range(KC):
                pz = psA if c % 2 == 0 else psB
                tr(pz[:, (c // 2) * 128:(c // 2 + 1) * 128], xn_t[:, c * 128:(c + 1) * 128], identb)
            yield
            for c in range(KC):
                o = hT[:, c, i * 128:(i + 1) * 128].par("norm")
                if c % 2 == 0:
                    act(o, psA[:, (c // 2) * 128:(c // 2 + 1) * 128], AF.Identity, bias=modS[:, v, c:c + 1], scale=modA[:, v, c:c + 1])
                else:
                    ts("dve", o, psB[:, (c // 2) * 128:(c // 2 + 1) * 128], modA[:, v, c:c + 1], ALU.mult, modS[:, v, c:c + 1], ALU.add)
            yield
        for g0 in range(0, 18, NX):
            lockstep([tile_gen(i) for i in range(g0, min(g0 + NX, 18))])

    def load_win(l, name, c0, ncols, top=False):
        w = AR.alloc(name, [KC, ncols], BF16, top=top)
        dma("pool", w, w_in[l, :, c0:c0 + ncols].re("(c p) n -> p c n", p=128))
        return w

    def proj_fm(w, col0, dst, p0, p1, evac):
        pos = p0
        while pos < p1:
            n = min(512, p1 - pos)
            ps = kb.psum_f()
            for k in range(KC):
                mm(ps[:, 0:n], w[:, k, col0:col0 + 128], hT[:, k, pos:pos + n], k == 0, k == KC - 1)
            evac(ps, pos, n)
            pos += n

    def phase_lru(l, b, with_ctx):
        AR.reset()
        w = wst[:, :, 0:256]
        xr = AR.alloc("xr", [2, NPOS])
        xc_ = AR.alloc("xc", [2, NPOS])
        ysum = AR.alloc("ysum", [2, NPOS])
        NB_ = 512
        tmp_sets = [{nm: AR.alloc("%s%d" % (nm, q), [NB_]) for nm in ("r", "i", "a", "a2", "bb")} for q in range(4)]
        for c in range(2):
            proj_fm(w, c * 128, xr, 0, NPOS, lambda ps, pos, n, c=c: cp("act", xr[:, c, pos:pos + n].par("proj"), ps[:, 0:n]))
        prefetch(l, PF_POOL)
        for c in range(2):
            cw = lambda k: lp[:, LP_CW + c * 4 + k:LP_CW + c * 4 + k + 1]
            for (s0, s1) in ((0, TC), (TC, NPOS)):
                ts("dve", xc_[:, c, s0:s1], xr[:, c, s0:s1], cw(2), ALU.mult, lp[:, LP_CB + c:LP_CB + c + 1], ALU.add)
                stt(xc_[:, c, s0 + 1:s1], xr[:, c, s0:s1 - 1], cw(1), xc_[:, c, s0 + 1:s1], ALU.mult, ALU.add)
                stt(xc_[:, c, s0 + 2:s1], xr[:, c, s0:s1 - 2], cw(0), xc_[:, c, s0 + 2:s1], ALU.mult, ALU.add)
                stt(xc_[:, c, s0:s1 - 1], xr[:, c, s0 + 1:s1], cw(3), xc_[:, c, s0:s1 - 1], ALU.mult, ALU.add)
        blocks = [(0, TC)] + [(TC + j * NB_, min(TC + (j + 1) * NB_, NPOS)) for j in range((T + NB_ - 1) // NB_)]

        def chain(d, c):
            dc = d * 2 + c
            tmp = tmp_sets[dc]
            out = ysum if d == 0 else xr
            order = blocks if d == 0 else [blocks[0]] + blocks[:0:-1]
            prev = None
            for (s0, s1) in order:
                n = s1 - s0
                r_, i_, a_, a2_, bb_ = (tmp[k][:, 0:n] for k in ("r", "i", "a", "a2", "bb"))
                for gi, dst in ((0, r_), (1, i_)):
                    q = 0
                    while q < n:
                        m = min(512, n - q)
                        ps = kb.psum_f()
                        mm(ps[:, 0:m], lruBD[:, d, gi, c, :], xc_[:, c, s0 + q:s0 + q + m], True, True)
                        bcol = (LP_BA if gi == 0 else LP_BX) + dc
                        act(dst[:, q:q + m], ps[:, 0:m], AF.Sigmoid, bias=lp[:, bcol:bcol + 1])
                        q += m
                yield
                act(a_, r_, AF.Exp, scale=lp[:, LP_NSP + dc:LP_NSP + dc + 1])
                act(a2_, r_, AF.Exp, scale=lp[:, LP_NSP2 + dc:LP_NSP2 + dc + 1])
                tt("pool", bb_, i_, xc_[:, c, s0:s1], ALU.mult)
                yield
                ts("dve", a2_, a2_, -1.0, ALU.mult, 1.0, ALU.add)
                yield
                act(a2_, a2_, AF.Sqrt)
                yield
                tt("dve", bb_, bb_, a2_, ALU.mult)
                init = 0.0 if prev is None else prev
                if d == 0:
                    kb.scan(out[:, c, s0:s1], a_, bb_, init)
                    prev = out[:, c, s1 - 1:s1]
                else:
                    kb.scan(out[:, c, s0:s1][:, ::-1], a_[:, ::-1], bb_[:, ::-1], init)
                    prev = out[:, c, s0:s0 + 1]
                yield
        lockstep([chain(0, 0), chain(1, 0), chain(0, 1), chain(1, 1)])
        for c in range(2):
            lo = 0 if with_ctx else TC
            tt("dve", brT[2][c][:, lo:NPOS], ysum[:, c, lo:NPOS], xr[:, c, lo:NPOS], ALU.add)

    AR_car = [kb.sb("car%d" % i, [128, 1]) for i in range(2)]

    def phase_pool(l, b, with_ctx):
        AR.reset()
        w = wst[:, :, 0:256]
        xp = AR.alloc("xp", [2, NPOS])
        cs = AR.alloc("cs", [2, NPOS + 2 * 17 + 2])
        pm = AR.alloc("pm", [2, NPOS])
        ones = AR.alloc("ones", [T])
        kb.memset("pool", ones, 1.0)
        for c in range(2):
            proj_fm(w, c * 128, xp, 0, NPOS, lambda ps, pos, n, c=c: cp("act", xp[:, c, pos:pos + n].par("proj"), ps[:, 0:n]))
        prefetch(l, PF_MLA)
        segs = [(0, TC, 0)] + [(TC, NPOS, TC + 17)]
        if not with_ctx:
            segs = segs[1:]
        for c in range(2):
            for (s0, s1, o0) in segs:
                L = s1 - s0
                kb.memset("pool", cs[:, c, o0:o0 + 9], 0.0)
                kb.scan(cs[:, c, o0 + 9:o0 + 9 + L], ones[:, 0:L], xp[:, c, s0:s1], 0.0)
                ts("dve", cs[:, c, o0 + 9 + L:o0 + 17 + L], cs[:, c, o0:o0 + 8], cs[:, c, o0 + 8 + L:o0 + 9 + L], ALU.add)
                for h in range(2):
                    hw = (1, 2, 4, 8)[2 * c + h]
                    rows = slice(h * 64, (h + 1) * 64)
                    base = o0 + 8
                    tt("dve", pm[rows, c, s0:s1], cs[rows, c, base + hw:base + hw + L], cs[rows, c, base - hw:base - hw + L], ALU.subtract)
                ts("dve", pm[:, c, s0:s1], pm[:, c, s0:s1], cpool[:, c:c + 1], ALU.mult)
                tt("dve", pm[:, c, s0:s0 + 8], pm[:, c, s0:s0 + 8], cpool[:, 2 + c * 8:2 + c * 8 + 8], ALU.mult)
                tt("dve", pm[:, c, s1 - 8:s1], pm[:, c, s1 - 8:s1], cpool[:, 18 + c * 8:18 + c * 8 + 8], ALU.mult)
                tt("pool", pm[:, c, s0:s1], pm[:, c, s0:s1], xp[:, c, s0:s1], ALU.subtract)
                pos = s0
                while pos < s1:
                    n = min(512, s1 - pos)
                    ps = kb.psum_f()
                    mm(ps[:, 0:n], poolBD[:, c, :], pm[:, c, pos:pos + n], True, True)
                    act(brT[3][c][:, pos:pos + n].par("pool"), ps[:, 0:n], AF.Identity, bias=lp[:, LP_PBS + c:LP_PBS + c + 1],
                        scale=lp[:, LP_PSC + c:LP_PSC + c + 1])
                    pos += n

    def phase_mla(l, b, with_ctx):
        AR.reset()
        wkv = wst[:, :, 0:160]
        wq = wst[:, :, 160:416]
        qT = AR.alloc("qT", [4, NPOS], BF16)
        kT = AR.alloc("kT", [4, NPOS], BF16)
        Vt = AR.alloc("Vt", [18, 4, 68], BF16)
        lo_fixed = AR.lo
        kb.memset("dve", Vt.re("p a b c -> p (a b c)"), 1.0)
        for h_ in range(4):
            kb.memset("pool", qT[:, h_, :], 0.0)
            kb.memset("pool", kT[:, h_, :], 0.0)
        NS = 3
        sm = [AR.alloc("sm%d" % i, [32]) for i in range(NS)]
        kr = [AR.alloc("kr%d" % i, [32]) for i in range(NS)]
        cn = [AR.alloc("cn%d" % i, [384], BF16) for i in range(NS)]
        cnT = [AR.alloc("cnT%d" % i, [3, 128], BF16) for i in range(NS)]
        sq = [AR.alloc("sq%d" % i, [704]) for i in range(NS)]
        qk = [AR.alloc("qk%d" % i, [2, 4, 96]) for i in range(NS)]
        rg = [AR.alloc("rg%d" % i, [2, 4, 96]) for i in range(NS)]
        qkb = [AR.alloc("qkb%d" % i, [2, 4, 96], BF16) for i in range(NS)]
        rt = [AR.alloc("rt%d" % i, [2, 2, 4, 32]) for i in range(NS)]
        kvs_ = [AR.alloc("kvs%d" % i, [512]) for i in range(NS)]
        qs_ = [AR.alloc("qs%d" % i, [384]) for i in range(NS)]

        def info(i):
            is_ctx = i < 2
            do_q = (not is_ctx) or with_ctx
            return is_ctx, do_q, i % NS, slice(i * 128, (i + 1) * 128)

        def stage_a(i):
            is_ctx, do_q, j, pos = info(i)
            s, cn_, cnT_, sq_ = sm[j], cn[j], cnT[j], sq[j]
            ps1 = kb.psum_any()
            for k in range(KC):
                mm(ps1[:, 0:160], hT[:, k, pos], wkv[:, k, :], k == 0, k == KC - 1)
            if do_q:
                for k in range(KC):
                    mm(ps1[:, 160:416], hT[:, k, pos], wq[:, k, :], k == 0, k == KC - 1)
            yield
            act(sq_[:, 0:128], ps1[:, 0:128], AF.Square, accum=s[:, 0:1])
            if do_q:
                act(sq_[:, 128:384], ps1[:, 160:416], AF.Square, accum=s[:, 1:2])
            else:
                kb.memset("pool", s[:, 1:2], 1.0)
            cp("act", kr[j], ps1[:, 128:160])
            yield
            act(s[:, 2:3], s[:, 0:1], AF.Sqrt, scale=1.0 / 128, bias=EPS)
            act(s[:, 3:4], s[:, 1:2], AF.Sqrt, scale=1.0 / 256, bias=EPS)
            yield
            kb.recip(s[:, 4:6], s[:, 2:4])
            ts("dve", cn_[:, 0:128], ps1[:, 0:128], s[:, 4:5], ALU.mult)
            if do_q:
                ts("dve", cn_[:, 128:384], ps1[:, 160:416], s[:, 5:6], ALU.mult)
            yield
            psT = kb.psum_any(bf16=True)
            for c in range(3 if do_q else 1):
                tr(psT[:, c * 128:(c + 1) * 128], cn_[:, c * 128:(c + 1) * 128], identb)
            yield
            cp("act", cnT_[:, 0:(3 if do_q else 1), :], psT[:, 0:(384 if do_q else 128)].re("p (c n) -> p c n", n=128))
            yield

        def stage_b(i):
            is_ctx, do_q, j, pos = info(i)
            s, cnT_, sq_, qk_, qkb_, rt_, rg_ = sm[j], cnT[j], sq[j], qk[j], qkb[j], rt[j], rg[j]
            pskv = kb.psum_any()
            mm(pskv, cnT_[:, 0, :], wukv, True, True)
            if do_q:
                psq = kb.psum_any()
                for c in range(2):
                    mm(psq[:, 0:384], cnT_[:, 1 + c, :], wuq[:, c, :], c == 0, c == 1)
            yield
            cp("act", kvs_[j], pskv)
            if do_q:
                cp("act", qs_[j], psq[:, 0:384])
            kv3 = kvs_[j].re("p (h e) -> p h e", h=4)
            q3 = qs_[j].re("p (h e) -> p h e", h=4)
            krope = kr[j]
            act(sq_[:, 640:672], krope, AF.Square, accum=s[:, 7:8])
            cp("act", Vt[:, i, :, 0:64].par("mlav"), kv3[:, :, 64:128])
            yield
            ksq = sq_[:, 0:256].re("p (h e) -> p h e", h=4)
            qsq = sq_[:, 256:640].re("p (h e) -> p h e", h=4)
            tt("pool", ksq, kv3[:, :, 0:64], kv3[:, :, 0:64], ALU.mult)
            if do_q:
                tt("pool", qsq, q3, q3, ALU.mult)
            yield
            kb.reduce(s[:, 12:16], ksq)
            if do_q:
                kb.reduce(s[:, 8:12], qsq)
            else:
                kb.memset("dve", s[:, 8:12], 1.0)
            ts("dve", s[:, 12:16], s[:, 12:16], s[:, 7:8], ALU.add)
            yield
            act(s[:, 8:16], s[:, 8:16], AF.Sqrt, scale=1.0 / 96, bias=EPS)
            yield
            kb.recip(s[:, 16:24], s[:, 8:16])
            yield
            tt("pool", rg_, s[:, 16:24].re("p (w h) -> p w h", w=2).us(3).bc([128, 2, 4, 96]),
               gains.us(2).bc([128, 2, 4, 96]), ALU.mult)
            yield
            if do_q:
                tt("dve", qk_[:, 0].par("qk"), q3, rg_[:, 0], ALU.mult)
            else:
                kb.memset("pool", qk_[:, 0].par("qk"), 0.0)
            tt("dve", qk_[:, 1, :, 0:64].par("qk"), kv3[:, :, 0:64], rg_[:, 1, :, 0:64], ALU.mult)
            tt("pool", qk_[:, 1, :, 64:96].par("qk"), krope.us(1).bc([128, 4, 32]), rg_[:, 1, :, 64:96], ALU.mult)
            yield
            if not is_ctx:
                ti = i - 2
                v = qk_[:, :, :, 64:96]
                t1_ = rt_[:, 0].par("rt")
                t2_ = rt_[:, 1]
                Cb = ropeC[:, ti, :].us(1).us(1).bc([128, 2, 4, 32])
                tt("pool", t1_, v, Cb, ALU.mult)
                for a in range(2):
                    for s_ in range(2):
                        o_ = t2_[:, :, :, a * 16 + s_ * 8:a * 16 + s_ * 8 + 8].par("rt")
                        i_ = qk_[:, :, :, 64 + a * 16 + (1 - s_) * 8:64 + a * 16 + (1 - s_) * 8 + 8]
                        Sb = ropeS[:, ti, a * 16 + s_ * 8:a * 16 + s_ * 8 + 8].us(1).us(1).bc([128, 2, 4, 8])
                        tt("dve" if (a + s_) % 2 == 0 else "pool", o_, i_, Sb, ALU.mult)
                yield
                tt("dve", v, t1_, t2_, ALU.add)
            cp("dve", qkb_, qk_)
            yield

        def stage_c(i):
            is_ctx, do_q, j, pos = info(i)
            qkb_ = qkb[j]
            psT2 = kb.psum_any(bf16=True)
            for w_ in range(2):
                if w_ == 0 and not do_q:
                    continue
                for h in range(4):
                    tr(psT2[0:96, (w_ * 4 + h) * 128:(w_ * 4 + h + 1) * 128], qkb_[:, w_, h, :], identb)
            yield
            if do_q:
                cp("act", qT[0:96, :, pos].par("mlaqk"), psT2[0:96, 0:512].re("p (h n) -> p h n", h=4))
            cp("act", kT[0:96, :, pos].par("mlaqk"), psT2[0:96, 512:1024].re("p (h n) -> p h n", h=4))

        def tile_gen(i):
            yield from stage_a(i)
            yield from stage_b(i)
            yield from stage_c(i)
        for g0 in range(0, 18, NS):
            lockstep([tile_gen(i) for i in range(g0, g0 + NS)])
        if "mla_stop1" in dbg:
            return
        prefetch(l, PF_GATE0)
        P.barrier()
        AR.lo = lo_fixed
        omla = AR.alloc("omla", [18, 256], BF16)
        pT = [AR.alloc("pT%d" % i, [512], BF16) for i in range(3)]
        rc = [AR.alloc("rc%d" % i, [4]) for i in range(2)]
        jobs = []
        if with_ctx:
            jobs.append((0, TC, 0, 2))
        for qb in range(4):
            jobs.append((TC + qb * 512, TC + (qb + 1) * 512, 0, 18))
        steps = []
        for (q0, q1, kt0, kt1) in jobs:
            for h in range(4):
                for kt in range(kt0, kt1):
                    steps.append((q0, q1, kt0, kt1, h, kt))

        def s_mm(si):
            q0, q1, kt0, kt1, h, kt = steps[si]
            mm(kb.psf[si % 2][:, 0:q1 - q0], kT[:, h, kt * 128:(kt + 1) * 128], qT[:, h, q0:q1], True, True)
        s_mm(0)
        for si, (q0, q1, kt0, kt1, h, kt) in enumerate(steps):
            nq = q1 - q0
            nqt = nq // 128
            pso = [kb.psf[2 + t_] for t_ in range(nqt)]
            if si + 1 < len(steps):
                s_mm(si + 1)
            pss = kb.psf[si % 2]
            p_ = pT[si % 3]
            act(p_[:, 0:nq], pss[:, 0:nq], AF.Exp)
            for t_ in range(nqt):
                mm(pso[t_][:, 0:68], p_[:, t_ * 128:(t_ + 1) * 128], Vt[:, kt, h, :], kt == kt0, kt == kt1 - 1)
            if kt == kt1 - 1:
                for t_ in range(nqt):
                    r_ = rc[t_ % 2]
                    kb.recip(r_[:, 0:1], pso[t_][:, 64:65])
                    ts("dve", omla[:, (q0 // 128) + t_, h * 64:(h + 1) * 64].par("omla"), pso[t_][:, 0:64], r_[:, 0:1], ALU.mult)
        for i in range(0 if with_ctx else 2, 18):
            psT = kb.psum_b()
            for c in range(2):
                tr(psT[:, c * 128:(c + 1) * 128], omla[:, i, c * 128:(c + 1) * 128], identb)
            for c in range(2):
                cp("act", brT[0][c][:, i * 128:(i + 1) * 128].par("brt0"), psT[:, c * 128:(c + 1) * 128])

    def phase_s5(l, b, with_ctx):
        AR.reset()
        AR2.lo = 0
        U = AR.alloc("U", [16, NCH])
        SN = [[AR.alloc("SN%d%d" % (d, ri), [8, NCH + 2]) for ri in range(2)] for d in range(2)]
        Ere = AR.alloc("Ere", [2, 8, 128])
        nEim = AR.alloc("nEim", [2, 8, 128])
        W3 = AR.alloc("W3", [16, 128])
        scr0 = AR.lo
        ws5 = wst[:, :, 0:256]
        Uc = AR.alloc("Uc", [16, 8, 16])
        kblocks = [(0, 32, 0), (32, 128, TC), (160, 128, TC + 1024)]
        for (k0, M, p0) in kblocks:
            pss = [kb.psum_f() for _ in range(4)]
            for r in range(8):
                ps = pss[r // 2]
                for k in range(KC):
                    mm(ps[0:M, (r % 2) * 256:(r % 2 + 1) * 256], hT[:, k, p0 + r:p0 + 8 * M:8], ws5[:, k, :], k == 0, k == KC - 1)
            for q in range(4):
                src = pss[q][0:M, :].re("p (r g i) -> p r g i", r=2, g=16)
                dst = Uc[0:M, :, 2 * q:2 * q + 2, :].re("p g r i -> p r g i").par("s5uc")
                cp("act" if q % 2 == 0 else "dve", dst, src)
            for gq in range(4):
                ps = kb.psum_f()
                for gg_ in range(4):
                    g = gq * 4 + gg_
                    tr(ps[:, gg_ * 128:gg_ * 128 + M], Uc[0:M, g, :, :].re("p r i -> p (r i)"), identf[0:M, 0:M])
                cp("act" if gq % 2 == 0 else "dve", U[:, gq * 4:gq * 4 + 4, k0:k0 + M].par("s5u"),
                   ps.re("p (g n) -> p g n", g=4)[:, :, 0:M])
        prefetch(l, PF_LRU)
        P.barrier()
        AR.lo = scr0
        if "s5_stop1" in dbg:
            return
        cached = (b > 0) and ("s5_nocache" not in dbg)
        if not cached:
            kb.memset("pool", W3, 0.0)
        else:
            dma("sp", Ere.re("p d g n -> p (d g n)"), st_main[0])
            dma("act", nEim.re("p d g n -> p (d g n)"), st_main[1])
            dma("sp", W3.re("p g n -> p (g n)"), st_main[2])
        prm = AR.alloc("prm", [40, 8])
        PW = [AR.alloc("pw%d" % i, [10, 8]) for i in range(2)]
        Bb = [AR.alloc("Bb%d" % i, [8, 16]) for i in range(2)]
        Braw = [AR.alloc("Braw%d" % i, [8, 16]) for i in range(2)]
        Craw = [AR.alloc("Craw%d" % i, [8, 16]) for i in range(2)]
        Dbc = AR.alloc("Dbc", [256])
        t8 = AR.alloc("t8", [8, 16])
        w3t = [AR.alloc("w3t%d" % i, [128]) for i in range(2)]
        tE = AR.alloc("tE", [8, 128])
        Fm = [AR.alloc("F%d" % i, [8, 8, 16]) for i in range(2)]
        Ep = [AR.alloc("Ep%d" % i, [8, 128]) for i in range(2)]
        Cnat = [AR.alloc("Cnat%d" % i, [16, 64], at=Ep[i].off, buf=Ep[i].buf) for i in range(2)]
        rs_sets = [[AR.alloc("rs%d_%d" % (q, i), [NCH], at=Fm[0].off + (q * 7 + i) * NCH) for i in range(6)]
                   for q in range(2)]
        rho_sets = [AR.alloc("rho1_%d" % q, [NCH], at=Fm[0].off + (q * 7 + 6) * NCH) for q in range(2)]
        assert Fm[0].off + 14 * NCH <= Ep[1].off + Ep[1].w
        W1 = [AR2.alloc("W1%d" % i, [16, 64]) for i in range(2)]
        tab = [AR2.alloc("tab%d" % i, [8, NCH]) for i in range(2)]
        dma("sp", Dbc, s5_d[l:l + 1, :].re("o n -> (o n)").pbc(128))
        dsel = cmask[:, 2, :]

        prm_bufs = [Buf("prm%d" % i) for i in range(40)]
        pw_bufs = [[Buf("pw%d_%d" % (ri, j)) for j in range(10)] for ri in range(2)]

        def pv(i):
            return TV(prm.ap[:, i, :], prm_bufs[i])

        def pw(ri, j):
            return TV(PW[ri].ap[:, j, :], pw_bufs[ri][j])

        for d in range(2):
            if d == 1:
                P.barrier()
            if not cached:
                dma("sp", pv(0), s5_a_re[l, d].re("(gp h) p -> (h p) gp", h=2), slow=True)
                dma("act", pv(1), s5_a_im[l, d].re("(gp h) p -> (h p) gp", h=2), slow=True)
                ldt = t8[:, 0, :]
                dma("sp", ldt, s5_log_dt[l, d:d + 1, :].re("o g -> (o g)").pbc(128))
                dma("sp", Cnat[0][0:16], s5_c_re[l, d].re("g o p -> o g p"))
                dma("act", Cnat[1][0:16], s5_c_im[l, d].re("g o p -> o g p"))
                for h in range(2):
                    rows = slice(h * 64, (h + 1) * 64)
                    cp("dve", pv(2)[rows], ldt[rows, h:16:2])
                    dma("sp", Braw[0][rows].par("ldw"), s5_b_re[l, d].re("(gp h) p i -> h p gp i", h=2)[h])
                    dma("act", Braw[1][rows].par("ldw"), s5_b_im[l, d].re("(gp h) p i -> h p gp i", h=2)[h])
                for ri in range(2):
                    ps = kb.psum_f()
                    for g in range(16):
                        gp, h = g // 2, g % 2
                        mm(ps[h * 64:(h + 1) * 64, gp * 16:(gp + 1) * 16], Cnat[ri][0:16, g, :], identf[0:16, 0:16], True, True)
                    cp("act", Craw[ri], ps[:, 0:128].re("p (g o) -> p g o", g=8))
                if "s5_b1" in dbg:
                    return
                act(pv(2), pv(2), AF.Exp)
                tt("dve", pv(3), pv(0), pv(2), ALU.mult)
                tt("dve", pv(4), pv(1), pv(2), ALU.mult)
                act(pv(5), pv(3), AF.Exp)
                for (dst, shift) in ((6, 0.0), (7, math.pi / 2)):
                    xx, kk, ki = pv(30), pv(31), pv(32)
                    ts("dve", xx, pv(4), shift, ALU.add)
                    ts("dve", kk, xx, 1.0 / (2 * math.pi), ALU.mult)
                    kint = TV(ki.ap.bitcast(I32), ki.buf)
                    cp("dve", kint, kk)
                    cp("dve", kk, kint)
                    stt(xx, kk, -6.28125, xx, ALU.mult, ALU.add)
                    stt(xx, kk, -(2 * math.pi - 6.28125), xx, ALU.mult, ALU.add)
                    ts("dve", xx, xx, math.pi, ALU.min, -math.pi, ALU.max)
                    act(pv(dst), xx, AF.Sin)
                kb.memset("dve", pw(0, 0), 1.0)
                kb.memset("dve", pw(1, 0), 0.0)
                tt("dve", pw(0, 1), pv(5), pv(7), ALU.mult)
                tt("dve", pw(1, 1), pv(5), pv(6), ALU.mult)
                for j in range(2, 9):
                    tt("dve", pv(30), pw(0, j - 1), pw(0, 1), ALU.mult)
                    tt("dve", pv(31), pw(1, j - 1), pw(1, 1), ALU.mult)
                    tt("dve", pw(0, j), pv(30), pv(31), ALU.subtract)
                    tt("dve", pv(30), pw(0, j - 1), pw(1, 1), ALU.mult)
                    tt("dve", pv(31), pw(1, j - 1), pw(0, 1), ALU.mult)
                    tt("dve", pw(1, j), pv(30), pv(31), ALU.add)
                act(pv(8), pv(3), AF.Exp, scale=-16.0)
                tt("dve", pw(0, 9), pw(0, 8), pv(8), ALU.mult)
                tt("dve", pw(1, 9), pw(1, 8), pv(8), ALU.mult)
                ts("dve", pw(1, 9), pw(1, 9), -1.0, ALU.mult)
                act(pv(9), pv(3), AF.Exp, scale=8.0)
                act(pv(10), pv(3), AF.Exp, scale=-8.0)
                tt("dve", pv(11), pw(0, 8), pv(10), ALU.mult)
                tt("dve", pv(12), pw(1, 8), pv(10), ALU.mult)
                ts("dve", pv(13), pw(0, 1), -1.0, ALU.add)
                tt("dve", pv(14), pv(0), pv(0), ALU.mult)
                tt("dve", pv(15), pv(1), pv(1), ALU.mult)
                tt("dve", pv(14), pv(14), pv(15), ALU.add)
                kb.recip(pv(14), pv(14))
                tt("dve", pv(15), pv(13), pv(0), ALU.mult)
                tt("dve", pv(16), pw(1, 1), pv(1), ALU.mult)
                tt("dve", pv(15), pv(15), pv(16), ALU.add)
                tt("dve", pv(15), pv(15), pv(14), ALU.mult)
                tt("dve", pv(16), pw(1, 1), pv(0), ALU.mult)
                tt("dve", pv(17), pv(13), pv(1), ALU.mult)
                tt("dve", pv(16), pv(16), pv(17), ALU.subtract)
                tt("dve", pv(16), pv(16), pv(14), ALU.mult)
                fre = pv(15).us(2).bc([128, 8, 16])
                fim = pv(16).us(2).bc([128, 8, 16])
                tt("dve", Bb[0], Braw[0], fre, ALU.mult)
                tt("dve", t8, Braw[1], fim, ALU.mult)
                tt("dve", Bb[0], Bb[0], t8, ALU.subtract)
                tt("dve", Bb[1], Braw[1], fre, ALU.mult)
                tt("dve", t8, Braw[0], fim, ALU.mult)
                tt("dve", Bb[1], Bb[1], t8, ALU.add)
                for t_ in range(8):
                    f_ = t_ + 1 if d == 0 else 8 - t_
                    pr = pw(0, f_).us(2).bc([128, 8, 16])
                    pi_ = pw(1, f_).us(2).bc([128, 8, 16])
                    eo = Ere[:, d, :, t_ * 16:(t_ + 1) * 16]
                    ei = nEim[:, d, :, t_ * 16:(t_ + 1) * 16]
                    tt("dve", eo, Craw[0], pr, ALU.mult)
                    tt("dve", t8, Craw[1], pi_, ALU.mult)
                    tt("dve", eo, eo, t8, ALU.subtract)
                    tt("dve", ei, Craw[0], pi_, ALU.mult)
                    tt("dve", t8, Craw[1], pr, ALU.mult)
                    tt("dve", ei, ei, t8, ALU.add)
                    ts("dve", ei, ei, -1.0, ALU.mult)
                for r in range(8):
                    e_ = 7 - r if d == 0 else r
                    pr = pw(0, e_).us(2).bc([128, 8, 16])
                    pi_ = pw(1, e_).us(2).bc([128, 8, 16])
                    tt("dve", Fm[0][:, :, r, :], Bb[0], pr, ALU.mult)
                    tt("dve", t8, Bb[1], pi_, ALU.mult)
                    tt("dve", Fm[0][:, :, r, :], Fm[0][:, :, r, :], t8, ALU.subtract)
                    tt("dve", Fm[1][:, :, r, :], Bb[1], pr, ALU.mult)
                    tt("dve", t8, Bb[0], pi_, ALU.mult)
                    tt("dve", Fm[1][:, :, r, :], Fm[1][:, :, r, :], t8, ALU.add)
                qr = pw(0, 9).us(2).bc([128, 8, 128])
                qi = pw(1, 9).us(2).bc([128, 8, 128])
                tt("dve", Ep[0], Ere[:, d], qr, ALU.mult)
                tt("dve", tE, nEim[:, d], qi, ALU.mult)
                tt("dve", Ep[0], Ep[0], tE, ALU.add)
                tt("dve", Ep[1], nEim[:, d], qr, ALU.mult)
                tt("dve", tE, Ere[:, d], qi, ALU.mult)
                tt("dve", Ep[1], Ep[1], tE, ALU.subtract)
                if "s5_b2" in dbg:
                    return
                for ri in range(2):
                    for gq in range(4):
                        ps = kb.psum_f()
                        for gg_ in range(4):
                            g = gq * 4 + gg_
                            gp, h = g // 2, g % 2
                            rows = slice(h * 64, (h + 1) * 64)
                            mm(ps[:, gg_ * 64:(gg_ + 1) * 64], Fm[ri][:, gp].re("p r i -> p (r i)"), identf[:, h * 64:(h + 1) * 64], True, True)
                        cp("act", W1[ri][:, gq * 4:gq * 4 + 4, :], ps[:, 0:256].re("p (g n) -> p g n", g=4))
                if "s5_b3" in dbg:
                    return
                for g in range(16):
                    gp, h = g // 2, g % 2
                    rows = slice(h * 64, (h + 1) * 64)
                    ps = kb.psf[h * 2 + (g // 2) % 2]
                    mm(ps[:, 0:128], Fm[0][rows, gp].re("p r i -> p (r i)"), Ep[0][rows, gp, :], True, False)
                    mm(ps[:, 0:128], Fm[1][rows, gp].re("p r i -> p (r i)"), Ep[1][rows, gp, :], False, True)
                    wt = w3t[g % 2]
                    tt("dve", wt, ps[:, 0:128], cmask[:, d, :], ALU.mult)
                    tt("pool", W3[:, g, :], W3[:, g, :], wt, ALU.add)
                if d == 0:
                    for g in range(16):
                        dcol = Dbc[:, g * 16:(g + 1) * 16].us(1).bc([128, 8, 16])
                        wt = w3t[g % 2]
                        tt("dve", wt.re("p (t o) -> p t o", t=8), dsel.re("p (t o) -> p t o", t=8), dcol, ALU.mult)
                        tt("pool", W3[:, g, :], W3[:, g, :], wt, ALU.add)
                if "s5_b4" in dbg:
                    return
                kb.memset("dve", tab[0][:, :, 0:1], 1.0)
                kb.memset("dve", tab[1][:, :, 0:1], 0.0)
                cp("dve", tab[0][:, :, 1:2], pv(11).us(2))
                cp("dve", tab[1][:, :, 1:2], pv(12).us(2))
                n = 2
                while n < NCH:
                    m = min(n, NCH - n)
                    tt("dve", pv(30), tab[0][:, :, n - 1], pv(11), ALU.mult)
                    tt("dve", pv(31), tab[1][:, :, n - 1], pv(12), ALU.mult)
                    tt("dve", pv(33), pv(30), pv(31), ALU.subtract)
                    tt("dve", pv(30), tab[0][:, :, n - 1], pv(12), ALU.mult)
                    tt("dve", pv(31), tab[1][:, :, n - 1], pv(11), ALU.mult)
                    tt("dve", pv(34), pv(30), pv(31), ALU.add)
                    Pr = pv(33).us(2).bc([128, 8, m])
                    Pi = pv(34).us(2).bc([128, 8, m])
                    ta = tE[:, :, 0:m]
                    tt("dve", tab[0][:, :, n:n + m], tab[0][:, :, 0:m], Pr, ALU.mult)
                    tt("dve", ta, tab[1][:, :, 0:m], Pi, ALU.mult)
                    tt("dve", tab[0][:, :, n:n + m], tab[0][:, :, n:n + m], ta, ALU.subtract)
                    tt("dve", tab[1][:, :, n:n + m], tab[0][:, :, 0:m], Pi, ALU.mult)
                    tt("dve", ta, tab[1][:, :, 0:m], Pr, ALU.mult)
                    tt("dve", tab[1][:, :, n:n + m], tab[1][:, :, n:n + m], ta, ALU.add)
                    n *= 2
                dma("sp", st_dir[d, :, 0:1024], W1[0].re("p g n -> p (g n)"))
                dma("sp", st_dir[d, :, 1024:2048], W1[1].re("p g n -> p (g n)"))
                dma("act", st_dir[d, :, 2048:2048 + 8 * NCH], tab[0].re("p g n -> p (g n)"))
                dma("act", st_dir[d, :, 2048 + 8 * NCH:2048 + 16 * NCH], tab[1].re("p g n -> p (g n)"))
                dma("sp", st_dir[d, :, 2048 + 16 * NCH:2048 + 16 * NCH + 8], pv(9))
            else:
                dma("sp", W1[0].re("p g n -> p (g n)"), st_dir[d, :, 0:1024])
                dma("sp", W1[1].re("p g n -> p (g n)"), st_dir[d, :, 1024:2048])
                dma("act", tab[0].re("p g n -> p (g n)"), st_dir[d, :, 2048:2048 + 8 * NCH])
                dma("act", tab[1].re("p g n -> p (g n)"), st_dir[d, :, 2048 + 8 * NCH:2048 + 16 * NCH])
                dma("sp", pv(9), st_dir[d, :, 2048 + 16 * NCH:2048 + 16 * NCH + 8])
            if "s5_stop2" in dbg:
                return
            P.barrier()
            def nat(tv, sl, rev):
                v = tv[:, sl]
                return v[:, ::-1] if rev else v

            def gp_gen(gp, d=d):
                rs, rho1 = rs_sets[gp % 2], rho_sets[gp % 2]
                psv = [kb.psum_f(), kb.psum_f()]
                for ri in range(2):
                    for h in range(2):
                        g = gp * 2 + h
                        mm(psv[ri][h * 64:(h + 1) * 64, 0:NCH], W1[ri][:, g, :], U[:, g, :], True, True)
                cp("pool", rho1, pv(9)[:, gp:gp + 1].bc([128, NCH]))
                yield
                Cn, Sn = tab[0][:, gp, :], tab[1][:, gp, :]
                if d == 0:
                    segs = [(slice(0, NCH), slice(0, NCH), False)]
                else:
                    segs = [(slice(0, 32), slice(0, 32), True), (slice(32, NCH), slice(32, NCH), True)]
                for (js, ks, rev) in segs:
                    vre, vim = nat(psv[0][:, 0:NCH], ks, rev), nat(psv[1][:, 0:NCH], ks, rev)
                    tt("dve", rs[0][:, js], vre, Cn[:, js], ALU.mult)
                    tt("dve", rs[1][:, js], vim, Sn[:, js], ALU.mult)
                    tt("dve", rs[4][:, js], vim, Cn[:, js], ALU.mult)
                    tt("dve", rs[5][:, js], vre, Sn[:, js], ALU.mult)
                yield
                tt("pool", rs[2], rs[0], rs[1], ALU.add)
                tt("pool", rs[3], rs[4], rs[5], ALU.subtract)
                yield
                kb.scan(rs[4], rho1, rs[2], 0.0)
                kb.scan(rs[5], rho1, rs[3], 0.0)
                yield
                tt("dve", rs[0], rs[4], Cn, ALU.mult)
                tt("dve", rs[1], rs[5], Sn, ALU.mult)
                tt("dve", rs[2], rs[4], Sn, ALU.mult)
                tt("dve", rs[3], rs[5], Cn, ALU.mult)
                yield
                for (js, ks, rev) in segs:
                    if d == 0:
                        osl = slice(1, NCH + 1)
                    else:
                        osl = slice(0, 32) if ks.start == 0 else slice(33, NCH + 1)
                    ore = nat(SN[d][0][:, gp, :], osl, rev).par("s5sn")
                    oim = nat(SN[d][1][:, gp, :], osl, rev).par("s5sn")
                    tt("pool", ore, rs[0][:, js], rs[1][:, js], ALU.subtract)
                    tt("pool", oim, rs[2][:, js], rs[3][:, js], ALU.add)
                yield
            for gp0 in range(0, 8, 2):
                lockstep([gp_gen(gp0), gp_gen(gp0 + 1)])
            for ri in range(2):
                if d == 0:
                    kb.memset("pool", SN[0][ri][:, :, 0:1], 0.0)
                else:
                    kb.memset("pool", SN[1][ri][:, :, 32:33], 0.0)
                    cp("pool", SN[1][ri][:, :, NCH + 1:NCH + 2], SN[1][ri][:, :, 0:1])
        if not cached:
            dma("sp", st_main[0], Ere.re("p d g n -> p (d g n)"))
            dma("act", st_main[1], nEim.re("p d g n -> p (d g n)"))
            dma("sp", st_main[2], W3.re("p g n -> p (g n)"))
        if "s5_stop3" in dbg:
            return
        P.barrier()
        AR.lo = scr0
        AR2.lo = 0
        Yc = AR.alloc("Yc", [8, 256])
        gg = AR.alloc("gg", [8, 256])
        g2 = AR.alloc("g2", [8, 256])
        sg = [AR.alloc("sg%d" % i, [512]) for i in range(2)]
        ggT = AR2.alloc("ggT", [2, NPOS])
        ggb = AR2.alloc("ggb", [2, NPOS], BF16)
        for (k0, M, p0) in kblocks:
            if k0 == 0 and not with_ctx:
                continue
            banks = [kb.psf[0], kb.psf[2], kb.psf[1], kb.psf[3]]
            for g in range(16):
                gp, h = g // 2, g % 2
                rows = slice(h * 64, (h + 1) * 64)
                bank = banks[h * 2 + (gp // 4)]
                o = bank[0:M, (gp % 4) * 128:(gp % 4 + 1) * 128]
                cf = slice(k0, k0 + M)
                cb_ = slice(k0 + 1, k0 + 1 + M) if k0 == 0 else slice(k0 + 2, k0 + 2 + M)
                mm(o, SN[0][0][rows, gp, cf], Ere[rows, 0, gp, :], True, False)
                mm(o, SN[0][1][rows, gp, cf], nEim[rows, 0, gp, :], False, False)
                mm(o, SN[1][0][rows, gp, cb_], Ere[rows, 1, gp, :], False, False)
                mm(o, SN[1][1][rows, gp, cb_], nEim[rows, 1, gp, :], False, False)
                mm(o, U[:, g, k0:k0 + M], W3[:, g, :], False, True)
            for h in range(2):
                for q in range(2):
                    bank = banks[h * 2 + q]
                    src = bank[0:M, :].re("p (g t o) -> p g t o", g=4, t=8)
                    dst = Yc[0:M].re("p t (g2 h o) -> p h g2 t o", h=2, o=16)[:, h, 4 * q:4 * q + 4].par("s5yc")
                    cp("act" if h == 0 else "dve", dst, src)
            if "s5_Y" in dbg:
                dma("sp", tap("s5Y", [NCH, 8 * 256])[k0:k0 + M, :], Yc[0:M].re("p t f -> p (t f)"))
            yv, gv, g2v = Yc[0:M], gg[0:M], g2[0:M]
            tt("pool", g2v, yv, yv, ALU.mult)
            ts("dve", g2v, g2v, 0.044715, ALU.mult, 1.0, ALU.add)
            tt("pool", g2v, g2v, yv, ALU.mult)
            act(g2v, g2v, AF.Sigmoid, scale=1.5957691216057308)
            tt("dve", gv, g2v, yv, ALU.mult)
            for t_ in range(8):
                pz = [kb.psum_f(), kb.psum_f()]
                for c in range(2):
                    tr(pz[c][:, 0:M], gv[:, t_, c * 128:(c + 1) * 128], identf[0:M, 0:M])
                for c in range(2):
                    cp("act" if c == 0 else "dve", ggT[:, c, p0 + t_:p0 + 8 * M:8].par("s5gg"), pz[c][:, 0:M])
        lo = 0 if with_ctx else TC
        for c in range(2):
            cp("dve", ggb[:, c, lo:NPOS], ggT[:, c, lo:NPOS])
        for c in range(2):
            pos = lo
            while pos < NPOS:
                n = min(512, NPOS - pos)
                ps = kb.psum_f()
                for k in range(2):
                    mm(ps[:, 0:n], wglu[:, k, c * 128:(c + 1) * 128], ggb[:, k, pos:pos + n], k == 0, k == 1)
                s_ = sg[(pos // 512) % 2]
                act(s_[:, 0:n], ps[:, 0:n], AF.Sigmoid)
                tt("dve", brT[1][c][:, pos:pos + n], s_[:, 0:n], ggT[:, c, pos:pos + n], ALU.mult)
                pos += n

    def phase_merge(l, b, with_ctx, nxt=None):
        AR.reset(top=True)
        lo = 0 if with_ctx else TC
        blocks = []
        pos = lo
        while pos < NPOS:
            n = min(512, NPOS - pos) if pos >= TC else TC - pos
            blocks.append((pos, n))
            pos += n
        ypT = AR.alloc("ypT", [KC, NPOS], BF16, top=True)
        wbr = AR.alloc("wbr", [4, 2, D], BF16, top=True)
        wo = AR.alloc("wo", [KC, D], BF16, top=True)
        wml = [AR.alloc("wml%d" % i, [KC, 4, 128], BF16, top=True) for i in range(2)]

        def load_wml(oc):
            for n_ in range(4):
                c0 = O_MERGE + n_ * D + oc * 128
                dma("pool", wml[oc % 2][:, :, n_, :].par("ldw"), w_in[l, :, c0:c0 + 128].re("(c p) n -> p c n", p=128))
        wg = AR.alloc("w_gate", [KC, 1024], BF16)
        for cc in range(1, 8):
            dma("pool", wg[:, :, cc * 128:(cc + 1) * 128].par("ldw"),
                w_in[l, :, O_GATE + cc * 128:O_GATE + (cc + 1) * 128].re("(c p) n -> p c n", p=128))
        for n_ in range(4):
            dma("pool", wbr[:, n_].par("ldw"), w_branch[l, n_].re("(c p) n -> p c n", p=128))
        load_wml(0)
        load_wml(1)
        dma("pool", wo, w_out[l].re("(c p) n -> p c n", p=128))
        sl = [AR.alloc("sl%d" % i, [512], BF16) for i in range(2)]
        it = 0
        for cc in range(8):
            for (p0, n) in blocks:
                ps = kb.psum_f()
                for k in range(KC):
                    wsl = wst[:, k, 0:128] if cc == 0 else wg[:, k, cc * 128:(cc + 1) * 128]
                    mm(ps[:, 0:n], wsl, hT[:, k, p0:p0 + n], k == 0, k == KC - 1)
                s_ = sl[it % 2]
                it += 1
                act(s_[:, 0:n], ps[:, 0:n], AF.Silu)
                br = brT[cc // 2][cc % 2][:, p0:p0 + n]
                tt("dve", br, br, s_[:, 0:n], ALU.mult)
            if cc == 0 and nxt is not None:
                prefetch(nxt, PF_S5)
        AR.reset()
        sg = [AR.alloc("sg%d" % i, [512]) for i in range(2)]
        tm = [AR.alloc("tm%d" % i, [512]) for i in range(2)]
        acc = [AR.alloc("acc%d" % i, [512]) for i in range(2)]
        it = 0
        ib = 0
        for oc in range(8):
            wm = wml[oc % 2]
            if 1 <= oc and oc + 1 < 8:
                load_wml(oc + 1)
            for (p0, n) in blocks:
                ac = acc[ib % 2]
                ib += 1
                for n_ in range(4):
                    psA = kb.psum_f()
                    for k in range(2):
                        mm(psA[:, 0:n], wbr[:, n_, k, oc * 128:(oc + 1) * 128], brT[n_][k][:, p0:p0 + n], k == 0, k == 1)
                    psB = kb.psum_f()
                    for k in range(KC):
                        mm(psB[:, 0:n], wm[:, k, n_, :], hT[:, k, p0:p0 + n], k == 0, k == KC - 1)
                    s_ = sg[it % 2]
                    t_ = tm[it % 2]
                    it += 1
                    act(s_[:, 0:n], psB[:, 0:n], AF.Sigmoid)
                    if n_ == 0:
                        tt("dve", ac[:, 0:n], psA[:, 0:n], s_[:, 0:n], ALU.mult)
                    elif n_ < 3:
                        tt("dve", t_[:, 0:n], psA[:, 0:n], s_[:, 0:n], ALU.mult)
                        tt("dve", ac[:, 0:n], ac[:, 0:n], t_[:, 0:n], ALU.add)
                    else:
                        tt("dve", t_[:, 0:n], psA[:, 0:n], s_[:, 0:n], ALU.mult)
                        tt("dve", ypT[:, oc, p0:p0 + n].par("ypt"), ac[:, 0:n], t_[:, 0:n], ALU.add)
        AR.reset()
        gbc = AR.alloc("gbc", [2, D])
        dma("sp", gbc[:, 0, :], modD[l, b, 2 * D:3 * D].pbc(128))
        if with_ctx:
            dma("sp", gbc[:, 1, :], modD[l, 2, 2 * D:3 * D].pbc(128))
        xt = [AR.alloc("xt%d" % i, [D]) for i in range(2)]
        yt = [AR.alloc("yt%d" % i, [D]) for i in range(2)]
        for i in range(0 if with_ctx else 2, 18):
            x_t, y_t = xt[i % 2], yt[i % 2]
            dma("sp", x_t, src_tile(l, b, i))
            for hf in range(2):
                ps = kb.psum_f()
                for k in range(KC):
                    mm(ps, ypT[:, k, i * 128:(i + 1) * 128], wo[:, k, hf * 512:(hf + 1) * 512], k == 0, k == KC - 1)
                tt("dve", y_t[:, hf * 512:(hf + 1) * 512], ps, gbc[:, 1 if i < 2 else 0, hf * 512:(hf + 1) * 512], ALU.mult)
            tt("pool", y_t, y_t, x_t, ALU.add)
            if i < 2:
                dst = (tap("xc1", [nb, TC, D]) if "x1out" in dbg else xcmid)[b, i * 128:(i + 1) * 128, :]
            else:
                dst = (xmid if (l < DEPTH - 1 and "x1out" not in dbg) else y_out)[b, (i - 2) * 128:(i - 1) * 128, :]
            dma("act", dst, y_t)

    def dump_br(name, n, lo):
        o = tap(name, [2, 128, NPOS])
        tmpf = AR.alloc("dump_" + name, [NPOS])
        for c in range(2):
            cp("dve", tmpf[:, lo:NPOS], brT[n][c][:, lo:NPOS])
            dma("sp", o[c, :, lo:NPOS], tmpf[:, lo:NPOS])

    prefetch(layers[0], PF_S5)
    for l in layers:
        with_ctx = l < DEPTH - 1
        prep_layer(l)
        for b in range(nb):
            if "prep_only" in dbg:
                continue
            phase_norm(l, b)
            if "hT" in dbg and b == 0 and l == layers[0]:
                o = tap("hT", [KC, 128, NPOS])
                AR.reset()
                tmpf = AR.alloc("dump_hT", [NPOS])
                for c in range(KC):
                    cp("dve", tmpf, hT[:, c, :])
                    dma("sp", o[c], tmpf)
            only = dbg & {"only_lru", "only_pool", "only_mla", "only_s5", "only_norm"}
            if not only or "only_s5" in only:
                phase_s5(l, b, with_ctx)
            if not only or "only_lru" in only:
                phase_lru(l, b, with_ctx)
            if not only or "only_pool" in only:
                phase_pool(l, b, with_ctx)
            if not only or "only_mla" in only:
                phase_mla(l, b, with_ctx)
            if "br" in dbg and b == 0 and l == layers[0]:
                AR.reset()
                lo = 0 if with_ctx else TC
                for n_, nm in enumerate(("mla", "s5", "lru", "pool")):
                    dump_br(nm, n_, lo)
            if "nomerge" not in dbg:
                if b + 1 < nb:
                    nxt = l
                else:
                    li = list(layers).index(l)
                    nxt = layers[li + 1] if li + 1 < len(layers) else None
                phase_merge(l, b, with_ctx, nxt)
    P.barrier()
    P.emit()
    kb.st.close()
    return nc, kb


def _consts():
    ident = np.eye(128, dtype=np.float32)
    rows_n = T // 64
    row = np.repeat(np.arange(rows_n), 64).astype(np.float32)
    col = np.tile(np.arange(64), rows_n).astype(np.float32)
    nf = 8
    inv = (np.float32(10000.0) ** (-np.arange(nf, dtype=np.float32) / nf)).astype(np.float32)
    ar = (row[:, None] * inv).astype(np.float32)
    ac = (col[:, None] * inv).astype(np.float32)
    cr, sr, cc, sc = np.cos(ar), np.sin(ar), np.cos(ac), np.sin(ac)
    ropeC = np.concatenate([cr, cr, cc, cc], axis=1).astype(np.float32)
    ropeS = np.concatenate([-sr, sr, -sc, sc], axis=1).astype(np.float32)
    cpool = np.ones((128, 34), np.float32)
    for c in range(2):
        for p in range(128):
            w = (2, 4, 8, 16)[2 * c + p // 64]
            hw = w // 2
            cpool[p, c] = 1.0 / w
            for t in range(8):
                cnt = t + hw if t < hw else w
                cpool[p, 2 + c * 8 + t] = w / cnt
            for j in range(8):
                dist = 8 - j
                cnt = dist + hw if dist < hw else w
                cpool[p, 18 + c * 8 + j] = w / cnt
    m = np.zeros((128, 3, 128), np.float32)
    for r in range(8):
        for t in range(8):
            if r <= t:
                m[r * 16:(r + 1) * 16, 0, t * 16:(t + 1) * 16] = 1.0
            if r >= t:
                m[r * 16:(r + 1) * 16, 1, t * 16:(t + 1) * 16] = 1.0
            if r == t:
                m[r * 16:(r + 1) * 16, 2, t * 16:(t + 1) * 16] = np.eye(16, dtype=np.float32)
    return {"c_ident": ident, "c_ropeC": ropeC, "c_ropeS": ropeS, "c_pool": cpool, "c_mask": m}


_WNAMES = ["w_ada", "b_ada", "norm_g", "w_in", "mla_q_norm", "mla_kv_norm", "mla_w_uq", "mla_w_ukv", "mla_q_gain",
           "mla_k_gain", "s5_a_re", "s5_a_im", "s5_log_dt", "s5_b_re", "s5_b_im", "s5_c_re", "s5_c_im", "s5_d",
           "s5_w_glu", "lru_conv_w", "lru_conv_b", "lru_lambda", "lru_w_a", "lru_b_a", "lru_w_x", "lru_b_x",
           "pool_w", "pool_b", "pool_scale", "w_branch", "w_out"]


def make_in_maps(inputs, n_cores, nb):
    consts = _consts()
    maps = []
    for r in range(n_cores):
        bs = slice(r * nb, (r + 1) * nb)
        m = {"x": np.ascontiguousarray(inputs["x"][bs], dtype=np.float32),
             "ctx": np.ascontiguousarray(inputs["ctx"][bs], dtype=np.float32)}
        cv = np.zeros((3, D), np.float32)
        cv[0:nb] = np.asarray(inputs["c"], dtype=np.float32)[bs]
        cv[2] = np.asarray(inputs["c_ctx"], dtype=np.float32)
        m["cvec"] = cv
        for k in _WNAMES:
            m[k] = np.ascontiguousarray(inputs[k], dtype=np.float32)
        m.update(consts)
        maps.append(m)
    return maps


_CACHE = {}


def kernel(**inputs):
    n_cores, nb = 8, 2
    if "nc" not in _CACHE:
        _CACHE["nc"] = build_program(nb=nb)[0]
    nc = _CACHE["nc"]
    maps = make_in_maps(inputs, n_cores, nb)
    res = run_bass_kernel_spmd(nc, maps, core_ids=list(range(n_cores)))
    out = np.concatenate([np.asarray(r["y"], dtype=np.float32) for r in res.results], axis=0)
    return out
```

```python
import math
import contextlib
import numpy as np
import concourse.bass as bass
import concourse.mybir as mybir
from concourse.bass_utils import run_bass_kernel_spmd

F32 = mybir.dt.float32
BF16 = mybir.dt.bfloat16
I32 = mybir.dt.int32
AF = mybir.ActivationFunctionType
ALU = mybir.AluOpType
AX = mybir.AxisListType

ENGS = ("pe", "act", "dve", "pool", "sp")
NDMA = 24

D = 1024
KC = 8
T = 2048
TC = 256
NPOS = T + TC
DEPTH = 2
O_KROPE, O_S5, O_LRU, O_CQ, O_POOL, O_GATE, O_MERGE, IN_W = 128, 160, 416, 672, 928, 1184, 2208, 6304
EPS = 1e-6
NCH = NPOS // 8


PAR_OFF = set()


class Buf:
    __slots__ = ("name", "w", "wp", "r")

    def __init__(self, name):
        self.name = name
        self.w = []
        self.wp = []
        self.r = []


class TV:
    par_ = False

    def __init__(self, ap, buf):
        self.ap, self.buf = ap, buf

    def par(self, tag=""):
        t = TV(self.ap, self.buf)
        t.par_ = tag not in PAR_OFF
        return t

    def __getitem__(self, k):
        return TV(self.ap[k], self.buf)

    def re(self, s, **kw):
        return TV(self.ap.rearrange(s, **kw), self.buf)

    def bc(self, shape):
        return TV(self.ap.to_broadcast(list(shape)), self.buf)

    def us(self, ax):
        return TV(self.ap.unsqueeze(ax), self.buf)

    def pbc(self, n):
        return TV(self.ap.partition_broadcast(n), self.buf)

    @property
    def shape(self):
        return tuple(self.ap.shape)


def _bufs(*xs):
    out = []
    for x in xs:
        if isinstance(x, TV):
            out.append(x.buf)
        elif isinstance(x, (list, tuple)):
            out.extend(_bufs(*x))
    return out


def _ap(x):
    return x.ap if isinstance(x, TV) else x


class Prog:
    def __init__(self, nc):
        self.nc = nc
        self.ops = {e: [] for e in ENGS}
        self.cnt = {e: 0 for e in ENGS}
        self.waited = {e: {} for e in ENGS}
        self.dma_i = 0
        self.dma_last = [0] * NDMA

    def _deps(self, eng, reads, writes, par=False):
        toks = []
        for b in reads:
            toks.extend(b.w)
            toks.extend(b.wp)
        for b in writes:
            toks.extend(b.w)
            if not par:
                toks.extend(b.wp)
            toks.extend(b.r)
        need = {}
        for (k, v, e) in toks:
            if e == "pe" and eng == "pe":
                continue
            if self.waited[eng].get(k, 0) >= v:
                continue
            if need.get(k, 0) < v:
                need[k] = v
        for k, v in need.items():
            self.waited[eng][k] = v
        return list(need.items())

    def _mark(self, tok, reads, writes, par=False):
        for b in writes:
            if par:
                best = {}
                for (k, v, e) in b.wp + [tok]:
                    if k not in best or best[k][1] < v:
                        best[k] = (k, v, e)
                b.wp = list(best.values())
            else:
                b.w = [tok]
                b.wp = []
                b.r = []
        for b in reads:
            if b in writes:
                continue
            b.r.append(tok)
            if len(b.r) > 16:
                best = {}
                for (k, v, e) in b.r:
                    if k not in best or best[k][1] < v:
                        best[k] = (k, v, e)
                b.r = list(best.values())

    def op(self, eng, fn, reads=(), writes=(), par=False):
        reads = list(dict.fromkeys(reads))
        writes = list(dict.fromkeys(writes))
        waits = self._deps(eng, reads, writes, par)
        self.cnt[eng] += 1
        tok = (eng, self.cnt[eng], eng)
        self.ops[eng].append((waits, fn, (eng, 1)))
        self._mark(tok, reads, writes, par)

    def dma(self, eng, fn, reads=(), writes=(), par=False):
        reads = list(dict.fromkeys(reads))
        writes = list(dict.fromkeys(writes))
        waits = self._deps(eng, reads, writes, par)
        slot = self.dma_i % NDMA
        self.dma_i += 1
        key = ("dma", slot)
        prev = self.dma_last[slot]
        if prev and self.waited[eng].get(key, 0) < prev:
            waits.append((key, prev))
            self.waited[eng][key] = prev
        val = prev + 16
        self.dma_last[slot] = val
        tok = (key, val, "dma")
        self.ops[eng].append((waits, fn, (key, 16)))
        self._mark(tok, reads, writes, par)

    def barrier(self):
        for E in ENGS:
            waits = []
            for e in ENGS:
                v = self.cnt[e]
                if v and self.waited[E].get(e, 0) < v:
                    waits.append((e, v))
                    self.waited[E][e] = v
            for slot in range(NDMA):
                v = self.dma_last[slot]
                key = ("dma", slot)
                if v and self.waited[E].get(key, 0) < v:
                    waits.append((key, v))
                    self.waited[E][key] = v
            if waits:
                self.ops[E].append((waits, None, None))

    def emit(self):
        nc = self.nc
        sems = {}
        with contextlib.ExitStack() as st:
            for e in ENGS:
                sems[e] = st.enter_context(nc.semaphore("s_" + e))
            for i in range(NDMA):
                sems[("dma", i)] = st.enter_context(nc.semaphore("s_dma%d" % i))
            block = st.enter_context(nc.Block())

            def run(engname):
                def body(eng):
                    for waits, fn, inc in self.ops[engname]:
                        for k, v in waits:
                            eng.wait_ge(sems[k], v)
                        if fn is None:
                            continue
                        ins = fn(eng)
                        ins.then_inc(sems[inc[0]], inc[1])
                return body
            block.tensor(run("pe"))
            block.scalar(run("act"))
            block.vector(run("dve"))
            block.gpsimd(run("pool"))
            block.sync(run("sp"))


class KB:
    def __init__(self, nc, nb, layers, dbg):
        self.nc = nc
        self.P = Prog(nc)
        self.nb = nb
        self.layers = layers
        self.dbg = dbg
        self.st = contextlib.ExitStack()
        self.din = {}
        self.dout = {}
        self.ps_i = 0
        self.psb_i = 0
        self.ps8_i = 0

    def tt(self, eng, out, a, b, op):
        self.P.op(eng, lambda e: e.tensor_tensor(out=out.ap, in0=a.ap, in1=b.ap, op=op), _bufs(a, b), _bufs(out), par=out.par_)

    def ts(self, eng, out, a, s1, op0, s2=None, op1=None):
        if op1 is None:
            self.P.op(eng, lambda e: e.tensor_scalar(out=out.ap, in0=a.ap, scalar1=_ap(s1), scalar2=None, op0=op0),
                      _bufs(a, s1), _bufs(out), par=out.par_)
        else:
            self.P.op(eng, lambda e: e.tensor_scalar(out=out.ap, in0=a.ap, scalar1=_ap(s1), scalar2=_ap(s2), op0=op0, op1=op1),
                      _bufs(a, s1, s2), _bufs(out), par=out.par_)

    def stt(self, out, a, s, b, op0, op1):
        self.P.op("dve", lambda e: e.scalar_tensor_tensor(out=out.ap, in0=a.ap, scalar=_ap(s), in1=b.ap, op0=op0, op1=op1),
                  _bufs(a, s, b), _bufs(out))

    def act(self, out, a, func, bias=0.0, scale=1.0, accum=None):
        if accum is None:
            self.P.op("act", lambda e: e.activation(out=out.ap, in_=a.ap, func=func, bias=_ap(bias), scale=_ap(scale)),
                      _bufs(a, bias, scale), _bufs(out), par=out.par_)
        else:
            self.P.op("act", lambda e: e.activation(out=out.ap, in_=a.ap, func=func, bias=_ap(bias), scale=_ap(scale), accum_out=accum.ap),
                      _bufs(a, bias, scale), _bufs(out, accum))

    def cp(self, eng, out, a):
        if eng == "act":
            self.P.op("act", lambda e: e.copy(out=out.ap, in_=a.ap), _bufs(a), _bufs(out), par=out.par_)
        else:
            self.P.op(eng, lambda e: e.tensor_copy(out=out.ap, in_=a.ap), _bufs(a), _bufs(out), par=out.par_)

    def memset(self, eng, out, v):
        self.P.op(eng, lambda e: e.memset(out.ap, v), [], _bufs(out), par=out.par_)

    def recip(self, out, a):
        self.P.op("dve", lambda e: e.reciprocal(out=out.ap, in_=a.ap), _bufs(a), _bufs(out))

    def mm(self, out, lhsT, rhs, start, stop):
        self.P.op("pe", lambda e: e.matmul(out.ap, lhsT=lhsT.ap, rhs=rhs.ap, start=start, stop=stop), _bufs(lhsT, rhs), _bufs(out))

    def tr(self, out, a, ident):
        self.P.op("pe", lambda e: e.transpose(out.ap, a.ap, ident.ap), _bufs(a, ident), _bufs(out))

    def scan(self, out, d0, d1, init):
        self.P.op("dve", lambda e: e.tensor_tensor_scan(out=out.ap, data0=d0.ap, data1=d1.ap, initial=_ap(init), op0=ALU.mult, op1=ALU.add),
                  _bufs(d0, d1, init), _bufs(out))

    def reduce(self, out, a, op=ALU.add):
        self.P.op("dve", lambda e: e.tensor_reduce(out=out.ap, in_=a.ap, axis=AX.X, op=op), _bufs(a), _bufs(out))

    def dma(self, q, out, a, slow=False):
        if slow:
            self.P.dma(q, lambda e: e.dma_start(out=out.ap, in_=a.ap, allow_slow_non_contiguous=True), _bufs(a), _bufs(out), par=out.par_)
        else:
            self.P.dma(q, lambda e: e.dma_start(out=out.ap, in_=a.ap), _bufs(a), _bufs(out), par=out.par_)

    def dram_in(self, name, shape, dt=F32):
        t = TV(self.nc.dram_tensor(name, list(shape), dt, kind="ExternalInput").ap(), Buf(name))
        self.din[name] = t
        return t

    def dram_out(self, name, shape, dt=F32):
        t = TV(self.nc.dram_tensor(name, list(shape), dt, kind="ExternalOutput").ap(), Buf(name))
        self.dout[name] = t
        return t

    def dram_tmp(self, name, shape, dt=F32):
        return TV(self.nc.dram_tensor(name, list(shape), dt, kind="Internal").ap(), Buf(name))

    def sb(self, name, shape, dt=F32):
        t = self.st.enter_context(self.nc.sbuf_tensor(name, list(shape), dt))
        return TV(t[:], Buf(name))

    def psum_f(self):
        i = self.ps_i % 6
        self.ps_i += 1
        return self.psf[i]

    def psum_b(self):
        i = self.psb_i % 2
        self.psb_i += 1
        return self.psb[i]

    def psum_any(self, bf16=False):
        i = self.ps8_i % 8
        self.ps8_i += 1
        return self.ps8b[i] if bf16 else self.ps8[i]


class Arena:
    def __init__(self, kb, words, ap=None):
        self.kb = kb
        self.words = words
        if ap is None:
            self.f = kb.st.enter_context(kb.nc.sbuf_tensor("arena_f", [128, words], F32))
        else:
            self.f = ap
        self.lo = 0
        self.hi = words

    def alloc(self, name, free, dt=F32, top=False, buf=None, at=None):
        nel = int(np.prod(free))
        w = nel if dt == F32 else (nel + 1) // 2
        w = ((w + 15) // 16) * 16
        if at is not None:
            off = at
        elif top:
            self.hi -= w
            off = self.hi
        else:
            off = self.lo
            self.lo += w
        assert self.lo <= self.hi, (name, self.lo, self.hi)
        assert off + w <= self.words
        if dt != F32:
            ap = self.f[:, off:off + w].bitcast(dt)[:, 0:nel]
        else:
            ap = self.f[:, off:off + nel]
        if len(free) > 1:
            names = " ".join("d%d" % i for i in range(len(free)))
            kw = {"d%d" % i: int(free[i]) for i in range(len(free))}
            ap = ap.rearrange("p (%s) -> p %s" % (names, names), **kw)
        tv = TV(ap, buf if buf is not None else Buf(name))
        tv.off = off
        tv.w = w
        return tv

    def reset(self, top=False):
        self.kb.P.barrier()
        self.lo = 0
        if top:
            self.hi = self.words


def lockstep(gens):
    gens = list(gens)
    while gens:
        nxt = []
        for g in gens:
            try:
                next(g)
                nxt.append(g)
            except StopIteration:
                pass
        gens = nxt


def build_program(nb=2, layers=(0, 1), dbg=None):
    nc = bass.Bass("TRN2", target_bir_lowering=False)
    kb = KB(nc, nb, layers, dbg)
    P = kb.P
    tt, ts, stt, act, cp, mm, tr, dma = kb.tt, kb.ts, kb.stt, kb.act, kb.cp, kb.mm, kb.tr, kb.dma
    dbg = dbg or set()
    PAR_OFF.clear()
    PAR_OFF.update(x[6:] for x in dbg if x.startswith("nopar_"))

    x_in = kb.dram_in("x", [nb, T, D])
    ctx_in = kb.dram_in("ctx", [nb, TC, D])
    cvec = kb.dram_in("cvec", [3, D])
    w_ada = kb.dram_in("w_ada", [DEPTH, D, 3 * D])
    b_ada = kb.dram_in("b_ada", [DEPTH, 3 * D])
    norm_g = kb.dram_in("norm_g", [DEPTH, D])
    w_in = kb.dram_in("w_in", [DEPTH, D, IN_W])
    mla_q_norm = kb.dram_in("mla_q_norm", [DEPTH, 256])
    mla_kv_norm = kb.dram_in("mla_kv_norm", [DEPTH, 128])
    mla_w_uq = kb.dram_in("mla_w_uq", [DEPTH, 256, 384])
    mla_w_ukv = kb.dram_in("mla_w_ukv", [DEPTH, 128, 512])
    mla_q_gain = kb.dram_in("mla_q_gain", [DEPTH, 96])
    mla_k_gain = kb.dram_in("mla_k_gain", [DEPTH, 96])
    s5_a_re = kb.dram_in("s5_a_re", [DEPTH, 2, 16, 64])
    s5_a_im = kb.dram_in("s5_a_im", [DEPTH, 2, 16, 64])
    s5_log_dt = kb.dram_in("s5_log_dt", [DEPTH, 2, 16])
    s5_b_re = kb.dram_in("s5_b_re", [DEPTH, 2, 16, 64, 16])
    s5_b_im = kb.dram_in("s5_b_im", [DEPTH, 2, 16, 64, 16])
    s5_c_re = kb.dram_in("s5_c_re", [DEPTH, 2, 16, 16, 64])
    s5_c_im = kb.dram_in("s5_c_im", [DEPTH, 2, 16, 16, 64])
    s5_d = kb.dram_in("s5_d", [DEPTH, 256])
    s5_w_glu = kb.dram_in("s5_w_glu", [DEPTH, 256, 256])
    lru_conv_w = kb.dram_in("lru_conv_w", [DEPTH, 4, 256])
    lru_conv_b = kb.dram_in("lru_conv_b", [DEPTH, 256])
    lru_lambda = kb.dram_in("lru_lambda", [DEPTH, 2, 256])
    lru_w_a = kb.dram_in("lru_w_a", [DEPTH, 2, 4, 64, 64])
    lru_b_a = kb.dram_in("lru_b_a", [DEPTH, 2, 256])
    lru_w_x = kb.dram_in("lru_w_x", [DEPTH, 2, 4, 64, 64])
    lru_b_x = kb.dram_in("lru_b_x", [DEPTH, 2, 256])
    pool_w = kb.dram_in("pool_w", [DEPTH, 4, 64, 64])
    pool_b = kb.dram_in("pool_b", [DEPTH, 256])
    pool_scale = kb.dram_in("pool_scale", [DEPTH, 256])
    w_branch = kb.dram_in("w_branch", [DEPTH, 4, 256, D])
    w_out = kb.dram_in("w_out", [DEPTH, D, D])
    c_ident = kb.dram_in("c_ident", [128, 128])
    c_ropeC = kb.dram_in("c_ropeC", [T, 32])
    c_ropeS = kb.dram_in("c_ropeS", [T, 32])
    c_pool = kb.dram_in("c_pool", [128, 2 + 16 + 16])
    c_mask = kb.dram_in("c_mask", [128, 3, 128])
    y_out = kb.dram_out("y", [nb, T, D])
    xmid = kb.dram_tmp("xmid", [nb, T, D])
    xcmid = kb.dram_tmp("xcmid", [nb, TC, D])
    modD = kb.dram_tmp("modD", [DEPTH, 3, 3 * D])
    st_main = kb.dram_tmp("st_main", [3, 128, 2048])
    st_dir = kb.dram_tmp("st_dir", [2, 128, 2048 + 16 * NCH + 8])
    dbg_out = {}

    def tap(name, shape):
        if name not in dbg_out:
            dbg_out[name] = kb.dram_out("dbg_" + name, shape)
        return dbg_out[name]

    kb.ps8 = []
    for i in range(8):
        t = kb.st.enter_context(nc.psum_tensor("psf%d" % i, [128, 512], F32))
        kb.ps8.append(TV(t[:], Buf("psf%d" % i)))
    kb.psf = kb.ps8[0:6]
    kb.psb = [TV(kb.ps8[i].ap.bitcast(BF16), kb.ps8[i].buf) for i in (6, 7)]
    kb.ps8b = [TV(kb.ps8[i].ap.bitcast(BF16), kb.ps8[i].buf) for i in range(8)]

    hT = kb.sb("hT", [128, KC, NPOS], BF16)
    bigbr = kb.sb("bigbr", [128, 8 * NPOS], BF16)
    _slot = {0: 0, 2: 2, 3: 4, 1: 6}
    brT = [[TV(bigbr.ap[:, (_slot[n] + c) * NPOS:(_slot[n] + c + 1) * NPOS], Buf("brT%d%d" % (n, c))) for c in range(2)]
           for n in range(4)]
    identf = kb.sb("identf", [128, 128])
    identb = kb.sb("identb", [128, 128], BF16)
    ropeC = kb.sb("ropeC", [128, 16, 32])
    ropeS = kb.sb("ropeS", [128, 16, 32])
    cpool = kb.sb("cpool", [128, 34])
    cmask = kb.sb("cmask", [128, 3, 128])
    modA = kb.sb("modA", [128, 3, KC])
    modS = kb.sb("modS", [128, 3, KC])
    lp = kb.sb("lp", [128, 64])
    lruBD = kb.sb("lruBD", [128, 2, 2, 2, 128])
    poolBD = kb.sb("poolBD", [128, 2, 128])
    wukv = kb.sb("wukv", [128, 512], BF16)
    wuq = kb.sb("wuq", [128, 2, 384], BF16)
    gains = kb.sb("gains", [128, 2, 96])
    wglu = kb.sb("wglu", [128, 2, 256], BF16)
    AR = Arena(kb, 28000)
    wst = kb.sb("wst", [128, KC, 416], BF16)

    def prefetch(l, parts):
        for (d0, c0, n) in parts:
            dma("pool", wst[:, :, d0:d0 + n].par("wst"), w_in[l, :, c0:c0 + n].re("(c p) n -> p c n", p=128))
    PF_S5 = [(0, O_S5, 256)]
    PF_LRU = [(0, O_LRU, 256)]
    PF_POOL = [(0, O_POOL, 256)]
    PF_MLA = [(0, 0, 160), (160, O_CQ, 256)]
    PF_GATE0 = [(0, O_GATE, 128)]
    AR2 = Arena(kb, 3 * NPOS, ap=bigbr.ap[:, 0:6 * NPOS].bitcast(F32))

    dma("sp", identf, c_ident)
    cp("dve", identb, identf)
    dma("sp", ropeC, c_ropeC.re("(i p) f -> p i f", p=128))
    dma("sp", ropeS, c_ropeS.re("(i p) f -> p i f", p=128))
    dma("sp", cpool, c_pool)
    dma("sp", cmask, c_mask)

    LP_CW = 0
    LP_CB = 8
    LP_NSP = 10
    LP_NSP2 = 14
    LP_BA = 18
    LP_BX = 22
    LP_PSC = 26
    LP_PBS = 28
    LP_G = 30
    LP_TMP = 40

    def prep_layer(l):
        AR.reset(top=True)
        cT = AR.alloc("cT", [KC, 3])
        for v in range(3):
            dma("sp", cT[:, :, v], cvec[v].re("(c p) -> p c", p=128), slow=True)
        cact = AR.alloc("cact", [KC, 3])
        act(cact, cT, AF.Silu)
        brow = AR.alloc("brow", [3 * D])
        dma("sp", brow[0:3, :], b_ada[l:l + 1, :].re("o n -> (o n)").pbc(3))
        modrow = AR.alloc("modrow", [3 * D])
        wa = [AR.alloc("wa%d" % i, [KC, 512]) for i in range(4)]
        for cb in range(4):
            for hh in range(2):
                dma("sp" if hh == 0 else "act", wa[cb][:, hh * 4:(hh + 1) * 4, :].par("wa"),
                    w_ada[l, hh * 512:(hh + 1) * 512, cb * 512:(cb + 1) * 512].re("(c p) n -> p c n", p=128))
        for cb in range(6):
            w = wa[cb % 4]
            if cb >= 4:
                for hh in range(2):
                    dma("sp" if hh == 0 else "act", w[:, hh * 4:(hh + 1) * 4, :].par("wa"),
                        w_ada[l, hh * 512:(hh + 1) * 512, cb * 512:(cb + 1) * 512].re("(c p) n -> p c n", p=128))
            ps = kb.psum_f()
            for k in range(KC):
                mm(ps[0:3, :], cact[:, k, :], w[:, k, :], k == 0, k == KC - 1)
            tt("dve", modrow[0:3, cb * 512:(cb + 1) * 512], ps[0:3, :], brow[0:3, cb * 512:(cb + 1) * 512], ALU.add)
        dma("sp", modD[l], modrow[0:3, :])
        sc = AR.alloc("sc", [3, KC])
        for v in range(3):
            dma("sp", modS[:, v, :], modD[l, v, 0:D].re("(c p) -> p c", p=128), slow=True)
            dma("sp", sc[:, v, :], modD[l, v, D:2 * D].re("(c p) -> p c", p=128), slow=True)
        dma("sp", lp[:, LP_G:LP_G + 8], norm_g[l].re("(c p) -> p c", p=128), slow=True)
        for v in range(3):
            stt(modA[:, v, :], sc[:, v, :], 1.0, lp[:, LP_G:LP_G + 8], ALU.add, ALU.mult)
        for k in range(4):
            dma("sp", lp[:, LP_CW:LP_CW + 8].re("p (c k) -> p c k", c=2)[:, :, k], lru_conv_w[l, k].re("(c p) -> p c", p=128), slow=True)
        dma("sp", lp[:, LP_CB:LP_CB + 2], lru_conv_b[l].re("(c p) -> p c", p=128), slow=True)
        lam = lp[:, LP_TMP:LP_TMP + 4]
        for d in range(2):
            dma("sp", lam[:, d * 2:d * 2 + 2], lru_lambda[l, d].re("(c p) -> p c", p=128), slow=True)
            dma("sp", lp[:, LP_BA + d * 2:LP_BA + d * 2 + 2], lru_b_a[l, d].re("(c p) -> p c", p=128), slow=True)
            dma("sp", lp[:, LP_BX + d * 2:LP_BX + d * 2 + 2], lru_b_x[l, d].re("(c p) -> p c", p=128), slow=True)
        t0 = lp[:, LP_TMP + 4:LP_TMP + 8]
        t1 = lp[:, LP_TMP + 8:LP_TMP + 12]
        t2 = lp[:, LP_TMP + 12:LP_TMP + 16]
        t3 = lp[:, LP_TMP + 16:LP_TMP + 20]
        ts("dve", t0, lam, -1.0, ALU.mult)
        tt("dve", t0, t0, lam, ALU.max)
        act(t1, t0, AF.Exp, scale=-1.0)
        ts("dve", t2, t1, 2.0, ALU.add)
        kb.recip(t2, t2)
        tt("dve", t2, t2, t1, ALU.mult)
        tt("dve", t3, t2, t2, ALU.mult)
        ts("dve", t0, t3, 1.0 / 11.0, ALU.mult, 1.0 / 9.0, ALU.add)
        for cf in (1.0 / 7.0, 1.0 / 5.0, 1.0 / 3.0, 1.0):
            tt("dve", t0, t0, t3, ALU.mult)
            ts("dve", t0, t0, cf, ALU.add)
        tt("dve", t0, t0, t2, ALU.mult)
        ts("dve", t1, lam, -1.0, ALU.mult, 0.0, ALU.max)
        stt(t0, t0, 2.0, t1, ALU.mult, ALU.add)
        ts("dve", lp[:, LP_NSP:LP_NSP + 4], t0, -8.0, ALU.mult)
        ts("dve", lp[:, LP_NSP2:LP_NSP2 + 4], t0, -16.0, ALU.mult)
        kb.memset("dve", lruBD, 0.0)
        for d in range(2):
            for gi, wsrc in enumerate((lru_w_a, lru_w_x)):
                for c in range(2):
                    for h in range(2):
                        dma("sp" if (c + h) % 2 == 0 else "act", lruBD[h * 64:(h + 1) * 64, d, gi, c, h * 64:(h + 1) * 64].par("ldw"), wsrc[l, d, 2 * c + h])
        kb.memset("dve", poolBD, 0.0)
        for c in range(2):
            for h in range(2):
                dma("sp", poolBD[h * 64:(h + 1) * 64, c, h * 64:(h + 1) * 64].par("ldw"), pool_w[l, 2 * c + h])
        dma("sp", lp[:, LP_PSC:LP_PSC + 2], pool_scale[l].re("(c p) -> p c", p=128), slow=True)
        dma("sp", lp[:, LP_PBS:LP_PBS + 2], pool_b[l].re("(c p) -> p c", p=128), slow=True)
        tt("dve", lp[:, LP_PBS:LP_PBS + 2], lp[:, LP_PBS:LP_PBS + 2], lp[:, LP_PSC:LP_PSC + 2], ALU.mult)
        kvn = lp[:, LP_TMP + 20:LP_TMP + 21]
        qn = lp[:, LP_TMP + 21:LP_TMP + 23]
        dma("sp", kvn, mla_kv_norm[l].re("(p o) -> p o", o=1), slow=True)
        dma("sp", qn, mla_q_norm[l].re("(c p) -> p c", p=128), slow=True)
        wtmp = AR.alloc("wtmp", [2, 512])
        dma("sp", wtmp[:, 0, :], mla_w_ukv[l])
        ts("dve", wukv, wtmp[:, 0, :], kvn, ALU.mult)
        wtmp2 = AR.alloc("wtmp2", [2, 384])
        dma("sp", wtmp2, mla_w_uq[l].re("(c p) n -> p c n", p=128))
        for c in range(2):
            ts("dve", wuq[:, c, :], wtmp2[:, c, :], qn[:, c:c + 1], ALU.mult)
        dma("sp", gains[:, 0, :], mla_q_gain[l:l + 1, :].re("o n -> (o n)").pbc(128))
        dma("sp", gains[:, 1, :], mla_k_gain[l:l + 1, :].re("o n -> (o n)").pbc(128))
        ts("dve", gains[:, 0, :], gains[:, 0, :], 96.0 ** -0.5, ALU.mult)
        dma("pool", wglu, s5_w_glu[l].re("(c p) n -> p c n", p=128))

    def src_tile(l, b, i):
        if i < 2:
            return (ctx_in if l == 0 else xcmid)[b, i * 128:(i + 1) * 128, :]
        return (x_in if l == 0 else xmid)[b, (i - 2) * 128:(i - 1) * 128, :]

    def phase_norm(l, b):
        AR.reset(top=True)
        NX = 4
        xt = [AR.alloc("xt%d" % i, [D]) for i in range(NX)]
        junk = [AR.alloc("junk%d" % i, [D]) for i in range(NX)]
        xn = [AR.alloc("xn%d" % i, [D], BF16) for i in range(NX)]
        st4 = [AR.alloc("st%d" % i, [4]) for i in range(NX)]
        def tile_gen(i):
            v = 2 if i < 2 else b
            x_t, xn_t, s4 = xt[i % NX], xn[i % NX], st4[i % NX]
            dma("sp" if i % 2 == 0 else "act", x_t, src_tile(l, b, i))
            yield
            act(junk[i % NX], x_t, AF.Square, accum=s4[:, 0:1])
            yield
            ts("dve", s4[:, 1:2], s4[:, 0:1], 1.0 / D, ALU.mult, EPS, ALU.add)
            yield
            act(s4[:, 2:3], s4[:, 1:2], AF.Sqrt)
            yield
            kb.recip(s4[:, 3:4], s4[:, 2:3])
            ts("dve", xn_t, x_t, s4[:, 3:4], ALU.mult)
            yield
            psA = kb.psum_any(bf16=True)
            psB = kb.psum_any(bf16=True)
            for c in range(KC):
                pz = psA if c % 2 == 0 else psB
                tr(pz[:, (c // 2) * 128:(c // 2 + 1) * 128], xn_t[:, c * 128:(c + 1) * 128], identb)
            yield
            for c in range(KC):
                o = hT[:, c, i * 128:(i + 1) * 128].par("norm")
                if c % 2 == 0:
                    act(o, psA[:, (c // 2) * 128:(c // 2 + 1) * 128], AF.Identity, bias=modS[:, v, c:c + 1], scale=modA[:, v, c:c + 1])
                else:
                    ts("dve", o, psB[:, (c // 2) * 128:(c // 2 + 1) * 128], modA[:, v, c:c + 1], ALU.mult, modS[:, v, c:c + 1], ALU.add)
            yield
        for g0 in range(0, 18, NX):
            lockstep([tile_gen(i) for i in range(g0, min(g0 + NX, 18))])

    def load_win(l, name, c0, ncols, top=False):
        w = AR.alloc(name, [KC, ncols], BF16, top=top)
        dma("pool", w, w_in[l, :, c0:c0 + ncols].re("(c p) n -> p c n", p=128))
        return w

    def proj_fm(w, col0, dst, p0, p1, evac):
        pos = p0
        while pos < p1:
            n = min(512, p1 - pos)
            ps = kb.psum_f()
            for k in range(KC):
                mm(ps[:, 0:n], w[:, k, col0:col0 + 128], hT[:, k, pos:pos + n], k == 0, k == KC - 1)
            evac(ps, pos, n)
            pos += n

    def phase_lru(l, b, with_ctx):
        AR.reset()
        w = wst[:, :, 0:256]
        xr = AR.alloc("xr", [2, NPOS])
        xc_ = AR.alloc("xc", [2, NPOS])
        ysum = AR.alloc("ysum", [2, NPOS])
        NB_ = 512
        tmp_sets = [{nm: AR.alloc("%s%d" % (nm, q), [NB_]) for nm in ("r", "i", "a", "a2", "bb")} for q in range(4)]
        for c in range(2):
            proj_fm(w, c * 128, xr, 0, NPOS, lambda ps, pos, n, c=c: cp("act", xr[:, c, pos:pos + n].par("proj"), ps[:, 0:n]))
        prefetch(l, PF_POOL)
        for c in range(2):
            cw = lambda k: lp[:, LP_CW + c * 4 + k:LP_CW + c * 4 + k + 1]
            for (s0, s1) in ((0, TC), (TC, NPOS)):
                ts("dve", xc_[:, c, s0:s1], xr[:, c, s0:s1], cw(2), ALU.mult, lp[:, LP_CB + c:LP_CB + c + 1], ALU.add)
                stt(xc_[:, c, s0 + 1:s1], xr[:, c, s0:s1 - 1], cw(1), xc_[:, c, s0 + 1:s1], ALU.mult, ALU.add)
                stt(xc_[:, c, s0 + 2:s1], xr[:, c, s0:s1 - 2], cw(0), xc_[:, c, s0 + 2:s1], ALU.mult, ALU.add)
                stt(xc_[:, c, s0:s1 - 1], xr[:, c, s0 + 1:s1], cw(3), xc_[:, c, s0:s1 - 1], ALU.mult, ALU.add)
        blocks = [(0, TC)] + [(TC + j * NB_, min(TC + (j + 1) * NB_, NPOS)) for j in range((T + NB_ - 1) // NB_)]

        def chain(d, c):
            dc = d * 2 + c
            tmp = tmp_sets[dc]
            out = ysum if d == 0 else xr
            order = blocks if d == 0 else [blocks[0]] + blocks[:0:-1]
            prev = None
            for (s0, s1) in order:
                n = s1 - s0
                r_, i_, a_, a2_, bb_ = (tmp[k][:, 0:n] for k in ("r", "i", "a", "a2", "bb"))
                for gi, dst in ((0, r_), (1, i_)):
                    q = 0
                    while q < n:
                        m = min(512, n - q)
                        ps = kb.psum_f()
                        mm(ps[:, 0:m], lruBD[:, d, gi, c, :], xc_[:, c, s0 + q:s0 + q + m], True, True)
                        bcol = (LP_BA if gi == 0 else LP_BX) + dc
                        act(dst[:, q:q + m], ps[:, 0:m], AF.Sigmoid, bias=lp[:, bcol:bcol + 1])
                        q += m
                yield
                act(a_, r_, AF.Exp, scale=lp[:, LP_NSP + dc:LP_NSP + dc + 1])
                act(a2_, r_, AF.Exp, scale=lp[:, LP_NSP2 + dc:LP_NSP2 + dc + 1])
                tt("pool", bb_, i_, xc_[:, c, s0:s1], ALU.mult)
                yield
                ts("dve", a2_, a2_, -1.0, ALU.mult, 1.0, ALU.add)
                yield
                act(a2_, a2_, AF.Sqrt)
                yield
                tt("dve", bb_, bb_, a2_, ALU.mult)
                init = 0.0 if prev is None else prev
                if d == 0:
                    kb.scan(out[:, c, s0:s1], a_, bb_, init)
                    prev = out[:, c, s1 - 1:s1]
                else:
                    kb.scan(out[:, c, s0:s1][:, ::-1], a_[:, ::-1], bb_[:, ::-1], init)
                    prev = out[:, c, s0:s0 + 1]
                yield
        lockstep([chain(0, 0), chain(1, 0), chain(0, 1), chain(1, 1)])
        for c in range(2):
            lo = 0 if with_ctx else TC
            tt("dve", brT[2][c][:, lo:NPOS], ysum[:, c, lo:NPOS], xr[:, c, lo:NPOS], ALU.add)

    AR_car = [kb.sb("car%d" % i, [128, 1]) for i in range(2)]

    def phase_pool(l, b, with_ctx):
        AR.reset()
        w = wst[:, :, 0:256]
        xp = AR.alloc("xp", [2, NPOS])
        cs = AR.alloc("cs", [2, NPOS + 2 * 17 + 2])
        pm = AR.alloc("pm", [2, NPOS])
        ones = AR.alloc("ones", [T])
        kb.memset("pool", ones, 1.0)
        for c in range(2):
            proj_fm(w, c * 128, xp, 0, NPOS, lambda ps, pos, n, c=c: cp("act", xp[:, c, pos:pos + n].par("proj"), ps[:, 0:n]))
        prefetch(l, PF_MLA)
        segs = [(0, TC, 0)] + [(TC, NPOS, TC + 17)]
        if not with_ctx:
            segs = segs[1:]
        for c in range(2):
            for (s0, s1, o0) in segs:
                L = s1 - s0
                kb.memset("pool", cs[:, c, o0:o0 + 9], 0.0)
                kb.scan(cs[:, c, o0 + 9:o0 + 9 + L], ones[:, 0:L], xp[:, c, s0:s1], 0.0)
                ts("dve", cs[:, c, o0 + 9 + L:o0 + 17 + L], cs[:, c, o0:o0 + 8], cs[:, c, o0 + 8 + L:o0 + 9 + L], ALU.add)
                for h in range(2):
                    hw = (1, 2, 4, 8)[2 * c + h]
                    rows = slice(h * 64, (h + 1) * 64)
                    base = o0 + 8
                    tt("dve", pm[rows, c, s0:s1], cs[rows, c, base + hw:base + hw + L], cs[rows, c, base - hw:base - hw + L], ALU.subtract)
                ts("dve", pm[:, c, s0:s1], pm[:, c, s0:s1], cpool[:, c:c + 1], ALU.mult)
                tt("dve", pm[:, c, s0:s0 + 8], pm[:, c, s0:s0 + 8], cpool[:, 2 + c * 8:2 + c * 8 + 8], ALU.mult)
                tt("dve", pm[:, c, s1 - 8:s1], pm[:, c, s1 - 8:s1], cpool[:, 18 + c * 8:18 + c * 8 + 8], ALU.mult)
                tt("pool", pm[:, c, s0:s1], pm[:, c, s0:s1], xp[:, c, s0:s1], ALU.subtract)
                pos = s0
                while pos < s1:
                    n = min(512, s1 - pos)
                    ps = kb.psum_f()
                    mm(ps[:, 0:n], poolBD[:, c, :], pm[:, c, pos:pos + n], True, True)
                    act(brT[3][c][:, pos:pos + n].par("pool"), ps[:, 0:n], AF.Identity, bias=lp[:, LP_PBS + c:LP_PBS + c + 1],
                        scale=lp[:, LP_PSC + c:LP_PSC + c + 1])
                    pos += n

    def phase_mla(l, b, with_ctx):
        AR.reset()
        wkv = wst[:, :, 0:160]
        wq = wst[:, :, 160:416]
        qT = AR.alloc("qT", [4, NPOS], BF16)
        kT = AR.alloc("kT", [4, NPOS], BF16)
        Vt = AR.alloc("Vt", [18, 4, 68], BF16)
        lo_fixed = AR.lo
        kb.memset("dve", Vt.re("p a b c -> p (a b c)"), 1.0)
        for h_ in range(4):
            kb.memset("pool", qT[:, h_, :], 0.0)
            kb.memset("pool", kT[:, h_, :], 0.0)
        NS = 3
        sm = [AR.alloc("sm%d" % i, [32]) for i in range(NS)]
        kr = [AR.alloc("kr%d" % i, [32]) for i in range(NS)]
        cn = [AR.alloc("cn%d" % i, [384], BF16) for i in range(NS)]
        cnT = [AR.alloc("cnT%d" % i, [3, 128], BF16) for i in range(NS)]
        sq = [AR.alloc("sq%d" % i, [704]) for i in range(NS)]
        qk = [AR.alloc("qk%d" % i, [2, 4, 96]) for i in range(NS)]
        rg = [AR.alloc("rg%d" % i, [2, 4, 96]) for i in range(NS)]
        qkb = [AR.alloc("qkb%d" % i, [2, 4, 96], BF16) for i in range(NS)]
        rt = [AR.alloc("rt%d" % i, [2, 2, 4, 32]) for i in range(NS)]
        kvs_ = [AR.alloc("kvs%d" % i, [512]) for i in range(NS)]
        qs_ = [AR.alloc("qs%d" % i, [384]) for i in range(NS)]

        def info(i):
            is_ctx = i < 2
            do_q = (not is_ctx) or with_ctx
            return is_ctx, do_q, i % NS, slice(i * 128, (i + 1) * 128)

        def stage_a(i):
            is_ctx, do_q, j, pos = info(i)
            s, cn_, cnT_, sq_ = sm[j], cn[j], cnT[j], sq[j]
            ps1 = kb.psum_any()
            for k in range(KC):
                mm(ps1[:, 0:160], hT[:, k, pos], wkv[:, k, :], k == 0, k == KC - 1)
            if do_q:
                for k in range(KC):
                    mm(ps1[:, 160:416], hT[:, k, pos], wq[:, k, :], k == 0, k == KC - 1)
            yield
            act(sq_[:, 0:128], ps1[:, 0:128], AF.Square, accum=s[:, 0:1])
            if do_q:
                act(sq_[:, 128:384], ps1[:, 160:416], AF.Square, accum=s[:, 1:2])
            else:
                kb.memset("pool", s[:, 1:2], 1.0)
            cp("act", kr[j], ps1[:, 128:160])
            yield
            act(s[:, 2:3], s[:, 0:1], AF.Sqrt, scale=1.0 / 128, bias=EPS)
            act(s[:, 3:4], s[:, 1:2], AF.Sqrt, scale=1.0 / 256, bias=EPS)
            yield
            kb.recip(s[:, 4:6], s[:, 2:4])
            ts("dve", cn_[:, 0:128], ps1[:, 0:128], s[:, 4:5], ALU.mult)
            if do_q:
                ts("dve", cn_[:, 128:384], ps1[:, 160:416], s[:, 5:6], ALU.mult)
            yield
            psT = kb.psum_any(bf16=True)
            for c in range(3 if do_q else 1):
                tr(psT[:, c * 128:(c + 1) * 128], cn_[:, c * 128:(c + 1) * 128], identb)
            yield
            cp("act", cnT_[:, 0:(3 if do_q else 1), :], psT[:, 0:(384 if do_q else 128)].re("p (c n) -> p c n", n=128))
            yield

        def stage_b(i):
            is_ctx, do_q, j, pos = info(i)
            s, cnT_, sq_, qk_, qkb_, rt_, rg_ = sm[j], cnT[j], sq[j], qk[j], qkb[j], rt[j], rg[j]
            pskv = kb.psum_any()
            mm(pskv, cnT_[:, 0, :], wukv, True, True)
            if do_q:
                psq = kb.psum_any()
                for c in range(2):
                    mm(psq[:, 0:384], cnT_[:, 1 + c, :], wuq[:, c, :], c == 0, c == 1)
            yield
            cp("act", kvs_[j], pskv)
            if do_q:
                cp("act", qs_[j], psq[:, 0:384])
            kv3 = kvs_[j].re("p (h e) -> p h e", h=4)
            q3 = qs_[j].re("p (h e) -> p h e", h=4)
            krope = kr[j]
            act(sq_[:, 640:672], krope, AF.Square, accum=s[:, 7:8])
            cp("act", Vt[:, i, :, 0:64].par("mlav"), kv3[:, :, 64:128])
            yield
            ksq = sq_[:, 0:256].re("p (h e) -> p h e", h=4)
            qsq = sq_[:, 256:640].re("p (h e) -> p h e", h=4)
            tt("pool", ksq, kv3[:, :, 0:64], kv3[:, :, 0:64], ALU.mult)
            if do_q:
                tt("pool", qsq, q3, q3, ALU.mult)
            yield
            kb.reduce(s[:, 12:16], ksq)
            if do_q:
                kb.reduce(s[:, 8:12], qsq)
            else:
                kb.memset("dve", s[:, 8:12], 1.0)
            ts("dve", s[:, 12:16], s[:, 12:16], s[:, 7:8], ALU.add)
            yield
            act(s[:, 8:16], s[:, 8:16], AF.Sqrt, scale=1.0 / 96, bias=EPS)
            yield
            kb.recip(s[:, 16:24], s[:, 8:16])
            yield
            tt("pool", rg_, s[:, 16:24].re("p (w h) -> p w h", w=2).us(3).bc([128, 2, 4, 96]),
               gains.us(2).bc([128, 2, 4, 96]), ALU.mult)
            yield
            if do_q:
                tt("dve", qk_[:, 0].par("qk"), q3, rg_[:, 0], ALU.mult)
            else:
                kb.memset("pool", qk_[:, 0].par("qk"), 0.0)
            tt("dve", qk_[:, 1, :, 0:64].par("qk"), kv3[:, :, 0:64], rg_[:, 1, :, 0:64], ALU.mult)
            tt("pool", qk_[:, 1, :, 64:96].par("qk"), krope.us(1).bc([128, 4, 32]), rg_[:, 1, :, 64:96], ALU.mult)
            yield
            if not is_ctx:
                ti = i - 2
                v = qk_[:, :, :, 64:96]
                t1_ = rt_[:, 0].par("rt")
                t2_ = rt_[:, 1]
                Cb = ropeC[:, ti, :].us(1).us(1).bc([128, 2, 4, 32])
                tt("pool", t1_, v, Cb, ALU.mult)
                for a in range(2):
                    for s_ in range(2):
                        o_ = t2_[:, :, :, a * 16 + s_ * 8:a * 16 + s_ * 8 + 8].par("rt")
                        i_ = qk_[:, :, :, 64 + a * 16 + (1 - s_) * 8:64 + a * 16 + (1 - s_) * 8 + 8]
                        Sb = ropeS[:, ti, a * 16 + s_ * 8:a * 16 + s_ * 8 + 8].us(1).us(1).bc([128, 2, 4, 8])
                        tt("dve" if (a + s_) % 2 == 0 else "pool", o_, i_, Sb, ALU.mult)
                yield
                tt("dve", v, t1_, t2_, ALU.add)
            cp("dve", qkb_, qk_)
            yield

        def stage_c(i):
            is_ctx, do_q, j, pos = info(i)
            qkb_ = qkb[j]
            psT2 = kb.psum_any(bf16=True)
            for w_ in range(2):
                if w_ == 0 and not do_q:
                    continue
                for h in range(4):
                    tr(psT2[0:96, (w_ * 4 + h) * 128:(w_ * 4 + h + 1) * 128], qkb_[:, w_, h, :], identb)
            yield
            if do_q:
                cp("act", qT[0:96, :, pos].par("mlaqk"), psT2[0:96, 0:512].re("p (h n) -> p h n", h=4))
            cp("act", kT[0:96, :, pos].par("mlaqk"), psT2[0:96, 512:1024].re("p (h n) -> p h n", h=4))

        def tile_gen(i):
            yield from stage_a(i)
            yield from stage_b(i)
            yield from stage_c(i)
        for g0 in range(0, 18, NS):
            lockstep([tile_gen(i) for i in range(g0, g0 + NS)])
        if "mla_stop1" in dbg:
            return
        prefetch(l, PF_GATE0)
        P.barrier()
        AR.lo = lo_fixed
        omla = AR.alloc("omla", [18, 256], BF16)
        pT = [AR.alloc("pT%d" % i, [512], BF16) for i in range(3)]
        rc = [AR.alloc("rc%d" % i, [4]) for i in range(2)]
        jobs = []
        if with_ctx:
            jobs.append((0, TC, 0, 2))
        for qb in range(4):
            jobs.append((TC + qb * 512, TC + (qb + 1) * 512, 0, 18))
        steps = []
        for (q0, q1, kt0, kt1) in jobs:
            for h in range(4):
                for kt in range(kt0, kt1):
                    steps.append((q0, q1, kt0, kt1, h, kt))

        def s_mm(si):
            q0, q1, kt0, kt1, h, kt = steps[si]
            mm(kb.psf[si % 2][:, 0:q1 - q0], kT[:, h, kt * 128:(kt + 1) * 128], qT[:, h, q0:q1], True, True)
        s_mm(0)
        for si, (q0, q1, kt0, kt1, h, kt) in enumerate(steps):
            nq = q1 - q0
            nqt = nq // 128
            pso = [kb.psf[2 + t_] for t_ in range(nqt)]
            if si + 1 < len(steps):
                s_mm(si + 1)
            pss = kb.psf[si % 2]
            p_ = pT[si % 3]
            act(p_[:, 0:nq], pss[:, 0:nq], AF.Exp)
            for t_ in range(nqt):
                mm(pso[t_][:, 0:68], p_[:, t_ * 128:(t_ + 1) * 128], Vt[:, kt, h, :], kt == kt0, kt == kt1 - 1)
            if kt == kt1 - 1:
                for t_ in range(nqt):
                    r_ = rc[t_ % 2]
                    kb.recip(r_[:, 0:1], pso[t_][:, 64:65])
                    ts("dve", omla[:, (q0 // 128) + t_, h * 64:(h + 1) * 64].par("omla"), pso[t_][:, 0:64], r_[:, 0:1], ALU.mult)
        for i in range(0 if with_ctx else 2, 18):
            psT = kb.psum_b()
            for c in range(2):
                tr(psT[:, c * 128:(c + 1) * 128], omla[:, i, c * 128:(c + 1) * 128], identb)
            for c in range(2):
                cp("act", brT[0][c][:, i * 128:(i + 1) * 128].par("brt0"), psT[:, c * 128:(c + 1) * 128])

    def phase_s5(l, b, with_ctx):
        AR.reset()
        AR2.lo = 0
        U = AR.alloc("U", [16, NCH])
        SN = [[AR.alloc("SN%d%d" % (d, ri), [8, NCH + 2]) for ri in range(2)] for d in range(2)]
        Ere = AR.alloc("Ere", [2, 8, 128])
        nEim = AR.alloc("nEim", [2, 8, 128])
        W3 = AR.alloc("W3", [16, 128])
        scr0 = AR.lo
        W1 = [AR2.alloc("W1%d" % i, [16, 64]) for i in range(2)]
        tab = [AR2.alloc("tab%d" % i, [8, NCH]) for i in range(2)]
        cached = (b > 0) and ("s5_nocache" not in dbg)

        def load_dir(d):
            dma("sp", W1[0].re("p g n -> p (g n)"), st_dir[d, :, 0:1024])
            dma("sp", W1[1].re("p g n -> p (g n)"), st_dir[d, :, 1024:2048])
            dma("act", tab[0].re("p g n -> p (g n)"), st_dir[d, :, 2048:2048 + 8 * NCH])
            dma("act", tab[1].re("p g n -> p (g n)"), st_dir[d, :, 2048 + 8 * NCH:2048 + 16 * NCH])
        if cached:
            dma("sp", Ere.re("p d g n -> p (d g n)"), st_main[0])
            dma("act", nEim.re("p d g n -> p (d g n)"), st_main[1])
            dma("sp", W3.re("p g n -> p (g n)"), st_main[2])
            load_dir(0)
        ws5 = wst[:, :, 0:256]
        Uc = AR.alloc("Uc", [16, 8, 16])
        kblocks = [(0, 32, 0), (32, 128, TC), (160, 128, TC + 1024)]
        for (k0, M, p0) in kblocks:
            pss = [kb.psum_f() for _ in range(4)]
            for r in range(8):
                ps = pss[r // 2]
                for k in range(KC):
                    mm(ps[0:M, (r % 2) * 256:(r % 2 + 1) * 256], hT[:, k, p0 + r:p0 + 8 * M:8], ws5[:, k, :], k == 0, k == KC - 1)
            for q in range(4):
                src = pss[q][0:M, :].re("p (r g i) -> p r g i", r=2, g=16)
                dst = Uc[0:M, :, 2 * q:2 * q + 2, :].re("p g r i -> p r g i").par("s5uc")
                cp("act" if q % 2 == 0 else "dve", dst, src)
            for gq in range(4):
                ps = kb.psum_f()
                for gg_ in range(4):
                    g = gq * 4 + gg_
                    tr(ps[:, gg_ * 128:gg_ * 128 + M], Uc[0:M, g, :, :].re("p r i -> p (r i)"), identf[0:M, 0:M])
                cp("act" if gq % 2 == 0 else "dve", U[:, gq * 4:gq * 4 + 4, k0:k0 + M].par("s5u"),
                   ps.re("p (g n) -> p g n", g=4)[:, :, 0:M])
        prefetch(l, PF_LRU)
        P.barrier()
        AR.lo = scr0
        if "s5_stop1" in dbg:
            return
        if not cached:
            kb.memset("pool", W3, 0.0)
        prm = AR.alloc("prm", [40, 8])
        PW = [AR.alloc("pw%d" % i, [10, 8]) for i in range(2)]
        Bb = [AR.alloc("Bb%d" % i, [8, 16]) for i in range(2)]
        Braw = [AR.alloc("Braw%d" % i, [8, 16]) for i in range(2)]
        Craw = [AR.alloc("Craw%d" % i, [8, 16]) for i in range(2)]
        Dbc = AR.alloc("Dbc", [256])
        t8 = AR.alloc("t8", [8, 16])
        w3t = [AR.alloc("w3t%d" % i, [128]) for i in range(2)]
        tE = AR.alloc("tE", [8, 128])
        Fm = [AR.alloc("F%d" % i, [8, 8, 16]) for i in range(2)]
        Ep = [AR.alloc("Ep%d" % i, [8, 128]) for i in range(2)]
        Cnat = [AR.alloc("Cnat%d" % i, [16, 64], at=Ep[i].off, buf=Ep[i].buf) for i in range(2)]
        rs_sets = [[AR.alloc("rs%d_%d" % (q, i), [NCH], at=Fm[0].off + (q * 7 + i) * NCH) for i in range(6)]
                   for q in range(2)]
        rho_sets = [AR.alloc("rho1_%d" % q, [NCH], at=Fm[0].off + (q * 7 + 6) * NCH) for q in range(2)]
        assert Fm[0].off + 14 * NCH <= Ep[1].off + Ep[1].w
        dma("sp", Dbc, s5_d[l:l + 1, :].re("o n -> (o n)").pbc(128))
        dsel = cmask[:, 2, :]

        prm_bufs = [Buf("prm%d" % i) for i in range(40)]
        pw_bufs = [[Buf("pw%d_%d" % (ri, j)) for j in range(10)] for ri in range(2)]

        def pv(i):
            return TV(prm.ap[:, i, :], prm_bufs[i])

        def pw(ri, j):
            return TV(PW[ri].ap[:, j, :], pw_bufs[ri][j])

        for d in range(2):
            if d == 1 and not cached:
                P.barrier()
            if not cached:
                dma("sp", pv(0), s5_a_re[l, d].re("(gp h) p -> (h p) gp", h=2), slow=True)
                dma("act", pv(1), s5_a_im[l, d].re("(gp h) p -> (h p) gp", h=2), slow=True)
                ldt = t8[:, 0, :]
                dma("sp", ldt, s5_log_dt[l, d:d + 1, :].re("o g -> (o g)").pbc(128))
                dma("sp", Cnat[0][0:16], s5_c_re[l, d].re("g o p -> o g p"))
                dma("act", Cnat[1][0:16], s5_c_im[l, d].re("g o p -> o g p"))
                for h in range(2):
                    rows = slice(h * 64, (h + 1) * 64)
                    cp("dve", pv(2)[rows], ldt[rows, h:16:2])
                    dma("sp", Braw[0][rows].par("ldw"), s5_b_re[l, d].re("(gp h) p i -> h p gp i", h=2)[h])
                    dma("act", Braw[1][rows].par("ldw"), s5_b_im[l, d].re("(gp h) p i -> h p gp i", h=2)[h])
                for ri in range(2):
                    ps = kb.psum_f()
                    for g in range(16):
                        gp, h = g // 2, g % 2
                        mm(ps[h * 64:(h + 1) * 64, gp * 16:(gp + 1) * 16], Cnat[ri][0:16, g, :], identf[0:16, 0:16], True, True)
                    cp("act", Craw[ri], ps[:, 0:128].re("p (g o) -> p g o", g=8))
                if "s5_b1" in dbg:
                    return
                act(pv(2), pv(2), AF.Exp)
                tt("dve", pv(3), pv(0), pv(2), ALU.mult)
                tt("dve", pv(4), pv(1), pv(2), ALU.mult)
                act(pv(5), pv(3), AF.Exp)
                for (dst, shift) in ((6, 0.0), (7, math.pi / 2)):
                    xx, kk, ki = pv(30), pv(31), pv(32)
                    ts("dve", xx, pv(4), shift, ALU.add)
                    ts("dve", kk, xx, 1.0 / (2 * math.pi), ALU.mult)
                    kint = TV(ki.ap.bitcast(I32), ki.buf)
                    cp("dve", kint, kk)
                    cp("dve", kk, kint)
                    stt(xx, kk, -6.28125, xx, ALU.mult, ALU.add)
                    stt(xx, kk, -(2 * math.pi - 6.28125), xx, ALU.mult, ALU.add)
                    ts("dve", xx, xx, math.pi, ALU.min, -math.pi, ALU.max)
                    act(pv(dst), xx, AF.Sin)
                kb.memset("dve", pw(0, 0), 1.0)
                kb.memset("dve", pw(1, 0), 0.0)
                tt("dve", pw(0, 1), pv(5), pv(7), ALU.mult)
                tt("dve", pw(1, 1), pv(5), pv(6), ALU.mult)
                for j in range(2, 9):
                    tt("dve", pv(30), pw(0, j - 1), pw(0, 1), ALU.mult)
                    tt("dve", pv(31), pw(1, j - 1), pw(1, 1), ALU.mult)
                    tt("dve", pw(0, j), pv(30), pv(31), ALU.subtract)
                    tt("dve", pv(30), pw(0, j - 1), pw(1, 1), ALU.mult)
                    tt("dve", pv(31), pw(1, j - 1), pw(0, 1), ALU.mult)
                    tt("dve", pw(1, j), pv(30), pv(31), ALU.add)
                act(pv(8), pv(3), AF.Exp, scale=-16.0)
                tt("dve", pw(0, 9), pw(0, 8), pv(8), ALU.mult)
                tt("dve", pw(1, 9), pw(1, 8), pv(8), ALU.mult)
                ts("dve", pw(1, 9), pw(1, 9), -1.0, ALU.mult)
                act(pv(9), pv(3), AF.Exp, scale=8.0)
                act(pv(10), pv(3), AF.Exp, scale=-8.0)
                tt("dve", pv(11), pw(0, 8), pv(10), ALU.mult)
                tt("dve", pv(12), pw(1, 8), pv(10), ALU.mult)
                ts("dve", pv(13), pw(0, 1), -1.0, ALU.add)
                tt("dve", pv(14), pv(0), pv(0), ALU.mult)
                tt("dve", pv(15), pv(1), pv(1), ALU.mult)
                tt("dve", pv(14), pv(14), pv(15), ALU.add)
                kb.recip(pv(14), pv(14))
                tt("dve", pv(15), pv(13), pv(0), ALU.mult)
                tt("dve", pv(16), pw(1, 1), pv(1), ALU.mult)
                tt("dve", pv(15), pv(15), pv(16), ALU.add)
                tt("dve", pv(15), pv(15), pv(14), ALU.mult)
                tt("dve", pv(16), pw(1, 1), pv(0), ALU.mult)
                tt("dve", pv(17), pv(13), pv(1), ALU.mult)
                tt("dve", pv(16), pv(16), pv(17), ALU.subtract)
                tt("dve", pv(16), pv(16), pv(14), ALU.mult)
                fre = pv(15).us(2).bc([128, 8, 16])
                fim = pv(16).us(2).bc([128, 8, 16])
                tt("dve", Bb[0], Braw[0], fre, ALU.mult)
                tt("dve", t8, Braw[1], fim, ALU.mult)
                tt("dve", Bb[0], Bb[0], t8, ALU.subtract)
                tt("dve", Bb[1], Braw[1], fre, ALU.mult)
                tt("dve", t8, Braw[0], fim, ALU.mult)
                tt("dve", Bb[1], Bb[1], t8, ALU.add)
                for t_ in range(8):
                    f_ = t_ + 1 if d == 0 else 8 - t_
                    pr = pw(0, f_).us(2).bc([128, 8, 16])
                    pi_ = pw(1, f_).us(2).bc([128, 8, 16])
                    eo = Ere[:, d, :, t_ * 16:(t_ + 1) * 16]
                    ei = nEim[:, d, :, t_ * 16:(t_ + 1) * 16]
                    tt("dve", eo, Craw[0], pr, ALU.mult)
                    tt("dve", t8, Craw[1], pi_, ALU.mult)
                    tt("dve", eo, eo, t8, ALU.subtract)
                    tt("dve", ei, Craw[0], pi_, ALU.mult)
                    tt("dve", t8, Craw[1], pr, ALU.mult)
                    tt("dve", ei, ei, t8, ALU.add)
                    ts("dve", ei, ei, -1.0, ALU.mult)
                for r in range(8):
                    e_ = 7 - r if d == 0 else r
                    pr = pw(0, e_).us(2).bc([128, 8, 16])
                    pi_ = pw(1, e_).us(2).bc([128, 8, 16])
                    tt("dve", Fm[0][:, :, r, :], Bb[0], pr, ALU.mult)
                    tt("dve", t8, Bb[1], pi_, ALU.mult)
                    tt("dve", Fm[0][:, :, r, :], Fm[0][:, :, r, :], t8, ALU.subtract)
                    tt("dve", Fm[1][:, :, r, :], Bb[1], pr, ALU.mult)
                    tt("dve", t8, Bb[0], pi_, ALU.mult)
                    tt("dve", Fm[1][:, :, r, :], Fm[1][:, :, r, :], t8, ALU.add)
                qr = pw(0, 9).us(2).bc([128, 8, 128])
                qi = pw(1, 9).us(2).bc([128, 8, 128])
                tt("dve", Ep[0], Ere[:, d], qr, ALU.mult)
                tt("dve", tE, nEim[:, d], qi, ALU.mult)
                tt("dve", Ep[0], Ep[0], tE, ALU.add)
                tt("dve", Ep[1], nEim[:, d], qr, ALU.mult)
                tt("dve", tE, Ere[:, d], qi, ALU.mult)
                tt("dve", Ep[1], Ep[1], tE, ALU.subtract)
                if "s5_b2" in dbg:
                    return
                for ri in range(2):
                    for gq in range(4):
                        ps = kb.psum_f()
                        for gg_ in range(4):
                            g = gq * 4 + gg_
                            gp, h = g // 2, g % 2
                            rows = slice(h * 64, (h + 1) * 64)
                            mm(ps[:, gg_ * 64:(gg_ + 1) * 64], Fm[ri][:, gp].re("p r i -> p (r i)"), identf[:, h * 64:(h + 1) * 64], True, True)
                        cp("act", W1[ri][:, gq * 4:gq * 4 + 4, :], ps[:, 0:256].re("p (g n) -> p g n", g=4))
                if "s5_b3" in dbg:
                    return
                for g in range(16):
                    gp, h = g // 2, g % 2
                    rows = slice(h * 64, (h + 1) * 64)
                    ps = kb.psf[h * 2 + (g // 2) % 2]
                    mm(ps[:, 0:128], Fm[0][rows, gp].re("p r i -> p (r i)"), Ep[0][rows, gp, :], True, False)
                    mm(ps[:, 0:128], Fm[1][rows, gp].re("p r i -> p (r i)"), Ep[1][rows, gp, :], False, True)
                    wt = w3t[g % 2]
                    tt("dve", wt, ps[:, 0:128], cmask[:, d, :], ALU.mult)
                    tt("pool", W3[:, g, :], W3[:, g, :], wt, ALU.add)
                if d == 0:
                    for g in range(16):
                        dcol = Dbc[:, g * 16:(g + 1) * 16].us(1).bc([128, 8, 16])
                        wt = w3t[g % 2]
                        tt("dve", wt.re("p (t o) -> p t o", t=8), dsel.re("p (t o) -> p t o", t=8), dcol, ALU.mult)
                        tt("pool", W3[:, g, :], W3[:, g, :], wt, ALU.add)
                if "s5_b4" in dbg:
                    return
                kb.memset("dve", tab[0][:, :, 0:1], 1.0)
                kb.memset("dve", tab[1][:, :, 0:1], 0.0)
                cp("dve", tab[0][:, :, 1:2], pv(11).us(2))
                cp("dve", tab[1][:, :, 1:2], pv(12).us(2))
                n = 2
                while n < NCH:
                    m = min(n, NCH - n)
                    tt("dve", pv(30), tab[0][:, :, n - 1], pv(11), ALU.mult)
                    tt("dve", pv(31), tab[1][:, :, n - 1], pv(12), ALU.mult)
                    tt("dve", pv(33), pv(30), pv(31), ALU.subtract)
                    tt("dve", pv(30), tab[0][:, :, n - 1], pv(12), ALU.mult)
                    tt("dve", pv(31), tab[1][:, :, n - 1], pv(11), ALU.mult)
                    tt("dve", pv(34), pv(30), pv(31), ALU.add)
                    Pr = pv(33).us(2).bc([128, 8, m])
                    Pi = pv(34).us(2).bc([128, 8, m])
                    ta = tE[:, :, 0:m]
                    tt("dve", tab[0][:, :, n:n + m], tab[0][:, :, 0:m], Pr, ALU.mult)
                    tt("dve", ta, tab[1][:, :, 0:m], Pi, ALU.mult)
                    tt("dve", tab[0][:, :, n:n + m], tab[0][:, :, n:n + m], ta, ALU.subtract)
                    tt("dve", tab[1][:, :, n:n + m], tab[0][:, :, 0:m], Pi, ALU.mult)
                    tt("dve", ta, tab[1][:, :, 0:m], Pr, ALU.mult)
                    tt("dve", tab[1][:, :, n:n + m], tab[1][:, :, n:n + m], ta, ALU.add)
                    n *= 2
                dma("sp", st_dir[d, :, 0:1024], W1[0].re("p g n -> p (g n)"))
                dma("sp", st_dir[d, :, 1024:2048], W1[1].re("p g n -> p (g n)"))
                dma("act", st_dir[d, :, 2048:2048 + 8 * NCH], tab[0].re("p g n -> p (g n)"))
                dma("act", st_dir[d, :, 2048 + 8 * NCH:2048 + 16 * NCH], tab[1].re("p g n -> p (g n)"))
                dma("sp", st_dir[d, :, 2048 + 16 * NCH:2048 + 16 * NCH + 8], pv(9))
            else:
                if d == 1:
                    load_dir(1)
                dma("sp", pv(9), st_dir[d, :, 2048 + 16 * NCH:2048 + 16 * NCH + 8])
            if "s5_stop2" in dbg:
                return
            if not cached:
                P.barrier()
            def nat(tv, sl, rev):
                v = tv[:, sl]
                return v[:, ::-1] if rev else v

            def gp_gen(gp, d=d):
                rs, rho1 = rs_sets[gp % 2], rho_sets[gp % 2]
                psv = [kb.psum_f(), kb.psum_f()]
                for ri in range(2):
                    for h in range(2):
                        g = gp * 2 + h
                        mm(psv[ri][h * 64:(h + 1) * 64, 0:NCH], W1[ri][:, g, :], U[:, g, :], True, True)
                cp("pool", rho1, pv(9)[:, gp:gp + 1].bc([128, NCH]))
                yield
                Cn, Sn = tab[0][:, gp, :], tab[1][:, gp, :]
                if d == 0:
                    segs = [(slice(0, NCH), slice(0, NCH), False)]
                else:
                    segs = [(slice(0, 32), slice(0, 32), True), (slice(32, NCH), slice(32, NCH), True)]
                for (js, ks, rev) in segs:
                    vre, vim = nat(psv[0][:, 0:NCH], ks, rev), nat(psv[1][:, 0:NCH], ks, rev)
                    tt("dve", rs[0][:, js], vre, Cn[:, js], ALU.mult)
                    tt("dve", rs[1][:, js], vim, Sn[:, js], ALU.mult)
                    tt("dve", rs[4][:, js], vim, Cn[:, js], ALU.mult)
                    tt("dve", rs[5][:, js], vre, Sn[:, js], ALU.mult)
                yield
                tt("pool", rs[2], rs[0], rs[1], ALU.add)
                tt("pool", rs[3], rs[4], rs[5], ALU.subtract)
                yield
                kb.scan(rs[4], rho1, rs[2], 0.0)
                kb.scan(rs[5], rho1, rs[3], 0.0)
                yield
                tt("dve", rs[0], rs[4], Cn, ALU.mult)
                tt("dve", rs[1], rs[5], Sn, ALU.mult)
                tt("dve", rs[2], rs[4], Sn, ALU.mult)
                tt("dve", rs[3], rs[5], Cn, ALU.mult)
                yield
                for (js, ks, rev) in segs:
                    if d == 0:
                        osl = slice(1, NCH + 1)
                    else:
                        osl = slice(0, 32) if ks.start == 0 else slice(33, NCH + 1)
                    ore = nat(SN[d][0][:, gp, :], osl, rev).par("s5sn")
                    oim = nat(SN[d][1][:, gp, :], osl, rev).par("s5sn")
                    tt("pool", ore, rs[0][:, js], rs[1][:, js], ALU.subtract)
                    tt("pool", oim, rs[2][:, js], rs[3][:, js], ALU.add)
                yield
            for gp0 in range(0, 8, 2):
                lockstep([gp_gen(gp0), gp_gen(gp0 + 1)])
            for ri in range(2):
                if d == 0:
                    kb.memset("pool", SN[0][ri][:, :, 0:1], 0.0)
                else:
                    kb.memset("pool", SN[1][ri][:, :, 32:33], 0.0)
                    cp("pool", SN[1][ri][:, :, NCH + 1:NCH + 2], SN[1][ri][:, :, 0:1])
        if not cached:
            dma("sp", st_main[0], Ere.re("p d g n -> p (d g n)"))
            dma("act", st_main[1], nEim.re("p d g n -> p (d g n)"))
            dma("sp", st_main[2], W3.re("p g n -> p (g n)"))
        if "s5_stop3" in dbg:
            return
        P.barrier()
        AR.lo = scr0
        AR2.lo = 0
        Yc = AR.alloc("Yc", [8, 256])
        gg = AR.alloc("gg", [8, 256])
        g2 = AR.alloc("g2", [8, 256])
        sg = [AR.alloc("sg%d" % i, [512]) for i in range(2)]
        ggT = AR2.alloc("ggT", [2, NPOS])
        ggb = AR2.alloc("ggb", [2, NPOS], BF16)
        banks = [kb.psf[0], kb.psf[2], kb.psf[1], kb.psf[3]]

        def r_mm(k0, M, p0):
            for g in range(16):
                gp, h = g // 2, g % 2
                rows = slice(h * 64, (h + 1) * 64)
                bank = banks[h * 2 + (gp // 4)]
                o = bank[0:M, (gp % 4) * 128:(gp % 4 + 1) * 128]
                cf = slice(k0, k0 + M)
                cb_ = slice(k0 + 1, k0 + 1 + M) if k0 == 0 else slice(k0 + 2, k0 + 2 + M)
                mm(o, SN[0][0][rows, gp, cf], Ere[rows, 0, gp, :], True, False)
                mm(o, SN[0][1][rows, gp, cf], nEim[rows, 0, gp, :], False, False)
                mm(o, SN[1][0][rows, gp, cb_], Ere[rows, 1, gp, :], False, False)
                mm(o, SN[1][1][rows, gp, cb_], nEim[rows, 1, gp, :], False, False)
                mm(o, U[:, g, k0:k0 + M], W3[:, g, :], False, True)

        def r_evac(k0, M, p0):
            for h in range(2):
                for q in range(2):
                    bank = banks[h * 2 + q]
                    src = bank[0:M, :].re("p (g t o) -> p g t o", g=4, t=8)
                    dst = Yc[0:M].re("p t (g2 h o) -> p h g2 t o", h=2, o=16)[:, h, 4 * q:4 * q + 4].par("s5yc")
                    cp("act" if h == 0 else "dve", dst, src)

        def r_rest(k0, M, p0):
            yv, gv, g2v = Yc[0:M], gg[0:M], g2[0:M]
            tt("pool", g2v, yv, yv, ALU.mult)
            ts("dve", g2v, g2v, 0.044715, ALU.mult, 1.0, ALU.add)
            tt("pool", g2v, g2v, yv, ALU.mult)
            act(g2v, g2v, AF.Sigmoid, scale=1.5957691216057308)
            tt("dve", gv, g2v, yv, ALU.mult)
            for t_ in range(8):
                pz = [kb.psf[4], kb.psf[5]]
                for c in range(2):
                    tr(pz[c][:, 0:M], gv[:, t_, c * 128:(c + 1) * 128], identf[0:M, 0:M])
                for c in range(2):
                    cp("act" if c == 0 else "dve", ggT[:, c, p0 + t_:p0 + 8 * M:8].par("s5gg"), pz[c][:, 0:M])
        rb = [blk for blk in kblocks if not (blk[0] == 0 and not with_ctx)]
        r_mm(*rb[0])
        for bi, blk in enumerate(rb):
            r_evac(*blk)
            if bi + 1 < len(rb):
                r_mm(*rb[bi + 1])
            r_rest(*blk)
        lo = 0 if with_ctx else TC
        for c in range(2):
            cp("dve", ggb[:, c, lo:NPOS], ggT[:, c, lo:NPOS])
        for c in range(2):
            pos = lo
            while pos < NPOS:
                n = min(512, NPOS - pos)
                ps = kb.psum_f()
                for k in range(2):
                    mm(ps[:, 0:n], wglu[:, k, c * 128:(c + 1) * 128], ggb[:, k, pos:pos + n], k == 0, k == 1)
                s_ = sg[(pos // 512) % 2]
                act(s_[:, 0:n], ps[:, 0:n], AF.Sigmoid)
                tt("dve", brT[1][c][:, pos:pos + n], s_[:, 0:n], ggT[:, c, pos:pos + n], ALU.mult)
                pos += n

    def phase_merge(l, b, with_ctx, nxt=None):
        AR.reset(top=True)
        lo = 0 if with_ctx else TC
        blocks = []
        pos = lo
        while pos < NPOS:
            n = min(512, NPOS - pos) if pos >= TC else TC - pos
            blocks.append((pos, n))
            pos += n
        ypT = AR.alloc("ypT", [KC, NPOS], BF16, top=True)
        wbr = AR.alloc("wbr", [4, 2, D], BF16, top=True)
        wo = AR.alloc("wo", [KC, D], BF16, top=True)
        wml = [AR.alloc("wml%d" % i, [KC, 4, 128], BF16, top=True) for i in range(2)]

        def load_wml(oc):
            for n_ in range(4):
                c0 = O_MERGE + n_ * D + oc * 128
                dma("pool", wml[oc % 2][:, :, n_, :].par("ldw"), w_in[l, :, c0:c0 + 128].re("(c p) n -> p c n", p=128))
        wg = AR.alloc("w_gate", [KC, 1024], BF16)
        for cc in range(1, 8):
            dma("pool", wg[:, :, cc * 128:(cc + 1) * 128].par("ldw"),
                w_in[l, :, O_GATE + cc * 128:O_GATE + (cc + 1) * 128].re("(c p) n -> p c n", p=128))
        for n_ in range(4):
            dma("pool", wbr[:, n_].par("ldw"), w_branch[l, n_].re("(c p) n -> p c n", p=128))
        load_wml(0)
        load_wml(1)
        dma("pool", wo, w_out[l].re("(c p) n -> p c n", p=128))
        sl = [AR.alloc("sl%d" % i, [512], BF16) for i in range(2)]
        it = 0
        for cc in range(8):
            for (p0, n) in blocks:
                ps = kb.psum_f()
                for k in range(KC):
                    wsl = wst[:, k, 0:128] if cc == 0 else wg[:, k, cc * 128:(cc + 1) * 128]
                    mm(ps[:, 0:n], wsl, hT[:, k, p0:p0 + n], k == 0, k == KC - 1)
                s_ = sl[it % 2]
                it += 1
                act(s_[:, 0:n], ps[:, 0:n], AF.Silu)
                br = brT[cc // 2][cc % 2][:, p0:p0 + n]
                tt("dve", br, br, s_[:, 0:n], ALU.mult)
            if cc == 0 and nxt is not None:
                prefetch(nxt, PF_S5)
        AR.reset()
        sg = [AR.alloc("sg%d" % i, [512]) for i in range(2)]
        tm = [AR.alloc("tm%d" % i, [512]) for i in range(2)]
        acc = [AR.alloc("acc%d" % i, [512]) for i in range(2)]
        it = 0
        ib = 0
        for oc in range(8):
            wm = wml[oc % 2]
            if 1 <= oc and oc + 1 < 8:
                load_wml(oc + 1)
            for (p0, n) in blocks:
                ac = acc[ib % 2]
                ib += 1
                for n_ in range(4):
                    psA = kb.psum_f()
                    for k in range(2):
                        mm(psA[:, 0:n], wbr[:, n_, k, oc * 128:(oc + 1) * 128], brT[n_][k][:, p0:p0 + n], k == 0, k == 1)
                    psB = kb.psum_f()
                    for k in range(KC):
                        mm(psB[:, 0:n], wm[:, k, n_, :], hT[:, k, p0:p0 + n], k == 0, k == KC - 1)
                    s_ = sg[it % 2]
                    t_ = tm[it % 2]
                    it += 1
                    act(s_[:, 0:n], psB[:, 0:n], AF.Sigmoid)
                    if n_ == 0:
                        tt("dve", ac[:, 0:n], psA[:, 0:n], s_[:, 0:n], ALU.mult)
                    elif n_ < 3:
                        tt("dve", t_[:, 0:n], psA[:, 0:n], s_[:, 0:n], ALU.mult)
                        tt("dve", ac[:, 0:n], ac[:, 0:n], t_[:, 0:n], ALU.add)
                    else:
                        tt("dve", t_[:, 0:n], psA[:, 0:n], s_[:, 0:n], ALU.mult)
                        tt("dve", ypT[:, oc, p0:p0 + n].par("ypt"), ac[:, 0:n], t_[:, 0:n], ALU.add)
        AR.reset()
        gbc = AR.alloc("gbc", [2, D])
        dma("sp", gbc[:, 0, :], modD[l, b, 2 * D:3 * D].pbc(128))
        if with_ctx:
            dma("sp", gbc[:, 1, :], modD[l, 2, 2 * D:3 * D].pbc(128))
        xt = [AR.alloc("xt%d" % i, [D]) for i in range(2)]
        yt = [AR.alloc("yt%d" % i, [D]) for i in range(2)]
        for i in range(0 if with_ctx else 2, 18):
            x_t, y_t = xt[i % 2], yt[i % 2]
            dma("sp", x_t, src_tile(l, b, i))
            for hf in range(2):
                ps = kb.psum_f()
                for k in range(KC):
                    mm(ps, ypT[:, k, i * 128:(i + 1) * 128], wo[:, k, hf * 512:(hf + 1) * 512], k == 0, k == KC - 1)
                tt("dve", y_t[:, hf * 512:(hf + 1) * 512], ps, gbc[:, 1 if i < 2 else 0, hf * 512:(hf + 1) * 512], ALU.mult)
            tt("pool", y_t, y_t, x_t, ALU.add)
            if i < 2:
                dst = (tap("xc1", [nb, TC, D]) if "x1out" in dbg else xcmid)[b, i * 128:(i + 1) * 128, :]
            else:
                dst = (xmid if (l < DEPTH - 1 and "x1out" not in dbg) else y_out)[b, (i - 2) * 128:(i - 1) * 128, :]
            dma("act", dst, y_t)

    def dump_br(name, n, lo):
        o = tap(name, [2, 128, NPOS])
        tmpf = AR.alloc("dump_" + name, [NPOS])
        for c in range(2):
            cp("dve", tmpf[:, lo:NPOS], brT[n][c][:, lo:NPOS])
            dma("sp", o[c, :, lo:NPOS], tmpf[:, lo:NPOS])

    prefetch(layers[0], PF_S5)
    for l in layers:
        with_ctx = l < DEPTH - 1
        prep_layer(l)
        for b in range(nb):
            if "prep_only" in dbg:
                continue
            phase_norm(l, b)
            if "hT" in dbg and b == 0 and l == layers[0]:
                o = tap("hT", [KC, 128, NPOS])
                AR.reset()
                tmpf = AR.alloc("dump_hT", [NPOS])
                for c in range(KC):
                    cp("dve", tmpf, hT[:, c, :])
                    dma("sp", o[c], tmpf)
            only = dbg & {"only_lru", "only_pool", "only_mla", "only_s5", "only_norm"}
            if not only or "only_s5" in only:
                phase_s5(l, b, with_ctx)
            if not only or "only_lru" in only:
                phase_lru(l, b, with_ctx)
            if not only or "only_pool" in only:
                phase_pool(l, b, with_ctx)
            if not only or "only_mla" in only:
                phase_mla(l, b, with_ctx)
            if "br" in dbg and b == 0 and l == layers[0]:
                AR.reset()
                lo = 0 if with_ctx else TC
                for n_, nm in enumerate(("mla", "s5", "lru", "pool")):
                    dump_br(nm, n_, lo)
            if "nomerge" not in dbg:
                if b + 1 < nb:
                    nxt = l
                else:
                    li = list(layers).index(l)
                    nxt = layers[li + 1] if li + 1 < len(layers) else None
                phase_merge(l, b, with_ctx, nxt)
    P.barrier()
    P.emit()
    kb.st.close()
    return nc, kb


def _consts():
    ident = np.eye(128, dtype=np.float32)
    rows_n = T // 64
    row = np.repeat(np.arange(rows_n), 64).astype(np.float32)
    col = np.tile(np.arange(64), rows_n).astype(np.float32)
    nf = 8
    inv = (np.float32(10000.0) ** (-np.arange(nf, dtype=np.float32) / nf)).astype(np.float32)
    ar = (row[:, None] * inv).astype(np.float32)
    ac = (col[:, None] * inv).astype(np.float32)
    cr, sr, cc, sc = np.cos(ar), np.sin(ar), np.cos(ac), np.sin(ac)
    ropeC = np.concatenate([cr, cr, cc, cc], axis=1).astype(np.float32)
    ropeS = np.concatenate([-sr, sr, -sc, sc], axis=1).astype(np.float32)
    cpool = np.ones((128, 34), np.float32)
    for c in range(2):
        for p in range(128):
            w = (2, 4, 8, 16)[2 * c + p // 64]
            hw = w // 2
            cpool[p, c] = 1.0 / w
            for t in range(8):
                cnt = t + hw if t < hw else w
                cpool[p, 2 + c * 8 + t] = w / cnt
            for j in range(8):
                dist = 8 - j
                cnt = dist + hw if dist < hw else w
                cpool[p, 18 + c * 8 + j] = w / cnt
    m = np.zeros((128, 3, 128), np.float32)
    for r in range(8):
        for t in range(8):
            if r <= t:
                m[r * 16:(r + 1) * 16, 0, t * 16:(t + 1) * 16] = 1.0
            if r >= t:
                m[r * 16:(r + 1) * 16, 1, t * 16:(t + 1) * 16] = 1.0
            if r == t:
                m[r * 16:(r + 1) * 16, 2, t * 16:(t + 1) * 16] = np.eye(16, dtype=np.float32)
    return {"c_ident": ident, "c_ropeC": ropeC, "c_ropeS": ropeS, "c_pool": cpool, "c_mask": m}


_WNAMES = ["w_ada", "b_ada", "norm_g", "w_in", "mla_q_norm", "mla_kv_norm", "mla_w_uq", "mla_w_ukv", "mla_q_gain",
           "mla_k_gain", "s5_a_re", "s5_a_im", "s5_log_dt", "s5_b_re", "s5_b_im", "s5_c_re", "s5_c_im", "s5_d",
           "s5_w_glu", "lru_conv_w", "lru_conv_b", "lru_lambda", "lru_w_a", "lru_b_a", "lru_w_x", "lru_b_x",
           "pool_w", "pool_b", "pool_scale", "w_branch", "w_out"]


def make_in_maps(inputs, n_cores, nb):
    consts = _consts()
    maps = []
    for r in range(n_cores):
        bs = slice(r * nb, (r + 1) * nb)
        m = {"x": np.ascontiguousarray(inputs["x"][bs], dtype=np.float32),
             "ctx": np.ascontiguousarray(inputs["ctx"][bs], dtype=np.float32)}
        cv = np.zeros((3, D), np.float32)
        cv[0:nb] = np.asarray(inputs["c"], dtype=np.float32)[bs]
        cv[2] = np.asarray(inputs["c_ctx"], dtype=np.float32)
        m["cvec"] = cv
        for k in _WNAMES:
            m[k] = np.ascontiguousarray(inputs[k], dtype=np.float32)
        m.update(consts)
        maps.append(m)
    return maps


_CACHE = {}


def kernel(**inputs):
    n_cores, nb = 8, 2
    if "nc" not in _CACHE:
        _CACHE["nc"] = build_program(nb=nb)[0]
    nc = _CACHE["nc"]
    maps = make_in_maps(inputs, n_cores, nb)
    res = run_bass_kernel_spmd(nc, maps, core_ids=list(range(n_cores)))
    out = np.concatenate([np.asarray(r["y"], dtype=np.float32) for r in res.results], axis=0)
    return out
```

```python
import math
import contextlib
import numpy as np
import concourse.bass as bass
import concourse.mybir as mybir
from concourse.bass_utils import run_bass_kernel_spmd

F32 = mybir.dt.float32
BF16 = mybir.dt.bfloat16
I32 = mybir.dt.int32
AF = mybir.ActivationFunctionType
ALU = mybir.AluOpType
AX = mybir.AxisListType

ENGS = ("pe", "act", "dve", "pool", "sp")
NDMA = 24

D = 1024
KC = 8
T = 2048
TC = 256
NPOS = T + TC
DEPTH = 2
O_KROPE, O_S5, O_LRU, O_CQ, O_POOL, O_GATE, O_MERGE, IN_W = 128, 160, 416, 672, 928, 1184, 2208, 6304
EPS = 1e-6
NCH = NPOS // 8


PAR_OFF = set()


class Buf:
    __slots__ = ("name", "w", "wp", "r")

    def __init__(self, name):
        self.name = name
        self.w = []
        self.wp = []
        self.r = []


class TV:
    par_ = False

    def __init__(self, ap, buf):
        self.ap, self.buf = ap, buf

    def par(self, tag=""):
        t = TV(self.ap, self.buf)
        t.par_ = tag not in PAR_OFF
        return t

    def __getitem__(self, k):
        return TV(self.ap[k], self.buf)

    def re(self, s, **kw):
        return TV(self.ap.rearrange(s, **kw), self.buf)

    def bc(self, shape):
        return TV(self.ap.to_broadcast(list(shape)), self.buf)

    def us(self, ax):
        return TV(self.ap.unsqueeze(ax), self.buf)

    def pbc(self, n):
        return TV(self.ap.partition_broadcast(n), self.buf)

    @property
    def shape(self):
        return tuple(self.ap.shape)


def _bufs(*xs):
    out = []
    for x in xs:
        if isinstance(x, TV):
            out.append(x.buf)
        elif isinstance(x, (list, tuple)):
            out.extend(_bufs(*x))
    return out


def _ap(x):
    return x.ap if isinstance(x, TV) else x


class Prog:
    def __init__(self, nc):
        self.nc = nc
        self.ops = {e: [] for e in ENGS}
        self.cnt = {e: 0 for e in ENGS}
        self.waited = {e: {} for e in ENGS}
        self.dma_i = 0
        self.dma_last = [0] * NDMA

    def _deps(self, eng, reads, writes, par=False):
        toks = []
        for b in reads:
            toks.extend(b.w)
            toks.extend(b.wp)
        for b in writes:
            toks.extend(b.w)
            if not par:
                toks.extend(b.wp)
            toks.extend(b.r)
        need = {}
        for (k, v, e) in toks:
            if e == "pe" and eng == "pe":
                continue
            if self.waited[eng].get(k, 0) >= v:
                continue
            if need.get(k, 0) < v:
                need[k] = v
        for k, v in need.items():
            self.waited[eng][k] = v
        return list(need.items())

    def _mark(self, tok, reads, writes, par=False):
        for b in writes:
            if par:
                best = {}
                for (k, v, e) in b.wp + [tok]:
                    if k not in best or best[k][1] < v:
                        best[k] = (k, v, e)
                b.wp = list(best.values())
            else:
                b.w = [tok]
                b.wp = []
                b.r = []
        for b in reads:
            if b in writes:
                continue
            b.r.append(tok)
            if len(b.r) > 16:
                best = {}
                for (k, v, e) in b.r:
                    if k not in best or best[k][1] < v:
                        best[k] = (k, v, e)
                b.r = list(best.values())

    def op(self, eng, fn, reads=(), writes=(), par=False):
        reads = list(dict.fromkeys(reads))
        writes = list(dict.fromkeys(writes))
        waits = self._deps(eng, reads, writes, par)
        self.cnt[eng] += 1
        tok = (eng, self.cnt[eng], eng)
        self.ops[eng].append((waits, fn, (eng, 1)))
        self._mark(tok, reads, writes, par)

    def dma(self, eng, fn, reads=(), writes=(), par=False):
        reads = list(dict.fromkeys(reads))
        writes = list(dict.fromkeys(writes))
        waits = self._deps(eng, reads, writes, par)
        slot = self.dma_i % NDMA
        self.dma_i += 1
        key = ("dma", slot)
        prev = self.dma_last[slot]
        if prev and self.waited[eng].get(key, 0) < prev:
            waits.append((key, prev))
            self.waited[eng][key] = prev
        val = prev + 16
        self.dma_last[slot] = val
        tok = (key, val, "dma")
        self.ops[eng].append((waits, fn, (key, 16)))
        self._mark(tok, reads, writes, par)

    def barrier(self):
        for E in ENGS:
            waits = []
            for e in ENGS:
                v = self.cnt[e]
                if v and self.waited[E].get(e, 0) < v:
                    waits.append((e, v))
                    self.waited[E][e] = v
            for slot in range(NDMA):
                v = self.dma_last[slot]
                key = ("dma", slot)
                if v and self.waited[E].get(key, 0) < v:
                    waits.append((key, v))
                    self.waited[E][key] = v
            if waits:
                self.ops[E].append((waits, None, None))

    def emit(self):
        nc = self.nc
        sems = {}
        with contextlib.ExitStack() as st:
            for e in ENGS:
                sems[e] = st.enter_context(nc.semaphore("s_" + e))
            for i in range(NDMA):
                sems[("dma", i)] = st.enter_context(nc.semaphore("s_dma%d" % i))
            block = st.enter_context(nc.Block())

            def run(engname):
                def body(eng):
                    for waits, fn, inc in self.ops[engname]:
                        for k, v in waits:
                            eng.wait_ge(sems[k], v)
                        if fn is None:
                            continue
                        ins = fn(eng)
                        ins.then_inc(sems[inc[0]], inc[1])
                return body
            block.tensor(run("pe"))
            block.scalar(run("act"))
            block.vector(run("dve"))
            block.gpsimd(run("pool"))
            block.sync(run("sp"))


class KB:
    def __init__(self, nc, nb, layers, dbg):
        self.nc = nc
        self.P = Prog(nc)
        self.nb = nb
        self.layers = layers
        self.dbg = dbg
        self.st = contextlib.ExitStack()
        self.din = {}
        self.dout = {}
        self.ps_i = 0
        self.psb_i = 0
        self.ps8_i = 0

    def tt(self, eng, out, a, b, op):
        self.P.op(eng, lambda e: e.tensor_tensor(out=out.ap, in0=a.ap, in1=b.ap, op=op), _bufs(a, b), _bufs(out), par=out.par_)

    def ts(self, eng, out, a, s1, op0, s2=None, op1=None):
        if op1 is None:
            self.P.op(eng, lambda e: e.tensor_scalar(out=out.ap, in0=a.ap, scalar1=_ap(s1), scalar2=None, op0=op0),
                      _bufs(a, s1), _bufs(out), par=out.par_)
        else:
            self.P.op(eng, lambda e: e.tensor_scalar(out=out.ap, in0=a.ap, scalar1=_ap(s1), scalar2=_ap(s2), op0=op0, op1=op1),
                      _bufs(a, s1, s2), _bufs(out), par=out.par_)

    def stt(self, out, a, s, b, op0, op1):
        self.P.op("dve", lambda e: e.scalar_tensor_tensor(out=out.ap, in0=a.ap, scalar=_ap(s), in1=b.ap, op0=op0, op1=op1),
                  _bufs(a, s, b), _bufs(out))

    def act(self, out, a, func, bias=0.0, scale=1.0, accum=None):
        if accum is None:
            self.P.op("act", lambda e: e.activation(out=out.ap, in_=a.ap, func=func, bias=_ap(bias), scale=_ap(scale)),
                      _bufs(a, bias, scale), _bufs(out), par=out.par_)
        else:
            self.P.op("act", lambda e: e.activation(out=out.ap, in_=a.ap, func=func, bias=_ap(bias), scale=_ap(scale), accum_out=accum.ap),
                      _bufs(a, bias, scale), _bufs(out, accum))

    def cp(self, eng, out, a):
        if eng == "act":
            self.P.op("act", lambda e: e.copy(out=out.ap, in_=a.ap), _bufs(a), _bufs(out), par=out.par_)
        else:
            self.P.op(eng, lambda e: e.tensor_copy(out=out.ap, in_=a.ap), _bufs(a), _bufs(out), par=out.par_)

    def memset(self, eng, out, v):
        self.P.op(eng, lambda e: e.memset(out.ap, v), [], _bufs(out), par=out.par_)

    def recip(self, out, a):
        self.P.op("dve", lambda e: e.reciprocal(out=out.ap, in_=a.ap), _bufs(a), _bufs(out))

    def mm(self, out, lhsT, rhs, start, stop):
        self.P.op("pe", lambda e: e.matmul(out.ap, lhsT=lhsT.ap, rhs=rhs.ap, start=start, stop=stop), _bufs(lhsT, rhs), _bufs(out))

    def tr(self, out, a, ident):
        self.P.op("pe", lambda e: e.transpose(out.ap, a.ap, ident.ap), _bufs(a, ident), _bufs(out))

    def scan(self, out, d0, d1, init):
        self.P.op("dve", lambda e: e.tensor_tensor_scan(out=out.ap, data0=d0.ap, data1=d1.ap, initial=_ap(init), op0=ALU.mult, op1=ALU.add),
                  _bufs(d0, d1, init), _bufs(out))

    def reduce(self, out, a, op=ALU.add):
        self.P.op("dve", lambda e: e.tensor_reduce(out=out.ap, in_=a.ap, axis=AX.X, op=op), _bufs(a), _bufs(out))

    def dma(self, q, out, a, slow=False):
        if slow:
            self.P.dma(q, lambda e: e.dma_start(out=out.ap, in_=a.ap, allow_slow_non_contiguous=True), _bufs(a), _bufs(out), par=out.par_)
        else:
            self.P.dma(q, lambda e: e.dma_start(out=out.ap, in_=a.ap), _bufs(a), _bufs(out), par=out.par_)

    def dram_in(self, name, shape, dt=F32):
        t = TV(self.nc.dram_tensor(name, list(shape), dt, kind="ExternalInput").ap(), Buf(name))
        self.din[name] = t
        return t

    def dram_out(self, name, shape, dt=F32):
        t = TV(self.nc.dram_tensor(name, list(shape), dt, kind="ExternalOutput").ap(), Buf(name))
        self.dout[name] = t
        return t

    def dram_tmp(self, name, shape, dt=F32):
        return TV(self.nc.dram_tensor(name, list(shape), dt, kind="Internal").ap(), Buf(name))

    def sb(self, name, shape, dt=F32):
        t = self.st.enter_context(self.nc.sbuf_tensor(name, list(shape), dt))
        return TV(t[:], Buf(name))

    def psum_f(self):
        i = self.ps_i % 6
        self.ps_i += 1
        return self.psf[i]

    def psum_b(self):
        i = self.psb_i % 2
        self.psb_i += 1
        return self.psb[i]

    def psum_any(self, bf16=False):
        i = self.ps8_i % 8
        self.ps8_i += 1
        return self.ps8b[i] if bf16 else self.ps8[i]


class Arena:
    def __init__(self, kb, words, ap=None):
        self.kb = kb
        self.words = words
        if ap is None:
            self.f = kb.st.enter_context(kb.nc.sbuf_tensor("arena_f", [128, words], F32))
        else:
            self.f = ap
        self.lo = 0
        self.hi = words

    def alloc(self, name, free, dt=F32, top=False, buf=None, at=None):
        nel = int(np.prod(free))
        w = nel if dt == F32 else (nel + 1) // 2
        w = ((w + 15) // 16) * 16
        if at is not None:
            off = at
        elif top:
            self.hi -= w
            off = self.hi
        else:
            off = self.lo
            self.lo += w
        assert self.lo <= self.hi, (name, self.lo, self.hi)
        assert off + w <= self.words
        if dt != F32:
            ap = self.f[:, off:off + w].bitcast(dt)[:, 0:nel]
        else:
            ap = self.f[:, off:off + nel]
        if len(free) > 1:
            names = " ".join("d%d" % i for i in range(len(free)))
            kw = {"d%d" % i: int(free[i]) for i in range(len(free))}
            ap = ap.rearrange("p (%s) -> p %s" % (names, names), **kw)
        tv = TV(ap, buf if buf is not None else Buf(name))
        tv.off = off
        tv.w = w
        return tv

    def reset(self, top=False):
        self.kb.P.barrier()
        self.lo = 0
        if top:
            self.hi = self.words


def lockstep(gens):
    gens = list(gens)
    while gens:
        nxt = []
        for g in gens:
            try:
                next(g)
                nxt.append(g)
            except StopIteration:
                pass
        gens = nxt


def build_program(nb=2, layers=(0, 1), dbg=None):
    nc = bass.Bass("TRN2", target_bir_lowering=False)
    kb = KB(nc, nb, layers, dbg)
    P = kb.P
    tt, ts, stt, act, cp, mm, tr, dma = kb.tt, kb.ts, kb.stt, kb.act, kb.cp, kb.mm, kb.tr, kb.dma
    dbg = dbg or set()
    PAR_OFF.clear()
    PAR_OFF.update(x[6:] for x in dbg if x.startswith("nopar_"))

    x_in = kb.dram_in("x", [nb, T, D])
    ctx_in = kb.dram_in("ctx", [nb, TC, D])
    cvec = kb.dram_in("cvec", [3, D])
    w_ada = kb.dram_in("w_ada", [DEPTH, D, 3 * D])
    b_ada = kb.dram_in("b_ada", [DEPTH, 3 * D])
    norm_g = kb.dram_in("norm_g", [DEPTH, D])
    w_in = kb.dram_in("w_in", [DEPTH, D, IN_W])
    mla_q_norm = kb.dram_in("mla_q_norm", [DEPTH, 256])
    mla_kv_norm = kb.dram_in("mla_kv_norm", [DEPTH, 128])
    mla_w_uq = kb.dram_in("mla_w_uq", [DEPTH, 256, 384])
    mla_w_ukv = kb.dram_in("mla_w_ukv", [DEPTH, 128, 512])
    mla_q_gain = kb.dram_in("mla_q_gain", [DEPTH, 96])
    mla_k_gain = kb.dram_in("mla_k_gain", [DEPTH, 96])
    s5_a_re = kb.dram_in("s5_a_re", [DEPTH, 2, 16, 64])
    s5_a_im = kb.dram_in("s5_a_im", [DEPTH, 2, 16, 64])
    s5_log_dt = kb.dram_in("s5_log_dt", [DEPTH, 2, 16])
    s5_b_re = kb.dram_in("s5_b_re", [DEPTH, 2, 16, 64, 16])
    s5_b_im = kb.dram_in("s5_b_im", [DEPTH, 2, 16, 64, 16])
    s5_c_re = kb.dram_in("s5_c_re", [DEPTH, 2, 16, 16, 64])
    s5_c_im = kb.dram_in("s5_c_im", [DEPTH, 2, 16, 16, 64])
    s5_d = kb.dram_in("s5_d", [DEPTH, 256])
    s5_w_glu = kb.dram_in("s5_w_glu", [DEPTH, 256, 256])
    lru_conv_w = kb.dram_in("lru_conv_w", [DEPTH, 4, 256])
    lru_conv_b = kb.dram_in("lru_conv_b", [DEPTH, 256])
    lru_lambda = kb.dram_in("lru_lambda", [DEPTH, 2, 256])
    lru_w_a = kb.dram_in("lru_w_a", [DEPTH, 2, 4, 64, 64])
    lru_b_a = kb.dram_in("lru_b_a", [DEPTH, 2, 256])
    lru_w_x = kb.dram_in("lru_w_x", [DEPTH, 2, 4, 64, 64])
    lru_b_x = kb.dram_in("lru_b_x", [DEPTH, 2, 256])
    pool_w = kb.dram_in("pool_w", [DEPTH, 4, 64, 64])
    pool_b = kb.dram_in("pool_b", [DEPTH, 256])
    pool_scale = kb.dram_in("pool_scale", [DEPTH, 256])
    w_branch = kb.dram_in("w_branch", [DEPTH, 4, 256, D])
    w_out = kb.dram_in("w_out", [DEPTH, D, D])
    c_ident = kb.dram_in("c_ident", [128, 128])
    c_ropeC = kb.dram_in("c_ropeC", [T, 32])
    c_ropeS = kb.dram_in("c_ropeS", [T, 32])
    c_pool = kb.dram_in("c_pool", [128, 2 + 16 + 16])
    c_mask = kb.dram_in("c_mask", [128, 3, 128])
    y_out = kb.dram_out("y", [nb, T, D])
    xmid = kb.dram_tmp("xmid", [nb, T, D])
    xcmid = kb.dram_tmp("xcmid", [nb, TC, D])
    modD = kb.dram_tmp("modD", [DEPTH, 3, 3 * D])
    st_main = kb.dram_tmp("st_main", [3, 128, 2048])
    st_dir = kb.dram_tmp("st_dir", [2, 128, 2048 + 16 * NCH + 8])
    dbg_out = {}

    def tap(name, shape):
        if name not in dbg_out:
            dbg_out[name] = kb.dram_out("dbg_" + name, shape)
        return dbg_out[name]

    kb.ps8 = []
    kb.pspair = []
    for i in range(4):
        t = kb.st.enter_context(nc.psum_tensor("psp%d" % i, [128, 1024], F32))
        kb.pspair.append(t[:])
        for hh in range(2):
            kb.ps8.append(TV(t[:][:, hh * 512:(hh + 1) * 512], Buf("psf%d" % (2 * i + hh))))
    kb.psf = kb.ps8[0:6]
    kb.psb = [TV(kb.ps8[i].ap.bitcast(BF16), kb.ps8[i].buf) for i in (6, 7)]
    kb.ps8b = [TV(kb.ps8[i].ap.bitcast(BF16), kb.ps8[i].buf) for i in range(8)]

    hT = kb.sb("hT", [128, KC, NPOS], BF16)
    bigbr = kb.sb("bigbr", [128, 8 * NPOS], BF16)
    _slot = {0: 0, 2: 2, 3: 4, 1: 6}
    brT = [[TV(bigbr.ap[:, (_slot[n] + c) * NPOS:(_slot[n] + c + 1) * NPOS], Buf("brT%d%d" % (n, c))) for c in range(2)]
           for n in range(4)]
    identf = kb.sb("identf", [128, 128])
    identb = kb.sb("identb", [128, 128], BF16)
    ropeC = kb.sb("ropeC", [128, 16, 32])
    ropeS = kb.sb("ropeS", [128, 16, 32])
    cpool = kb.sb("cpool", [128, 34])
    cmask = kb.sb("cmask", [128, 3, 128])
    modA = kb.sb("modA", [128, 3, KC])
    modS = kb.sb("modS", [128, 3, KC])
    lp = kb.sb("lp", [128, 64])
    lruBD = kb.sb("lruBD", [128, 2, 2, 2, 128])
    poolBD = kb.sb("poolBD", [128, 2, 128])
    wukv = kb.sb("wukv", [128, 512], BF16)
    wuq = kb.sb("wuq", [128, 2, 384], BF16)
    gains = kb.sb("gains", [128, 2, 96])
    wglu = kb.sb("wglu", [128, 2, 256], BF16)
    AR = Arena(kb, 28000)
    wst = kb.sb("wst", [128, KC, 416], BF16)

    def prefetch(l, parts):
        for (d0, c0, n) in parts:
            dma("pool", wst[:, :, d0:d0 + n].par("wst"), w_in[l, :, c0:c0 + n].re("(c p) n -> p c n", p=128))
    PF_S5 = [(0, O_S5, 256)]
    PF_LRU = [(0, O_LRU, 256)]
    PF_POOL = [(0, O_POOL, 256)]
    PF_MLA = [(0, 0, 160), (160, O_CQ, 256)]
    PF_GATE0 = [(0, O_GATE, 128)]
    AR2 = Arena(kb, 3 * NPOS, ap=bigbr.ap[:, 0:6 * NPOS].bitcast(F32))

    dma("sp", identf, c_ident)
    cp("dve", identb, identf)
    dma("sp", ropeC, c_ropeC.re("(i p) f -> p i f", p=128))
    dma("sp", ropeS, c_ropeS.re("(i p) f -> p i f", p=128))
    dma("sp", cpool, c_pool)
    dma("sp", cmask, c_mask)

    LP_CW = 0
    LP_CB = 8
    LP_NSP = 10
    LP_NSP2 = 14
    LP_BA = 18
    LP_BX = 22
    LP_PSC = 26
    LP_PBS = 28
    LP_G = 30
    LP_TMP = 40

    def prep_layer(l):
        AR.reset(top=True)
        cT = AR.alloc("cT", [KC, 3])
        for v in range(3):
            dma("sp", cT[:, :, v], cvec[v].re("(c p) -> p c", p=128), slow=True)
        cact = AR.alloc("cact", [KC, 3])
        act(cact, cT, AF.Silu)
        brow = AR.alloc("brow", [3 * D])
        dma("sp", brow[0:3, :], b_ada[l:l + 1, :].re("o n -> (o n)").pbc(3))
        modrow = AR.alloc("modrow", [3 * D])
        wa = [AR.alloc("wa%d" % i, [KC, 512]) for i in range(4)]
        for cb in range(4):
            for hh in range(2):
                dma("sp" if hh == 0 else "act", wa[cb][:, hh * 4:(hh + 1) * 4, :].par("wa"),
                    w_ada[l, hh * 512:(hh + 1) * 512, cb * 512:(cb + 1) * 512].re("(c p) n -> p c n", p=128))
        for cb in range(6):
            w = wa[cb % 4]
            if cb >= 4:
                for hh in range(2):
                    dma("sp" if hh == 0 else "act", w[:, hh * 4:(hh + 1) * 4, :].par("wa"),
                        w_ada[l, hh * 512:(hh + 1) * 512, cb * 512:(cb + 1) * 512].re("(c p) n -> p c n", p=128))
            ps = kb.psum_f()
            for k in range(KC):
                mm(ps[0:3, :], cact[:, k, :], w[:, k, :], k == 0, k == KC - 1)
            tt("dve", modrow[0:3, cb * 512:(cb + 1) * 512], ps[0:3, :], brow[0:3, cb * 512:(cb + 1) * 512], ALU.add)
        dma("sp", modD[l], modrow[0:3, :])
        sc = AR.alloc("sc", [3, KC])
        for v in range(3):
            dma("sp", modS[:, v, :], modD[l, v, 0:D].re("(c p) -> p c", p=128), slow=True)
            dma("sp", sc[:, v, :], modD[l, v, D:2 * D].re("(c p) -> p c", p=128), slow=True)
        dma("sp", lp[:, LP_G:LP_G + 8], norm_g[l].re("(c p) -> p c", p=128), slow=True)
        for v in range(3):
            stt(modA[:, v, :], sc[:, v, :], 1.0, lp[:, LP_G:LP_G + 8], ALU.add, ALU.mult)
        for k in range(4):
            dma("sp", lp[:, LP_CW:LP_CW + 8].re("p (c k) -> p c k", c=2)[:, :, k], lru_conv_w[l, k].re("(c p) -> p c", p=128), slow=True)
        dma("sp", lp[:, LP_CB:LP_CB + 2], lru_conv_b[l].re("(c p) -> p c", p=128), slow=True)
        lam = lp[:, LP_TMP:LP_TMP + 4]
        for d in range(2):
            dma("sp", lam[:, d * 2:d * 2 + 2], lru_lambda[l, d].re("(c p) -> p c", p=128), slow=True)
            dma("sp", lp[:, LP_BA + d * 2:LP_BA + d * 2 + 2], lru_b_a[l, d].re("(c p) -> p c", p=128), slow=True)
            dma("sp", lp[:, LP_BX + d * 2:LP_BX + d * 2 + 2], lru_b_x[l, d].re("(c p) -> p c", p=128), slow=True)
        t0 = lp[:, LP_TMP + 4:LP_TMP + 8]
        t1 = lp[:, LP_TMP + 8:LP_TMP + 12]
        t2 = lp[:, LP_TMP + 12:LP_TMP + 16]
        t3 = lp[:, LP_TMP + 16:LP_TMP + 20]
        ts("dve", t0, lam, -1.0, ALU.mult)
        tt("dve", t0, t0, lam, ALU.max)
        act(t1, t0, AF.Exp, scale=-1.0)
        ts("dve", t2, t1, 2.0, ALU.add)
        kb.recip(t2, t2)
        tt("dve", t2, t2, t1, ALU.mult)
        tt("dve", t3, t2, t2, ALU.mult)
        ts("dve", t0, t3, 1.0 / 11.0, ALU.mult, 1.0 / 9.0, ALU.add)
        for cf in (1.0 / 7.0, 1.0 / 5.0, 1.0 / 3.0, 1.0):
            tt("dve", t0, t0, t3, ALU.mult)
            ts("dve", t0, t0, cf, ALU.add)
        tt("dve", t0, t0, t2, ALU.mult)
        ts("dve", t1, lam, -1.0, ALU.mult, 0.0, ALU.max)
        stt(t0, t0, 2.0, t1, ALU.mult, ALU.add)
        ts("dve", lp[:, LP_NSP:LP_NSP + 4], t0, -8.0, ALU.mult)
        ts("dve", lp[:, LP_NSP2:LP_NSP2 + 4], t0, -16.0, ALU.mult)
        kb.memset("dve", lruBD, 0.0)
        for d in range(2):
            for gi, wsrc in enumerate((lru_w_a, lru_w_x)):
                for c in range(2):
                    for h in range(2):
                        dma("sp" if (c + h) % 2 == 0 else "act", lruBD[h * 64:(h + 1) * 64, d, gi, c, h * 64:(h + 1) * 64].par("ldw"), wsrc[l, d, 2 * c + h])
        kb.memset("dve", poolBD, 0.0)
        for c in range(2):
            for h in range(2):
                dma("sp", poolBD[h * 64:(h + 1) * 64, c, h * 64:(h + 1) * 64].par("ldw"), pool_w[l, 2 * c + h])
        dma("sp", lp[:, LP_PSC:LP_PSC + 2], pool_scale[l].re("(c p) -> p c", p=128), slow=True)
        dma("sp", lp[:, LP_PBS:LP_PBS + 2], pool_b[l].re("(c p) -> p c", p=128), slow=True)
        tt("dve", lp[:, LP_PBS:LP_PBS + 2], lp[:, LP_PBS:LP_PBS + 2], lp[:, LP_PSC:LP_PSC + 2], ALU.mult)
        kvn = lp[:, LP_TMP + 20:LP_TMP + 21]
        qn = lp[:, LP_TMP + 21:LP_TMP + 23]
        dma("sp", kvn, mla_kv_norm[l].re("(p o) -> p o", o=1), slow=True)
        dma("sp", qn, mla_q_norm[l].re("(c p) -> p c", p=128), slow=True)
        wtmp = AR.alloc("wtmp", [2, 512])
        dma("sp", wtmp[:, 0, :], mla_w_ukv[l])
        ts("dve", wukv, wtmp[:, 0, :], kvn, ALU.mult)
        wtmp2 = AR.alloc("wtmp2", [2, 384])
        dma("sp", wtmp2, mla_w_uq[l].re("(c p) n -> p c n", p=128))
        for c in range(2):
            ts("dve", wuq[:, c, :], wtmp2[:, c, :], qn[:, c:c + 1], ALU.mult)
        dma("sp", gains[:, 0, :], mla_q_gain[l:l + 1, :].re("o n -> (o n)").pbc(128))
        dma("sp", gains[:, 1, :], mla_k_gain[l:l + 1, :].re("o n -> (o n)").pbc(128))
        ts("dve", gains[:, 0, :], gains[:, 0, :], 96.0 ** -0.5, ALU.mult)
        dma("pool", wglu, s5_w_glu[l].re("(c p) n -> p c n", p=128))

    def src_tile(l, b, i):
        if i < 2:
            return (ctx_in if l == 0 else xcmid)[b, i * 128:(i + 1) * 128, :]
        return (x_in if l == 0 else xmid)[b, (i - 2) * 128:(i - 1) * 128, :]

    def phase_norm(l, b):
        AR.reset(top=True)
        NX = 4
        xt = [AR.alloc("xt%d" % i, [D]) for i in range(NX)]
        junk = [AR.alloc("junk%d" % i, [D]) for i in range(NX)]
        xn = [AR.alloc("xn%d" % i, [D], BF16) for i in range(NX)]
        st4 = [AR.alloc("st%d" % i, [4]) for i in range(NX)]
        def tile_gen(i):
            v = 2 if i < 2 else b
            x_t, xn_t, s4 = xt[i % NX], xn[i % NX], st4[i % NX]
            dma("sp" if i % 2 == 0 else "act", x_t, src_tile(l, b, i))
            yield
            act(junk[i % NX], x_t, AF.Square, accum=s4[:, 0:1])
            yield
            ts("dve", s4[:, 1:2], s4[:, 0:1], 1.0 / D, ALU.mult, EPS, ALU.add)
            yield
            act(s4[:, 2:3], s4[:, 1:2], AF.Sqrt)
            yield
            kb.recip(s4[:, 3:4], s4[:, 2:3])
            ts("dve", xn_t, x_t, s4[:, 3:4], ALU.mult)
            yield
            psA = kb.psum_any(bf16=True)
            psB = kb.psum_any(bf16=True)
            for c in range(KC):
                pz = psA if c % 2 == 0 else psB
                tr(pz[:, (c // 2) * 128:(c // 2 + 1) * 128], xn_t[:, c * 128:(c + 1) * 128], identb)
            yield
            for c in range(KC):
                o = hT[:, c, i * 128:(i + 1) * 128].par("norm")
                if c % 2 == 0:
                    act(o, psA[:, (c // 2) * 128:(c // 2 + 1) * 128], AF.Identity, bias=modS[:, v, c:c + 1], scale=modA[:, v, c:c + 1])
                else:
                    ts("dve", o, psB[:, (c // 2) * 128:(c // 2 + 1) * 128], modA[:, v, c:c + 1], ALU.mult, modS[:, v, c:c + 1], ALU.add)
            yield
        for g0 in range(0, 18, NX):
            lockstep([tile_gen(i) for i in range(g0, min(g0 + NX, 18))])

    def load_win(l, name, c0, ncols, top=False):
        w = AR.alloc(name, [KC, ncols], BF16, top=top)
        dma("pool", w, w_in[l, :, c0:c0 + ncols].re("(c p) n -> p c n", p=128))
        return w

    def proj_fm(w, col0, dst, p0, p1, evac):
        pos = p0
        while pos < p1:
            n = min(512, p1 - pos)
            ps = kb.psum_f()
            for k in range(KC):
                mm(ps[:, 0:n], w[:, k, col0:col0 + 128], hT[:, k, pos:pos + n], k == 0, k == KC - 1)
            evac(ps, pos, n)
            pos += n

    def phase_lru(l, b, with_ctx):
        AR.reset()
        w = wst[:, :, 0:256]
        xr = AR.alloc("xr", [2, NPOS])
        xc_ = AR.alloc("xc", [2, NPOS])
        ysum = AR.alloc("ysum", [2, NPOS])
        NB_ = 512
        tmp_sets = [{nm: AR.alloc("%s%d" % (nm, q), [NB_]) for nm in ("r", "i", "a", "a2", "bb")} for q in range(4)]
        for c in range(2):
            proj_fm(w, c * 128, xr, 0, NPOS, lambda ps, pos, n, c=c: cp("act", xr[:, c, pos:pos + n].par("proj"), ps[:, 0:n]))
        prefetch(l, PF_POOL)
        for c in range(2):
            cw = lambda k: lp[:, LP_CW + c * 4 + k:LP_CW + c * 4 + k + 1]
            for (s0, s1) in ((0, TC), (TC, NPOS)):
                ts("dve", xc_[:, c, s0:s1], xr[:, c, s0:s1], cw(2), ALU.mult, lp[:, LP_CB + c:LP_CB + c + 1], ALU.add)
                stt(xc_[:, c, s0 + 1:s1], xr[:, c, s0:s1 - 1], cw(1), xc_[:, c, s0 + 1:s1], ALU.mult, ALU.add)
                stt(xc_[:, c, s0 + 2:s1], xr[:, c, s0:s1 - 2], cw(0), xc_[:, c, s0 + 2:s1], ALU.mult, ALU.add)
                stt(xc_[:, c, s0:s1 - 1], xr[:, c, s0 + 1:s1], cw(3), xc_[:, c, s0:s1 - 1], ALU.mult, ALU.add)
        blocks = [(0, TC)] + [(TC + j * NB_, min(TC + (j + 1) * NB_, NPOS)) for j in range((T + NB_ - 1) // NB_)]

        def chain(d, c):
            dc = d * 2 + c
            tmp = tmp_sets[dc]
            out = ysum if d == 0 else xr
            order = blocks if d == 0 else [blocks[0]] + blocks[:0:-1]
            prev = None
            for (s0, s1) in order:
                n = s1 - s0
                r_, i_, a_, a2_, bb_ = (tmp[k][:, 0:n] for k in ("r", "i", "a", "a2", "bb"))
                for gi, dst in ((0, r_), (1, i_)):
                    q = 0
                    while q < n:
                        m = min(512, n - q)
                        ps = kb.psum_f()
                        mm(ps[:, 0:m], lruBD[:, d, gi, c, :], xc_[:, c, s0 + q:s0 + q + m], True, True)
                        bcol = (LP_BA if gi == 0 else LP_BX) + dc
                        act(dst[:, q:q + m], ps[:, 0:m], AF.Sigmoid, bias=lp[:, bcol:bcol + 1])
                        q += m
                yield
                act(a_, r_, AF.Exp, scale=lp[:, LP_NSP + dc:LP_NSP + dc + 1])
                act(a2_, r_, AF.Exp, scale=lp[:, LP_NSP2 + dc:LP_NSP2 + dc + 1])
                tt("pool", bb_, i_, xc_[:, c, s0:s1], ALU.mult)
                yield
                ts("dve", a2_, a2_, -1.0, ALU.mult, 1.0, ALU.add)
                yield
                act(a2_, a2_, AF.Sqrt)
                yield
                tt("dve", bb_, bb_, a2_, ALU.mult)
                init = 0.0 if prev is None else prev
                if d == 0:
                    kb.scan(out[:, c, s0:s1], a_, bb_, init)
                    prev = out[:, c, s1 - 1:s1]
                else:
                    kb.scan(out[:, c, s0:s1][:, ::-1], a_[:, ::-1], bb_[:, ::-1], init)
                    prev = out[:, c, s0:s0 + 1]
                yield
        lockstep([chain(0, 0), chain(1, 0), chain(0, 1), chain(1, 1)])
        for c in range(2):
            lo = 0 if with_ctx else TC
            tt("dve", brT[2][c][:, lo:NPOS], ysum[:, c, lo:NPOS], xr[:, c, lo:NPOS], ALU.add)

    AR_car = [kb.sb("car%d" % i, [128, 1]) for i in range(2)]

    def phase_pool(l, b, with_ctx):
        AR.reset()
        w = wst[:, :, 0:256]
        xp = AR.alloc("xp", [2, NPOS])
        cs = AR.alloc("cs", [2, NPOS + 2 * 17 + 2])
        pm = AR.alloc("pm", [2, NPOS])
        ones = AR.alloc("ones", [T])
        kb.memset("pool", ones, 1.0)
        for c in range(2):
            proj_fm(w, c * 128, xp, 0, NPOS, lambda ps, pos, n, c=c: cp("act", xp[:, c, pos:pos + n].par("proj"), ps[:, 0:n]))
        prefetch(l, PF_MLA)
        segs = [(0, TC, 0)] + [(TC, NPOS, TC + 17)]
        if not with_ctx:
            segs = segs[1:]
        for c in range(2):
            for (s0, s1, o0) in segs:
                L = s1 - s0
                kb.memset("pool", cs[:, c, o0:o0 + 9], 0.0)
                kb.scan(cs[:, c, o0 + 9:o0 + 9 + L], ones[:, 0:L], xp[:, c, s0:s1], 0.0)
                ts("dve", cs[:, c, o0 + 9 + L:o0 + 17 + L], cs[:, c, o0:o0 + 8], cs[:, c, o0 + 8 + L:o0 + 9 + L], ALU.add)
                for h in range(2):
                    hw = (1, 2, 4, 8)[2 * c + h]
                    rows = slice(h * 64, (h + 1) * 64)
                    base = o0 + 8
                    tt("dve", pm[rows, c, s0:s1], cs[rows, c, base + hw:base + hw + L], cs[rows, c, base - hw:base - hw + L], ALU.subtract)
                ts("dve", pm[:, c, s0:s1], pm[:, c, s0:s1], cpool[:, c:c + 1], ALU.mult)
                tt("dve", pm[:, c, s0:s0 + 8], pm[:, c, s0:s0 + 8], cpool[:, 2 + c * 8:2 + c * 8 + 8], ALU.mult)
                tt("dve", pm[:, c, s1 - 8:s1], pm[:, c, s1 - 8:s1], cpool[:, 18 + c * 8:18 + c * 8 + 8], ALU.mult)
                tt("pool", pm[:, c, s0:s1], pm[:, c, s0:s1], xp[:, c, s0:s1], ALU.subtract)
                pos = s0
                while pos < s1:
                    n = min(512, s1 - pos)
                    ps = kb.psum_f()
                    mm(ps[:, 0:n], poolBD[:, c, :], pm[:, c, pos:pos + n], True, True)
                    act(brT[3][c][:, pos:pos + n].par("pool"), ps[:, 0:n], AF.Identity, bias=lp[:, LP_PBS + c:LP_PBS + c + 1],
                        scale=lp[:, LP_PSC + c:LP_PSC + c + 1])
                    pos += n

    def phase_mla(l, b, with_ctx):
        AR.reset()
        wkv = wst[:, :, 0:160]
        wq = wst[:, :, 160:416]
        qT = AR.alloc("qT", [4, NPOS], BF16)
        kT = AR.alloc("kT", [4, NPOS], BF16)
        Vt = AR.alloc("Vt", [18, 4, 68], BF16)
        lo_fixed = AR.lo
        kb.memset("dve", Vt.re("p a b c -> p (a b c)"), 1.0)
        for h_ in range(4):
            kb.memset("pool", qT[:, h_, :], 0.0)
            kb.memset("pool", kT[:, h_, :], 0.0)
        NS = 3
        sm = [AR.alloc("sm%d" % i, [32]) for i in range(NS)]
        kr = [AR.alloc("kr%d" % i, [32]) for i in range(NS)]
        cn = [AR.alloc("cn%d" % i, [384], BF16) for i in range(NS)]
        cnT = [AR.alloc("cnT%d" % i, [3, 128], BF16) for i in range(NS)]
        sq = [AR.alloc("sq%d" % i, [704]) for i in range(NS)]
        qk = [AR.alloc("qk%d" % i, [2, 4, 96]) for i in range(NS)]
        rg = [AR.alloc("rg%d" % i, [2, 4, 96]) for i in range(NS)]
        qkb = [AR.alloc("qkb%d" % i, [2, 4, 96], BF16) for i in range(NS)]
        rt = [AR.alloc("rt%d" % i, [2, 2, 4, 32]) for i in range(NS)]
        kvs_ = [AR.alloc("kvs%d" % i, [512]) for i in range(NS)]
        qs_ = [AR.alloc("qs%d" % i, [384]) for i in range(NS)]

        def info(i):
            is_ctx = i < 2
            do_q = (not is_ctx) or with_ctx
            return is_ctx, do_q, i % NS, slice(i * 128, (i + 1) * 128)

        def stage_a(i):
            is_ctx, do_q, j, pos = info(i)
            s, cn_, cnT_, sq_ = sm[j], cn[j], cnT[j], sq[j]
            ps1 = kb.psum_any()
            for k in range(KC):
                mm(ps1[:, 0:160], hT[:, k, pos], wkv[:, k, :], k == 0, k == KC - 1)
            if do_q:
                for k in range(KC):
                    mm(ps1[:, 160:416], hT[:, k, pos], wq[:, k, :], k == 0, k == KC - 1)
            yield
            act(sq_[:, 0:128], ps1[:, 0:128], AF.Square, accum=s[:, 0:1])
            if do_q:
                act(sq_[:, 128:384], ps1[:, 160:416], AF.Square, accum=s[:, 1:2])
            else:
                kb.memset("pool", s[:, 1:2], 1.0)
            cp("act", kr[j], ps1[:, 128:160])
            yield
            act(s[:, 2:3], s[:, 0:1], AF.Sqrt, scale=1.0 / 128, bias=EPS)
            act(s[:, 3:4], s[:, 1:2], AF.Sqrt, scale=1.0 / 256, bias=EPS)
            yield
            kb.recip(s[:, 4:6], s[:, 2:4])
            ts("dve", cn_[:, 0:128], ps1[:, 0:128], s[:, 4:5], ALU.mult)
            if do_q:
                ts("dve", cn_[:, 128:384], ps1[:, 160:416], s[:, 5:6], ALU.mult)
            yield
            psT = kb.psum_any(bf16=True)
            for c in range(3 if do_q else 1):
                tr(psT[:, c * 128:(c + 1) * 128], cn_[:, c * 128:(c + 1) * 128], identb)
            yield
            cp("act", cnT_[:, 0:(3 if do_q else 1), :], psT[:, 0:(384 if do_q else 128)].re("p (c n) -> p c n", n=128))
            yield

        def stage_b(i):
            is_ctx, do_q, j, pos = info(i)
            s, cnT_, sq_, qk_, qkb_, rt_, rg_ = sm[j], cnT[j], sq[j], qk[j], qkb[j], rt[j], rg[j]
            pskv = kb.psum_any()
            mm(pskv, cnT_[:, 0, :], wukv, True, True)
            if do_q:
                psq = kb.psum_any()
                for c in range(2):
                    mm(psq[:, 0:384], cnT_[:, 1 + c, :], wuq[:, c, :], c == 0, c == 1)
            yield
            cp("act", kvs_[j], pskv)
            if do_q:
                cp("act", qs_[j], psq[:, 0:384])
            kv3 = kvs_[j].re("p (h e) -> p h e", h=4)
            q3 = qs_[j].re("p (h e) -> p h e", h=4)
            krope = kr[j]
            act(sq_[:, 640:672], krope, AF.Square, accum=s[:, 7:8])
            cp("act", Vt[:, i, :, 0:64].par("mlav"), kv3[:, :, 64:128])
            yield
            ksq = sq_[:, 0:256].re("p (h e) -> p h e", h=4)
            qsq = sq_[:, 256:640].re("p (h e) -> p h e", h=4)
            tt("pool", ksq, kv3[:, :, 0:64], kv3[:, :, 0:64], ALU.mult)
            if do_q:
                tt("pool", qsq, q3, q3, ALU.mult)
            yield
            kb.reduce(s[:, 12:16], ksq)
            if do_q:
                kb.reduce(s[:, 8:12], qsq)
            else:
                kb.memset("dve", s[:, 8:12], 1.0)
            ts("dve", s[:, 12:16], s[:, 12:16], s[:, 7:8], ALU.add)
            yield
            act(s[:, 8:16], s[:, 8:16], AF.Sqrt, scale=1.0 / 96, bias=EPS)
            yield
            kb.recip(s[:, 16:24], s[:, 8:16])
            yield
            tt("pool", rg_, s[:, 16:24].re("p (w h) -> p w h", w=2).us(3).bc([128, 2, 4, 96]),
               gains.us(2).bc([128, 2, 4, 96]), ALU.mult)
            yield
            if do_q:
                tt("dve", qk_[:, 0].par("qk"), q3, rg_[:, 0], ALU.mult)
            else:
                kb.memset("pool", qk_[:, 0].par("qk"), 0.0)
            tt("dve", qk_[:, 1, :, 0:64].par("qk"), kv3[:, :, 0:64], rg_[:, 1, :, 0:64], ALU.mult)
            tt("pool", qk_[:, 1, :, 64:96].par("qk"), krope.us(1).bc([128, 4, 32]), rg_[:, 1, :, 64:96], ALU.mult)
            yield
            if not is_ctx:
                ti = i - 2
                v = qk_[:, :, :, 64:96]
                t1_ = rt_[:, 0].par("rt")
                t2_ = rt_[:, 1]
                Cb = ropeC[:, ti, :].us(1).us(1).bc([128, 2, 4, 32])
                tt("pool", t1_, v, Cb, ALU.mult)
                for a in range(2):
                    for s_ in range(2):
                        o_ = t2_[:, :, :, a * 16 + s_ * 8:a * 16 + s_ * 8 + 8].par("rt")
                        i_ = qk_[:, :, :, 64 + a * 16 + (1 - s_) * 8:64 + a * 16 + (1 - s_) * 8 + 8]
                        Sb = ropeS[:, ti, a * 16 + s_ * 8:a * 16 + s_ * 8 + 8].us(1).us(1).bc([128, 2, 4, 8])
                        tt("dve" if (a + s_) % 2 == 0 else "pool", o_, i_, Sb, ALU.mult)
                yield
                tt("dve", v, t1_, t2_, ALU.add)
            cp("dve", qkb_, qk_)
            yield

        def stage_c(i):
            is_ctx, do_q, j, pos = info(i)
            qkb_ = qkb[j]
            psT2 = kb.psum_any(bf16=True)
            for w_ in range(2):
                if w_ == 0 and not do_q:
                    continue
                for h in range(4):
                    tr(psT2[0:96, (w_ * 4 + h) * 128:(w_ * 4 + h + 1) * 128], qkb_[:, w_, h, :], identb)
            yield
            if do_q:
                cp("act", qT[0:96, :, pos].par("mlaqk"), psT2[0:96, 0:512].re("p (h n) -> p h n", h=4))
            cp("act", kT[0:96, :, pos].par("mlaqk"), psT2[0:96, 512:1024].re("p (h n) -> p h n", h=4))

        def tile_gen(i):
            yield from stage_a(i)
            yield from stage_b(i)
            yield from stage_c(i)
        for g0 in range(0, 18, NS):
            lockstep([tile_gen(i) for i in range(g0, g0 + NS)])
        if "mla_stop1" in dbg:
            return
        prefetch(l, PF_GATE0)
        P.barrier()
        AR.lo = lo_fixed
        omla = AR.alloc("omla", [18, 256], BF16)
        pT = [AR.alloc("pT%d" % i, [2, 512], BF16) for i in range(3)]
        rc = [AR.alloc("rc%d" % i, [4]) for i in range(2)]
        jobs = []
        if with_ctx:
            jobs.append((0, TC, 0, 2))
        for qb in range(4):
            jobs.append((TC + qb * 512, TC + (qb + 1) * 512, 0, 18))
        pairs = []
        for (q0, q1, kt0, kt1) in jobs:
            for h in range(4):
                for kt in range(kt0, kt1, 2):
                    pairs.append((q0, q1, kt0, kt1, h, kt))
        SB = (0, 3)

        def s_mm(pi_):
            q0, q1, kt0, kt1, h, kt = pairs[pi_]
            tsel = SB[pi_ % 2]
            for j in range(2):
                mm(kb.ps8[2 * tsel + j][:, 0:q1 - q0], kT[:, h, (kt + j) * 128:(kt + j + 1) * 128], qT[:, h, q0:q1], True, True)
        s_mm(0)
        for pi_, (q0, q1, kt0, kt1, h, kt) in enumerate(pairs):
            nq = q1 - q0
            nqt = nq // 128
            pso = [kb.psf[2 + t_] for t_ in range(nqt)]
            if pi_ + 1 < len(pairs):
                s_mm(pi_ + 1)
            tsel = SB[pi_ % 2]
            p_ = pT[pi_ % 3]
            src = kb.pspair[tsel].rearrange("p (j n) -> p j n", j=2)[:, :, 0:nq]
            dst = p_.ap[:, :, 0:nq]
            P.op("act", lambda e, src=src, dst=dst: e.activation(out=dst, in_=src, func=AF.Exp, bias=0.0, scale=1.0),
                 [kb.ps8[2 * tsel].buf, kb.ps8[2 * tsel + 1].buf], [p_.buf])
            for j in range(2):
                for t_ in range(nqt):
                    mm(pso[t_][:, 0:68], p_[:, j, t_ * 128:(t_ + 1) * 128], Vt[:, kt + j, h, :], kt + j == kt0, kt + j == kt1 - 1)
            if kt + 2 >= kt1:
                for t_ in range(nqt):
                    r_ = rc[t_ % 2]
                    kb.recip(r_[:, 0:1], pso[t_][:, 64:65])
                    ts("dve", omla[:, (q0 // 128) + t_, h * 64:(h + 1) * 64].par("omla"), pso[t_][:, 0:64], r_[:, 0:1], ALU.mult)
        for i in range(0 if with_ctx else 2, 18):
            psT = kb.psum_b()
            for c in range(2):
                tr(psT[:, c * 128:(c + 1) * 128], omla[:, i, c * 128:(c + 1) * 128], identb)
            for c in range(2):
                cp("act", brT[0][c][:, i * 128:(i + 1) * 128].par("brt0"), psT[:, c * 128:(c + 1) * 128])

    def phase_s5(l, b, with_ctx):
        AR.reset()
        AR2.lo = 0
        U = AR.alloc("U", [16, NCH])
        SN = [[AR.alloc("SN%d%d" % (d, ri), [8, NCH + 2]) for ri in range(2)] for d in range(2)]
        Ere = AR.alloc("Ere", [2, 8, 128])
        nEim = AR.alloc("nEim", [2, 8, 128])
        W3 = AR.alloc("W3", [16, 128])
        scr0 = AR.lo
        W1 = [AR2.alloc("W1%d" % i, [16, 64]) for i in range(2)]
        tab = [AR2.alloc("tab%d" % i, [8, NCH]) for i in range(2)]
        cached = (b > 0) and ("s5_nocache" not in dbg)

        def load_dir(d):
            dma("sp", W1[0].re("p g n -> p (g n)"), st_dir[d, :, 0:1024])
            dma("sp", W1[1].re("p g n -> p (g n)"), st_dir[d, :, 1024:2048])
            dma("act", tab[0].re("p g n -> p (g n)"), st_dir[d, :, 2048:2048 + 8 * NCH])
            dma("act", tab[1].re("p g n -> p (g n)"), st_dir[d, :, 2048 + 8 * NCH:2048 + 16 * NCH])
        if cached:
            dma("sp", Ere.re("p d g n -> p (d g n)"), st_main[0])
            dma("act", nEim.re("p d g n -> p (d g n)"), st_main[1])
            dma("sp", W3.re("p g n -> p (g n)"), st_main[2])
            load_dir(0)
        ws5 = wst[:, :, 0:256]
        Uc = AR.alloc("Uc", [16, 8, 16])
        kblocks = [(0, 32, 0), (32, 128, TC), (160, 128, TC + 1024)]
        for (k0, M, p0) in kblocks:
            pss = [kb.psum_f() for _ in range(4)]
            for r in range(8):
                ps = pss[r // 2]
                for k in range(KC):
                    mm(ps[0:M, (r % 2) * 256:(r % 2 + 1) * 256], hT[:, k, p0 + r:p0 + 8 * M:8], ws5[:, k, :], k == 0, k == KC - 1)
            for q in range(4):
                src = pss[q][0:M, :].re("p (r g i) -> p r g i", r=2, g=16)
                dst = Uc[0:M, :, 2 * q:2 * q + 2, :].re("p g r i -> p r g i").par("s5uc")
                cp("act" if q % 2 == 0 else "dve", dst, src)
            for gq in range(4):
                ps = kb.psum_f()
                for gg_ in range(4):
                    g = gq * 4 + gg_
                    tr(ps[:, gg_ * 128:gg_ * 128 + M], Uc[0:M, g, :, :].re("p r i -> p (r i)"), identf[0:M, 0:M])
                cp("act" if gq % 2 == 0 else "dve", U[:, gq * 4:gq * 4 + 4, k0:k0 + M].par("s5u"),
                   ps.re("p (g n) -> p g n", g=4)[:, :, 0:M])
        prefetch(l, PF_LRU)
        P.barrier()
        AR.lo = scr0
        if "s5_stop1" in dbg:
            return
        if not cached:
            kb.memset("pool", W3, 0.0)
        prm = AR.alloc("prm", [40, 8])
        PW = [AR.alloc("pw%d" % i, [10, 8]) for i in range(2)]
        Bb = [AR.alloc("Bb%d" % i, [8, 16]) for i in range(2)]
        Braw = [AR.alloc("Braw%d" % i, [8, 16]) for i in range(2)]
        Craw = [AR.alloc("Craw%d" % i, [8, 16]) for i in range(2)]
        Dbc = AR.alloc("Dbc", [256])
        t8 = AR.alloc("t8", [8, 16])
        w3t = [AR.alloc("w3t%d" % i, [128]) for i in range(2)]
        tE = AR.alloc("tE", [8, 128])
        Fm = [AR.alloc("F%d" % i, [8, 8, 16]) for i in range(2)]
        Ep = [AR.alloc("Ep%d" % i, [8, 128]) for i in range(2)]
        Cnat = [AR.alloc("Cnat%d" % i, [16, 64], at=Ep[i].off, buf=Ep[i].buf) for i in range(2)]
        rs_sets = [[AR.alloc("rs%d_%d" % (q, i), [NCH], at=Fm[0].off + (q * 7 + i) * NCH) for i in range(6)]
                   for q in range(2)]
        rho_sets = [AR.alloc("rho1_%d" % q, [NCH], at=Fm[0].off + (q * 7 + 6) * NCH) for q in range(2)]
        assert Fm[0].off + 14 * NCH <= Ep[1].off + Ep[1].w
        dma("sp", Dbc, s5_d[l:l + 1, :].re("o n -> (o n)").pbc(128))
        dsel = cmask[:, 2, :]

        prm_bufs = [Buf("prm%d" % i) for i in range(40)]
        pw_bufs = [[Buf("pw%d_%d" % (ri, j)) for j in range(10)] for ri in range(2)]

        def pv(i):
            return TV(prm.ap[:, i, :], prm_bufs[i])

        def pw(ri, j):
            return TV(PW[ri].ap[:, j, :], pw_bufs[ri][j])

        for d in range(2):
            if d == 1 and not cached:
                P.barrier()
            if not cached:
                dma("sp", pv(0), s5_a_re[l, d].re("(gp h) p -> (h p) gp", h=2), slow=True)
                dma("act", pv(1), s5_a_im[l, d].re("(gp h) p -> (h p) gp", h=2), slow=True)
                ldt = t8[:, 0, :]
                dma("sp", ldt, s5_log_dt[l, d:d + 1, :].re("o g -> (o g)").pbc(128))
                dma("sp", Cnat[0][0:16], s5_c_re[l, d].re("g o p -> o g p"))
                dma("act", Cnat[1][0:16], s5_c_im[l, d].re("g o p -> o g p"))
                for h in range(2):
                    rows = slice(h * 64, (h + 1) * 64)
                    cp("dve", pv(2)[rows], ldt[rows, h:16:2])
                    dma("sp", Braw[0][rows].par("ldw"), s5_b_re[l, d].re("(gp h) p i -> h p gp i", h=2)[h])
                    dma("act", Braw[1][rows].par("ldw"), s5_b_im[l, d].re("(gp h) p i -> h p gp i", h=2)[h])
                for ri in range(2):
                    ps = kb.psum_f()
                    for g in range(16):
                        gp, h = g // 2, g % 2
                        mm(ps[h * 64:(h + 1) * 64, gp * 16:(gp + 1) * 16], Cnat[ri][0:16, g, :], identf[0:16, 0:16], True, True)
                    cp("act", Craw[ri], ps[:, 0:128].re("p (g o) -> p g o", g=8))
                if "s5_b1" in dbg:
                    return
                act(pv(2), pv(2), AF.Exp)
                tt("dve", pv(3), pv(0), pv(2), ALU.mult)
                tt("dve", pv(4), pv(1), pv(2), ALU.mult)
                act(pv(5), pv(3), AF.Exp)
                for (dst, shift) in ((6, 0.0), (7, math.pi / 2)):
                    xx, kk, ki = pv(30), pv(31), pv(32)
                    ts("dve", xx, pv(4), shift, ALU.add)
                    ts("dve", kk, xx, 1.0 / (2 * math.pi), ALU.mult)
                    kint = TV(ki.ap.bitcast(I32), ki.buf)
                    cp("dve", kint, kk)
                    cp("dve", kk, kint)
                    stt(xx, kk, -6.28125, xx, ALU.mult, ALU.add)
                    stt(xx, kk, -(2 * math.pi - 6.28125), xx, ALU.mult, ALU.add)
                    ts("dve", xx, xx, math.pi, ALU.min, -math.pi, ALU.max)
                    act(pv(dst), xx, AF.Sin)
                kb.memset("dve", pw(0, 0), 1.0)
                kb.memset("dve", pw(1, 0), 0.0)
                tt("dve", pw(0, 1), pv(5), pv(7), ALU.mult)
                tt("dve", pw(1, 1), pv(5), pv(6), ALU.mult)
                for j in range(2, 9):
                    tt("dve", pv(30), pw(0, j - 1), pw(0, 1), ALU.mult)
                    tt("dve", pv(31), pw(1, j - 1), pw(1, 1), ALU.mult)
                    tt("dve", pw(0, j), pv(30), pv(31), ALU.subtract)
                    tt("dve", pv(30), pw(0, j - 1), pw(1, 1), ALU.mult)
                    tt("dve", pv(31), pw(1, j - 1), pw(0, 1), ALU.mult)
                    tt("dve", pw(1, j), pv(30), pv(31), ALU.add)
                act(pv(8), pv(3), AF.Exp, scale=-16.0)
                tt("dve", pw(0, 9), pw(0, 8), pv(8), ALU.mult)
                tt("dve", pw(1, 9), pw(1, 8), pv(8), ALU.mult)
                ts("dve", pw(1, 9), pw(1, 9), -1.0, ALU.mult)
                act(pv(9), pv(3), AF.Exp, scale=8.0)
                act(pv(10), pv(3), AF.Exp, scale=-8.0)
                tt("dve", pv(11), pw(0, 8), pv(10), ALU.mult)
                tt("dve", pv(12), pw(1, 8), pv(10), ALU.mult)
                ts("dve", pv(13), pw(0, 1), -1.0, ALU.add)
                tt("dve", pv(14), pv(0), pv(0), ALU.mult)
                tt("dve", pv(15), pv(1), pv(1), ALU.mult)
                tt("dve", pv(14), pv(14), pv(15), ALU.add)
                kb.recip(pv(14), pv(14))
                tt("dve", pv(15), pv(13), pv(0), ALU.mult)
                tt("dve", pv(16), pw(1, 1), pv(1), ALU.mult)
                tt("dve", pv(15), pv(15), pv(16), ALU.add)
                tt("dve", pv(15), pv(15), pv(14), ALU.mult)
                tt("dve", pv(16), pw(1, 1), pv(0), ALU.mult)
                tt("dve", pv(17), pv(13), pv(1), ALU.mult)
                tt("dve", pv(16), pv(16), pv(17), ALU.subtract)
                tt("dve", pv(16), pv(16), pv(14), ALU.mult)
                fre = pv(15).us(2).bc([128, 8, 16])
                fim = pv(16).us(2).bc([128, 8, 16])
                tt("dve", Bb[0], Braw[0], fre, ALU.mult)
                tt("dve", t8, Braw[1], fim, ALU.mult)
                tt("dve", Bb[0], Bb[0], t8, ALU.subtract)
                tt("dve", Bb[1], Braw[1], fre, ALU.mult)
                tt("dve", t8, Braw[0], fim, ALU.mult)
                tt("dve", Bb[1], Bb[1], t8, ALU.add)
                for t_ in range(8):
                    f_ = t_ + 1 if d == 0 else 8 - t_
                    pr = pw(0, f_).us(2).bc([128, 8, 16])
                    pi_ = pw(1, f_).us(2).bc([128, 8, 16])
                    eo = Ere[:, d, :, t_ * 16:(t_ + 1) * 16]
                    ei = nEim[:, d, :, t_ * 16:(t_ + 1) * 16]
                    tt("dve", eo, Craw[0], pr, ALU.mult)
                    tt("dve", t8, Craw[1], pi_, ALU.mult)
                    tt("dve", eo, eo, t8, ALU.subtract)
                    tt("dve", ei, Craw[0], pi_, ALU.mult)
                    tt("dve", t8, Craw[1], pr, ALU.mult)
                    tt("dve", ei, ei, t8, ALU.add)
                    ts("dve", ei, ei, -1.0, ALU.mult)
                for r in range(8):
                    e_ = 7 - r if d == 0 else r
                    pr = pw(0, e_).us(2).bc([128, 8, 16])
                    pi_ = pw(1, e_).us(2).bc([128, 8, 16])
                    tt("dve", Fm[0][:, :, r, :], Bb[0], pr, ALU.mult)
                    tt("dve", t8, Bb[1], pi_, ALU.mult)
                    tt("dve", Fm[0][:, :, r, :], Fm[0][:, :, r, :], t8, ALU.subtract)
                    tt("dve", Fm[1][:, :, r, :], Bb[1], pr, ALU.mult)
                    tt("dve", t8, Bb[0], pi_, ALU.mult)
                    tt("dve", Fm[1][:, :, r, :], Fm[1][:, :, r, :], t8, ALU.add)
                qr = pw(0, 9).us(2).bc([128, 8, 128])
                qi = pw(1, 9).us(2).bc([128, 8, 128])
                tt("dve", Ep[0], Ere[:, d], qr, ALU.mult)
                tt("dve", tE, nEim[:, d], qi, ALU.mult)
                tt("dve", Ep[0], Ep[0], tE, ALU.add)
                tt("dve", Ep[1], nEim[:, d], qr, ALU.mult)
                tt("dve", tE, Ere[:, d], qi, ALU.mult)
                tt("dve", Ep[1], Ep[1], tE, ALU.subtract)
                if "s5_b2" in dbg:
                    return
                for ri in range(2):
                    for gq in range(4):
                        ps = kb.psum_f()
                        for gg_ in range(4):
                            g = gq * 4 + gg_
                            gp, h = g // 2, g % 2
                            rows = slice(h * 64, (h + 1) * 64)
                            mm(ps[:, gg_ * 64:(gg_ + 1) * 64], Fm[ri][:, gp].re("p r i -> p (r i)"), identf[:, h * 64:(h + 1) * 64], True, True)
                        cp("act", W1[ri][:, gq * 4:gq * 4 + 4, :], ps[:, 0:256].re("p (g n) -> p g n", g=4))
                if "s5_b3" in dbg:
                    return
                for g in range(16):
                    gp, h = g // 2, g % 2
                    rows = slice(h * 64, (h + 1) * 64)
                    ps = kb.psf[h * 2 + (g // 2) % 2]
                    mm(ps[:, 0:128], Fm[0][rows, gp].re("p r i -> p (r i)"), Ep[0][rows, gp, :], True, False)
                    mm(ps[:, 0:128], Fm[1][rows, gp].re("p r i -> p (r i)"), Ep[1][rows, gp, :], False, True)
                    wt = w3t[g % 2]
                    tt("dve", wt, ps[:, 0:128], cmask[:, d, :], ALU.mult)
                    tt("pool", W3[:, g, :], W3[:, g, :], wt, ALU.add)
                if d == 0:
                    for g in range(16):
                        dcol = Dbc[:, g * 16:(g + 1) * 16].us(1).bc([128, 8, 16])
                        wt = w3t[g % 2]
                        tt("dve", wt.re("p (t o) -> p t o", t=8), dsel.re("p (t o) -> p t o", t=8), dcol, ALU.mult)
                        tt("pool", W3[:, g, :], W3[:, g, :], wt, ALU.add)
                if "s5_b4" in dbg:
                    return
                kb.memset("dve", tab[0][:, :, 0:1], 1.0)
                kb.memset("dve", tab[1][:, :, 0:1], 0.0)
                cp("dve", tab[0][:, :, 1:2], pv(11).us(2))
                cp("dve", tab[1][:, :, 1:2], pv(12).us(2))
                n = 2
                while n < NCH:
                    m = min(n, NCH - n)
                    tt("dve", pv(30), tab[0][:, :, n - 1], pv(11), ALU.mult)
                    tt("dve", pv(31), tab[1][:, :, n - 1], pv(12), ALU.mult)
                    tt("dve", pv(33), pv(30), pv(31), ALU.subtract)
                    tt("dve", pv(30), tab[0][:, :, n - 1], pv(12), ALU.mult)
                    tt("dve", pv(31), tab[1][:, :, n - 1], pv(11), ALU.mult)
                    tt("dve", pv(34), pv(30), pv(31), ALU.add)
                    Pr = pv(33).us(2).bc([128, 8, m])
                    Pi = pv(34).us(2).bc([128, 8, m])
                    ta = tE[:, :, 0:m]
                    tt("dve", tab[0][:, :, n:n + m], tab[0][:, :, 0:m], Pr, ALU.mult)
                    tt("dve", ta, tab[1][:, :, 0:m], Pi, ALU.mult)
                    tt("dve", tab[0][:, :, n:n + m], tab[0][:, :, n:n + m], ta, ALU.subtract)
                    tt("dve", tab[1][:, :, n:n + m], tab[0][:, :, 0:m], Pi, ALU.mult)
                    tt("dve", ta, tab[1][:, :, 0:m], Pr, ALU.mult)
                    tt("dve", tab[1][:, :, n:n + m], tab[1][:, :, n:n + m], ta, ALU.add)
                    n *= 2
                dma("sp", st_dir[d, :, 0:1024], W1[0].re("p g n -> p (g n)"))
                dma("sp", st_dir[d, :, 1024:2048], W1[1].re("p g n -> p (g n)"))
                dma("act", st_dir[d, :, 2048:2048 + 8 * NCH], tab[0].re("p g n -> p (g n)"))
                dma("act", st_dir[d, :, 2048 + 8 * NCH:2048 + 16 * NCH], tab[1].re("p g n -> p (g n)"))
                dma("sp", st_dir[d, :, 2048 + 16 * NCH:2048 + 16 * NCH + 8], pv(9))
            else:
                if d == 1:
                    load_dir(1)
                dma("sp", pv(9), st_dir[d, :, 2048 + 16 * NCH:2048 + 16 * NCH + 8])
            if "s5_stop2" in dbg:
                return
            if not cached:
                P.barrier()
            def nat(tv, sl, rev):
                v = tv[:, sl]
                return v[:, ::-1] if rev else v

            def gp_gen(gp, d=d):
                rs, rho1 = rs_sets[gp % 2], rho_sets[gp % 2]
                psv = [kb.psum_f(), kb.psum_f()]
                for ri in range(2):
                    for h in range(2):
                        g = gp * 2 + h
                        mm(psv[ri][h * 64:(h + 1) * 64, 0:NCH], W1[ri][:, g, :], U[:, g, :], True, True)
                cp("pool", rho1, pv(9)[:, gp:gp + 1].bc([128, NCH]))
                yield
                Cn, Sn = tab[0][:, gp, :], tab[1][:, gp, :]
                if d == 0:
                    segs = [(slice(0, NCH), slice(0, NCH), False)]
                else:
                    segs = [(slice(0, 32), slice(0, 32), True), (slice(32, NCH), slice(32, NCH), True)]
                for (js, ks, rev) in segs:
                    vre, vim = nat(psv[0][:, 0:NCH], ks, rev), nat(psv[1][:, 0:NCH], ks, rev)
                    tt("dve", rs[0][:, js], vre, Cn[:, js], ALU.mult)
                    tt("dve", rs[1][:, js], vim, Sn[:, js], ALU.mult)
                    tt("dve", rs[4][:, js], vim, Cn[:, js], ALU.mult)
                    tt("dve", rs[5][:, js], vre, Sn[:, js], ALU.mult)
                yield
                tt("pool", rs[2], rs[0], rs[1], ALU.add)
                tt("pool", rs[3], rs[4], rs[5], ALU.subtract)
                yield
                kb.scan(rs[4], rho1, rs[2], 0.0)
                kb.scan(rs[5], rho1, rs[3], 0.0)
                yield
                tt("dve", rs[0], rs[4], Cn, ALU.mult)
                tt("dve", rs[1], rs[5], Sn, ALU.mult)
                tt("dve", rs[2], rs[4], Sn, ALU.mult)
                tt("dve", rs[3], rs[5], Cn, ALU.mult)
                yield
                for (js, ks, rev) in segs:
                    if d == 0:
                        osl = slice(1, NCH + 1)
                    else:
                        osl = slice(0, 32) if ks.start == 0 else slice(33, NCH + 1)
                    ore = nat(SN[d][0][:, gp, :], osl, rev).par("s5sn")
                    oim = nat(SN[d][1][:, gp, :], osl, rev).par("s5sn")
                    tt("pool", ore, rs[0][:, js], rs[1][:, js], ALU.subtract)
                    tt("pool", oim, rs[2][:, js], rs[3][:, js], ALU.add)
                yield
            for gp0 in range(0, 8, 2):
                lockstep([gp_gen(gp0), gp_gen(gp0 + 1)])
            for ri in range(2):
                if d == 0:
                    kb.memset("pool", SN[0][ri][:, :, 0:1], 0.0)
                else:
                    kb.memset("pool", SN[1][ri][:, :, 32:33], 0.0)
                    cp("pool", SN[1][ri][:, :, NCH + 1:NCH + 2], SN[1][ri][:, :, 0:1])
        if not cached:
            dma("sp", st_main[0], Ere.re("p d g n -> p (d g n)"))
            dma("act", st_main[1], nEim.re("p d g n -> p (d g n)"))
            dma("sp", st_main[2], W3.re("p g n -> p (g n)"))
        if "s5_stop3" in dbg:
            return
        P.barrier()
        AR.lo = scr0
        AR2.lo = 0
        Yc = AR.alloc("Yc", [8, 256])
        gg = AR.alloc("gg", [8, 256])
        g2 = AR.alloc("g2", [8, 256])
        sg = [AR.alloc("sg%d" % i, [512]) for i in range(2)]
        ggT = AR2.alloc("ggT", [2, NPOS])
        ggb = AR2.alloc("ggb", [2, NPOS], BF16)
        banks = [kb.psf[0], kb.psf[2], kb.psf[1], kb.psf[3]]

        def r_mm(k0, M, p0):
            for g in range(16):
                gp, h = g // 2, g % 2
                rows = slice(h * 64, (h + 1) * 64)
                bank = banks[h * 2 + (gp // 4)]
                o = bank[0:M, (gp % 4) * 128:(gp % 4 + 1) * 128]
                cf = slice(k0, k0 + M)
                cb_ = slice(k0 + 1, k0 + 1 + M) if k0 == 0 else slice(k0 + 2, k0 + 2 + M)
                mm(o, SN[0][0][rows, gp, cf], Ere[rows, 0, gp, :], True, False)
                mm(o, SN[0][1][rows, gp, cf], nEim[rows, 0, gp, :], False, False)
                mm(o, SN[1][0][rows, gp, cb_], Ere[rows, 1, gp, :], False, False)
                mm(o, SN[1][1][rows, gp, cb_], nEim[rows, 1, gp, :], False, False)
                mm(o, U[:, g, k0:k0 + M], W3[:, g, :], False, True)

        def r_evac(k0, M, p0):
            for h in range(2):
                for q in range(2):
                    bank = banks[h * 2 + q]
                    src = bank[0:M, :].re("p (g t o) -> p g t o", g=4, t=8)
                    dst = Yc[0:M].re("p t (g2 h o) -> p h g2 t o", h=2, o=16)[:, h, 4 * q:4 * q + 4].par("s5yc")
                    cp("act" if h == 0 else "dve", dst, src)

        def r_rest(k0, M, p0):
            yv, gv, g2v = Yc[0:M], gg[0:M], g2[0:M]
            tt("pool", g2v, yv, yv, ALU.mult)
            ts("dve", g2v, g2v, 0.044715, ALU.mult, 1.0, ALU.add)
            tt("pool", g2v, g2v, yv, ALU.mult)
            act(g2v, g2v, AF.Sigmoid, scale=1.5957691216057308)
            tt("dve", gv, g2v, yv, ALU.mult)
            for t_ in range(8):
                pz = [kb.psf[4], kb.psf[5]]
                for c in range(2):
                    tr(pz[c][:, 0:M], gv[:, t_, c * 128:(c + 1) * 128], identf[0:M, 0:M])
                for c in range(2):
                    cp("act" if c == 0 else "dve", ggT[:, c, p0 + t_:p0 + 8 * M:8].par("s5gg"), pz[c][:, 0:M])
        rb = [blk for blk in kblocks if not (blk[0] == 0 and not with_ctx)]
        r_mm(*rb[0])
        for bi, blk in enumerate(rb):
            r_evac(*blk)
            if bi + 1 < len(rb):
                r_mm(*rb[bi + 1])
            r_rest(*blk)
        lo = 0 if with_ctx else TC
        for c in range(2):
            cp("dve", ggb[:, c, lo:NPOS], ggT[:, c, lo:NPOS])
        for c in range(2):
            pos = lo
            while pos < NPOS:
                n = min(512, NPOS - pos)
                ps = kb.psum_f()
                for k in range(2):
                    mm(ps[:, 0:n], wglu[:, k, c * 128:(c + 1) * 128], ggb[:, k, pos:pos + n], k == 0, k == 1)
                s_ = sg[(pos // 512) % 2]
                act(s_[:, 0:n], ps[:, 0:n], AF.Sigmoid)
                tt("dve", brT[1][c][:, pos:pos + n], s_[:, 0:n], ggT[:, c, pos:pos + n], ALU.mult)
                pos += n

    def phase_merge(l, b, with_ctx, nxt=None):
        AR.reset(top=True)
        lo = 0 if with_ctx else TC
        blocks = []
        pos = lo
        while pos < NPOS:
            n = min(512, NPOS - pos) if pos >= TC else TC - pos
            blocks.append((pos, n))
            pos += n
        ypT = AR.alloc("ypT", [KC, NPOS], BF16, top=True)
        wbr = AR.alloc("wbr", [4, 2, D], BF16, top=True)
        wo = AR.alloc("wo", [KC, D], BF16, top=True)
        wml = [AR.alloc("wml%d" % i, [KC, 4, 128], BF16, top=True) for i in range(2)]

        def load_wml(oc):
            for n_ in range(4):
                c0 = O_MERGE + n_ * D + oc * 128
                dma("pool", wml[oc % 2][:, :, n_, :].par("ldw"), w_in[l, :, c0:c0 + 128].re("(c p) n -> p c n", p=128))
        wg = AR.alloc("w_gate", [KC, 1024], BF16)
        for cc in range(1, 8):
            dma("pool", wg[:, :, cc * 128:(cc + 1) * 128].par("ldw"),
                w_in[l, :, O_GATE + cc * 128:O_GATE + (cc + 1) * 128].re("(c p) n -> p c n", p=128))
        for n_ in range(4):
            dma("pool", wbr[:, n_].par("ldw"), w_branch[l, n_].re("(c p) n -> p c n", p=128))
        load_wml(0)
        load_wml(1)
        dma("pool", wo, w_out[l].re("(c p) n -> p c n", p=128))
        sl = [AR.alloc("sl%d" % i, [512], BF16) for i in range(2)]
        it = 0
        for cc in range(8):
            for (p0, n) in blocks:
                ps = kb.psum_f()
                for k in range(KC):
                    wsl = wst[:, k, 0:128] if cc == 0 else wg[:, k, cc * 128:(cc + 1) * 128]
                    mm(ps[:, 0:n], wsl, hT[:, k, p0:p0 + n], k == 0, k == KC - 1)
                s_ = sl[it % 2]
                it += 1
                act(s_[:, 0:n], ps[:, 0:n], AF.Silu)
                br = brT[cc // 2][cc % 2][:, p0:p0 + n]
                tt("dve", br, br, s_[:, 0:n], ALU.mult)
            if cc == 0 and nxt is not None:
                prefetch(nxt, PF_S5)
        AR.reset()
        sg = [AR.alloc("sg%d" % i, [512]) for i in range(2)]
        tm = [AR.alloc("tm%d" % i, [512]) for i in range(2)]
        acc = [AR.alloc("acc%d" % i, [512]) for i in range(2)]
        it = 0
        ib = 0
        for oc in range(8):
            wm = wml[oc % 2]
            if 1 <= oc and oc + 1 < 8:
                load_wml(oc + 1)
            for (p0, n) in blocks:
                ac = acc[ib % 2]
                ib += 1
                for n_ in range(4):
                    psA = kb.psum_f()
                    for k in range(2):
                        mm(psA[:, 0:n], wbr[:, n_, k, oc * 128:(oc + 1) * 128], brT[n_][k][:, p0:p0 + n], k == 0, k == 1)
                    psB = kb.psum_f()
                    for k in range(KC):
                        mm(psB[:, 0:n], wm[:, k, n_, :], hT[:, k, p0:p0 + n], k == 0, k == KC - 1)
                    s_ = sg[it % 2]
                    t_ = tm[it % 2]
                    it += 1
                    act(s_[:, 0:n], psB[:, 0:n], AF.Sigmoid)
                    if n_ == 0:
                        tt("dve", ac[:, 0:n], psA[:, 0:n], s_[:, 0:n], ALU.mult)
                    elif n_ < 3:
                        tt("dve", t_[:, 0:n], psA[:, 0:n], s_[:, 0:n], ALU.mult)
                        tt("dve", ac[:, 0:n], ac[:, 0:n], t_[:, 0:n], ALU.add)
                    else:
                        tt("dve", t_[:, 0:n], psA[:, 0:n], s_[:, 0:n], ALU.mult)
                        tt("dve", ypT[:, oc, p0:p0 + n].par("ypt"), ac[:, 0:n], t_[:, 0:n], ALU.add)
        AR.reset()
        gbc = AR.alloc("gbc", [2, D])
        dma("sp", gbc[:, 0, :], modD[l, b, 2 * D:3 * D].pbc(128))
        if with_ctx:
            dma("sp", gbc[:, 1, :], modD[l, 2, 2 * D:3 * D].pbc(128))
        xt = [AR.alloc("xt%d" % i, [D]) for i in range(2)]
        yt = [AR.alloc("yt%d" % i, [D]) for i in range(2)]
        for i in range(0 if with_ctx else 2, 18):
            x_t, y_t = xt[i % 2], yt[i % 2]
            dma("sp", x_t, src_tile(l, b, i))
            for hf in range(2):
                ps = kb.psum_f()
                for k in range(KC):
                    mm(ps, ypT[:, k, i * 128:(i + 1) * 128], wo[:, k, hf * 512:(hf + 1) * 512], k == 0, k == KC - 1)
                tt("dve", y_t[:, hf * 512:(hf + 1) * 512], ps, gbc[:, 1 if i < 2 else 0, hf * 512:(hf + 1) * 512], ALU.mult)
            tt("pool", y_t, y_t, x_t, ALU.add)
            if i < 2:
                dst = (tap("xc1", [nb, TC, D]) if "x1out" in dbg else xcmid)[b, i * 128:(i + 1) * 128, :]
            else:
                dst = (xmid if (l < DEPTH - 1 and "x1out" not in dbg) else y_out)[b, (i - 2) * 128:(i - 1) * 128, :]
            dma("act", dst, y_t)

    def dump_br(name, n, lo):
        o = tap(name, [2, 128, NPOS])
        tmpf = AR.alloc("dump_" + name, [NPOS])
        for c in range(2):
            cp("dve", tmpf[:, lo:NPOS], brT[n][c][:, lo:NPOS])
            dma("sp", o[c, :, lo:NPOS], tmpf[:, lo:NPOS])

    prefetch(layers[0], PF_S5)
    for l in layers:
        with_ctx = l < DEPTH - 1
        prep_layer(l)
        for b in range(nb):
            if "prep_only" in dbg:
                continue
            phase_norm(l, b)
            if "hT" in dbg and b == 0 and l == layers[0]:
                o = tap("hT", [KC, 128, NPOS])
                AR.reset()
                tmpf = AR.alloc("dump_hT", [NPOS])
                for c in range(KC):
                    cp("dve", tmpf, hT[:, c, :])
                    dma("sp", o[c], tmpf)
            only = dbg & {"only_lru", "only_pool", "only_mla", "only_s5", "only_norm"}
            if not only or "only_s5" in only:
                phase_s5(l, b, with_ctx)
            if not only or "only_lru" in only:
                phase_lru(l, b, with_ctx)
            if not only or "only_pool" in only:
                phase_pool(l, b, with_ctx)
            if not only or "only_mla" in only:
                phase_mla(l, b, with_ctx)
            if "br" in dbg and b == 0 and l == layers[0]:
                AR.reset()
                lo = 0 if with_ctx else TC
                for n_, nm in enumerate(("mla", "s5", "lru", "pool")):
                    dump_br(nm, n_, lo)
            if "nomerge" not in dbg:
                if b + 1 < nb:
                    nxt = l
                else:
                    li = list(layers).index(l)
                    nxt = layers[li + 1] if li + 1 < len(layers) else None
                phase_merge(l, b, with_ctx, nxt)
    P.barrier()
    P.emit()
    kb.st.close()
    return nc, kb


def _consts():
    ident = np.eye(128, dtype=np.float32)
    rows_n = T // 64
    row = np.repeat(np.arange(rows_n), 64).astype(np.float32)
    col = np.tile(np.arange(64), rows_n).astype(np.float32)
    nf = 8
    inv = (np.float32(10000.0) ** (-np.arange(nf, dtype=np.float32) / nf)).astype(np.float32)
    ar = (row[:, None] * inv).astype(np.float32)
    ac = (col[:, None] * inv).astype(np.float32)
    cr, sr, cc, sc = np.cos(ar), np.sin(ar), np.cos(ac), np.sin(ac)
    ropeC = np.concatenate([cr, cr, cc, cc], axis=1).astype(np.float32)
    ropeS = np.concatenate([-sr, sr, -sc, sc], axis=1).astype(np.float32)
    cpool = np.ones((128, 34), np.float32)
    for c in range(2):
        for p in range(128):
            w = (2, 4, 8, 16)[2 * c + p // 64]
            hw = w // 2
            cpool[p, c] = 1.0 / w
            for t in range(8):
                cnt = t + hw if t < hw else w
                cpool[p, 2 + c * 8 + t] = w / cnt
            for j in range(8):
                dist = 8 - j
                cnt = dist + hw if dist < hw else w
                cpool[p, 18 + c * 8 + j] = w / cnt
    m = np.zeros((128, 3, 128), np.float32)
    for r in range(8):
        for t in range(8):
            if r <= t:
                m[r * 16:(r + 1) * 16, 0, t * 16:(t + 1) * 16] = 1.0
            if r >= t:
                m[r * 16:(r + 1) * 16, 1, t * 16:(t + 1) * 16] = 1.0
            if r == t:
                m[r * 16:(r + 1) * 16, 2, t * 16:(t + 1) * 16] = np.eye(16, dtype=np.float32)
    return {"c_ident": ident, "c_ropeC": ropeC, "c_ropeS": ropeS, "c_pool": cpool, "c_mask": m}


_WNAMES = ["w_ada", "b_ada", "norm_g", "w_in", "mla_q_norm", "mla_kv_norm", "mla_w_uq", "mla_w_ukv", "mla_q_gain",
           "mla_k_gain", "s5_a_re", "s5_a_im", "s5_log_dt", "s5_b_re", "s5_b_im", "s5_c_re", "s5_c_im", "s5_d",
           "s5_w_glu", "lru_conv_w", "lru_conv_b", "lru_lambda", "lru_w_a", "lru_b_a", "lru_w_x", "lru_b_x",
           "pool_w", "pool_b", "pool_scale", "w_branch", "w_out"]


def make_in_maps(inputs, n_cores, nb):
    consts = _consts()
    maps = []
    for r in range(n_cores):
        bs = slice(r * nb, (r + 1) * nb)
        m = {"x": np.ascontiguousarray(inputs["x"][bs], dtype=np.float32),
             "ctx": np.ascontiguousarray(inputs["ctx"][bs], dtype=np.float32)}
        cv = np.zeros((3, D), np.float32)
        cv[0:nb] = np.asarray(inputs["c"], dtype=np.float32)[bs]
        cv[2] = np.asarray(inputs["c_ctx"], dtype=np.float32)
        m["cvec"] = cv
        for k in _WNAMES:
            m[k] = np.ascontiguousarray(inputs[k], dtype=np.float32)
        m.update(consts)
        maps.append(m)
    return maps


_CACHE = {}


def kernel(**inputs):
    n_cores, nb = 8, 2
    if "nc" not in _CACHE:
        _CACHE["nc"] = build_program(nb=nb)[0]
    nc = _CACHE["nc"]
    maps = make_in_maps(inputs, n_cores, nb)
    res = run_bass_kernel_spmd(nc, maps, core_ids=list(range(n_cores)))
    out = np.concatenate([np.asarray(r["y"], dtype=np.float32) for r in res.results], axis=0)
    return out
```

```python
import math
import contextlib
import numpy as np
import concourse.bass as bass
import concourse.mybir as mybir
from concourse.bass_utils import run_bass_kernel_spmd

F32 = mybir.dt.float32
BF16 = mybir.dt.bfloat16
I32 = mybir.dt.int32
AF = mybir.ActivationFunctionType
ALU = mybir.AluOpType
AX = mybir.AxisListType

ENGS = ("pe", "act", "dve", "pool", "sp")
NDMA = 24

D = 1024
KC = 8
T = 2048
TC = 256
NPOS = T + TC
DEPTH = 2
O_KROPE, O_S5, O_LRU, O_CQ, O_POOL, O_GATE, O_MERGE, IN_W = 128, 160, 416, 672, 928, 1184, 2208, 6304
EPS = 1e-6
NCH = NPOS // 8


PAR_OFF = set()


class Buf:
    __slots__ = ("name", "w", "wp", "r")

    def __init__(self, name):
        self.name = name
        self.w = []
        self.wp = []
        self.r = []


class TV:
    par_ = False

    def __init__(self, ap, buf):
        self.ap, self.buf = ap, buf

    def par(self, tag=""):
        t = TV(self.ap, self.buf)
        t.par_ = tag not in PAR_OFF
        return t

    def __getitem__(self, k):
        return TV(self.ap[k], self.buf)

    def re(self, s, **kw):
        return TV(self.ap.rearrange(s, **kw), self.buf)

    def bc(self, shape):
        return TV(self.ap.to_broadcast(list(shape)), self.buf)

    def us(self, ax):
        return TV(self.ap.unsqueeze(ax), self.buf)

    def pbc(self, n):
        return TV(self.ap.partition_broadcast(n), self.buf)

    @property
    def shape(self):
        return tuple(self.ap.shape)


def _bufs(*xs):
    out = []
    for x in xs:
        if isinstance(x, TV):
            out.append(x.buf)
        elif isinstance(x, (list, tuple)):
            out.extend(_bufs(*x))
    return out


def _ap(x):
    return x.ap if isinstance(x, TV) else x


class Prog:
    def __init__(self, nc):
        self.nc = nc
        self.ops = {e: [] for e in ENGS}
        self.cnt = {e: 0 for e in ENGS}
        self.waited = {e: {} for e in ENGS}
        self.dma_i = 0
        self.dma_last = [0] * NDMA

    def _deps(self, eng, reads, writes, par=False):
        toks = []
        for b in reads:
            toks.extend(b.w)
            toks.extend(b.wp)
        for b in writes:
            toks.extend(b.w)
            if not par:
                toks.extend(b.wp)
            toks.extend(b.r)
        need = {}
        for (k, v, e) in toks:
            if e == "pe" and eng == "pe":
                continue
            if self.waited[eng].get(k, 0) >= v:
                continue
            if need.get(k, 0) < v:
                need[k] = v
        for k, v in need.items():
            self.waited[eng][k] = v
        return list(need.items())

    def _mark(self, tok, reads, writes, par=False):
        for b in writes:
            if par:
                best = {}
                for (k, v, e) in b.wp + [tok]:
                    if k not in best or best[k][1] < v:
                        best[k] = (k, v, e)
                b.wp = list(best.values())
            else:
                b.w = [tok]
                b.wp = []
                b.r = []
        for b in reads:
            if b in writes:
                continue
            b.r.append(tok)
            if len(b.r) > 16:
                best = {}
                for (k, v, e) in b.r:
                    if k not in best or best[k][1] < v:
                        best[k] = (k, v, e)
                b.r = list(best.values())

    def op(self, eng, fn, reads=(), writes=(), par=False):
        reads = list(dict.fromkeys(reads))
        writes = list(dict.fromkeys(writes))
        waits = self._deps(eng, reads, writes, par)
        self.cnt[eng] += 1
        tok = (eng, self.cnt[eng], eng)
        self.ops[eng].append((waits, fn, (eng, 1)))
        self._mark(tok, reads, writes, par)

    def dma(self, eng, fn, reads=(), writes=(), par=False):
        reads = list(dict.fromkeys(reads))
        writes = list(dict.fromkeys(writes))
        waits = self._deps(eng, reads, writes, par)
        slot = self.dma_i % NDMA
        self.dma_i += 1
        key = ("dma", slot)
        prev = self.dma_last[slot]
        if prev and self.waited[eng].get(key, 0) < prev:
            waits.append((key, prev))
            self.waited[eng][key] = prev
        val = prev + 16
        self.dma_last[slot] = val
        tok = (key, val, "dma")
        self.ops[eng].append((waits, fn, (key, 16)))
        self._mark(tok, reads, writes, par)

    def barrier(self):
        for E in ENGS:
            waits = []
            for e in ENGS:
                v = self.cnt[e]
                if v and self.waited[E].get(e, 0) < v:
                    waits.append((e, v))
                    self.waited[E][e] = v
            for slot in range(NDMA):
                v = self.dma_last[slot]
                key = ("dma", slot)
                if v and self.waited[E].get(key, 0) < v:
                    waits.append((key, v))
                    self.waited[E][key] = v
            if waits:
                self.ops[E].append((waits, None, None))

    def emit(self):
        nc = self.nc
        sems = {}
        with contextlib.ExitStack() as st:
            for e in ENGS:
                sems[e] = st.enter_context(nc.semaphore("s_" + e))
            for i in range(NDMA):
                sems[("dma", i)] = st.enter_context(nc.semaphore("s_dma%d" % i))
            block = st.enter_context(nc.Block())

            def run(engname):
                def body(eng):
                    for waits, fn, inc in self.ops[engname]:
                        for k, v in waits:
                            eng.wait_ge(sems[k], v)
                        if fn is None:
                            continue
                        ins = fn(eng)
                        ins.then_inc(sems[inc[0]], inc[1])
                return body
            block.tensor(run("pe"))
            block.scalar(run("act"))
            block.vector(run("dve"))
            block.gpsimd(run("pool"))
            block.sync(run("sp"))


class KB:
    def __init__(self, nc, nb, layers, dbg):
        self.nc = nc
        self.P = Prog(nc)
        self.nb = nb
        self.layers = layers
        self.dbg = dbg
        self.st = contextlib.ExitStack()
        self.din = {}
        self.dout = {}
        self.ps_i = 0
        self.psb_i = 0
        self.ps8_i = 0

    def tt(self, eng, out, a, b, op):
        self.P.op(eng, lambda e: e.tensor_tensor(out=out.ap, in0=a.ap, in1=b.ap, op=op), _bufs(a, b), _bufs(out), par=out.par_)

    def ts(self, eng, out, a, s1, op0, s2=None, op1=None):
        if op1 is None:
            self.P.op(eng, lambda e: e.tensor_scalar(out=out.ap, in0=a.ap, scalar1=_ap(s1), scalar2=None, op0=op0),
                      _bufs(a, s1), _bufs(out), par=out.par_)
        else:
            self.P.op(eng, lambda e: e.tensor_scalar(out=out.ap, in0=a.ap, scalar1=_ap(s1), scalar2=_ap(s2), op0=op0, op1=op1),
                      _bufs(a, s1, s2), _bufs(out), par=out.par_)

    def stt(self, out, a, s, b, op0, op1):
        self.P.op("dve", lambda e: e.scalar_tensor_tensor(out=out.ap, in0=a.ap, scalar=_ap(s), in1=b.ap, op0=op0, op1=op1),
                  _bufs(a, s, b), _bufs(out))

    def act(self, out, a, func, bias=0.0, scale=1.0, accum=None):
        if accum is None:
            self.P.op("act", lambda e: e.activation(out=out.ap, in_=a.ap, func=func, bias=_ap(bias), scale=_ap(scale)),
                      _bufs(a, bias, scale), _bufs(out), par=out.par_)
        else:
            self.P.op("act", lambda e: e.activation(out=out.ap, in_=a.ap, func=func, bias=_ap(bias), scale=_ap(scale), accum_out=accum.ap),
                      _bufs(a, bias, scale), _bufs(out, accum))

    def cp(self, eng, out, a):
        if eng == "act":
            self.P.op("act", lambda e: e.copy(out=out.ap, in_=a.ap), _bufs(a), _bufs(out), par=out.par_)
        else:
            self.P.op(eng, lambda e: e.tensor_copy(out=out.ap, in_=a.ap), _bufs(a), _bufs(out), par=out.par_)

    def memset(self, eng, out, v):
        self.P.op(eng, lambda e: e.memset(out.ap, v), [], _bufs(out), par=out.par_)

    def recip(self, out, a):
        self.P.op("dve", lambda e: e.reciprocal(out=out.ap, in_=a.ap), _bufs(a), _bufs(out))

    def mm(self, out, lhsT, rhs, start, stop):
        self.P.op("pe", lambda e: e.matmul(out.ap, lhsT=lhsT.ap, rhs=rhs.ap, start=start, stop=stop), _bufs(lhsT, rhs), _bufs(out))

    def tr(self, out, a, ident):
        self.P.op("pe", lambda e: e.transpose(out.ap, a.ap, ident.ap), _bufs(a, ident), _bufs(out))

    def scan(self, out, d0, d1, init):
        self.P.op("dve", lambda e: e.tensor_tensor_scan(out=out.ap, data0=d0.ap, data1=d1.ap, initial=_ap(init), op0=ALU.mult, op1=ALU.add),
                  _bufs(d0, d1, init), _bufs(out))

    def reduce(self, out, a, op=ALU.add):
        self.P.op("dve", lambda e: e.tensor_reduce(out=out.ap, in_=a.ap, axis=AX.X, op=op), _bufs(a), _bufs(out))

    def dma(self, q, out, a, slow=False):
        if slow:
            self.P.dma(q, lambda e: e.dma_start(out=out.ap, in_=a.ap, allow_slow_non_contiguous=True), _bufs(a), _bufs(out), par=out.par_)
        else:
            self.P.dma(q, lambda e: e.dma_start(out=out.ap, in_=a.ap), _bufs(a), _bufs(out), par=out.par_)

    def dram_in(self, name, shape, dt=F32):
        t = TV(self.nc.dram_tensor(name, list(shape), dt, kind="ExternalInput").ap(), Buf(name))
        self.din[name] = t
        return t

    def dram_out(self, name, shape, dt=F32):
        t = TV(self.nc.dram_tensor(name, list(shape), dt, kind="ExternalOutput").ap(), Buf(name))
        self.dout[name] = t
        return t

    def dram_tmp(self, name, shape, dt=F32):
        return TV(self.nc.dram_tensor(name, list(shape), dt, kind="Internal").ap(), Buf(name))

    def sb(self, name, shape, dt=F32):
        t = self.st.enter_context(self.nc.sbuf_tensor(name, list(shape), dt))
        return TV(t[:], Buf(name))

    def psum_f(self):
        i = self.ps_i % 6
        self.ps_i += 1
        return self.psf[i]

    def psum_b(self):
        i = self.psb_i % 2
        self.psb_i += 1
        return self.psb[i]

    def psum_any(self, bf16=False):
        i = self.ps8_i % 8
        self.ps8_i += 1
        return self.ps8b[i] if bf16 else self.ps8[i]


class Arena:
    def __init__(self, kb, words, ap=None):
        self.kb = kb
        self.words = words
        if ap is None:
            self.f = kb.st.enter_context(kb.nc.sbuf_tensor("arena_f", [128, words], F32))
        else:
            self.f = ap
        self.lo = 0
        self.hi = words

    def alloc(self, name, free, dt=F32, top=False, buf=None, at=None):
        nel = int(np.prod(free))
        w = nel if dt == F32 else (nel + 1) // 2
        w = ((w + 15) // 16) * 16
        if at is not None:
            off = at
        elif top:
            self.hi -= w
            off = self.hi
        else:
            off = self.lo
            self.lo += w
        assert self.lo <= self.hi, (name, self.lo, self.hi)
        assert off + w <= self.words
        if dt != F32:
            ap = self.f[:, off:off + w].bitcast(dt)[:, 0:nel]
        else:
            ap = self.f[:, off:off + nel]
        if len(free) > 1:
            names = " ".join("d%d" % i for i in range(len(free)))
            kw = {"d%d" % i: int(free[i]) for i in range(len(free))}
            ap = ap.rearrange("p (%s) -> p %s" % (names, names), **kw)
        tv = TV(ap, buf if buf is not None else Buf(name))
        tv.off = off
        tv.w = w
        return tv

    def reset(self, top=False):
        self.kb.P.barrier()
        self.lo = 0
        if top:
            self.hi = self.words


def lockstep(gens):
    gens = list(gens)
    while gens:
        nxt = []
        for g in gens:
            try:
                next(g)
                nxt.append(g)
            except StopIteration:
                pass
        gens = nxt


def build_program(nb=2, layers=(0, 1), dbg=None):
    nc = bass.Bass("TRN2", target_bir_lowering=False)
    kb = KB(nc, nb, layers, dbg)
    P = kb.P
    tt, ts, stt, act, cp, mm, tr, dma = kb.tt, kb.ts, kb.stt, kb.act, kb.cp, kb.mm, kb.tr, kb.dma
    dbg = dbg or set()
    PAR_OFF.clear()
    PAR_OFF.update(x[6:] for x in dbg if x.startswith("nopar_"))

    x_in = kb.dram_in("x", [nb, T, D])
    ctx_in = kb.dram_in("ctx", [nb, TC, D])
    cvec = kb.dram_in("cvec", [3, D])
    w_ada = kb.dram_in("w_ada", [DEPTH, D, 3 * D])
    b_ada = kb.dram_in("b_ada", [DEPTH, 3 * D])
    norm_g = kb.dram_in("norm_g", [DEPTH, D])
    w_in = kb.dram_in("w_in", [DEPTH, D, IN_W])
    mla_q_norm = kb.dram_in("mla_q_norm", [DEPTH, 256])
    mla_kv_norm = kb.dram_in("mla_kv_norm", [DEPTH, 128])
    mla_w_uq = kb.dram_in("mla_w_uq", [DEPTH, 256, 384])
    mla_w_ukv = kb.dram_in("mla_w_ukv", [DEPTH, 128, 512])
    mla_q_gain = kb.dram_in("mla_q_gain", [DEPTH, 96])
    mla_k_gain = kb.dram_in("mla_k_gain", [DEPTH, 96])
    s5_a_re = kb.dram_in("s5_a_re", [DEPTH, 2, 16, 64])
    s5_a_im = kb.dram_in("s5_a_im", [DEPTH, 2, 16, 64])
    s5_log_dt = kb.dram_in("s5_log_dt", [DEPTH, 2, 16])
    s5_b_re = kb.dram_in("s5_b_re", [DEPTH, 2, 16, 64, 16])
    s5_b_im = kb.dram_in("s5_b_im", [DEPTH, 2, 16, 64, 16])
    s5_c_re = kb.dram_in("s5_c_re", [DEPTH, 2, 16, 16, 64])
    s5_c_im = kb.dram_in("s5_c_im", [DEPTH, 2, 16, 16, 64])
    s5_d = kb.dram_in("s5_d", [DEPTH, 256])
    s5_w_glu = kb.dram_in("s5_w_glu", [DEPTH, 256, 256])
    lru_conv_w = kb.dram_in("lru_conv_w", [DEPTH, 4, 256])
    lru_conv_b = kb.dram_in("lru_conv_b", [DEPTH, 256])
    lru_lambda = kb.dram_in("lru_lambda", [DEPTH, 2, 256])
    lru_w_a = kb.dram_in("lru_w_a", [DEPTH, 2, 4, 64, 64])
    lru_b_a = kb.dram_in("lru_b_a", [DEPTH, 2, 256])
    lru_w_x = kb.dram_in("lru_w_x", [DEPTH, 2, 4, 64, 64])
    lru_b_x = kb.dram_in("lru_b_x", [DEPTH, 2, 256])
    pool_w = kb.dram_in("pool_w", [DEPTH, 4, 64, 64])
    pool_b = kb.dram_in("pool_b", [DEPTH, 256])
    pool_scale = kb.dram_in("pool_scale", [DEPTH, 256])
    w_branch = kb.dram_in("w_branch", [DEPTH, 4, 256, D])
    w_out = kb.dram_in("w_out", [DEPTH, D, D])
    c_ident = kb.dram_in("c_ident", [128, 128])
    c_ropeC = kb.dram_in("c_ropeC", [T, 32])
    c_ropeS = kb.dram_in("c_ropeS", [T, 32])
    c_pool = kb.dram_in("c_pool", [128, 2 + 16 + 16])
    c_mask = kb.dram_in("c_mask", [128, 3, 128])
    y_out = kb.dram_out("y", [nb, T, D])
    xmid = kb.dram_tmp("xmid", [nb, T, D])
    xcmid = kb.dram_tmp("xcmid", [nb, TC, D])
    modD = kb.dram_tmp("modD", [DEPTH, 3, 3 * D])
    st_main = kb.dram_tmp("st_main", [3, 128, 2048])
    st_dir = kb.dram_tmp("st_dir", [2, 128, 2048 + 16 * NCH + 8])
    dbg_out = {}

    def tap(name, shape):
        if name not in dbg_out:
            dbg_out[name] = kb.dram_out("dbg_" + name, shape)
        return dbg_out[name]

    kb.ps8 = []
    kb.pspair = []
    for i in range(4):
        t = kb.st.enter_context(nc.psum_tensor("psp%d" % i, [128, 1024], F32))
        kb.pspair.append(t[:])
        for hh in range(2):
            kb.ps8.append(TV(t[:][:, hh * 512:(hh + 1) * 512], Buf("psf%d" % (2 * i + hh))))
    kb.psf = kb.ps8[0:6]
    kb.psb = [TV(kb.ps8[i].ap.bitcast(BF16), kb.ps8[i].buf) for i in (6, 7)]
    kb.ps8b = [TV(kb.ps8[i].ap.bitcast(BF16), kb.ps8[i].buf) for i in range(8)]

    hT = kb.sb("hT", [128, KC, NPOS], BF16)
    bigbr = kb.sb("bigbr", [128, 8 * NPOS], BF16)
    _slot = {0: 0, 2: 2, 3: 4, 1: 6}
    brT = [[TV(bigbr.ap[:, (_slot[n] + c) * NPOS:(_slot[n] + c + 1) * NPOS], Buf("brT%d%d" % (n, c))) for c in range(2)]
           for n in range(4)]
    identf = kb.sb("identf", [128, 128])
    identb = kb.sb("identb", [128, 128], BF16)
    ropeC = kb.sb("ropeC", [128, 16, 32])
    ropeS = kb.sb("ropeS", [128, 16, 32])
    cpool = kb.sb("cpool", [128, 34])
    cmask = kb.sb("cmask", [128, 3, 128])
    modA = kb.sb("modA", [128, 3, KC])
    modS = kb.sb("modS", [128, 3, KC])
    lp = kb.sb("lp", [128, 64])
    lruBD = kb.sb("lruBD", [128, 2, 2, 2, 128])
    poolBD = kb.sb("poolBD", [128, 2, 128])
    wukv = kb.sb("wukv", [128, 512], BF16)
    wuq = kb.sb("wuq", [128, 2, 384], BF16)
    gains = kb.sb("gains", [128, 2, 96])
    wglu = kb.sb("wglu", [128, 2, 256], BF16)
    AR = Arena(kb, 28000)
    wst = kb.sb("wst", [128, KC, 416], BF16)

    def prefetch(l, parts):
        for (d0, c0, n) in parts:
            dma("pool", wst[:, :, d0:d0 + n].par("wst"), w_in[l, :, c0:c0 + n].re("(c p) n -> p c n", p=128))
    PF_S5 = [(0, O_S5, 256)]
    PF_LRU = [(0, O_LRU, 256)]
    PF_POOL = [(0, O_POOL, 256)]
    PF_MLA = [(0, 0, 160), (160, O_CQ, 256)]
    PF_GATE0 = [(0, O_GATE, 128)]
    AR2 = Arena(kb, 3 * NPOS, ap=bigbr.ap[:, 0:6 * NPOS].bitcast(F32))

    dma("sp", identf, c_ident)
    cp("dve", identb, identf)
    dma("sp", ropeC, c_ropeC.re("(i p) f -> p i f", p=128))
    dma("sp", ropeS, c_ropeS.re("(i p) f -> p i f", p=128))
    dma("sp", cpool, c_pool)
    dma("sp", cmask, c_mask)

    LP_CW = 0
    LP_CB = 8
    LP_NSP = 10
    LP_NSP2 = 14
    LP_BA = 18
    LP_BX = 22
    LP_PSC = 26
    LP_PBS = 28
    LP_G = 30
    LP_TMP = 40

    def prep_layer(l):
        AR.reset(top=True)
        cT = AR.alloc("cT", [KC, 3])
        for v in range(3):
            dma("sp", cT[:, :, v], cvec[v].re("(c p) -> p c", p=128), slow=True)
        cact = AR.alloc("cact", [KC, 3])
        act(cact, cT, AF.Silu)
        brow = AR.alloc("brow", [3 * D])
        dma("sp", brow[0:3, :], b_ada[l:l + 1, :].re("o n -> (o n)").pbc(3))
        modrow = AR.alloc("modrow", [3 * D])
        wa = [AR.alloc("wa%d" % i, [KC, 512]) for i in range(4)]
        for cb in range(4):
            for hh in range(2):
                dma("sp" if hh == 0 else "act", wa[cb][:, hh * 4:(hh + 1) * 4, :].par("wa"),
                    w_ada[l, hh * 512:(hh + 1) * 512, cb * 512:(cb + 1) * 512].re("(c p) n -> p c n", p=128))
        for cb in range(6):
            w = wa[cb % 4]
            if cb >= 4:
                for hh in range(2):
                    dma("sp" if hh == 0 else "act", w[:, hh * 4:(hh + 1) * 4, :].par("wa"),
                        w_ada[l, hh * 512:(hh + 1) * 512, cb * 512:(cb + 1) * 512].re("(c p) n -> p c n", p=128))
            ps = kb.psum_f()
            for k in range(KC):
                mm(ps[0:3, :], cact[:, k, :], w[:, k, :], k == 0, k == KC - 1)
            tt("dve", modrow[0:3, cb * 512:(cb + 1) * 512], ps[0:3, :], brow[0:3, cb * 512:(cb + 1) * 512], ALU.add)
        dma("sp", modD[l], modrow[0:3, :])
        sc = AR.alloc("sc", [3, KC])
        for v in range(3):
            dma("sp", modS[:, v, :], modD[l, v, 0:D].re("(c p) -> p c", p=128), slow=True)
            dma("sp", sc[:, v, :], modD[l, v, D:2 * D].re("(c p) -> p c", p=128), slow=True)
        dma("sp", lp[:, LP_G:LP_G + 8], norm_g[l].re("(c p) -> p c", p=128), slow=True)
        for v in range(3):
            stt(modA[:, v, :], sc[:, v, :], 1.0, lp[:, LP_G:LP_G + 8], ALU.add, ALU.mult)
        for k in range(4):
            dma("sp", lp[:, LP_CW:LP_CW + 8].re("p (c k) -> p c k", c=2)[:, :, k], lru_conv_w[l, k].re("(c p) -> p c", p=128), slow=True)
        dma("sp", lp[:, LP_CB:LP_CB + 2], lru_conv_b[l].re("(c p) -> p c", p=128), slow=True)
        lam = lp[:, LP_TMP:LP_TMP + 4]
        for d in range(2):
            dma("sp", lam[:, d * 2:d * 2 + 2], lru_lambda[l, d].re("(c p) -> p c", p=128), slow=True)
            dma("sp", lp[:, LP_BA + d * 2:LP_BA + d * 2 + 2], lru_b_a[l, d].re("(c p) -> p c", p=128), slow=True)
            dma("sp", lp[:, LP_BX + d * 2:LP_BX + d * 2 + 2], lru_b_x[l, d].re("(c p) -> p c", p=128), slow=True)
        t0 = lp[:, LP_TMP + 4:LP_TMP + 8]
        t1 = lp[:, LP_TMP + 8:LP_TMP + 12]
        t2 = lp[:, LP_TMP + 12:LP_TMP + 16]
        t3 = lp[:, LP_TMP + 16:LP_TMP + 20]
        ts("dve", t0, lam, -1.0, ALU.mult)
        tt("dve", t0, t0, lam, ALU.max)
        act(t1, t0, AF.Exp, scale=-1.0)
        ts("dve", t2, t1, 2.0, ALU.add)
        kb.recip(t2, t2)
        tt("dve", t2, t2, t1, ALU.mult)
        tt("dve", t3, t2, t2, ALU.mult)
        ts("dve", t0, t3, 1.0 / 11.0, ALU.mult, 1.0 / 9.0, ALU.add)
        for cf in (1.0 / 7.0, 1.0 / 5.0, 1.0 / 3.0, 1.0):
            tt("dve", t0, t0, t3, ALU.mult)
            ts("dve", t0, t0, cf, ALU.add)
        tt("dve", t0, t0, t2, ALU.mult)
        ts("dve", t1, lam, -1.0, ALU.mult, 0.0, ALU.max)
        stt(t0, t0, 2.0, t1, ALU.mult, ALU.add)
        ts("dve", lp[:, LP_NSP:LP_NSP + 4], t0, -8.0, ALU.mult)
        ts("dve", lp[:, LP_NSP2:LP_NSP2 + 4], t0, -16.0, ALU.mult)
        kb.memset("dve", lruBD, 0.0)
        for d in range(2):
            for gi, wsrc in enumerate((lru_w_a, lru_w_x)):
                for c in range(2):
                    for h in range(2):
                        dma("sp" if (c + h) % 2 == 0 else "act", lruBD[h * 64:(h + 1) * 64, d, gi, c, h * 64:(h + 1) * 64].par("ldw"), wsrc[l, d, 2 * c + h])
        kb.memset("dve", poolBD, 0.0)
        for c in range(2):
            for h in range(2):
                dma("sp", poolBD[h * 64:(h + 1) * 64, c, h * 64:(h + 1) * 64].par("ldw"), pool_w[l, 2 * c + h])
        dma("sp", lp[:, LP_PSC:LP_PSC + 2], pool_scale[l].re("(c p) -> p c", p=128), slow=True)
        dma("sp", lp[:, LP_PBS:LP_PBS + 2], pool_b[l].re("(c p) -> p c", p=128), slow=True)
        tt("dve", lp[:, LP_PBS:LP_PBS + 2], lp[:, LP_PBS:LP_PBS + 2], lp[:, LP_PSC:LP_PSC + 2], ALU.mult)
        kvn = lp[:, LP_TMP + 20:LP_TMP + 21]
        qn = lp[:, LP_TMP + 21:LP_TMP + 23]
        dma("sp", kvn, mla_kv_norm[l].re("(p o) -> p o", o=1), slow=True)
        dma("sp", qn, mla_q_norm[l].re("(c p) -> p c", p=128), slow=True)
        wtmp = AR.alloc("wtmp", [2, 512])
        dma("sp", wtmp[:, 0, :], mla_w_ukv[l])
        ts("dve", wukv, wtmp[:, 0, :], kvn, ALU.mult)
        wtmp2 = AR.alloc("wtmp2", [2, 384])
        dma("sp", wtmp2, mla_w_uq[l].re("(c p) n -> p c n", p=128))
        for c in range(2):
            ts("dve", wuq[:, c, :], wtmp2[:, c, :], qn[:, c:c + 1], ALU.mult)
        dma("sp", gains[:, 0, :], mla_q_gain[l:l + 1, :].re("o n -> (o n)").pbc(128))
        dma("sp", gains[:, 1, :], mla_k_gain[l:l + 1, :].re("o n -> (o n)").pbc(128))
        ts("dve", gains[:, 0, :], gains[:, 0, :], 96.0 ** -0.5, ALU.mult)
        dma("pool", wglu, s5_w_glu[l].re("(c p) n -> p c n", p=128))

    def src_tile(l, b, i):
        if i < 2:
            return (ctx_in if l == 0 else xcmid)[b, i * 128:(i + 1) * 128, :]
        return (x_in if l == 0 else xmid)[b, (i - 2) * 128:(i - 1) * 128, :]

    def phase_norm(l, b):
        AR.reset(top=True)
        NX = 4
        xt = [AR.alloc("xt%d" % i, [D]) for i in range(NX)]
        junk = [AR.alloc("junk%d" % i, [D]) for i in range(NX)]
        xn = [AR.alloc("xn%d" % i, [D], BF16) for i in range(NX)]
        st4 = [AR.alloc("st%d" % i, [4]) for i in range(NX)]
        def tile_gen(i):
            v = 2 if i < 2 else b
            x_t, xn_t, s4 = xt[i % NX], xn[i % NX], st4[i % NX]
            dma("sp" if i % 2 == 0 else "act", x_t, src_tile(l, b, i))
            yield
            act(junk[i % NX], x_t, AF.Square, accum=s4[:, 0:1])
            yield
            ts("dve", s4[:, 1:2], s4[:, 0:1], 1.0 / D, ALU.mult, EPS, ALU.add)
            yield
            act(s4[:, 2:3], s4[:, 1:2], AF.Sqrt)
            yield
            kb.recip(s4[:, 3:4], s4[:, 2:3])
            ts("dve", xn_t, x_t, s4[:, 3:4], ALU.mult)
            yield
            psA = kb.psum_any(bf16=True)
            psB = kb.psum_any(bf16=True)
            for c in range(KC):
                pz = psA if c % 2 == 0 else psB
                tr(pz[:, (c // 2) * 128:(c // 2 + 1) * 128], xn_t[:, c * 128:(c + 1) * 128], identb)
            yield
            for c in range(KC):
                o = hT[:, c, i * 128:(i + 1) * 128].par("norm")
                if c % 2 == 0:
                    act(o, psA[:, (c // 2) * 128:(c // 2 + 1) * 128], AF.Identity, bias=modS[:, v, c:c + 1], scale=modA[:, v, c:c + 1])
                else:
                    ts("dve", o, psB[:, (c // 2) * 128:(c // 2 + 1) * 128], modA[:, v, c:c + 1], ALU.mult, modS[:, v, c:c + 1], ALU.add)
            yield
        for g0 in range(0, 18, NX):
            lockstep([tile_gen(i) for i in range(g0, min(g0 + NX, 18))])

    def load_win(l, name, c0, ncols, top=False):
        w = AR.alloc(name, [KC, ncols], BF16, top=top)
        dma("pool", w, w_in[l, :, c0:c0 + ncols].re("(c p) n -> p c n", p=128))
        return w

    def proj_fm(w, col0, dst, p0, p1, evac):
        pos = p0
        while pos < p1:
            n = min(512, p1 - pos)
            ps = kb.psum_f()
            for k in range(KC):
                mm(ps[:, 0:n], w[:, k, col0:col0 + 128], hT[:, k, pos:pos + n], k == 0, k == KC - 1)
            evac(ps, pos, n)
            pos += n

    def phase_lru(l, b, with_ctx):
        AR.reset()
        w = wst[:, :, 0:256]
        xr_ = [AR.alloc("xr%d" % c, [NPOS]) for c in range(2)]
        xcc = [AR.alloc("xc%d" % c, [NPOS]) for c in range(2)]
        ys_ = [AR.alloc("ysum%d" % c, [NPOS]) for c in range(2)]
        NB_ = 512
        tmp_sets = [{nm: AR.alloc("%s%d" % (nm, q), [NB_]) for nm in ("r", "i", "a", "a2", "bb")} for q in range(4)]
        for c in range(2):
            proj_fm(w, c * 128, None, 0, NPOS, lambda ps, pos, n, c=c: cp("act", xr_[c][:, pos:pos + n].par("proj"), ps[:, 0:n]))
        prefetch(l, PF_POOL)
        for c in range(2):
            cw = lambda k: lp[:, LP_CW + c * 4 + k:LP_CW + c * 4 + k + 1]
            for (s0, s1) in ((0, TC), (TC, NPOS)):
                ts("dve", xcc[c][:, s0:s1], xr_[c][:, s0:s1], cw(2), ALU.mult, lp[:, LP_CB + c:LP_CB + c + 1], ALU.add)
                stt(xcc[c][:, s0 + 1:s1], xr_[c][:, s0:s1 - 1], cw(1), xcc[c][:, s0 + 1:s1], ALU.mult, ALU.add)
                stt(xcc[c][:, s0 + 2:s1], xr_[c][:, s0:s1 - 2], cw(0), xcc[c][:, s0 + 2:s1], ALU.mult, ALU.add)
                stt(xcc[c][:, s0:s1 - 1], xr_[c][:, s0 + 1:s1], cw(3), xcc[c][:, s0:s1 - 1], ALU.mult, ALU.add)
        blocks = [(0, TC)] + [(TC + j * NB_, min(TC + (j + 1) * NB_, NPOS)) for j in range((T + NB_ - 1) // NB_)]

        def chain(d, c):
            dc = d * 2 + c
            tmp = tmp_sets[dc]
            out = ys_[c] if d == 0 else xr_[c]
            order = blocks if d == 0 else [blocks[0]] + blocks[:0:-1]
            prev = None
            for (s0, s1) in order:
                n = s1 - s0
                r_, i_, a_, a2_, bb_ = (tmp[k][:, 0:n] for k in ("r", "i", "a", "a2", "bb"))
                for gi, dst in ((0, r_), (1, i_)):
                    q = 0
                    while q < n:
                        m = min(512, n - q)
                        ps = kb.psum_f()
                        mm(ps[:, 0:m], lruBD[:, d, gi, c, :], xcc[c][:, s0 + q:s0 + q + m], True, True)
                        bcol = (LP_BA if gi == 0 else LP_BX) + dc
                        act(dst[:, q:q + m], ps[:, 0:m], AF.Sigmoid, bias=lp[:, bcol:bcol + 1])
                        q += m
                yield
                act(a_, r_, AF.Exp, scale=lp[:, LP_NSP + dc:LP_NSP + dc + 1])
                act(a2_, r_, AF.Exp, scale=lp[:, LP_NSP2 + dc:LP_NSP2 + dc + 1])
                tt("pool", bb_, i_, xcc[c][:, s0:s1], ALU.mult)
                yield
                ts("dve", a2_, a2_, -1.0, ALU.mult, 1.0, ALU.add)
                yield
                act(a2_, a2_, AF.Sqrt)
                yield
                tt("dve", bb_, bb_, a2_, ALU.mult)
                init = 0.0 if prev is None else prev
                if d == 0:
                    kb.scan(out[:, s0:s1], a_, bb_, init)
                    prev = out[:, s1 - 1:s1]
                else:
                    kb.scan(out[:, s0:s1][:, ::-1], a_[:, ::-1], bb_[:, ::-1], init)
                    prev = out[:, s0:s0 + 1]
                yield
        lockstep([chain(0, 0), chain(1, 0), chain(0, 1), chain(1, 1)])
        for c in range(2):
            lo = 0 if with_ctx else TC
            tt("dve", brT[2][c][:, lo:NPOS], ys_[c][:, lo:NPOS], xr_[c][:, lo:NPOS], ALU.add)

    AR_car = [kb.sb("car%d" % i, [128, 1]) for i in range(2)]

    def phase_pool(l, b, with_ctx):
        AR.reset()
        w = wst[:, :, 0:256]
        xp = AR.alloc("xp", [2, NPOS])
        cs_ = [AR.alloc("cs%d" % c, [NPOS + 2 * 17 + 2]) for c in range(2)]
        pm_ = [AR.alloc("pm%d" % c, [NPOS]) for c in range(2)]
        ones = AR.alloc("ones", [T])
        kb.memset("pool", ones, 1.0)
        for c in range(2):
            proj_fm(w, c * 128, xp, 0, NPOS, lambda ps, pos, n, c=c: cp("act", xp[:, c, pos:pos + n].par("proj"), ps[:, 0:n]))
        prefetch(l, PF_MLA)
        segs = [(0, TC, 0)] + [(TC, NPOS, TC + 17)]
        if not with_ctx:
            segs = segs[1:]
        def chunk_gen(c):
            for (s0, s1, o0) in segs:
                L = s1 - s0
                kb.memset("pool", cs_[c][:, o0:o0 + 9], 0.0)
                kb.scan(cs_[c][:, o0 + 9:o0 + 9 + L], ones[:, 0:L], xp[:, c, s0:s1], 0.0)
                yield
                ts("dve", cs_[c][:, o0 + 9 + L:o0 + 17 + L], cs_[c][:, o0:o0 + 8], cs_[c][:, o0 + 8 + L:o0 + 9 + L], ALU.add)
                yield
                for h in range(2):
                    hw = (1, 2, 4, 8)[2 * c + h]
                    rows = slice(h * 64, (h + 1) * 64)
                    base = o0 + 8
                    tt("dve", pm_[c][rows, s0:s1], cs_[c][rows, base + hw:base + hw + L], cs_[c][rows, base - hw:base - hw + L], ALU.subtract)
                yield
                ts("dve", pm_[c][:, s0:s1], pm_[c][:, s0:s1], cpool[:, c:c + 1], ALU.mult)
                yield
                tt("dve", pm_[c][:, s0:s0 + 8], pm_[c][:, s0:s0 + 8], cpool[:, 2 + c * 8:2 + c * 8 + 8], ALU.mult)
                tt("dve", pm_[c][:, s1 - 8:s1], pm_[c][:, s1 - 8:s1], cpool[:, 18 + c * 8:18 + c * 8 + 8], ALU.mult)
                yield
                tt("pool", pm_[c][:, s0:s1], pm_[c][:, s0:s1], xp[:, c, s0:s1], ALU.subtract)
                yield
                pos = s0
                while pos < s1:
                    n = min(512, s1 - pos)
                    ps = kb.psum_f()
                    mm(ps[:, 0:n], poolBD[:, c, :], pm_[c][:, pos:pos + n], True, True)
                    act(brT[3][c][:, pos:pos + n].par("pool"), ps[:, 0:n], AF.Identity, bias=lp[:, LP_PBS + c:LP_PBS + c + 1],
                        scale=lp[:, LP_PSC + c:LP_PSC + c + 1])
                    pos += n
                    yield
        lockstep([chunk_gen(0), chunk_gen(1)])

    def phase_mla(l, b, with_ctx):
        AR.reset()
        wkv = wst[:, :, 0:160]
        wq = wst[:, :, 160:416]
        qT = AR.alloc("qT", [4, NPOS], BF16)
        kT = AR.alloc("kT", [4, NPOS], BF16)
        Vt = AR.alloc("Vt", [18, 4, 68], BF16)
        lo_fixed = AR.lo
        kb.memset("dve", Vt.re("p a b c -> p (a b c)"), 1.0)
        for h_ in range(4):
            kb.memset("pool", qT[:, h_, :], 0.0)
            kb.memset("pool", kT[:, h_, :], 0.0)
        NS = 3
        sm = [AR.alloc("sm%d" % i, [32]) for i in range(NS)]
        kr = [AR.alloc("kr%d" % i, [32]) for i in range(NS)]
        cn = [AR.alloc("cn%d" % i, [384], BF16) for i in range(NS)]
        cnT = [AR.alloc("cnT%d" % i, [3, 128], BF16) for i in range(NS)]
        sq = [AR.alloc("sq%d" % i, [704]) for i in range(NS)]
        qk = [AR.alloc("qk%d" % i, [2, 4, 96]) for i in range(NS)]
        rg = [AR.alloc("rg%d" % i, [2, 4, 96]) for i in range(NS)]
        qkb = [AR.alloc("qkb%d" % i, [2, 4, 96], BF16) for i in range(NS)]
        rt = [AR.alloc("rt%d" % i, [2, 2, 4, 32]) for i in range(NS)]
        kvs_ = [AR.alloc("kvs%d" % i, [512]) for i in range(NS)]
        qs_ = [AR.alloc("qs%d" % i, [384]) for i in range(NS)]

        def info(i):
            is_ctx = i < 2
            do_q = (not is_ctx) or with_ctx
            return is_ctx, do_q, i % NS, slice(i * 128, (i + 1) * 128)

        def stage_a(i):
            is_ctx, do_q, j, pos = info(i)
            s, cn_, cnT_, sq_ = sm[j], cn[j], cnT[j], sq[j]
            ps1 = kb.psum_any()
            for k in range(KC):
                mm(ps1[:, 0:160], hT[:, k, pos], wkv[:, k, :], k == 0, k == KC - 1)
            if do_q:
                for k in range(KC):
                    mm(ps1[:, 160:416], hT[:, k, pos], wq[:, k, :], k == 0, k == KC - 1)
            yield
            act(sq_[:, 0:128], ps1[:, 0:128], AF.Square, accum=s[:, 0:1])
            if do_q:
                act(sq_[:, 128:384], ps1[:, 160:416], AF.Square, accum=s[:, 1:2])
            else:
                kb.memset("pool", s[:, 1:2], 1.0)
            cp("act", kr[j], ps1[:, 128:160])
            yield
            act(s[:, 2:3], s[:, 0:1], AF.Sqrt, scale=1.0 / 128, bias=EPS)
            act(s[:, 3:4], s[:, 1:2], AF.Sqrt, scale=1.0 / 256, bias=EPS)
            yield
            kb.recip(s[:, 4:6], s[:, 2:4])
            ts("dve", cn_[:, 0:128], ps1[:, 0:128], s[:, 4:5], ALU.mult)
            if do_q:
                ts("dve", cn_[:, 128:384], ps1[:, 160:416], s[:, 5:6], ALU.mult)
            yield
            psT = kb.psum_any(bf16=True)
            for c in range(3 if do_q else 1):
                tr(psT[:, c * 128:(c + 1) * 128], cn_[:, c * 128:(c + 1) * 128], identb)
            yield
            cp("act", cnT_[:, 0:(3 if do_q else 1), :], psT[:, 0:(384 if do_q else 128)].re("p (c n) -> p c n", n=128))
            yield

        def stage_b(i):
            is_ctx, do_q, j, pos = info(i)
            s, cnT_, sq_, qk_, qkb_, rt_, rg_ = sm[j], cnT[j], sq[j], qk[j], qkb[j], rt[j], rg[j]
            pskv = kb.psum_any()
            mm(pskv, cnT_[:, 0, :], wukv, True, True)
            if do_q:
                psq = kb.psum_any()
                for c in range(2):
                    mm(psq[:, 0:384], cnT_[:, 1 + c, :], wuq[:, c, :], c == 0, c == 1)
            yield
            cp("act", kvs_[j], pskv)
            if do_q:
                cp("act", qs_[j], psq[:, 0:384])
            kv3 = kvs_[j].re("p (h e) -> p h e", h=4)
            q3 = qs_[j].re("p (h e) -> p h e", h=4)
            krope = kr[j]
            act(sq_[:, 640:672], krope, AF.Square, accum=s[:, 7:8])
            cp("act", Vt[:, i, :, 0:64].par("mlav"), kv3[:, :, 64:128])
            yield
            ksq = sq_[:, 0:256].re("p (h e) -> p h e", h=4)
            qsq = sq_[:, 256:640].re("p (h e) -> p h e", h=4)
            tt("pool", ksq, kv3[:, :, 0:64], kv3[:, :, 0:64], ALU.mult)
            if do_q:
                tt("pool", qsq, q3, q3, ALU.mult)
            yield
            kb.reduce(s[:, 12:16], ksq)
            if do_q:
                kb.reduce(s[:, 8:12], qsq)
            else:
                kb.memset("dve", s[:, 8:12], 1.0)
            ts("dve", s[:, 12:16], s[:, 12:16], s[:, 7:8], ALU.add)
            yield
            act(s[:, 8:16], s[:, 8:16], AF.Sqrt, scale=1.0 / 96, bias=EPS)
            yield
            kb.recip(s[:, 16:24], s[:, 8:16])
            yield
            tt("pool", rg_, s[:, 16:24].re("p (w h) -> p w h", w=2).us(3).bc([128, 2, 4, 96]),
               gains.us(2).bc([128, 2, 4, 96]), ALU.mult)
            yield
            if do_q:
                tt("dve", qk_[:, 0].par("qk"), q3, rg_[:, 0], ALU.mult)
            else:
                kb.memset("pool", qk_[:, 0].par("qk"), 0.0)
            tt("dve", qk_[:, 1, :, 0:64].par("qk"), kv3[:, :, 0:64], rg_[:, 1, :, 0:64], ALU.mult)
            tt("pool", qk_[:, 1, :, 64:96].par("qk"), krope.us(1).bc([128, 4, 32]), rg_[:, 1, :, 64:96], ALU.mult)
            yield
            if not is_ctx:
                ti = i - 2
                v = qk_[:, :, :, 64:96]
                t1_ = rt_[:, 0].par("rt")
                t2_ = rt_[:, 1]
                Cb = ropeC[:, ti, :].us(1).us(1).bc([128, 2, 4, 32])
                tt("pool", t1_, v, Cb, ALU.mult)
                for a in range(2):
                    for s_ in range(2):
                        o_ = t2_[:, :, :, a * 16 + s_ * 8:a * 16 + s_ * 8 + 8].par("rt")
                        i_ = qk_[:, :, :, 64 + a * 16 + (1 - s_) * 8:64 + a * 16 + (1 - s_) * 8 + 8]
                        Sb = ropeS[:, ti, a * 16 + s_ * 8:a * 16 + s_ * 8 + 8].us(1).us(1).bc([128, 2, 4, 8])
                        tt("dve" if (a + s_) % 2 == 0 else "pool", o_, i_, Sb, ALU.mult)
                yield
                tt("dve", v, t1_, t2_, ALU.add)
            cp("dve", qkb_, qk_)
            yield

        def stage_c(i):
            is_ctx, do_q, j, pos = info(i)
            qkb_ = qkb[j]
            psT2 = kb.psum_any(bf16=True)
            for w_ in range(2):
                if w_ == 0 and not do_q:
                    continue
                for h in range(4):
                    tr(psT2[0:96, (w_ * 4 + h) * 128:(w_ * 4 + h + 1) * 128], qkb_[:, w_, h, :], identb)
            yield
            if do_q:
                cp("act", qT[0:96, :, pos].par("mlaqk"), psT2[0:96, 0:512].re("p (h n) -> p h n", h=4))
            cp("act", kT[0:96, :, pos].par("mlaqk"), psT2[0:96, 512:1024].re("p (h n) -> p h n", h=4))

        def tile_gen(i):
            yield from stage_a(i)
            yield from stage_b(i)
            yield from stage_c(i)
        for g0 in range(0, 18, NS):
            lockstep([tile_gen(i) for i in range(g0, g0 + NS)])
        if "mla_stop1" in dbg:
            return
        prefetch(l, PF_GATE0)
        P.barrier()
        AR.lo = lo_fixed
        omla = AR.alloc("omla", [18, 256], BF16)
        pT = [AR.alloc("pT%d" % i, [2, 512], BF16) for i in range(3)]
        rc = [AR.alloc("rc%d" % i, [4]) for i in range(2)]
        jobs = []
        if with_ctx:
            jobs.append((0, TC, 0, 2))
        for qb in range(4):
            jobs.append((TC + qb * 512, TC + (qb + 1) * 512, 0, 18))
        pairs = []
        for (q0, q1, kt0, kt1) in jobs:
            for h in range(4):
                for kt in range(kt0, kt1, 2):
                    pairs.append((q0, q1, kt0, kt1, h, kt))
        SB = (0, 3)

        def s_mm(pi_):
            q0, q1, kt0, kt1, h, kt = pairs[pi_]
            tsel = SB[pi_ % 2]
            for j in range(2):
                mm(kb.ps8[2 * tsel + j][:, 0:q1 - q0], kT[:, h, (kt + j) * 128:(kt + j + 1) * 128], qT[:, h, q0:q1], True, True)
        s_mm(0)
        for pi_, (q0, q1, kt0, kt1, h, kt) in enumerate(pairs):
            nq = q1 - q0
            nqt = nq // 128
            pso = [kb.psf[2 + t_] for t_ in range(nqt)]
            if pi_ + 1 < len(pairs):
                s_mm(pi_ + 1)
            tsel = SB[pi_ % 2]
            p_ = pT[pi_ % 3]
            src = kb.pspair[tsel].rearrange("p (j n) -> p j n", j=2)[:, :, 0:nq]
            dst = p_.ap[:, :, 0:nq]
            P.op("act", lambda e, src=src, dst=dst: e.activation(out=dst, in_=src, func=AF.Exp, bias=0.0, scale=1.0),
                 [kb.ps8[2 * tsel].buf, kb.ps8[2 * tsel + 1].buf], [p_.buf])
            for j in range(2):
                for t_ in range(nqt):
                    mm(pso[t_][:, 0:68], p_[:, j, t_ * 128:(t_ + 1) * 128], Vt[:, kt + j, h, :], kt + j == kt0, kt + j == kt1 - 1)
            if kt + 2 >= kt1:
                for t_ in range(nqt):
                    r_ = rc[t_ % 2]
                    kb.recip(r_[:, 0:1], pso[t_][:, 64:65])
                    ts("dve", omla[:, (q0 // 128) + t_, h * 64:(h + 1) * 64].par("omla"), pso[t_][:, 0:64], r_[:, 0:1], ALU.mult)
        for i in range(0 if with_ctx else 2, 18):
            psT = kb.psum_b()
            for c in range(2):
                tr(psT[:, c * 128:(c + 1) * 128], omla[:, i, c * 128:(c + 1) * 128], identb)
            for c in range(2):
                cp("act", brT[0][c][:, i * 128:(i + 1) * 128].par("brt0"), psT[:, c * 128:(c + 1) * 128])

    def phase_s5(l, b, with_ctx):
        AR.reset()
        AR2.lo = 0
        U = AR.alloc("U", [16, NCH])
        SN = [[AR.alloc("SN%d%d" % (d, ri), [8, NCH + 2]) for ri in range(2)] for d in range(2)]
        Ere = AR.alloc("Ere", [2, 8, 128])
        nEim = AR.alloc("nEim", [2, 8, 128])
        W3 = AR.alloc("W3", [16, 128])
        scr0 = AR.lo
        W1 = [AR2.alloc("W1%d" % i, [16, 64]) for i in range(2)]
        tab = [AR2.alloc("tab%d" % i, [8, NCH]) for i in range(2)]
        cached = (b > 0) and ("s5_nocache" not in dbg)

        def load_dir(d):
            dma("sp", W1[0].re("p g n -> p (g n)"), st_dir[d, :, 0:1024])
            dma("sp", W1[1].re("p g n -> p (g n)"), st_dir[d, :, 1024:2048])
            dma("act", tab[0].re("p g n -> p (g n)"), st_dir[d, :, 2048:2048 + 8 * NCH])
            dma("act", tab[1].re("p g n -> p (g n)"), st_dir[d, :, 2048 + 8 * NCH:2048 + 16 * NCH])
        if cached:
            dma("sp", Ere.re("p d g n -> p (d g n)"), st_main[0])
            dma("act", nEim.re("p d g n -> p (d g n)"), st_main[1])
            dma("sp", W3.re("p g n -> p (g n)"), st_main[2])
            load_dir(0)
        ws5 = wst[:, :, 0:256]
        Uc = AR.alloc("Uc", [16, 8, 16])
        kblocks = [(0, 32, 0), (32, 128, TC), (160, 128, TC + 1024)]
        for (k0, M, p0) in kblocks:
            pss = [kb.psum_f() for _ in range(4)]
            for r in range(8):
                ps = pss[r // 2]
                for k in range(KC):
                    mm(ps[0:M, (r % 2) * 256:(r % 2 + 1) * 256], hT[:, k, p0 + r:p0 + 8 * M:8], ws5[:, k, :], k == 0, k == KC - 1)
            for q in range(4):
                src = pss[q][0:M, :].re("p (r g i) -> p r g i", r=2, g=16)
                dst = Uc[0:M, :, 2 * q:2 * q + 2, :].re("p g r i -> p r g i").par("s5uc")
                cp("act" if q % 2 == 0 else "dve", dst, src)
            for gq in range(4):
                ps = kb.psum_f()
                for gg_ in range(4):
                    g = gq * 4 + gg_
                    tr(ps[:, gg_ * 128:gg_ * 128 + M], Uc[0:M, g, :, :].re("p r i -> p (r i)"), identf[0:M, 0:M])
                cp("act" if gq % 2 == 0 else "dve", U[:, gq * 4:gq * 4 + 4, k0:k0 + M].par("s5u"),
                   ps.re("p (g n) -> p g n", g=4)[:, :, 0:M])
        prefetch(l, PF_LRU)
        P.barrier()
        AR.lo = scr0
        if "s5_stop1" in dbg:
            return
        if not cached:
            kb.memset("pool", W3, 0.0)
        prm = AR.alloc("prm", [40, 8])
        PW = [AR.alloc("pw%d" % i, [10, 8]) for i in range(2)]
        Bb = [AR.alloc("Bb%d" % i, [8, 16]) for i in range(2)]
        Braw = [AR.alloc("Braw%d" % i, [8, 16]) for i in range(2)]
        Craw = [AR.alloc("Craw%d" % i, [8, 16]) for i in range(2)]
        Dbc = AR.alloc("Dbc", [256])
        t8 = AR.alloc("t8", [8, 16])
        w3t = [AR.alloc("w3t%d" % i, [128]) for i in range(2)]
        tE = AR.alloc("tE", [8, 128])
        Fm = [AR.alloc("F%d" % i, [8, 8, 16]) for i in range(2)]
        Ep = [AR.alloc("Ep%d" % i, [8, 128]) for i in range(2)]
        Cnat = [AR.alloc("Cnat%d" % i, [16, 64], at=Ep[i].off, buf=Ep[i].buf) for i in range(2)]
        rs_sets = [[AR.alloc("rs%d_%d" % (q, i), [NCH], at=Fm[0].off + (q * 7 + i) * NCH) for i in range(6)]
                   for q in range(2)]
        rho_sets = [AR.alloc("rho1_%d" % q, [NCH], at=Fm[0].off + (q * 7 + 6) * NCH) for q in range(2)]
        assert Fm[0].off + 14 * NCH <= Ep[1].off + Ep[1].w
        dma("sp", Dbc, s5_d[l:l + 1, :].re("o n -> (o n)").pbc(128))
        dsel = cmask[:, 2, :]

        prm_bufs = [Buf("prm%d" % i) for i in range(40)]
        pw_bufs = [[Buf("pw%d_%d" % (ri, j)) for j in range(10)] for ri in range(2)]

        def pv(i):
            return TV(prm.ap[:, i, :], prm_bufs[i])

        def pw(ri, j):
            return TV(PW[ri].ap[:, j, :], pw_bufs[ri][j])

        for d in range(2):
            if d == 1 and not cached:
                P.barrier()
            if not cached:
                dma("sp", pv(0), s5_a_re[l, d].re("(gp h) p -> (h p) gp", h=2), slow=True)
                dma("act", pv(1), s5_a_im[l, d].re("(gp h) p -> (h p) gp", h=2), slow=True)
                ldt = t8[:, 0, :]
                dma("sp", ldt, s5_log_dt[l, d:d + 1, :].re("o g -> (o g)").pbc(128))
                dma("sp", Cnat[0][0:16], s5_c_re[l, d].re("g o p -> o g p"))
                dma("act", Cnat[1][0:16], s5_c_im[l, d].re("g o p -> o g p"))
                for h in range(2):
                    rows = slice(h * 64, (h + 1) * 64)
                    cp("dve", pv(2)[rows], ldt[rows, h:16:2])
                    dma("sp", Braw[0][rows].par("ldw"), s5_b_re[l, d].re("(gp h) p i -> h p gp i", h=2)[h])
                    dma("act", Braw[1][rows].par("ldw"), s5_b_im[l, d].re("(gp h) p i -> h p gp i", h=2)[h])
                for ri in range(2):
                    ps = kb.psum_f()
                    for g in range(16):
                        gp, h = g // 2, g % 2
                        mm(ps[h * 64:(h + 1) * 64, gp * 16:(gp + 1) * 16], Cnat[ri][0:16, g, :], identf[0:16, 0:16], True, True)
                    cp("act", Craw[ri], ps[:, 0:128].re("p (g o) -> p g o", g=8))
                if "s5_b1" in dbg:
                    return
                act(pv(2), pv(2), AF.Exp)
                tt("dve", pv(3), pv(0), pv(2), ALU.mult)
                tt("dve", pv(4), pv(1), pv(2), ALU.mult)
                act(pv(5), pv(3), AF.Exp)
                for (dst, shift) in ((6, 0.0), (7, math.pi / 2)):
                    xx, kk, ki = pv(30), pv(31), pv(32)
                    ts("dve", xx, pv(4), shift, ALU.add)
                    ts("dve", kk, xx, 1.0 / (2 * math.pi), ALU.mult)
                    kint = TV(ki.ap.bitcast(I32), ki.buf)
                    cp("dve", kint, kk)
                    cp("dve", kk, kint)
                    stt(xx, kk, -6.28125, xx, ALU.mult, ALU.add)
                    stt(xx, kk, -(2 * math.pi - 6.28125), xx, ALU.mult, ALU.add)
                    ts("dve", xx, xx, math.pi, ALU.min, -math.pi, ALU.max)
                    act(pv(dst), xx, AF.Sin)
                kb.memset("dve", pw(0, 0), 1.0)
                kb.memset("dve", pw(1, 0), 0.0)
                tt("dve", pw(0, 1), pv(5), pv(7), ALU.mult)
                tt("dve", pw(1, 1), pv(5), pv(6), ALU.mult)
                for j in range(2, 9):
                    tt("dve", pv(30), pw(0, j - 1), pw(0, 1), ALU.mult)
                    tt("dve", pv(31), pw(1, j - 1), pw(1, 1), ALU.mult)
                    tt("dve", pw(0, j), pv(30), pv(31), ALU.subtract)
                    tt("dve", pv(30), pw(0, j - 1), pw(1, 1), ALU.mult)
                    tt("dve", pv(31), pw(1, j - 1), pw(0, 1), ALU.mult)
                    tt("dve", pw(1, j), pv(30), pv(31), ALU.add)
                act(pv(8), pv(3), AF.Exp, scale=-16.0)
                tt("dve", pw(0, 9), pw(0, 8), pv(8), ALU.mult)
                tt("dve", pw(1, 9), pw(1, 8), pv(8), ALU.mult)
                ts("dve", pw(1, 9), pw(1, 9), -1.0, ALU.mult)
                act(pv(9), pv(3), AF.Exp, scale=8.0)
                act(pv(10), pv(3), AF.Exp, scale=-8.0)
                tt("dve", pv(11), pw(0, 8), pv(10), ALU.mult)
                tt("dve", pv(12), pw(1, 8), pv(10), ALU.mult)
                ts("dve", pv(13), pw(0, 1), -1.0, ALU.add)
                tt("dve", pv(14), pv(0), pv(0), ALU.mult)
                tt("dve", pv(15), pv(1), pv(1), ALU.mult)
                tt("dve", pv(14), pv(14), pv(15), ALU.add)
                kb.recip(pv(14), pv(14))
                tt("dve", pv(15), pv(13), pv(0), ALU.mult)
                tt("dve", pv(16), pw(1, 1), pv(1), ALU.mult)
                tt("dve", pv(15), pv(15), pv(16), ALU.add)
                tt("dve", pv(15), pv(15), pv(14), ALU.mult)
                tt("dve", pv(16), pw(1, 1), pv(0), ALU.mult)
                tt("dve", pv(17), pv(13), pv(1), ALU.mult)
                tt("dve", pv(16), pv(16), pv(17), ALU.subtract)
                tt("dve", pv(16), pv(16), pv(14), ALU.mult)
                fre = pv(15).us(2).bc([128, 8, 16])
                fim = pv(16).us(2).bc([128, 8, 16])
                tt("dve", Bb[0], Braw[0], fre, ALU.mult)
                tt("dve", t8, Braw[1], fim, ALU.mult)
                tt("dve", Bb[0], Bb[0], t8, ALU.subtract)
                tt("dve", Bb[1], Braw[1], fre, ALU.mult)
                tt("dve", t8, Braw[0], fim, ALU.mult)
                tt("dve", Bb[1], Bb[1], t8, ALU.add)
                for t_ in range(8):
                    f_ = t_ + 1 if d == 0 else 8 - t_
                    pr = pw(0, f_).us(2).bc([128, 8, 16])
                    pi_ = pw(1, f_).us(2).bc([128, 8, 16])
                    eo = Ere[:, d, :, t_ * 16:(t_ + 1) * 16]
                    ei = nEim[:, d, :, t_ * 16:(t_ + 1) * 16]
                    tt("dve", eo, Craw[0], pr, ALU.mult)
                    tt("dve", t8, Craw[1], pi_, ALU.mult)
                    tt("dve", eo, eo, t8, ALU.subtract)
                    tt("dve", ei, Craw[0], pi_, ALU.mult)
                    tt("dve", t8, Craw[1], pr, ALU.mult)
                    tt("dve", ei, ei, t8, ALU.add)
                    ts("dve", ei, ei, -1.0, ALU.mult)
                for r in range(8):
                    e_ = 7 - r if d == 0 else r
                    pr = pw(0, e_).us(2).bc([128, 8, 16])
                    pi_ = pw(1, e_).us(2).bc([128, 8, 16])
                    tt("dve", Fm[0][:, :, r, :], Bb[0], pr, ALU.mult)
                    tt("dve", t8, Bb[1], pi_, ALU.mult)
                    tt("dve", Fm[0][:, :, r, :], Fm[0][:, :, r, :], t8, ALU.subtract)
                    tt("dve", Fm[1][:, :, r, :], Bb[1], pr, ALU.mult)
                    tt("dve", t8, Bb[0], pi_, ALU.mult)
                    tt("dve", Fm[1][:, :, r, :], Fm[1][:, :, r, :], t8, ALU.add)
                qr = pw(0, 9).us(2).bc([128, 8, 128])
                qi = pw(1, 9).us(2).bc([128, 8, 128])
                tt("dve", Ep[0], Ere[:, d], qr, ALU.mult)
                tt("dve", tE, nEim[:, d], qi, ALU.mult)
                tt("dve", Ep[0], Ep[0], tE, ALU.add)
                tt("dve", Ep[1], nEim[:, d], qr, ALU.mult)
                tt("dve", tE, Ere[:, d], qi, ALU.mult)
                tt("dve", Ep[1], Ep[1], tE, ALU.subtract)
                if "s5_b2" in dbg:
                    return
                for ri in range(2):
                    for gq in range(4):
                        ps = kb.psum_f()
                        for gg_ in range(4):
                            g = gq * 4 + gg_
                            gp, h = g // 2, g % 2
                            rows = slice(h * 64, (h + 1) * 64)
                            mm(ps[:, gg_ * 64:(gg_ + 1) * 64], Fm[ri][:, gp].re("p r i -> p (r i)"), identf[:, h * 64:(h + 1) * 64], True, True)
                        cp("act", W1[ri][:, gq * 4:gq * 4 + 4, :], ps[:, 0:256].re("p (g n) -> p g n", g=4))
                if "s5_b3" in dbg:
                    return
                for g in range(16):
                    gp, h = g // 2, g % 2
                    rows = slice(h * 64, (h + 1) * 64)
                    ps = kb.psf[h * 2 + (g // 2) % 2]
                    mm(ps[:, 0:128], Fm[0][rows, gp].re("p r i -> p (r i)"), Ep[0][rows, gp, :], True, False)
                    mm(ps[:, 0:128], Fm[1][rows, gp].re("p r i -> p (r i)"), Ep[1][rows, gp, :], False, True)
                    wt = w3t[g % 2]
                    tt("dve", wt, ps[:, 0:128], cmask[:, d, :], ALU.mult)
                    tt("pool", W3[:, g, :], W3[:, g, :], wt, ALU.add)
                if d == 0:
                    for g in range(16):
                        dcol = Dbc[:, g * 16:(g + 1) * 16].us(1).bc([128, 8, 16])
                        wt = w3t[g % 2]
                        tt("dve", wt.re("p (t o) -> p t o", t=8), dsel.re("p (t o) -> p t o", t=8), dcol, ALU.mult)
                        tt("pool", W3[:, g, :], W3[:, g, :], wt, ALU.add)
                if "s5_b4" in dbg:
                    return
                kb.memset("dve", tab[0][:, :, 0:1], 1.0)
                kb.memset("dve", tab[1][:, :, 0:1], 0.0)
                cp("dve", tab[0][:, :, 1:2], pv(11).us(2))
                cp("dve", tab[1][:, :, 1:2], pv(12).us(2))
                n = 2
                while n < NCH:
                    m = min(n, NCH - n)
                    tt("dve", pv(30), tab[0][:, :, n - 1], pv(11), ALU.mult)
                    tt("dve", pv(31), tab[1][:, :, n - 1], pv(12), ALU.mult)
                    tt("dve", pv(33), pv(30), pv(31), ALU.subtract)
                    tt("dve", pv(30), tab[0][:, :, n - 1], pv(12), ALU.mult)
                    tt("dve", pv(31), tab[1][:, :, n - 1], pv(11), ALU.mult)
                    tt("dve", pv(34), pv(30), pv(31), ALU.add)
                    Pr = pv(33).us(2).bc([128, 8, m])
                    Pi = pv(34).us(2).bc([128, 8, m])
                    ta = tE[:, :, 0:m]
                    tt("dve", tab[0][:, :, n:n + m], tab[0][:, :, 0:m], Pr, ALU.mult)
                    tt("dve", ta, tab[1][:, :, 0:m], Pi, ALU.mult)
                    tt("dve", tab[0][:, :, n:n + m], tab[0][:, :, n:n + m], ta, ALU.subtract)
                    tt("dve", tab[1][:, :, n:n + m], tab[0][:, :, 0:m], Pi, ALU.mult)
                    tt("dve", ta, tab[1][:, :, 0:m], Pr, ALU.mult)
                    tt("dve", tab[1][:, :, n:n + m], tab[1][:, :, n:n + m], ta, ALU.add)
                    n *= 2
                dma("sp", st_dir[d, :, 0:1024], W1[0].re("p g n -> p (g n)"))
                dma("sp", st_dir[d, :, 1024:2048], W1[1].re("p g n -> p (g n)"))
                dma("act", st_dir[d, :, 2048:2048 + 8 * NCH], tab[0].re("p g n -> p (g n)"))
                dma("act", st_dir[d, :, 2048 + 8 * NCH:2048 + 16 * NCH], tab[1].re("p g n -> p (g n)"))
                dma("sp", st_dir[d, :, 2048 + 16 * NCH:2048 + 16 * NCH + 8], pv(9))
            else:
                if d == 1:
                    load_dir(1)
                dma("sp", pv(9), st_dir[d, :, 2048 + 16 * NCH:2048 + 16 * NCH + 8])
            if "s5_stop2" in dbg:
                return
            if not cached:
                P.barrier()
            def nat(tv, sl, rev):
                v = tv[:, sl]
                return v[:, ::-1] if rev else v

            def gp_gen(gp, d=d):
                rs, rho1 = rs_sets[gp % 2], rho_sets[gp % 2]
                psv = [kb.psum_f(), kb.psum_f()]
                for ri in range(2):
                    for h in range(2):
                        g = gp * 2 + h
                        mm(psv[ri][h * 64:(h + 1) * 64, 0:NCH], W1[ri][:, g, :], U[:, g, :], True, True)
                cp("pool", rho1, pv(9)[:, gp:gp + 1].bc([128, NCH]))
                yield
                Cn, Sn = tab[0][:, gp, :], tab[1][:, gp, :]
                if d == 0:
                    segs = [(slice(0, NCH), slice(0, NCH), False)]
                else:
                    segs = [(slice(0, 32), slice(0, 32), True), (slice(32, NCH), slice(32, NCH), True)]
                for (js, ks, rev) in segs:
                    vre, vim = nat(psv[0][:, 0:NCH], ks, rev), nat(psv[1][:, 0:NCH], ks, rev)
                    tt("dve", rs[0][:, js], vre, Cn[:, js], ALU.mult)
                    tt("dve", rs[1][:, js], vim, Sn[:, js], ALU.mult)
                    tt("dve", rs[4][:, js], vim, Cn[:, js], ALU.mult)
                    tt("dve", rs[5][:, js], vre, Sn[:, js], ALU.mult)
                yield
                tt("pool", rs[2], rs[0], rs[1], ALU.add)
                tt("pool", rs[3], rs[4], rs[5], ALU.subtract)
                yield
                kb.scan(rs[4], rho1, rs[2], 0.0)
                kb.scan(rs[5], rho1, rs[3], 0.0)
                yield
                tt("dve", rs[0], rs[4], Cn, ALU.mult)
                tt("dve", rs[1], rs[5], Sn, ALU.mult)
                tt("dve", rs[2], rs[4], Sn, ALU.mult)
                tt("dve", rs[3], rs[5], Cn, ALU.mult)
                yield
                for (js, ks, rev) in segs:
                    if d == 0:
                        osl = slice(1, NCH + 1)
                    else:
                        osl = slice(0, 32) if ks.start == 0 else slice(33, NCH + 1)
                    ore = nat(SN[d][0][:, gp, :], osl, rev).par("s5sn")
                    oim = nat(SN[d][1][:, gp, :], osl, rev).par("s5sn")
                    tt("pool", ore, rs[0][:, js], rs[1][:, js], ALU.subtract)
                    tt("pool", oim, rs[2][:, js], rs[3][:, js], ALU.add)
                yield
            for gp0 in range(0, 8, 2):
                lockstep([gp_gen(gp0), gp_gen(gp0 + 1)])
            for ri in range(2):
                if d == 0:
                    kb.memset("pool", SN[0][ri][:, :, 0:1], 0.0)
                else:
                    kb.memset("pool", SN[1][ri][:, :, 32:33], 0.0)
                    cp("pool", SN[1][ri][:, :, NCH + 1:NCH + 2], SN[1][ri][:, :, 0:1])
        if not cached:
            dma("sp", st_main[0], Ere.re("p d g n -> p (d g n)"))
            dma("act", st_main[1], nEim.re("p d g n -> p (d g n)"))
            dma("sp", st_main[2], W3.re("p g n -> p (g n)"))
        if "s5_stop3" in dbg:
            return
        P.barrier()
        AR.lo = scr0
        AR2.lo = 0
        Yc = AR.alloc("Yc", [8, 256])
        gg = AR.alloc("gg", [8, 256])
        g2 = AR.alloc("g2", [8, 256])
        sg = [AR.alloc("sg%d" % i, [512]) for i in range(2)]
        ggT = AR2.alloc("ggT", [2, NPOS])
        ggb = AR2.alloc("ggb", [2, NPOS], BF16)
        banks = [kb.psf[0], kb.psf[2], kb.psf[1], kb.psf[3]]

        def r_mm(k0, M, p0):
            for g in range(16):
                gp, h = g // 2, g % 2
                rows = slice(h * 64, (h + 1) * 64)
                bank = banks[h * 2 + (gp // 4)]
                o = bank[0:M, (gp % 4) * 128:(gp % 4 + 1) * 128]
                cf = slice(k0, k0 + M)
                cb_ = slice(k0 + 1, k0 + 1 + M) if k0 == 0 else slice(k0 + 2, k0 + 2 + M)
                mm(o, SN[0][0][rows, gp, cf], Ere[rows, 0, gp, :], True, False)
                mm(o, SN[0][1][rows, gp, cf], nEim[rows, 0, gp, :], False, False)
                mm(o, SN[1][0][rows, gp, cb_], Ere[rows, 1, gp, :], False, False)
                mm(o, SN[1][1][rows, gp, cb_], nEim[rows, 1, gp, :], False, False)
                mm(o, U[:, g, k0:k0 + M], W3[:, g, :], False, True)

        def r_evac(k0, M, p0):
            for h in range(2):
                for q in range(2):
                    bank = banks[h * 2 + q]
                    src = bank[0:M, :].re("p (g t o) -> p g t o", g=4, t=8)
                    dst = Yc[0:M].re("p t (g2 h o) -> p h g2 t o", h=2, o=16)[:, h, 4 * q:4 * q + 4].par("s5yc")
                    cp("act" if h == 0 else "dve", dst, src)

        def r_rest(k0, M, p0):
            yv, gv, g2v = Yc[0:M], gg[0:M], g2[0:M]
            tt("pool", g2v, yv, yv, ALU.mult)
            ts("dve", g2v, g2v, 0.044715, ALU.mult, 1.0, ALU.add)
            tt("pool", g2v, g2v, yv, ALU.mult)
            act(g2v, g2v, AF.Sigmoid, scale=1.5957691216057308)
            tt("dve", gv, g2v, yv, ALU.mult)
            for t_ in range(8):
                pz = [kb.psf[4], kb.psf[5]]
                for c in range(2):
                    tr(pz[c][:, 0:M], gv[:, t_, c * 128:(c + 1) * 128], identf[0:M, 0:M])
                for c in range(2):
                    cp("act" if c == 0 else "dve", ggT[:, c, p0 + t_:p0 + 8 * M:8].par("s5gg"), pz[c][:, 0:M])
        rb = [blk for blk in kblocks if not (blk[0] == 0 and not with_ctx)]
        r_mm(*rb[0])
        for bi, blk in enumerate(rb):
            r_evac(*blk)
            if bi + 1 < len(rb):
                r_mm(*rb[bi + 1])
            r_rest(*blk)
        lo = 0 if with_ctx else TC
        for c in range(2):
            cp("dve", ggb[:, c, lo:NPOS], ggT[:, c, lo:NPOS])
        for c in range(2):
            pos = lo
            while pos < NPOS:
                n = min(512, NPOS - pos)
                ps = kb.psum_f()
                for k in range(2):
                    mm(ps[:, 0:n], wglu[:, k, c * 128:(c + 1) * 128], ggb[:, k, pos:pos + n], k == 0, k == 1)
                s_ = sg[(pos // 512) % 2]
                act(s_[:, 0:n], ps[:, 0:n], AF.Sigmoid)
                tt("dve", brT[1][c][:, pos:pos + n], s_[:, 0:n], ggT[:, c, pos:pos + n], ALU.mult)
                pos += n

    def phase_merge(l, b, with_ctx, nxt=None):
        AR.reset(top=True)
        lo = 0 if with_ctx else TC
        blocks = []
        pos = lo
        while pos < NPOS:
            n = min(512, NPOS - pos) if pos >= TC else TC - pos
            blocks.append((pos, n))
            pos += n
        ypT = AR.alloc("ypT", [KC, NPOS], BF16, top=True)
        wbr = AR.alloc("wbr", [4, 2, D], BF16, top=True)
        wo = AR.alloc("wo", [KC, D], BF16, top=True)
        wml = [AR.alloc("wml%d" % i, [KC, 4, 128], BF16, top=True) for i in range(2)]

        def load_wml(oc):
            for n_ in range(4):
                c0 = O_MERGE + n_ * D + oc * 128
                dma("pool", wml[oc % 2][:, :, n_, :].par("ldw"), w_in[l, :, c0:c0 + 128].re("(c p) n -> p c n", p=128))
        wg = AR.alloc("w_gate", [KC, 1024], BF16)
        for cc in range(1, 8):
            dma("pool", wg[:, :, cc * 128:(cc + 1) * 128].par("ldw"),
                w_in[l, :, O_GATE + cc * 128:O_GATE + (cc + 1) * 128].re("(c p) n -> p c n", p=128))
        for n_ in range(4):
            dma("pool", wbr[:, n_].par("ldw"), w_branch[l, n_].re("(c p) n -> p c n", p=128))
        load_wml(0)
        load_wml(1)
        dma("pool", wo, w_out[l].re("(c p) n -> p c n", p=128))
        sl = [AR.alloc("sl%d" % i, [512], BF16) for i in range(2)]
        it = 0
        for cc in range(8):
            for (p0, n) in blocks:
                ps = kb.psum_f()
                for k in range(KC):
                    wsl = wst[:, k, 0:128] if cc == 0 else wg[:, k, cc * 128:(cc + 1) * 128]
                    mm(ps[:, 0:n], wsl, hT[:, k, p0:p0 + n], k == 0, k == KC - 1)
                s_ = sl[it % 2]
                it += 1
                act(s_[:, 0:n], ps[:, 0:n], AF.Silu)
                br = brT[cc // 2][cc % 2][:, p0:p0 + n]
                tt("dve", br, br, s_[:, 0:n], ALU.mult)
            if cc == 0 and nxt is not None:
                prefetch(nxt, PF_S5)
        AR.reset()
        sg = [AR.alloc("sg%d" % i, [512]) for i in range(2)]
        tm = [AR.alloc("tm%d" % i, [512]) for i in range(2)]
        acc = [AR.alloc("acc%d" % i, [512]) for i in range(2)]
        it = 0
        ib = 0
        for oc in range(8):
            wm = wml[oc % 2]
            if 1 <= oc and oc + 1 < 8:
                load_wml(oc + 1)
            for (p0, n) in blocks:
                ac = acc[ib % 2]
                ib += 1
                for n_ in range(4):
                    psA = kb.psum_f()
                    for k in range(2):
                        mm(psA[:, 0:n], wbr[:, n_, k, oc * 128:(oc + 1) * 128], brT[n_][k][:, p0:p0 + n], k == 0, k == 1)
                    psB = kb.psum_f()
                    for k in range(KC):
                        mm(psB[:, 0:n], wm[:, k, n_, :], hT[:, k, p0:p0 + n], k == 0, k == KC - 1)
                    s_ = sg[it % 2]
                    t_ = tm[it % 2]
                    it += 1
                    act(s_[:, 0:n], psB[:, 0:n], AF.Sigmoid)
                    if n_ == 0:
                        tt("dve", ac[:, 0:n], psA[:, 0:n], s_[:, 0:n], ALU.mult)
                    elif n_ < 3:
                        tt("dve", t_[:, 0:n], psA[:, 0:n], s_[:, 0:n], ALU.mult)
                        tt("dve", ac[:, 0:n], ac[:, 0:n], t_[:, 0:n], ALU.add)
                    else:
                        tt("dve", t_[:, 0:n], psA[:, 0:n], s_[:, 0:n], ALU.mult)
                        tt("dve", ypT[:, oc, p0:p0 + n].par("ypt"), ac[:, 0:n], t_[:, 0:n], ALU.add)
        AR.reset()
        gbc = AR.alloc("gbc", [2, D])
        dma("sp", gbc[:, 0, :], modD[l, b, 2 * D:3 * D].pbc(128))
        if with_ctx:
            dma("sp", gbc[:, 1, :], modD[l, 2, 2 * D:3 * D].pbc(128))
        xt = [AR.alloc("xt%d" % i, [D]) for i in range(2)]
        yt = [AR.alloc("yt%d" % i, [D]) for i in range(2)]
        for i in range(0 if with_ctx else 2, 18):
            x_t, y_t = xt[i % 2], yt[i % 2]
            dma("sp", x_t, src_tile(l, b, i))
            for hf in range(2):
                ps = kb.psum_f()
                for k in range(KC):
                    mm(ps, ypT[:, k, i * 128:(i + 1) * 128], wo[:, k, hf * 512:(hf + 1) * 512], k == 0, k == KC - 1)
                tt("dve", y_t[:, hf * 512:(hf + 1) * 512], ps, gbc[:, 1 if i < 2 else 0, hf * 512:(hf + 1) * 512], ALU.mult)
            tt("pool", y_t, y_t, x_t, ALU.add)
            if i < 2:
                dst = (tap("xc1", [nb, TC, D]) if "x1out" in dbg else xcmid)[b, i * 128:(i + 1) * 128, :]
            else:
                dst = (xmid if (l < DEPTH - 1 and "x1out" not in dbg) else y_out)[b, (i - 2) * 128:(i - 1) * 128, :]
            dma("act", dst, y_t)

    def dump_br(name, n, lo):
        o = tap(name, [2, 128, NPOS])
        tmpf = AR.alloc("dump_" + name, [NPOS])
        for c in range(2):
            cp("dve", tmpf[:, lo:NPOS], brT[n][c][:, lo:NPOS])
            dma("sp", o[c, :, lo:NPOS], tmpf[:, lo:NPOS])

    prefetch(layers[0], PF_S5)
    for l in layers:
        with_ctx = l < DEPTH - 1
        prep_layer(l)
        for b in range(nb):
            if "prep_only" in dbg:
                continue
            phase_norm(l, b)
            if "hT" in dbg and b == 0 and l == layers[0]:
                o = tap("hT", [KC, 128, NPOS])
                AR.reset()
                tmpf = AR.alloc("dump_hT", [NPOS])
                for c in range(KC):
                    cp("dve", tmpf, hT[:, c, :])
                    dma("sp", o[c], tmpf)
            only = dbg & {"only_lru", "only_pool", "only_mla", "only_s5", "only_norm"}
            if not only or "only_s5" in only:
                phase_s5(l, b, with_ctx)
            if not only or "only_lru" in only:
                phase_lru(l, b, with_ctx)
            if not only or "only_pool" in only:
                phase_pool(l, b, with_ctx)
            if not only or "only_mla" in only:
                phase_mla(l, b, with_ctx)
            if "br" in dbg and b == 0 and l == layers[0]:
                AR.reset()
                lo = 0 if with_ctx else TC
                for n_, nm in enumerate(("mla", "s5", "lru", "pool")):
                    dump_br(nm, n_, lo)
            if "nomerge" not in dbg:
                if b + 1 < nb:
                    nxt = l
                else:
                    li = list(layers).index(l)
                    nxt = layers[li + 1] if li + 1 < len(layers) else None
                phase_merge(l, b, with_ctx, nxt)
    P.barrier()
    P.emit()
    kb.st.close()
    return nc, kb


def _consts():
    ident = np.eye(128, dtype=np.float32)
    rows_n = T // 64
    row = np.repeat(np.arange(rows_n), 64).astype(np.float32)
    col = np.tile(np.arange(64), rows_n).astype(np.float32)
    nf = 8
    inv = (np.float32(10000.0) ** (-np.arange(nf, dtype=np.float32) / nf)).astype(np.float32)
    ar = (row[:, None] * inv).astype(np.float32)
    ac = (col[:, None] * inv).astype(np.float32)
    cr, sr, cc, sc = np.cos(ar), np.sin(ar), np.cos(ac), np.sin(ac)
    ropeC = np.concatenate([cr, cr, cc, cc], axis=1).astype(np.float32)
    ropeS = np.concatenate([-sr, sr, -sc, sc], axis=1).astype(np.float32)
    cpool = np.ones((128, 34), np.float32)
    for c in range(2):
        for p in range(128):
            w = (2, 4, 8, 16)[2 * c + p // 64]
            hw = w // 2
            cpool[p, c] = 1.0 / w
            for t in range(8):
                cnt = t + hw if t < hw else w
                cpool[p, 2 + c * 8 + t] = w / cnt
            for j in range(8):
                dist = 8 - j
                cnt = dist + hw if dist < hw else w
                cpool[p, 18 + c * 8 + j] = w / cnt
    m = np.zeros((128, 3, 128), np.float32)
    for r in range(8):
        for t in range(8):
            if r <= t:
                m[r * 16:(r + 1) * 16, 0, t * 16:(t + 1) * 16] = 1.0
            if r >= t:
                m[r * 16:(r + 1) * 16, 1, t * 16:(t + 1) * 16] = 1.0
            if r == t:
                m[r * 16:(r + 1) * 16, 2, t * 16:(t + 1) * 16] = np.eye(16, dtype=np.float32)
    return {"c_ident": ident, "c_ropeC": ropeC, "c_ropeS": ropeS, "c_pool": cpool, "c_mask": m}


_WNAMES = ["w_ada", "b_ada", "norm_g", "w_in", "mla_q_norm", "mla_kv_norm", "mla_w_uq", "mla_w_ukv", "mla_q_gain",
           "mla_k_gain", "s5_a_re", "s5_a_im", "s5_log_dt", "s5_b_re", "s5_b_im", "s5_c_re", "s5_c_im", "s5_d",
           "s5_w_glu", "lru_conv_w", "lru_conv_b", "lru_lambda", "lru_w_a", "lru_b_a", "lru_w_x", "lru_b_x",
           "pool_w", "pool_b", "pool_scale", "w_branch", "w_out"]


def make_in_maps(inputs, n_cores, nb):
    consts = _consts()
    maps = []
    for r in range(n_cores):
        bs = slice(r * nb, (r + 1) * nb)
        m = {"x": np.ascontiguousarray(inputs["x"][bs], dtype=np.float32),
             "ctx": np.ascontiguousarray(inputs["ctx"][bs], dtype=np.float32)}
        cv = np.zeros((3, D), np.float32)
        cv[0:nb] = np.asarray(inputs["c"], dtype=np.float32)[bs]
        cv[2] = np.asarray(inputs["c_ctx"], dtype=np.float32)
        m["cvec"] = cv
        for k in _WNAMES:
            m[k] = np.ascontiguousarray(inputs[k], dtype=np.float32)
        m.update(consts)
        maps.append(m)
    return maps


_CACHE = {}


def kernel(**inputs):
    n_cores, nb = 8, 2
    if "nc" not in _CACHE:
        _CACHE["nc"] = build_program(nb=nb)[0]
    nc = _CACHE["nc"]
    maps = make_in_maps(inputs, n_cores, nb)
    res = run_bass_kernel_spmd(nc, maps, core_ids=list(range(n_cores)))
    out = np.concatenate([np.asarray(r["y"], dtype=np.float32) for r in res.results], axis=0)
    return out
```

```python
import math
import contextlib
import numpy as np
import concourse.bass as bass
import concourse.mybir as mybir
from concourse.bass_utils import run_bass_kernel_spmd

F32 = mybir.dt.float32
BF16 = mybir.dt.bfloat16
I32 = mybir.dt.int32
AF = mybir.ActivationFunctionType
ALU = mybir.AluOpType
AX = mybir.AxisListType

ENGS = ("pe", "act", "dve", "pool", "sp")
NDMA = 24

D = 1024
KC = 8
T = 2048
TC = 256
NPOS = T + TC
DEPTH = 2
O_KROPE, O_S5, O_LRU, O_CQ, O_POOL, O_GATE, O_MERGE, IN_W = 128, 160, 416, 672, 928, 1184, 2208, 6304
EPS = 1e-6
NCH = NPOS // 8


PAR_OFF = set()


class Buf:
    __slots__ = ("name", "w", "wp", "r")

    def __init__(self, name):
        self.name = name
        self.w = []
        self.wp = []
        self.r = []


class TV:
    par_ = False

    def __init__(self, ap, buf):
        self.ap, self.buf = ap, buf

    def par(self, tag=""):
        t = TV(self.ap, self.buf)
        t.par_ = tag not in PAR_OFF
        return t

    def __getitem__(self, k):
        return TV(self.ap[k], self.buf)

    def re(self, s, **kw):
        return TV(self.ap.rearrange(s, **kw), self.buf)

    def bc(self, shape):
        return TV(self.ap.to_broadcast(list(shape)), self.buf)

    def us(self, ax):
        return TV(self.ap.unsqueeze(ax), self.buf)

    def pbc(self, n):
        return TV(self.ap.partition_broadcast(n), self.buf)

    @property
    def shape(self):
        return tuple(self.ap.shape)


def _bufs(*xs):
    out = []
    for x in xs:
        if isinstance(x, TV):
            out.append(x.buf)
        elif isinstance(x, (list, tuple)):
            out.extend(_bufs(*x))
    return out


def _ap(x):
    return x.ap if isinstance(x, TV) else x


class Prog:
    def __init__(self, nc):
        self.nc = nc
        self.ops = {e: [] for e in ENGS}
        self.cnt = {e: 0 for e in ENGS}
        self.waited = {e: {} for e in ENGS}
        self.dma_i = 0
        self.dma_last = [0] * NDMA

    def _deps(self, eng, reads, writes, par=False):
        toks = []
        for b in reads:
            toks.extend(b.w)
            toks.extend(b.wp)
        for b in writes:
            toks.extend(b.w)
            if not par:
                toks.extend(b.wp)
            toks.extend(b.r)
        need = {}
        for (k, v, e) in toks:
            if e == "pe" and eng == "pe":
                continue
            if self.waited[eng].get(k, 0) >= v:
                continue
            if need.get(k, 0) < v:
                need[k] = v
        for k, v in need.items():
            self.waited[eng][k] = v
        return list(need.items())

    def _mark(self, tok, reads, writes, par=False):
        for b in writes:
            if par:
                best = {}
                for (k, v, e) in b.wp + [tok]:
                    if k not in best or best[k][1] < v:
                        best[k] = (k, v, e)
                b.wp = list(best.values())
            else:
                b.w = [tok]
                b.wp = []
                b.r = []
        for b in reads:
            if b in writes:
                continue
            b.r.append(tok)
            if len(b.r) > 16:
                best = {}
                for (k, v, e) in b.r:
                    if k not in best or best[k][1] < v:
                        best[k] = (k, v, e)
                b.r = list(best.values())

    def op(self, eng, fn, reads=(), writes=(), par=False):
        reads = list(dict.fromkeys(reads))
        writes = list(dict.fromkeys(writes))
        waits = self._deps(eng, reads, writes, par)
        self.cnt[eng] += 1
        tok = (eng, self.cnt[eng], eng)
        self.ops[eng].append((waits, fn, (eng, 1)))
        self._mark(tok, reads, writes, par)

    def dma(self, eng, fn, reads=(), writes=(), par=False):
        reads = list(dict.fromkeys(reads))
        writes = list(dict.fromkeys(writes))
        waits = self._deps(eng, reads, writes, par)
        slot = self.dma_i % NDMA
        self.dma_i += 1
        key = ("dma", slot)
        prev = self.dma_last[slot]
        if prev and self.waited[eng].get(key, 0) < prev:
            waits.append((key, prev))
            self.waited[eng][key] = prev
        val = prev + 16
        self.dma_last[slot] = val
        tok = (key, val, "dma")
        self.ops[eng].append((waits, fn, (key, 16)))
        self._mark(tok, reads, writes, par)

    def barrier(self):
        for E in ENGS:
            waits = []
            for e in ENGS:
                v = self.cnt[e]
                if v and self.waited[E].get(e, 0) < v:
                    waits.append((e, v))
                    self.waited[E][e] = v
            for slot in range(NDMA):
                v = self.dma_last[slot]
                key = ("dma", slot)
                if v and self.waited[E].get(key, 0) < v:
                    waits.append((key, v))
                    self.waited[E][key] = v
            if waits:
                self.ops[E].append((waits, None, None))

    def emit(self):
        nc = self.nc
        sems = {}
        with contextlib.ExitStack() as st:
            for e in ENGS:
                sems[e] = st.enter_context(nc.semaphore("s_" + e))
            for i in range(NDMA):
                sems[("dma", i)] = st.enter_context(nc.semaphore("s_dma%d" % i))
            block = st.enter_context(nc.Block())

            def run(engname):
                def body(eng):
                    for waits, fn, inc in self.ops[engname]:
                        for k, v in waits:
                            eng.wait_ge(sems[k], v)
                        if fn is None:
                            continue
                        ins = fn(eng)
                        ins.then_inc(sems[inc[0]], inc[1])
                return body
            block.tensor(run("pe"))
            block.scalar(run("act"))
            block.vector(run("dve"))
            block.gpsimd(run("pool"))
            block.sync(run("sp"))


class KB:
    def __init__(self, nc, nb, layers, dbg):
        self.nc = nc
        self.P = Prog(nc)
        self.nb = nb
        self.layers = layers
        self.dbg = dbg
        self.st = contextlib.ExitStack()
        self.din = {}
        self.dout = {}
        self.ps_i = 0
        self.psb_i = 0
        self.ps8_i = 0

    def tt(self, eng, out, a, b, op):
        self.P.op(eng, lambda e: e.tensor_tensor(out=out.ap, in0=a.ap, in1=b.ap, op=op), _bufs(a, b), _bufs(out), par=out.par_)

    def ts(self, eng, out, a, s1, op0, s2=None, op1=None):
        if op1 is None:
            self.P.op(eng, lambda e: e.tensor_scalar(out=out.ap, in0=a.ap, scalar1=_ap(s1), scalar2=None, op0=op0),
                      _bufs(a, s1), _bufs(out), par=out.par_)
        else:
            self.P.op(eng, lambda e: e.tensor_scalar(out=out.ap, in0=a.ap, scalar1=_ap(s1), scalar2=_ap(s2), op0=op0, op1=op1),
                      _bufs(a, s1, s2), _bufs(out), par=out.par_)

    def stt(self, out, a, s, b, op0, op1):
        self.P.op("dve", lambda e: e.scalar_tensor_tensor(out=out.ap, in0=a.ap, scalar=_ap(s), in1=b.ap, op0=op0, op1=op1),
                  _bufs(a, s, b), _bufs(out))

    def act(self, out, a, func, bias=0.0, scale=1.0, accum=None):
        if accum is None:
            self.P.op("act", lambda e: e.activation(out=out.ap, in_=a.ap, func=func, bias=_ap(bias), scale=_ap(scale)),
                      _bufs(a, bias, scale), _bufs(out), par=out.par_)
        else:
            self.P.op("act", lambda e: e.activation(out=out.ap, in_=a.ap, func=func, bias=_ap(bias), scale=_ap(scale), accum_out=accum.ap),
                      _bufs(a, bias, scale), _bufs(out, accum))

    def cp(self, eng, out, a):
        if eng == "act":
            self.P.op("act", lambda e: e.copy(out=out.ap, in_=a.ap), _bufs(a), _bufs(out), par=out.par_)
        else:
            self.P.op(eng, lambda e: e.tensor_copy(out=out.ap, in_=a.ap), _bufs(a), _bufs(out), par=out.par_)

    def memset(self, eng, out, v):
        self.P.op(eng, lambda e: e.memset(out.ap, v), [], _bufs(out), par=out.par_)

    def recip(self, out, a):
        self.P.op("dve", lambda e: e.reciprocal(out=out.ap, in_=a.ap), _bufs(a), _bufs(out))

    def mm(self, out, lhsT, rhs, start, stop):
        self.P.op("pe", lambda e: e.matmul(out.ap, lhsT=lhsT.ap, rhs=rhs.ap, start=start, stop=stop), _bufs(lhsT, rhs), _bufs(out))

    def tr(self, out, a, ident):
        self.P.op("pe", lambda e: e.transpose(out.ap, a.ap, ident.ap), _bufs(a, ident), _bufs(out))

    def scan(self, out, d0, d1, init):
        self.P.op("dve", lambda e: e.tensor_tensor_scan(out=out.ap, data0=d0.ap, data1=d1.ap, initial=_ap(init), op0=ALU.mult, op1=ALU.add),
                  _bufs(d0, d1, init), _bufs(out))

    def reduce(self, out, a, op=ALU.add):
        self.P.op("dve", lambda e: e.tensor_reduce(out=out.ap, in_=a.ap, axis=AX.X, op=op), _bufs(a), _bufs(out))

    def dma(self, q, out, a, slow=False):
        if slow:
            self.P.dma(q, lambda e: e.dma_start(out=out.ap, in_=a.ap, allow_slow_non_contiguous=True), _bufs(a), _bufs(out), par=out.par_)
        else:
            self.P.dma(q, lambda e: e.dma_start(out=out.ap, in_=a.ap), _bufs(a), _bufs(out), par=out.par_)

    def dram_in(self, name, shape, dt=F32):
        t = TV(self.nc.dram_tensor(name, list(shape), dt, kind="ExternalInput").ap(), Buf(name))
        self.din[name] = t
        return t

    def dram_out(self, name, shape, dt=F32):
        t = TV(self.nc.dram_tensor(name, list(shape), dt, kind="ExternalOutput").ap(), Buf(name))
        self.dout[name] = t
        return t

    def dram_tmp(self, name, shape, dt=F32):
        return TV(self.nc.dram_tensor(name, list(shape), dt, kind="Internal").ap(), Buf(name))

    def sb(self, name, shape, dt=F32):
        t = self.st.enter_context(self.nc.sbuf_tensor(name, list(shape), dt))
        return TV(t[:], Buf(name))

    def psum_f(self):
        i = self.ps_i % 6
        self.ps_i += 1
        return self.psf[i]

    def psum_b(self):
        i = self.psb_i % 2
        self.psb_i += 1
        return self.psb[i]

    def psum_any(self, bf16=False):
        i = self.ps8_i % 8
        self.ps8_i += 1
        return self.ps8b[i] if bf16 else self.ps8[i]


class Arena:
    def __init__(self, kb, words, ap=None):
        self.kb = kb
        self.words = words
        if ap is None:
            self.f = kb.st.enter_context(kb.nc.sbuf_tensor("arena_f", [128, words], F32))
        else:
            self.f = ap
        self.lo = 0
        self.hi = words

    def alloc(self, name, free, dt=F32, top=False, buf=None, at=None):
        nel = int(np.prod(free))
        w = nel if dt == F32 else (nel + 1) // 2
        w = ((w + 15) // 16) * 16
        if at is not None:
            off = at
        elif top:
            self.hi -= w
            off = self.hi
        else:
            off = self.lo
            self.lo += w
        assert self.lo <= self.hi, (name, self.lo, self.hi)
        assert off + w <= self.words
        if dt != F32:
            ap = self.f[:, off:off + w].bitcast(dt)[:, 0:nel]
        else:
            ap = self.f[:, off:off + nel]
        if len(free) > 1:
            names = " ".join("d%d" % i for i in range(len(free)))
            kw = {"d%d" % i: int(free[i]) for i in range(len(free))}
            ap = ap.rearrange("p (%s) -> p %s" % (names, names), **kw)
        tv = TV(ap, buf if buf is not None else Buf(name))
        tv.off = off
        tv.w = w
        return tv

    def reset(self, top=False):
        self.kb.P.barrier()
        self.lo = 0
        if top:
            self.hi = self.words


def lockstep(gens):
    gens = list(gens)
    while gens:
        nxt = []
        for g in gens:
            try:
                next(g)
                nxt.append(g)
            except StopIteration:
                pass
        gens = nxt


def build_program(nb=2, layers=(0, 1), dbg=None):
    nc = bass.Bass("TRN2", target_bir_lowering=False)
    kb = KB(nc, nb, layers, dbg)
    P = kb.P
    tt, ts, stt, act, cp, mm, tr, dma = kb.tt, kb.ts, kb.stt, kb.act, kb.cp, kb.mm, kb.tr, kb.dma
    dbg = dbg or set()
    PAR_OFF.clear()
    PAR_OFF.update(x[6:] for x in dbg if x.startswith("nopar_"))

    x_in = kb.dram_in("x", [nb, T, D])
    ctx_in = kb.dram_in("ctx", [nb, TC, D])
    cvec = kb.dram_in("cvec", [3, D])
    w_ada = kb.dram_in("w_ada", [DEPTH, D, 3 * D])
    b_ada = kb.dram_in("b_ada", [DEPTH, 3 * D])
    norm_g = kb.dram_in("norm_g", [DEPTH, D])
    w_in = kb.dram_in("w_in", [DEPTH, D, IN_W])
    mla_q_norm = kb.dram_in("mla_q_norm", [DEPTH, 256])
    mla_kv_norm = kb.dram_in("mla_kv_norm", [DEPTH, 128])
    mla_w_uq = kb.dram_in("mla_w_uq", [DEPTH, 256, 384])
    mla_w_ukv = kb.dram_in("mla_w_ukv", [DEPTH, 128, 512])
    mla_q_gain = kb.dram_in("mla_q_gain", [DEPTH, 96])
    mla_k_gain = kb.dram_in("mla_k_gain", [DEPTH, 96])
    s5_a_re = kb.dram_in("s5_a_re", [DEPTH, 2, 16, 64])
    s5_a_im = kb.dram_in("s5_a_im", [DEPTH, 2, 16, 64])
    s5_log_dt = kb.dram_in("s5_log_dt", [DEPTH, 2, 16])
    s5_b_re = kb.dram_in("s5_b_re", [DEPTH, 2, 16, 64, 16])
    s5_b_im = kb.dram_in("s5_b_im", [DEPTH, 2, 16, 64, 16])
    s5_c_re = kb.dram_in("s5_c_re", [DEPTH, 2, 16, 16, 64])
    s5_c_im = kb.dram_in("s5_c_im", [DEPTH, 2, 16, 16, 64])
    s5_d = kb.dram_in("s5_d", [DEPTH, 256])
    s5_w_glu = kb.dram_in("s5_w_glu", [DEPTH, 256, 256])
    lru_conv_w = kb.dram_in("lru_conv_w", [DEPTH, 4, 256])
    lru_conv_b = kb.dram_in("lru_conv_b", [DEPTH, 256])
    lru_lambda = kb.dram_in("lru_lambda", [DEPTH, 2, 256])
    lru_w_a = kb.dram_in("lru_w_a", [DEPTH, 2, 4, 64, 64])
    lru_b_a = kb.dram_in("lru_b_a", [DEPTH, 2, 256])
    lru_w_x = kb.dram_in("lru_w_x", [DEPTH, 2, 4, 64, 64])
    lru_b_x = kb.dram_in("lru_b_x", [DEPTH, 2, 256])
    pool_w = kb.dram_in("pool_w", [DEPTH, 4, 64, 64])
    pool_b = kb.dram_in("pool_b", [DEPTH, 256])
    pool_scale = kb.dram_in("pool_scale", [DEPTH, 256])
    w_branch = kb.dram_in("w_branch", [DEPTH, 4, 256, D])
    w_out = kb.dram_in("w_out", [DEPTH, D, D])
    c_ident = kb.dram_in("c_ident", [128, 128])
    c_ropeC = kb.dram_in("c_ropeC", [T, 32])
    c_ropeS = kb.dram_in("c_ropeS", [T, 32])
    c_pool = kb.dram_in("c_pool", [128, 2 + 16 + 16])
    c_mask = kb.dram_in("c_mask", [128, 3, 128])
    y_out = kb.dram_out("y", [nb, T, D])
    xmid = kb.dram_tmp("xmid", [nb, T, D])
    xcmid = kb.dram_tmp("xcmid", [nb, TC, D])
    modD = kb.dram_tmp("modD", [DEPTH, 3, 3 * D])
    st_main = kb.dram_tmp("st_main", [3, 128, 2048])
    st_dir = kb.dram_tmp("st_dir", [2, 128, 2048 + 16 * NCH + 8])
    dbg_out = {}

    def tap(name, shape):
        if name not in dbg_out:
            dbg_out[name] = kb.dram_out("dbg_" + name, shape)
        return dbg_out[name]

    kb.ps8 = []
    kb.pspair = []
    for i in range(4):
        t = kb.st.enter_context(nc.psum_tensor("psp%d" % i, [128, 1024], F32))
        kb.pspair.append(t[:])
        for hh in range(2):
            kb.ps8.append(TV(t[:][:, hh * 512:(hh + 1) * 512], Buf("psf%d" % (2 * i + hh))))
    kb.psf = kb.ps8[0:6]
    kb.psb = [TV(kb.ps8[i].ap.bitcast(BF16), kb.ps8[i].buf) for i in (6, 7)]
    kb.ps8b = [TV(kb.ps8[i].ap.bitcast(BF16), kb.ps8[i].buf) for i in range(8)]

    hT = kb.sb("hT", [128, KC, NPOS], BF16)
    bigbr = kb.sb("bigbr", [128, 8 * NPOS], BF16)
    _slot = {0: 0, 2: 2, 3: 4, 1: 6}
    brT = [[TV(bigbr.ap[:, (_slot[n] + c) * NPOS:(_slot[n] + c + 1) * NPOS], Buf("brT%d%d" % (n, c))) for c in range(2)]
           for n in range(4)]
    identf = kb.sb("identf", [128, 128])
    identb = kb.sb("identb", [128, 128], BF16)
    ropeC = kb.sb("ropeC", [128, 16, 32])
    ropeS = kb.sb("ropeS", [128, 16, 32])
    cpool = kb.sb("cpool", [128, 34])
    cmask = kb.sb("cmask", [128, 3, 128])
    modA = kb.sb("modA", [128, 3, KC])
    modS = kb.sb("modS", [128, 3, KC])
    lp = kb.sb("lp", [128, 64])
    lruBD = kb.sb("lruBD", [128, 2, 2, 2, 128])
    poolBD = kb.sb("poolBD", [128, 2, 128])
    wukv = kb.sb("wukv", [128, 512], BF16)
    wuq = kb.sb("wuq", [128, 2, 384], BF16)
    gains = kb.sb("gains", [128, 2, 96])
    wglu = kb.sb("wglu", [128, 2, 256], BF16)
    AR = Arena(kb, 28000)
    wst = kb.sb("wst", [128, KC, 416], BF16)

    def prefetch(l, parts):
        for (d0, c0, n) in parts:
            dma("pool", wst[:, :, d0:d0 + n].par("wst"), w_in[l, :, c0:c0 + n].re("(c p) n -> p c n", p=128))
    PF_S5 = [(0, O_S5, 256)]
    PF_LRU = [(0, O_LRU, 256)]
    PF_POOL = [(0, O_POOL, 256)]
    PF_MLA = [(0, 0, 160), (160, O_CQ, 256)]
    PF_GATE0 = [(0, O_GATE, 128)]
    AR2 = Arena(kb, 3 * NPOS, ap=bigbr.ap[:, 0:6 * NPOS].bitcast(F32))

    dma("sp", identf, c_ident)
    cp("dve", identb, identf)
    dma("sp", ropeC, c_ropeC.re("(i p) f -> p i f", p=128))
    dma("sp", ropeS, c_ropeS.re("(i p) f -> p i f", p=128))
    dma("sp", cpool, c_pool)
    dma("sp", cmask, c_mask)

    LP_CW = 0
    LP_CB = 8
    LP_NSP = 10
    LP_NSP2 = 14
    LP_BA = 18
    LP_BX = 22
    LP_PSC = 26
    LP_PBS = 28
    LP_G = 30
    LP_TMP = 40

    def prep_layer(l):
        AR.reset(top=True)
        cT = AR.alloc("cT", [KC, 3])
        for v in range(3):
            dma("sp", cT[:, :, v], cvec[v].re("(c p) -> p c", p=128), slow=True)
        cact = AR.alloc("cact", [KC, 3])
        act(cact, cT, AF.Silu)
        brow = AR.alloc("brow", [3 * D])
        dma("sp", brow[0:3, :], b_ada[l:l + 1, :].re("o n -> (o n)").pbc(3))
        modrow = AR.alloc("modrow", [3 * D])
        wa = [AR.alloc("wa%d" % i, [KC, 512]) for i in range(4)]
        for cb in range(4):
            for hh in range(2):
                dma("sp" if hh == 0 else "act", wa[cb][:, hh * 4:(hh + 1) * 4, :].par("wa"),
                    w_ada[l, hh * 512:(hh + 1) * 512, cb * 512:(cb + 1) * 512].re("(c p) n -> p c n", p=128))
        for cb in range(6):
            w = wa[cb % 4]
            if cb >= 4:
                for hh in range(2):
                    dma("sp" if hh == 0 else "act", w[:, hh * 4:(hh + 1) * 4, :].par("wa"),
                        w_ada[l, hh * 512:(hh + 1) * 512, cb * 512:(cb + 1) * 512].re("(c p) n -> p c n", p=128))
            ps = kb.psum_f()
            for k in range(KC):
                mm(ps[0:3, :], cact[:, k, :], w[:, k, :], k == 0, k == KC - 1)
            tt("dve", modrow[0:3, cb * 512:(cb + 1) * 512], ps[0:3, :], brow[0:3, cb * 512:(cb + 1) * 512], ALU.add)
        dma("sp", modD[l], modrow[0:3, :])
        sc = AR.alloc("sc", [3, KC])
        for v in range(3):
            dma("sp", modS[:, v, :], modD[l, v, 0:D].re("(c p) -> p c", p=128), slow=True)
            dma("sp", sc[:, v, :], modD[l, v, D:2 * D].re("(c p) -> p c", p=128), slow=True)
        dma("sp", lp[:, LP_G:LP_G + 8], norm_g[l].re("(c p) -> p c", p=128), slow=True)
        for v in range(3):
            stt(modA[:, v, :], sc[:, v, :], 1.0, lp[:, LP_G:LP_G + 8], ALU.add, ALU.mult)
        for k in range(4):
            dma("sp", lp[:, LP_CW:LP_CW + 8].re("p (c k) -> p c k", c=2)[:, :, k], lru_conv_w[l, k].re("(c p) -> p c", p=128), slow=True)
        dma("sp", lp[:, LP_CB:LP_CB + 2], lru_conv_b[l].re("(c p) -> p c", p=128), slow=True)
        lam = lp[:, LP_TMP:LP_TMP + 4]
        for d in range(2):
            dma("sp", lam[:, d * 2:d * 2 + 2], lru_lambda[l, d].re("(c p) -> p c", p=128), slow=True)
            dma("sp", lp[:, LP_BA + d * 2:LP_BA + d * 2 + 2], lru_b_a[l, d].re("(c p) -> p c", p=128), slow=True)
            dma("sp", lp[:, LP_BX + d * 2:LP_BX + d * 2 + 2], lru_b_x[l, d].re("(c p) -> p c", p=128), slow=True)
        t0 = lp[:, LP_TMP + 4:LP_TMP + 8]
        t1 = lp[:, LP_TMP + 8:LP_TMP + 12]
        t2 = lp[:, LP_TMP + 12:LP_TMP + 16]
        t3 = lp[:, LP_TMP + 16:LP_TMP + 20]
        ts("dve", t0, lam, -1.0, ALU.mult)
        tt("dve", t0, t0, lam, ALU.max)
        act(t1, t0, AF.Exp, scale=-1.0)
        ts("dve", t2, t1, 2.0, ALU.add)
        kb.recip(t2, t2)
        tt("dve", t2, t2, t1, ALU.mult)
        tt("dve", t3, t2, t2, ALU.mult)
        ts("dve", t0, t3, 1.0 / 11.0, ALU.mult, 1.0 / 9.0, ALU.add)
        for cf in (1.0 / 7.0, 1.0 / 5.0, 1.0 / 3.0, 1.0):
            tt("dve", t0, t0, t3, ALU.mult)
            ts("dve", t0, t0, cf, ALU.add)
        tt("dve", t0, t0, t2, ALU.mult)
        ts("dve", t1, lam, -1.0, ALU.mult, 0.0, ALU.max)
        stt(t0, t0, 2.0, t1, ALU.mult, ALU.add)
        ts("dve", lp[:, LP_NSP:LP_NSP + 4], t0, -8.0, ALU.mult)
        ts("dve", lp[:, LP_NSP2:LP_NSP2 + 4], t0, -16.0, ALU.mult)
        kb.memset("dve", lruBD, 0.0)
        for d in range(2):
            for gi, wsrc in enumerate((lru_w_a, lru_w_x)):
                for c in range(2):
                    for h in range(2):
                        dma("sp" if (c + h) % 2 == 0 else "act", lruBD[h * 64:(h + 1) * 64, d, gi, c, h * 64:(h + 1) * 64].par("ldw"), wsrc[l, d, 2 * c + h])
        kb.memset("dve", poolBD, 0.0)
        for c in range(2):
            for h in range(2):
                dma("sp", poolBD[h * 64:(h + 1) * 64, c, h * 64:(h + 1) * 64].par("ldw"), pool_w[l, 2 * c + h])
        dma("sp", lp[:, LP_PSC:LP_PSC + 2], pool_scale[l].re("(c p) -> p c", p=128), slow=True)
        dma("sp", lp[:, LP_PBS:LP_PBS + 2], pool_b[l].re("(c p) -> p c", p=128), slow=True)
        tt("dve", lp[:, LP_PBS:LP_PBS + 2], lp[:, LP_PBS:LP_PBS + 2], lp[:, LP_PSC:LP_PSC + 2], ALU.mult)
        kvn = lp[:, LP_TMP + 20:LP_TMP + 21]
        qn = lp[:, LP_TMP + 21:LP_TMP + 23]
        dma("sp", kvn, mla_kv_norm[l].re("(p o) -> p o", o=1), slow=True)
        dma("sp", qn, mla_q_norm[l].re("(c p) -> p c", p=128), slow=True)
        wtmp = AR.alloc("wtmp", [2, 512])
        dma("sp", wtmp[:, 0, :], mla_w_ukv[l])
        ts("dve", wukv, wtmp[:, 0, :], kvn, ALU.mult)
        wtmp2 = AR.alloc("wtmp2", [2, 384])
        dma("sp", wtmp2, mla_w_uq[l].re("(c p) n -> p c n", p=128))
        for c in range(2):
            ts("dve", wuq[:, c, :], wtmp2[:, c, :], qn[:, c:c + 1], ALU.mult)
        dma("sp", gains[:, 0, :], mla_q_gain[l:l + 1, :].re("o n -> (o n)").pbc(128))
        dma("sp", gains[:, 1, :], mla_k_gain[l:l + 1, :].re("o n -> (o n)").pbc(128))
        ts("dve", gains[:, 0, :], gains[:, 0, :], 96.0 ** -0.5, ALU.mult)
        dma("pool", wglu, s5_w_glu[l].re("(c p) n -> p c n", p=128))

    def src_tile(l, b, i):
        if i < 2:
            return (ctx_in if l == 0 else xcmid)[b, i * 128:(i + 1) * 128, :]
        return (x_in if l == 0 else xmid)[b, (i - 2) * 128:(i - 1) * 128, :]

    def phase_norm(l, b):
        AR.reset(top=True)
        NX = 4
        xt = [AR.alloc("xt%d" % i, [D]) for i in range(NX)]
        junk = [AR.alloc("junk%d" % i, [D]) for i in range(NX)]
        xn = [AR.alloc("xn%d" % i, [D], BF16) for i in range(NX)]
        st4 = [AR.alloc("st%d" % i, [4]) for i in range(NX)]
        def tile_gen(i):
            v = 2 if i < 2 else b
            x_t, xn_t, s4 = xt[i % NX], xn[i % NX], st4[i % NX]
            dma("sp" if i % 2 == 0 else "act", x_t, src_tile(l, b, i))
            yield
            act(junk[i % NX], x_t, AF.Square, accum=s4[:, 0:1])
            yield
            ts("dve", s4[:, 1:2], s4[:, 0:1], 1.0 / D, ALU.mult, EPS, ALU.add)
            yield
            act(s4[:, 2:3], s4[:, 1:2], AF.Sqrt)
            yield
            kb.recip(s4[:, 3:4], s4[:, 2:3])
            ts("dve", xn_t, x_t, s4[:, 3:4], ALU.mult)
            yield
            psA = kb.psum_any(bf16=True)
            psB = kb.psum_any(bf16=True)
            for c in range(KC):
                pz = psA if c % 2 == 0 else psB
                tr(pz[:, (c // 2) * 128:(c // 2 + 1) * 128], xn_t[:, c * 128:(c + 1) * 128], identb)
            yield
            for c in range(KC):
                o = hT[:, c, i * 128:(i + 1) * 128].par("norm")
                if c % 2 == 0:
                    act(o, psA[:, (c // 2) * 128:(c // 2 + 1) * 128], AF.Identity, bias=modS[:, v, c:c + 1], scale=modA[:, v, c:c + 1])
                else:
                    ts("dve", o, psB[:, (c // 2) * 128:(c // 2 + 1) * 128], modA[:, v, c:c + 1], ALU.mult, modS[:, v, c:c + 1], ALU.add)
            yield
        for g0 in range(0, 18, NX):
            lockstep([tile_gen(i) for i in range(g0, min(g0 + NX, 18))])

    def load_win(l, name, c0, ncols, top=False):
        w = AR.alloc(name, [KC, ncols], BF16, top=top)
        dma("pool", w, w_in[l, :, c0:c0 + ncols].re("(c p) n -> p c n", p=128))
        return w

    def proj_fm(w, col0, dst, p0, p1, evac):
        pos = p0
        while pos < p1:
            n = min(512, p1 - pos)
            ps = kb.psum_f()
            for k in range(KC):
                mm(ps[:, 0:n], w[:, k, col0:col0 + 128], hT[:, k, pos:pos + n], k == 0, k == KC - 1)
            evac(ps, pos, n)
            pos += n

    def phase_lru(l, b, with_ctx):
        AR.reset()
        w = wst[:, :, 0:256]
        xr_ = [AR.alloc("xr%d" % c, [NPOS]) for c in range(2)]
        xcc = [AR.alloc("xc%d" % c, [NPOS]) for c in range(2)]
        ys_ = [AR.alloc("ysum%d" % c, [NPOS]) for c in range(2)]
        NB_ = 512
        tmp_sets = [{nm: AR.alloc("%s%d" % (nm, q), [NB_]) for nm in ("r", "i", "a", "a2", "bb")} for q in range(4)]
        for c in range(2):
            proj_fm(w, c * 128, None, 0, NPOS, lambda ps, pos, n, c=c: cp("act", xr_[c][:, pos:pos + n].par("proj"), ps[:, 0:n]))
        prefetch(l, PF_POOL)
        for c in range(2):
            cw = lambda k: lp[:, LP_CW + c * 4 + k:LP_CW + c * 4 + k + 1]
            for (s0, s1) in ((0, TC), (TC, NPOS)):
                ts("dve", xcc[c][:, s0:s1], xr_[c][:, s0:s1], cw(2), ALU.mult, lp[:, LP_CB + c:LP_CB + c + 1], ALU.add)
                stt(xcc[c][:, s0 + 1:s1], xr_[c][:, s0:s1 - 1], cw(1), xcc[c][:, s0 + 1:s1], ALU.mult, ALU.add)
                stt(xcc[c][:, s0 + 2:s1], xr_[c][:, s0:s1 - 2], cw(0), xcc[c][:, s0 + 2:s1], ALU.mult, ALU.add)
                stt(xcc[c][:, s0:s1 - 1], xr_[c][:, s0 + 1:s1], cw(3), xcc[c][:, s0:s1 - 1], ALU.mult, ALU.add)
        blocks = [(0, TC)] + [(TC + j * NB_, min(TC + (j + 1) * NB_, NPOS)) for j in range((T + NB_ - 1) // NB_)]

        def chain(d, c):
            dc = d * 2 + c
            tmp = tmp_sets[dc]
            out = ys_[c] if d == 0 else xr_[c]
            order = blocks if d == 0 else [blocks[0]] + blocks[:0:-1]
            prev = None
            for (s0, s1) in order:
                n = s1 - s0
                r_, i_, a_, a2_, bb_ = (tmp[k][:, 0:n] for k in ("r", "i", "a", "a2", "bb"))
                for gi, dst in ((0, r_), (1, i_)):
                    q = 0
                    while q < n:
                        m = min(512, n - q)
                        ps = kb.psum_f()
                        mm(ps[:, 0:m], lruBD[:, d, gi, c, :], xcc[c][:, s0 + q:s0 + q + m], True, True)
                        bcol = (LP_BA if gi == 0 else LP_BX) + dc
                        act(dst[:, q:q + m], ps[:, 0:m], AF.Sigmoid, bias=lp[:, bcol:bcol + 1])
                        q += m
                yield
                act(a_, r_, AF.Exp, scale=lp[:, LP_NSP + dc:LP_NSP + dc + 1])
                act(a2_, r_, AF.Exp, scale=lp[:, LP_NSP2 + dc:LP_NSP2 + dc + 1])
                tt("pool", bb_, i_, xcc[c][:, s0:s1], ALU.mult)
                yield
                ts("dve", a2_, a2_, -1.0, ALU.mult, 1.0, ALU.add)
                yield
                act(a2_, a2_, AF.Sqrt)
                yield
                tt("dve", bb_, bb_, a2_, ALU.mult)
                init = 0.0 if prev is None else prev
                if d == 0:
                    kb.scan(out[:, s0:s1], a_, bb_, init)
                    prev = out[:, s1 - 1:s1]
                else:
                    kb.scan(out[:, s0:s1][:, ::-1], a_[:, ::-1], bb_[:, ::-1], init)
                    prev = out[:, s0:s0 + 1]
                yield
        lockstep([chain(0, 0), chain(1, 0), chain(0, 1), chain(1, 1)])
        for c in range(2):
            lo = 0 if with_ctx else TC
            tt("dve", brT[2][c][:, lo:NPOS], ys_[c][:, lo:NPOS], xr_[c][:, lo:NPOS], ALU.add)

    AR_car = [kb.sb("car%d" % i, [128, 1]) for i in range(2)]

    def phase_pool(l, b, with_ctx):
        AR.reset()
        w = wst[:, :, 0:256]
        xp = AR.alloc("xp", [2, NPOS])
        cs_ = [AR.alloc("cs%d" % c, [NPOS + 2 * 17 + 2]) for c in range(2)]
        pm_ = [AR.alloc("pm%d" % c, [NPOS]) for c in range(2)]
        ones = AR.alloc("ones", [T])
        kb.memset("pool", ones, 1.0)
        for c in range(2):
            proj_fm(w, c * 128, xp, 0, NPOS, lambda ps, pos, n, c=c: cp("act", xp[:, c, pos:pos + n].par("proj"), ps[:, 0:n]))
        prefetch(l, PF_MLA)
        segs = [(0, TC, 0)] + [(TC, NPOS, TC + 17)]
        if not with_ctx:
            segs = segs[1:]
        def chunk_gen(c):
            for (s0, s1, o0) in segs:
                L = s1 - s0
                kb.memset("pool", cs_[c][:, o0:o0 + 9], 0.0)
                kb.scan(cs_[c][:, o0 + 9:o0 + 9 + L], ones[:, 0:L], xp[:, c, s0:s1], 0.0)
                yield
                ts("dve", cs_[c][:, o0 + 9 + L:o0 + 17 + L], cs_[c][:, o0:o0 + 8], cs_[c][:, o0 + 8 + L:o0 + 9 + L], ALU.add)
                yield
                for h in range(2):
                    hw = (1, 2, 4, 8)[2 * c + h]
                    rows = slice(h * 64, (h + 1) * 64)
                    base = o0 + 8
                    tt("dve", pm_[c][rows, s0:s1], cs_[c][rows, base + hw:base + hw + L], cs_[c][rows, base - hw:base - hw + L], ALU.subtract)
                yield
                ts("dve", pm_[c][:, s0:s1], pm_[c][:, s0:s1], cpool[:, c:c + 1], ALU.mult)
                yield
                tt("dve", pm_[c][:, s0:s0 + 8], pm_[c][:, s0:s0 + 8], cpool[:, 2 + c * 8:2 + c * 8 + 8], ALU.mult)
                tt("dve", pm_[c][:, s1 - 8:s1], pm_[c][:, s1 - 8:s1], cpool[:, 18 + c * 8:18 + c * 8 + 8], ALU.mult)
                yield
                tt("pool", pm_[c][:, s0:s1], pm_[c][:, s0:s1], xp[:, c, s0:s1], ALU.subtract)
                yield
                pos = s0
                while pos < s1:
                    n = min(512, s1 - pos)
                    ps = kb.psum_f()
                    mm(ps[:, 0:n], poolBD[:, c, :], pm_[c][:, pos:pos + n], True, True)
                    act(brT[3][c][:, pos:pos + n].par("pool"), ps[:, 0:n], AF.Identity, bias=lp[:, LP_PBS + c:LP_PBS + c + 1],
                        scale=lp[:, LP_PSC + c:LP_PSC + c + 1])
                    pos += n
                    yield
        lockstep([chunk_gen(0), chunk_gen(1)])

    def phase_mla(l, b, with_ctx):
        AR.reset()
        wkv = wst[:, :, 0:160]
        wq = wst[:, :, 160:416]
        qT = AR.alloc("qT", [4, NPOS], BF16)
        kT = AR.alloc("kT", [4, NPOS], BF16)
        Vt = AR.alloc("Vt", [18, 4, 68], BF16)
        lo_fixed = AR.lo
        kb.memset("dve", Vt.re("p a b c -> p (a b c)"), 1.0)
        for h_ in range(4):
            kb.memset("pool", qT[:, h_, :], 0.0)
            kb.memset("pool", kT[:, h_, :], 0.0)
        NS = 3
        sm = [AR.alloc("sm%d" % i, [32]) for i in range(NS)]
        kr = [AR.alloc("kr%d" % i, [32]) for i in range(NS)]
        cn = [AR.alloc("cn%d" % i, [384], BF16) for i in range(NS)]
        cnT = [AR.alloc("cnT%d" % i, [3, 128], BF16) for i in range(NS)]
        sq = [AR.alloc("sq%d" % i, [704]) for i in range(NS)]
        qk = [AR.alloc("qk%d" % i, [2, 4, 96]) for i in range(NS)]
        rg = [AR.alloc("rg%d" % i, [2, 4, 96]) for i in range(NS)]
        qkb = [AR.alloc("qkb%d" % i, [2, 4, 96], BF16) for i in range(NS)]
        rt = [AR.alloc("rt%d" % i, [2, 2, 4, 32]) for i in range(NS)]
        kvs_ = [AR.alloc("kvs%d" % i, [512]) for i in range(NS)]
        qs_ = [AR.alloc("qs%d" % i, [384]) for i in range(NS)]

        def info(i):
            is_ctx = i < 2
            do_q = (not is_ctx) or with_ctx
            return is_ctx, do_q, i % NS, slice(i * 128, (i + 1) * 128)

        def stage_a(i):
            is_ctx, do_q, j, pos = info(i)
            s, cn_, cnT_, sq_ = sm[j], cn[j], cnT[j], sq[j]
            ps1 = kb.psum_any()
            for k in range(KC):
                mm(ps1[:, 0:160], hT[:, k, pos], wkv[:, k, :], k == 0, k == KC - 1)
            if do_q:
                for k in range(KC):
                    mm(ps1[:, 160:416], hT[:, k, pos], wq[:, k, :], k == 0, k == KC - 1)
            yield
            act(sq_[:, 0:128], ps1[:, 0:128], AF.Square, accum=s[:, 0:1])
            if do_q:
                act(sq_[:, 128:384], ps1[:, 160:416], AF.Square, accum=s[:, 1:2])
            else:
                kb.memset("pool", s[:, 1:2], 1.0)
            cp("act", kr[j], ps1[:, 128:160])
            yield
            act(s[:, 2:3], s[:, 0:1], AF.Sqrt, scale=1.0 / 128, bias=EPS)
            act(s[:, 3:4], s[:, 1:2], AF.Sqrt, scale=1.0 / 256, bias=EPS)
            yield
            kb.recip(s[:, 4:6], s[:, 2:4])
            ts("dve", cn_[:, 0:128], ps1[:, 0:128], s[:, 4:5], ALU.mult)
            if do_q:
                ts("dve", cn_[:, 128:384], ps1[:, 160:416], s[:, 5:6], ALU.mult)
            yield
            psT = kb.psum_any(bf16=True)
            for c in range(3 if do_q else 1):
                tr(psT[:, c * 128:(c + 1) * 128], cn_[:, c * 128:(c + 1) * 128], identb)
            yield
            cp("act", cnT_[:, 0:(3 if do_q else 1), :], psT[:, 0:(384 if do_q else 128)].re("p (c n) -> p c n", n=128))
            yield

        def stage_b(i):
            is_ctx, do_q, j, pos = info(i)
            s, cnT_, sq_, qk_, qkb_, rt_, rg_ = sm[j], cnT[j], sq[j], qk[j], qkb[j], rt[j], rg[j]
            pskv = kb.psum_any()
            mm(pskv, cnT_[:, 0, :], wukv, True, True)
            if do_q:
                psq = kb.psum_any()
                for c in range(2):
                    mm(psq[:, 0:384], cnT_[:, 1 + c, :], wuq[:, c, :], c == 0, c == 1)
            yield
            cp("act", kvs_[j], pskv)
            if do_q:
                cp("act", qs_[j], psq[:, 0:384])
            kv3 = kvs_[j].re("p (h e) -> p h e", h=4)
            q3 = qs_[j].re("p (h e) -> p h e", h=4)
            krope = kr[j]
            act(sq_[:, 640:672], krope, AF.Square, accum=s[:, 7:8])
            cp("act", Vt[:, i, :, 0:64].par("mlav"), kv3[:, :, 64:128])
            yield
            ksq = sq_[:, 0:256].re("p (h e) -> p h e", h=4)
            qsq = sq_[:, 256:640].re("p (h e) -> p h e", h=4)
            tt("pool", ksq, kv3[:, :, 0:64], kv3[:, :, 0:64], ALU.mult)
            if do_q:
                tt("pool", qsq, q3, q3, ALU.mult)
            yield
            kb.reduce(s[:, 12:16], ksq)
            if do_q:
                kb.reduce(s[:, 8:12], qsq)
            else:
                kb.memset("dve", s[:, 8:12], 1.0)
            ts("dve", s[:, 12:16], s[:, 12:16], s[:, 7:8], ALU.add)
            yield
            act(s[:, 8:16], s[:, 8:16], AF.Sqrt, scale=1.0 / 96, bias=EPS)
            yield
            kb.recip(s[:, 16:24], s[:, 8:16])
            yield
            tt("pool", rg_, s[:, 16:24].re("p (w h) -> p w h", w=2).us(3).bc([128, 2, 4, 96]),
               gains.us(2).bc([128, 2, 4, 96]), ALU.mult)
            yield
            if do_q:
                tt("dve", qk_[:, 0].par("qk"), q3, rg_[:, 0], ALU.mult)
            else:
                kb.memset("pool", qk_[:, 0].par("qk"), 0.0)
            tt("dve", qk_[:, 1, :, 0:64].par("qk"), kv3[:, :, 0:64], rg_[:, 1, :, 0:64], ALU.mult)
            tt("pool", qk_[:, 1, :, 64:96].par("qk"), krope.us(1).bc([128, 4, 32]), rg_[:, 1, :, 64:96], ALU.mult)
            yield
            if not is_ctx:
                ti = i - 2
                v = qk_[:, :, :, 64:96]
                t1_ = rt_[:, 0].par("rt")
                t2_ = rt_[:, 1]
                Cb = ropeC[:, ti, :].us(1).us(1).bc([128, 2, 4, 32])
                tt("pool", t1_, v, Cb, ALU.mult)
                for a in range(2):
                    for s_ in range(2):
                        o_ = t2_[:, :, :, a * 16 + s_ * 8:a * 16 + s_ * 8 + 8].par("rt")
                        i_ = qk_[:, :, :, 64 + a * 16 + (1 - s_) * 8:64 + a * 16 + (1 - s_) * 8 + 8]
                        Sb = ropeS[:, ti, a * 16 + s_ * 8:a * 16 + s_ * 8 + 8].us(1).us(1).bc([128, 2, 4, 8])
                        tt("dve" if (a + s_) % 2 == 0 else "pool", o_, i_, Sb, ALU.mult)
                yield
                tt("dve", v, t1_, t2_, ALU.add)
            cp("dve", qkb_, qk_)
            yield

        def stage_c(i):
            is_ctx, do_q, j, pos = info(i)
            qkb_ = qkb[j]
            psT2 = kb.psum_any(bf16=True)
            for w_ in range(2):
                if w_ == 0 and not do_q:
                    continue
                for h in range(4):
                    tr(psT2[0:96, (w_ * 4 + h) * 128:(w_ * 4 + h + 1) * 128], qkb_[:, w_, h, :], identb)
            yield
            if do_q:
                cp("act", qT[0:96, :, pos].par("mlaqk"), psT2[0:96, 0:512].re("p (h n) -> p h n", h=4))
            cp("act", kT[0:96, :, pos].par("mlaqk"), psT2[0:96, 512:1024].re("p (h n) -> p h n", h=4))

        def tile_gen(i):
            yield from stage_a(i)
            yield from stage_b(i)
            yield from stage_c(i)
        for g0 in range(0, 18, NS):
            lockstep([tile_gen(i) for i in range(g0, g0 + NS)])
        if "mla_stop1" in dbg:
            return
        prefetch(l, PF_GATE0)
        P.barrier()
        AR.lo = lo_fixed
        omla = AR.alloc("omla", [18, 256], BF16)
        pT = [AR.alloc("pT%d" % i, [2, 512], BF16) for i in range(3)]
        rc = [AR.alloc("rc%d" % i, [4]) for i in range(2)]
        jobs = []
        if with_ctx:
            jobs.append((0, TC, 0, 2))
        for qb in range(4):
            jobs.append((TC + qb * 512, TC + (qb + 1) * 512, 0, 18))
        pairs = []
        for (q0, q1, kt0, kt1) in jobs:
            for h in range(4):
                for kt in range(kt0, kt1, 2):
                    pairs.append((q0, q1, kt0, kt1, h, kt))
        SB = (0, 3)

        def s_mm(pi_):
            q0, q1, kt0, kt1, h, kt = pairs[pi_]
            tsel = SB[pi_ % 2]
            for j in range(2):
                mm(kb.ps8[2 * tsel + j][:, 0:q1 - q0], kT[:, h, (kt + j) * 128:(kt + j + 1) * 128], qT[:, h, q0:q1], True, True)
        s_mm(0)
        for pi_, (q0, q1, kt0, kt1, h, kt) in enumerate(pairs):
            nq = q1 - q0
            nqt = nq // 128
            pso = [kb.psf[2 + t_] for t_ in range(nqt)]
            if pi_ + 1 < len(pairs):
                s_mm(pi_ + 1)
            tsel = SB[pi_ % 2]
            p_ = pT[pi_ % 3]
            src = kb.pspair[tsel].rearrange("p (j n) -> p j n", j=2)[:, :, 0:nq]
            dst = p_.ap[:, :, 0:nq]
            P.op("act", lambda e, src=src, dst=dst: e.activation(out=dst, in_=src, func=AF.Exp, bias=0.0, scale=1.0),
                 [kb.ps8[2 * tsel].buf, kb.ps8[2 * tsel + 1].buf], [p_.buf])
            for j in range(2):
                for t_ in range(nqt):
                    mm(pso[t_][:, 0:68], p_[:, j, t_ * 128:(t_ + 1) * 128], Vt[:, kt + j, h, :], kt + j == kt0, kt + j == kt1 - 1)
            if kt + 2 >= kt1:
                for t_ in range(nqt):
                    r_ = rc[t_ % 2]
                    kb.recip(r_[:, 0:1], pso[t_][:, 64:65])
                    ts("dve", omla[:, (q0 // 128) + t_, h * 64:(h + 1) * 64].par("omla"), pso[t_][:, 0:64], r_[:, 0:1], ALU.mult)
        for i in range(0 if with_ctx else 2, 18):
            psT = kb.psum_b()
            for c in range(2):
                tr(psT[:, c * 128:(c + 1) * 128], omla[:, i, c * 128:(c + 1) * 128], identb)
            for c in range(2):
                cp("act", brT[0][c][:, i * 128:(i + 1) * 128].par("brt0"), psT[:, c * 128:(c + 1) * 128])

    def phase_s5(l, b, with_ctx):
        AR.reset()
        AR2.lo = 0
        U = AR.alloc("U", [16, NCH])
        SN = [[AR.alloc("SN%d%d" % (d, ri), [8, NCH + 2]) for ri in range(2)] for d in range(2)]
        Ere = AR.alloc("Ere", [2, 8, 128])
        nEim = AR.alloc("nEim", [2, 8, 128])
        W3 = AR.alloc("W3", [16, 128])
        scr0 = AR.lo
        W1 = [AR2.alloc("W1%d" % i, [16, 64]) for i in range(2)]
        tab = [AR2.alloc("tab%d" % i, [8, NCH]) for i in range(2)]
        cached = (b > 0) and ("s5_nocache" not in dbg)

        def load_dir(d):
            dma("sp", W1[0].re("p g n -> p (g n)"), st_dir[d, :, 0:1024])
            dma("sp", W1[1].re("p g n -> p (g n)"), st_dir[d, :, 1024:2048])
            dma("act", tab[0].re("p g n -> p (g n)"), st_dir[d, :, 2048:2048 + 8 * NCH])
            dma("act", tab[1].re("p g n -> p (g n)"), st_dir[d, :, 2048 + 8 * NCH:2048 + 16 * NCH])
        if cached:
            dma("sp", Ere.re("p d g n -> p (d g n)"), st_main[0])
            dma("act", nEim.re("p d g n -> p (d g n)"), st_main[1])
            dma("sp", W3.re("p g n -> p (g n)"), st_main[2])
            load_dir(0)
        ws5 = wst[:, :, 0:256]
        Uc = AR.alloc("Uc", [16, 8, 16])
        kblocks = [(0, 32, 0), (32, 128, TC), (160, 128, TC + 1024)]
        for (k0, M, p0) in kblocks:
            pss = [kb.psum_f() for _ in range(4)]
            for r in range(8):
                ps = pss[r // 2]
                for k in range(KC):
                    mm(ps[0:M, (r % 2) * 256:(r % 2 + 1) * 256], hT[:, k, p0 + r:p0 + 8 * M:8], ws5[:, k, :], k == 0, k == KC - 1)
            for q in range(4):
                src = pss[q][0:M, :].re("p (r g i) -> p r g i", r=2, g=16)
                dst = Uc[0:M, :, 2 * q:2 * q + 2, :].re("p g r i -> p r g i").par("s5uc")
                cp("act" if q % 2 == 0 else "dve", dst, src)
            for gq in range(4):
                ps = kb.psum_f()
                for gg_ in range(4):
                    g = gq * 4 + gg_
                    tr(ps[:, gg_ * 128:gg_ * 128 + M], Uc[0:M, g, :, :].re("p r i -> p (r i)"), identf[0:M, 0:M])
                cp("act" if gq % 2 == 0 else "dve", U[:, gq * 4:gq * 4 + 4, k0:k0 + M].par("s5u"),
                   ps.re("p (g n) -> p g n", g=4)[:, :, 0:M])
        prefetch(l, PF_LRU)
        P.barrier()
        AR.lo = scr0
        if "s5_stop1" in dbg:
            return
        if not cached:
            kb.memset("pool", W3, 0.0)
        prm = AR.alloc("prm", [40, 8])
        PW = [AR.alloc("pw%d" % i, [10, 8]) for i in range(2)]
        Bb = [AR.alloc("Bb%d" % i, [8, 16]) for i in range(2)]
        Braw = [AR.alloc("Braw%d" % i, [8, 16]) for i in range(2)]
        Craw = [AR.alloc("Craw%d" % i, [8, 16]) for i in range(2)]
        Dbc = AR.alloc("Dbc", [256])
        t8 = AR.alloc("t8", [8, 16])
        w3t = [AR.alloc("w3t%d" % i, [128]) for i in range(2)]
        tE = AR.alloc("tE", [8, 128])
        Fm = [AR.alloc("F%d" % i, [8, 8, 16]) for i in range(2)]
        Ep = [AR.alloc("Ep%d" % i, [8, 128]) for i in range(2)]
        Cnat = [AR.alloc("Cnat%d" % i, [16, 64], at=Ep[i].off, buf=Ep[i].buf) for i in range(2)]
        rs_sets = [[AR.alloc("rs%d_%d" % (q, i), [NCH], at=Fm[0].off + (q * 7 + i) * NCH) for i in range(6)]
                   for q in range(2)]
        rho_sets = [AR.alloc("rho1_%d" % q, [NCH], at=Fm[0].off + (q * 7 + 6) * NCH) for q in range(2)]
        assert Fm[0].off + 14 * NCH <= Ep[1].off + Ep[1].w
        dma("sp", Dbc, s5_d[l:l + 1, :].re("o n -> (o n)").pbc(128))
        dsel = cmask[:, 2, :]

        prm_bufs = [Buf("prm%d" % i) for i in range(40)]
        pw_bufs = [[Buf("pw%d_%d" % (ri, j)) for j in range(10)] for ri in range(2)]

        def pv(i):
            return TV(prm.ap[:, i, :], prm_bufs[i])

        def pw(ri, j):
            return TV(PW[ri].ap[:, j, :], pw_bufs[ri][j])

        for d in range(2):
            if d == 1 and not cached:
                P.barrier()
            if not cached:
                dma("sp", pv(0), s5_a_re[l, d].re("(gp h) p -> (h p) gp", h=2), slow=True)
                dma("act", pv(1), s5_a_im[l, d].re("(gp h) p -> (h p) gp", h=2), slow=True)
                ldt = t8[:, 0, :]
                dma("sp", ldt, s5_log_dt[l, d:d + 1, :].re("o g -> (o g)").pbc(128))
                dma("sp", Cnat[0][0:16], s5_c_re[l, d].re("g o p -> o g p"))
                dma("act", Cnat[1][0:16], s5_c_im[l, d].re("g o p -> o g p"))
                for h in range(2):
                    rows = slice(h * 64, (h + 1) * 64)
                    cp("dve", pv(2)[rows], ldt[rows, h:16:2])
                    dma("sp", Braw[0][rows].par("ldw"), s5_b_re[l, d].re("(gp h) p i -> h p gp i", h=2)[h])
                    dma("act", Braw[1][rows].par("ldw"), s5_b_im[l, d].re("(gp h) p i -> h p gp i", h=2)[h])
                for ri in range(2):
                    ps = kb.psum_f()
                    for g in range(16):
                        gp, h = g // 2, g % 2
                        mm(ps[h * 64:(h + 1) * 64, gp * 16:(gp + 1) * 16], Cnat[ri][0:16, g, :], identf[0:16, 0:16], True, True)
                    cp("act", Craw[ri], ps[:, 0:128].re("p (g o) -> p g o", g=8))
                if "s5_b1" in dbg:
                    return
                act(pv(2), pv(2), AF.Exp)
                tt("dve", pv(3), pv(0), pv(2), ALU.mult)
                tt("dve", pv(4), pv(1), pv(2), ALU.mult)
                act(pv(5), pv(3), AF.Exp)
                for (dst, shift) in ((6, 0.0), (7, math.pi / 2)):
                    xx, kk, ki = pv(30), pv(31), pv(32)
                    ts("dve", xx, pv(4), shift, ALU.add)
                    ts("dve", kk, xx, 1.0 / (2 * math.pi), ALU.mult)
                    kint = TV(ki.ap.bitcast(I32), ki.buf)
                    cp("dve", kint, kk)
                    cp("dve", kk, kint)
                    stt(xx, kk, -6.28125, xx, ALU.mult, ALU.add)
                    stt(xx, kk, -(2 * math.pi - 6.28125), xx, ALU.mult, ALU.add)
                    ts("dve", xx, xx, math.pi, ALU.min, -math.pi, ALU.max)
                    act(pv(dst), xx, AF.Sin)
                kb.memset("dve", pw(0, 0), 1.0)
                kb.memset("dve", pw(1, 0), 0.0)
                tt("dve", pw(0, 1), pv(5), pv(7), ALU.mult)
                tt("dve", pw(1, 1), pv(5), pv(6), ALU.mult)
                for j in range(2, 9):
                    tt("dve", pv(30), pw(0, j - 1), pw(0, 1), ALU.mult)
                    tt("dve", pv(31), pw(1, j - 1), pw(1, 1), ALU.mult)
                    tt("dve", pw(0, j), pv(30), pv(31), ALU.subtract)
                    tt("dve", pv(30), pw(0, j - 1), pw(1, 1), ALU.mult)
                    tt("dve", pv(31), pw(1, j - 1), pw(0, 1), ALU.mult)
                    tt("dve", pw(1, j), pv(30), pv(31), ALU.add)
                act(pv(8), pv(3), AF.Exp, scale=-16.0)
                tt("dve", pw(0, 9), pw(0, 8), pv(8), ALU.mult)
                tt("dve", pw(1, 9), pw(1, 8), pv(8), ALU.mult)
                ts("dve", pw(1, 9), pw(1, 9), -1.0, ALU.mult)
                act(pv(9), pv(3), AF.Exp, scale=8.0)
                act(pv(10), pv(3), AF.Exp, scale=-8.0)
                tt("dve", pv(11), pw(0, 8), pv(10), ALU.mult)
                tt("dve", pv(12), pw(1, 8), pv(10), ALU.mult)
                ts("dve", pv(13), pw(0, 1), -1.0, ALU.add)
                tt("dve", pv(14), pv(0), pv(0), ALU.mult)
                tt("dve", pv(15), pv(1), pv(1), ALU.mult)
                tt("dve", pv(14), pv(14), pv(15), ALU.add)
                kb.recip(pv(14), pv(14))
                tt("dve", pv(15), pv(13), pv(0), ALU.mult)
                tt("dve", pv(16), pw(1, 1), pv(1), ALU.mult)
                tt("dve", pv(15), pv(15), pv(16), ALU.add)
                tt("dve", pv(15), pv(15), pv(14), ALU.mult)
                tt("dve", pv(16), pw(1, 1), pv(0), ALU.mult)
                tt("dve", pv(17), pv(13), pv(1), ALU.mult)
                tt("dve", pv(16), pv(16), pv(17), ALU.subtract)
                tt("dve", pv(16), pv(16), pv(14), ALU.mult)
                fre = pv(15).us(2).bc([128, 8, 16])
                fim = pv(16).us(2).bc([128, 8, 16])
                tt("dve", Bb[0], Braw[0], fre, ALU.mult)
                tt("dve", t8, Braw[1], fim, ALU.mult)
                tt("dve", Bb[0], Bb[0], t8, ALU.subtract)
                tt("dve", Bb[1], Braw[1], fre, ALU.mult)
                tt("dve", t8, Braw[0], fim, ALU.mult)
                tt("dve", Bb[1], Bb[1], t8, ALU.add)
                for t_ in range(8):
                    f_ = t_ + 1 if d == 0 else 8 - t_
                    pr = pw(0, f_).us(2).bc([128, 8, 16])
                    pi_ = pw(1, f_).us(2).bc([128, 8, 16])
                    eo = Ere[:, d, :, t_ * 16:(t_ + 1) * 16]
                    ei = nEim[:, d, :, t_ * 16:(t_ + 1) * 16]
                    tt("dve", eo, Craw[0], pr, ALU.mult)
                    tt("dve", t8, Craw[1], pi_, ALU.mult)
                    tt("dve", eo, eo, t8, ALU.subtract)
                    tt("dve", ei, Craw[0], pi_, ALU.mult)
                    tt("dve", t8, Craw[1], pr, ALU.mult)
                    tt("dve", ei, ei, t8, ALU.add)
                    ts("dve", ei, ei, -1.0, ALU.mult)
                for r in range(8):
                    e_ = 7 - r if d == 0 else r
                    pr = pw(0, e_).us(2).bc([128, 8, 16])
                    pi_ = pw(1, e_).us(2).bc([128, 8, 16])
                    tt("dve", Fm[0][:, :, r, :], Bb[0], pr, ALU.mult)
                    tt("dve", t8, Bb[1], pi_, ALU.mult)
                    tt("dve", Fm[0][:, :, r, :], Fm[0][:, :, r, :], t8, ALU.subtract)
                    tt("dve", Fm[1][:, :, r, :], Bb[1], pr, ALU.mult)
                    tt("dve", t8, Bb[0], pi_, ALU.mult)
                    tt("dve", Fm[1][:, :, r, :], Fm[1][:, :, r, :], t8, ALU.add)
                qr = pw(0, 9).us(2).bc([128, 8, 128])
                qi = pw(1, 9).us(2).bc([128, 8, 128])
                tt("dve", Ep[0], Ere[:, d], qr, ALU.mult)
                tt("dve", tE, nEim[:, d], qi, ALU.mult)
                tt("dve", Ep[0], Ep[0], tE, ALU.add)
                tt("dve", Ep[1], nEim[:, d], qr, ALU.mult)
                tt("dve", tE, Ere[:, d], qi, ALU.mult)
                tt("dve", Ep[1], Ep[1], tE, ALU.subtract)
                if "s5_b2" in dbg:
                    return
                for ri in range(2):
                    for gq in range(4):
                        ps = kb.psum_f()
                        for gg_ in range(4):
                            g = gq * 4 + gg_
                            gp, h = g // 2, g % 2
                            rows = slice(h * 64, (h + 1) * 64)
                            mm(ps[:, gg_ * 64:(gg_ + 1) * 64], Fm[ri][:, gp].re("p r i -> p (r i)"), identf[:, h * 64:(h + 1) * 64], True, True)
                        cp("act", W1[ri][:, gq * 4:gq * 4 + 4, :], ps[:, 0:256].re("p (g n) -> p g n", g=4))
                if "s5_b3" in dbg:
                    return
                for g in range(16):
                    gp, h = g // 2, g % 2
                    rows = slice(h * 64, (h + 1) * 64)
                    ps = kb.psf[h * 2 + (g // 2) % 2]
                    mm(ps[:, 0:128], Fm[0][rows, gp].re("p r i -> p (r i)"), Ep[0][rows, gp, :], True, False)
                    mm(ps[:, 0:128], Fm[1][rows, gp].re("p r i -> p (r i)"), Ep[1][rows, gp, :], False, True)
                    wt = w3t[g % 2]
                    tt("dve", wt, ps[:, 0:128], cmask[:, d, :], ALU.mult)
                    tt("pool", W3[:, g, :], W3[:, g, :], wt, ALU.add)
                if d == 0:
                    for g in range(16):
                        dcol = Dbc[:, g * 16:(g + 1) * 16].us(1).bc([128, 8, 16])
                        wt = w3t[g % 2]
                        tt("dve", wt.re("p (t o) -> p t o", t=8), dsel.re("p (t o) -> p t o", t=8), dcol, ALU.mult)
                        tt("pool", W3[:, g, :], W3[:, g, :], wt, ALU.add)
                if "s5_b4" in dbg:
                    return
                kb.memset("dve", tab[0][:, :, 0:1], 1.0)
                kb.memset("dve", tab[1][:, :, 0:1], 0.0)
                cp("dve", tab[0][:, :, 1:2], pv(11).us(2))
                cp("dve", tab[1][:, :, 1:2], pv(12).us(2))
                n = 2
                while n < NCH:
                    m = min(n, NCH - n)
                    tt("dve", pv(30), tab[0][:, :, n - 1], pv(11), ALU.mult)
                    tt("dve", pv(31), tab[1][:, :, n - 1], pv(12), ALU.mult)
                    tt("dve", pv(33), pv(30), pv(31), ALU.subtract)
                    tt("dve", pv(30), tab[0][:, :, n - 1], pv(12), ALU.mult)
                    tt("dve", pv(31), tab[1][:, :, n - 1], pv(11), ALU.mult)
                    tt("dve", pv(34), pv(30), pv(31), ALU.add)
                    Pr = pv(33).us(2).bc([128, 8, m])
                    Pi = pv(34).us(2).bc([128, 8, m])
                    ta = tE[:, :, 0:m]
                    tt("dve", tab[0][:, :, n:n + m], tab[0][:, :, 0:m], Pr, ALU.mult)
                    tt("dve", ta, tab[1][:, :, 0:m], Pi, ALU.mult)
                    tt("dve", tab[0][:, :, n:n + m], tab[0][:, :, n:n + m], ta, ALU.subtract)
                    tt("dve", tab[1][:, :, n:n + m], tab[0][:, :, 0:m], Pi, ALU.mult)
                    tt("dve", ta, tab[1][:, :, 0:m], Pr, ALU.mult)
                    tt("dve", tab[1][:, :, n:n + m], tab[1][:, :, n:n + m], ta, ALU.add)
                    n *= 2
                dma("sp", st_dir[d, :, 0:1024], W1[0].re("p g n -> p (g n)"))
                dma("sp", st_dir[d, :, 1024:2048], W1[1].re("p g n -> p (g n)"))
                dma("act", st_dir[d, :, 2048:2048 + 8 * NCH], tab[0].re("p g n -> p (g n)"))
                dma("act", st_dir[d, :, 2048 + 8 * NCH:2048 + 16 * NCH], tab[1].re("p g n -> p (g n)"))
                dma("sp", st_dir[d, :, 2048 + 16 * NCH:2048 + 16 * NCH + 8], pv(9))
            else:
                if d == 1:
                    load_dir(1)
                dma("sp", pv(9), st_dir[d, :, 2048 + 16 * NCH:2048 + 16 * NCH + 8])
            if "s5_stop2" in dbg:
                return
            if not cached:
                P.barrier()
            def nat(tv, sl, rev):
                v = tv[:, sl]
                return v[:, ::-1] if rev else v

            def gp_gen(gp, d=d):
                rs, rho1 = rs_sets[gp % 2], rho_sets[gp % 2]
                psv = [kb.psum_f(), kb.psum_f()]
                for ri in range(2):
                    for h in range(2):
                        g = gp * 2 + h
                        mm(psv[ri][h * 64:(h + 1) * 64, 0:NCH], W1[ri][:, g, :], U[:, g, :], True, True)
                cp("pool", rho1, pv(9)[:, gp:gp + 1].bc([128, NCH]))
                yield
                Cn, Sn = tab[0][:, gp, :], tab[1][:, gp, :]
                if d == 0:
                    segs = [(slice(0, NCH), slice(0, NCH), False)]
                else:
                    segs = [(slice(0, 32), slice(0, 32), True), (slice(32, NCH), slice(32, NCH), True)]
                for (js, ks, rev) in segs:
                    vre, vim = nat(psv[0][:, 0:NCH], ks, rev), nat(psv[1][:, 0:NCH], ks, rev)
                    tt("dve", rs[0][:, js], vre, Cn[:, js], ALU.mult)
                    tt("dve", rs[1][:, js], vim, Sn[:, js], ALU.mult)
                    tt("dve", rs[4][:, js], vim, Cn[:, js], ALU.mult)
                    tt("dve", rs[5][:, js], vre, Sn[:, js], ALU.mult)
                yield
                tt("pool", rs[2], rs[0], rs[1], ALU.add)
                tt("pool", rs[3], rs[4], rs[5], ALU.subtract)
                yield
                kb.scan(rs[4], rho1, rs[2], 0.0)
                kb.scan(rs[5], rho1, rs[3], 0.0)
                yield
                tt("dve", rs[0], rs[4], Cn, ALU.mult)
                tt("dve", rs[1], rs[5], Sn, ALU.mult)
                tt("dve", rs[2], rs[4], Sn, ALU.mult)
                tt("dve", rs[3], rs[5], Cn, ALU.mult)
                yield
                for (js, ks, rev) in segs:
                    if d == 0:
                        osl = slice(1, NCH + 1)
                    else:
                        osl = slice(0, 32) if ks.start == 0 else slice(33, NCH + 1)
                    ore = nat(SN[d][0][:, gp, :], osl, rev).par("s5sn")
                    oim = nat(SN[d][1][:, gp, :], osl, rev).par("s5sn")
                    tt("pool", ore, rs[0][:, js], rs[1][:, js], ALU.subtract)
                    tt("pool", oim, rs[2][:, js], rs[3][:, js], ALU.add)
                yield
            for gp0 in range(0, 8, 2):
                lockstep([gp_gen(gp0), gp_gen(gp0 + 1)])
            for ri in range(2):
                if d == 0:
                    kb.memset("pool", SN[0][ri][:, :, 0:1], 0.0)
                else:
                    kb.memset("pool", SN[1][ri][:, :, 32:33], 0.0)
                    cp("pool", SN[1][ri][:, :, NCH + 1:NCH + 2], SN[1][ri][:, :, 0:1])
        if not cached:
            dma("sp", st_main[0], Ere.re("p d g n -> p (d g n)"))
            dma("act", st_main[1], nEim.re("p d g n -> p (d g n)"))
            dma("sp", st_main[2], W3.re("p g n -> p (g n)"))
        if "s5_stop3" in dbg:
            return
        P.barrier()
        AR.lo = scr0
        AR2.lo = 0
        Yc = AR.alloc("Yc", [8, 256])
        gg = AR.alloc("gg", [8, 256])
        g2 = AR.alloc("g2", [8, 256])
        sg = [AR.alloc("sg%d" % i, [512]) for i in range(2)]
        ggT = AR2.alloc("ggT", [2, NPOS])
        ggb = AR2.alloc("ggb", [2, NPOS], BF16)
        banks = [kb.psf[0], kb.psf[2], kb.psf[1], kb.psf[3]]

        def r_mm(k0, M, p0):
            for g in range(16):
                gp, h = g // 2, g % 2
                rows = slice(h * 64, (h + 1) * 64)
                bank = banks[h * 2 + (gp // 4)]
                o = bank[0:M, (gp % 4) * 128:(gp % 4 + 1) * 128]
                cf = slice(k0, k0 + M)
                cb_ = slice(k0 + 1, k0 + 1 + M) if k0 == 0 else slice(k0 + 2, k0 + 2 + M)
                mm(o, SN[0][0][rows, gp, cf], Ere[rows, 0, gp, :], True, False)
                mm(o, SN[0][1][rows, gp, cf], nEim[rows, 0, gp, :], False, False)
                mm(o, SN[1][0][rows, gp, cb_], Ere[rows, 1, gp, :], False, False)
                mm(o, SN[1][1][rows, gp, cb_], nEim[rows, 1, gp, :], False, False)
                mm(o, U[:, g, k0:k0 + M], W3[:, g, :], False, True)

        def r_evac(k0, M, p0):
            for h in range(2):
                for q in range(2):
                    bank = banks[h * 2 + q]
                    src = bank[0:M, :].re("p (g t o) -> p g t o", g=4, t=8)
                    dst = Yc[0:M].re("p t (g2 h o) -> p h g2 t o", h=2, o=16)[:, h, 4 * q:4 * q + 4].par("s5yc")
                    cp("act" if h == 0 else "dve", dst, src)

        def r_rest(k0, M, p0):
            yv, gv, g2v = Yc[0:M], gg[0:M], g2[0:M]
            act(g2v, yv, AF.Square)
            ts("dve", g2v, g2v, 0.044715, ALU.mult, 1.0, ALU.add)
            tt("dve", g2v, g2v, yv, ALU.mult)
            act(g2v, g2v, AF.Sigmoid, scale=1.5957691216057308)
            tt("dve", gv, g2v, yv, ALU.mult)
            for t_ in range(8):
                pz = [kb.psf[4], kb.psf[5]]
                for c in range(2):
                    tr(pz[c][:, 0:M], gv[:, t_, c * 128:(c + 1) * 128], identf[0:M, 0:M])
                for c in range(2):
                    cp("act" if c == 0 else "dve", ggT[:, c, p0 + t_:p0 + 8 * M:8].par("s5gg"), pz[c][:, 0:M])
        rb = [blk for blk in kblocks if not (blk[0] == 0 and not with_ctx)]
        r_mm(*rb[0])
        for bi, blk in enumerate(rb):
            r_evac(*blk)
            if bi + 1 < len(rb):
                r_mm(*rb[bi + 1])
            r_rest(*blk)
        lo = 0 if with_ctx else TC
        for c in range(2):
            cp("dve", ggb[:, c, lo:NPOS], ggT[:, c, lo:NPOS])
        for c in range(2):
            pos = lo
            while pos < NPOS:
                n = min(512, NPOS - pos)
                ps = kb.psum_f()
                for k in range(2):
                    mm(ps[:, 0:n], wglu[:, k, c * 128:(c + 1) * 128], ggb[:, k, pos:pos + n], k == 0, k == 1)
                s_ = sg[(pos // 512) % 2]
                act(s_[:, 0:n], ps[:, 0:n], AF.Sigmoid)
                tt("dve", brT[1][c][:, pos:pos + n], s_[:, 0:n], ggT[:, c, pos:pos + n], ALU.mult)
                pos += n

    def phase_merge(l, b, with_ctx, nxt=None):
        AR.reset(top=True)
        lo = 0 if with_ctx else TC
        blocks = []
        pos = lo
        while pos < NPOS:
            n = min(512, NPOS - pos) if pos >= TC else TC - pos
            blocks.append((pos, n))
            pos += n
        ypT = AR.alloc("ypT", [KC, NPOS], BF16, top=True)
        wbr = AR.alloc("wbr", [4, 2, D], BF16, top=True)
        wo = AR.alloc("wo", [KC, D], BF16, top=True)
        wml = [AR.alloc("wml%d" % i, [KC, 4, 128], BF16, top=True) for i in range(2)]

        def load_wml(oc):
            for n_ in range(4):
                c0 = O_MERGE + n_ * D + oc * 128
                dma("pool", wml[oc % 2][:, :, n_, :].par("ldw"), w_in[l, :, c0:c0 + 128].re("(c p) n -> p c n", p=128))
        wg = AR.alloc("w_gate", [KC, 1024], BF16)
        for cc in range(1, 8):
            dma("pool", wg[:, :, cc * 128:(cc + 1) * 128].par("ldw"),
                w_in[l, :, O_GATE + cc * 128:O_GATE + (cc + 1) * 128].re("(c p) n -> p c n", p=128))
        for n_ in range(4):
            dma("pool", wbr[:, n_].par("ldw"), w_branch[l, n_].re("(c p) n -> p c n", p=128))
        load_wml(0)
        load_wml(1)
        dma("pool", wo, w_out[l].re("(c p) n -> p c n", p=128))
        sl = [AR.alloc("sl%d" % i, [512], BF16) for i in range(2)]
        it = 0
        for cc in range(8):
            for (p0, n) in blocks:
                ps = kb.psum_f()
                for k in range(KC):
                    wsl = wst[:, k, 0:128] if cc == 0 else wg[:, k, cc * 128:(cc + 1) * 128]
                    mm(ps[:, 0:n], wsl, hT[:, k, p0:p0 + n], k == 0, k == KC - 1)
                s_ = sl[it % 2]
                it += 1
                act(s_[:, 0:n], ps[:, 0:n], AF.Silu)
                br = brT[cc // 2][cc % 2][:, p0:p0 + n]
                tt("dve", br, br, s_[:, 0:n], ALU.mult)
            if cc == 0 and nxt is not None:
                prefetch(nxt, PF_S5)
        AR.reset()
        sg = [AR.alloc("sg%d" % i, [512]) for i in range(2)]
        tm = [AR.alloc("tm%d" % i, [512]) for i in range(2)]
        acc = [AR.alloc("acc%d" % i, [512]) for i in range(2)]
        it = 0
        ib = 0
        for oc in range(8):
            wm = wml[oc % 2]
            if 1 <= oc and oc + 1 < 8:
                load_wml(oc + 1)
            for (p0, n) in blocks:
                ac = acc[ib % 2]
                ib += 1
                for n_ in range(4):
                    psA = kb.psum_f()
                    for k in range(2):
                        mm(psA[:, 0:n], wbr[:, n_, k, oc * 128:(oc + 1) * 128], brT[n_][k][:, p0:p0 + n], k == 0, k == 1)
                    psB = kb.psum_f()
                    for k in range(KC):
                        mm(psB[:, 0:n], wm[:, k, n_, :], hT[:, k, p0:p0 + n], k == 0, k == KC - 1)
                    s_ = sg[it % 2]
                    t_ = tm[it % 2]
                    it += 1
                    act(s_[:, 0:n], psB[:, 0:n], AF.Sigmoid)
                    if n_ == 0:
                        tt("dve", ac[:, 0:n], psA[:, 0:n], s_[:, 0:n], ALU.mult)
                    elif n_ < 3:
                        tt("dve", t_[:, 0:n], psA[:, 0:n], s_[:, 0:n], ALU.mult)
                        tt("dve", ac[:, 0:n], ac[:, 0:n], t_[:, 0:n], ALU.add)
                    else:
                        tt("dve", t_[:, 0:n], psA[:, 0:n], s_[:, 0:n], ALU.mult)
                        tt("dve", ypT[:, oc, p0:p0 + n].par("ypt"), ac[:, 0:n], t_[:, 0:n], ALU.add)
        AR.reset()
        gbc = AR.alloc("gbc", [2, D])
        dma("sp", gbc[:, 0, :], modD[l, b, 2 * D:3 * D].pbc(128))
        if with_ctx:
            dma("sp", gbc[:, 1, :], modD[l, 2, 2 * D:3 * D].pbc(128))
        xt = [AR.alloc("xt%d" % i, [D]) for i in range(2)]
        yt = [AR.alloc("yt%d" % i, [D]) for i in range(2)]
        for i in range(0 if with_ctx else 2, 18):
            x_t, y_t = xt[i % 2], yt[i % 2]
            dma("sp", x_t, src_tile(l, b, i))
            for hf in range(2):
                ps = kb.psum_f()
                for k in range(KC):
                    mm(ps, ypT[:, k, i * 128:(i + 1) * 128], wo[:, k, hf * 512:(hf + 1) * 512], k == 0, k == KC - 1)
                tt("dve", y_t[:, hf * 512:(hf + 1) * 512], ps, gbc[:, 1 if i < 2 else 0, hf * 512:(hf + 1) * 512], ALU.mult)
            tt("dve", y_t, y_t, x_t, ALU.add)
            if i < 2:
                dst = (tap("xc1", [nb, TC, D]) if "x1out" in dbg else xcmid)[b, i * 128:(i + 1) * 128, :]
            else:
                dst = (xmid if (l < DEPTH - 1 and "x1out" not in dbg) else y_out)[b, (i - 2) * 128:(i - 1) * 128, :]
            dma("act", dst, y_t)

    def dump_br(name, n, lo):
        o = tap(name, [2, 128, NPOS])
        tmpf = AR.alloc("dump_" + name, [NPOS])
        for c in range(2):
            cp("dve", tmpf[:, lo:NPOS], brT[n][c][:, lo:NPOS])
            dma("sp", o[c, :, lo:NPOS], tmpf[:, lo:NPOS])

    prefetch(layers[0], PF_S5)
    for l in layers:
        with_ctx = l < DEPTH - 1
        prep_layer(l)
        for b in range(nb):
            if "prep_only" in dbg:
                continue
            phase_norm(l, b)
            if "hT" in dbg and b == 0 and l == layers[0]:
                o = tap("hT", [KC, 128, NPOS])
                AR.reset()
                tmpf = AR.alloc("dump_hT", [NPOS])
                for c in range(KC):
                    cp("dve", tmpf, hT[:, c, :])
                    dma("sp", o[c], tmpf)
            only = dbg & {"only_lru", "only_pool", "only_mla", "only_s5", "only_norm"}
            if not only or "only_s5" in only:
                phase_s5(l, b, with_ctx)
            if not only or "only_lru" in only:
                phase_lru(l, b, with_ctx)
            if not only or "only_pool" in only:
                phase_pool(l, b, with_ctx)
            if not only or "only_mla" in only:
                phase_mla(l, b, with_ctx)
            if "br" in dbg and b == 0 and l == layers[0]:
                AR.reset()
                lo = 0 if with_ctx else TC
                for n_, nm in enumerate(("mla", "s5", "lru", "pool")):
                    dump_br(nm, n_, lo)
            if "nomerge" not in dbg:
                if b + 1 < nb:
                    nxt = l
                else:
                    li = list(layers).index(l)
                    nxt = layers[li + 1] if li + 1 < len(layers) else None
                phase_merge(l, b, with_ctx, nxt)
    P.barrier()
    P.emit()
    kb.st.close()
    return nc, kb


def _consts():
    ident = np.eye(128, dtype=np.float32)
    rows_n = T // 64
    row = np.repeat(np.arange(rows_n), 64).astype(np.float32)
    col = np.tile(np.arange(64), rows_n).astype(np.float32)
    nf = 8
    inv = (np.float32(10000.0) ** (-np.arange(nf, dtype=np.float32) / nf)).astype(np.float32)
    ar = (row[:, None] * inv).astype(np.float32)
    ac = (col[:, None] * inv).astype(np.float32)
    cr, sr, cc, sc = np.cos(ar), np.sin(ar), np.cos(ac), np.sin(ac)
    ropeC = np.concatenate([cr, cr, cc, cc], axis=1).astype(np.float32)
    ropeS = np.concatenate([-sr, sr, -sc, sc], axis=1).astype(np.float32)
    cpool = np.ones((128, 34), np.float32)
    for c in range(2):
        for p in range(128):
            w = (2, 4, 8, 16)[2 * c + p // 64]
            hw = w // 2
            cpool[p, c] = 1.0 / w
            for t in range(8):
                cnt = t + hw if t < hw else w
                cpool[p, 2 + c * 8 + t] = w / cnt
            for j in range(8):
                dist = 8 - j
                cnt = dist + hw if dist < hw else w
                cpool[p, 18 + c * 8 + j] = w / cnt
    m = np.zeros((128, 3, 128), np.float32)
    for r in range(8):
        for t in range(8):
            if r <= t:
                m[r * 16:(r + 1) * 16, 0, t * 16:(t + 1) * 16] = 1.0
            if r >= t:
                m[r * 16:(r + 1) * 16, 1, t * 16:(t + 1) * 16] = 1.0
            if r == t:
                m[r * 16:(r + 1) * 16, 2, t * 16:(t + 1) * 16] = np.eye(16, dtype=np.float32)
    return {"c_ident": ident, "c_ropeC": ropeC, "c_ropeS": ropeS, "c_pool": cpool, "c_mask": m}


_WNAMES = ["w_ada", "b_ada", "norm_g", "w_in", "mla_q_norm", "mla_kv_norm", "mla_w_uq", "mla_w_ukv", "mla_q_gain",
           "mla_k_gain", "s5_a_re", "s5_a_im", "s5_log_dt", "s5_b_re", "s5_b_im", "s5_c_re", "s5_c_im", "s5_d",
           "s5_w_glu", "lru_conv_w", "lru_conv_b", "lru_lambda", "lru_w_a", "lru_b_a", "lru_w_x", "lru_b_x",
           "pool_w", "pool_b", "pool_scale", "w_branch", "w_out"]


def make_in_maps(inputs, n_cores, nb):
    consts = _consts()
    maps = []
    for r in range(n_cores):
        bs = slice(r * nb, (r + 1) * nb)
        m = {"x": np.ascontiguousarray(inputs["x"][bs], dtype=np.float32),
             "ctx": np.ascontiguousarray(inputs["ctx"][bs], dtype=np.float32)}
        cv = np.zeros((3, D), np.float32)
        cv[0:nb] = np.asarray(inputs["c"], dtype=np.float32)[bs]
        cv[2] = np.asarray(inputs["c_ctx"], dtype=np.float32)
        m["cvec"] = cv
        for k in _WNAMES:
            m[k] = np.ascontiguousarray(inputs[k], dtype=np.float32)
        m.update(consts)
        maps.append(m)
    return maps


_CACHE = {}


def kernel(**inputs):
    n_cores, nb = 8, 2
    if "nc" not in _CACHE:
        _CACHE["nc"] = build_program(nb=nb)[0]
    nc = _CACHE["nc"]
    maps = make_in_maps(inputs, n_cores, nb)
    res = run_bass_kernel_spmd(nc, maps, core_ids=list(range(n_cores)))
    out = np.concatenate([np.asarray(r["y"], dtype=np.float32) for r in res.results], axis=0)
    return out
```

```python
import math
import contextlib
import numpy as np
import concourse.bass as bass
import concourse.mybir as mybir
from concourse.bass_utils import run_bass_kernel_spmd

F32 = mybir.dt.float32
BF16 = mybir.dt.bfloat16
I32 = mybir.dt.int32
AF = mybir.ActivationFunctionType
ALU = mybir.AluOpType
AX = mybir.AxisListType

ENGS = ("pe", "act", "dve", "pool", "sp")
NDMA = 24

D = 1024
KC = 8
T = 2048
TC = 256
NPOS = T + TC
DEPTH = 2
O_KROPE, O_S5, O_LRU, O_CQ, O_POOL, O_GATE, O_MERGE, IN_W = 128, 160, 416, 672, 928, 1184, 2208, 6304
EPS = 1e-6
NCH = NPOS // 8


PAR_OFF = set()


class Buf:
    __slots__ = ("name", "w", "wp", "r")

    def __init__(self, name):
        self.name = name
        self.w = []
        self.wp = []
        self.r = []


class TV:
    par_ = False

    def __init__(self, ap, buf):
        self.ap, self.buf = ap, buf

    def par(self, tag=""):
        t = TV(self.ap, self.buf)
        t.par_ = tag not in PAR_OFF
        return t

    def __getitem__(self, k):
        return TV(self.ap[k], self.buf)

    def re(self, s, **kw):
        return TV(self.ap.rearrange(s, **kw), self.buf)

    def bc(self, shape):
        return TV(self.ap.to_broadcast(list(shape)), self.buf)

    def us(self, ax):
        return TV(self.ap.unsqueeze(ax), self.buf)

    def pbc(self, n):
        return TV(self.ap.partition_broadcast(n), self.buf)

    @property
    def shape(self):
        return tuple(self.ap.shape)


def _bufs(*xs):
    out = []
    for x in xs:
        if isinstance(x, TV):
            out.append(x.buf)
        elif isinstance(x, (list, tuple)):
            out.extend(_bufs(*x))
    return out


def _ap(x):
    return x.ap if isinstance(x, TV) else x


class Prog:
    def __init__(self, nc):
        self.nc = nc
        self.ops = {e: [] for e in ENGS}
        self.cnt = {e: 0 for e in ENGS}
        self.waited = {e: {} for e in ENGS}
        self.dma_i = 0
        self.dma_last = [0] * NDMA

    def _deps(self, eng, reads, writes, par=False):
        toks = []
        for b in reads:
            toks.extend(b.w)
            toks.extend(b.wp)
        for b in writes:
            toks.extend(b.w)
            if not par:
                toks.extend(b.wp)
            toks.extend(b.r)
        need = {}
        for (k, v, e) in toks:
            if e == "pe" and eng == "pe":
                continue
            if self.waited[eng].get(k, 0) >= v:
                continue
            if need.get(k, 0) < v:
                need[k] = v
        for k, v in need.items():
            self.waited[eng][k] = v
        return list(need.items())

    def _mark(self, tok, reads, writes, par=False):
        for b in writes:
            if par:
                best = {}
                for (k, v, e) in b.wp + [tok]:
                    if k not in best or best[k][1] < v:
                        best[k] = (k, v, e)
                b.wp = list(best.values())
            else:
                b.w = [tok]
                b.wp = []
                b.r = []
        for b in reads:
            if b in writes:
                continue
            b.r.append(tok)
            if len(b.r) > 16:
                best = {}
                for (k, v, e) in b.r:
                    if k not in best or best[k][1] < v:
                        best[k] = (k, v, e)
                b.r = list(best.values())

    def op(self, eng, fn, reads=(), writes=(), par=False):
        reads = list(dict.fromkeys(reads))
        writes = list(dict.fromkeys(writes))
        waits = self._deps(eng, reads, writes, par)
        self.cnt[eng] += 1
        tok = (eng, self.cnt[eng], eng)
        self.ops[eng].append((waits, fn, (eng, 1)))
        self._mark(tok, reads, writes, par)

    def dma(self, eng, fn, reads=(), writes=(), par=False):
        reads = list(dict.fromkeys(reads))
        writes = list(dict.fromkeys(writes))
        waits = self._deps(eng, reads, writes, par)
        slot = self.dma_i % NDMA
        self.dma_i += 1
        key = ("dma", slot)
        prev = self.dma_last[slot]
        if prev and self.waited[eng].get(key, 0) < prev:
            waits.append((key, prev))
            self.waited[eng][key] = prev
        val = prev + 16
        self.dma_last[slot] = val
        tok = (key, val, "dma")
        self.ops[eng].append((waits, fn, (key, 16)))
        self._mark(tok, reads, writes, par)

    def barrier(self):
        for E in ENGS:
            waits = []
            for e in ENGS:
                v = self.cnt[e]
                if v and self.waited[E].get(e, 0) < v:
                    waits.append((e, v))
                    self.waited[E][e] = v
            for slot in range(NDMA):
                v = self.dma_last[slot]
                key = ("dma", slot)
                if v and self.waited[E].get(key, 0) < v:
                    waits.append((key, v))
                    self.waited[E][key] = v
            if waits:
                self.ops[E].append((waits, None, None))

    def emit(self):
        nc = self.nc
        sems = {}
        with contextlib.ExitStack() as st:
            for e in ENGS:
                sems[e] = st.enter_context(nc.semaphore("s_" + e))
            for i in range(NDMA):
                sems[("dma", i)] = st.enter_context(nc.semaphore("s_dma%d" % i))
            block = st.enter_context(nc.Block())

            def run(engname):
                def body(eng):
                    for waits, fn, inc in self.ops[engname]:
                        for k, v in waits:
                            eng.wait_ge(sems[k], v)
                        if fn is None:
                            continue
                        ins = fn(eng)
                        ins.then_inc(sems[inc[0]], inc[1])
                return body
            block.tensor(run("pe"))
            block.scalar(run("act"))
            block.vector(run("dve"))
            block.gpsimd(run("pool"))
            block.sync(run("sp"))


class KB:
    def __init__(self, nc, nb, layers, dbg):
        self.nc = nc
        self.P = Prog(nc)
        self.nb = nb
        self.layers = layers
        self.dbg = dbg
        self.st = contextlib.ExitStack()
        self.din = {}
        self.dout = {}
        self.ps_i = 0
        self.psb_i = 0
        self.ps8_i = 0

    def tt(self, eng, out, a, b, op):
        self.P.op(eng, lambda e: e.tensor_tensor(out=out.ap, in0=a.ap, in1=b.ap, op=op), _bufs(a, b), _bufs(out), par=out.par_)

    def ts(self, eng, out, a, s1, op0, s2=None, op1=None):
        if op1 is None:
            self.P.op(eng, lambda e: e.tensor_scalar(out=out.ap, in0=a.ap, scalar1=_ap(s1), scalar2=None, op0=op0),
                      _bufs(a, s1), _bufs(out), par=out.par_)
        else:
            self.P.op(eng, lambda e: e.tensor_scalar(out=out.ap, in0=a.ap, scalar1=_ap(s1), scalar2=_ap(s2), op0=op0, op1=op1),
                      _bufs(a, s1, s2), _bufs(out), par=out.par_)

    def stt(self, out, a, s, b, op0, op1):
        self.P.op("dve", lambda e: e.scalar_tensor_tensor(out=out.ap, in0=a.ap, scalar=_ap(s), in1=b.ap, op0=op0, op1=op1),
                  _bufs(a, s, b), _bufs(out))

    def act(self, out, a, func, bias=0.0, scale=1.0, accum=None):
        if accum is None:
            self.P.op("act", lambda e: e.activation(out=out.ap, in_=a.ap, func=func, bias=_ap(bias), scale=_ap(scale)),
                      _bufs(a, bias, scale), _bufs(out), par=out.par_)
        else:
            self.P.op("act", lambda e: e.activation(out=out.ap, in_=a.ap, func=func, bias=_ap(bias), scale=_ap(scale), accum_out=accum.ap),
                      _bufs(a, bias, scale), _bufs(out, accum))

    def cp(self, eng, out, a):
        if eng == "act":
            self.P.op("act", lambda e: e.copy(out=out.ap, in_=a.ap), _bufs(a), _bufs(out), par=out.par_)
        else:
            self.P.op(eng, lambda e: e.tensor_copy(out=out.ap, in_=a.ap), _bufs(a), _bufs(out), par=out.par_)

    def memset(self, eng, out, v):
        self.P.op(eng, lambda e: e.memset(out.ap, v), [], _bufs(out), par=out.par_)

    def recip(self, out, a):
        self.P.op("dve", lambda e: e.reciprocal(out=out.ap, in_=a.ap), _bufs(a), _bufs(out))

    def mm(self, out, lhsT, rhs, start, stop):
        self.P.op("pe", lambda e: e.matmul(out.ap, lhsT=lhsT.ap, rhs=rhs.ap, start=start, stop=stop), _bufs(lhsT, rhs), _bufs(out))

    def tr(self, out, a, ident):
        self.P.op("pe", lambda e: e.transpose(out.ap, a.ap, ident.ap), _bufs(a, ident), _bufs(out))

    def scan(self, out, d0, d1, init):
        self.P.op("dve", lambda e: e.tensor_tensor_scan(out=out.ap, data0=d0.ap, data1=d1.ap, initial=_ap(init), op0=ALU.mult, op1=ALU.add),
                  _bufs(d0, d1, init), _bufs(out))

    def reduce(self, out, a, op=ALU.add):
        self.P.op("dve", lambda e: e.tensor_reduce(out=out.ap, in_=a.ap, axis=AX.X, op=op), _bufs(a), _bufs(out))

    def dma(self, q, out, a, slow=False):
        if slow:
            self.P.dma(q, lambda e: e.dma_start(out=out.ap, in_=a.ap, allow_slow_non_contiguous=True), _bufs(a), _bufs(out), par=out.par_)
        else:
            self.P.dma(q, lambda e: e.dma_start(out=out.ap, in_=a.ap), _bufs(a), _bufs(out), par=out.par_)

    def dram_in(self, name, shape, dt=F32):
        t = TV(self.nc.dram_tensor(name, list(shape), dt, kind="ExternalInput").ap(), Buf(name))
        self.din[name] = t
        return t

    def dram_out(self, name, shape, dt=F32):
        t = TV(self.nc.dram_tensor(name, list(shape), dt, kind="ExternalOutput").ap(), Buf(name))
        self.dout[name] = t
        return t

    def dram_tmp(self, name, shape, dt=F32):
        return TV(self.nc.dram_tensor(name, list(shape), dt, kind="Internal").ap(), Buf(name))

    def sb(self, name, shape, dt=F32):
        t = self.st.enter_context(self.nc.sbuf_tensor(name, list(shape), dt))
        return TV(t[:], Buf(name))

    def psum_f(self):
        i = self.ps_i % 6
        self.ps_i += 1
        return self.psf[i]

    def psum_b(self):
        i = self.psb_i % 2
        self.psb_i += 1
        return self.psb[i]

    def psum_any(self, bf16=False):
        i = self.ps8_i % 8
        self.ps8_i += 1
        return self.ps8b[i] if bf16 else self.ps8[i]


class Arena:
    def __init__(self, kb, words, ap=None):
        self.kb = kb
        self.words = words
        if ap is None:
            self.f = kb.st.enter_context(kb.nc.sbuf_tensor("arena_f", [128, words], F32))
        else:
            self.f = ap
        self.lo = 0
        self.hi = words

    def alloc(self, name, free, dt=F32, top=False, buf=None, at=None):
        nel = int(np.prod(free))
        w = nel if dt == F32 else (nel + 1) // 2
        w = ((w + 15) // 16) * 16
        if at is not None:
            off = at
        elif top:
            self.hi -= w
            off = self.hi
        else:
            off = self.lo
            self.lo += w
        assert self.lo <= self.hi, (name, self.lo, self.hi)
        assert off + w <= self.words
        if dt != F32:
            ap = self.f[:, off:off + w].bitcast(dt)[:, 0:nel]
        else:
            ap = self.f[:, off:off + nel]
        if len(free) > 1:
            names = " ".join("d%d" % i for i in range(len(free)))
            kw = {"d%d" % i: int(free[i]) for i in range(len(free))}
            ap = ap.rearrange("p (%s) -> p %s" % (names, names), **kw)
        tv = TV(ap, buf if buf is not None else Buf(name))
        tv.off = off
        tv.w = w
        return tv

    def reset(self, top=False):
        self.kb.P.barrier()
        self.lo = 0
        if top:
            self.hi = self.words


def lockstep(gens):
    gens = list(gens)
    while gens:
        nxt = []
        for g in gens:
            try:
                next(g)
                nxt.append(g)
            except StopIteration:
                pass
        gens = nxt


def build_program(nb=2, layers=(0, 1), dbg=None):
    nc = bass.Bass("TRN2", target_bir_lowering=False)
    kb = KB(nc, nb, layers, dbg)
    P = kb.P
    tt, ts, stt, act, cp, mm, tr, dma = kb.tt, kb.ts, kb.stt, kb.act, kb.cp, kb.mm, kb.tr, kb.dma
    dbg = dbg or set()
    PAR_OFF.clear()
    PAR_OFF.update(x[6:] for x in dbg if x.startswith("nopar_"))

    x_in = kb.dram_in("x", [nb, T, D])
    ctx_in = kb.dram_in("ctx", [nb, TC, D])
    cvec = kb.dram_in("cvec", [3, D])
    w_ada = kb.dram_in("w_ada", [DEPTH, D, 3 * D])
    b_ada = kb.dram_in("b_ada", [DEPTH, 3 * D])
    norm_g = kb.dram_in("norm_g", [DEPTH, D])
    w_in = kb.dram_in("w_in", [DEPTH, D, IN_W])
    mla_q_norm = kb.dram_in("mla_q_norm", [DEPTH, 256])
    mla_kv_norm = kb.dram_in("mla_kv_norm", [DEPTH, 128])
    mla_w_uq = kb.dram_in("mla_w_uq", [DEPTH, 256, 384])
    mla_w_ukv = kb.dram_in("mla_w_ukv", [DEPTH, 128, 512])
    mla_q_gain = kb.dram_in("mla_q_gain", [DEPTH, 96])
    mla_k_gain = kb.dram_in("mla_k_gain", [DEPTH, 96])
    s5_a_re = kb.dram_in("s5_a_re", [DEPTH, 2, 16, 64])
    s5_a_im = kb.dram_in("s5_a_im", [DEPTH, 2, 16, 64])
    s5_log_dt = kb.dram_in("s5_log_dt", [DEPTH, 2, 16])
    s5_b_re = kb.dram_in("s5_b_re", [DEPTH, 2, 16, 64, 16])
    s5_b_im = kb.dram_in("s5_b_im", [DEPTH, 2, 16, 64, 16])
    s5_c_re = kb.dram_in("s5_c_re", [DEPTH, 2, 16, 16, 64])
    s5_c_im = kb.dram_in("s5_c_im", [DEPTH, 2, 16, 16, 64])
    s5_d = kb.dram_in("s5_d", [DEPTH, 256])
    s5_w_glu = kb.dram_in("s5_w_glu", [DEPTH, 256, 256])
    lru_conv_w = kb.dram_in("lru_conv_w", [DEPTH, 4, 256])
    lru_conv_b = kb.dram_in("lru_conv_b", [DEPTH, 256])
    lru_lambda = kb.dram_in("lru_lambda", [DEPTH, 2, 256])
    lru_w_a = kb.dram_in("lru_w_a", [DEPTH, 2, 4, 64, 64])
    lru_b_a = kb.dram_in("lru_b_a", [DEPTH, 2, 256])
    lru_w_x = kb.dram_in("lru_w_x", [DEPTH, 2, 4, 64, 64])
    lru_b_x = kb.dram_in("lru_b_x", [DEPTH, 2, 256])
    pool_w = kb.dram_in("pool_w", [DEPTH, 4, 64, 64])
    pool_b = kb.dram_in("pool_b", [DEPTH, 256])
    pool_scale = kb.dram_in("pool_scale", [DEPTH, 256])
    w_branch = kb.dram_in("w_branch", [DEPTH, 4, 256, D])
    w_out = kb.dram_in("w_out", [DEPTH, D, D])
    c_ident = kb.dram_in("c_ident", [128, 128])
    c_ropeC = kb.dram_in("c_ropeC", [T, 32])
    c_ropeS = kb.dram_in("c_ropeS", [T, 32])
    c_pool = kb.dram_in("c_pool", [128, 2 + 16 + 16])
    c_mask = kb.dram_in("c_mask", [128, 3, 128])
    y_out = kb.dram_out("y", [nb, T, D])
    xmid = kb.dram_tmp("xmid", [nb, T, D])
    xcmid = kb.dram_tmp("xcmid", [nb, TC, D])
    modD = kb.dram_tmp("modD", [DEPTH, 3, 3 * D])
    st_main = kb.dram_tmp("st_main", [3, 128, 2048])
    st_dir = kb.dram_tmp("st_dir", [2, 128, 2048 + 16 * NCH + 8])
    dbg_out = {}

    def tap(name, shape):
        if name not in dbg_out:
            dbg_out[name] = kb.dram_out("dbg_" + name, shape)
        return dbg_out[name]

    kb.ps8 = []
    kb.pspair = []
    for i in range(4):
        t = kb.st.enter_context(nc.psum_tensor("psp%d" % i, [128, 1024], F32))
        kb.pspair.append(t[:])
        for hh in range(2):
            kb.ps8.append(TV(t[:][:, hh * 512:(hh + 1) * 512], Buf("psf%d" % (2 * i + hh))))
    kb.psf = kb.ps8[0:6]
    kb.psb = [TV(kb.ps8[i].ap.bitcast(BF16), kb.ps8[i].buf) for i in (6, 7)]
    kb.ps8b = [TV(kb.ps8[i].ap.bitcast(BF16), kb.ps8[i].buf) for i in range(8)]

    hT = kb.sb("hT", [128, KC, NPOS], BF16)
    bigbr = kb.sb("bigbr", [128, 8 * NPOS], BF16)
    _slot = {0: 0, 2: 2, 3: 4, 1: 6}
    brT = [[TV(bigbr.ap[:, (_slot[n] + c) * NPOS:(_slot[n] + c + 1) * NPOS], Buf("brT%d%d" % (n, c))) for c in range(2)]
           for n in range(4)]
    identf = kb.sb("identf", [128, 128])
    identb = kb.sb("identb", [128, 128], BF16)
    ropeC = kb.sb("ropeC", [128, 16, 32])
    ropeS = kb.sb("ropeS", [128, 16, 32])
    cpool = kb.sb("cpool", [128, 34])
    cmask = kb.sb("cmask", [128, 3, 128])
    modA = kb.sb("modA", [128, 3, KC])
    modS = kb.sb("modS", [128, 3, KC])
    lp = kb.sb("lp", [128, 64])
    lruBD = kb.sb("lruBD", [128, 2, 2, 2, 128])
    poolBD = kb.sb("poolBD", [128, 2, 128])
    wukv = kb.sb("wukv", [128, 512], BF16)
    wuq = kb.sb("wuq", [128, 2, 384], BF16)
    gains = kb.sb("gains", [128, 2, 96])
    wglu = kb.sb("wglu", [128, 2, 256], BF16)
    AR = Arena(kb, 28000)
    wst = kb.sb("wst", [128, KC, 416], BF16)

    def prefetch(l, parts):
        for (d0, c0, n) in parts:
            dma("pool", wst[:, :, d0:d0 + n].par("wst"), w_in[l, :, c0:c0 + n].re("(c p) n -> p c n", p=128))
    PF_S5 = [(0, O_S5, 256)]
    PF_LRU = [(0, O_LRU, 256)]
    PF_POOL = [(0, O_POOL, 256)]
    PF_MLA = [(0, 0, 160), (160, O_CQ, 256)]
    PF_GATE0 = [(0, O_GATE, 128)]
    AR2 = Arena(kb, 3 * NPOS, ap=bigbr.ap[:, 0:6 * NPOS].bitcast(F32))

    dma("sp", identf, c_ident)
    cp("dve", identb, identf)
    dma("sp", ropeC, c_ropeC.re("(i p) f -> p i f", p=128))
    dma("sp", ropeS, c_ropeS.re("(i p) f -> p i f", p=128))
    dma("sp", cpool, c_pool)
    dma("sp", cmask, c_mask)

    LP_CW = 0
    LP_CB = 8
    LP_NSP = 10
    LP_NSP2 = 14
    LP_BA = 18
    LP_BX = 22
    LP_PSC = 26
    LP_PBS = 28
    LP_G = 30
    LP_TMP = 40

    def prep_layer(l):
        AR.reset(top=True)
        cT = AR.alloc("cT", [KC, 3])
        for v in range(3):
            dma("sp", cT[:, :, v], cvec[v].re("(c p) -> p c", p=128), slow=True)
        cact = AR.alloc("cact", [KC, 3])
        act(cact, cT, AF.Silu)
        brow = AR.alloc("brow", [3 * D])
        dma("sp", brow[0:3, :], b_ada[l:l + 1, :].re("o n -> (o n)").pbc(3))
        modrow = AR.alloc("modrow", [3 * D])
        wa = [AR.alloc("wa%d" % i, [KC, 512]) for i in range(4)]
        for cb in range(4):
            for hh in range(2):
                dma("sp" if hh == 0 else "act", wa[cb][:, hh * 4:(hh + 1) * 4, :].par("wa"),
                    w_ada[l, hh * 512:(hh + 1) * 512, cb * 512:(cb + 1) * 512].re("(c p) n -> p c n", p=128))
        for cb in range(6):
            w = wa[cb % 4]
            if cb >= 4:
                for hh in range(2):
                    dma("sp" if hh == 0 else "act", w[:, hh * 4:(hh + 1) * 4, :].par("wa"),
                        w_ada[l, hh * 512:(hh + 1) * 512, cb * 512:(cb + 1) * 512].re("(c p) n -> p c n", p=128))
            ps = kb.psum_f()
            for k in range(KC):
                mm(ps[0:3, :], cact[:, k, :], w[:, k, :], k == 0, k == KC - 1)
            tt("dve", modrow[0:3, cb * 512:(cb + 1) * 512], ps[0:3, :], brow[0:3, cb * 512:(cb + 1) * 512], ALU.add)
        dma("sp", modD[l], modrow[0:3, :])
        sc = AR.alloc("sc", [3, KC])
        for v in range(3):
            dma("sp", modS[:, v, :], modD[l, v, 0:D].re("(c p) -> p c", p=128), slow=True)
            dma("sp", sc[:, v, :], modD[l, v, D:2 * D].re("(c p) -> p c", p=128), slow=True)
        dma("sp", lp[:, LP_G:LP_G + 8], norm_g[l].re("(c p) -> p c", p=128), slow=True)
        for v in range(3):
            stt(modA[:, v, :], sc[:, v, :], 1.0, lp[:, LP_G:LP_G + 8], ALU.add, ALU.mult)
        for k in range(4):
            dma("sp", lp[:, LP_CW:LP_CW + 8].re("p (c k) -> p c k", c=2)[:, :, k], lru_conv_w[l, k].re("(c p) -> p c", p=128), slow=True)
        dma("sp", lp[:, LP_CB:LP_CB + 2], lru_conv_b[l].re("(c p) -> p c", p=128), slow=True)
        lam = lp[:, LP_TMP:LP_TMP + 4]
        for d in range(2):
            dma("sp", lam[:, d * 2:d * 2 + 2], lru_lambda[l, d].re("(c p) -> p c", p=128), slow=True)
            dma("sp", lp[:, LP_BA + d * 2:LP_BA + d * 2 + 2], lru_b_a[l, d].re("(c p) -> p c", p=128), slow=True)
            dma("sp", lp[:, LP_BX + d * 2:LP_BX + d * 2 + 2], lru_b_x[l, d].re("(c p) -> p c", p=128), slow=True)
        t0 = lp[:, LP_TMP + 4:LP_TMP + 8]
        t1 = lp[:, LP_TMP + 8:LP_TMP + 12]
        t2 = lp[:, LP_TMP + 12:LP_TMP + 16]
        t3 = lp[:, LP_TMP + 16:LP_TMP + 20]
        ts("dve", t0, lam, -1.0, ALU.mult)
        tt("dve", t0, t0, lam, ALU.max)
        act(t1, t0, AF.Exp, scale=-1.0)
        ts("dve", t2, t1, 2.0, ALU.add)
        kb.recip(t2, t2)
        tt("dve", t2, t2, t1, ALU.mult)
        tt("dve", t3, t2, t2, ALU.mult)
        ts("dve", t0, t3, 1.0 / 11.0, ALU.mult, 1.0 / 9.0, ALU.add)
        for cf in (1.0 / 7.0, 1.0 / 5.0, 1.0 / 3.0, 1.0):
            tt("dve", t0, t0, t3, ALU.mult)
            ts("dve", t0, t0, cf, ALU.add)
        tt("dve", t0, t0, t2, ALU.mult)
        ts("dve", t1, lam, -1.0, ALU.mult, 0.0, ALU.max)
        stt(t0, t0, 2.0, t1, ALU.mult, ALU.add)
        ts("dve", lp[:, LP_NSP:LP_NSP + 4], t0, -8.0, ALU.mult)
        ts("dve", lp[:, LP_NSP2:LP_NSP2 + 4], t0, -16.0, ALU.mult)
        kb.memset("dve", lruBD, 0.0)
        for d in range(2):
            for gi, wsrc in enumerate((lru_w_a, lru_w_x)):
                for c in range(2):
                    for h in range(2):
                        dma("sp" if (c + h) % 2 == 0 else "act", lruBD[h * 64:(h + 1) * 64, d, gi, c, h * 64:(h + 1) * 64].par("ldw"), wsrc[l, d, 2 * c + h])
        kb.memset("dve", poolBD, 0.0)
        for c in range(2):
            for h in range(2):
                dma("sp", poolBD[h * 64:(h + 1) * 64, c, h * 64:(h + 1) * 64].par("ldw"), pool_w[l, 2 * c + h])
        dma("sp", lp[:, LP_PSC:LP_PSC + 2], pool_scale[l].re("(c p) -> p c", p=128), slow=True)
        dma("sp", lp[:, LP_PBS:LP_PBS + 2], pool_b[l].re("(c p) -> p c", p=128), slow=True)
        tt("dve", lp[:, LP_PBS:LP_PBS + 2], lp[:, LP_PBS:LP_PBS + 2], lp[:, LP_PSC:LP_PSC + 2], ALU.mult)
        kvn = lp[:, LP_TMP + 20:LP_TMP + 21]
        qn = lp[:, LP_TMP + 21:LP_TMP + 23]
        dma("sp", kvn, mla_kv_norm[l].re("(p o) -> p o", o=1), slow=True)
        dma("sp", qn, mla_q_norm[l].re("(c p) -> p c", p=128), slow=True)
        wtmp = AR.alloc("wtmp", [2, 512])
        dma("sp", wtmp[:, 0, :], mla_w_ukv[l])
        ts("dve", wukv, wtmp[:, 0, :], kvn, ALU.mult)
        wtmp2 = AR.alloc("wtmp2", [2, 384])
        dma("sp", wtmp2, mla_w_uq[l].re("(c p) n -> p c n", p=128))
        for c in range(2):
            ts("dve", wuq[:, c, :], wtmp2[:, c, :], qn[:, c:c + 1], ALU.mult)
        dma("sp", gains[:, 0, :], mla_q_gain[l:l + 1, :].re("o n -> (o n)").pbc(128))
        dma("sp", gains[:, 1, :], mla_k_gain[l:l + 1, :].re("o n -> (o n)").pbc(128))
        ts("dve", gains[:, 0, :], gains[:, 0, :], 96.0 ** -0.5, ALU.mult)
        dma("pool", wglu, s5_w_glu[l].re("(c p) n -> p c n", p=128))

    def src_tile(l, b, i):
        if i < 2:
            return (ctx_in if l == 0 else xcmid)[b, i * 128:(i + 1) * 128, :]
        return (x_in if l == 0 else xmid)[b, (i - 2) * 128:(i - 1) * 128, :]

    def phase_norm(l, b):
        AR.reset(top=True)
        NX = 4
        xt = [AR.alloc("xt%d" % i, [D]) for i in range(NX)]
        junk = [AR.alloc("junk%d" % i, [D]) for i in range(NX)]
        xn = [AR.alloc("xn%d" % i, [D], BF16) for i in range(NX)]
        st4 = [AR.alloc("st%d" % i, [4]) for i in range(NX)]
        def tile_gen(i):
            v = 2 if i < 2 else b
            x_t, xn_t, s4 = xt[i % NX], xn[i % NX], st4[i % NX]
            dma("sp" if i % 2 == 0 else "act", x_t, src_tile(l, b, i))
            yield
            act(junk[i % NX], x_t, AF.Square, accum=s4[:, 0:1])
            yield
            ts("dve", s4[:, 1:2], s4[:, 0:1], 1.0 / D, ALU.mult, EPS, ALU.add)
            yield
            act(s4[:, 2:3], s4[:, 1:2], AF.Sqrt)
            yield
            kb.recip(s4[:, 3:4], s4[:, 2:3])
            ts("dve", xn_t, x_t, s4[:, 3:4], ALU.mult)
            yield
            psA = kb.psum_any(bf16=True)
            psB = kb.psum_any(bf16=True)
            for c in range(KC):
                pz = psA if c % 2 == 0 else psB
                tr(pz[:, (c // 2) * 128:(c // 2 + 1) * 128], xn_t[:, c * 128:(c + 1) * 128], identb)
            yield
            for c in range(KC):
                o = hT[:, c, i * 128:(i + 1) * 128].par("norm")
                if c % 2 == 0:
                    act(o, psA[:, (c // 2) * 128:(c // 2 + 1) * 128], AF.Identity, bias=modS[:, v, c:c + 1], scale=modA[:, v, c:c + 1])
                else:
                    ts("dve", o, psB[:, (c // 2) * 128:(c // 2 + 1) * 128], modA[:, v, c:c + 1], ALU.mult, modS[:, v, c:c + 1], ALU.add)
            yield
        for g0 in range(0, 18, NX):
            lockstep([tile_gen(i) for i in range(g0, min(g0 + NX, 18))])

    def load_win(l, name, c0, ncols, top=False):
        w = AR.alloc(name, [KC, ncols], BF16, top=top)
        dma("pool", w, w_in[l, :, c0:c0 + ncols].re("(c p) n -> p c n", p=128))
        return w

    def proj_fm(w, col0, dst, p0, p1, evac):
        pos = p0
        while pos < p1:
            n = min(512, p1 - pos)
            ps = kb.psum_f()
            for k in range(KC):
                mm(ps[:, 0:n], w[:, k, col0:col0 + 128], hT[:, k, pos:pos + n], k == 0, k == KC - 1)
            evac(ps, pos, n)
            pos += n

    def phase_lru(l, b, with_ctx):
        AR.reset()
        w = wst[:, :, 0:256]
        xr_ = [AR.alloc("xr%d" % c, [NPOS]) for c in range(2)]
        xcc = [AR.alloc("xc%d" % c, [NPOS]) for c in range(2)]
        ys_ = [AR.alloc("ysum%d" % c, [NPOS]) for c in range(2)]
        NB_ = 512
        tmp_sets = [{nm: AR.alloc("%s%d" % (nm, q), [NB_]) for nm in ("r", "i", "a", "a2", "bb")} for q in range(4)]
        for c in range(2):
            proj_fm(w, c * 128, None, 0, NPOS, lambda ps, pos, n, c=c: cp("act", xr_[c][:, pos:pos + n].par("proj"), ps[:, 0:n]))
        prefetch(l, PF_POOL)
        for c in range(2):
            cw = lambda k: lp[:, LP_CW + c * 4 + k:LP_CW + c * 4 + k + 1]
            for (s0, s1) in ((0, TC), (TC, NPOS)):
                ts("dve", xcc[c][:, s0:s1], xr_[c][:, s0:s1], cw(2), ALU.mult, lp[:, LP_CB + c:LP_CB + c + 1], ALU.add)
                stt(xcc[c][:, s0 + 1:s1], xr_[c][:, s0:s1 - 1], cw(1), xcc[c][:, s0 + 1:s1], ALU.mult, ALU.add)
                stt(xcc[c][:, s0 + 2:s1], xr_[c][:, s0:s1 - 2], cw(0), xcc[c][:, s0 + 2:s1], ALU.mult, ALU.add)
                stt(xcc[c][:, s0:s1 - 1], xr_[c][:, s0 + 1:s1], cw(3), xcc[c][:, s0:s1 - 1], ALU.mult, ALU.add)
        blocks = [(0, TC)] + [(TC + j * NB_, min(TC + (j + 1) * NB_, NPOS)) for j in range((T + NB_ - 1) // NB_)]

        def chain(d, c):
            dc = d * 2 + c
            tmp = tmp_sets[dc]
            out = ys_[c] if d == 0 else xr_[c]
            order = blocks if d == 0 else [blocks[0]] + blocks[:0:-1]
            prev = None
            for (s0, s1) in order:
                n = s1 - s0
                r_, i_, a_, a2_, bb_ = (tmp[k][:, 0:n] for k in ("r", "i", "a", "a2", "bb"))
                for gi, dst in ((0, r_), (1, i_)):
                    q = 0
                    while q < n:
                        m = min(512, n - q)
                        ps = kb.psum_f()
                        mm(ps[:, 0:m], lruBD[:, d, gi, c, :], xcc[c][:, s0 + q:s0 + q + m], True, True)
                        bcol = (LP_BA if gi == 0 else LP_BX) + dc
                        act(dst[:, q:q + m], ps[:, 0:m], AF.Sigmoid, bias=lp[:, bcol:bcol + 1])
                        q += m
                yield
                act(a_, r_, AF.Exp, scale=lp[:, LP_NSP + dc:LP_NSP + dc + 1])
                act(a2_, r_, AF.Exp, scale=lp[:, LP_NSP2 + dc:LP_NSP2 + dc + 1])
                tt("pool", bb_, i_, xcc[c][:, s0:s1], ALU.mult)
                yield
                ts("dve", a2_, a2_, -1.0, ALU.mult, 1.0, ALU.add)
                yield
                act(a2_, a2_, AF.Sqrt)
                yield
                tt("dve", bb_, bb_, a2_, ALU.mult)
                init = 0.0 if prev is None else prev
                if d == 0:
                    kb.scan(out[:, s0:s1], a_, bb_, init)
                    prev = out[:, s1 - 1:s1]
                else:
                    kb.scan(out[:, s0:s1][:, ::-1], a_[:, ::-1], bb_[:, ::-1], init)
                    prev = out[:, s0:s0 + 1]
                yield
        lockstep([chain(0, 0), chain(1, 0), chain(0, 1), chain(1, 1)])
        for c in range(2):
            lo = 0 if with_ctx else TC
            tt("dve", brT[2][c][:, lo:NPOS], ys_[c][:, lo:NPOS], xr_[c][:, lo:NPOS], ALU.add)

    AR_car = [kb.sb("car%d" % i, [128, 1]) for i in range(2)]

    def phase_pool(l, b, with_ctx):
        AR.reset()
        w = wst[:, :, 0:256]
        xp = AR.alloc("xp", [2, NPOS])
        cs_ = [AR.alloc("cs%d" % c, [NPOS + 2 * 17 + 2]) for c in range(2)]
        pm_ = [AR.alloc("pm%d" % c, [NPOS]) for c in range(2)]
        ones = AR.alloc("ones", [T])
        kb.memset("pool", ones, 1.0)
        for c in range(2):
            proj_fm(w, c * 128, xp, 0, NPOS, lambda ps, pos, n, c=c: cp("act", xp[:, c, pos:pos + n].par("proj"), ps[:, 0:n]))
        prefetch(l, PF_MLA)
        segs = [(0, TC, 0)] + [(TC, NPOS, TC + 17)]
        if not with_ctx:
            segs = segs[1:]
        def chunk_gen(c):
            for (s0, s1, o0) in segs:
                L = s1 - s0
                kb.memset("pool", cs_[c][:, o0:o0 + 9], 0.0)
                kb.scan(cs_[c][:, o0 + 9:o0 + 9 + L], ones[:, 0:L], xp[:, c, s0:s1], 0.0)
                yield
                ts("dve", cs_[c][:, o0 + 9 + L:o0 + 17 + L], cs_[c][:, o0:o0 + 8], cs_[c][:, o0 + 8 + L:o0 + 9 + L], ALU.add)
                yield
                for h in range(2):
                    hw = (1, 2, 4, 8)[2 * c + h]
                    rows = slice(h * 64, (h + 1) * 64)
                    base = o0 + 8
                    tt("dve", pm_[c][rows, s0:s1], cs_[c][rows, base + hw:base + hw + L], cs_[c][rows, base - hw:base - hw + L], ALU.subtract)
                yield
                ts("dve", pm_[c][:, s0:s1], pm_[c][:, s0:s1], cpool[:, c:c + 1], ALU.mult)
                yield
                tt("dve", pm_[c][:, s0:s0 + 8], pm_[c][:, s0:s0 + 8], cpool[:, 2 + c * 8:2 + c * 8 + 8], ALU.mult)
                tt("dve", pm_[c][:, s1 - 8:s1], pm_[c][:, s1 - 8:s1], cpool[:, 18 + c * 8:18 + c * 8 + 8], ALU.mult)
                yield
                tt("dve", pm_[c][:, s0:s1], pm_[c][:, s0:s1], xp[:, c, s0:s1], ALU.subtract)
                yield
                pos = s0
                while pos < s1:
                    n = min(512, s1 - pos)
                    ps = kb.psum_f()
                    mm(ps[:, 0:n], poolBD[:, c, :], pm_[c][:, pos:pos + n], True, True)
                    act(brT[3][c][:, pos:pos + n].par("pool"), ps[:, 0:n], AF.Identity, bias=lp[:, LP_PBS + c:LP_PBS + c + 1],
                        scale=lp[:, LP_PSC + c:LP_PSC + c + 1])
                    pos += n
                    yield
        lockstep([chunk_gen(0), chunk_gen(1)])

    def phase_mla(l, b, with_ctx):
        AR.reset()
        wkv = wst[:, :, 0:160]
        wq = wst[:, :, 160:416]
        qT = AR.alloc("qT", [4, NPOS], BF16)
        kT = AR.alloc("kT", [4, NPOS], BF16)
        Vt = AR.alloc("Vt", [18, 4, 68], BF16)
        lo_fixed = AR.lo
        kb.memset("dve", Vt.re("p a b c -> p (a b c)"), 1.0)
        assert kT.off == qT.off + qT.w
        zv = AR.f[:, qT.off:kT.off + kT.w]
        P.op("act", lambda e, zv=zv: e.memzero(zv), [], [qT.buf, kT.buf])
        NS = 3
        sm = [AR.alloc("sm%d" % i, [32]) for i in range(NS)]
        kr = [AR.alloc("kr%d" % i, [32]) for i in range(NS)]
        cn = [AR.alloc("cn%d" % i, [384], BF16) for i in range(NS)]
        cnT = [AR.alloc("cnT%d" % i, [3, 128], BF16) for i in range(NS)]
        sq = [AR.alloc("sq%d" % i, [704]) for i in range(NS)]
        qk = [AR.alloc("qk%d" % i, [2, 4, 96]) for i in range(NS)]
        rg = [AR.alloc("rg%d" % i, [2, 4, 96]) for i in range(NS)]
        qkb = [AR.alloc("qkb%d" % i, [2, 4, 96], BF16) for i in range(NS)]
        rt = [AR.alloc("rt%d" % i, [2, 2, 4, 32]) for i in range(NS)]
        kvs_ = [AR.alloc("kvs%d" % i, [512]) for i in range(NS)]
        qs_ = [AR.alloc("qs%d" % i, [384]) for i in range(NS)]

        def info(i):
            is_ctx = i < 2
            do_q = (not is_ctx) or with_ctx
            return is_ctx, do_q, i % NS, slice(i * 128, (i + 1) * 128)

        def stage_a(i):
            is_ctx, do_q, j, pos = info(i)
            s, cn_, cnT_, sq_ = sm[j], cn[j], cnT[j], sq[j]
            ps1 = kb.psum_any()
            for k in range(KC):
                mm(ps1[:, 0:160], hT[:, k, pos], wkv[:, k, :], k == 0, k == KC - 1)
            if do_q:
                for k in range(KC):
                    mm(ps1[:, 160:416], hT[:, k, pos], wq[:, k, :], k == 0, k == KC - 1)
            yield
            act(sq_[:, 0:128], ps1[:, 0:128], AF.Square, accum=s[:, 0:1])
            if do_q:
                act(sq_[:, 128:384], ps1[:, 160:416], AF.Square, accum=s[:, 1:2])
            else:
                kb.memset("pool", s[:, 1:2], 1.0)
            cp("act", kr[j], ps1[:, 128:160])
            yield
            act(s[:, 2:3], s[:, 0:1], AF.Sqrt, scale=1.0 / 128, bias=EPS)
            act(s[:, 3:4], s[:, 1:2], AF.Sqrt, scale=1.0 / 256, bias=EPS)
            yield
            kb.recip(s[:, 4:6], s[:, 2:4])
            ts("dve", cn_[:, 0:128], ps1[:, 0:128], s[:, 4:5], ALU.mult)
            if do_q:
                ts("dve", cn_[:, 128:384], ps1[:, 160:416], s[:, 5:6], ALU.mult)
            yield
            psT = kb.psum_any(bf16=True)
            for c in range(3 if do_q else 1):
                tr(psT[:, c * 128:(c + 1) * 128], cn_[:, c * 128:(c + 1) * 128], identb)
            yield
            cp("act", cnT_[:, 0:(3 if do_q else 1), :], psT[:, 0:(384 if do_q else 128)].re("p (c n) -> p c n", n=128))
            yield

        def stage_b(i):
            is_ctx, do_q, j, pos = info(i)
            s, cnT_, sq_, qk_, qkb_, rt_, rg_ = sm[j], cnT[j], sq[j], qk[j], qkb[j], rt[j], rg[j]
            pskv = kb.psum_any()
            mm(pskv, cnT_[:, 0, :], wukv, True, True)
            if do_q:
                psq = kb.psum_any()
                for c in range(2):
                    mm(psq[:, 0:384], cnT_[:, 1 + c, :], wuq[:, c, :], c == 0, c == 1)
            yield
            cp("act", kvs_[j], pskv)
            if do_q:
                cp("act", qs_[j], psq[:, 0:384])
            kv3 = kvs_[j].re("p (h e) -> p h e", h=4)
            q3 = qs_[j].re("p (h e) -> p h e", h=4)
            krope = kr[j]
            act(sq_[:, 640:672], krope, AF.Square, accum=s[:, 7:8])
            cp("act", Vt[:, i, :, 0:64].par("mlav"), kv3[:, :, 64:128])
            yield
            ksq = sq_[:, 0:256].re("p (h e) -> p h e", h=4)
            qsq = sq_[:, 256:640].re("p (h e) -> p h e", h=4)
            tt("pool", ksq, kv3[:, :, 0:64], kv3[:, :, 0:64], ALU.mult)
            if do_q:
                tt("pool", qsq, q3, q3, ALU.mult)
            yield
            kb.reduce(s[:, 12:16], ksq)
            if do_q:
                kb.reduce(s[:, 8:12], qsq)
            else:
                kb.memset("dve", s[:, 8:12], 1.0)
            ts("dve", s[:, 12:16], s[:, 12:16], s[:, 7:8], ALU.add)
            yield
            act(s[:, 8:16], s[:, 8:16], AF.Sqrt, scale=1.0 / 96, bias=EPS)
            yield
            kb.recip(s[:, 16:24], s[:, 8:16])
            yield
            tt("pool", rg_, s[:, 16:24].re("p (w h) -> p w h", w=2).us(3).bc([128, 2, 4, 96]),
               gains.us(2).bc([128, 2, 4, 96]), ALU.mult)
            yield
            if do_q:
                tt("dve", qk_[:, 0].par("qk"), q3, rg_[:, 0], ALU.mult)
            else:
                kb.memset("pool", qk_[:, 0].par("qk"), 0.0)
            tt("dve", qk_[:, 1, :, 0:64].par("qk"), kv3[:, :, 0:64], rg_[:, 1, :, 0:64], ALU.mult)
            tt("pool", qk_[:, 1, :, 64:96].par("qk"), krope.us(1).bc([128, 4, 32]), rg_[:, 1, :, 64:96], ALU.mult)
            yield
            if not is_ctx:
                ti = i - 2
                v = qk_[:, :, :, 64:96]
                t1_ = rt_[:, 0].par("rt")
                t2_ = rt_[:, 1]
                Cb = ropeC[:, ti, :].us(1).us(1).bc([128, 2, 4, 32])
                tt("pool", t1_, v, Cb, ALU.mult)
                for a in range(2):
                    for s_ in range(2):
                        o_ = t2_[:, :, :, a * 16 + s_ * 8:a * 16 + s_ * 8 + 8].par("rt")
                        i_ = qk_[:, :, :, 64 + a * 16 + (1 - s_) * 8:64 + a * 16 + (1 - s_) * 8 + 8]
                        Sb = ropeS[:, ti, a * 16 + s_ * 8:a * 16 + s_ * 8 + 8].us(1).us(1).bc([128, 2, 4, 8])
                        tt("dve" if (a + s_) % 2 == 0 else "pool", o_, i_, Sb, ALU.mult)
                yield
                tt("dve", v, t1_, t2_, ALU.add)
            cp("dve", qkb_, qk_)
            yield

        def stage_c(i):
            is_ctx, do_q, j, pos = info(i)
            qkb_ = qkb[j]
            psT2 = kb.psum_any(bf16=True)
            for w_ in range(2):
                if w_ == 0 and not do_q:
                    continue
                for h in range(4):
                    tr(psT2[0:96, (w_ * 4 + h) * 128:(w_ * 4 + h + 1) * 128], qkb_[:, w_, h, :], identb)
            yield
            if do_q:
                cp("act", qT[0:96, :, pos].par("mlaqk"), psT2[0:96, 0:512].re("p (h n) -> p h n", h=4))
            cp("act", kT[0:96, :, pos].par("mlaqk"), psT2[0:96, 512:1024].re("p (h n) -> p h n", h=4))

        def tile_gen(i):
            yield from stage_a(i)
            yield from stage_b(i)
            yield from stage_c(i)
        for g0 in range(0, 18, NS):
            lockstep([tile_gen(i) for i in range(g0, g0 + NS)])
        if "mla_stop1" in dbg:
            return
        prefetch(l, PF_GATE0)
        P.barrier()
        AR.lo = lo_fixed
        omla = AR.alloc("omla", [18, 256], BF16)
        pT = [AR.alloc("pT%d" % i, [2, 512], BF16) for i in range(3)]
        rc = [AR.alloc("rc%d" % i, [4]) for i in range(2)]
        jobs = []
        if with_ctx:
            jobs.append((0, TC, 0, 2))
        for qb in range(4):
            jobs.append((TC + qb * 512, TC + (qb + 1) * 512, 0, 18))
        pairs = []
        for (q0, q1, kt0, kt1) in jobs:
            for h in range(4):
                for kt in range(kt0, kt1, 2):
                    pairs.append((q0, q1, kt0, kt1, h, kt))
        SB = (0, 3)

        def s_mm(pi_):
            q0, q1, kt0, kt1, h, kt = pairs[pi_]
            tsel = SB[pi_ % 2]
            for j in range(2):
                mm(kb.ps8[2 * tsel + j][:, 0:q1 - q0], kT[:, h, (kt + j) * 128:(kt + j + 1) * 128], qT[:, h, q0:q1], True, True)
        s_mm(0)
        for pi_, (q0, q1, kt0, kt1, h, kt) in enumerate(pairs):
            nq = q1 - q0
            nqt = nq // 128
            pso = [kb.psf[2 + t_] for t_ in range(nqt)]
            if pi_ + 1 < len(pairs):
                s_mm(pi_ + 1)
            tsel = SB[pi_ % 2]
            p_ = pT[pi_ % 3]
            src = kb.pspair[tsel].rearrange("p (j n) -> p j n", j=2)[:, :, 0:nq]
            dst = p_.ap[:, :, 0:nq]
            P.op("act", lambda e, src=src, dst=dst: e.activation(out=dst, in_=src, func=AF.Exp, bias=0.0, scale=1.0),
                 [kb.ps8[2 * tsel].buf, kb.ps8[2 * tsel + 1].buf], [p_.buf])
            for j in range(2):
                for t_ in range(nqt):
                    mm(pso[t_][:, 0:68], p_[:, j, t_ * 128:(t_ + 1) * 128], Vt[:, kt + j, h, :], kt + j == kt0, kt + j == kt1 - 1)
            if kt + 2 >= kt1:
                for t_ in range(nqt):
                    r_ = rc[t_ % 2]
                    kb.recip(r_[:, 0:1], pso[t_][:, 64:65])
                    ts("dve", omla[:, (q0 // 128) + t_, h * 64:(h + 1) * 64].par("omla"), pso[t_][:, 0:64], r_[:, 0:1], ALU.mult)
        for i in range(0 if with_ctx else 2, 18):
            psT = kb.psum_b()
            for c in range(2):
                tr(psT[:, c * 128:(c + 1) * 128], omla[:, i, c * 128:(c + 1) * 128], identb)
            for c in range(2):
                cp("act", brT[0][c][:, i * 128:(i + 1) * 128].par("brt0"), psT[:, c * 128:(c + 1) * 128])

    def phase_s5(l, b, with_ctx):
        AR.reset()
        AR2.lo = 0
        U = AR.alloc("U", [16, NCH])
        SN = [[AR.alloc("SN%d%d" % (d, ri), [8, NCH + 2]) for ri in range(2)] for d in range(2)]
        Ere = AR.alloc("Ere", [2, 8, 128])
        nEim = AR.alloc("nEim", [2, 8, 128])
        W3 = AR.alloc("W3", [16, 128])
        scr0 = AR.lo
        W1 = [AR2.alloc("W1%d" % i, [16, 64]) for i in range(2)]
        tab = [AR2.alloc("tab%d" % i, [8, NCH]) for i in range(2)]
        cached = (b > 0) and ("s5_nocache" not in dbg)

        def load_dir(d):
            dma("sp", W1[0].re("p g n -> p (g n)"), st_dir[d, :, 0:1024])
            dma("sp", W1[1].re("p g n -> p (g n)"), st_dir[d, :, 1024:2048])
            dma("act", tab[0].re("p g n -> p (g n)"), st_dir[d, :, 2048:2048 + 8 * NCH])
            dma("act", tab[1].re("p g n -> p (g n)"), st_dir[d, :, 2048 + 8 * NCH:2048 + 16 * NCH])
        if cached:
            dma("sp", Ere.re("p d g n -> p (d g n)"), st_main[0])
            dma("act", nEim.re("p d g n -> p (d g n)"), st_main[1])
            dma("sp", W3.re("p g n -> p (g n)"), st_main[2])
            load_dir(0)
        ws5 = wst[:, :, 0:256]
        Uc = AR.alloc("Uc", [16, 8, 16])
        kblocks = [(0, 32, 0), (32, 128, TC), (160, 128, TC + 1024)]
        for (k0, M, p0) in kblocks:
            pss = [kb.psum_f() for _ in range(4)]
            for r in range(8):
                ps = pss[r // 2]
                for k in range(KC):
                    mm(ps[0:M, (r % 2) * 256:(r % 2 + 1) * 256], hT[:, k, p0 + r:p0 + 8 * M:8], ws5[:, k, :], k == 0, k == KC - 1)
            for q in range(4):
                src = pss[q][0:M, :].re("p (r g i) -> p r g i", r=2, g=16)
                dst = Uc[0:M, :, 2 * q:2 * q + 2, :].re("p g r i -> p r g i").par("s5uc")
                cp("act" if q % 2 == 0 else "dve", dst, src)
            for gq in range(4):
                ps = kb.psum_f()
                for gg_ in range(4):
                    g = gq * 4 + gg_
                    tr(ps[:, gg_ * 128:gg_ * 128 + M], Uc[0:M, g, :, :].re("p r i -> p (r i)"), identf[0:M, 0:M])
                cp("act" if gq % 2 == 0 else "dve", U[:, gq * 4:gq * 4 + 4, k0:k0 + M].par("s5u"),
                   ps.re("p (g n) -> p g n", g=4)[:, :, 0:M])
        prefetch(l, PF_LRU)
        P.barrier()
        AR.lo = scr0
        if "s5_stop1" in dbg:
            return
        if not cached:
            kb.memset("pool", W3, 0.0)
        prm = AR.alloc("prm", [40, 8])
        PW = [AR.alloc("pw%d" % i, [10, 8]) for i in range(2)]
        Bb = [AR.alloc("Bb%d" % i, [8, 16]) for i in range(2)]
        Braw = [AR.alloc("Braw%d" % i, [8, 16]) for i in range(2)]
        Craw = [AR.alloc("Craw%d" % i, [8, 16]) for i in range(2)]
        Dbc = AR.alloc("Dbc", [256])
        t8 = AR.alloc("t8", [8, 16])
        w3t = [AR.alloc("w3t%d" % i, [128]) for i in range(2)]
        tE = AR.alloc("tE", [8, 128])
        Fm = [AR.alloc("F%d" % i, [8, 8, 16]) for i in range(2)]
        Ep = [AR.alloc("Ep%d" % i, [8, 128]) for i in range(2)]
        Cnat = [AR.alloc("Cnat%d" % i, [16, 64], at=Ep[i].off, buf=Ep[i].buf) for i in range(2)]
        rs_sets = [[AR.alloc("rs%d_%d" % (q, i), [NCH], at=Fm[0].off + (q * 7 + i) * NCH) for i in range(6)]
                   for q in range(2)]
        rho_sets = [AR.alloc("rho1_%d" % q, [NCH], at=Fm[0].off + (q * 7 + 6) * NCH) for q in range(2)]
        assert Fm[0].off + 14 * NCH <= Ep[1].off + Ep[1].w
        dma("sp", Dbc, s5_d[l:l + 1, :].re("o n -> (o n)").pbc(128))
        dsel = cmask[:, 2, :]

        prm_bufs = [Buf("prm%d" % i) for i in range(40)]
        pw_bufs = [[Buf("pw%d_%d" % (ri, j)) for j in range(10)] for ri in range(2)]

        def pv(i):
            return TV(prm.ap[:, i, :], prm_bufs[i])

        def pw(ri, j):
            return TV(PW[ri].ap[:, j, :], pw_bufs[ri][j])

        for d in range(2):
            if d == 1 and not cached:
                P.barrier()
            if not cached:
                dma("sp", pv(0), s5_a_re[l, d].re("(gp h) p -> (h p) gp", h=2), slow=True)
                dma("act", pv(1), s5_a_im[l, d].re("(gp h) p -> (h p) gp", h=2), slow=True)
                ldt = t8[:, 0, :]
                dma("sp", ldt, s5_log_dt[l, d:d + 1, :].re("o g -> (o g)").pbc(128))
                dma("sp", Cnat[0][0:16], s5_c_re[l, d].re("g o p -> o g p"))
                dma("act", Cnat[1][0:16], s5_c_im[l, d].re("g o p -> o g p"))
                for h in range(2):
                    rows = slice(h * 64, (h + 1) * 64)
                    cp("dve", pv(2)[rows], ldt[rows, h:16:2])
                    dma("sp", Braw[0][rows].par("ldw"), s5_b_re[l, d].re("(gp h) p i -> h p gp i", h=2)[h])
                    dma("act", Braw[1][rows].par("ldw"), s5_b_im[l, d].re("(gp h) p i -> h p gp i", h=2)[h])
                for ri in range(2):
                    ps = kb.psum_f()
                    for g in range(16):
                        gp, h = g // 2, g % 2
                        mm(ps[h * 64:(h + 1) * 64, gp * 16:(gp + 1) * 16], Cnat[ri][0:16, g, :], identf[0:16, 0:16], True, True)
                    cp("act", Craw[ri], ps[:, 0:128].re("p (g o) -> p g o", g=8))
                if "s5_b1" in dbg:
                    return
                act(pv(2), pv(2), AF.Exp)
                tt("dve", pv(3), pv(0), pv(2), ALU.mult)
                tt("dve", pv(4), pv(1), pv(2), ALU.mult)
                act(pv(5), pv(3), AF.Exp)
                for (dst, shift) in ((6, 0.0), (7, math.pi / 2)):
                    xx, kk, ki = pv(30), pv(31), pv(32)
                    ts("dve", xx, pv(4), shift, ALU.add)
                    ts("dve", kk, xx, 1.0 / (2 * math.pi), ALU.mult)
                    kint = TV(ki.ap.bitcast(I32), ki.buf)
                    cp("dve", kint, kk)
                    cp("dve", kk, kint)
                    stt(xx, kk, -6.28125, xx, ALU.mult, ALU.add)
                    stt(xx, kk, -(2 * math.pi - 6.28125), xx, ALU.mult, ALU.add)
                    ts("dve", xx, xx, math.pi, ALU.min, -math.pi, ALU.max)
                    act(pv(dst), xx, AF.Sin)
                kb.memset("dve", pw(0, 0), 1.0)
                kb.memset("dve", pw(1, 0), 0.0)
                tt("dve", pw(0, 1), pv(5), pv(7), ALU.mult)
                tt("dve", pw(1, 1), pv(5), pv(6), ALU.mult)
                for j in range(2, 9):
                    tt("dve", pv(30), pw(0, j - 1), pw(0, 1), ALU.mult)
                    tt("dve", pv(31), pw(1, j - 1), pw(1, 1), ALU.mult)
                    tt("dve", pw(0, j), pv(30), pv(31), ALU.subtract)
                    tt("dve", pv(30), pw(0, j - 1), pw(1, 1), ALU.mult)
                    tt("dve", pv(31), pw(1, j - 1), pw(0, 1), ALU.mult)
                    tt("dve", pw(1, j), pv(30), pv(31), ALU.add)
                act(pv(8), pv(3), AF.Exp, scale=-16.0)
                tt("dve", pw(0, 9), pw(0, 8), pv(8), ALU.mult)
                tt("dve", pw(1, 9), pw(1, 8), pv(8), ALU.mult)
                ts("dve", pw(1, 9), pw(1, 9), -1.0, ALU.mult)
                act(pv(9), pv(3), AF.Exp, scale=8.0)
                act(pv(10), pv(3), AF.Exp, scale=-8.0)
                tt("dve", pv(11), pw(0, 8), pv(10), ALU.mult)
                tt("dve", pv(12), pw(1, 8), pv(10), ALU.mult)
                ts("dve", pv(13), pw(0, 1), -1.0, ALU.add)
                tt("dve", pv(14), pv(0), pv(0), ALU.mult)
                tt("dve", pv(15), pv(1), pv(1), ALU.mult)
                tt("dve", pv(14), pv(14), pv(15), ALU.add)
                kb.recip(pv(14), pv(14))
                tt("dve", pv(15), pv(13), pv(0), ALU.mult)
                tt("dve", pv(16), pw(1, 1), pv(1), ALU.mult)
                tt("dve", pv(15), pv(15), pv(16), ALU.add)
                tt("dve", pv(15), pv(15), pv(14), ALU.mult)
                tt("dve", pv(16), pw(1, 1), pv(0), ALU.mult)
                tt("dve", pv(17), pv(13), pv(1), ALU.mult)
                tt("dve", pv(16), pv(16), pv(17), ALU.subtract)
                tt("dve", pv(16), pv(16), pv(14), ALU.mult)
                fre = pv(15).us(2).bc([128, 8, 16])
                fim = pv(16).us(2).bc([128, 8, 16])
                tt("dve", Bb[0], Braw[0], fre, ALU.mult)
                tt("dve", t8, Braw[1], fim, ALU.mult)
                tt("dve", Bb[0], Bb[0], t8, ALU.subtract)
                tt("dve", Bb[1], Braw[1], fre, ALU.mult)
                tt("dve", t8, Braw[0], fim, ALU.mult)
                tt("dve", Bb[1], Bb[1], t8, ALU.add)
                for t_ in range(8):
                    f_ = t_ + 1 if d == 0 else 8 - t_
                    pr = pw(0, f_).us(2).bc([128, 8, 16])
                    pi_ = pw(1, f_).us(2).bc([128, 8, 16])
                    eo = Ere[:, d, :, t_ * 16:(t_ + 1) * 16]
                    ei = nEim[:, d, :, t_ * 16:(t_ + 1) * 16]
                    tt("dve", eo, Craw[0], pr, ALU.mult)
                    tt("dve", t8, Craw[1], pi_, ALU.mult)
                    tt("dve", eo, eo, t8, ALU.subtract)
                    tt("dve", ei, Craw[0], pi_, ALU.mult)
                    tt("dve", t8, Craw[1], pr, ALU.mult)
                    tt("dve", ei, ei, t8, ALU.add)
                    ts("dve", ei, ei, -1.0, ALU.mult)
                for r in range(8):
                    e_ = 7 - r if d == 0 else r
                    pr = pw(0, e_).us(2).bc([128, 8, 16])
                    pi_ = pw(1, e_).us(2).bc([128, 8, 16])
                    tt("dve", Fm[0][:, :, r, :], Bb[0], pr, ALU.mult)
                    tt("dve", t8, Bb[1], pi_, ALU.mult)
                    tt("dve", Fm[0][:, :, r, :], Fm[0][:, :, r, :], t8, ALU.subtract)
                    tt("dve", Fm[1][:, :, r, :], Bb[1], pr, ALU.mult)
                    tt("dve", t8, Bb[0], pi_, ALU.mult)
                    tt("dve", Fm[1][:, :, r, :], Fm[1][:, :, r, :], t8, ALU.add)
                qr = pw(0, 9).us(2).bc([128, 8, 128])
                qi = pw(1, 9).us(2).bc([128, 8, 128])
                tt("dve", Ep[0], Ere[:, d], qr, ALU.mult)
                tt("dve", tE, nEim[:, d], qi, ALU.mult)
                tt("dve", Ep[0], Ep[0], tE, ALU.add)
                tt("dve", Ep[1], nEim[:, d], qr, ALU.mult)
                tt("dve", tE, Ere[:, d], qi, ALU.mult)
                tt("dve", Ep[1], Ep[1], tE, ALU.subtract)
                if "s5_b2" in dbg:
                    return
                for ri in range(2):
                    for gq in range(4):
                        ps = kb.psum_f()
                        for gg_ in range(4):
                            g = gq * 4 + gg_
                            gp, h = g // 2, g % 2
                            rows = slice(h * 64, (h + 1) * 64)
                            mm(ps[:, gg_ * 64:(gg_ + 1) * 64], Fm[ri][:, gp].re("p r i -> p (r i)"), identf[:, h * 64:(h + 1) * 64], True, True)
                        cp("act", W1[ri][:, gq * 4:gq * 4 + 4, :], ps[:, 0:256].re("p (g n) -> p g n", g=4))
                if "s5_b3" in dbg:
                    return
                for g in range(16):
                    gp, h = g // 2, g % 2
                    rows = slice(h * 64, (h + 1) * 64)
                    ps = kb.psf[h * 2 + (g // 2) % 2]
                    mm(ps[:, 0:128], Fm[0][rows, gp].re("p r i -> p (r i)"), Ep[0][rows, gp, :], True, False)
                    mm(ps[:, 0:128], Fm[1][rows, gp].re("p r i -> p (r i)"), Ep[1][rows, gp, :], False, True)
                    wt = w3t[g % 2]
                    tt("dve", wt, ps[:, 0:128], cmask[:, d, :], ALU.mult)
                    tt("pool", W3[:, g, :], W3[:, g, :], wt, ALU.add)
                if d == 0:
                    for g in range(16):
                        dcol = Dbc[:, g * 16:(g + 1) * 16].us(1).bc([128, 8, 16])
                        wt = w3t[g % 2]
                        tt("dve", wt.re("p (t o) -> p t o", t=8), dsel.re("p (t o) -> p t o", t=8), dcol, ALU.mult)
                        tt("pool", W3[:, g, :], W3[:, g, :], wt, ALU.add)
                if "s5_b4" in dbg:
                    return
                kb.memset("dve", tab[0][:, :, 0:1], 1.0)
                kb.memset("dve", tab[1][:, :, 0:1], 0.0)
                cp("dve", tab[0][:, :, 1:2], pv(11).us(2))
                cp("dve", tab[1][:, :, 1:2], pv(12).us(2))
                n = 2
                while n < NCH:
                    m = min(n, NCH - n)
                    tt("dve", pv(30), tab[0][:, :, n - 1], pv(11), ALU.mult)
                    tt("dve", pv(31), tab[1][:, :, n - 1], pv(12), ALU.mult)
                    tt("dve", pv(33), pv(30), pv(31), ALU.subtract)
                    tt("dve", pv(30), tab[0][:, :, n - 1], pv(12), ALU.mult)
                    tt("dve", pv(31), tab[1][:, :, n - 1], pv(11), ALU.mult)
                    tt("dve", pv(34), pv(30), pv(31), ALU.add)
                    Pr = pv(33).us(2).bc([128, 8, m])
                    Pi = pv(34).us(2).bc([128, 8, m])
                    ta = tE[:, :, 0:m]
                    tt("dve", tab[0][:, :, n:n + m], tab[0][:, :, 0:m], Pr, ALU.mult)
                    tt("dve", ta, tab[1][:, :, 0:m], Pi, ALU.mult)
                    tt("dve", tab[0][:, :, n:n + m], tab[0][:, :, n:n + m], ta, ALU.subtract)
                    tt("dve", tab[1][:, :, n:n + m], tab[0][:, :, 0:m], Pi, ALU.mult)
                    tt("dve", ta, tab[1][:, :, 0:m], Pr, ALU.mult)
                    tt("dve", tab[1][:, :, n:n + m], tab[1][:, :, n:n + m], ta, ALU.add)
                    n *= 2
                dma("sp", st_dir[d, :, 0:1024], W1[0].re("p g n -> p (g n)"))
                dma("sp", st_dir[d, :, 1024:2048], W1[1].re("p g n -> p (g n)"))
                dma("act", st_dir[d, :, 2048:2048 + 8 * NCH], tab[0].re("p g n -> p (g n)"))
                dma("act", st_dir[d, :, 2048 + 8 * NCH:2048 + 16 * NCH], tab[1].re("p g n -> p (g n)"))
                dma("sp", st_dir[d, :, 2048 + 16 * NCH:2048 + 16 * NCH + 8], pv(9))
            else:
                if d == 1:
                    load_dir(1)
                dma("sp", pv(9), st_dir[d, :, 2048 + 16 * NCH:2048 + 16 * NCH + 8])
            if "s5_stop2" in dbg:
                return
            if not cached:
                P.barrier()
            def nat(tv, sl, rev):
                v = tv[:, sl]
                return v[:, ::-1] if rev else v

            def gp_gen(gp, d=d):
                rs, rho1 = rs_sets[gp % 2], rho_sets[gp % 2]
                psv = [kb.psum_f(), kb.psum_f()]
                for ri in range(2):
                    for h in range(2):
                        g = gp * 2 + h
                        mm(psv[ri][h * 64:(h + 1) * 64, 0:NCH], W1[ri][:, g, :], U[:, g, :], True, True)
                cp("pool", rho1, pv(9)[:, gp:gp + 1].bc([128, NCH]))
                yield
                Cn, Sn = tab[0][:, gp, :], tab[1][:, gp, :]
                if d == 0:
                    segs = [(slice(0, NCH), slice(0, NCH), False)]
                else:
                    segs = [(slice(0, 32), slice(0, 32), True), (slice(32, NCH), slice(32, NCH), True)]
                for (js, ks, rev) in segs:
                    vre, vim = nat(psv[0][:, 0:NCH], ks, rev), nat(psv[1][:, 0:NCH], ks, rev)
                    tt("dve", rs[0][:, js], vre, Cn[:, js], ALU.mult)
                    tt("dve", rs[1][:, js], vim, Sn[:, js], ALU.mult)
                    tt("dve", rs[4][:, js], vim, Cn[:, js], ALU.mult)
                    tt("dve", rs[5][:, js], vre, Sn[:, js], ALU.mult)
                yield
                tt("pool", rs[2], rs[0], rs[1], ALU.add)
                tt("pool", rs[3], rs[4], rs[5], ALU.subtract)
                yield
                kb.scan(rs[4], rho1, rs[2], 0.0)
                kb.scan(rs[5], rho1, rs[3], 0.0)
                yield
                tt("dve", rs[0], rs[4], Cn, ALU.mult)
                tt("dve", rs[1], rs[5], Sn, ALU.mult)
                tt("dve", rs[2], rs[4], Sn, ALU.mult)
                tt("dve", rs[3], rs[5], Cn, ALU.mult)
                yield
                for (js, ks, rev) in segs:
                    if d == 0:
                        osl = slice(1, NCH + 1)
                    else:
                        osl = slice(0, 32) if ks.start == 0 else slice(33, NCH + 1)
                    ore = nat(SN[d][0][:, gp, :], osl, rev).par("s5sn")
                    oim = nat(SN[d][1][:, gp, :], osl, rev).par("s5sn")
                    tt("pool", ore, rs[0][:, js], rs[1][:, js], ALU.subtract)
                    tt("pool", oim, rs[2][:, js], rs[3][:, js], ALU.add)
                yield
            for gp0 in range(0, 8, 2):
                lockstep([gp_gen(gp0), gp_gen(gp0 + 1)])
            for ri in range(2):
                if d == 0:
                    kb.memset("pool", SN[0][ri][:, :, 0:1], 0.0)
                else:
                    kb.memset("pool", SN[1][ri][:, :, 32:33], 0.0)
                    cp("pool", SN[1][ri][:, :, NCH + 1:NCH + 2], SN[1][ri][:, :, 0:1])
        if not cached:
            dma("sp", st_main[0], Ere.re("p d g n -> p (d g n)"))
            dma("act", st_main[1], nEim.re("p d g n -> p (d g n)"))
            dma("sp", st_main[2], W3.re("p g n -> p (g n)"))
        if "s5_stop3" in dbg:
            return
        P.barrier()
        AR.lo = scr0
        AR2.lo = 0
        Yc = AR.alloc("Yc", [8, 256])
        gg = AR.alloc("gg", [8, 256])
        g2 = AR.alloc("g2", [8, 256])
        sg = [AR.alloc("sg%d" % i, [512]) for i in range(2)]
        ggT = AR2.alloc("ggT", [2, NPOS])
        ggb = AR2.alloc("ggb", [2, NPOS], BF16)
        banks = [kb.psf[0], kb.psf[2], kb.psf[1], kb.psf[3]]

        def r_mm(k0, M, p0):
            for g in range(16):
                gp, h = g // 2, g % 2
                rows = slice(h * 64, (h + 1) * 64)
                bank = banks[h * 2 + (gp // 4)]
                o = bank[0:M, (gp % 4) * 128:(gp % 4 + 1) * 128]
                cf = slice(k0, k0 + M)
                cb_ = slice(k0 + 1, k0 + 1 + M) if k0 == 0 else slice(k0 + 2, k0 + 2 + M)
                mm(o, SN[0][0][rows, gp, cf], Ere[rows, 0, gp, :], True, False)
                mm(o, SN[0][1][rows, gp, cf], nEim[rows, 0, gp, :], False, False)
                mm(o, SN[1][0][rows, gp, cb_], Ere[rows, 1, gp, :], False, False)
                mm(o, SN[1][1][rows, gp, cb_], nEim[rows, 1, gp, :], False, False)
                mm(o, U[:, g, k0:k0 + M], W3[:, g, :], False, True)

        def r_evac(k0, M, p0):
            for h in range(2):
                for q in range(2):
                    bank = banks[h * 2 + q]
                    src = bank[0:M, :].re("p (g t o) -> p g t o", g=4, t=8)
                    dst = Yc[0:M].re("p t (g2 h o) -> p h g2 t o", h=2, o=16)[:, h, 4 * q:4 * q + 4].par("s5yc")
                    cp("act" if h == 0 else "dve", dst, src)

        def r_rest(k0, M, p0):
            yv, gv, g2v = Yc[0:M], gg[0:M], g2[0:M]
            act(g2v, yv, AF.Square)
            ts("dve", g2v, g2v, 0.044715, ALU.mult, 1.0, ALU.add)
            tt("dve", g2v, g2v, yv, ALU.mult)
            act(g2v, g2v, AF.Sigmoid, scale=1.5957691216057308)
            tt("dve", gv, g2v, yv, ALU.mult)
            for t_ in range(8):
                pz = [kb.psf[4], kb.psf[5]]
                for c in range(2):
                    tr(pz[c][:, 0:M], gv[:, t_, c * 128:(c + 1) * 128], identf[0:M, 0:M])
                for c in range(2):
                    cp("act" if c == 0 else "dve", ggT[:, c, p0 + t_:p0 + 8 * M:8].par("s5gg"), pz[c][:, 0:M])
        rb = [blk for blk in kblocks if not (blk[0] == 0 and not with_ctx)]
        r_mm(*rb[0])
        for bi, blk in enumerate(rb):
            r_evac(*blk)
            if bi + 1 < len(rb):
                r_mm(*rb[bi + 1])
            r_rest(*blk)
        lo = 0 if with_ctx else TC
        for c in range(2):
            cp("dve", ggb[:, c, lo:NPOS], ggT[:, c, lo:NPOS])
        for c in range(2):
            pos = lo
            while pos < NPOS:
                n = min(512, NPOS - pos)
                ps = kb.psum_f()
                for k in range(2):
                    mm(ps[:, 0:n], wglu[:, k, c * 128:(c + 1) * 128], ggb[:, k, pos:pos + n], k == 0, k == 1)
                s_ = sg[(pos // 512) % 2]
                act(s_[:, 0:n], ps[:, 0:n], AF.Sigmoid)
                tt("dve", brT[1][c][:, pos:pos + n], s_[:, 0:n], ggT[:, c, pos:pos + n], ALU.mult)
                pos += n

    def phase_merge(l, b, with_ctx, nxt=None):
        AR.reset(top=True)
        lo = 0 if with_ctx else TC
        blocks = []
        pos = lo
        while pos < NPOS:
            n = min(512, NPOS - pos) if pos >= TC else TC - pos
            blocks.append((pos, n))
            pos += n
        ypT = AR.alloc("ypT", [KC, NPOS], BF16, top=True)
        wbr = AR.alloc("wbr", [4, 2, D], BF16, top=True)
        wo = AR.alloc("wo", [KC, D], BF16, top=True)
        wml = [AR.alloc("wml%d" % i, [KC, 4, 128], BF16, top=True) for i in range(2)]

        def load_wml(oc):
            for n_ in range(4):
                c0 = O_MERGE + n_ * D + oc * 128
                dma("pool", wml[oc % 2][:, :, n_, :].par("ldw"), w_in[l, :, c0:c0 + 128].re("(c p) n -> p c n", p=128))
        wg = AR.alloc("w_gate", [KC, 1024], BF16)
        for cc in range(1, 8):
            dma("pool", wg[:, :, cc * 128:(cc + 1) * 128].par("ldw"),
                w_in[l, :, O_GATE + cc * 128:O_GATE + (cc + 1) * 128].re("(c p) n -> p c n", p=128))
        for n_ in range(4):
            dma("pool", wbr[:, n_].par("ldw"), w_branch[l, n_].re("(c p) n -> p c n", p=128))
        load_wml(0)
        load_wml(1)
        dma("pool", wo, w_out[l].re("(c p) n -> p c n", p=128))
        sl = [AR.alloc("sl%d" % i, [512], BF16) for i in range(2)]
        it = 0
        for cc in range(8):
            for (p0, n) in blocks:
                ps = kb.psum_f()
                for k in range(KC):
                    wsl = wst[:, k, 0:128] if cc == 0 else wg[:, k, cc * 128:(cc + 1) * 128]
                    mm(ps[:, 0:n], wsl, hT[:, k, p0:p0 + n], k == 0, k == KC - 1)
                s_ = sl[it % 2]
                it += 1
                act(s_[:, 0:n], ps[:, 0:n], AF.Silu)
                br = brT[cc // 2][cc % 2][:, p0:p0 + n]
                tt("dve", br, br, s_[:, 0:n], ALU.mult)
            if cc == 0 and nxt is not None:
                prefetch(nxt, PF_S5)
        AR.reset()
        sg = [AR.alloc("sg%d" % i, [512]) for i in range(2)]
        tm = [AR.alloc("tm%d" % i, [512]) for i in range(2)]
        acc = [AR.alloc("acc%d" % i, [512]) for i in range(2)]
        it = 0
        ib = 0
        for oc in range(8):
            wm = wml[oc % 2]
            if 1 <= oc and oc + 1 < 8:
                load_wml(oc + 1)
            for (p0, n) in blocks:
                ac = acc[ib % 2]
                ib += 1
                for n_ in range(4):
                    psA = kb.psum_f()
                    for k in range(2):
                        mm(psA[:, 0:n], wbr[:, n_, k, oc * 128:(oc + 1) * 128], brT[n_][k][:, p0:p0 + n], k == 0, k == 1)
                    psB = kb.psum_f()
                    for k in range(KC):
                        mm(psB[:, 0:n], wm[:, k, n_, :], hT[:, k, p0:p0 + n], k == 0, k == KC - 1)
                    s_ = sg[it % 2]
                    t_ = tm[it % 2]
                    it += 1
                    act(s_[:, 0:n], psB[:, 0:n], AF.Sigmoid)
                    if n_ == 0:
                        tt("dve", ac[:, 0:n], psA[:, 0:n], s_[:, 0:n], ALU.mult)
                    elif n_ < 3:
                        tt("dve", t_[:, 0:n], psA[:, 0:n], s_[:, 0:n], ALU.mult)
                        tt("dve", ac[:, 0:n], ac[:, 0:n], t_[:, 0:n], ALU.add)
                    else:
                        tt("dve", t_[:, 0:n], psA[:, 0:n], s_[:, 0:n], ALU.mult)
                        tt("dve", ypT[:, oc, p0:p0 + n].par("ypt"), ac[:, 0:n], t_[:, 0:n], ALU.add)
        AR.reset()
        gbc = AR.alloc("gbc", [2, D])
        dma("sp", gbc[:, 0, :], modD[l, b, 2 * D:3 * D].pbc(128))
        if with_ctx:
            dma("sp", gbc[:, 1, :], modD[l, 2, 2 * D:3 * D].pbc(128))
        xt = [AR.alloc("xt%d" % i, [D]) for i in range(2)]
        yt = [AR.alloc("yt%d" % i, [D]) for i in range(2)]
        for i in range(0 if with_ctx else 2, 18):
            x_t, y_t = xt[i % 2], yt[i % 2]
            dma("sp", x_t, src_tile(l, b, i))
            for hf in range(2):
                ps = kb.psum_f()
                for k in range(KC):
                    mm(ps, ypT[:, k, i * 128:(i + 1) * 128], wo[:, k, hf * 512:(hf + 1) * 512], k == 0, k == KC - 1)
                tt("dve", y_t[:, hf * 512:(hf + 1) * 512], ps, gbc[:, 1 if i < 2 else 0, hf * 512:(hf + 1) * 512], ALU.mult)
            tt("dve", y_t, y_t, x_t, ALU.add)
            if i < 2:
                dst = (tap("xc1", [nb, TC, D]) if "x1out" in dbg else xcmid)[b, i * 128:(i + 1) * 128, :]
            else:
                dst = (xmid if (l < DEPTH - 1 and "x1out" not in dbg) else y_out)[b, (i - 2) * 128:(i - 1) * 128, :]
            dma("act", dst, y_t)

    def dump_br(name, n, lo):
        o = tap(name, [2, 128, NPOS])
        tmpf = AR.alloc("dump_" + name, [NPOS])
        for c in range(2):
            cp("dve", tmpf[:, lo:NPOS], brT[n][c][:, lo:NPOS])
            dma("sp", o[c, :, lo:NPOS], tmpf[:, lo:NPOS])

    prefetch(layers[0], PF_S5)
    for l in layers:
        with_ctx = l < DEPTH - 1
        prep_layer(l)
        for b in range(nb):
            if "prep_only" in dbg:
                continue
            phase_norm(l, b)
            if "hT" in dbg and b == 0 and l == layers[0]:
                o = tap("hT", [KC, 128, NPOS])
                AR.reset()
                tmpf = AR.alloc("dump_hT", [NPOS])
                for c in range(KC):
                    cp("dve", tmpf, hT[:, c, :])
                    dma("sp", o[c], tmpf)
            only = dbg & {"only_lru", "only_pool", "only_mla", "only_s5", "only_norm"}
            if not only or "only_s5" in only:
                phase_s5(l, b, with_ctx)
            if not only or "only_lru" in only:
                phase_lru(l, b, with_ctx)
            if not only or "only_pool" in only:
                phase_pool(l, b, with_ctx)
            if not only or "only_mla" in only:
                phase_mla(l, b, with_ctx)
            if "br" in dbg and b == 0 and l == layers[0]:
                AR.reset()
                lo = 0 if with_ctx else TC
                for n_, nm in enumerate(("mla", "s5", "lru", "pool")):
                    dump_br(nm, n_, lo)
            if "nomerge" not in dbg:
                if b + 1 < nb:
                    nxt = l
                else:
                    li = list(layers).index(l)
                    nxt = layers[li + 1] if li + 1 < len(layers) else None
                phase_merge(l, b, with_ctx, nxt)
    P.barrier()
    P.emit()
    kb.st.close()
    return nc, kb


def _consts():
    ident = np.eye(128, dtype=np.float32)
    rows_n = T // 64
    row = np.repeat(np.arange(rows_n), 64).astype(np.float32)
    col = np.tile(np.arange(64), rows_n).astype(np.float32)
    nf = 8
    inv = (np.float32(10000.0) ** (-np.arange(nf, dtype=np.float32) / nf)).astype(np.float32)
    ar = (row[:, None] * inv).astype(np.float32)
    ac = (col[:, None] * inv).astype(np.float32)
    cr, sr, cc, sc = np.cos(ar), np.sin(ar), np.cos(ac), np.sin(ac)
    ropeC = np.concatenate([cr, cr, cc, cc], axis=1).astype(np.float32)
    ropeS = np.concatenate([-sr, sr, -sc, sc], axis=1).astype(np.float32)
    cpool = np.ones((128, 34), np.float32)
    for c in range(2):
        for p in range(128):
            w = (2, 4, 8, 16)[2 * c + p // 64]
            hw = w // 2
            cpool[p, c] = 1.0 / w
            for t in range(8):
                cnt = t + hw if t < hw else w
                cpool[p, 2 + c * 8 + t] = w / cnt
            for j in range(8):
                dist = 8 - j
                cnt = dist + hw if dist < hw else w
                cpool[p, 18 + c * 8 + j] = w / cnt
    m = np.zeros((128, 3, 128), np.float32)
    for r in range(8):
        for t in range(8):
            if r <= t:
                m[r * 16:(r + 1) * 16, 0, t * 16:(t + 1) * 16] = 1.0
            if r >= t:
                m[r * 16:(r + 1) * 16, 1, t * 16:(t + 1) * 16] = 1.0
            if r == t:
                m[r * 16:(r + 1) * 16, 2, t * 16:(t + 1) * 16] = np.eye(16, dtype=np.float32)
    return {"c_ident": ident, "c_ropeC": ropeC, "c_ropeS": ropeS, "c_pool": cpool, "c_mask": m}


_WNAMES = ["w_ada", "b_ada", "norm_g", "w_in", "mla_q_norm", "mla_kv_norm", "mla_w_uq", "mla_w_ukv", "mla_q_gain",
           "mla_k_gain", "s5_a_re", "s5_a_im", "s5_log_dt", "s5_b_re", "s5_b_im", "s5_c_re", "s5_c_im", "s5_d",
           "s5_w_glu", "lru_conv_w", "lru_conv_b", "lru_lambda", "lru_w_a", "lru_b_a", "lru_w_x", "lru_b_x",
           "pool_w", "pool_b", "pool_scale", "w_branch", "w_out"]


def make_in_maps(inputs, n_cores, nb):
    consts = _consts()
    maps = []
    for r in range(n_cores):
        bs = slice(r * nb, (r + 1) * nb)
        m = {"x": np.ascontiguousarray(inputs["x"][bs], dtype=np.float32),
             "ctx": np.ascontiguousarray(inputs["ctx"][bs], dtype=np.float32)}
        cv = np.zeros((3, D), np.float32)
        cv[0:nb] = np.asarray(inputs["c"], dtype=np.float32)[bs]
        cv[2] = np.asarray(inputs["c_ctx"], dtype=np.float32)
        m["cvec"] = cv
        for k in _WNAMES:
            m[k] = np.ascontiguousarray(inputs[k], dtype=np.float32)
        m.update(consts)
        maps.append(m)
    return maps


_CACHE = {}


def kernel(**inputs):
    n_cores, nb = 8, 2
    if "nc" not in _CACHE:
        _CACHE["nc"] = build_program(nb=nb)[0]
    nc = _CACHE["nc"]
    maps = make_in_maps(inputs, n_cores, nb)
    res = run_bass_kernel_spmd(nc, maps, core_ids=list(range(n_cores)))
    out = np.concatenate([np.asarray(r["y"], dtype=np.float32) for r in res.results], axis=0)
    return out
```

```python
import math
import contextlib
import numpy as np
import concourse.bass as bass
import concourse.mybir as mybir
from concourse.bass_utils import run_bass_kernel_spmd

F32 = mybir.dt.float32
BF16 = mybir.dt.bfloat16
I32 = mybir.dt.int32
AF = mybir.ActivationFunctionType
ALU = mybir.AluOpType
AX = mybir.AxisListType

ENGS = ("pe", "act", "dve", "pool", "sp")
NDMA = 24

D = 1024
KC = 8
T = 2048
TC = 256
NPOS = T + TC
DEPTH = 2
O_KROPE, O_S5, O_LRU, O_CQ, O_POOL, O_GATE, O_MERGE, IN_W = 128, 160, 416, 672, 928, 1184, 2208, 6304
EPS = 1e-6
NCH = NPOS // 8


PAR_OFF = set()


class Buf:
    __slots__ = ("name", "w", "wp", "r")

    def __init__(self, name):
        self.name = name
        self.w = []
        self.wp = []
        self.r = []


class TV:
    par_ = False

    def __init__(self, ap, buf):
        self.ap, self.buf = ap, buf

    def par(self, tag=""):
        t = TV(self.ap, self.buf)
        t.par_ = tag not in PAR_OFF
        return t

    def __getitem__(self, k):
        return TV(self.ap[k], self.buf)

    def re(self, s, **kw):
        return TV(self.ap.rearrange(s, **kw), self.buf)

    def bc(self, shape):
        return TV(self.ap.to_broadcast(list(shape)), self.buf)

    def us(self, ax):
        return TV(self.ap.unsqueeze(ax), self.buf)

    def pbc(self, n):
        return TV(self.ap.partition_broadcast(n), self.buf)

    @property
    def shape(self):
        return tuple(self.ap.shape)


def _bufs(*xs):
    out = []
    for x in xs:
        if isinstance(x, TV):
            out.append(x.buf)
        elif isinstance(x, (list, tuple)):
            out.extend(_bufs(*x))
    return out


def _ap(x):
    return x.ap if isinstance(x, TV) else x


class Prog:
    def __init__(self, nc):
        self.nc = nc
        self.ops = {e: [] for e in ENGS}
        self.cnt = {e: 0 for e in ENGS}
        self.waited = {e: {} for e in ENGS}
        self.dma_i = 0
        self.dma_last = [0] * NDMA

    def _deps(self, eng, reads, writes, par=False):
        toks = []
        for b in reads:
            toks.extend(b.w)
            toks.extend(b.wp)
        for b in writes:
            toks.extend(b.w)
            if not par:
                toks.extend(b.wp)
            toks.extend(b.r)
        need = {}
        for (k, v, e) in toks:
            if e == "pe" and eng == "pe":
                continue
            if self.waited[eng].get(k, 0) >= v:
                continue
            if need.get(k, 0) < v:
                need[k] = v
        for k, v in need.items():
            self.waited[eng][k] = v
        return list(need.items())

    def _mark(self, tok, reads, writes, par=False):
        for b in writes:
            if par:
                best = {}
                for (k, v, e) in b.wp + [tok]:
                    if k not in best or best[k][1] < v:
                        best[k] = (k, v, e)
                b.wp = list(best.values())
            else:
                b.w = [tok]
                b.wp = []
                b.r = []
        for b in reads:
            if b in writes:
                continue
            b.r.append(tok)
            if len(b.r) > 16:
                best = {}
                for (k, v, e) in b.r:
                    if k not in best or best[k][1] < v:
                        best[k] = (k, v, e)
                b.r = list(best.values())

    def op(self, eng, fn, reads=(), writes=(), par=False):
        reads = list(dict.fromkeys(reads))
        writes = list(dict.fromkeys(writes))
        waits = self._deps(eng, reads, writes, par)
        self.cnt[eng] += 1
        tok = (eng, self.cnt[eng], eng)
        self.ops[eng].append((waits, fn, (eng, 1)))
        self._mark(tok, reads, writes, par)

    def dma(self, eng, fn, reads=(), writes=(), par=False):
        reads = list(dict.fromkeys(reads))
        writes = list(dict.fromkeys(writes))
        waits = self._deps(eng, reads, writes, par)
        slot = self.dma_i % NDMA
        self.dma_i += 1
        key = ("dma", slot)
        prev = self.dma_last[slot]
        if prev and self.waited[eng].get(key, 0) < prev:
            waits.append((key, prev))
            self.waited[eng][key] = prev
        val = prev + 16
        self.dma_last[slot] = val
        tok = (key, val, "dma")
        self.ops[eng].append((waits, fn, (key, 16)))
        self._mark(tok, reads, writes, par)

    def barrier(self):
        for E in ENGS:
            waits = []
            for e in ENGS:
                v = self.cnt[e]
                if v and self.waited[E].get(e, 0) < v:
                    waits.append((e, v))
                    self.waited[E][e] = v
            for slot in range(NDMA):
                v = self.dma_last[slot]
                key = ("dma", slot)
                if v and self.waited[E].get(key, 0) < v:
                    waits.append((key, v))
                    self.waited[E][key] = v
            if waits:
                self.ops[E].append((waits, None, None))

    def emit(self):
        nc = self.nc
        sems = {}
        with contextlib.ExitStack() as st:
            for e in ENGS:
                sems[e] = st.enter_context(nc.semaphore("s_" + e))
            for i in range(NDMA):
                sems[("dma", i)] = st.enter_context(nc.semaphore("s_dma%d" % i))
            block = st.enter_context(nc.Block())

            def run(engname):
                def body(eng):
                    for waits, fn, inc in self.ops[engname]:
                        for k, v in waits:
                            eng.wait_ge(sems[k], v)
                        if fn is None:
                            continue
                        ins = fn(eng)
                        ins.then_inc(sems[inc[0]], inc[1])
                return body
            block.tensor(run("pe"))
            block.scalar(run("act"))
            block.vector(run("dve"))
            block.gpsimd(run("pool"))
            block.sync(run("sp"))


class KB:
    def __init__(self, nc, nb, layers, dbg):
        self.nc = nc
        self.P = Prog(nc)
        self.nb = nb
        self.layers = layers
        self.dbg = dbg
        self.st = contextlib.ExitStack()
        self.din = {}
        self.dout = {}
        self.ps_i = 0
        self.psb_i = 0
        self.ps8_i = 0

    def tt(self, eng, out, a, b, op):
        self.P.op(eng, lambda e: e.tensor_tensor(out=out.ap, in0=a.ap, in1=b.ap, op=op), _bufs(a, b), _bufs(out), par=out.par_)

    def ts(self, eng, out, a, s1, op0, s2=None, op1=None):
        if op1 is None:
            self.P.op(eng, lambda e: e.tensor_scalar(out=out.ap, in0=a.ap, scalar1=_ap(s1), scalar2=None, op0=op0),
                      _bufs(a, s1), _bufs(out), par=out.par_)
        else:
            self.P.op(eng, lambda e: e.tensor_scalar(out=out.ap, in0=a.ap, scalar1=_ap(s1), scalar2=_ap(s2), op0=op0, op1=op1),
                      _bufs(a, s1, s2), _bufs(out), par=out.par_)

    def stt(self, out, a, s, b, op0, op1):
        self.P.op("dve", lambda e: e.scalar_tensor_tensor(out=out.ap, in0=a.ap, scalar=_ap(s), in1=b.ap, op0=op0, op1=op1),
                  _bufs(a, s, b), _bufs(out))

    def act(self, out, a, func, bias=0.0, scale=1.0, accum=None):
        if accum is None:
            self.P.op("act", lambda e: e.activation(out=out.ap, in_=a.ap, func=func, bias=_ap(bias), scale=_ap(scale)),
                      _bufs(a, bias, scale), _bufs(out), par=out.par_)
        else:
            self.P.op("act", lambda e: e.activation(out=out.ap, in_=a.ap, func=func, bias=_ap(bias), scale=_ap(scale), accum_out=accum.ap),
                      _bufs(a, bias, scale), _bufs(out, accum))

    def cp(self, eng, out, a):
        if eng == "act":
            self.P.op("act", lambda e: e.copy(out=out.ap, in_=a.ap), _bufs(a), _bufs(out), par=out.par_)
        else:
            self.P.op(eng, lambda e: e.tensor_copy(out=out.ap, in_=a.ap), _bufs(a), _bufs(out), par=out.par_)

    def memset(self, eng, out, v):
        self.P.op(eng, lambda e: e.memset(out.ap, v), [], _bufs(out), par=out.par_)

    def recip(self, out, a):
        self.P.op("dve", lambda e: e.reciprocal(out=out.ap, in_=a.ap), _bufs(a), _bufs(out))

    def mm(self, out, lhsT, rhs, start, stop):
        self.P.op("pe", lambda e: e.matmul(out.ap, lhsT=lhsT.ap, rhs=rhs.ap, start=start, stop=stop), _bufs(lhsT, rhs), _bufs(out))

    def tr(self, out, a, ident):
        self.P.op("pe", lambda e: e.transpose(out.ap, a.ap, ident.ap), _bufs(a, ident), _bufs(out))

    def scan(self, out, d0, d1, init):
        self.P.op("dve", lambda e: e.tensor_tensor_scan(out=out.ap, data0=d0.ap, data1=d1.ap, initial=_ap(init), op0=ALU.mult, op1=ALU.add),
                  _bufs(d0, d1, init), _bufs(out))

    def reduce(self, out, a, op=ALU.add):
        self.P.op("dve", lambda e: e.tensor_reduce(out=out.ap, in_=a.ap, axis=AX.X, op=op), _bufs(a), _bufs(out))

    def dma(self, q, out, a, slow=False):
        if slow:
            self.P.dma(q, lambda e: e.dma_start(out=out.ap, in_=a.ap, allow_slow_non_contiguous=True), _bufs(a), _bufs(out), par=out.par_)
        else:
            self.P.dma(q, lambda e: e.dma_start(out=out.ap, in_=a.ap), _bufs(a), _bufs(out), par=out.par_)

    def dram_in(self, name, shape, dt=F32):
        t = TV(self.nc.dram_tensor(name, list(shape), dt, kind="ExternalInput").ap(), Buf(name))
        self.din[name] = t
        return t

    def dram_out(self, name, shape, dt=F32):
        t = TV(self.nc.dram_tensor(name, list(shape), dt, kind="ExternalOutput").ap(), Buf(name))
        self.dout[name] = t
        return t

    def dram_tmp(self, name, shape, dt=F32):
        return TV(self.nc.dram_tensor(name, list(shape), dt, kind="Internal").ap(), Buf(name))

    def sb(self, name, shape, dt=F32):
        t = self.st.enter_context(self.nc.sbuf_tensor(name, list(shape), dt))
        return TV(t[:], Buf(name))

    def psum_f(self):
        i = self.ps_i % 6
        self.ps_i += 1
        return self.psf[i]

    def psum_b(self):
        i = self.psb_i % 2
        self.psb_i += 1
        return self.psb[i]

    def psum_any(self, bf16=False):
        i = self.ps8_i % 8
        self.ps8_i += 1
        return self.ps8b[i] if bf16 else self.ps8[i]


class Arena:
    def __init__(self, kb, words, ap=None):
        self.kb = kb
        self.words = words
        if ap is None:
            self.f = kb.st.enter_context(kb.nc.sbuf_tensor("arena_f", [128, words], F32))
        else:
            self.f = ap
        self.lo = 0
        self.hi = words

    def alloc(self, name, free, dt=F32, top=False, buf=None, at=None):
        nel = int(np.prod(free))
        w = nel if dt == F32 else (nel + 1) // 2
        w = ((w + 15) // 16) * 16
        if at is not None:
            off = at
        elif top:
            self.hi -= w
            off = self.hi
        else:
            off = self.lo
            self.lo += w
        assert self.lo <= self.hi, (name, self.lo, self.hi)
        assert off + w <= self.words
        if dt != F32:
            ap = self.f[:, off:off + w].bitcast(dt)[:, 0:nel]
        else:
            ap = self.f[:, off:off + nel]
        if len(free) > 1:
            names = " ".join("d%d" % i for i in range(len(free)))
            kw = {"d%d" % i: int(free[i]) for i in range(len(free))}
            ap = ap.rearrange("p (%s) -> p %s" % (names, names), **kw)
        tv = TV(ap, buf if buf is not None else Buf(name))
        tv.off = off
        tv.w = w
        return tv

    def reset(self, top=False):
        self.kb.P.barrier()
        self.lo = 0
        if top:
            self.hi = self.words


def lockstep(gens):
    gens = list(gens)
    while gens:
        nxt = []
        for g in gens:
            try:
                next(g)
                nxt.append(g)
            except StopIteration:
                pass
        gens = nxt


def build_program(nb=2, layers=(0, 1), dbg=None):
    nc = bass.Bass("TRN2", target_bir_lowering=False)
    kb = KB(nc, nb, layers, dbg)
    P = kb.P
    tt, ts, stt, act, cp, mm, tr, dma = kb.tt, kb.ts, kb.stt, kb.act, kb.cp, kb.mm, kb.tr, kb.dma
    dbg = dbg or set()
    PAR_OFF.clear()
    PAR_OFF.update(x[6:] for x in dbg if x.startswith("nopar_"))

    x_in = kb.dram_in("x", [nb, T, D])
    ctx_in = kb.dram_in("ctx", [nb, TC, D])
    cvec = kb.dram_in("cvec", [3, D])
    w_ada = kb.dram_in("w_ada", [DEPTH, D, 3 * D])
    b_ada = kb.dram_in("b_ada", [DEPTH, 3 * D])
    norm_g = kb.dram_in("norm_g", [DEPTH, D])
    w_in = kb.dram_in("w_in", [DEPTH, D, IN_W])
    mla_q_norm = kb.dram_in("mla_q_norm", [DEPTH, 256])
    mla_kv_norm = kb.dram_in("mla_kv_norm", [DEPTH, 128])
    mla_w_uq = kb.dram_in("mla_w_uq", [DEPTH, 256, 384])
    mla_w_ukv = kb.dram_in("mla_w_ukv", [DEPTH, 128, 512])
    mla_q_gain = kb.dram_in("mla_q_gain", [DEPTH, 96])
    mla_k_gain = kb.dram_in("mla_k_gain", [DEPTH, 96])
    s5_a_re = kb.dram_in("s5_a_re", [DEPTH, 2, 16, 64])
    s5_a_im = kb.dram_in("s5_a_im", [DEPTH, 2, 16, 64])
    s5_log_dt = kb.dram_in("s5_log_dt", [DEPTH, 2, 16])
    s5_b_re = kb.dram_in("s5_b_re", [DEPTH, 2, 16, 64, 16])
    s5_b_im = kb.dram_in("s5_b_im", [DEPTH, 2, 16, 64, 16])
    s5_c_re = kb.dram_in("s5_c_re", [DEPTH, 2, 16, 16, 64])
    s5_c_im = kb.dram_in("s5_c_im", [DEPTH, 2, 16, 16, 64])
    s5_d = kb.dram_in("s5_d", [DEPTH, 256])
    s5_w_glu = kb.dram_in("s5_w_glu", [DEPTH, 256, 256])
    lru_conv_w = kb.dram_in("lru_conv_w", [DEPTH, 4, 256])
    lru_conv_b = kb.dram_in("lru_conv_b", [DEPTH, 256])
    lru_lambda = kb.dram_in("lru_lambda", [DEPTH, 2, 256])
    lru_w_a = kb.dram_in("lru_w_a", [DEPTH, 2, 4, 64, 64])
    lru_b_a = kb.dram_in("lru_b_a", [DEPTH, 2, 256])
    lru_w_x = kb.dram_in("lru_w_x", [DEPTH, 2, 4, 64, 64])
    lru_b_x = kb.dram_in("lru_b_x", [DEPTH, 2, 256])
    pool_w = kb.dram_in("pool_w", [DEPTH, 4, 64, 64])
    pool_b = kb.dram_in("pool_b", [DEPTH, 256])
    pool_scale = kb.dram_in("pool_scale", [DEPTH, 256])
    w_branch = kb.dram_in("w_branch", [DEPTH, 4, 256, D])
    w_out = kb.dram_in("w_out", [DEPTH, D, D])
    c_ident = kb.dram_in("c_ident", [128, 128])
    c_ropeC = kb.dram_in("c_ropeC", [T, 32])
    c_ropeS = kb.dram_in("c_ropeS", [T, 32])
    c_pool = kb.dram_in("c_pool", [128, 2 + 16 + 16])
    c_mask = kb.dram_in("c_mask", [128, 3, 128])
    y_out = kb.dram_out("y", [nb, T, D])
    xmid = kb.dram_tmp("xmid", [nb, T, D])
    xcmid = kb.dram_tmp("xcmid", [nb, TC, D])
    modD = kb.dram_tmp("modD", [DEPTH, 3, 3 * D])
    st_main = kb.dram_tmp("st_main", [3, 128, 2048])
    st_dir = kb.dram_tmp("st_dir", [2, 128, 2048 + 16 * NCH + 8])
    dbg_out = {}

    def tap(name, shape):
        if name not in dbg_out:
            dbg_out[name] = kb.dram_out("dbg_" + name, shape)
        return dbg_out[name]

    kb.ps8 = []
    kb.pspair = []
    for i in range(4):
        t = kb.st.enter_context(nc.psum_tensor("psp%d" % i, [128, 1024], F32))
        kb.pspair.append(t[:])
        for hh in range(2):
            kb.ps8.append(TV(t[:][:, hh * 512:(hh + 1) * 512], Buf("psf%d" % (2 * i + hh))))
    kb.psf = kb.ps8[0:6]
    kb.psb = [TV(kb.ps8[i].ap.bitcast(BF16), kb.ps8[i].buf) for i in (6, 7)]
    kb.ps8b = [TV(kb.ps8[i].ap.bitcast(BF16), kb.ps8[i].buf) for i in range(8)]

    hT = kb.sb("hT", [128, KC, NPOS], BF16)
    bigbr = kb.sb("bigbr", [128, 8 * NPOS], BF16)
    _slot = {0: 0, 2: 2, 3: 4, 1: 6}
    brT = [[TV(bigbr.ap[:, (_slot[n] + c) * NPOS:(_slot[n] + c + 1) * NPOS], Buf("brT%d%d" % (n, c))) for c in range(2)]
           for n in range(4)]
    identf = kb.sb("identf", [128, 128])
    identb = kb.sb("identb", [128, 128], BF16)
    ropeC = kb.sb("ropeC", [128, 16, 32])
    ropeS = kb.sb("ropeS", [128, 16, 32])
    cpool = kb.sb("cpool", [128, 34])
    cmask = kb.sb("cmask", [128, 3, 128])
    modA = kb.sb("modA", [128, 3, KC])
    modS = kb.sb("modS", [128, 3, KC])
    lp = kb.sb("lp", [128, 64])
    lruBD = kb.sb("lruBD", [128, 2, 2, 2, 128])
    poolBD = kb.sb("poolBD", [128, 2, 128])
    wukv = kb.sb("wukv", [128, 512], BF16)
    wuq = kb.sb("wuq", [128, 2, 384], BF16)
    gains = kb.sb("gains", [128, 2, 96])
    wglu = kb.sb("wglu", [128, 2, 256], BF16)
    AR = Arena(kb, 28000)
    wst = kb.sb("wst", [128, KC, 416], BF16)

    def prefetch(l, parts):
        for (d0, c0, n) in parts:
            dma("pool", wst[:, :, d0:d0 + n].par("wst"), w_in[l, :, c0:c0 + n].re("(c p) n -> p c n", p=128))
    PF_S5 = [(0, O_S5, 256)]
    PF_LRU = [(0, O_LRU, 256)]
    PF_POOL = [(0, O_POOL, 256)]
    PF_MLA = [(0, 0, 160), (160, O_CQ, 256)]
    PF_GATE0 = [(0, O_GATE, 128)]
    AR2 = Arena(kb, 3 * NPOS, ap=bigbr.ap[:, 0:6 * NPOS].bitcast(F32))

    dma("sp", identf, c_ident)
    cp("dve", identb, identf)
    dma("sp", ropeC, c_ropeC.re("(i p) f -> p i f", p=128))
    dma("sp", ropeS, c_ropeS.re("(i p) f -> p i f", p=128))
    dma("sp", cpool, c_pool)
    dma("sp", cmask, c_mask)

    LP_CW = 0
    LP_CB = 8
    LP_NSP = 10
    LP_NSP2 = 14
    LP_BA = 18
    LP_BX = 22
    LP_PSC = 26
    LP_PBS = 28
    LP_G = 30
    LP_TMP = 40

    def prep_layer(l):
        AR.reset(top=True)
        cT = AR.alloc("cT", [KC, 3])
        for v in range(3):
            dma("sp", cT[:, :, v], cvec[v].re("(c p) -> p c", p=128), slow=True)
        cact = AR.alloc("cact", [KC, 3])
        act(cact, cT, AF.Silu)
        brow = AR.alloc("brow", [3 * D])
        dma("sp", brow[0:3, :], b_ada[l:l + 1, :].re("o n -> (o n)").pbc(3))
        modrow = AR.alloc("modrow", [3 * D])
        wa = [AR.alloc("wa%d" % i, [KC, 512]) for i in range(4)]
        for cb in range(4):
            for hh in range(2):
                dma("sp" if hh == 0 else "act", wa[cb][:, hh * 4:(hh + 1) * 4, :].par("wa"),
                    w_ada[l, hh * 512:(hh + 1) * 512, cb * 512:(cb + 1) * 512].re("(c p) n -> p c n", p=128))
        for cb in range(6):
            w = wa[cb % 4]
            if cb >= 4:
                for hh in range(2):
                    dma("sp" if hh == 0 else "act", w[:, hh * 4:(hh + 1) * 4, :].par("wa"),
                        w_ada[l, hh * 512:(hh + 1) * 512, cb * 512:(cb + 1) * 512].re("(c p) n -> p c n", p=128))
            ps = kb.psum_f()
            for k in range(KC):
                mm(ps[0:3, :], cact[:, k, :], w[:, k, :], k == 0, k == KC - 1)
            tt("dve", modrow[0:3, cb * 512:(cb + 1) * 512], ps[0:3, :], brow[0:3, cb * 512:(cb + 1) * 512], ALU.add)
        dma("sp", modD[l], modrow[0:3, :])
        sc = AR.alloc("sc", [3, KC])
        for v in range(3):
            dma("sp", modS[:, v, :], modD[l, v, 0:D].re("(c p) -> p c", p=128), slow=True)
            dma("sp", sc[:, v, :], modD[l, v, D:2 * D].re("(c p) -> p c", p=128), slow=True)
        dma("sp", lp[:, LP_G:LP_G + 8], norm_g[l].re("(c p) -> p c", p=128), slow=True)
        for v in range(3):
            stt(modA[:, v, :], sc[:, v, :], 1.0, lp[:, LP_G:LP_G + 8], ALU.add, ALU.mult)
        for k in range(4):
            dma("sp", lp[:, LP_CW:LP_CW + 8].re("p (c k) -> p c k", c=2)[:, :, k], lru_conv_w[l, k].re("(c p) -> p c", p=128), slow=True)
        dma("sp", lp[:, LP_CB:LP_CB + 2], lru_conv_b[l].re("(c p) -> p c", p=128), slow=True)
        lam = lp[:, LP_TMP:LP_TMP + 4]
        for d in range(2):
            dma("sp", lam[:, d * 2:d * 2 + 2], lru_lambda[l, d].re("(c p) -> p c", p=128), slow=True)
            dma("sp", lp[:, LP_BA + d * 2:LP_BA + d * 2 + 2], lru_b_a[l, d].re("(c p) -> p c", p=128), slow=True)
            dma("sp", lp[:, LP_BX + d * 2:LP_BX + d * 2 + 2], lru_b_x[l, d].re("(c p) -> p c", p=128), slow=True)
        t0 = lp[:, LP_TMP + 4:LP_TMP + 8]
        t1 = lp[:, LP_TMP + 8:LP_TMP + 12]
        t2 = lp[:, LP_TMP + 12:LP_TMP + 16]
        t3 = lp[:, LP_TMP + 16:LP_TMP + 20]
        ts("dve", t0, lam, -1.0, ALU.mult)
        tt("dve", t0, t0, lam, ALU.max)
        act(t1, t0, AF.Exp, scale=-1.0)
        ts("dve", t2, t1, 2.0, ALU.add)
        kb.recip(t2, t2)
        tt("dve", t2, t2, t1, ALU.mult)
        tt("dve", t3, t2, t2, ALU.mult)
        ts("dve", t0, t3, 1.0 / 11.0, ALU.mult, 1.0 / 9.0, ALU.add)
        for cf in (1.0 / 7.0, 1.0 / 5.0, 1.0 / 3.0, 1.0):
            tt("dve", t0, t0, t3, ALU.mult)
            ts("dve", t0, t0, cf, ALU.add)
        tt("dve", t0, t0, t2, ALU.mult)
        ts("dve", t1, lam, -1.0, ALU.mult, 0.0, ALU.max)
        stt(t0, t0, 2.0, t1, ALU.mult, ALU.add)
        ts("dve", lp[:, LP_NSP:LP_NSP + 4], t0, -8.0, ALU.mult)
        ts("dve", lp[:, LP_NSP2:LP_NSP2 + 4], t0, -16.0, ALU.mult)
        kb.memset("dve", lruBD, 0.0)
        for d in range(2):
            for gi, wsrc in enumerate((lru_w_a, lru_w_x)):
                for c in range(2):
                    for h in range(2):
                        dma("sp" if (c + h) % 2 == 0 else "act", lruBD[h * 64:(h + 1) * 64, d, gi, c, h * 64:(h + 1) * 64].par("ldw"), wsrc[l, d, 2 * c + h])
        kb.memset("dve", poolBD, 0.0)
        for c in range(2):
            for h in range(2):
                dma("sp", poolBD[h * 64:(h + 1) * 64, c, h * 64:(h + 1) * 64].par("ldw"), pool_w[l, 2 * c + h])
        dma("sp", lp[:, LP_PSC:LP_PSC + 2], pool_scale[l].re("(c p) -> p c", p=128), slow=True)
        dma("sp", lp[:, LP_PBS:LP_PBS + 2], pool_b[l].re("(c p) -> p c", p=128), slow=True)
        tt("dve", lp[:, LP_PBS:LP_PBS + 2], lp[:, LP_PBS:LP_PBS + 2], lp[:, LP_PSC:LP_PSC + 2], ALU.mult)
        kvn = lp[:, LP_TMP + 20:LP_TMP + 21]
        qn = lp[:, LP_TMP + 21:LP_TMP + 23]
        dma("sp", kvn, mla_kv_norm[l].re("(p o) -> p o", o=1), slow=True)
        dma("sp", qn, mla_q_norm[l].re("(c p) -> p c", p=128), slow=True)
        wtmp = AR.alloc("wtmp", [2, 512])
        dma("sp", wtmp[:, 0, :], mla_w_ukv[l])
        ts("dve", wukv, wtmp[:, 0, :], kvn, ALU.mult)
        wtmp2 = AR.alloc("wtmp2", [2, 384])
        dma("sp", wtmp2, mla_w_uq[l].re("(c p) n -> p c n", p=128))
        for c in range(2):
            ts("dve", wuq[:, c, :], wtmp2[:, c, :], qn[:, c:c + 1], ALU.mult)
        dma("sp", gains[:, 0, :], mla_q_gain[l:l + 1, :].re("o n -> (o n)").pbc(128))
        dma("sp", gains[:, 1, :], mla_k_gain[l:l + 1, :].re("o n -> (o n)").pbc(128))
        ts("dve", gains[:, 0, :], gains[:, 0, :], 96.0 ** -0.5, ALU.mult)
        dma("pool", wglu, s5_w_glu[l].re("(c p) n -> p c n", p=128))

    def src_tile(l, b, i):
        if i < 2:
            return (ctx_in if l == 0 else xcmid)[b, i * 128:(i + 1) * 128, :]
        return (x_in if l == 0 else xmid)[b, (i - 2) * 128:(i - 1) * 128, :]

    def phase_norm(l, b):
        AR.reset(top=True)
        NX = 4
        xt = [AR.alloc("xt%d" % i, [D]) for i in range(NX)]
        junk = [AR.alloc("junk%d" % i, [D]) for i in range(NX)]
        xn = [AR.alloc("xn%d" % i, [D], BF16) for i in range(NX)]
        st4 = [AR.alloc("st%d" % i, [4]) for i in range(NX)]
        def tile_gen(i):
            v = 2 if i < 2 else b
            x_t, xn_t, s4 = xt[i % NX], xn[i % NX], st4[i % NX]
            dma("sp" if i % 2 == 0 else "act", x_t, src_tile(l, b, i))
            yield
            act(junk[i % NX], x_t, AF.Square, accum=s4[:, 0:1])
            yield
            ts("dve", s4[:, 1:2], s4[:, 0:1], 1.0 / D, ALU.mult, EPS, ALU.add)
            yield
            act(s4[:, 2:3], s4[:, 1:2], AF.Sqrt)
            yield
            kb.recip(s4[:, 3:4], s4[:, 2:3])
            ts("dve", xn_t, x_t, s4[:, 3:4], ALU.mult)
            yield
            psA = kb.psum_any(bf16=True)
            psB = kb.psum_any(bf16=True)
            for c in range(KC):
                pz = psA if c % 2 == 0 else psB
                tr(pz[:, (c // 2) * 128:(c // 2 + 1) * 128], xn_t[:, c * 128:(c + 1) * 128], identb)
            yield
            for c in range(KC):
                o = hT[:, c, i * 128:(i + 1) * 128].par("norm")
                if c % 2 == 0:
                    act(o, psA[:, (c // 2) * 128:(c // 2 + 1) * 128], AF.Identity, bias=modS[:, v, c:c + 1], scale=modA[:, v, c:c + 1])
                else:
                    ts("dve", o, psB[:, (c // 2) * 128:(c // 2 + 1) * 128], modA[:, v, c:c + 1], ALU.mult, modS[:, v, c:c + 1], ALU.add)
            yield
        for g0 in range(0, 18, NX):
            lockstep([tile_gen(i) for i in range(g0, min(g0 + NX, 18))])

    def load_win(l, name, c0, ncols, top=False):
        w = AR.alloc(name, [KC, ncols], BF16, top=top)
        dma("pool", w, w_in[l, :, c0:c0 + ncols].re("(c p) n -> p c n", p=128))
        return w

    def proj_fm(w, col0, dst, p0, p1, evac):
        pos = p0
        while pos < p1:
            n = min(512, p1 - pos)
            ps = kb.psum_f()
            for k in range(KC):
                mm(ps[:, 0:n], w[:, k, col0:col0 + 128], hT[:, k, pos:pos + n], k == 0, k == KC - 1)
            evac(ps, pos, n)
            pos += n

    def phase_lru(l, b, with_ctx):
        AR.reset()
        w = wst[:, :, 0:256]
        xr_ = [AR.alloc("xr%d" % c, [NPOS]) for c in range(2)]
        xcc = [AR.alloc("xc%d" % c, [NPOS]) for c in range(2)]
        ys_ = [AR.alloc("ysum%d" % c, [NPOS]) for c in range(2)]
        NB_ = 512
        tmp_sets = [{nm: AR.alloc("%s%d" % (nm, q), [NB_]) for nm in ("r", "i", "a", "a2", "bb")} for q in range(4)]
        for c in range(2):
            proj_fm(w, c * 128, None, 0, NPOS, lambda ps, pos, n, c=c: cp("act", xr_[c][:, pos:pos + n].par("proj"), ps[:, 0:n]))
        prefetch(l, PF_POOL)
        for c in range(2):
            cw = lambda k: lp[:, LP_CW + c * 4 + k:LP_CW + c * 4 + k + 1]
            for (s0, s1) in ((0, TC), (TC, NPOS)):
                ts("dve", xcc[c][:, s0:s1], xr_[c][:, s0:s1], cw(2), ALU.mult, lp[:, LP_CB + c:LP_CB + c + 1], ALU.add)
                stt(xcc[c][:, s0 + 1:s1], xr_[c][:, s0:s1 - 1], cw(1), xcc[c][:, s0 + 1:s1], ALU.mult, ALU.add)
                stt(xcc[c][:, s0 + 2:s1], xr_[c][:, s0:s1 - 2], cw(0), xcc[c][:, s0 + 2:s1], ALU.mult, ALU.add)
                stt(xcc[c][:, s0:s1 - 1], xr_[c][:, s0 + 1:s1], cw(3), xcc[c][:, s0:s1 - 1], ALU.mult, ALU.add)
        blocks = [(0, TC)] + [(TC + j * NB_, min(TC + (j + 1) * NB_, NPOS)) for j in range((T + NB_ - 1) // NB_)]

        def chain(d, c):
            dc = d * 2 + c
            tmp = tmp_sets[dc]
            out = ys_[c] if d == 0 else xr_[c]
            order = blocks if d == 0 else [blocks[0]] + blocks[:0:-1]
            prev = None
            for (s0, s1) in order:
                n = s1 - s0
                r_, i_, a_, a2_, bb_ = (tmp[k][:, 0:n] for k in ("r", "i", "a", "a2", "bb"))
                for gi, dst in ((0, r_), (1, i_)):
                    q = 0
                    while q < n:
                        m = min(512, n - q)
                        ps = kb.psum_f()
                        mm(ps[:, 0:m], lruBD[:, d, gi, c, :], xcc[c][:, s0 + q:s0 + q + m], True, True)
                        bcol = (LP_BA if gi == 0 else LP_BX) + dc
                        act(dst[:, q:q + m], ps[:, 0:m], AF.Sigmoid, bias=lp[:, bcol:bcol + 1])
                        q += m
                yield
                act(a_, r_, AF.Exp, scale=lp[:, LP_NSP + dc:LP_NSP + dc + 1])
                act(a2_, r_, AF.Exp, scale=lp[:, LP_NSP2 + dc:LP_NSP2 + dc + 1])
                tt("dve", bb_, i_, xcc[c][:, s0:s1], ALU.mult)
                yield
                ts("dve", a2_, a2_, -1.0, ALU.mult, 1.0, ALU.add)
                yield
                act(a2_, a2_, AF.Sqrt)
                yield
                tt("dve", bb_, bb_, a2_, ALU.mult)
                init = 0.0 if prev is None else prev
                if d == 0:
                    kb.scan(out[:, s0:s1], a_, bb_, init)
                    prev = out[:, s1 - 1:s1]
                else:
                    kb.scan(out[:, s0:s1][:, ::-1], a_[:, ::-1], bb_[:, ::-1], init)
                    prev = out[:, s0:s0 + 1]
                yield
        lockstep([chain(0, 0), chain(1, 0), chain(0, 1), chain(1, 1)])
        for c in range(2):
            lo = 0 if with_ctx else TC
            tt("dve", brT[2][c][:, lo:NPOS], ys_[c][:, lo:NPOS], xr_[c][:, lo:NPOS], ALU.add)

    AR_car = [kb.sb("car%d" % i, [128, 1]) for i in range(2)]

    def phase_pool(l, b, with_ctx):
        AR.reset()
        w = wst[:, :, 0:256]
        xp = AR.alloc("xp", [2, NPOS])
        cs_ = [AR.alloc("cs%d" % c, [NPOS + 2 * 17 + 2]) for c in range(2)]
        pm_ = [AR.alloc("pm%d" % c, [NPOS]) for c in range(2)]
        ones = AR.alloc("ones", [T])
        kb.memset("dve", ones, 1.0)
        for c in range(2):
            proj_fm(w, c * 128, xp, 0, NPOS, lambda ps, pos, n, c=c: cp("act", xp[:, c, pos:pos + n].par("proj"), ps[:, 0:n]))
        prefetch(l, PF_MLA)
        segs = [(0, TC, 0)] + [(TC, NPOS, TC + 17)]
        if not with_ctx:
            segs = segs[1:]
        def chunk_gen(c):
            for (s0, s1, o0) in segs:
                L = s1 - s0
                kb.memset("pool", cs_[c][:, o0:o0 + 9], 0.0)
                kb.scan(cs_[c][:, o0 + 9:o0 + 9 + L], ones[:, 0:L], xp[:, c, s0:s1], 0.0)
                yield
                ts("dve", cs_[c][:, o0 + 9 + L:o0 + 17 + L], cs_[c][:, o0:o0 + 8], cs_[c][:, o0 + 8 + L:o0 + 9 + L], ALU.add)
                yield
                for h in range(2):
                    hw = (1, 2, 4, 8)[2 * c + h]
                    rows = slice(h * 64, (h + 1) * 64)
                    base = o0 + 8
                    tt("dve", pm_[c][rows, s0:s1], cs_[c][rows, base + hw:base + hw + L], cs_[c][rows, base - hw:base - hw + L], ALU.subtract)
                yield
                ts("dve", pm_[c][:, s0:s1], pm_[c][:, s0:s1], cpool[:, c:c + 1], ALU.mult)
                yield
                tt("dve", pm_[c][:, s0:s0 + 8], pm_[c][:, s0:s0 + 8], cpool[:, 2 + c * 8:2 + c * 8 + 8], ALU.mult)
                tt("dve", pm_[c][:, s1 - 8:s1], pm_[c][:, s1 - 8:s1], cpool[:, 18 + c * 8:18 + c * 8 + 8], ALU.mult)
                yield
                tt("dve", pm_[c][:, s0:s1], pm_[c][:, s0:s1], xp[:, c, s0:s1], ALU.subtract)
                yield
                pos = s0
                while pos < s1:
                    n = min(512, s1 - pos)
                    ps = kb.psum_f()
                    mm(ps[:, 0:n], poolBD[:, c, :], pm_[c][:, pos:pos + n], True, True)
                    act(brT[3][c][:, pos:pos + n].par("pool"), ps[:, 0:n], AF.Identity, bias=lp[:, LP_PBS + c:LP_PBS + c + 1],
                        scale=lp[:, LP_PSC + c:LP_PSC + c + 1])
                    pos += n
                    yield
        lockstep([chunk_gen(0), chunk_gen(1)])

    def phase_mla(l, b, with_ctx):
        AR.reset()
        wkv = wst[:, :, 0:160]
        wq = wst[:, :, 160:416]
        qT = AR.alloc("qT", [4, NPOS], BF16)
        kT = AR.alloc("kT", [4, NPOS], BF16)
        Vt = AR.alloc("Vt", [18, 4, 68], BF16)
        lo_fixed = AR.lo
        kb.memset("dve", Vt.re("p a b c -> p (a b c)"), 1.0)
        assert kT.off == qT.off + qT.w
        zv = AR.f[:, qT.off:kT.off + kT.w]
        P.op("act", lambda e, zv=zv: e.memzero(zv), [], [qT.buf, kT.buf])
        NS = 3
        sm = [AR.alloc("sm%d" % i, [32]) for i in range(NS)]
        kr = [AR.alloc("kr%d" % i, [32]) for i in range(NS)]
        cn = [AR.alloc("cn%d" % i, [384], BF16) for i in range(NS)]
        cnT = [AR.alloc("cnT%d" % i, [3, 128], BF16) for i in range(NS)]
        sq = [AR.alloc("sq%d" % i, [704]) for i in range(NS)]
        qk = [AR.alloc("qk%d" % i, [2, 4, 96]) for i in range(NS)]
        rg = [AR.alloc("rg%d" % i, [2, 4, 96]) for i in range(NS)]
        qkb = [AR.alloc("qkb%d" % i, [2, 4, 96], BF16) for i in range(NS)]
        rt = [AR.alloc("rt%d" % i, [2, 2, 4, 32]) for i in range(NS)]
        kvs_ = [AR.alloc("kvs%d" % i, [512]) for i in range(NS)]
        qs_ = [AR.alloc("qs%d" % i, [384]) for i in range(NS)]

        def info(i):
            is_ctx = i < 2
            do_q = (not is_ctx) or with_ctx
            return is_ctx, do_q, i % NS, slice(i * 128, (i + 1) * 128)

        def stage_a(i):
            is_ctx, do_q, j, pos = info(i)
            s, cn_, cnT_, sq_ = sm[j], cn[j], cnT[j], sq[j]
            ps1 = kb.psum_any()
            for k in range(KC):
                mm(ps1[:, 0:160], hT[:, k, pos], wkv[:, k, :], k == 0, k == KC - 1)
            if do_q:
                for k in range(KC):
                    mm(ps1[:, 160:416], hT[:, k, pos], wq[:, k, :], k == 0, k == KC - 1)
            yield
            act(sq_[:, 0:128], ps1[:, 0:128], AF.Square, accum=s[:, 0:1])
            if do_q:
                act(sq_[:, 128:384], ps1[:, 160:416], AF.Square, accum=s[:, 1:2])
            else:
                kb.memset("pool", s[:, 1:2], 1.0)
            cp("act", kr[j], ps1[:, 128:160])
            yield
            act(s[:, 2:3], s[:, 0:1], AF.Sqrt, scale=1.0 / 128, bias=EPS)
            act(s[:, 3:4], s[:, 1:2], AF.Sqrt, scale=1.0 / 256, bias=EPS)
            yield
            kb.recip(s[:, 4:6], s[:, 2:4])
            ts("dve", cn_[:, 0:128], ps1[:, 0:128], s[:, 4:5], ALU.mult)
            if do_q:
                ts("dve", cn_[:, 128:384], ps1[:, 160:416], s[:, 5:6], ALU.mult)
            yield
            psT = kb.psum_any(bf16=True)
            for c in range(3 if do_q else 1):
                tr(psT[:, c * 128:(c + 1) * 128], cn_[:, c * 128:(c + 1) * 128], identb)
            yield
            cp("act", cnT_[:, 0:(3 if do_q else 1), :], psT[:, 0:(384 if do_q else 128)].re("p (c n) -> p c n", n=128))
            yield

        def stage_b(i):
            is_ctx, do_q, j, pos = info(i)
            s, cnT_, sq_, qk_, qkb_, rt_, rg_ = sm[j], cnT[j], sq[j], qk[j], qkb[j], rt[j], rg[j]
            pskv = kb.psum_any()
            mm(pskv, cnT_[:, 0, :], wukv, True, True)
            if do_q:
                psq = kb.psum_any()
                for c in range(2):
                    mm(psq[:, 0:384], cnT_[:, 1 + c, :], wuq[:, c, :], c == 0, c == 1)
            yield
            cp("act", kvs_[j], pskv)
            if do_q:
                cp("act", qs_[j], psq[:, 0:384])
            kv3 = kvs_[j].re("p (h e) -> p h e", h=4)
            q3 = qs_[j].re("p (h e) -> p h e", h=4)
            krope = kr[j]
            act(sq_[:, 640:672], krope, AF.Square, accum=s[:, 7:8])
            cp("act", Vt[:, i, :, 0:64].par("mlav"), kv3[:, :, 64:128])
            yield
            ksq = sq_[:, 0:256].re("p (h e) -> p h e", h=4)
            qsq = sq_[:, 256:640].re("p (h e) -> p h e", h=4)
            tt("pool", ksq, kv3[:, :, 0:64], kv3[:, :, 0:64], ALU.mult)
            if do_q:
                tt("pool", qsq, q3, q3, ALU.mult)
            yield
            kb.reduce(s[:, 12:16], ksq)
            if do_q:
                kb.reduce(s[:, 8:12], qsq)
            else:
                kb.memset("dve", s[:, 8:12], 1.0)
            ts("dve", s[:, 12:16], s[:, 12:16], s[:, 7:8], ALU.add)
            yield
            act(s[:, 8:16], s[:, 8:16], AF.Sqrt, scale=1.0 / 96, bias=EPS)
            yield
            kb.recip(s[:, 16:24], s[:, 8:16])
            yield
            tt("dve", rg_, s[:, 16:24].re("p (w h) -> p w h", w=2).us(3).bc([128, 2, 4, 96]),
               gains.us(2).bc([128, 2, 4, 96]), ALU.mult)
            yield
            if do_q:
                tt("dve", qk_[:, 0].par("qk"), q3, rg_[:, 0], ALU.mult)
            else:
                kb.memset("pool", qk_[:, 0].par("qk"), 0.0)
            tt("dve", qk_[:, 1, :, 0:64].par("qk"), kv3[:, :, 0:64], rg_[:, 1, :, 0:64], ALU.mult)
            tt("pool", qk_[:, 1, :, 64:96].par("qk"), krope.us(1).bc([128, 4, 32]), rg_[:, 1, :, 64:96], ALU.mult)
            yield
            if not is_ctx:
                ti = i - 2
                v = qk_[:, :, :, 64:96]
                t1_ = rt_[:, 0].par("rt")
                t2_ = rt_[:, 1]
                Cb = ropeC[:, ti, :].us(1).us(1).bc([128, 2, 4, 32])
                tt("pool", t1_, v, Cb, ALU.mult)
                for a in range(2):
                    for s_ in range(2):
                        o_ = t2_[:, :, :, a * 16 + s_ * 8:a * 16 + s_ * 8 + 8].par("rt")
                        i_ = qk_[:, :, :, 64 + a * 16 + (1 - s_) * 8:64 + a * 16 + (1 - s_) * 8 + 8]
                        Sb = ropeS[:, ti, a * 16 + s_ * 8:a * 16 + s_ * 8 + 8].us(1).us(1).bc([128, 2, 4, 8])
                        tt("dve" if (a + s_) % 2 == 0 else "pool", o_, i_, Sb, ALU.mult)
                yield
                tt("dve", v, t1_, t2_, ALU.add)
            cp("dve", qkb_, qk_)
            yield

        def stage_c(i):
            is_ctx, do_q, j, pos = info(i)
            qkb_ = qkb[j]
            psT2 = kb.psum_any(bf16=True)
            for w_ in range(2):
                if w_ == 0 and not do_q:
                    continue
                for h in range(4):
                    tr(psT2[0:96, (w_ * 4 + h) * 128:(w_ * 4 + h + 1) * 128], qkb_[:, w_, h, :], identb)
            yield
            if do_q:
                cp("act", qT[0:96, :, pos].par("mlaqk"), psT2[0:96, 0:512].re("p (h n) -> p h n", h=4))
            cp("act", kT[0:96, :, pos].par("mlaqk"), psT2[0:96, 512:1024].re("p (h n) -> p h n", h=4))

        def tile_gen(i):
            yield from stage_a(i)
            yield from stage_b(i)
            yield from stage_c(i)
        for g0 in range(0, 18, NS):
            lockstep([tile_gen(i) for i in range(g0, g0 + NS)])
        if "mla_stop1" in dbg:
            return
        prefetch(l, PF_GATE0)
        P.barrier()
        AR.lo = lo_fixed
        omla = AR.alloc("omla", [18, 256], BF16)
        pT = [AR.alloc("pT%d" % i, [2, 512], BF16) for i in range(3)]
        rc = [AR.alloc("rc%d" % i, [4]) for i in range(2)]
        jobs = []
        if with_ctx:
            jobs.append((0, TC, 0, 2))
        for qb in range(4):
            jobs.append((TC + qb * 512, TC + (qb + 1) * 512, 0, 18))
        pairs = []
        for (q0, q1, kt0, kt1) in jobs:
            for h in range(4):
                for kt in range(kt0, kt1, 2):
                    pairs.append((q0, q1, kt0, kt1, h, kt))
        SB = (0, 3)

        def s_mm(pi_):
            q0, q1, kt0, kt1, h, kt = pairs[pi_]
            tsel = SB[pi_ % 2]
            for j in range(2):
                mm(kb.ps8[2 * tsel + j][:, 0:q1 - q0], kT[:, h, (kt + j) * 128:(kt + j + 1) * 128], qT[:, h, q0:q1], True, True)
        s_mm(0)
        for pi_, (q0, q1, kt0, kt1, h, kt) in enumerate(pairs):
            nq = q1 - q0
            nqt = nq // 128
            pso = [kb.psf[2 + t_] for t_ in range(nqt)]
            if pi_ + 1 < len(pairs):
                s_mm(pi_ + 1)
            tsel = SB[pi_ % 2]
            p_ = pT[pi_ % 3]
            src = kb.pspair[tsel].rearrange("p (j n) -> p j n", j=2)[:, :, 0:nq]
            dst = p_.ap[:, :, 0:nq]
            P.op("act", lambda e, src=src, dst=dst: e.activation(out=dst, in_=src, func=AF.Exp, bias=0.0, scale=1.0),
                 [kb.ps8[2 * tsel].buf, kb.ps8[2 * tsel + 1].buf], [p_.buf])
            for j in range(2):
                for t_ in range(nqt):
                    mm(pso[t_][:, 0:68], p_[:, j, t_ * 128:(t_ + 1) * 128], Vt[:, kt + j, h, :], kt + j == kt0, kt + j == kt1 - 1)
            if kt + 2 >= kt1:
                for t_ in range(nqt):
                    r_ = rc[t_ % 2]
                    kb.recip(r_[:, 0:1], pso[t_][:, 64:65])
                    ts("dve", omla[:, (q0 // 128) + t_, h * 64:(h + 1) * 64].par("omla"), pso[t_][:, 0:64], r_[:, 0:1], ALU.mult)
        for i in range(0 if with_ctx else 2, 18):
            psT = kb.psum_b()
            for c in range(2):
                tr(psT[:, c * 128:(c + 1) * 128], omla[:, i, c * 128:(c + 1) * 128], identb)
            for c in range(2):
                cp("act", brT[0][c][:, i * 128:(i + 1) * 128].par("brt0"), psT[:, c * 128:(c + 1) * 128])

    def phase_s5(l, b, with_ctx):
        AR.reset()
        AR2.lo = 0
        U = AR.alloc("U", [16, NCH])
        SN = [[AR.alloc("SN%d%d" % (d, ri), [8, NCH + 2]) for ri in range(2)] for d in range(2)]
        Ere = AR.alloc("Ere", [2, 8, 128])
        nEim = AR.alloc("nEim", [2, 8, 128])
        W3 = AR.alloc("W3", [16, 128])
        scr0 = AR.lo
        W1 = [AR2.alloc("W1%d" % i, [16, 64]) for i in range(2)]
        tab = [AR2.alloc("tab%d" % i, [8, NCH]) for i in range(2)]
        cached = (b > 0) and ("s5_nocache" not in dbg)

        def load_dir(d):
            dma("sp", W1[0].re("p g n -> p (g n)"), st_dir[d, :, 0:1024])
            dma("sp", W1[1].re("p g n -> p (g n)"), st_dir[d, :, 1024:2048])
            dma("act", tab[0].re("p g n -> p (g n)"), st_dir[d, :, 2048:2048 + 8 * NCH])
            dma("act", tab[1].re("p g n -> p (g n)"), st_dir[d, :, 2048 + 8 * NCH:2048 + 16 * NCH])
        if cached:
            dma("sp", Ere.re("p d g n -> p (d g n)"), st_main[0])
            dma("act", nEim.re("p d g n -> p (d g n)"), st_main[1])
            dma("sp", W3.re("p g n -> p (g n)"), st_main[2])
            load_dir(0)
        ws5 = wst[:, :, 0:256]
        Uc = AR.alloc("Uc", [16, 8, 16])
        kblocks = [(0, 32, 0), (32, 128, TC), (160, 128, TC + 1024)]
        for (k0, M, p0) in kblocks:
            pss = [kb.psum_f() for _ in range(4)]
            for r in range(8):
                ps = pss[r // 2]
                for k in range(KC):
                    mm(ps[0:M, (r % 2) * 256:(r % 2 + 1) * 256], hT[:, k, p0 + r:p0 + 8 * M:8], ws5[:, k, :], k == 0, k == KC - 1)
            for q in range(4):
                src = pss[q][0:M, :].re("p (r g i) -> p r g i", r=2, g=16)
                dst = Uc[0:M, :, 2 * q:2 * q + 2, :].re("p g r i -> p r g i").par("s5uc")
                cp("act" if q % 2 == 0 else "dve", dst, src)
            for gq in range(4):
                ps = kb.psum_f()
                for gg_ in range(4):
                    g = gq * 4 + gg_
                    tr(ps[:, gg_ * 128:gg_ * 128 + M], Uc[0:M, g, :, :].re("p r i -> p (r i)"), identf[0:M, 0:M])
                cp("act" if gq % 2 == 0 else "dve", U[:, gq * 4:gq * 4 + 4, k0:k0 + M].par("s5u"),
                   ps.re("p (g n) -> p g n", g=4)[:, :, 0:M])
        prefetch(l, PF_LRU)
        P.barrier()
        AR.lo = scr0
        if "s5_stop1" in dbg:
            return
        if not cached:
            P.op("act", lambda e: e.memzero(W3.ap), [], [W3.buf])
        prm = AR.alloc("prm", [40, 8])
        PW = [AR.alloc("pw%d" % i, [10, 8]) for i in range(2)]
        Bb = [AR.alloc("Bb%d" % i, [8, 16]) for i in range(2)]
        Braw = [AR.alloc("Braw%d" % i, [8, 16]) for i in range(2)]
        Craw = [AR.alloc("Craw%d" % i, [8, 16]) for i in range(2)]
        Dbc = AR.alloc("Dbc", [256])
        t8 = AR.alloc("t8", [8, 16])
        w3t = [AR.alloc("w3t%d" % i, [128]) for i in range(2)]
        tE = AR.alloc("tE", [8, 128])
        Fm = [AR.alloc("F%d" % i, [8, 8, 16]) for i in range(2)]
        Ep = [AR.alloc("Ep%d" % i, [8, 128]) for i in range(2)]
        Cnat = [AR.alloc("Cnat%d" % i, [16, 64], at=Ep[i].off, buf=Ep[i].buf) for i in range(2)]
        rs_sets = [[AR.alloc("rs%d_%d" % (q, i), [NCH], at=Fm[0].off + (q * 7 + i) * NCH) for i in range(6)]
                   for q in range(2)]
        rho_sets = [AR.alloc("rho1_%d" % q, [NCH], at=Fm[0].off + (q * 7 + 6) * NCH) for q in range(2)]
        assert Fm[0].off + 14 * NCH <= Ep[1].off + Ep[1].w
        dma("sp", Dbc, s5_d[l:l + 1, :].re("o n -> (o n)").pbc(128))
        dsel = cmask[:, 2, :]

        prm_bufs = [Buf("prm%d" % i) for i in range(40)]
        pw_bufs = [[Buf("pw%d_%d" % (ri, j)) for j in range(10)] for ri in range(2)]

        def pv(i):
            return TV(prm.ap[:, i, :], prm_bufs[i])

        def pw(ri, j):
            return TV(PW[ri].ap[:, j, :], pw_bufs[ri][j])

        for d in range(2):
            if d == 1 and not cached:
                P.barrier()
            if not cached:
                dma("sp", pv(0), s5_a_re[l, d].re("(gp h) p -> (h p) gp", h=2), slow=True)
                dma("act", pv(1), s5_a_im[l, d].re("(gp h) p -> (h p) gp", h=2), slow=True)
                ldt = t8[:, 0, :]
                dma("sp", ldt, s5_log_dt[l, d:d + 1, :].re("o g -> (o g)").pbc(128))
                dma("sp", Cnat[0][0:16], s5_c_re[l, d].re("g o p -> o g p"))
                dma("act", Cnat[1][0:16], s5_c_im[l, d].re("g o p -> o g p"))
                for h in range(2):
                    rows = slice(h * 64, (h + 1) * 64)
                    cp("dve", pv(2)[rows], ldt[rows, h:16:2])
                    dma("sp", Braw[0][rows].par("ldw"), s5_b_re[l, d].re("(gp h) p i -> h p gp i", h=2)[h])
                    dma("act", Braw[1][rows].par("ldw"), s5_b_im[l, d].re("(gp h) p i -> h p gp i", h=2)[h])
                for ri in range(2):
                    ps = kb.psum_f()
                    for g in range(16):
                        gp, h = g // 2, g % 2
                        mm(ps[h * 64:(h + 1) * 64, gp * 16:(gp + 1) * 16], Cnat[ri][0:16, g, :], identf[0:16, 0:16], True, True)
                    cp("act", Craw[ri], ps[:, 0:128].re("p (g o) -> p g o", g=8))
                if "s5_b1" in dbg:
                    return
                act(pv(2), pv(2), AF.Exp)
                tt("dve", pv(3), pv(0), pv(2), ALU.mult)
                tt("dve", pv(4), pv(1), pv(2), ALU.mult)
                act(pv(5), pv(3), AF.Exp)
                for (dst, shift) in ((6, 0.0), (7, math.pi / 2)):
                    xx, kk, ki = pv(30), pv(31), pv(32)
                    ts("dve", xx, pv(4), shift, ALU.add)
                    ts("dve", kk, xx, 1.0 / (2 * math.pi), ALU.mult)
                    kint = TV(ki.ap.bitcast(I32), ki.buf)
                    cp("dve", kint, kk)
                    cp("dve", kk, kint)
                    stt(xx, kk, -6.28125, xx, ALU.mult, ALU.add)
                    stt(xx, kk, -(2 * math.pi - 6.28125), xx, ALU.mult, ALU.add)
                    ts("dve", xx, xx, math.pi, ALU.min, -math.pi, ALU.max)
                    act(pv(dst), xx, AF.Sin)
                kb.memset("dve", pw(0, 0), 1.0)
                kb.memset("dve", pw(1, 0), 0.0)
                tt("dve", pw(0, 1), pv(5), pv(7), ALU.mult)
                tt("dve", pw(1, 1), pv(5), pv(6), ALU.mult)
                for j in range(2, 9):
                    tt("dve", pv(30), pw(0, j - 1), pw(0, 1), ALU.mult)
                    tt("dve", pv(31), pw(1, j - 1), pw(1, 1), ALU.mult)
                    tt("dve", pw(0, j), pv(30), pv(31), ALU.subtract)
                    tt("dve", pv(30), pw(0, j - 1), pw(1, 1), ALU.mult)
                    tt("dve", pv(31), pw(1, j - 1), pw(0, 1), ALU.mult)
                    tt("dve", pw(1, j), pv(30), pv(31), ALU.add)
                act(pv(8), pv(3), AF.Exp, scale=-16.0)
                tt("dve", pw(0, 9), pw(0, 8), pv(8), ALU.mult)
                tt("dve", pw(1, 9), pw(1, 8), pv(8), ALU.mult)
                ts("dve", pw(1, 9), pw(1, 9), -1.0, ALU.mult)
                act(pv(9), pv(3), AF.Exp, scale=8.0)
                act(pv(10), pv(3), AF.Exp, scale=-8.0)
                tt("dve", pv(11), pw(0, 8), pv(10), ALU.mult)
                tt("dve", pv(12), pw(1, 8), pv(10), ALU.mult)
                ts("dve", pv(13), pw(0, 1), -1.0, ALU.add)
                tt("dve", pv(14), pv(0), pv(0), ALU.mult)
                tt("dve", pv(15), pv(1), pv(1), ALU.mult)
                tt("dve", pv(14), pv(14), pv(15), ALU.add)
                kb.recip(pv(14), pv(14))
                tt("dve", pv(15), pv(13), pv(0), ALU.mult)
                tt("dve", pv(16), pw(1, 1), pv(1), ALU.mult)
                tt("dve", pv(15), pv(15), pv(16), ALU.add)
                tt("dve", pv(15), pv(15), pv(14), ALU.mult)
                tt("dve", pv(16), pw(1, 1), pv(0), ALU.mult)
                tt("dve", pv(17), pv(13), pv(1), ALU.mult)
                tt("dve", pv(16), pv(16), pv(17), ALU.subtract)
                tt("dve", pv(16), pv(16), pv(14), ALU.mult)
                fre = pv(15).us(2).bc([128, 8, 16])
                fim = pv(16).us(2).bc([128, 8, 16])
                tt("dve", Bb[0], Braw[0], fre, ALU.mult)
                tt("dve", t8, Braw[1], fim, ALU.mult)
                tt("dve", Bb[0], Bb[0], t8, ALU.subtract)
                tt("dve", Bb[1], Braw[1], fre, ALU.mult)
                tt("dve", t8, Braw[0], fim, ALU.mult)
                tt("dve", Bb[1], Bb[1], t8, ALU.add)
                for t_ in range(8):
                    f_ = t_ + 1 if d == 0 else 8 - t_
                    pr = pw(0, f_).us(2).bc([128, 8, 16])
                    pi_ = pw(1, f_).us(2).bc([128, 8, 16])
                    eo = Ere[:, d, :, t_ * 16:(t_ + 1) * 16]
                    ei = nEim[:, d, :, t_ * 16:(t_ + 1) * 16]
                    tt("dve", eo, Craw[0], pr, ALU.mult)
                    tt("dve", t8, Craw[1], pi_, ALU.mult)
                    tt("dve", eo, eo, t8, ALU.subtract)
                    tt("dve", ei, Craw[0], pi_, ALU.mult)
                    tt("dve", t8, Craw[1], pr, ALU.mult)
                    tt("dve", ei, ei, t8, ALU.add)
                    ts("dve", ei, ei, -1.0, ALU.mult)
                for r in range(8):
                    e_ = 7 - r if d == 0 else r
                    pr = pw(0, e_).us(2).bc([128, 8, 16])
                    pi_ = pw(1, e_).us(2).bc([128, 8, 16])
                    tt("dve", Fm[0][:, :, r, :], Bb[0], pr, ALU.mult)
                    tt("dve", t8, Bb[1], pi_, ALU.mult)
                    tt("dve", Fm[0][:, :, r, :], Fm[0][:, :, r, :], t8, ALU.subtract)
                    tt("dve", Fm[1][:, :, r, :], Bb[1], pr, ALU.mult)
                    tt("dve", t8, Bb[0], pi_, ALU.mult)
                    tt("dve", Fm[1][:, :, r, :], Fm[1][:, :, r, :], t8, ALU.add)
                qr = pw(0, 9).us(2).bc([128, 8, 128])
                qi = pw(1, 9).us(2).bc([128, 8, 128])
                tt("dve", Ep[0], Ere[:, d], qr, ALU.mult)
                tt("dve", tE, nEim[:, d], qi, ALU.mult)
                tt("dve", Ep[0], Ep[0], tE, ALU.add)
                tt("dve", Ep[1], nEim[:, d], qr, ALU.mult)
                tt("dve", tE, Ere[:, d], qi, ALU.mult)
                tt("dve", Ep[1], Ep[1], tE, ALU.subtract)
                if "s5_b2" in dbg:
                    return
                for ri in range(2):
                    for gq in range(4):
                        ps = kb.psum_f()
                        for gg_ in range(4):
                            g = gq * 4 + gg_
                            gp, h = g // 2, g % 2
                            rows = slice(h * 64, (h + 1) * 64)
                            mm(ps[:, gg_ * 64:(gg_ + 1) * 64], Fm[ri][:, gp].re("p r i -> p (r i)"), identf[:, h * 64:(h + 1) * 64], True, True)
                        cp("act", W1[ri][:, gq * 4:gq * 4 + 4, :], ps[:, 0:256].re("p (g n) -> p g n", g=4))
                if "s5_b3" in dbg:
                    return
                for g in range(16):
                    gp, h = g // 2, g % 2
                    rows = slice(h * 64, (h + 1) * 64)
                    ps = kb.psf[h * 2 + (g // 2) % 2]
                    mm(ps[:, 0:128], Fm[0][rows, gp].re("p r i -> p (r i)"), Ep[0][rows, gp, :], True, False)
                    mm(ps[:, 0:128], Fm[1][rows, gp].re("p r i -> p (r i)"), Ep[1][rows, gp, :], False, True)
                    wt = w3t[g % 2]
                    tt("dve", wt, ps[:, 0:128], cmask[:, d, :], ALU.mult)
                    tt("pool", W3[:, g, :], W3[:, g, :], wt, ALU.add)
                if d == 0:
                    for g in range(16):
                        dcol = Dbc[:, g * 16:(g + 1) * 16].us(1).bc([128, 8, 16])
                        wt = w3t[g % 2]
                        tt("dve", wt.re("p (t o) -> p t o", t=8), dsel.re("p (t o) -> p t o", t=8), dcol, ALU.mult)
                        tt("pool", W3[:, g, :], W3[:, g, :], wt, ALU.add)
                if "s5_b4" in dbg:
                    return
                kb.memset("dve", tab[0][:, :, 0:1], 1.0)
                kb.memset("dve", tab[1][:, :, 0:1], 0.0)
                cp("dve", tab[0][:, :, 1:2], pv(11).us(2))
                cp("dve", tab[1][:, :, 1:2], pv(12).us(2))
                n = 2
                while n < NCH:
                    m = min(n, NCH - n)
                    tt("dve", pv(30), tab[0][:, :, n - 1], pv(11), ALU.mult)
                    tt("dve", pv(31), tab[1][:, :, n - 1], pv(12), ALU.mult)
                    tt("dve", pv(33), pv(30), pv(31), ALU.subtract)
                    tt("dve", pv(30), tab[0][:, :, n - 1], pv(12), ALU.mult)
                    tt("dve", pv(31), tab[1][:, :, n - 1], pv(11), ALU.mult)
                    tt("dve", pv(34), pv(30), pv(31), ALU.add)
                    Pr = pv(33).us(2).bc([128, 8, m])
                    Pi = pv(34).us(2).bc([128, 8, m])
                    ta = tE[:, :, 0:m]
                    tt("dve", tab[0][:, :, n:n + m], tab[0][:, :, 0:m], Pr, ALU.mult)
                    tt("dve", ta, tab[1][:, :, 0:m], Pi, ALU.mult)
                    tt("dve", tab[0][:, :, n:n + m], tab[0][:, :, n:n + m], ta, ALU.subtract)
                    tt("dve", tab[1][:, :, n:n + m], tab[0][:, :, 0:m], Pi, ALU.mult)
                    tt("dve", ta, tab[1][:, :, 0:m], Pr, ALU.mult)
                    tt("dve", tab[1][:, :, n:n + m], tab[1][:, :, n:n + m], ta, ALU.add)
                    n *= 2
                dma("sp", st_dir[d, :, 0:1024], W1[0].re("p g n -> p (g n)"))
                dma("sp", st_dir[d, :, 1024:2048], W1[1].re("p g n -> p (g n)"))
                dma("act", st_dir[d, :, 2048:2048 + 8 * NCH], tab[0].re("p g n -> p (g n)"))
                dma("act", st_dir[d, :, 2048 + 8 * NCH:2048 + 16 * NCH], tab[1].re("p g n -> p (g n)"))
                dma("sp", st_dir[d, :, 2048 + 16 * NCH:2048 + 16 * NCH + 8], pv(9))
            else:
                if d == 1:
                    load_dir(1)
                dma("sp", pv(9), st_dir[d, :, 2048 + 16 * NCH:2048 + 16 * NCH + 8])
            if "s5_stop2" in dbg:
                return
            if not cached:
                P.barrier()
            def nat(tv, sl, rev):
                v = tv[:, sl]
                return v[:, ::-1] if rev else v

            def gp_gen(gp, d=d):
                rs, rho1 = rs_sets[gp % 2], rho_sets[gp % 2]
                psv = [kb.psum_f(), kb.psum_f()]
                for ri in range(2):
                    for h in range(2):
                        g = gp * 2 + h
                        mm(psv[ri][h * 64:(h + 1) * 64, 0:NCH], W1[ri][:, g, :], U[:, g, :], True, True)
                cp("pool", rho1, pv(9)[:, gp:gp + 1].bc([128, NCH]))
                yield
                Cn, Sn = tab[0][:, gp, :], tab[1][:, gp, :]
                if d == 0:
                    segs = [(slice(0, NCH), slice(0, NCH), False)]
                else:
                    segs = [(slice(0, 32), slice(0, 32), True), (slice(32, NCH), slice(32, NCH), True)]
                for (js, ks, rev) in segs:
                    vre, vim = nat(psv[0][:, 0:NCH], ks, rev), nat(psv[1][:, 0:NCH], ks, rev)
                    tt("dve", rs[0][:, js], vre, Cn[:, js], ALU.mult)
                    tt("dve", rs[1][:, js], vim, Sn[:, js], ALU.mult)
                    tt("dve", rs[4][:, js], vim, Cn[:, js], ALU.mult)
                    tt("dve", rs[5][:, js], vre, Sn[:, js], ALU.mult)
                yield
                tt("dve", rs[2], rs[0], rs[1], ALU.add)
                tt("dve", rs[3], rs[4], rs[5], ALU.subtract)
                yield
                kb.scan(rs[4], rho1, rs[2], 0.0)
                kb.scan(rs[5], rho1, rs[3], 0.0)
                yield
                tt("dve", rs[0], rs[4], Cn, ALU.mult)
                tt("dve", rs[1], rs[5], Sn, ALU.mult)
                tt("dve", rs[2], rs[4], Sn, ALU.mult)
                tt("dve", rs[3], rs[5], Cn, ALU.mult)
                yield
                for (js, ks, rev) in segs:
                    if d == 0:
                        osl = slice(1, NCH + 1)
                    else:
                        osl = slice(0, 32) if ks.start == 0 else slice(33, NCH + 1)
                    ore = nat(SN[d][0][:, gp, :], osl, rev).par("s5sn")
                    oim = nat(SN[d][1][:, gp, :], osl, rev).par("s5sn")
                    tt("dve", ore, rs[0][:, js], rs[1][:, js], ALU.subtract)
                    tt("pool", oim, rs[2][:, js], rs[3][:, js], ALU.add)
                yield
            for gp0 in range(0, 8, 2):
                lockstep([gp_gen(gp0), gp_gen(gp0 + 1)])
            for ri in range(2):
                if d == 0:
                    kb.memset("pool", SN[0][ri][:, :, 0:1], 0.0)
                else:
                    kb.memset("pool", SN[1][ri][:, :, 32:33], 0.0)
                    cp("pool", SN[1][ri][:, :, NCH + 1:NCH + 2], SN[1][ri][:, :, 0:1])
        if not cached:
            dma("sp", st_main[0], Ere.re("p d g n -> p (d g n)"))
            dma("act", st_main[1], nEim.re("p d g n -> p (d g n)"))
            dma("sp", st_main[2], W3.re("p g n -> p (g n)"))
        if "s5_stop3" in dbg:
            return
        P.barrier()
        AR.lo = scr0
        AR2.lo = 0
        Yc = AR.alloc("Yc", [8, 256])
        gg = AR.alloc("gg", [8, 256])
        g2 = AR.alloc("g2", [8, 256])
        sg = [AR.alloc("sg%d" % i, [512]) for i in range(2)]
        ggT = AR2.alloc("ggT", [2, NPOS])
        ggb = AR2.alloc("ggb", [2, NPOS], BF16)
        banks = [kb.psf[0], kb.psf[2], kb.psf[1], kb.psf[3]]

        def r_mm(k0, M, p0):
            for g in range(16):
                gp, h = g // 2, g % 2
                rows = slice(h * 64, (h + 1) * 64)
                bank = banks[h * 2 + (gp // 4)]
                o = bank[0:M, (gp % 4) * 128:(gp % 4 + 1) * 128]
                cf = slice(k0, k0 + M)
                cb_ = slice(k0 + 1, k0 + 1 + M) if k0 == 0 else slice(k0 + 2, k0 + 2 + M)
                mm(o, SN[0][0][rows, gp, cf], Ere[rows, 0, gp, :], True, False)
                mm(o, SN[0][1][rows, gp, cf], nEim[rows, 0, gp, :], False, False)
                mm(o, SN[1][0][rows, gp, cb_], Ere[rows, 1, gp, :], False, False)
                mm(o, SN[1][1][rows, gp, cb_], nEim[rows, 1, gp, :], False, False)
                mm(o, U[:, g, k0:k0 + M], W3[:, g, :], False, True)

        def r_evac(k0, M, p0):
            for h in range(2):
                for q in range(2):
                    bank = banks[h * 2 + q]
                    src = bank[0:M, :].re("p (g t o) -> p g t o", g=4, t=8)
                    dst = Yc[0:M].re("p t (g2 h o) -> p h g2 t o", h=2, o=16)[:, h, 4 * q:4 * q + 4].par("s5yc")
                    cp("act" if h == 0 else "dve", dst, src)

        def r_rest(k0, M, p0):
            yv, gv, g2v = Yc[0:M], gg[0:M], g2[0:M]
            act(g2v, yv, AF.Square)
            ts("dve", g2v, g2v, 0.044715, ALU.mult, 1.0, ALU.add)
            tt("dve", g2v, g2v, yv, ALU.mult)
            act(g2v, g2v, AF.Sigmoid, scale=1.5957691216057308)
            tt("dve", gv, g2v, yv, ALU.mult)
            for t_ in range(8):
                pz = [kb.psf[4], kb.psf[5]]
                for c in range(2):
                    tr(pz[c][:, 0:M], gv[:, t_, c * 128:(c + 1) * 128], identf[0:M, 0:M])
                for c in range(2):
                    cp("act" if c == 0 else "dve", ggT[:, c, p0 + t_:p0 + 8 * M:8].par("s5gg"), pz[c][:, 0:M])
        rb = [blk for blk in kblocks if not (blk[0] == 0 and not with_ctx)]
        r_mm(*rb[0])
        for bi, blk in enumerate(rb):
            r_evac(*blk)
            if bi + 1 < len(rb):
                r_mm(*rb[bi + 1])
            r_rest(*blk)
        lo = 0 if with_ctx else TC
        for c in range(2):
            cp("dve", ggb[:, c, lo:NPOS], ggT[:, c, lo:NPOS])
        for c in range(2):
            pos = lo
            while pos < NPOS:
                n = min(512, NPOS - pos)
                ps = kb.psum_f()
                for k in range(2):
                    mm(ps[:, 0:n], wglu[:, k, c * 128:(c + 1) * 128], ggb[:, k, pos:pos + n], k == 0, k == 1)
                s_ = sg[(pos // 512) % 2]
                act(s_[:, 0:n], ps[:, 0:n], AF.Sigmoid)
                tt("dve", brT[1][c][:, pos:pos + n], s_[:, 0:n], ggT[:, c, pos:pos + n], ALU.mult)
                pos += n

    def phase_merge(l, b, with_ctx, nxt=None):
        AR.reset(top=True)
        lo = 0 if with_ctx else TC
        blocks = []
        pos = lo
        while pos < NPOS:
            n = min(512, NPOS - pos) if pos >= TC else TC - pos
            blocks.append((pos, n))
            pos += n
        ypT = AR.alloc("ypT", [KC, NPOS], BF16, top=True)
        wbr = AR.alloc("wbr", [4, 2, D], BF16, top=True)
        wo = AR.alloc("wo", [KC, D], BF16, top=True)
        wml = [AR.alloc("wml%d" % i, [KC, 4, 128], BF16, top=True) for i in range(2)]

        def load_wml(oc):
            for n_ in range(4):
                c0 = O_MERGE + n_ * D + oc * 128
                dma("pool", wml[oc % 2][:, :, n_, :].par("ldw"), w_in[l, :, c0:c0 + 128].re("(c p) n -> p c n", p=128))
        wg = AR.alloc("w_gate", [KC, 1024], BF16)
        for cc in range(1, 8):
            dma("pool", wg[:, :, cc * 128:(cc + 1) * 128].par("ldw"),
                w_in[l, :, O_GATE + cc * 128:O_GATE + (cc + 1) * 128].re("(c p) n -> p c n", p=128))
        for n_ in range(4):
            dma("pool", wbr[:, n_].par("ldw"), w_branch[l, n_].re("(c p) n -> p c n", p=128))
        load_wml(0)
        load_wml(1)
        dma("pool", wo, w_out[l].re("(c p) n -> p c n", p=128))
        sl = [AR.alloc("sl%d" % i, [512], BF16) for i in range(2)]
        it = 0
        for cc in range(8):
            for (p0, n) in blocks:
                ps = kb.psum_f()
                for k in range(KC):
                    wsl = wst[:, k, 0:128] if cc == 0 else wg[:, k, cc * 128:(cc + 1) * 128]
                    mm(ps[:, 0:n], wsl, hT[:, k, p0:p0 + n], k == 0, k == KC - 1)
                s_ = sl[it % 2]
                it += 1
                act(s_[:, 0:n], ps[:, 0:n], AF.Silu)
                br = brT[cc // 2][cc % 2][:, p0:p0 + n]
                tt("dve", br, br, s_[:, 0:n], ALU.mult)
            if cc == 0 and nxt is not None:
                prefetch(nxt, PF_S5)
        AR.reset()
        sg = [AR.alloc("sg%d" % i, [512]) for i in range(2)]
        tm = [AR.alloc("tm%d" % i, [512]) for i in range(2)]
        acc = [AR.alloc("acc%d" % i, [512]) for i in range(2)]
        it = 0
        ib = 0
        for oc in range(8):
            wm = wml[oc % 2]
            if 1 <= oc and oc + 1 < 8:
                load_wml(oc + 1)
            for (p0, n) in blocks:
                ac = acc[ib % 2]
                ib += 1
                for n_ in range(4):
                    psA = kb.psum_f()
                    for k in range(2):
                        mm(psA[:, 0:n], wbr[:, n_, k, oc * 128:(oc + 1) * 128], brT[n_][k][:, p0:p0 + n], k == 0, k == 1)
                    psB = kb.psum_f()
                    for k in range(KC):
                        mm(psB[:, 0:n], wm[:, k, n_, :], hT[:, k, p0:p0 + n], k == 0, k == KC - 1)
                    s_ = sg[it % 2]
                    t_ = tm[it % 2]
                    it += 1
                    act(s_[:, 0:n], psB[:, 0:n], AF.Sigmoid)
                    if n_ == 0:
                        tt("dve", ac[:, 0:n], psA[:, 0:n], s_[:, 0:n], ALU.mult)
                    elif n_ < 3:
                        tt("dve", t_[:, 0:n], psA[:, 0:n], s_[:, 0:n], ALU.mult)
                        tt("dve", ac[:, 0:n], ac[:, 0:n], t_[:, 0:n], ALU.add)
                    else:
                        tt("dve", t_[:, 0:n], psA[:, 0:n], s_[:, 0:n], ALU.mult)
                        tt("dve", ypT[:, oc, p0:p0 + n].par("ypt"), ac[:, 0:n], t_[:, 0:n], ALU.add)
        AR.reset()
        gbc = AR.alloc("gbc", [2, D])
        dma("sp", gbc[:, 0, :], modD[l, b, 2 * D:3 * D].pbc(128))
        if with_ctx:
            dma("sp", gbc[:, 1, :], modD[l, 2, 2 * D:3 * D].pbc(128))
        xt = [AR.alloc("xt%d" % i, [D]) for i in range(2)]
        yt = [AR.alloc("yt%d" % i, [D]) for i in range(2)]
        for i in range(0 if with_ctx else 2, 18):
            x_t, y_t = xt[i % 2], yt[i % 2]
            dma("sp", x_t, src_tile(l, b, i))
            for hf in range(2):
                ps = kb.psum_f()
                for k in range(KC):
                    mm(ps, ypT[:, k, i * 128:(i + 1) * 128], wo[:, k, hf * 512:(hf + 1) * 512], k == 0, k == KC - 1)
                tt("dve", y_t[:, hf * 512:(hf + 1) * 512], ps, gbc[:, 1 if i < 2 else 0, hf * 512:(hf + 1) * 512], ALU.mult)
            tt("dve", y_t, y_t, x_t, ALU.add)
            if i < 2:
                dst = (tap("xc1", [nb, TC, D]) if "x1out" in dbg else xcmid)[b, i * 128:(i + 1) * 128, :]
            else:
                dst = (xmid if (l < DEPTH - 1 and "x1out" not in dbg) else y_out)[b, (i - 2) * 128:(i - 1) * 128, :]
            dma("act", dst, y_t)

    def dump_br(name, n, lo):
        o = tap(name, [2, 128, NPOS])
        tmpf = AR.alloc("dump_" + name, [NPOS])
        for c in range(2):
            cp("dve", tmpf[:, lo:NPOS], brT[n][c][:, lo:NPOS])
            dma("sp", o[c, :, lo:NPOS], tmpf[:, lo:NPOS])

    prefetch(layers[0], PF_S5)
    for l in layers:
        with_ctx = l < DEPTH - 1
        prep_layer(l)
        for b in range(nb):
            if "prep_only" in dbg:
                continue
            phase_norm(l, b)
            if "hT" in dbg and b == 0 and l == layers[0]:
                o = tap("hT", [KC, 128, NPOS])
                AR.reset()
                tmpf = AR.alloc("dump_hT", [NPOS])
                for c in range(KC):
                    cp("dve", tmpf, hT[:, c, :])
                    dma("sp", o[c], tmpf)
            only = dbg & {"only_lru", "only_pool", "only_mla", "only_s5", "only_norm"}
            if not only or "only_s5" in only:
                phase_s5(l, b, with_ctx)
            if not only or "only_lru" in only:
                phase_lru(l, b, with_ctx)
            if not only or "only_pool" in only:
                phase_pool(l, b, with_ctx)
            if not only or "only_mla" in only:
                phase_mla(l, b, with_ctx)
            if "br" in dbg and b == 0 and l == layers[0]:
                AR.reset()
                lo = 0 if with_ctx else TC
                for n_, nm in enumerate(("mla", "s5", "lru", "pool")):
                    dump_br(nm, n_, lo)
            if "nomerge" not in dbg:
                if b + 1 < nb:
                    nxt = l
                else:
                    li = list(layers).index(l)
                    nxt = layers[li + 1] if li + 1 < len(layers) else None
                phase_merge(l, b, with_ctx, nxt)
    P.barrier()
    P.emit()
    kb.st.close()
    return nc, kb


def _consts():
    ident = np.eye(128, dtype=np.float32)
    rows_n = T // 64
    row = np.repeat(np.arange(rows_n), 64).astype(np.float32)
    col = np.tile(np.arange(64), rows_n).astype(np.float32)
    nf = 8
    inv = (np.float32(10000.0) ** (-np.arange(nf, dtype=np.float32) / nf)).astype(np.float32)
    ar = (row[:, None] * inv).astype(np.float32)
    ac = (col[:, None] * inv).astype(np.float32)
    cr, sr, cc, sc = np.cos(ar), np.sin(ar), np.cos(ac), np.sin(ac)
    ropeC = np.concatenate([cr, cr, cc, cc], axis=1).astype(np.float32)
    ropeS = np.concatenate([-sr, sr, -sc, sc], axis=1).astype(np.float32)
    cpool = np.ones((128, 34), np.float32)
    for c in range(2):
        for p in range(128):
            w = (2, 4, 8, 16)[2 * c + p // 64]
            hw = w // 2
            cpool[p, c] = 1.0 / w
            for t in range(8):
                cnt = t + hw if t < hw else w
                cpool[p, 2 + c * 8 + t] = w / cnt
            for j in range(8):
                dist = 8 - j
                cnt = dist + hw if dist < hw else w
                cpool[p, 18 + c * 8 + j] = w / cnt
    m = np.zeros((128, 3, 128), np.float32)
    for r in range(8):
        for t in range(8):
            if r <= t:
                m[r * 16:(r + 1) * 16, 0, t * 16:(t + 1) * 16] = 1.0
            if r >= t:
                m[r * 16:(r + 1) * 16, 1, t * 16:(t + 1) * 16] = 1.0
            if r == t:
                m[r * 16:(r + 1) * 16, 2, t * 16:(t + 1) * 16] = np.eye(16, dtype=np.float32)
    return {"c_ident": ident, "c_ropeC": ropeC, "c_ropeS": ropeS, "c_pool": cpool, "c_mask": m}


_WNAMES = ["w_ada", "b_ada", "norm_g", "w_in", "mla_q_norm", "mla_kv_norm", "mla_w_uq", "mla_w_ukv", "mla_q_gain",
           "mla_k_gain", "s5_a_re", "s5_a_im", "s5_log_dt", "s5_b_re", "s5_b_im", "s5_c_re", "s5_c_im", "s5_d",
           "s5_w_glu", "lru_conv_w", "lru_conv_b", "lru_lambda", "lru_w_a", "lru_b_a", "lru_w_x", "lru_b_x",
           "pool_w", "pool_b", "pool_scale", "w_branch", "w_out"]


def make_in_maps(inputs, n_cores, nb):
    consts = _consts()
    maps = []
    for r in range(n_cores):
        bs = slice(r * nb, (r + 1) * nb)
        m = {"x": np.ascontiguousarray(inputs["x"][bs], dtype=np.float32),
             "ctx": np.ascontiguousarray(inputs["ctx"][bs], dtype=np.float32)}
        cv = np.zeros((3, D), np.float32)
        cv[0:nb] = np.asarray(inputs["c"], dtype=np.float32)[bs]
        cv[2] = np.asarray(inputs["c_ctx"], dtype=np.float32)
        m["cvec"] = cv
        for k in _WNAMES:
            m[k] = np.ascontiguousarray(inputs[k], dtype=np.float32)
        m.update(consts)
        maps.append(m)
    return maps


_CACHE = {}


def kernel(**inputs):
    n_cores, nb = 8, 2
    if "nc" not in _CACHE:
        _CACHE["nc"] = build_program(nb=nb)[0]
    nc = _CACHE["nc"]
    maps = make_in_maps(inputs, n_cores, nb)
    res = run_bass_kernel_spmd(nc, maps, core_ids=list(range(n_cores)))
    out = np.concatenate([np.asarray(r["y"], dtype=np.float32) for r in res.results], axis=0)
    return out
```
